# Optimizing a Trainium2 kernel written in Bass

```python
import math
import jax, jax.numpy as jnp
from jax import lax
import numpy as np

D_MODEL = 1024
BATCH = 2
SEQ = 8192
DEPTH = 2

CHUNK = 64
D_MIX = D_MODEL
D_SSM = D_MIX // 2
D_MLSTM = D_MIX - D_SSM
SSM_GROUP = 16
SSM_GROUPS = D_SSM // SSM_GROUP
SSM_STATE = 64
ML_HEADS = 4
ML_HEAD_DIM = D_MLSTM // ML_HEADS
CONV_WIDTH = 4
D_FF = 4 * D_MODEL
D_IN = D_SSM + 4 * D_MLSTM + 2 * ML_HEADS
EPS = 1e-6
DT_MIN = 1e-3
DT_MAX = 1e-1

kernel_name = "hybrid_s5_mlstm_parallel_heads"


def rmsnorm(x, g):
    xf = x.astype(jnp.float32)
    y = xf * lax.rsqrt(jnp.mean(jnp.square(xf), axis=-1, keepdims=True) + EPS)
    return (y * g.astype(jnp.float32)).astype(x.dtype)


def causal_depthwise_conv(x, w, b):
    c = x.shape[-1]
    y = lax.conv_general_dilated(
        x, w[:, None, :].astype(x.dtype), window_strides=(1,),
        padding=[(CONV_WIDTH - 1, 0)], dimension_numbers=("NWC", "WIO", "NWC"),
        feature_group_count=c)
    return y + b.astype(x.dtype)


def s5_mixer(u, a_re, a_im, log_dt, b_re, b_im, c_re, c_im, d, w_glu, b_glu, out_g):
    bsz, L, _ = u.shape
    f32 = jnp.float32
    uf = u.astype(f32).reshape(bsz, L, SSM_GROUPS, SSM_GROUP)
    a_re = a_re.astype(f32); a_im = a_im.astype(f32)
    dt = jnp.exp(log_dt.astype(f32))[:, None]
    mag = jnp.exp(a_re * dt)
    ab_re = mag * jnp.cos(a_im * dt)
    ab_im = mag * jnp.sin(a_im * dt)
    den = jnp.square(a_re) + jnp.square(a_im)
    zr = ab_re - 1.0
    s_re = (zr * a_re + ab_im * a_im) / den
    s_im = (ab_im * a_re - zr * a_im) / den
    b_re = b_re.astype(f32); b_im = b_im.astype(f32)
    bb_re = s_re[..., None] * b_re - s_im[..., None] * b_im
    bb_im = s_re[..., None] * b_im + s_im[..., None] * b_re
    bu_re = jnp.einsum('blgp,gnp->blgn', uf, bb_re)
    bu_im = jnp.einsum('blgp,gnp->blgn', uf, bb_im)
    aa_re = jnp.broadcast_to(ab_re, bu_re.shape)
    aa_im = jnp.broadcast_to(ab_im, bu_im.shape)

    def combine(left, right):
        a1r, a1i, b1r, b1i = left
        a2r, a2i, b2r, b2i = right
        ar = a1r * a2r - a1i * a2i
        ai = a1r * a2i + a1i * a2r
        br = a2r * b1r - a2i * b1i + b2r
        bi = a2r * b1i + a2i * b1r + b2i
        return ar, ai, br, bi

    _, _, x_re, x_im = lax.associative_scan(combine, (aa_re, aa_im, bu_re, bu_im), axis=1)
    y = (jnp.einsum('blgn,gpn->blgp', x_re, c_re.astype(f32))
         - jnp.einsum('blgn,gpn->blgp', x_im, c_im.astype(f32))
         + d.astype(f32) * uf)
    y = y.reshape(bsz, L, D_SSM)
    z = jax.nn.gelu(y)
    z = z * jax.nn.sigmoid(z @ w_glu.astype(f32) + b_glu.astype(f32))
    return rmsnorm(z, out_g).astype(u.dtype)


def mlstm_mixer(q_raw, k_raw, v, o_raw, i_raw, f_raw, conv_w, conv_b, b_i, b_f, norm_g):
    bsz, L, _ = v.shape
    nc = L // CHUNK
    f32 = jnp.float32
    qk = jax.nn.silu(causal_depthwise_conv(jnp.concatenate([q_raw, k_raw], -1), conv_w, conv_b))
    q, k = jnp.split(qk, 2, axis=-1)

    def heads(t):
        return t.astype(f32).reshape(bsz, nc, CHUNK, ML_HEADS, ML_HEAD_DIM).transpose(1, 0, 3, 2, 4)

    def gates(t):
        return t.astype(f32).reshape(bsz, nc, CHUNK, ML_HEADS).transpose(1, 0, 3, 2)

    qc = heads(q)
    kc = heads(k) * (1.0 / math.sqrt(ML_HEAD_DIM))
    vc = heads(v)
    ig = gates(i_raw + b_i.astype(i_raw.dtype))
    lf = gates(jax.nn.log_sigmoid((f_raw + b_f.astype(f_raw.dtype)).astype(f32)))
    mask = jnp.tril(jnp.ones((CHUNK, CHUNK), dtype=bool))

    def step(carry, xs):
        c_st, n_st, m_st = carry
        qb, kb, vb, ib, fb = xs
        bcum = jnp.cumsum(fb, axis=-1)
        dmat = bcum[..., :, None] - bcum[..., None, :] + ib[..., None, :]
        dmat = jnp.where(mask, dmat, -jnp.inf)
        inter = bcum + m_st[..., None]
        m_t = jnp.maximum(inter, jnp.max(dmat, axis=-1))
        s = jnp.einsum('bhtd,bhsd->bhts', qb, kb) * jnp.exp(dmat - m_t[..., None])
        sc = jnp.exp(inter - m_t)
        num = (jnp.einsum('bhts,bhsv->bhtv', s, vb)
               + sc[..., None] * jnp.einsum('bhtk,bhkv->bhtv', qb, c_st))
        den = jnp.sum(s, axis=-1) + sc * jnp.einsum('bhtk,bhk->bht', qb, n_st)
        h = num / jnp.maximum(jnp.abs(den), jnp.exp(-m_t))[..., None]
        b_last = bcum[..., -1]
        g = b_last[..., None] - bcum + ib
        m_new = jnp.maximum(b_last + m_st, jnp.max(g, axis=-1))
        decay = jnp.exp(b_last + m_st - m_new)
        wk = jnp.exp(g - m_new[..., None])
        c_new = decay[..., None, None] * c_st + jnp.einsum('bhs,bhsk,bhsv->bhkv', wk, kb, vb)
        n_new = decay[..., None] * n_st + jnp.einsum('bhs,bhsk->bhk', wk, kb)
        return (c_new, n_new, m_new), h

    init = (jnp.zeros((bsz, ML_HEADS, ML_HEAD_DIM, ML_HEAD_DIM), f32),
            jnp.zeros((bsz, ML_HEADS, ML_HEAD_DIM), f32),
            jnp.zeros((bsz, ML_HEADS), f32))
    _, h = lax.scan(step, init, (qc, kc, vc, ig, lf))
    h = h.transpose(1, 0, 3, 2, 4).reshape(bsz, L, ML_HEADS, ML_HEAD_DIM)
    o = jax.nn.sigmoid(o_raw.astype(f32)).reshape(bsz, L, ML_HEADS, ML_HEAD_DIM)
    h = o * h
    h = rmsnorm(h, norm_g.reshape(ML_HEADS, ML_HEAD_DIM))
    return h.reshape(bsz, L, D_MLSTM).astype(v.dtype)


def setup_inputs(seed: int = 0) -> dict:
    key = jax.random.key(seed)
    ks = jax.random.split(key, 26)
    f32 = jnp.float32
    G, N, P, H = SSM_GROUPS, SSM_STATE, SSM_GROUP, ML_HEADS

    def nrm(k, shape, scale):
        return jax.random.normal(k, shape, f32) * scale

    def gain(k, shape):
        return 1.0 + 0.01 * jax.random.normal(k, shape, f32)

    x = jax.random.normal(ks[0], (BATCH, SEQ, D_MODEL), f32)
    norm_mix_g = gain(ks[1], (DEPTH, D_MODEL))
    w_in = nrm(ks[2], (DEPTH, D_MODEL, D_IN), D_MODEL ** -0.5)
    s5_a_re = -0.5 + 0.01 * jax.random.normal(ks[3], (DEPTH, G, N), f32)
    s5_a_im = (math.pi * jnp.arange(N, dtype=f32))[None, None, :] + 0.01 * jax.random.normal(ks[4], (DEPTH, G, N), f32)
    s5_log_dt = jax.random.uniform(ks[5], (DEPTH, G), f32, math.log(DT_MIN), math.log(DT_MAX))
    s5_b_re = nrm(ks[6], (DEPTH, G, N, P), (2 * P) ** -0.5)
    s5_b_im = nrm(ks[7], (DEPTH, G, N, P), (2 * P) ** -0.5)
    s5_c_re = nrm(ks[8], (DEPTH, G, P, N), N ** -0.5)
    s5_c_im = nrm(ks[9], (DEPTH, G, P, N), N ** -0.5)
    s5_d = nrm(ks[10], (DEPTH, G, P), 1.0)
    s5_w_glu = nrm(ks[11], (DEPTH, D_SSM, D_SSM), D_SSM ** -0.5)
    s5_b_glu = nrm(ks[12], (DEPTH, D_SSM), 0.01)
    s5_out_g = gain(ks[13], (DEPTH, D_SSM))
    ml_conv_w = nrm(ks[14], (DEPTH, CONV_WIDTH, 2 * D_MLSTM), CONV_WIDTH ** -0.5)
    ml_conv_b = nrm(ks[15], (DEPTH, 2 * D_MLSTM), 0.01)
    ml_b_i = nrm(ks[16], (DEPTH, H), 0.1)
    ml_b_f = jnp.linspace(3.0, 6.0, H, dtype=f32)[None, :] + 0.01 * jax.random.normal(ks[17], (DEPTH, H), f32)
    ml_norm_g = gain(ks[18], (DEPTH, D_MLSTM))
    w_out = nrm(ks[19], (DEPTH, D_MIX, D_MODEL), D_MIX ** -0.5)
    norm_ffn_g = gain(ks[20], (DEPTH, D_MODEL))
    w_ff1 = nrm(ks[21], (DEPTH, D_MODEL, D_FF), D_MODEL ** -0.5)
    w_ff2 = nrm(ks[22], (DEPTH, D_FF, D_MODEL), D_FF ** -0.5)
    final_norm_g = gain(ks[23], (D_MODEL,))
    return {"x": x, "norm_mix_g": norm_mix_g, "w_in": w_in,
            "s5_a_re": s5_a_re, "s5_a_im": s5_a_im, "s5_log_dt": s5_log_dt,
            "s5_b_re": s5_b_re, "s5_b_im": s5_b_im, "s5_c_re": s5_c_re, "s5_c_im": s5_c_im,
            "s5_d": s5_d, "s5_w_glu": s5_w_glu, "s5_b_glu": s5_b_glu, "s5_out_g": s5_out_g,
            "ml_conv_w": ml_conv_w, "ml_conv_b": ml_conv_b, "ml_b_i": ml_b_i, "ml_b_f": ml_b_f,
            "ml_norm_g": ml_norm_g, "w_out": w_out, "norm_ffn_g": norm_ffn_g,
            "w_ff1": w_ff1, "w_ff2": w_ff2, "final_norm_g": final_norm_g}


def reference(x, norm_mix_g, w_in, s5_a_re, s5_a_im, s5_log_dt, s5_b_re, s5_b_im,
              s5_c_re, s5_c_im, s5_d, s5_w_glu, s5_b_glu, s5_out_g, ml_conv_w, ml_conv_b,
              ml_b_i, ml_b_f, ml_norm_g, w_out, norm_ffn_g, w_ff1, w_ff2, final_norm_g):
    sizes = [D_SSM, D_MLSTM, D_MLSTM, D_MLSTM, D_MLSTM, ML_HEADS, ML_HEADS]
    cuts = [int(c) for c in np.cumsum(sizes)[:-1]]
    for l in range(DEPTH):
        h = rmsnorm(x, norm_mix_g[l])
        p = h @ w_in[l]
        u, q_raw, k_raw, v, o_raw, i_raw, f_raw = jnp.split(p, cuts, axis=-1)
        y_ssm = s5_mixer(u, s5_a_re[l], s5_a_im[l], s5_log_dt[l], s5_b_re[l], s5_b_im[l],
                         s5_c_re[l], s5_c_im[l], s5_d[l], s5_w_glu[l], s5_b_glu[l], s5_out_g[l])
        y_ml = mlstm_mixer(q_raw, k_raw, v, o_raw, i_raw, f_raw, ml_conv_w[l], ml_conv_b[l],
                           ml_b_i[l], ml_b_f[l], ml_norm_g[l])
        x = x + jnp.concatenate([y_ssm, y_ml], axis=-1) @ w_out[l]
        h = rmsnorm(x, norm_ffn_g[l])
        x = x + jnp.square(jax.nn.relu(h @ w_ff1[l])) @ w_ff2[l]
    return rmsnorm(x, final_norm_g)
```

```python
import numpy as np
import concourse.bass as bass
import concourse.mybir as mybir
from concourse.bass_utils import run_bass_kernel_spmd

F32 = mybir.dt.float32
BF16 = mybir.dt.bfloat16
AF = mybir.ActivationFunctionType
ALU = mybir.AluOpType
AX = mybir.AxisListType


class Buf:
    __slots__ = ("name", "w", "r")

    def __init__(self, name):
        self.name = name
        self.w = {}
        self.r = {}


class Sched:
    def __init__(self, nc, ctx):
        self.nc = nc
        self.ctx = ctx
        self.eng = {"pe": nc.tensor, "act": nc.scalar, "dve": nc.vector, "pool": nc.gpsimd, "sp": nc.sync}
        self.sem = {}
        self.cnt = {}
        for k in self.eng:
            self.sem[k] = ctx.enter_context(nc.semaphore("s_" + k))
            self.cnt[k] = 0
        self.waited = {k: {} for k in self.eng}
        self.ndma = 0
        self.nbuf = 0

    def buf(self, name=None):
        self.nbuf += 1
        return Buf(name or f"b{self.nbuf}")

    def bufs(self, n, name="b"):
        return [self.buf(f"{name}{i}") for i in range(n)]

    def dma_sem(self, name=None):
        self.ndma += 1
        key = name or f"dma{self.ndma}"
        self.sem[key] = self.ctx.enter_context(self.nc.semaphore("s_" + key))
        self.cnt[key] = 0
        return key

    def _deps(self, e, reads, writes):
        deps = {}
        for b in reads:
            for k, c in b.w.items():
                if deps.get(k, 0) < c:
                    deps[k] = c
        for b in writes:
            for k, c in b.w.items():
                if deps.get(k, 0) < c:
                    deps[k] = c
            for k, c in b.r.items():
                if deps.get(k, 0) < c:
                    deps[k] = c
        eng = self.eng[e]
        for k, c in deps.items():
            if k == e and e == "pe":
                continue
            if self.waited[e].get(k, 0) < c:
                eng.wait_ge(self.sem[k], c)
                self.waited[e][k] = c

    def _record(self, key, c, reads, writes):
        for b in writes:
            b.w = {key: c}
            b.r = {}
        for b in reads:
            if b.r.get(key, 0) < c:
                b.r[key] = c

    def op(self, e, fn, reads=(), writes=(), inc=True):
        self._deps(e, reads, writes)
        ins = fn(self.eng[e])
        if inc:
            self.cnt[e] += 1
            ins.then_inc(self.sem[e], 1)
            self._record(e, self.cnt[e], reads, writes)
        else:
            self._record(e, self.cnt[e] + 1, reads, writes)
        return ins

    def seal(self, key, bufs):
        c = self.cnt[key]
        for b in bufs:
            if key in b.w:
                b.w[key] = c

    def handoff(self, news, olds):
        w = {}
        r = {}
        for ob in olds:
            for k2, c2 in ob.w.items():
                if w.get(k2, 0) < c2:
                    w[k2] = c2
            for k2, c2 in ob.r.items():
                if r.get(k2, 0) < c2:
                    r[k2] = c2
        for nb in news:
            nb.w = dict(w)
            nb.r = dict(r)

    def dma(self, q, dsem, out, in_, reads=(), writes=(), **kw):
        self._deps(q, reads, writes)
        ins = self.eng[q].dma_start(out=out, in_=in_, **kw)
        self.cnt[dsem] += 16
        ins.then_inc(self.sem[dsem], 16)
        self._record(dsem, self.cnt[dsem], reads, writes)
        return ins

    def wait_all(self, e, bufs):
        self._deps(e, bufs, ())


import numpy as np
from contextlib import ExitStack

NT = 2048
NTT = 16
D = 1024
DIN = 2568
DFF = 4096
EPS = 1e-6
KB = 1024


class KB_:
    pass


def build(cfg):
    nc = bass.Bass("TRN2", target_bir_lowering=False)
    k = KB_()
    k.nc = nc
    k.cfg = cfg
    L = cfg.get("nlayers", 1)
    dr = {}

    def din(name, shape, dt=F32):
        dr[name] = nc.dram_tensor(name, list(shape), dt, kind="ExternalInput").ap()
        return dr[name]

    def dout(name, shape, dt=F32):
        dr[name] = nc.dram_tensor(name, list(shape), dt, kind="ExternalOutput").ap()
        return dr[name]

    din("xin", [NT, D]); din("xhalo", [3, D])
    mode = cfg.get("mode", "B")
    din("norm_mix_g", [L, D]); din("w_in", [L, D, DIN])
    if mode != "A":
        din("w_out", [L, D, D])
        din("norm_ffn_g", [L, D]); din("w_ff1", [L, D, DFF]); din("w_ff2", [L, DFF, D])
        din("final_norm_g", [D])
    din("ml_conv_w", [L, 4, 1024]); din("ml_conv_b", [L, 1024])
    din("ident", [128, 128]); din("causal", [128, 128]); din("ones", [128, 128])
    din("ml_b_i", [L, 4]); din("ml_b_f", [L, 4]); din("ml_norm_g", [L, 512])
    mode = cfg.get("mode", "B")
    din("pred", [128, 8])
    if mode == "B":
        din("summ_all", [8, 128, 556])
    if mode == "A":
        dout("summ", [128, 556])
    din("s5_a_re", [L, 32, 64]); din("s5_a_im", [L, 32, 64]); din("s5_log_dt", [L, 32])
    din("s5_b_re", [L, 32, 64, 16]); din("s5_b_im", [L, 32, 64, 16]); din("s5_c_re", [L, 32, 16, 64]); din("s5_c_im", [L, 32, 16, 64])
    din("s5_d", [L, 32, 16]); din("s5_b_glu", [L, 512]); din("s5_out_g", [L, 512])
    if mode != "A":
        din("s5_w_glu", [L, 512, 512])
    din("par01", [128, 2]); din("bdmask", [128, 128])
    if mode != "A":
        dout("xout", [NT, D])
    if cfg.get("dbg_ht2"):
        dout("dbg_ht", [128, 8 * 2051], BF16)
    if cfg.get("debug"):
        dout("dbg_u", [128, 4 * 2048], BF16)
        dout("dbg_qk", [128, 8 * 2048], BF16)
        dout("dbg_v", [128, 16 * 4 * 129], BF16)
        dout("dbg_o", [128, 16 * 512], BF16)
        dout("dbg_if", [128, 128])
        dout("dbg_yc", [128, 8 * 2048], BF16)
        dout("dbg_g", [128, 16 * 64])
        dout("dbg_zl", [128, 4096]); dout("dbg_zs", [128, 16 * 2 * 129], BF16); dout("dbg_zg", [128, 8192], BF16)
        dout("dbg_sc", [128, 1216]); dout("dbg_cp", [128, 17408], BF16); dout("dbg_bd", [128, 8192], BF16); dout("dbg_wz", [128, 16384], BF16)
        dout("dbg_zend", [128, 32])
    k.dr = dr

    with ExitStack() as ctx:
        S = Sched(nc, ctx)
        k.S = S
        ctx.enter_context(nc.allow_non_contiguous_dma(reason="small param loads"))
        ctx.enter_context(nc.allow_low_precision(reason="bf16 matmul operands"))
        A0_B, A1_B, A2_B = 64 * KB, 33 * KB + 256, 79 * KB
        arena = ctx.enter_context(nc.sbuf_tensor("arena", [128, (A0_B + A1_B + A2_B) // 4], F32))
        ring = ctx.enter_context(nc.sbuf_tensor("ring", [128, 3 * 4096], BF16))
        cst = ctx.enter_context(nc.sbuf_tensor("cst", [128, 1024], F32))
        banks = [ctx.enter_context(nc.psum_tensor(f"ps{i}", [128, 512], F32)) for i in range(8)]
        bank_bufs = S.bufs(8, "bank")
        k.bank_i = 0
        block = ctx.enter_context(nc.Block())

        def view(base, off, nbytes, dt):
            assert off % 4 == 0 and nbytes % 4 == 0
            a = arena[:, (base + off) // 4:(base + off + nbytes) // 4]
            return a if dt == F32 else a.bitcast(dt)
        A0, A1, A2 = 0, A0_B, A0_B + A1_B

        def next_bank():
            i = k.bank_i
            k.bank_i = (i + 1) % 8
            return banks[i], bank_bufs[i]

        ident_f = cst[:, 0:128]
        ident_b = cst[:, 128:192].bitcast(BF16)
        b_cst = S.buf("cst")
        dq = S.dma_sem("dq_misc")
        S.dma("sp", dq, ident_f, dr["ident"], writes=[b_cst])
        tri_f = cst[:, 384:512]
        ones_f = cst[:, 512:640]
        tri_b = cst[:, 192:256].bitcast(BF16)
        S.dma("sp", dq, tri_f, dr["causal"], writes=[b_cst])
        S.dma("sp", dq, ones_f, dr["ones"], writes=[b_cst])
        par01 = cst[:, 752:754]
        bdm = cst[:, 768:896]
        ones_b = cst[:, 896:960].bitcast(BF16)
        S.dma("sp", dq, par01, dr["par01"], writes=[b_cst])
        pred = cst[:, 972:980]
        pmask = cst[:, 980:988]
        S.dma("sp", dq, pred, dr["pred"], writes=[b_cst])
        S.dma("sp", dq, bdm, dr["bdmask"], writes=[b_cst])
        S.seal(dq, [b_cst])
        S.op("dve", lambda e: e.tensor_copy(out=ones_b, in_=ones_f), reads=[b_cst], writes=[b_cst])
        S.op("dve", lambda e: e.tensor_scalar(out=pmask, in0=pred, scalar1=1e6, scalar2=-1e6, op0=ALU.mult, op1=ALU.add), reads=[b_cst], writes=[b_cst])
        S.op("dve", lambda e: e.tensor_copy(out=ident_b, in_=ident_f), reads=[b_cst], writes=[b_cst])
        S.op("dve", lambda e: e.tensor_copy(out=tri_b, in_=tri_f), reads=[b_cst], writes=[b_cst])

        X = view(A0, 0, 64 * KB, F32).rearrange("p (t d) -> p t d", t=NTT)
        X_b = S.bufs(NTT, "X")
        HT = view(A1, 0, 8 * 2051 * 2 + 0, BF16) if False else view(A1, 0, 32832, BF16)[:, 0:8 * 2051].rearrange("p (k t) -> p k t", k=8)
        HT_b = S.bufs(NTT + 1, "HT")
        YC = view(A1, 0, 32 * KB, BF16).rearrange("p (k t) -> p k t", k=8)
        QK = view(A0, 0, 32 * KB, BF16).rearrange("p (f t) -> p f t", f=8)
        OG = view(A0, 32 * KB, 16 * KB, BF16).rearrange("p (t d) -> p t d", t=NTT)
        US = view(A0, 48 * KB, 16 * KB, BF16).rearrange("p (q i c) -> p q i c", q=4, i=16)
        VA = view(A2, 0, 16512, BF16).rearrange("p (t h v) -> p t h v", t=NTT, h=4)
        GR = view(A2, 16512, 512, F32)
        SCR = A2 + 17024

        xq = [S.dma_sem(f"xq{i}") for i in range(16)]
        hq = S.dma_sem("hq"); gq = S.dma_sem("gq")
        oq = S.dma_sem("oq")
        wq = [S.dma_sem(f"wq{i}") for i in range(3)]
        ring_b = S.bufs(3, "ring")
        k.wi = 0

        def wload(src_ap, nk, ncols):
            i = k.wi % 3
            k.wi += 1
            v = ring[:, i * 4096: i * 4096 + nk * ncols].rearrange("p (k n) -> p k n", k=nk)
            S.dma("pool", wq[i], v, src_ap, writes=[ring_b[i]])
            return v, ring_b[i]

        class WStream:
            def __init__(self, chunks):
                self.chunks = chunks
                self.loaded = []

            def get(self, i, ahead=2):
                while len(self.loaded) < min(len(self.chunks), i + 1 + ahead):
                    self.loaded.append(wload(*self.chunks[len(self.loaded)]))
                return self.loaded[i]

        k.pq = None

        def load_pvec(dst, src_1d, b, q="sp"):
            S.dma(q, k.pq, dst, src_1d.rearrange("(k p) -> p k", p=128), writes=[b])

        tmp_b = S.bufs(4, "tmp")

        def layer(l, xin_ap, xh_ap, xout_ap, last):
            prm = cst[:, 256:256 + 64]
            b_prm = S.buf("prm")
            pq_l = S.dma_sem(f"pq{l}"); k.pq = pq_l
            g1 = cst[:, 640:648]; g2 = cst[:, 648:656]
            cw = cst[:, 656:688].rearrange("p (j f) -> p j f", j=4)
            cb = cst[:, 688:696]
            load_pvec(g1, dr["norm_mix_g"][l], b_prm)
            if mode != "A":
                load_pvec(g2, dr["norm_ffn_g"][l], b_prm)
            for j in range(4):
                load_pvec(cw[:, j, :], dr["ml_conv_w"][l, j], b_prm)
            load_pvec(cb, dr["ml_conv_b"][l], b_prm)
            mlg = cst[:, 740:744]
            load_pvec(mlg, dr["ml_norm_g"][l], b_prm)
            bif = cst[:, 744:752]
            bi_ = dr["ml_b_i"][l]; bf_ = dr["ml_b_f"][l]
            S.dma("sp", pq_l, bif[:, 0:4], bass.AP(bi_.tensor, bi_.offset, [[0, 128], [1, 4]]), writes=[b_prm])
            S.dma("sp", pq_l, bif[:, 4:8], bass.AP(bf_.tensor, bf_.offset, [[0, 128], [1, 4]]), writes=[b_prm])
            dcol = cst[:, 960:964]; bglu = cst[:, 964:968]; outg = cst[:, 968:972]
            load_pvec(dcol, dr["s5_d"][l].rearrange("g p -> (g p)"), b_prm)
            load_pvec(bglu, dr["s5_b_glu"][l], b_prm)
            load_pvec(outg, dr["s5_out_g"][l], b_prm)
            S.seal(pq_l, [b_prm])

            ssq = cst[:, 700:717]
            rstd = cst[:, 720:737]
            b_st = S.bufs(17, "st")
            junk = view(SCR, 0, 2048, BF16)
            b_junk = S.buf("junk")
            xnb = [view(SCR, 2048 + i * 2048, 2048, BF16) for i in range(2)]
            b_xnb = S.bufs(2, "xnb")
            xh_t = view(SCR, 6144, 4096, F32)
            b_xh = S.buf("xh")

            def rms_stats(src, np_, col, bsrc):
                S.op("act", lambda e: e.activation(out=junk[:np_], in_=src, func=AF.Square, accum_out=ssq[:np_, col:col + 1]),
                     reads=[bsrc], writes=[b_junk, b_st[col]])
                S.op("dve", lambda e: e.tensor_scalar(out=rstd[:np_, col:col + 1], in0=ssq[:np_, col:col + 1], scalar1=1.0 / D, scalar2=EPS,
                                                      op0=ALU.mult, op1=ALU.add), reads=[b_st[col]], writes=[b_st[col]])
                S.op("act", lambda e: e.activation(out=rstd[:np_, col:col + 1], in_=rstd[:np_, col:col + 1], func=AF.Sqrt),
                     reads=[b_st[col]], writes=[b_st[col]])
                S.op("dve", lambda e: e.reciprocal(out=rstd[:np_, col:col + 1], in_=rstd[:np_, col:col + 1]), reads=[b_st[col]], writes=[b_st[col]])

            def norm_to_hT(gvec, with_halo, from_dram):
                gb = bass.AP(gvec.tensor, gvec.offset, [list(gvec.ap[0]), list(gvec.ap[1]), [0, 128]])
                for tt in range(NTT):
                    if from_dram:
                        S.dma("sp", xq[tt], X[:, tt, :], xin_ap[tt * 128:(tt + 1) * 128, :], writes=[X_b[tt]])
                for tt in range(NTT + (1 if with_halo else 0)):
                    halo = tt == NTT
                    np_ = 3 if halo else 128
                    if halo:
                        S.dma("sp", hq, xh_t[:3, :], xh_ap, writes=[b_xh])
                        src, bsrc = xh_t[:3, :], b_xh
                    else:
                        src, bsrc = X[:, tt, :], X_b[tt]
                    rms_stats(src, np_, tt, bsrc)
                    xb = xnb[tt % 2]; bx = b_xnb[tt % 2]
                    S.op("act", lambda e: e.activation(out=xb[:np_], in_=src, func=AF.Copy, scale=rstd[:np_, tt:tt + 1]),
                         reads=[bsrc, b_st[tt]], writes=[bx])
                    bank, bb = next_bank()
                    pb = bank[:, 0:512].bitcast(BF16).rearrange("p (k t) -> p k t", k=8)
                    for kk in range(8):
                        S.op("pe", lambda e: e.transpose(pb[:, kk, 0:np_], xb[:np_, kk * 128:(kk + 1) * 128], ident_b[:np_, :np_]),
                             inc=(kk == 7), reads=[bx, b_cst], writes=[bb])
                    c0 = 0 if halo else 3 + tt * 128
                    S.op("dve", lambda e: e.tensor_tensor(out=HT[:, :, c0:c0 + np_], in0=pb[:, :, 0:np_], in1=gb[:, :, 0:np_], op=ALU.mult),
                         reads=[bb, b_prm], writes=[HT_b[tt]])

            norm_to_hT(g1, True, True)

            win = dr["w_in"][l].rearrange("(k p) n -> p k n", p=128)
            chunks = [(win[:, :, c * 512:(c + 1) * 512], 8, 512) for c in range(5)] + [(win[:, :, 2560:2568], 8, 8)]
            ws = WStream(chunks)
            stage = [view(SCR, 10240 + i * 8448, 8448, F32) for i in range(2)]
            b_stage = S.bufs(2, "stage")
            acc = view(SCR, 10240 + 2 * 8448, 8192, F32)
            b_acc = S.buf("acc")
            b_us = S.bufs(4, "us")
            b_qk = S.bufs(8, "qk")
            b_va = S.bufs(NTT, "va")
            b_og = S.bufs(NTT, "og")
            b_gr = S.buf("gr")
            allHT = HT_b
            S.op("pool", lambda e: e.memset(VA[:, :, :, 128:129], 1.0), writes=b_va)
            for ci in range(3):
                wc, bw = ws.get(ci)
                for ft in range(4):
                    f = (ci - 1) * 4 + ft
                    if ci > 0:
                        st = stage[f % 2]; bs = b_stage[f % 2]
                        bank, bb = next_bank()
                        for kk in range(8):
                            S.op("pe", lambda e: e.matmul(bank[:, 0:3], lhsT=wc[:, kk, ft * 128:(ft + 1) * 128], rhs=HT[:, kk, 0:3],
                                                          start=(kk == 0), stop=(kk == 7)), inc=(kk == 7), reads=[bw, HT_b[NTT]], writes=[bb])
                        S.op("act", lambda e: e.activation(out=st[:, 0:3], in_=bank[:, 0:3], func=AF.Copy), reads=[bb], writes=[bs])
                    for nb in range(4):
                        bank, bb = next_bank()
                        for kk in range(8):
                            S.op("pe", lambda e: e.matmul(bank[:, :], lhsT=wc[:, kk, ft * 128:(ft + 1) * 128],
                                                          rhs=HT[:, kk, 3 + nb * 512:3 + (nb + 1) * 512], start=(kk == 0), stop=(kk == 7)),
                                 inc=(kk == 7), reads=[bw] + allHT[nb * 4:nb * 4 + 4], writes=[bb])
                        if ci == 0:
                            dst = US[:, ft, :, nb * 32:(nb + 1) * 32]
                            src = bank[:, :].rearrange("p (c i) -> p i c", i=16)
                            S.op("act", lambda e: e.activation(out=dst, in_=src, func=AF.Copy), reads=[bb], writes=[b_us[ft]])
                        else:
                            S.op("act", lambda e: e.activation(out=st[:, 3 + nb * 512:3 + (nb + 1) * 512], in_=bank[:, :], func=AF.Copy),
                                 reads=[bb], writes=[bs])
                    if ci > 0:
                        S.op("dve", lambda e: e.tensor_scalar(out=acc, in0=st[:, 0:2048], scalar1=cw[:, 0, f:f + 1], scalar2=None, op0=ALU.mult),
                             reads=[bs, b_prm], writes=[b_acc])
                        for j in range(1, 4):
                            S.op("dve", lambda e: e.scalar_tensor_tensor(out=acc, in0=st[:, j:j + 2048], scalar=cw[:, j, f:f + 1], in1=acc,
                                                                         op0=ALU.mult, op1=ALU.add), reads=[bs, b_prm, b_acc], writes=[b_acc])
                        S.op("act", lambda e: e.activation(out=QK[:, f, :], in_=acc, func=AF.Silu, bias=cb[:, f:f + 1]),
                             reads=[b_acc, b_prm], writes=[b_qk[f]])
            for ci in (3, 4):
                wc, bw = ws.get(ci)
                for tt in range(NTT):
                    bank, bb = next_bank()
                    for kk in range(8):
                        S.op("pe", lambda e: e.matmul(bank[:, :], lhsT=HT[:, kk, 3 + tt * 128:3 + (tt + 1) * 128], rhs=wc[:, kk, :],
                                                      start=(kk == 0), stop=(kk == 7)), inc=(kk == 7), reads=[bw, HT_b[tt]], writes=[bb])
                    if ci == 3:
                        S.op("act", lambda e: e.activation(out=VA[:, tt, :, 0:128], in_=bank[:, :].rearrange("p (h v) -> p h v", h=4), func=AF.Copy),
                             reads=[bb], writes=[b_va[tt]])
                    else:
                        S.op("act", lambda e: e.activation(out=OG[:, tt, :], in_=bank[:, :], func=AF.Sigmoid), reads=[bb], writes=[b_og[tt]])
            wc, bw = ws.get(5)
            bank, bb = next_bank()
            for tt in range(NTT):
                for kk in range(8):
                    S.op("pe", lambda e: e.matmul(bank[:, tt * 8:(tt + 1) * 8], lhsT=HT[:, kk, 3 + tt * 128:3 + (tt + 1) * 128], rhs=wc[:, kk, :],
                                                  start=(kk == 0), stop=(kk == 7)), inc=(kk == 7), reads=[bw, HT_b[tt]], writes=[bb])
            S.op("dve", lambda e: e.tensor_copy(out=GR, in_=bank[:, 0:128]), reads=[bb], writes=[b_gr])

            if cfg.get("debug"):
                dbq = S.dma_sem("dbq")
                b_dbg = S.buf("dbg")
                S.dma("sp", dbq, dr["dbg_u"], view(A0, 48 * KB, 16 * KB, BF16), reads=b_us, writes=[b_dbg])
                S.dma("sp", dbq, dr["dbg_qk"], view(A0, 0, 32 * KB, BF16), reads=b_qk, writes=[b_dbg])
                S.dma("sp", dbq, dr["dbg_v"], view(A2, 0, 16512, BF16), reads=b_va, writes=[b_dbg])
                S.dma("sp", dbq, dr["dbg_o"], view(A0, 32 * KB, 16 * KB, BF16), reads=b_og, writes=[b_dbg])
                S.dma("sp", dbq, dr["dbg_if"], GR, reads=[b_gr], writes=[b_dbg])
                pass

            b_yc = S.bufs(NTT, "yc")
            S.handoff(b_yc, HT_b)

            MS = SCR
            def garr(i):
                return view(MS, i * 256, 256, F32).rearrange("p (m h) -> p m h", h=4)
            nlf, ig, nF, g, Gm, R0, Rr, w0, wv, clamp, dl, ep, incl, t1 = [garr(i) for i in range(14)]
            sm = view(MS, 14 * 256, 256, F32)
            gcol = sm[:, 0:1]; dm = sm[:, 4:8]; rec = sm[:, 8:12]; ss = sm[:, 12:16]; rs4 = sm[:, 16:20]
            Gb = view(MS, 15 * 256, 512, F32)
            b_g = S.buf("gates")
            b_sm = S.buf("sm")
            KT = view(MS, 4608, 16512, BF16)
            kTok = KT[:, 0:16 * 512].rearrange("p (m d) -> p m d", m=16)
            Cin = KT[:, 0:64 * 129].rearrange("p (c v) -> p c v", c=64)
            b_kt = S.bufs(16, "kt")
            CL = view(MS, 4608 + 16512, 64 * 129 * 4, F32).rearrange("p (c v) -> p c v", c=64)
            b_cl = S.bufs(16, "cl")
            T0 = MS + 4608 + 16512 + 64 * 129 * 4
            Ct = view(T0, 0, 2064, F32).rearrange("p (h v) -> p h v", h=4)
            tmpC = view(T0, 2064, 2064, F32).rearrange("p (h v) -> p h v", h=4)
            b_ct = S.buf("ct"); b_tc = S.buf("tmpc")
            Vw = [view(T0, 4128 + i * 1032, 1032, BF16).rearrange("p (h v) -> p h v", h=4) for i in range(2)]
            b_vw = S.bufs(2, "vw")
            PT = [view(T0, 6192 + i * 256, 256, BF16) for i in range(2)]
            b_pt = S.bufs(2, "pt")
            hbuf = [view(T0, 6704 + i * 512, 512, F32) for i in range(4)]
            b_hb = S.bufs(4, "hb")
            hn = [view(T0, 8752 + i * 256, 256, BF16) for i in range(2)]
            b_hn = S.bufs(2, "hn")
            junk2 = view(T0, 9264, 256, BF16)
            assert T0 + 9520 <= A2 + A2_B, (T0 + 9520 - A2 - A2_B)

            handoff = S.handoff
            handoff([b_g, b_sm] + b_kt + b_cl + [b_ct, b_tc] + b_vw + b_pt + b_hb + b_hn, b_stage + [b_acc, b_junk, b_xh] + b_xnb)
            GRv = GR.rearrange("p (m c) -> p m c", c=8)

            def bc_m(ap4):
                return bass.AP(ap4.tensor, ap4.offset, [list(ap4.ap[0]), [0, 16], list(ap4.ap[1])])

            def bc_last(ap, n):
                return bass.AP(ap.tensor, ap.offset, [list(x) for x in ap.ap] + [[0, n]])

            def flat(a):
                return a.rearrange("p m h -> p (m h)")
            LN_S = -0.5 * float(np.log(128.0))
            S.op("dve", lambda e: e.tensor_tensor(out=ig, in0=GRv[:, :, 0:4], in1=bc_m(bif[:, 0:4]), op=ALU.add), reads=[b_gr, b_prm], writes=[b_g])
            S.op("dve", lambda e: e.tensor_tensor(out=t1, in0=GRv[:, :, 4:8], in1=bc_m(bif[:, 4:8]), op=ALU.add), reads=[b_gr, b_prm], writes=[b_g])
            S.op("act", lambda e: e.activation(out=t1, in_=t1, func=AF.Exp, scale=-1.0), reads=[b_g], writes=[b_g])
            S.op("act", lambda e: e.activation(out=nlf, in_=t1, func=AF.Ln, bias=1.0), reads=[b_g], writes=[b_g])
            bankA, bbA = next_bank()
            S.op("pe", lambda e: e.matmul(bankA[:, 0:64], lhsT=tri_f, rhs=flat(nlf), start=True, stop=True), reads=[b_g, b_cst], writes=[bbA])
            bankB, bbB = next_bank()
            S.op("pe", lambda e: e.matmul(bankB[:, 0:64], lhsT=ones_f, rhs=flat(nlf), start=True, stop=True), reads=[b_g, b_cst], writes=[bbB])
            bA = bankA[:, 0:64].rearrange("p (m h) -> p m h", h=4)
            bB = bankB[:, 0:64].rearrange("p (m h) -> p m h", h=4)
            for h in range(4):
                S.op("dve", lambda e: e.tensor_tensor_scan(out=incl[:, :, h], data0=ones_f[:, 0:16], data1=bB[:, :, h], initial=0.0,
                                                           op0=ALU.mult, op1=ALU.add), reads=[bbB, b_cst, b_g], writes=[b_g])
            S.op("dve", lambda e: e.tensor_tensor(out=t1, in0=incl, in1=bB, op=ALU.subtract), reads=[b_g, bbB], writes=[b_g])
            S.op("dve", lambda e: e.tensor_tensor(out=nF, in0=t1, in1=bA, op=ALU.add), reads=[b_g, bbA], writes=[b_g])
            S.op("dve", lambda e: e.tensor_tensor(out=g, in0=ig, in1=nF, op=ALU.add), reads=[b_g], writes=[b_g])
            bankT, bbT = next_bank()
            S.op("pe", lambda e: e.transpose(bankT[0:64, 0:128], flat(g), ident_f), reads=[b_g, b_cst], writes=[bbT])
            S.op("dve", lambda e: e.tensor_reduce(out=gcol[0:64, :], in_=bankT[0:64, 0:128], axis=AX.X, op=ALU.max), reads=[bbT], writes=[b_sm])
            S.op("dve", lambda e: e.tensor_copy(out=Gb[0:64, :], in_=bc_last(gcol[0:64, 0:1], 128)[:, 0, :]), reads=[b_sm], writes=[b_sm])
            bankG, bbG = next_bank()
            S.op("pe", lambda e: e.matmul(bankG[:, 0:64], lhsT=Gb[0:64, :], rhs=ident_f[0:64, 0:64], start=True, stop=True), reads=[b_sm, b_cst], writes=[bbG])
            S.op("dve", lambda e: e.tensor_copy(out=flat(Gm), in_=bankG[:, 0:64]), reads=[bbG], writes=[b_g])
            for h in range(4):
                S.op("dve", lambda e: e.tensor_tensor_scan(out=R0[:, :, h], data0=Gm[:, :, h], data1=Gm[:, :, h], initial=-1e30,
                                                           op0=ALU.max, op1=ALU.max), reads=[b_g], writes=[b_g])
            S.op("dve", lambda e: e.tensor_tensor(out=t1, in0=g, in1=R0, op=ALU.subtract), reads=[b_g], writes=[b_g])
            S.op("act", lambda e: e.activation(out=w0, in_=t1, func=AF.Exp, bias=LN_S), reads=[b_g], writes=[b_g])
            S.op("dve", lambda e: e.tensor_tensor(out=t1[:, 1:16, :], in0=R0[:, 0:15, :], in1=R0[:, 1:16, :], op=ALU.subtract), reads=[b_g], writes=[b_g])
            S.op("act", lambda e: e.activation(out=dl[:, 1:16, :], in_=t1[:, 1:16, :], func=AF.Exp), reads=[b_g], writes=[b_g])
            for m in range(16):
                cs = slice(m * 128, (m + 1) * 128)
                bank, bb = next_bank()
                pb = bank[:, 0:256].bitcast(BF16)
                for h in range(4):
                    S.op("pe", lambda e: e.transpose(pb[:, h * 128:(h + 1) * 128], QK[:, 4 + h, cs], ident_b), inc=(h == 3),
                         reads=[b_qk[4 + h], b_cst], writes=[bb])
                S.op("act", lambda e: e.activation(out=kTok[:, m, :], in_=pb, func=AF.Copy), reads=[bb], writes=[b_kt[m]])
                vw = Vw[m % 2]; bv = b_vw[m % 2]
                S.op("dve", lambda e: e.tensor_tensor(out=vw, in0=VA[:, m, :, :], in1=bc_last(w0[:, m, :], 129), op=ALU.mult),
                     reads=[b_va[m], b_g], writes=[bv])
                for h in range(4):
                    bank, bb = next_bank()
                    S.op("pe", lambda e: e.matmul(bank[:, 0:129], lhsT=kTok[:, m, h * 128:(h + 1) * 128], rhs=vw[:, h, :], start=True, stop=True),
                         reads=[b_kt[m], bv], writes=[bb])
                    if h % 2 == 0:
                        S.op("act", lambda e: e.activation(out=CL[:, m * 4 + h, :], in_=bank[:, 0:129], func=AF.Copy), reads=[bb], writes=[b_cl[m]])
                    else:
                        S.op("dve", lambda e: e.tensor_copy(out=CL[:, m * 4 + h, :], in_=bank[:, 0:129]), reads=[bb], writes=[b_cl[m]])
                if m == 0:
                    S.op("dve", lambda e: e.tensor_copy(out=Ct, in_=CL[:, 0:4, :]), reads=[b_cl[0]], writes=[b_ct])
                else:
                    S.op("dve", lambda e: e.tensor_tensor(out=Ct, in0=Ct, in1=bc_last(dl[:, m, :], 129), op=ALU.mult), reads=[b_ct, b_g], writes=[b_ct])
                    S.op("dve", lambda e: e.tensor_tensor(out=Ct, in0=Ct, in1=CL[:, m * 4:m * 4 + 4, :], op=ALU.add), reads=[b_ct, b_cl[m]], writes=[b_ct])
            mst = sm[:, 20:24]
            exq = S.dma_sem(f"exq{l}")
            b_out = S.buf("out")
            if mode == "A":
                S.dma("sp", oq, dr["summ"][:, 32:548], Ct.rearrange("p h v -> p (h v)"), reads=[b_ct], writes=[b_out])
                S.dma("sp", oq, dr["summ"][:, 548:552], R0[:, 15, :], reads=[b_g], writes=[b_out])
                S.dma("sp", oq, dr["summ"][:, 552:556], incl[:, 15, :], reads=[b_g], writes=[b_out])
            else:
                sa = dr["summ_all"]
                small = Gb.rearrange("p (j c) -> p j c", j=8)[:, :, 0:8]
                S.dma("sp", exq, small, sa[:, :, 548:556].rearrange("j p c -> p j c"), writes=[b_sm])
                S.seal(exq, [b_sm])
                cm = sm[:, 20:24]; mx = sm[:, 24:28]; ta = sm[:, 28:32]; tb = sm[:, 32:36]; r0j = sm[:, 36:40]; nfj = sm[:, 40:44]; tq = sm[:, 44:48]
                S.op("dve", lambda e: e.memset(tmpC, 0.0), reads=[b_tc], writes=[b_tc])
                S.op("dve", lambda e: e.memset(cm, 0.0), reads=[b_sm], writes=[b_sm])
                cq = [S.dma_sem(f"cq{l}_{i}") for i in range(2)]
                Cj = [Ct, view(T0, 4128, 2064, F32).rearrange("p (h v) -> p h v", h=4)]
                b_cj = [b_ct, S.buf("cj1")]
                S.handoff([b_cj[1]], b_vw)
                for j in range(8):
                    cj = Cj[j % 2]; bcj = b_cj[j % 2]
                    S.dma("sp", cq[j % 2], cj.rearrange("p h v -> p (h v)"), sa[j, :, 32:548], writes=[bcj])
                    S.op("dve", lambda e: e.tensor_scalar(out=r0j, in0=small[:, j, 0:4], scalar1=pred[:, j:j + 1], scalar2=pmask[:, j:j + 1],
                                                          op0=ALU.mult, op1=ALU.add), reads=[b_sm, b_cst], writes=[b_sm])
                    S.op("dve", lambda e: e.tensor_scalar(out=nfj, in0=small[:, j, 4:8], scalar1=pred[:, j:j + 1], scalar2=None, op0=ALU.mult),
                         reads=[b_sm, b_cst], writes=[b_sm])
                    S.op("dve", lambda e: e.tensor_tensor(out=mx, in0=cm, in1=r0j, op=ALU.max), reads=[b_sm], writes=[b_sm])
                    S.op("dve", lambda e: e.tensor_tensor(out=tq, in0=cm, in1=mx, op=ALU.subtract), reads=[b_sm], writes=[b_sm])
                    S.op("act", lambda e: e.activation(out=ta, in_=tq, func=AF.Exp), reads=[b_sm], writes=[b_sm])
                    S.op("dve", lambda e: e.tensor_tensor(out=tq, in0=r0j, in1=mx, op=ALU.subtract), reads=[b_sm], writes=[b_sm])
                    S.op("act", lambda e: e.activation(out=tb, in_=tq, func=AF.Exp), reads=[b_sm], writes=[b_sm])
                    S.op("dve", lambda e: e.tensor_tensor(out=tmpC, in0=tmpC, in1=bc_last(ta, 129), op=ALU.mult), reads=[b_tc, b_sm], writes=[b_tc])
                    S.op("dve", lambda e: e.tensor_tensor(out=cj, in0=cj, in1=bc_last(tb, 129), op=ALU.mult), reads=[bcj, b_sm], writes=[bcj])
                    S.op("dve", lambda e: e.tensor_tensor(out=tmpC, in0=tmpC, in1=cj, op=ALU.add), reads=[b_tc, bcj], writes=[b_tc])
                    S.op("dve", lambda e: e.tensor_tensor(out=cm, in0=mx, in1=nfj, op=ALU.subtract), reads=[b_sm], writes=[b_sm])
            if mode != "A":
                S.op("dve", lambda e: e.tensor_tensor(out=Rr, in0=R0, in1=bc_m(mst), op=ALU.max), reads=[b_g, b_sm], writes=[b_g])
                S.op("dve", lambda e: e.tensor_tensor(out=t1, in0=g, in1=Rr, op=ALU.subtract), reads=[b_g], writes=[b_g])
                S.op("act", lambda e: e.activation(out=wv, in_=t1, func=AF.Exp, bias=LN_S), reads=[b_g], writes=[b_g])
                S.op("dve", lambda e: e.tensor_tensor(out=t1, in0=nF, in1=Rr, op=ALU.subtract), reads=[b_g], writes=[b_g])
                S.op("act", lambda e: e.activation(out=clamp, in_=t1, func=AF.Exp), reads=[b_g], writes=[b_g])
                S.op("dve", lambda e: e.tensor_tensor(out=t1, in0=R0, in1=Rr, op=ALU.subtract), reads=[b_g], writes=[b_g])
                S.op("act", lambda e: e.activation(out=ep, in_=t1, func=AF.Exp), reads=[b_g], writes=[b_g])
                S.op("dve", lambda e: e.tensor_tensor(out=t1[:, 1:16, :], in0=Rr[:, 0:15, :], in1=Rr[:, 1:16, :], op=ALU.subtract), reads=[b_g], writes=[b_g])
                S.op("dve", lambda e: e.tensor_tensor(out=t1[:, 0, :], in0=mst, in1=Rr[:, 0, :], op=ALU.subtract), reads=[b_g, b_sm], writes=[b_g])
                S.op("act", lambda e: e.activation(out=dl, in_=t1, func=AF.Exp), reads=[b_g], writes=[b_g])
                S.op("dve", lambda e: e.tensor_copy(out=Ct, in_=tmpC), reads=[b_tc], writes=[b_ct])
                b_cin = b_kt
                for m in range(16):
                    S.op("dve", lambda e: e.tensor_tensor(out=tmpC, in0=Ct, in1=bc_last(dl[:, m, :], 129), op=ALU.mult), reads=[b_ct, b_g], writes=[b_tc])
                    S.op("act", lambda e: e.activation(out=Cin[:, m * 4:m * 4 + 4, :], in_=tmpC, func=AF.Copy), reads=[b_tc], writes=b_cin)
                    S.op("dve", lambda e: e.tensor_tensor(out=Ct, in0=CL[:, m * 4:m * 4 + 4, :], in1=bc_last(ep[:, m, :], 129), op=ALU.mult),
                         reads=[b_cl[m], b_g], writes=[b_ct])
                    S.op("dve", lambda e: e.tensor_tensor(out=Ct, in0=Ct, in1=tmpC, op=ALU.add), reads=[b_ct, b_tc], writes=[b_ct])
                k.pti = 0
                for m in range(16):
                    cs = slice(m * 128, (m + 1) * 128)
                    for h in range(4):
                        bankS, bbS = next_bank()
                        S.op("pe", lambda e: e.matmul(bankS[:, 0:128], lhsT=QK[:, 4 + h, cs], rhs=QK[:, h, cs], start=True, stop=True),
                             reads=[b_qk[4 + h], b_qk[h]], writes=[bbS])
                        pi = k.pti % 2; k.pti += 1
                        S.op("dve", lambda e: e.scalar_tensor_tensor(out=PT[pi], in0=bankS[:, 0:128], scalar=wv[:, m, h:h + 1], in1=tri_f,
                                                                     op0=ALU.mult, op1=ALU.mult), reads=[bbS, b_g, b_cst], writes=[b_pt[pi]])
                        bankN, bbN = next_bank()
                        S.op("pe", lambda e: e.matmul(bankN[:, 0:129], lhsT=PT[pi], rhs=VA[:, m, h, :], start=True, stop=False), inc=False,
                             reads=[b_pt[pi], b_va[m]], writes=[bbN])
                        S.op("pe", lambda e: e.matmul(bankN[:, 0:129], lhsT=QK[:, h, cs], rhs=Cin[:, m * 4 + h, :], start=False, stop=True),
                             reads=[b_qk[h]] + b_cin, writes=[bbN])
                        S.op("act", lambda e: e.activation(out=dm[:, h:h + 1], in_=bankN[:, 128:129], func=AF.Abs), reads=[bbN], writes=[b_sm])
                        S.op("dve", lambda e: e.tensor_scalar(out=dm[:, h:h + 1], in0=dm[:, h:h + 1], scalar1=clamp[:, m, h:h + 1], scalar2=None,
                                                              op0=ALU.max), reads=[b_sm, b_g], writes=[b_sm])
                        S.op("dve", lambda e: e.reciprocal(out=rec[:, h:h + 1], in_=dm[:, h:h + 1]), reads=[b_sm], writes=[b_sm])
                        S.op("dve", lambda e: e.scalar_tensor_tensor(out=hbuf[h], in0=bankN[:, 0:128], scalar=rec[:, h:h + 1], in1=OG[:, m, h * 128:(h + 1) * 128],
                                                                     op0=ALU.mult, op1=ALU.mult), reads=[bbN, b_sm, b_og[m]], writes=[b_hb[h]])
                        S.op("act", lambda e: e.activation(out=junk2, in_=hbuf[h], func=AF.Square, accum_out=ss[:, h:h + 1]),
                             reads=[b_hb[h]], writes=[b_hn[0], b_sm])
                    S.op("dve", lambda e: e.tensor_scalar(out=rs4, in0=ss, scalar1=1.0 / 128, scalar2=EPS, op0=ALU.mult, op1=ALU.add), reads=[b_sm], writes=[b_sm])
                    S.op("act", lambda e: e.activation(out=rs4, in_=rs4, func=AF.Sqrt), reads=[b_sm], writes=[b_sm])
                    S.op("dve", lambda e: e.reciprocal(out=rs4, in_=rs4), reads=[b_sm], writes=[b_sm])
                    bankO, bbO = next_bank()
                    po = bankO[:, 0:256].bitcast(BF16).rearrange("p (h t) -> p h t", h=4)
                    for h in range(4):
                        hi = h % 2
                        S.op("dve", lambda e: e.tensor_scalar(out=hn[hi], in0=hbuf[h], scalar1=rs4[:, h:h + 1], scalar2=None, op0=ALU.mult),
                             reads=[b_hb[h], b_sm], writes=[b_hn[hi]])
                        S.op("pe", lambda e: e.transpose(po[:, h, :], hn[hi], ident_b), reads=[b_hn[hi], b_cst], writes=[bbO])
                    S.op("dve", lambda e: e.tensor_tensor(out=YC[:, 4:8, cs], in0=po, in1=bc_last(mlg, 128), op=ALU.mult), reads=[bbO, b_prm], writes=[b_yc[m]])

            if cfg.get("debug"):
                S.dma("sp", dbq, dr["dbg_g"].rearrange("p (a c) -> p a c", a=16), view(MS, 0, 4096, F32).rearrange("p (a c) -> p a c", a=16), reads=[b_g, b_sm], writes=[b_dbg])

            TCH = 16; NC_ = 128
            WZ = view(A0, 0, 32 * KB, BF16).rearrange("p (q i x n) -> p q i x n", q=4, i=16, x=2)
            BD = view(A0, 32 * KB, 16 * KB, BF16).rearrange("p (q j n) -> p q j n", q=4, j=16)
            CP = view(A2, 0, 34816, BF16).rearrange("p (j r x n) -> p j r x n", j=17, r=16, x=2)
            EC = view(A2, 34816, 8192, F32).rearrange("p (r c) -> p r c", r=16)
            ES = view(A2, 34816 + 8192, 8192, F32).rearrange("p (r c) -> p r c", r=16)
            ZL = view(A2, 51200, 16384, F32).rearrange("p (r x c) -> p r x c", r=16, x=2)
            ZS = view(A2, 67584, 8256, BF16).rearrange("p (r x c) -> p r x c", r=16, x=2)
            SP_ = A2 + 76032
            PW = view(SP_, 0, 2176, F32).rearrange("p (j x r) -> p j x r", j=17, x=2)
            def sc(i):
                return view(SP_, 2176 + i * 64, 64, F32)
            assert SP_ + 2176 + 30 * 64 <= A2 + A2_B
            WW = view(A0, 0, 16384, F32).rearrange("p (r x c) -> p r x c", r=16, x=2)
            ZG = view(A2, 51200, 16384, BF16).rearrange("p (q i c) -> p q i c", q=4, i=16)
            GT = A0 + 16 * KB
            GEN = A2 + 51200
            b_s5 = S.buf("s5gen")
            b_wz = S.bufs(4, "wz"); b_bd = S.bufs(4, "bd"); b_cp = S.buf("cp"); b_tab = S.buf("tab")
            b_zl = S.bufs(16, "zl"); b_ww = S.bufs(16, "ww"); b_zs = S.bufs(16, "zs"); b_zg = S.bufs(4, "zg")
            olds = [b_g, b_sm] + b_kt + b_cl + [b_ct, b_tc] + b_vw + b_pt + b_hb + b_hn + b_va + [b_gr] + b_qk + b_og
            handoff([b_s5, b_cp, b_tab] + b_wz + b_bd + b_zl + b_ww + b_zs + b_zg, olds)
            s5q = S.dma_sem(f"s5q{l}")
            (s_are, s_aim, s_dt, s_mag, s_th, s_t, s_sin, s_cos, s_abr, s_abi, s_den, s_zr, s_sre, s_sim, s_t2, s_magL,
             s_l128r, s_l128i, s_t3, s_m128) = [sc(i) for i in range(20)]
            zend = sc(20)[:, 0:16]
            zend = view(SP_, 2176 + 20 * 64, 128, F32).rearrange("p (r x) -> p r x", x=2)
            sst = view(SP_, 2176 + 22 * 64, 128, F32).rearrange("p (r x) -> p r x", x=2)

            def TT(out, in0, in1, op, rd=(), wr=None, eng="dve"):
                S.op(eng, lambda e: e.tensor_tensor(out=out, in0=in0, in1=in1, op=op), reads=[b_s5] + list(rd), writes=[b_s5] if wr is None else wr)

            def TS(out, in0, s1, s2, op0, op1=None, rd=(), wr=None):
                if op1 is None:
                    S.op("dve", lambda e: e.tensor_scalar(out=out, in0=in0, scalar1=s1, scalar2=None, op0=op0), reads=[b_s5] + list(rd), writes=[b_s5] if wr is None else wr)
                else:
                    S.op("dve", lambda e: e.tensor_scalar(out=out, in0=in0, scalar1=s1, scalar2=s2, op0=op0, op1=op1), reads=[b_s5] + list(rd), writes=[b_s5] if wr is None else wr)

            def AC(out, in_, func, rd=(), wr=None, **kw):
                S.op("act", lambda e: e.activation(out=out, in_=in_, func=func, **kw), reads=[b_s5] + list(rd), writes=[b_s5] if wr is None else wr)

            def cmul(o_r, o_i, a_r, a_i, b_r, b_i, t1_, t2_, rd=(), wr=None, neg_im=False):
                TT(t1_, a_r, b_r, ALU.mult, rd); TT(t2_, a_i, b_i, ALU.mult, rd)
                TT(o_r, t1_, t2_, ALU.subtract, rd, wr)
                TT(t1_, a_r, b_i, ALU.mult, rd); TT(t2_, a_i, b_r, ALU.mult, rd)
                if neg_im:
                    TT(t1_, t1_, t2_, ALU.add, rd)
                    TS(o_i, t1_, -1.0, None, ALU.mult, rd=rd, wr=wr)
                else:
                    TT(o_i, t1_, t2_, ALU.add, rd, wr)

            araw = view(GEN, 0, 1024, F32)
            S.dma("sp", s5q, araw[0:16, 0:128], dr["s5_a_re"][l].rearrange("(r gl) n -> r (gl n)", gl=2), writes=[b_s5])
            S.dma("sp", s5q, araw[0:16, 128:256], dr["s5_a_im"][l].rearrange("(r gl) n -> r (gl n)", gl=2), writes=[b_s5])
            ldt = dr["s5_log_dt"][l]
            for gl in range(2):
                S.dma("sp", s5q, s_dt[gl * 64:(gl + 1) * 64, :], bass.AP(ldt.tensor, ldt.offset + gl, [[0, 64], [2, 16]]), writes=[b_s5])
            Bsm = [view(GEN, 1024 + x * 1024, 1024, F32).rearrange("p (r c) -> p r c", r=16) for x in range(2)]
            for x, nm in enumerate(["s5_b_re", "s5_b_im"]):
                bsrc = dr[nm][l]
                for gl in range(2):
                    S.dma("sp", s5q, Bsm[x][gl * 64:(gl + 1) * 64, :, :],
                          bass.AP(bsrc.tensor, bsrc.offset + gl * 1024, [[16, 64], [2048, 16], [1, 16]]), writes=[b_s5])
            Craw = [view(GEN, 3072 + x * 1024, 1024, F32).rearrange("p (q n) -> p q n", q=4) for x in range(2)]
            for x, nm in enumerate(["s5_c_re", "s5_c_im"]):
                csrc = dr[nm][l]
                S.dma("sp", s5q, Craw[x], bass.AP(csrc.tensor, csrc.offset, [[64, 128], [8192, 4], [1, 64]]), writes=[b_s5])
            S.seal(s5q, [b_s5])
            bank, bb = next_bank()
            S.op("pe", lambda e: e.transpose(bank[:, 0:16], araw[0:16, 0:128], ident_f[0:16, 0:16]), reads=[b_s5, b_cst], writes=[bb])
            S.op("pe", lambda e: e.transpose(bank[:, 16:32], araw[0:16, 128:256], ident_f[0:16, 0:16]), reads=[b_s5, b_cst], writes=[bb])
            S.op("dve", lambda e: e.tensor_copy(out=s_are, in_=bank[:, 0:16]), reads=[bb], writes=[b_s5])
            S.op("dve", lambda e: e.tensor_copy(out=s_aim, in_=bank[:, 16:32]), reads=[bb], writes=[b_s5])
            PI = float(np.pi)
            AC(s_dt, s_dt, AF.Exp)
            TT(s_t, s_are, s_dt, ALU.mult)
            AC(s_mag, s_t, AF.Exp)
            AC(s_magL, s_t, AF.Exp, scale=float(TCH))
            AC(s_m128, s_t, AF.Exp, scale=float(TCH * NC_))
            TT(s_th, s_aim, s_dt, ALU.mult)
            for thr in (1.0, 3.0, 5.0, 7.0):
                TS(s_t, s_th, thr * PI, -2.0 * PI, ALU.is_gt, ALU.mult)
                if thr == 1.0:
                    TT(s_t2, s_th, s_t, ALU.add)
                else:
                    TT(s_t2, s_t2, s_t, ALU.add)
            AC(s_sin, s_t2, AF.Sin)
            TS(s_t3, s_t2, 0.5 * PI, None, ALU.add)
            TS(s_t, s_t3, PI, -2.0 * PI, ALU.is_gt, ALU.mult)
            TT(s_t3, s_t3, s_t, ALU.add)
            AC(s_cos, s_t3, AF.Sin)
            TT(s_abr, s_mag, s_cos, ALU.mult); TT(s_abi, s_mag, s_sin, ALU.mult)
            TT(s_t, s_are, s_are, ALU.mult); TT(s_t2, s_aim, s_aim, ALU.mult); TT(s_den, s_t, s_t2, ALU.add)
            S.op("dve", lambda e: e.reciprocal(out=s_den, in_=s_den), reads=[b_s5], writes=[b_s5])
            TS(s_zr, s_abr, -1.0, None, ALU.add)
            TT(s_t, s_zr, s_are, ALU.mult); TT(s_t2, s_abi, s_aim, ALU.mult); TT(s_t, s_t, s_t2, ALU.add); TT(s_sre, s_t, s_den, ALU.mult)
            TT(s_t, s_abi, s_are, ALU.mult); TT(s_t2, s_zr, s_aim, ALU.mult); TT(s_t, s_t, s_t2, ALU.subtract); TT(s_sim, s_t, s_den, ALU.mult)
            S.op("dve", lambda e: e.memset(PW[:, 0, 0, :], 1.0), reads=[b_s5], writes=[b_s5])
            S.op("dve", lambda e: e.memset(PW[:, 0, 1, :], 0.0), reads=[b_s5], writes=[b_s5])
            S.op("dve", lambda e: e.tensor_copy(out=PW[:, 1, 0, :], in_=s_abr), reads=[b_s5], writes=[b_s5])
            S.op("dve", lambda e: e.tensor_copy(out=PW[:, 1, 1, :], in_=s_abi), reads=[b_s5], writes=[b_s5])
            pt1 = view(GEN, 5120, 1024, F32).rearrange("p (j r) -> p j r", r=16)
            pt2 = view(GEN, 6144, 1024, F32).rearrange("p (j r) -> p j r", r=16)
            kk_ = 1
            while kk_ < 16:
                def bj(a):
                    return bass.AP(a.tensor, a.offset, [list(a.ap[0]), [0, kk_], list(a.ap[1])])
                cmul(PW[:, kk_ + 1:2 * kk_ + 1, 0, :], PW[:, kk_ + 1:2 * kk_ + 1, 1, :], PW[:, 1:kk_ + 1, 0, :], PW[:, 1:kk_ + 1, 1, :],
                     bj(PW[:, kk_, 0, :]), bj(PW[:, kk_, 1, :]), pt1[:, 0:kk_, :], pt2[:, 0:kk_, :])
                kk_ *= 2
            S.op("dve", lambda e: e.reciprocal(out=s_t, in_=s_magL), reads=[b_s5], writes=[b_s5])
            TT(EC[:, :, 0], PW[:, 16, 0, :], s_t, ALU.mult, wr=[b_s5, b_tab]); TT(ES[:, :, 0], PW[:, 16, 1, :], s_t, ALU.mult, wr=[b_s5, b_tab])
            et1 = view(GEN, 7168, 4096, F32).rearrange("p (r c) -> p r c", r=16)
            et2 = view(GEN, 11264, 4096, F32).rearrange("p (r c) -> p r c", r=16)
            kk_ = 1
            while kk_ < NC_:
                cmul(EC[:, :, kk_:2 * kk_], ES[:, :, kk_:2 * kk_], EC[:, :, 0:kk_], ES[:, :, 0:kk_],
                     bc_last(EC[:, :, kk_ - 1], kk_), bc_last(ES[:, :, kk_ - 1], kk_), et1[:, :, 0:kk_], et2[:, :, 0:kk_], rd=[b_tab], wr=[b_s5, b_tab])
                kk_ *= 2
            TT(s_l128r, EC[:, :, NC_ - 1], s_m128, ALU.mult, rd=[b_tab]); TT(s_l128i, ES[:, :, NC_ - 1], s_m128, ALU.mult, rd=[b_tab])
            Cin_ = [view(GEN, 5120 + x * 2048, 2048, F32).rearrange("p (q n) -> p q n", q=4) for x in range(2)]
            Cp = [view(GEN, 9216 + x * 2048, 2048, F32).rearrange("p (r n) -> p r n", r=16) for x in range(2)]
            ct1 = view(GEN, 13312, 2048, F32).rearrange("p (r n) -> p r n", r=16)
            ct2 = view(A0, 0, 2048, F32).rearrange("p (r n) -> p r n", r=16)
            for x in range(2):
                TS(Cin_[x][:, :, 0:64], Craw[x], par01[:, 0:1], None, ALU.mult, rd=[b_cst])
                TS(Cin_[x][:, :, 64:128], Craw[x], par01[:, 1:2], None, ALU.mult, rd=[b_cst])
                bank, bb = next_bank()
                for q in range(4):
                    S.op("pe", lambda e: e.transpose(bank[:, q * 128:(q + 1) * 128], Cin_[x][:, q, :], ident_f), inc=(q == 3), reads=[b_s5, b_cst], writes=[bb])
                S.op("dve", lambda e: e.tensor_copy(out=Cp[x].rearrange("p r n -> p (r n)"), in_=bank[:, :]), reads=[bb], writes=[b_s5])
            for j in range(17):
                pr = bc_last(PW[:, j, 0, :], 32); pi_ = bc_last(PW[:, j, 1, :], 32)
                TT(ct1, Cp[0], pr, ALU.mult); TT(ct2, Cp[1], pi_, ALU.mult, rd=b_wz, wr=[b_s5] + b_wz)
                TT(CP[:, j, :, 0, :], ct1, ct2, ALU.subtract, wr=[b_s5, b_cp])
                TT(ct1, Cp[0], pi_, ALU.mult); TT(ct2, Cp[1], pr, ALU.mult, rd=b_wz, wr=[b_s5] + b_wz)
                S.op("dve", lambda e: e.scalar_tensor_tensor(out=CP[:, j, :, 1, :], in0=ct1, scalar=-1.0, in1=ct2, op0=ALU.mult, op1=ALU.subtract),
                     reads=[b_s5], writes=[b_s5, b_cp])

            BB = [view(GEN, 5120 + x * 2048, 2048, F32).rearrange("p (r n) -> p r n", r=16) for x in range(2)]
            BBb = [view(GEN, 9216 + x * 1024, 1024, BF16).rearrange("p (r n) -> p r n", r=16) for x in range(2)]
            bt1 = view(GEN, 11264, 1024, F32).rearrange("p (r c) -> p r c", r=16)
            bt2 = view(GEN, 12288, 1024, F32).rearrange("p (r c) -> p r c", r=16)
            for x in range(2):
                S.op("dve", lambda e: e.memset(BB[x], 0.0), reads=[b_s5], writes=[b_s5])
            sre_b = bc_last(s_sre, 16); sim_b = bc_last(s_sim, 16)
            TT(bt1, Bsm[0], sre_b, ALU.mult); TT(bt2, Bsm[1], sim_b, ALU.mult)
            for gl in range(2):
                ps_ = slice(gl * 64, (gl + 1) * 64)
                TT(BB[0][ps_, :, gl * 16:(gl + 1) * 16], bt1[ps_], bt2[ps_], ALU.subtract)
            TT(bt1, Bsm[1], sre_b, ALU.mult); TT(bt2, Bsm[0], sim_b, ALU.mult)
            for gl in range(2):
                ps_ = slice(gl * 64, (gl + 1) * 64)
                TT(BB[1][ps_, :, gl * 16:(gl + 1) * 16], bt1[ps_], bt2[ps_], ALU.add)
            for x in range(2):
                S.op("dve", lambda e: e.tensor_copy(out=BBb[x], in_=BB[x]), reads=[b_s5], writes=[b_s5])
            bdt = view(GEN, 13312, 512, F32)
            for j in range(16):
                bank, bb = next_bank()
                for q in range(4):
                    for x in range(2):
                        S.op("pe", lambda e: e.matmul(bank[:, q * 128:(q + 1) * 128], lhsT=BBb[x][:, 4 * q:4 * q + 4, :].rearrange("p r n -> p (r n)"),
                                                      rhs=CP[:, j, 4 * q:4 * q + 4, x, :], start=(x == 0), stop=(x == 1)), inc=(q == 3 and x == 1),
                             reads=[b_s5, b_cp], writes=[bb])
                if j == 0:
                    for q in range(4):
                        S.op("dve", lambda e: e.tensor_tensor(out=bdt, in0=bank[:, q * 128:(q + 1) * 128], in1=bdm, op=ALU.mult), reads=[bb, b_cst, b_s5], writes=[b_s5])
                        S.op("dve", lambda e: e.scalar_tensor_tensor(out=BD[:, q, 0, :], in0=ident_f, scalar=dcol[:, q:q + 1], in1=bdt, op0=ALU.mult, op1=ALU.add),
                             reads=[b_s5, b_cst, b_prm], writes=[b_bd[q]])
                else:
                    bdm_b = bass.AP(bdm.tensor, bdm.offset, [list(bdm.ap[0]), [0, 4], list(bdm.ap[1])])
                    S.op("dve", lambda e: e.tensor_tensor(out=BD[:, :, j, :], in0=bank[:, :].rearrange("p (q n) -> p q n", q=4), in1=bdm_b, op=ALU.mult),
                         reads=[bb, b_cst], writes=b_bd)
            mt1 = view(GEN, 13824, 2048, F32).rearrange("p (r n) -> p r n", r=16)
            mt2 = view(GEN, 1024, 2048, F32).rearrange("p (r n) -> p r n", r=16)
            MB = [view(GEN, 3072 + x * 1024, 1024, BF16).rearrange("p (r n) -> p r n", r=16) for x in range(2)]
            for i in range(16):
                j = 15 - i
                pr = bc_last(PW[:, j, 0, :], 32); pi_ = bc_last(PW[:, j, 1, :], 32)
                TT(mt1, BB[0], pr, ALU.mult); TT(mt2, BB[1], pi_, ALU.mult); TT(MB[0], mt1, mt2, ALU.subtract)
                TT(mt1, BB[0], pi_, ALU.mult); TT(mt2, BB[1], pr, ALU.mult); TT(MB[1], mt1, mt2, ALU.add)
                bank, bb = next_bank()
                pb = bank[:, :].bitcast(BF16).rearrange("p (q x n) -> p q x n", q=4, x=2)
                for q in range(4):
                    for x in range(2):
                        S.op("pe", lambda e: e.transpose(pb[:, q, x, :], MB[x][:, 4 * q:4 * q + 4, :].rearrange("p r n -> p (r n)"), ident_b),
                             inc=(q == 3 and x == 1), reads=[b_s5, b_cst], writes=[bb])
                S.op("act", lambda e: e.activation(out=WZ[:, :, i, :, :], in_=pb, func=AF.Copy), reads=[bb], writes=b_wz)
            handoff(b_zl, b_zl + [b_s5])
            for q in range(4):
                for rr in range(4):
                    r = 4 * q + rr
                    bank, bb = next_bank()
                    for x in range(2):
                        col = x * 128
                        for i in range(16):
                            S.op("pe", lambda e: e.matmul(bank[:, col:col + 128], lhsT=WZ[32 * rr:32 * rr + 32, q, i, x, :], rhs=US[32 * rr:32 * rr + 32, q, i, :],
                                                          start=(i == 0), stop=(i == 15), tile_position=(32 * rr, 0)), inc=(i == 15 and x == 1),
                                 reads=[b_wz[q], b_us[q]], writes=[bb])
                    S.op("act", lambda e: e.activation(out=ZL[:, r, :, :].rearrange("p x c -> p (x c)"), in_=bank[:, 0:256], func=AF.Copy),
                         reads=[bb], writes=[b_zl[r]])
            if cfg.get("debug"):
                S.dma("sp", dbq, dr["dbg_zl"], view(A2, 51200, 16384, F32), reads=b_zl, writes=[b_dbg])
                S.dma("sp", dbq, dr["dbg_sc"], view(SP_, 0, 4864, F32), reads=[b_s5], writes=[b_dbg])
                S.dma("sp", dbq, dr["dbg_cp"], view(A2, 0, 34816, BF16), reads=[b_cp], writes=[b_dbg])
                S.dma("sp", dbq, dr["dbg_bd"], view(A0, 32 * KB, 16 * KB, BF16), reads=b_bd, writes=[b_dbg])
                S.dma("sp", dbq, dr["dbg_wz"], view(A0, 0, 32 * KB, BF16), reads=b_wz, writes=[b_dbg])
            handoff(b_ww, b_wz + b_ww)
            dt1 = view(GT, 0, 8192, F32).rearrange("p (r c) -> p r c", r=16)
            dt2 = view(GT, 8192, 8192, F32).rearrange("p (r c) -> p r c", r=16)
            b_dt = S.buf("dt"); handoff([b_dt], b_wz)
            magL_b = bc_last(s_magL, NC_)

            def scan_and_mod(init_ap, b_init, final):
                for r in range(16):
                    for x in range(2):
                        ini = 0.0 if init_ap is None else init_ap[:, r, x:x + 1]
                        S.op("dve", lambda e: e.tensor_tensor_scan(out=ZL[:, r, x, :], data0=magL_b[:, r, :], data1=WW[:, r, x, :], initial=ini,
                                                                   op0=ALU.mult, op1=ALU.add), reads=[b_ww[r], b_s5] + ([b_init] if b_init else []), writes=[b_zl[r]])
                if not final:
                    cmul(zend[:, :, 0], zend[:, :, 1], ZL[:, :, 0, NC_ - 1], ZL[:, :, 1, NC_ - 1], EC[:, :, NC_ - 1], ES[:, :, NC_ - 1], s_t, s_t2,
                         rd=b_zl + [b_tab])
                else:
                    S.op("dve", lambda e: e.tensor_tensor(out=dt1, in0=EC, in1=ZL[:, :, 0, :], op=ALU.mult), reads=[b_tab] + b_zl, writes=[b_dt])
                    S.op("dve", lambda e: e.tensor_tensor(out=dt2, in0=ES, in1=ZL[:, :, 1, :], op=ALU.mult), reads=[b_tab] + b_zl, writes=[b_dt])
                    S.op("dve", lambda e: e.tensor_tensor(out=ZS[:, :, 0, 1:NC_ + 1], in0=dt1, in1=dt2, op=ALU.subtract), reads=[b_dt], writes=b_zs)
                    S.op("dve", lambda e: e.tensor_tensor(out=dt1, in0=EC, in1=ZL[:, :, 1, :], op=ALU.mult), reads=[b_tab] + b_zl, writes=[b_dt])
                    S.op("dve", lambda e: e.tensor_tensor(out=dt2, in0=ES, in1=ZL[:, :, 0, :], op=ALU.mult), reads=[b_tab] + b_zl, writes=[b_dt])
                    S.op("dve", lambda e: e.tensor_tensor(out=ZS[:, :, 1, 1:NC_ + 1], in0=dt1, in1=dt2, op=ALU.add), reads=[b_dt], writes=b_zs)
                    S.op("dve", lambda e: e.tensor_copy(out=ZS[:, :, :, 0], in_=init_ap), reads=[b_init], writes=b_zs)

            S.op("dve", lambda e: e.tensor_tensor(out=dt1, in0=EC, in1=ZL[:, :, 0, :], op=ALU.mult), reads=[b_tab] + b_zl, writes=[b_dt])
            S.op("dve", lambda e: e.tensor_tensor(out=dt2, in0=ES, in1=ZL[:, :, 1, :], op=ALU.mult), reads=[b_tab] + b_zl, writes=[b_dt])
            S.op("dve", lambda e: e.tensor_tensor(out=WW[:, :, 0, :], in0=dt1, in1=dt2, op=ALU.add), reads=[b_dt], writes=b_ww)
            S.op("dve", lambda e: e.tensor_tensor(out=dt1, in0=EC, in1=ZL[:, :, 1, :], op=ALU.mult), reads=[b_tab] + b_zl, writes=[b_dt])
            S.op("dve", lambda e: e.tensor_tensor(out=dt2, in0=ES, in1=ZL[:, :, 0, :], op=ALU.mult), reads=[b_tab] + b_zl, writes=[b_dt])
            S.op("dve", lambda e: e.tensor_tensor(out=WW[:, :, 1, :], in0=dt1, in1=dt2, op=ALU.subtract), reads=[b_dt], writes=b_ww)
            scan_and_mod(None, None, False)
            if cfg.get("debug"):
                S.dma("sp", dbq, dr["dbg_zend"], zend.rearrange("p r x -> p (r x)"), reads=[b_s5], writes=[b_dbg])
            if mode == "A":
                S.dma("sp", oq, dr["summ"][:, 0:32], zend.rearrange("p r x -> p (r x)"), reads=[b_s5], writes=[b_out])
                S.wait_all("sp", [b_out])
            if mode != "A":
                sa = dr["summ_all"]
                zall = view(GT, 0, 1024, F32).rearrange("p (j r x) -> p j r x", j=8, x=2)
                zq = S.dma_sem(f"zq{l}")
                S.dma("sp", zq, zall.rearrange("p j r x -> p j (r x)"), sa[:, :, 0:32].rearrange("j p c -> p j c"), reads=[b_dt], writes=[b_dt])
                S.op("dve", lambda e: e.memset(sst, 0.0), reads=[b_s5], writes=[b_s5])
                ctr = sc(24)[:, 0:16]; cti = sc(25)[:, 0:16]
                for j in range(8):
                    cmul(ctr, cti, sst[:, :, 0], sst[:, :, 1], s_l128r, s_l128i, s_t, s_t2)
                    TT(ctr, ctr, zall[:, j, :, 0], ALU.add, rd=[b_dt]); TT(cti, cti, zall[:, j, :, 1], ALU.add, rd=[b_dt])
                    TT(ctr, ctr, sst[:, :, 0], ALU.subtract); TT(cti, cti, sst[:, :, 1], ALU.subtract)
                    S.op("dve", lambda e: e.scalar_tensor_tensor(out=sst[:, :, 0], in0=ctr, scalar=pred[:, j:j + 1], in1=sst[:, :, 0], op0=ALU.mult, op1=ALU.add),
                         reads=[b_s5, b_cst], writes=[b_s5])
                    S.op("dve", lambda e: e.scalar_tensor_tensor(out=sst[:, :, 1], in0=cti, scalar=pred[:, j:j + 1], in1=sst[:, :, 1], op0=ALU.mult, op1=ALU.add),
                         reads=[b_s5, b_cst], writes=[b_s5])
                scan_and_mod(sst, b_s5, True)
                if cfg.get("debug"):
                    S.dma("sp", dbq, dr["dbg_zs"], view(A2, 67584, 8256, BF16), reads=b_zs, writes=[b_dbg])
                handoff(b_zg, b_zl + b_zg)
                for q in range(4):
                    for ib in range(4):
                        bank, bb = next_bank()
                        for i4 in range(4):
                            ip = ib * 4 + i4
                            col = i4 * 128
                            for i in range(ip + 1):
                                S.op("pe", lambda e: e.matmul(bank[:, col:col + 128], lhsT=BD[:, q, ip - i, :], rhs=US[:, q, i, :], start=(i == 0), stop=False),
                                     inc=False, reads=[b_bd[q], b_us[q]], writes=[bb])
                            for rr in range(4):
                                r = 4 * q + rr
                                for x in range(2):
                                    lastw = (rr == 3 and x == 1)
                                    S.op("pe", lambda e: e.matmul(bank[32 * rr:32 * rr + 32, col:col + 128], lhsT=CP[:, ip + 1, r, x, :], rhs=ZS[:, r, x, 0:NC_],
                                                                  start=False, stop=lastw, tile_position=(0, 32 * rr)), inc=(lastw and i4 == 3),
                                         reads=[b_cp, b_zs[r]], writes=[bb])
                        S.op("act", lambda e: e.activation(out=ZG[:, q, ib * 4:ib * 4 + 4, :].rearrange("p i c -> p (i c)"), in_=bank[:, :], func=AF.Gelu_apprx_tanh),
                             reads=[bb], writes=[b_zg[q]])
                if cfg.get("debug"):
                    S.dma("sp", dbq, dr["dbg_zg"], view(A2, 51200, 16384, BF16), reads=b_zg, writes=[b_dbg])
                wg = dr["s5_w_glu"][l].rearrange("(k p) n -> p k n", p=128)
                wgl, bwg = wload(wg, 4, 512)
                gate = view(GT, 0, 2048, F32)
                ZZ = [view(GT, 2048 + ft * 2048, 2048, F32) for ft in range(4)]
                sqb = [view(GT, 10240 + i * 1024, 1024, BF16) for i in range(2)]
                rst = view(GT, 12288, 2048, F32)
                b_gate = S.buf("gate"); b_zz = S.bufs(4, "zz"); b_sqb = S.bufs(2, "sqb"); b_rst = S.buf("rst")
                handoff([b_gate, b_rst] + b_zz + b_sqb, [b_dt] + b_ww)
                YCv = YC[:, 0:4, :].rearrange("p f (c i) -> p f i c", i=16)
                for cb in range(4):
                    bankq, bbq = next_bank()
                    for ft in range(4):
                        bank, bb = next_bank()
                        for kk in range(4):
                            S.op("pe", lambda e: e.matmul(bank[:, :], lhsT=wgl[:, kk, ft * 128:(ft + 1) * 128], rhs=ZG[:, kk, cb * 4:cb * 4 + 4, :],
                                                          start=(kk == 0), stop=(kk == 3)), inc=(kk == 3), reads=[bwg] + b_zg, writes=[bb])
                        S.op("act", lambda e: e.activation(out=gate, in_=bank[:, :], func=AF.Sigmoid, bias=bglu[:, ft:ft + 1]), reads=[bb, b_prm], writes=[b_gate])
                        S.op("dve", lambda e: e.tensor_tensor(out=ZZ[ft], in0=ZG[:, ft, cb * 4:cb * 4 + 4, :].rearrange("p i c -> p (i c)"), in1=gate, op=ALU.mult),
                             reads=[b_zg[ft], b_gate], writes=[b_zz[ft]])
                        S.op("act", lambda e: e.activation(out=sqb[ft % 2], in_=ZZ[ft], func=AF.Square), reads=[b_zz[ft]], writes=[b_sqb[ft % 2]])
                        S.op("pe", lambda e: e.matmul(bankq[:, :], lhsT=ones_b, rhs=sqb[ft % 2], start=(ft == 0), stop=(ft == 3)), inc=True,
                             reads=[b_sqb[ft % 2], b_cst], writes=[bbq])
                    S.op("dve", lambda e: e.tensor_scalar(out=rst, in0=bankq[:, :], scalar1=1.0 / 512, scalar2=EPS, op0=ALU.mult, op1=ALU.add), reads=[bbq], writes=[b_rst])
                    S.op("act", lambda e: e.activation(out=rst, in_=rst, func=AF.Sqrt), reads=[b_rst], writes=[b_rst])
                    S.op("dve", lambda e: e.reciprocal(out=rst, in_=rst), reads=[b_rst], writes=[b_rst])
                    for ft in range(4):
                        S.op("dve", lambda e: e.scalar_tensor_tensor(out=YCv[:, ft, cb * 4:cb * 4 + 4, :], in0=ZZ[ft].rearrange("p (i c) -> p i c", i=4),
                                                                     scalar=outg[:, ft:ft + 1], in1=rst.rearrange("p (i c) -> p i c", i=4), op0=ALU.mult, op1=ALU.mult),
                             reads=[b_zz[ft], b_rst, b_prm], writes=b_yc)
                if cfg.get("debug"):
                    S.dma("sp", dbq, dr["dbg_yc"], view(A1, 0, 32 * KB, BF16), reads=b_yc, writes=[b_dbg])

            if mode != "A":
                S.handoff(X_b, b_qk + b_og + b_us + b_wz + b_bd + b_ww + [b_dt, b_gate, b_rst] + b_zz + b_sqb)
                for tt in range(NTT):
                    S.dma("sp", xq[tt], X[:, tt, :], xin_ap[tt * 128:(tt + 1) * 128, :], writes=[X_b[tt]])
                wo = dr["w_out"][l].rearrange("(k p) n -> p k n", p=128)
                ws = WStream([(wo[:, :, h * 512:(h + 1) * 512], 8, 512) for h in range(2)])
                for h in range(2):
                    wc, bw = ws.get(h)
                    for tt in range(NTT):
                        bank, bb = next_bank()
                        for kk in range(8):
                            S.op("pe", lambda e: e.matmul(bank[:, :], lhsT=YC[:, kk, tt * 128:(tt + 1) * 128], rhs=wc[:, kk, :],
                                                          start=(kk == 0), stop=(kk == 7)), inc=(kk == 7), reads=[bw, b_yc[tt]], writes=[bb])
                        S.op("dve", lambda e: e.tensor_tensor(out=X[:, tt, h * 512:(h + 1) * 512], in0=X[:, tt, h * 512:(h + 1) * 512], in1=bank[:, :], op=ALU.add),
                             reads=[bb, X_b[tt]], writes=[X_b[tt]])

                if cfg.get("dbg_x1"):
                    b_o1 = S.buf("o1")
                    for tt in range(NTT):
                        S.dma("sp", oq, xout_ap[tt * 128:(tt + 1) * 128, :], X[:, tt, :], reads=[X_b[tt]], writes=[b_o1])
                    S.wait_all("sp", [b_o1])
                    return
                S.handoff(HT_b, b_yc)
                a2_users = [b_g, b_sm] + b_kt + b_cl + [b_ct, b_tc] + b_vw + b_pt + b_hb + b_hn + [b_cp, b_tab, b_s5] + b_zl + b_zs + b_zg + b_va + [b_gr]
                handoff([b_junk, b_xh] + b_xnb, a2_users)
                norm_to_hT(g2, False, False)
                if cfg.get("dbg_ht2"):
                    b_o1 = S.buf("o1")
                    S.dma("sp", oq, dr["dbg_ht"], view(A1, 0, 32832, BF16)[:, 0:8 * 2051], reads=HT_b, writes=[b_o1])
                w1 = dr["w_ff1"][l].rearrange("(k p) n -> p k n", p=128)
                w2 = dr["w_ff2"][l].rearrange("(k p) n -> p k n", p=128)
                c1 = [(w1[:, :, hc * 512:(hc + 1) * 512], 8, 512) for hc in range(8)]
                c2 = [(w2[:, hc * 4:(hc + 1) * 4, :], 4, 1024) for hc in range(8)]
                chunks = [c1[0]]
                for hc in range(8):
                    if hc + 1 < 8:
                        chunks.append(c1[hc + 1])
                    chunks.append(c2[hc])
                ws = WStream(chunks)
                k.wci = 0

                def wnext():
                    r = ws.get(k.wci)
                    k.wci += 1
                    return r
                hid = [view(SCR, i * 16 * KB, 16 * KB, BF16).rearrange("p (f t) -> p f t", f=4) for i in range(2)]
                b_hid = [S.bufs(4, f"hid{i}") for i in range(2)]
                sq = [view(SCR, 32 * KB + i * 2048, 2048, F32) for i in range(2)]
                b_sq = S.bufs(2, "sq")
                k.sqi = 0
                handoff(b_hid[0] + b_hid[1] + b_sq, a2_users + [b_junk, b_xh] + b_xnb)

                def ffn1(hc):
                    wc, bw = wnext()
                    hb = hid[hc % 2]
                    for ft in range(4):
                        for nb in range(4):
                            bank, bb = next_bank()
                            for kk in range(8):
                                S.op("pe", lambda e: e.matmul(bank[:, :], lhsT=wc[:, kk, ft * 128:(ft + 1) * 128], rhs=HT[:, kk, 3 + nb * 512:3 + (nb + 1) * 512],
                                                              start=(kk == 0), stop=(kk == 7)), inc=(kk == 7), reads=[bw] + HT_b[nb * 4:nb * 4 + 4], writes=[bb])
                            si = k.sqi % 2; k.sqi += 1
                            S.op("act", lambda e: e.activation(out=sq[si], in_=bank[:, :], func=AF.Square), reads=[bb], writes=[b_sq[si]])
                            S.op("dve", lambda e: e.scalar_tensor_tensor(out=hb[:, ft, nb * 512:(nb + 1) * 512], in0=bank[:, :], scalar=0.0, in1=sq[si],
                                                                         op0=ALU.is_gt, op1=ALU.mult), reads=[bb, b_sq[si]], writes=[b_hid[hc % 2][nb]])

                def ffn2(hc):
                    wc, bw = wnext()
                    hb = hid[hc % 2]
                    for tt in range(NTT):
                        for h in range(2):
                            bank, bb = next_bank()
                            for kk in range(4):
                                S.op("pe", lambda e: e.matmul(bank[:, :], lhsT=hb[:, kk, tt * 128:(tt + 1) * 128], rhs=wc[:, kk, h * 512:(h + 1) * 512],
                                                              start=(kk == 0), stop=(kk == 3)), inc=(kk == 3), reads=[bw, b_hid[hc % 2][tt // 4]], writes=[bb])
                            S.op("dve", lambda e: e.tensor_tensor(out=X[:, tt, h * 512:(h + 1) * 512], in0=X[:, tt, h * 512:(h + 1) * 512], in1=bank[:, :], op=ALU.add),
                                 reads=[bb, X_b[tt]], writes=[X_b[tt]])

                ffn1(0)
                for hc in range(8):
                    if hc + 1 < 8:
                        ffn1(hc + 1)
                    ffn2(hc)

                if last:
                    gfin = view(SCR, 40 * KB, 4096, F32)
                    b_gf = S.buf("gfin")
                    fg = dr["final_norm_g"]
                    S.dma("sp", gq, gfin, bass.AP(fg.tensor, fg.offset, [[0, 128], [1, D]]), writes=[b_gf])
                    ot = [view(SCR, 44 * KB + i * 4096, 4096, F32) for i in range(2)]
                    b_ot = S.bufs(2, "ot")
                    for tt in range(NTT):
                        rms_stats(X[:, tt, :], 128, tt, X_b[tt])
                        S.op("dve", lambda e: e.scalar_tensor_tensor(out=ot[tt % 2], in0=X[:, tt, :], scalar=rstd[:, tt:tt + 1], in1=gfin,
                                                                     op0=ALU.mult, op1=ALU.mult), reads=[X_b[tt], b_st[tt], b_gf], writes=[b_ot[tt % 2]])
                        S.dma("sp", oq, xout_ap[tt * 128:(tt + 1) * 128, :], ot[tt % 2], reads=[b_ot[tt % 2]], writes=[b_out])
                else:
                    for tt in range(NTT):
                        S.dma("sp", oq, xout_ap[tt * 128:(tt + 1) * 128, :], X[:, tt, :], reads=[X_b[tt]], writes=[b_out])
                S.wait_all("sp", [b_out])

        layers = cfg["layers"]
        for li, l in enumerate(layers):
            layer(l, dr["xin"], dr["xhalo"], dr.get("xout"), last=cfg.get("final", False) and li == len(layers) - 1)
    return nc


_NC_CACHE = {}
N_CORES = 8
LAYER_KEYS = ["norm_mix_g", "w_in", "w_out", "norm_ffn_g", "w_ff1", "w_ff2", "ml_conv_w", "ml_conv_b", "ml_b_i", "ml_b_f",
              "ml_norm_g", "s5_a_re", "s5_a_im", "s5_log_dt", "s5_b_re", "s5_b_im", "s5_c_re", "s5_c_im", "s5_d", "s5_w_glu",
              "s5_b_glu", "s5_out_g"]
A_SKIP = {"w_out", "norm_ffn_g", "w_ff1", "w_ff2", "s5_w_glu"}


def _get_nc(mode, final):
    key = (mode, final)
    if key not in _NC_CACHE:
        _NC_CACHE[key] = build(dict(layers=[0], nlayers=1, mode=mode, final=final, debug=False))
    return _NC_CACHE[key]


def _consts():
    par = np.zeros((128, 2), np.float32)
    par[:, 1] = (np.arange(128) // 16) % 2
    par[:, 0] = 1 - par[:, 1]
    return {"ident": np.eye(128, dtype=np.float32), "causal": np.triu(np.ones((128, 128), np.float32)),
            "ones": np.ones((128, 128), np.float32), "par01": par,
            "bdmask": np.kron(np.eye(8), np.ones((16, 16))).astype(np.float32)}


def kernel(**inputs):
    x = np.ascontiguousarray(inputs["x"], dtype=np.float32)
    nb, ls, d = x.shape
    per = ls // 4
    consts = _consts()
    cur = [np.ascontiguousarray(x[c // 4, (c % 4) * per:(c % 4 + 1) * per]) for c in range(N_CORES)]
    preds = []
    for c in range(N_CORES):
        p = np.zeros((128, 8), np.float32)
        for j in range(N_CORES):
            if j // 4 == c // 4 and j < c:
                p[:, j] = 1.0
        preds.append(p)
    depth = inputs["w_in"].shape[0]
    for l in range(depth):
        halos = [np.zeros((3, d), np.float32) if c % 4 == 0 else np.ascontiguousarray(cur[c - 1][-3:]) for c in range(N_CORES)]
        lw = {k: np.ascontiguousarray(np.asarray(inputs[k], dtype=np.float32)[l:l + 1]) for k in LAYER_KEYS}
        final = (l == depth - 1)
        ncA = _get_nc("A", False)
        mapsA = []
        for c in range(N_CORES):
            m = {"xin": cur[c], "xhalo": halos[c], "pred": preds[c]}
            m.update(consts)
            m.update({k: v for k, v in lw.items() if k not in A_SKIP})
            mapsA.append(m)
        resA = run_bass_kernel_spmd(ncA, mapsA, core_ids=list(range(N_CORES)))
        summ_all = np.ascontiguousarray(np.stack([np.asarray(resA.results[c]["summ"]) for c in range(N_CORES)]))
        ncB = _get_nc("B", final)
        mapsB = []
        for c in range(N_CORES):
            m = {"xin": cur[c], "xhalo": halos[c], "pred": preds[c], "summ_all": summ_all,
                 "final_norm_g": np.ascontiguousarray(inputs["final_norm_g"], dtype=np.float32)}
            m.update(consts)
            m.update(lw)
            mapsB.append(m)
        resB = run_bass_kernel_spmd(ncB, mapsB, core_ids=list(range(N_CORES)))
        cur = [np.asarray(resB.results[c]["xout"]) for c in range(N_CORES)]
    out = np.empty_like(x)
    for c in range(N_CORES):
        out[c // 4, (c % 4) * per:(c % 4 + 1) * per] = cur[c]
    return out
```

```python
import numpy as np
import concourse.bass as bass
import concourse.mybir as mybir
from concourse.bass_utils import run_bass_kernel_spmd

F32 = mybir.dt.float32
BF16 = mybir.dt.bfloat16
AF = mybir.ActivationFunctionType
ALU = mybir.AluOpType
AX = mybir.AxisListType


class Buf:
    __slots__ = ("name", "w", "r")

    def __init__(self, name):
        self.name = name
        self.w = {}
        self.r = {}


class Sched:
    def __init__(self, nc, ctx):
        self.nc = nc
        self.ctx = ctx
        self.eng = {"pe": nc.tensor, "act": nc.scalar, "dve": nc.vector, "pool": nc.gpsimd, "sp": nc.sync}
        self.sem = {}
        self.cnt = {}
        for k in self.eng:
            self.sem[k] = ctx.enter_context(nc.semaphore("s_" + k))
            self.cnt[k] = 0
        self.waited = {k: {} for k in self.eng}
        self.ndma = 0
        self.nbuf = 0
        self.mute = False

    def buf(self, name=None):
        self.nbuf += 1
        return Buf(name or f"b{self.nbuf}")

    def bufs(self, n, name="b"):
        return [self.buf(f"{name}{i}") for i in range(n)]

    def dma_sem(self, name=None):
        self.ndma += 1
        key = name or f"dma{self.ndma}"
        self.sem[key] = self.ctx.enter_context(self.nc.semaphore("s_" + key))
        self.cnt[key] = 0
        return key

    def _deps(self, e, reads, writes):
        deps = {}
        for b in reads:
            for k, c in b.w.items():
                if deps.get(k, 0) < c:
                    deps[k] = c
        for b in writes:
            for k, c in b.w.items():
                if deps.get(k, 0) < c:
                    deps[k] = c
            for k, c in b.r.items():
                if deps.get(k, 0) < c:
                    deps[k] = c
        eng = self.eng[e]
        for k, c in deps.items():
            if k == e and e == "pe":
                continue
            if self.waited[e].get(k, 0) < c:
                eng.wait_ge(self.sem[k], c)
                self.waited[e][k] = c

    def _record(self, key, c, reads, writes):
        for b in writes:
            b.w = {key: c}
            b.r = {}
        for b in reads:
            if b.r.get(key, 0) < c:
                b.r[key] = c

    def op(self, e, fn, reads=(), writes=(), inc=True):
        if self.mute:
            return None
        self._deps(e, reads, writes)
        ins = fn(self.eng[e])
        if inc:
            self.cnt[e] += 1
            ins.then_inc(self.sem[e], 1)
            self._record(e, self.cnt[e], reads, writes)
        else:
            self._record(e, self.cnt[e] + 1, reads, writes)
        return ins

    def seal(self, key, bufs):
        if self.mute:
            return
        c = self.cnt[key]
        for b in bufs:
            if key in b.w:
                b.w[key] = c

    def handoff(self, news, olds):
        w = {}
        r = {}
        for ob in olds:
            for k2, c2 in ob.w.items():
                if w.get(k2, 0) < c2:
                    w[k2] = c2
            for k2, c2 in ob.r.items():
                if r.get(k2, 0) < c2:
                    r[k2] = c2
        for nb in news:
            nb.w = dict(w)
            nb.r = dict(r)

    def dma(self, q, dsem, out, in_, reads=(), writes=(), **kw):
        if self.mute:
            return None
        self._deps(q, reads, writes)
        ins = self.eng[q].dma_start(out=out, in_=in_, **kw)
        self.cnt[dsem] += 16
        ins.then_inc(self.sem[dsem], 16)
        self._record(dsem, self.cnt[dsem], reads, writes)
        return ins

    def wait_all(self, e, bufs):
        if self.mute:
            return
        self._deps(e, bufs, ())


import numpy as np
from contextlib import ExitStack

NT = 2048
NTT = 16
D = 1024
DIN = 2568
DFF = 4096
EPS = 1e-6
KB = 1024


class KB_:
    pass


def build(cfg):
    nc = bass.Bass("TRN2", target_bir_lowering=False)
    k = KB_()
    k.nc = nc
    k.cfg = cfg
    L = cfg.get("nlayers", 1)
    dr = {}

    def din(name, shape, dt=F32):
        dr[name] = nc.dram_tensor(name, list(shape), dt, kind="ExternalInput").ap()
        return dr[name]

    def dout(name, shape, dt=F32):
        dr[name] = nc.dram_tensor(name, list(shape), dt, kind="ExternalOutput").ap()
        return dr[name]

    mode = cfg.get("mode", "B")
    IMP = (mode == "B")
    EXP = (mode == "A")
    XF = [("e_a0", 16384), ("e_va", 4256), ("e_gates", 1152), ("e_cl", 8256), ("e_ww", 4096), ("e_cp", 8704), ("e_bd", 4096),
          ("e_tab", 4096), ("e_sp", 1216)]
    din("xin", [NT, D])
    if IMP:
        _din_real = din

        def din(name, shape, dt=F32, _real=_din_real):
            dr[name] = nc.dram_tensor(name, list(shape), dt).ap()
            return dr[name]
    if True:
        din("xhalo", [3, D])
        din("norm_mix_g", [L, D]); din("w_in", [L, D, DIN])
        din("ml_conv_w", [L, 4, 1024]); din("ml_conv_b", [L, 1024])
        din("ml_b_i", [L, 4]); din("ml_b_f", [L, 4])
        din("s5_a_re", [L, 32, 64]); din("s5_a_im", [L, 32, 64]); din("s5_log_dt", [L, 32])
        din("s5_b_re", [L, 32, 64, 16]); din("s5_b_im", [L, 32, 64, 16]); din("s5_c_re", [L, 32, 16, 64]); din("s5_c_im", [L, 32, 16, 64])
        din("s5_d", [L, 32, 16])
        din("par01", [128, 2]); din("bdmask", [128, 128])
    if IMP:
        din = _din_real
    if mode != "A":
        din("w_out", [L, D, D])
        din("norm_ffn_g", [L, D]); din("w_ff1", [L, D, DFF]); din("w_ff2", [L, DFF, D])
        din("final_norm_g", [D])
        din("s5_w_glu", [L, 512, 512])
        din("summ_all", [8, 128, 556])
    din("ident", [128, 128]); din("causal", [128, 128]); din("ones", [128, 128])
    din("ml_norm_g", [L, 512]); din("s5_b_glu", [L, 512]); din("s5_out_g", [L, 512])
    din("pred", [128, 8])
    if EXP:
        dout("summ", [128, 556])
        for nm_, w_ in XF:
            dout(nm_, [128, w_])
    if IMP:
        for nm_, w_ in XF:
            din(nm_, [128, w_])
    if mode != "A":
        dout("xout", [NT, D])
    if cfg.get("dbg_ht2"):
        dout("dbg_ht", [128, 8 * 2051], BF16)
    if cfg.get("debug"):
        dout("dbg_u", [128, 4 * 2048], BF16)
        dout("dbg_qk", [128, 8 * 2048], BF16)
        dout("dbg_v", [128, 16 * 4 * 129], BF16)
        dout("dbg_o", [128, 16 * 512], BF16)
        dout("dbg_if", [128, 128])
        dout("dbg_yc", [128, 8 * 2048], BF16)
        dout("dbg_g", [128, 16 * 64])
        dout("dbg_zl", [128, 4096]); dout("dbg_zs", [128, 16 * 2 * 129], BF16); dout("dbg_zg", [128, 8192], BF16)
        dout("dbg_sc", [128, 1216]); dout("dbg_cp", [128, 17408], BF16); dout("dbg_bd", [128, 8192], BF16); dout("dbg_wz", [128, 16384], BF16)
        dout("dbg_zend", [128, 32])
    k.dr = dr

    with ExitStack() as ctx:
        S = Sched(nc, ctx)
        k.S = S
        ctx.enter_context(nc.allow_non_contiguous_dma(reason="small param loads"))
        ctx.enter_context(nc.allow_low_precision(reason="bf16 matmul operands"))
        A0_B, A1_B, A2_B = 64 * KB, 33 * KB + 256, 79 * KB
        arena = ctx.enter_context(nc.sbuf_tensor("arena", [128, (A0_B + A1_B + A2_B) // 4], F32))
        ring = ctx.enter_context(nc.sbuf_tensor("ring", [128, 3 * 4096], BF16))
        cst = ctx.enter_context(nc.sbuf_tensor("cst", [128, 1024], F32))
        banks = [ctx.enter_context(nc.psum_tensor(f"ps{i}", [128, 512], F32)) for i in range(8)]
        bank_bufs = S.bufs(8, "bank")
        k.bank_i = 0
        block = ctx.enter_context(nc.Block())

        def view(base, off, nbytes, dt):
            assert off % 4 == 0 and nbytes % 4 == 0
            a = arena[:, (base + off) // 4:(base + off + nbytes) // 4]
            return a if dt == F32 else a.bitcast(dt)
        A0, A1, A2 = 0, A0_B, A0_B + A1_B

        def next_bank():
            i = k.bank_i
            k.bank_i = (i + 1) % 8
            return banks[i], bank_bufs[i]

        ident_f = cst[:, 0:128]
        ident_b = cst[:, 128:192].bitcast(BF16)
        b_cst = S.buf("cst")
        dq = S.dma_sem("dq_misc")
        S.dma("sp", dq, ident_f, dr["ident"], writes=[b_cst])
        tri_f = cst[:, 384:512]
        ones_f = cst[:, 512:640]
        tri_b = cst[:, 192:256].bitcast(BF16)
        S.dma("sp", dq, tri_f, dr["causal"], writes=[b_cst])
        S.dma("sp", dq, ones_f, dr["ones"], writes=[b_cst])
        par01 = cst[:, 752:754]
        bdm = cst[:, 768:896]
        ones_b = cst[:, 896:960].bitcast(BF16)
        if not IMP:
            S.dma("sp", dq, par01, dr["par01"], writes=[b_cst])
        pred = cst[:, 972:980]
        pmask = cst[:, 980:988]
        S.dma("sp", dq, pred, dr["pred"], writes=[b_cst])
        if not IMP:
            S.dma("sp", dq, bdm, dr["bdmask"], writes=[b_cst])
        S.seal(dq, [b_cst])
        S.op("dve", lambda e: e.tensor_copy(out=ones_b, in_=ones_f), reads=[b_cst], writes=[b_cst])
        S.op("dve", lambda e: e.tensor_scalar(out=pmask, in0=pred, scalar1=1e6, scalar2=-1e6, op0=ALU.mult, op1=ALU.add), reads=[b_cst], writes=[b_cst])
        S.op("dve", lambda e: e.tensor_copy(out=ident_b, in_=ident_f), reads=[b_cst], writes=[b_cst])
        S.op("dve", lambda e: e.tensor_copy(out=tri_b, in_=tri_f), reads=[b_cst], writes=[b_cst])

        X = view(A0, 0, 64 * KB, F32).rearrange("p (t d) -> p t d", t=NTT)
        X_b = S.bufs(NTT, "X")
        HT = view(A1, 0, 8 * 2051 * 2 + 0, BF16) if False else view(A1, 0, 32832, BF16)[:, 0:8 * 2051].rearrange("p (k t) -> p k t", k=8)
        HT_b = S.bufs(NTT + 1, "HT")
        YC = view(A1, 0, 32 * KB, BF16).rearrange("p (k t) -> p k t", k=8)
        QK = view(A0, 0, 32 * KB, BF16).rearrange("p (f t) -> p f t", f=8)
        OG = view(A0, 32 * KB, 16 * KB, BF16).rearrange("p (t d) -> p t d", t=NTT)
        US = view(A0, 48 * KB, 16 * KB, BF16).rearrange("p (q i c) -> p q i c", q=4, i=16)
        VA = view(A2, 0, 16512, BF16).rearrange("p (t h v) -> p t h v", t=NTT, h=4)
        GR = view(A2, 16512, 512, F32)
        SCR = A2 + 17024

        xq = [S.dma_sem(f"xq{i}") for i in range(16)]
        hq = S.dma_sem("hq"); gq = S.dma_sem("gq")
        oq = S.dma_sem("oq")
        wq = [S.dma_sem(f"wq{i}") for i in range(3)]
        ring_b = S.bufs(3, "ring")
        k.wi = 0

        def wload(src_ap, nk, ncols):
            i = k.wi % 3
            k.wi += 1
            v = ring[:, i * 4096: i * 4096 + nk * ncols].rearrange("p (k n) -> p k n", k=nk)
            S.dma("pool", wq[i], v, src_ap, writes=[ring_b[i]])
            return v, ring_b[i]

        class WStream:
            def __init__(self, chunks):
                self.chunks = chunks
                self.loaded = []

            def get(self, i, ahead=2):
                while len(self.loaded) < min(len(self.chunks), i + 1 + ahead):
                    self.loaded.append(wload(*self.chunks[len(self.loaded)]))
                return self.loaded[i]

        k.pq = None

        def load_pvec(dst, src_1d, b, q="sp"):
            S.dma(q, k.pq, dst, src_1d.rearrange("(k p) -> p k", p=128), writes=[b])

        tmp_b = S.bufs(4, "tmp")

        def quiesce():
            for e_ in ("pe", "act", "dve", "pool"):
                if S.cnt[e_] > 0:
                    nc.sync.wait_ge(S.sem[e_], S.cnt[e_])
            for key_, c_ in S.cnt.items():
                if key_ not in S.eng and c_ > 0:
                    nc.sync.wait_ge(S.sem[key_], c_)

        def layer(l, xin_ap, xh_ap, xout_ap, last):
            stop = cfg.get("stop")
            prm = cst[:, 256:256 + 64]
            b_prm = S.buf("prm")
            pq_l = S.dma_sem(f"pq{l}"); k.pq = pq_l
            g1 = cst[:, 640:648]; g2 = cst[:, 648:656]
            cw = cst[:, 656:688].rearrange("p (j f) -> p j f", j=4)
            cb = cst[:, 688:696]
            b_out = S.buf("out")
            if not IMP:
                load_pvec(g1, dr["norm_mix_g"][l], b_prm)
                for j in range(4):
                    load_pvec(cw[:, j, :], dr["ml_conv_w"][l, j], b_prm)
                load_pvec(cb, dr["ml_conv_b"][l], b_prm)
            if mode != "A":
                load_pvec(g2, dr["norm_ffn_g"][l], b_prm)
            mlg = cst[:, 740:744]
            load_pvec(mlg, dr["ml_norm_g"][l], b_prm)
            bif = cst[:, 744:752]
            dcol = cst[:, 960:964]; bglu = cst[:, 964:968]; outg = cst[:, 968:972]
            if not IMP:
                bi_ = dr["ml_b_i"][l]; bf_ = dr["ml_b_f"][l]
                S.dma("sp", pq_l, bif[:, 0:4], bass.AP(bi_.tensor, bi_.offset, [[0, 128], [1, 4]]), writes=[b_prm])
                S.dma("sp", pq_l, bif[:, 4:8], bass.AP(bf_.tensor, bf_.offset, [[0, 128], [1, 4]]), writes=[b_prm])
                load_pvec(dcol, dr["s5_d"][l].rearrange("g p -> (g p)"), b_prm)
            load_pvec(bglu, dr["s5_b_glu"][l], b_prm)
            load_pvec(outg, dr["s5_out_g"][l], b_prm)
            S.seal(pq_l, [b_prm])

            def xfer(name, ap, bufs):
                was = S.mute; S.mute = False
                if EXP:
                    S.dma("sp", oq, dr[name], ap, reads=bufs, writes=[b_out])
                elif IMP:
                    q_ = S.dma_sem(f"{name}_{l}")
                    S.dma("sp", q_, ap, dr[name], writes=bufs)
                S.mute = was
            if IMP:
                S.mute = True
            ssq = cst[:, 700:717]
            rstd = cst[:, 720:737]
            b_st = S.bufs(17, "st")
            junk = view(SCR, 0, 2048, BF16)
            b_junk = S.buf("junk")
            xnb = [view(SCR, 2048 + i * 2048, 2048, BF16) for i in range(2)]
            b_xnb = S.bufs(2, "xnb")
            xh_t = view(SCR, 6144, 4096, F32)
            b_xh = S.buf("xh")

            def rms_stats(src, np_, col, bsrc):
                S.op("act", lambda e: e.activation(out=junk[:np_], in_=src, func=AF.Square, accum_out=ssq[:np_, col:col + 1]),
                     reads=[bsrc], writes=[b_junk, b_st[col]])
                S.op("dve", lambda e: e.tensor_scalar(out=rstd[:np_, col:col + 1], in0=ssq[:np_, col:col + 1], scalar1=1.0 / D, scalar2=EPS,
                                                      op0=ALU.mult, op1=ALU.add), reads=[b_st[col]], writes=[b_st[col]])
                S.op("act", lambda e: e.activation(out=rstd[:np_, col:col + 1], in_=rstd[:np_, col:col + 1], func=AF.Sqrt),
                     reads=[b_st[col]], writes=[b_st[col]])
                S.op("dve", lambda e: e.reciprocal(out=rstd[:np_, col:col + 1], in_=rstd[:np_, col:col + 1]), reads=[b_st[col]], writes=[b_st[col]])

            def norm_to_hT(gvec, with_halo, from_dram):
                gb = bass.AP(gvec.tensor, gvec.offset, [list(gvec.ap[0]), list(gvec.ap[1]), [0, 128]])
                for tt in range(NTT):
                    if from_dram:
                        S.dma("sp", xq[tt], X[:, tt, :], xin_ap[tt * 128:(tt + 1) * 128, :], writes=[X_b[tt]])
                for tt in range(NTT + (1 if with_halo else 0)):
                    halo = tt == NTT
                    np_ = 3 if halo else 128
                    if halo:
                        S.dma("sp", hq, xh_t[:3, :], xh_ap, writes=[b_xh])
                        src, bsrc = xh_t[:3, :], b_xh
                    else:
                        src, bsrc = X[:, tt, :], X_b[tt]
                    rms_stats(src, np_, tt, bsrc)
                    xb = xnb[tt % 2]; bx = b_xnb[tt % 2]
                    S.op("act", lambda e: e.activation(out=xb[:np_], in_=src, func=AF.Copy, scale=rstd[:np_, tt:tt + 1]),
                         reads=[bsrc, b_st[tt]], writes=[bx])
                    bank, bb = next_bank()
                    pb = bank[:, 0:512].bitcast(BF16).rearrange("p (k t) -> p k t", k=8)
                    for kk in range(8):
                        S.op("pe", lambda e: e.transpose(pb[:, kk, 0:np_], xb[:np_, kk * 128:(kk + 1) * 128], ident_b[:np_, :np_]),
                             inc=(kk == 7), reads=[bx, b_cst], writes=[bb])
                    c0 = 0 if halo else 3 + tt * 128
                    S.op("dve", lambda e: e.tensor_tensor(out=HT[:, :, c0:c0 + np_], in0=pb[:, :, 0:np_], in1=gb[:, :, 0:np_], op=ALU.mult),
                         reads=[bb, b_prm], writes=[HT_b[tt]])

            norm_to_hT(g1, True, True)

            if stop == "norm1":
                quiesce(); return
            win = dr["w_in"][l].rearrange("(k p) n -> p k n", p=128)
            chunks = [(win[:, :, c * 512:(c + 1) * 512], 8, 512) for c in range(5)] + [(win[:, :, 2560:2568], 8, 8)]
            ws = WStream(chunks)
            stage = [view(SCR, 10240 + i * 8448, 8448, F32) for i in range(2)]
            b_stage = S.bufs(2, "stage")
            acc = view(SCR, 10240 + 2 * 8448, 8192, F32)
            b_acc = S.buf("acc")
            b_us = S.bufs(4, "us")
            b_qk = S.bufs(8, "qk")
            b_va = S.bufs(NTT, "va")
            b_og = S.bufs(NTT, "og")
            b_gr = S.buf("gr")
            allHT = HT_b
            S.op("pool", lambda e: e.memset(VA[:, :, :, 128:129], 1.0), writes=b_va)
            for ci in range(3):
                wc, bw = ws.get(ci)
                for ft in range(4):
                    f = (ci - 1) * 4 + ft
                    if ci > 0:
                        st = stage[f % 2]; bs = b_stage[f % 2]
                        bank, bb = next_bank()
                        for kk in range(8):
                            S.op("pe", lambda e: e.matmul(bank[:, 0:3], lhsT=wc[:, kk, ft * 128:(ft + 1) * 128], rhs=HT[:, kk, 0:3],
                                                          start=(kk == 0), stop=(kk == 7)), inc=(kk == 7), reads=[bw, HT_b[NTT]], writes=[bb])
                        S.op("act", lambda e: e.activation(out=st[:, 0:3], in_=bank[:, 0:3], func=AF.Copy), reads=[bb], writes=[bs])
                    for nb in range(4):
                        bank, bb = next_bank()
                        for kk in range(8):
                            S.op("pe", lambda e: e.matmul(bank[:, :], lhsT=wc[:, kk, ft * 128:(ft + 1) * 128],
                                                          rhs=HT[:, kk, 3 + nb * 512:3 + (nb + 1) * 512], start=(kk == 0), stop=(kk == 7)),
                                 inc=(kk == 7), reads=[bw] + allHT[nb * 4:nb * 4 + 4], writes=[bb])
                        if ci == 0:
                            dst = US[:, ft, :, nb * 32:(nb + 1) * 32]
                            src = bank[:, :].rearrange("p (c i) -> p i c", i=16)
                            S.op("act", lambda e: e.activation(out=dst, in_=src, func=AF.Copy), reads=[bb], writes=[b_us[ft]])
                        else:
                            S.op("act", lambda e: e.activation(out=st[:, 3 + nb * 512:3 + (nb + 1) * 512], in_=bank[:, :], func=AF.Copy),
                                 reads=[bb], writes=[bs])
                    if ci > 0:
                        S.op("dve", lambda e: e.tensor_scalar(out=acc, in0=st[:, 0:2048], scalar1=cw[:, 0, f:f + 1], scalar2=None, op0=ALU.mult),
                             reads=[bs, b_prm], writes=[b_acc])
                        for j in range(1, 4):
                            S.op("dve", lambda e: e.scalar_tensor_tensor(out=acc, in0=st[:, j:j + 2048], scalar=cw[:, j, f:f + 1], in1=acc,
                                                                         op0=ALU.mult, op1=ALU.add), reads=[bs, b_prm, b_acc], writes=[b_acc])
                        S.op("act", lambda e: e.activation(out=QK[:, f, :], in_=acc, func=AF.Silu, bias=cb[:, f:f + 1]),
                             reads=[b_acc, b_prm], writes=[b_qk[f]])
            for ci in (3, 4):
                wc, bw = ws.get(ci)
                for tt in range(NTT):
                    bank, bb = next_bank()
                    for kk in range(8):
                        S.op("pe", lambda e: e.matmul(bank[:, :], lhsT=HT[:, kk, 3 + tt * 128:3 + (tt + 1) * 128], rhs=wc[:, kk, :],
                                                      start=(kk == 0), stop=(kk == 7)), inc=(kk == 7), reads=[bw, HT_b[tt]], writes=[bb])
                    if ci == 3:
                        S.op("act", lambda e: e.activation(out=VA[:, tt, :, 0:128], in_=bank[:, :].rearrange("p (h v) -> p h v", h=4), func=AF.Copy),
                             reads=[bb], writes=[b_va[tt]])
                    else:
                        S.op("act", lambda e: e.activation(out=OG[:, tt, :], in_=bank[:, :], func=AF.Sigmoid), reads=[bb], writes=[b_og[tt]])
            wc, bw = ws.get(5)
            bank, bb = next_bank()
            for tt in range(NTT):
                for kk in range(8):
                    S.op("pe", lambda e: e.matmul(bank[:, tt * 8:(tt + 1) * 8], lhsT=HT[:, kk, 3 + tt * 128:3 + (tt + 1) * 128], rhs=wc[:, kk, :],
                                                  start=(kk == 0), stop=(kk == 7)), inc=(kk == 7), reads=[bw, HT_b[tt]], writes=[bb])
            S.op("dve", lambda e: e.tensor_copy(out=GR, in_=bank[:, 0:128]), reads=[bb], writes=[b_gr])

            if cfg.get("debug"):
                dbq = S.dma_sem("dbq")
                b_dbg = S.buf("dbg")
                S.dma("sp", dbq, dr["dbg_u"], view(A0, 48 * KB, 16 * KB, BF16), reads=b_us, writes=[b_dbg])
                S.dma("sp", dbq, dr["dbg_qk"], view(A0, 0, 32 * KB, BF16), reads=b_qk, writes=[b_dbg])
                S.dma("sp", dbq, dr["dbg_v"], view(A2, 0, 16512, BF16), reads=b_va, writes=[b_dbg])
                S.dma("sp", dbq, dr["dbg_o"], view(A0, 32 * KB, 16 * KB, BF16), reads=b_og, writes=[b_dbg])
                S.dma("sp", dbq, dr["dbg_if"], GR, reads=[b_gr], writes=[b_dbg])
                pass

            if stop == "win":
                quiesce(); return
            xfer("e_a0", view(A0, 0, 64 * KB, F32), b_qk + b_og + b_us)
            xfer("e_va", view(A2, 0, 17024, F32), b_va + [b_gr])
            b_yc = S.bufs(NTT, "yc")
            S.handoff(b_yc, HT_b)

            MS = SCR
            def garr(i):
                return view(MS, i * 256, 256, F32).rearrange("p (m h) -> p m h", h=4)
            nlf, ig, nF, g, Gm, R0, Rr, w0, wv, clamp, dl, ep, incl, t1 = [garr(i) for i in range(14)]
            sm = view(MS, 14 * 256, 256, F32)
            gcol = sm[:, 0:1]; dm = sm[:, 4:8]; rec = sm[:, 8:12]; ss = sm[:, 12:16]; rs4 = sm[:, 16:20]
            Gb = view(MS, 15 * 256, 512, F32)
            b_g = S.buf("gates")
            b_sm = S.buf("sm")
            KT = view(MS, 4608, 16512, BF16)
            kTok = KT[:, 0:16 * 512].rearrange("p (m d) -> p m d", m=16)
            Cin = KT[:, 0:64 * 129].rearrange("p (c v) -> p c v", c=64)
            b_kt = S.bufs(16, "kt")
            CL = view(MS, 4608 + 16512, 64 * 129 * 4, F32).rearrange("p (c v) -> p c v", c=64)
            b_cl = S.bufs(16, "cl")
            T0 = MS + 4608 + 16512 + 64 * 129 * 4
            Ct = view(T0, 0, 2064, F32).rearrange("p (h v) -> p h v", h=4)
            tmpC = view(T0, 2064, 2064, F32).rearrange("p (h v) -> p h v", h=4)
            b_ct = S.buf("ct"); b_tc = S.buf("tmpc")
            Vw = [view(T0, 4128 + i * 1032, 1032, BF16).rearrange("p (h v) -> p h v", h=4) for i in range(2)]
            b_vw = S.bufs(2, "vw")
            PT = [view(T0, 6192 + i * 256, 256, BF16) for i in range(2)]
            b_pt = S.bufs(2, "pt")
            hbuf = [view(T0, 6704 + i * 512, 512, F32) for i in range(4)]
            b_hb = S.bufs(4, "hb")
            hn = [view(T0, 8752 + i * 256, 256, BF16) for i in range(2)]
            b_hn = S.bufs(2, "hn")
            junk2 = view(T0, 9264, 256, BF16)
            assert T0 + 9520 <= A2 + A2_B, (T0 + 9520 - A2 - A2_B)

            handoff = S.handoff
            handoff([b_g, b_sm] + b_kt + b_cl + [b_ct, b_tc] + b_vw + b_pt + b_hb + b_hn, b_stage + [b_acc, b_junk, b_xh] + b_xnb)
            GRv = GR.rearrange("p (m c) -> p m c", c=8)

            def bc_m(ap4):
                return bass.AP(ap4.tensor, ap4.offset, [list(ap4.ap[0]), [0, 16], list(ap4.ap[1])])

            def bc_last(ap, n):
                return bass.AP(ap.tensor, ap.offset, [list(x) for x in ap.ap] + [[0, n]])

            def flat(a):
                return a.rearrange("p m h -> p (m h)")
            LN_S = -0.5 * float(np.log(128.0))
            S.op("dve", lambda e: e.tensor_tensor(out=ig, in0=GRv[:, :, 0:4], in1=bc_m(bif[:, 0:4]), op=ALU.add), reads=[b_gr, b_prm], writes=[b_g])
            S.op("dve", lambda e: e.tensor_tensor(out=t1, in0=GRv[:, :, 4:8], in1=bc_m(bif[:, 4:8]), op=ALU.add), reads=[b_gr, b_prm], writes=[b_g])
            S.op("act", lambda e: e.activation(out=t1, in_=t1, func=AF.Exp, scale=-1.0), reads=[b_g], writes=[b_g])
            S.op("act", lambda e: e.activation(out=nlf, in_=t1, func=AF.Ln, bias=1.0), reads=[b_g], writes=[b_g])
            bankA, bbA = next_bank()
            S.op("pe", lambda e: e.matmul(bankA[:, 0:64], lhsT=tri_f, rhs=flat(nlf), start=True, stop=True), reads=[b_g, b_cst], writes=[bbA])
            bankB, bbB = next_bank()
            S.op("pe", lambda e: e.matmul(bankB[:, 0:64], lhsT=ones_f, rhs=flat(nlf), start=True, stop=True), reads=[b_g, b_cst], writes=[bbB])
            bA = bankA[:, 0:64].rearrange("p (m h) -> p m h", h=4)
            bB = bankB[:, 0:64].rearrange("p (m h) -> p m h", h=4)
            for h in range(4):
                S.op("dve", lambda e: e.tensor_tensor_scan(out=incl[:, :, h], data0=ones_f[:, 0:16], data1=bB[:, :, h], initial=0.0,
                                                           op0=ALU.mult, op1=ALU.add), reads=[bbB, b_cst, b_g], writes=[b_g])
            S.op("dve", lambda e: e.tensor_tensor(out=t1, in0=incl, in1=bB, op=ALU.subtract), reads=[b_g, bbB], writes=[b_g])
            S.op("dve", lambda e: e.tensor_tensor(out=nF, in0=t1, in1=bA, op=ALU.add), reads=[b_g, bbA], writes=[b_g])
            S.op("dve", lambda e: e.tensor_tensor(out=g, in0=ig, in1=nF, op=ALU.add), reads=[b_g], writes=[b_g])
            bankT, bbT = next_bank()
            S.op("pe", lambda e: e.transpose(bankT[0:64, 0:128], flat(g), ident_f), reads=[b_g, b_cst], writes=[bbT])
            S.op("dve", lambda e: e.tensor_reduce(out=gcol[0:64, :], in_=bankT[0:64, 0:128], axis=AX.X, op=ALU.max), reads=[bbT], writes=[b_sm])
            S.op("dve", lambda e: e.tensor_copy(out=Gb[0:64, :], in_=bc_last(gcol[0:64, 0:1], 128)[:, 0, :]), reads=[b_sm], writes=[b_sm])
            bankG, bbG = next_bank()
            S.op("pe", lambda e: e.matmul(bankG[:, 0:64], lhsT=Gb[0:64, :], rhs=ident_f[0:64, 0:64], start=True, stop=True), reads=[b_sm, b_cst], writes=[bbG])
            S.op("dve", lambda e: e.tensor_copy(out=flat(Gm), in_=bankG[:, 0:64]), reads=[bbG], writes=[b_g])
            for h in range(4):
                S.op("dve", lambda e: e.tensor_tensor_scan(out=R0[:, :, h], data0=Gm[:, :, h], data1=Gm[:, :, h], initial=-1e30,
                                                           op0=ALU.max, op1=ALU.max), reads=[b_g], writes=[b_g])
            S.op("dve", lambda e: e.tensor_tensor(out=t1, in0=g, in1=R0, op=ALU.subtract), reads=[b_g], writes=[b_g])
            S.op("act", lambda e: e.activation(out=w0, in_=t1, func=AF.Exp, bias=LN_S), reads=[b_g], writes=[b_g])
            S.op("dve", lambda e: e.tensor_tensor(out=t1[:, 1:16, :], in0=R0[:, 0:15, :], in1=R0[:, 1:16, :], op=ALU.subtract), reads=[b_g], writes=[b_g])
            S.op("act", lambda e: e.activation(out=dl[:, 1:16, :], in_=t1[:, 1:16, :], func=AF.Exp), reads=[b_g], writes=[b_g])
            for m in range(16):
                cs = slice(m * 128, (m + 1) * 128)
                bank, bb = next_bank()
                pb = bank[:, 0:256].bitcast(BF16)
                for h in range(4):
                    S.op("pe", lambda e: e.transpose(pb[:, h * 128:(h + 1) * 128], QK[:, 4 + h, cs], ident_b), inc=(h == 3),
                         reads=[b_qk[4 + h], b_cst], writes=[bb])
                S.op("act", lambda e: e.activation(out=kTok[:, m, :], in_=pb, func=AF.Copy), reads=[bb], writes=[b_kt[m]])
                vw = Vw[m % 2]; bv = b_vw[m % 2]
                S.op("dve", lambda e: e.tensor_tensor(out=vw, in0=VA[:, m, :, :], in1=bc_last(w0[:, m, :], 129), op=ALU.mult),
                     reads=[b_va[m], b_g], writes=[bv])
                for h in range(4):
                    bank, bb = next_bank()
                    S.op("pe", lambda e: e.matmul(bank[:, 0:129], lhsT=kTok[:, m, h * 128:(h + 1) * 128], rhs=vw[:, h, :], start=True, stop=True),
                         reads=[b_kt[m], bv], writes=[bb])
                    if h % 2 == 0:
                        S.op("act", lambda e: e.activation(out=CL[:, m * 4 + h, :], in_=bank[:, 0:129], func=AF.Copy), reads=[bb], writes=[b_cl[m]])
                    else:
                        S.op("dve", lambda e: e.tensor_copy(out=CL[:, m * 4 + h, :], in_=bank[:, 0:129]), reads=[bb], writes=[b_cl[m]])
                if m == 0:
                    S.op("dve", lambda e: e.tensor_copy(out=Ct, in_=CL[:, 0:4, :]), reads=[b_cl[0]], writes=[b_ct])
                else:
                    S.op("dve", lambda e: e.tensor_tensor(out=Ct, in0=Ct, in1=bc_last(dl[:, m, :], 129), op=ALU.mult), reads=[b_ct, b_g], writes=[b_ct])
                    S.op("dve", lambda e: e.tensor_tensor(out=Ct, in0=Ct, in1=CL[:, m * 4:m * 4 + 4, :], op=ALU.add), reads=[b_ct, b_cl[m]], writes=[b_ct])
            xfer("e_gates", view(MS, 0, 4608, F32), [b_g, b_sm])
            xfer("e_cl", view(MS, 4608 + 16512, 64 * 129 * 4, F32), b_cl)
            if IMP:
                S.mute = False
            mst = sm[:, 20:24]
            exq = S.dma_sem(f"exq{l}")
            if mode == "A":
                S.dma("sp", oq, dr["summ"][:, 32:548], Ct.rearrange("p h v -> p (h v)"), reads=[b_ct], writes=[b_out])
                S.dma("sp", oq, dr["summ"][:, 548:552], R0[:, 15, :], reads=[b_g], writes=[b_out])
                S.dma("sp", oq, dr["summ"][:, 552:556], incl[:, 15, :], reads=[b_g], writes=[b_out])
            else:
                sa = dr["summ_all"]
                small = Gb.rearrange("p (j c) -> p j c", j=8)[:, :, 0:8]
                S.dma("sp", exq, small, sa[:, :, 548:556].rearrange("j p c -> p j c"), writes=[b_sm])
                S.seal(exq, [b_sm])
                cm = sm[:, 20:24]; mx = sm[:, 24:28]; ta = sm[:, 28:32]; tb = sm[:, 32:36]; r0j = sm[:, 36:40]; nfj = sm[:, 40:44]; tq = sm[:, 44:48]
                S.op("dve", lambda e: e.memset(tmpC, 0.0), reads=[b_tc], writes=[b_tc])
                S.op("dve", lambda e: e.memset(cm, 0.0), reads=[b_sm], writes=[b_sm])
                cq = [S.dma_sem(f"cq{l}_{i}") for i in range(2)]
                Cj = [Ct, view(T0, 4128, 2064, F32).rearrange("p (h v) -> p h v", h=4)]
                b_cj = [b_ct, S.buf("cj1")]
                S.handoff([b_cj[1]], b_vw)
                for j in range(8):
                    cj = Cj[j % 2]; bcj = b_cj[j % 2]
                    S.dma("sp", cq[j % 2], cj.rearrange("p h v -> p (h v)"), sa[j, :, 32:548], writes=[bcj])
                    S.op("dve", lambda e: e.tensor_scalar(out=r0j, in0=small[:, j, 0:4], scalar1=pred[:, j:j + 1], scalar2=pmask[:, j:j + 1],
                                                          op0=ALU.mult, op1=ALU.add), reads=[b_sm, b_cst], writes=[b_sm])
                    S.op("dve", lambda e: e.tensor_scalar(out=nfj, in0=small[:, j, 4:8], scalar1=pred[:, j:j + 1], scalar2=None, op0=ALU.mult),
                         reads=[b_sm, b_cst], writes=[b_sm])
                    S.op("dve", lambda e: e.tensor_tensor(out=mx, in0=cm, in1=r0j, op=ALU.max), reads=[b_sm], writes=[b_sm])
                    S.op("dve", lambda e: e.tensor_tensor(out=tq, in0=cm, in1=mx, op=ALU.subtract), reads=[b_sm], writes=[b_sm])
                    S.op("act", lambda e: e.activation(out=ta, in_=tq, func=AF.Exp), reads=[b_sm], writes=[b_sm])
                    S.op("dve", lambda e: e.tensor_tensor(out=tq, in0=r0j, in1=mx, op=ALU.subtract), reads=[b_sm], writes=[b_sm])
                    S.op("act", lambda e: e.activation(out=tb, in_=tq, func=AF.Exp), reads=[b_sm], writes=[b_sm])
                    S.op("dve", lambda e: e.tensor_tensor(out=tmpC, in0=tmpC, in1=bc_last(ta, 129), op=ALU.mult), reads=[b_tc, b_sm], writes=[b_tc])
                    S.op("dve", lambda e: e.tensor_tensor(out=cj, in0=cj, in1=bc_last(tb, 129), op=ALU.mult), reads=[bcj, b_sm], writes=[bcj])
                    S.op("dve", lambda e: e.tensor_tensor(out=tmpC, in0=tmpC, in1=cj, op=ALU.add), reads=[b_tc, bcj], writes=[b_tc])
                    S.op("dve", lambda e: e.tensor_tensor(out=cm, in0=mx, in1=nfj, op=ALU.subtract), reads=[b_sm], writes=[b_sm])
            if mode != "A":
                S.op("dve", lambda e: e.tensor_tensor(out=Rr, in0=R0, in1=bc_m(mst), op=ALU.max), reads=[b_g, b_sm], writes=[b_g])
                S.op("dve", lambda e: e.tensor_tensor(out=t1, in0=g, in1=Rr, op=ALU.subtract), reads=[b_g], writes=[b_g])
                S.op("act", lambda e: e.activation(out=wv, in_=t1, func=AF.Exp, bias=LN_S), reads=[b_g], writes=[b_g])
                S.op("dve", lambda e: e.tensor_tensor(out=t1, in0=nF, in1=Rr, op=ALU.subtract), reads=[b_g], writes=[b_g])
                S.op("act", lambda e: e.activation(out=clamp, in_=t1, func=AF.Exp), reads=[b_g], writes=[b_g])
                S.op("dve", lambda e: e.tensor_tensor(out=t1, in0=R0, in1=Rr, op=ALU.subtract), reads=[b_g], writes=[b_g])
                S.op("act", lambda e: e.activation(out=ep, in_=t1, func=AF.Exp), reads=[b_g], writes=[b_g])
                S.op("dve", lambda e: e.tensor_tensor(out=t1[:, 1:16, :], in0=Rr[:, 0:15, :], in1=Rr[:, 1:16, :], op=ALU.subtract), reads=[b_g], writes=[b_g])
                S.op("dve", lambda e: e.tensor_tensor(out=t1[:, 0, :], in0=mst, in1=Rr[:, 0, :], op=ALU.subtract), reads=[b_g, b_sm], writes=[b_g])
                S.op("act", lambda e: e.activation(out=dl, in_=t1, func=AF.Exp), reads=[b_g], writes=[b_g])
                S.op("dve", lambda e: e.tensor_copy(out=Ct, in_=tmpC), reads=[b_tc], writes=[b_ct])
                b_cin = b_kt
                for m in range(16):
                    S.op("dve", lambda e: e.tensor_tensor(out=tmpC, in0=Ct, in1=bc_last(dl[:, m, :], 129), op=ALU.mult), reads=[b_ct, b_g], writes=[b_tc])
                    S.op("act", lambda e: e.activation(out=Cin[:, m * 4:m * 4 + 4, :], in_=tmpC, func=AF.Copy), reads=[b_tc], writes=b_cin)
                    S.op("dve", lambda e: e.tensor_tensor(out=Ct, in0=CL[:, m * 4:m * 4 + 4, :], in1=bc_last(ep[:, m, :], 129), op=ALU.mult),
                         reads=[b_cl[m], b_g], writes=[b_ct])
                    S.op("dve", lambda e: e.tensor_tensor(out=Ct, in0=Ct, in1=tmpC, op=ALU.add), reads=[b_ct, b_tc], writes=[b_ct])
                k.pti = 0
                for m in range(16):
                    cs = slice(m * 128, (m + 1) * 128)
                    for h in range(4):
                        bankS, bbS = next_bank()
                        S.op("pe", lambda e: e.matmul(bankS[:, 0:128], lhsT=QK[:, 4 + h, cs], rhs=QK[:, h, cs], start=True, stop=True),
                             reads=[b_qk[4 + h], b_qk[h]], writes=[bbS])
                        pi = k.pti % 2; k.pti += 1
                        S.op("dve", lambda e: e.scalar_tensor_tensor(out=PT[pi], in0=bankS[:, 0:128], scalar=wv[:, m, h:h + 1], in1=tri_f,
                                                                     op0=ALU.mult, op1=ALU.mult), reads=[bbS, b_g, b_cst], writes=[b_pt[pi]])
                        bankN, bbN = next_bank()
                        S.op("pe", lambda e: e.matmul(bankN[:, 0:129], lhsT=PT[pi], rhs=VA[:, m, h, :], start=True, stop=False), inc=False,
                             reads=[b_pt[pi], b_va[m]], writes=[bbN])
                        S.op("pe", lambda e: e.matmul(bankN[:, 0:129], lhsT=QK[:, h, cs], rhs=Cin[:, m * 4 + h, :], start=False, stop=True),
                             reads=[b_qk[h]] + b_cin, writes=[bbN])
                        S.op("act", lambda e: e.activation(out=dm[:, h:h + 1], in_=bankN[:, 128:129], func=AF.Abs), reads=[bbN], writes=[b_sm])
                        S.op("dve", lambda e: e.tensor_scalar(out=dm[:, h:h + 1], in0=dm[:, h:h + 1], scalar1=clamp[:, m, h:h + 1], scalar2=None,
                                                              op0=ALU.max), reads=[b_sm, b_g], writes=[b_sm])
                        S.op("dve", lambda e: e.reciprocal(out=rec[:, h:h + 1], in_=dm[:, h:h + 1]), reads=[b_sm], writes=[b_sm])
                        S.op("dve", lambda e: e.scalar_tensor_tensor(out=hbuf[h], in0=bankN[:, 0:128], scalar=rec[:, h:h + 1], in1=OG[:, m, h * 128:(h + 1) * 128],
                                                                     op0=ALU.mult, op1=ALU.mult), reads=[bbN, b_sm, b_og[m]], writes=[b_hb[h]])
                        S.op("act", lambda e: e.activation(out=junk2, in_=hbuf[h], func=AF.Square, accum_out=ss[:, h:h + 1]),
                             reads=[b_hb[h]], writes=[b_hn[0], b_sm])
                    S.op("dve", lambda e: e.tensor_scalar(out=rs4, in0=ss, scalar1=1.0 / 128, scalar2=EPS, op0=ALU.mult, op1=ALU.add), reads=[b_sm], writes=[b_sm])
                    S.op("act", lambda e: e.activation(out=rs4, in_=rs4, func=AF.Sqrt), reads=[b_sm], writes=[b_sm])
                    S.op("dve", lambda e: e.reciprocal(out=rs4, in_=rs4), reads=[b_sm], writes=[b_sm])
                    bankO, bbO = next_bank()
                    po = bankO[:, 0:256].bitcast(BF16).rearrange("p (h t) -> p h t", h=4)
                    for h in range(4):
                        hi = h % 2
                        S.op("dve", lambda e: e.tensor_scalar(out=hn[hi], in0=hbuf[h], scalar1=rs4[:, h:h + 1], scalar2=None, op0=ALU.mult),
                             reads=[b_hb[h], b_sm], writes=[b_hn[hi]])
                        S.op("pe", lambda e: e.transpose(po[:, h, :], hn[hi], ident_b), reads=[b_hn[hi], b_cst], writes=[bbO])
                    S.op("dve", lambda e: e.tensor_tensor(out=YC[:, 4:8, cs], in0=po, in1=bc_last(mlg, 128), op=ALU.mult), reads=[bbO, b_prm], writes=[b_yc[m]])

            if cfg.get("debug"):
                S.dma("sp", dbq, dr["dbg_g"].rearrange("p (a c) -> p a c", a=16), view(MS, 0, 4096, F32).rearrange("p (a c) -> p a c", a=16), reads=[b_g, b_sm], writes=[b_dbg])

            if stop == "ml":
                quiesce(); return
            TCH = 16; NC_ = 128
            WZ = view(A0, 0, 32 * KB, BF16).rearrange("p (q i x n) -> p q i x n", q=4, i=16, x=2)
            BD = view(A0, 32 * KB, 16 * KB, BF16).rearrange("p (q j n) -> p q j n", q=4, j=16)
            CP = view(A2, 0, 34816, BF16).rearrange("p (j r x n) -> p j r x n", j=17, r=16, x=2)
            EC = view(A2, 34816, 8192, F32).rearrange("p (r c) -> p r c", r=16)
            ES = view(A2, 34816 + 8192, 8192, F32).rearrange("p (r c) -> p r c", r=16)
            ZL = view(A2, 51200, 16384, F32).rearrange("p (r x c) -> p r x c", r=16, x=2)
            ZS = view(A2, 67584, 8256, BF16).rearrange("p (r x c) -> p r x c", r=16, x=2)
            SP_ = A2 + 76032
            PW = view(SP_, 0, 2176, F32).rearrange("p (j x r) -> p j x r", j=17, x=2)
            def sc(i):
                return view(SP_, 2176 + i * 64, 64, F32)
            assert SP_ + 2176 + 30 * 64 <= A2 + A2_B
            WW = view(A0, 0, 16384, F32).rearrange("p (r x c) -> p r x c", r=16, x=2)
            ZG = view(A2, 51200, 16384, BF16).rearrange("p (q i c) -> p q i c", q=4, i=16)
            GT = A0 + 16 * KB
            GEN = A2 + 51200
            b_s5 = S.buf("s5gen")
            b_wz = S.bufs(4, "wz"); b_bd = S.bufs(4, "bd"); b_cp = S.buf("cp"); b_tab = S.buf("tab")
            b_zl = S.bufs(16, "zl"); b_ww = S.bufs(16, "ww"); b_zs = S.bufs(16, "zs"); b_zg = S.bufs(4, "zg")
            olds = [b_g, b_sm] + b_kt + b_cl + [b_ct, b_tc] + b_vw + b_pt + b_hb + b_hn + b_va + [b_gr] + b_qk + b_og
            handoff([b_s5, b_cp, b_tab] + b_wz + b_bd + b_zl + b_ww + b_zs + b_zg, olds)
            s5q = S.dma_sem(f"s5q{l}")
            (s_are, s_aim, s_dt, s_mag, s_th, s_t, s_sin, s_cos, s_abr, s_abi, s_den, s_zr, s_sre, s_sim, s_t2, s_magL,
             s_l128r, s_l128i, s_t3, s_m128) = [sc(i) for i in range(20)]
            zend = sc(20)[:, 0:16]
            zend = view(SP_, 2176 + 20 * 64, 128, F32).rearrange("p (r x) -> p r x", x=2)
            sst = view(SP_, 2176 + 22 * 64, 128, F32).rearrange("p (r x) -> p r x", x=2)

            def TT(out, in0, in1, op, rd=(), wr=None, eng="dve"):
                S.op(eng, lambda e: e.tensor_tensor(out=out, in0=in0, in1=in1, op=op), reads=[b_s5] + list(rd), writes=[b_s5] if wr is None else wr)

            def TS(out, in0, s1, s2, op0, op1=None, rd=(), wr=None):
                if op1 is None:
                    S.op("dve", lambda e: e.tensor_scalar(out=out, in0=in0, scalar1=s1, scalar2=None, op0=op0), reads=[b_s5] + list(rd), writes=[b_s5] if wr is None else wr)
                else:
                    S.op("dve", lambda e: e.tensor_scalar(out=out, in0=in0, scalar1=s1, scalar2=s2, op0=op0, op1=op1), reads=[b_s5] + list(rd), writes=[b_s5] if wr is None else wr)

            def AC(out, in_, func, rd=(), wr=None, **kw):
                S.op("act", lambda e: e.activation(out=out, in_=in_, func=func, **kw), reads=[b_s5] + list(rd), writes=[b_s5] if wr is None else wr)

            def cmul(o_r, o_i, a_r, a_i, b_r, b_i, t1_, t2_, rd=(), wr=None, neg_im=False):
                TT(t1_, a_r, b_r, ALU.mult, rd); TT(t2_, a_i, b_i, ALU.mult, rd)
                TT(o_r, t1_, t2_, ALU.subtract, rd, wr)
                TT(t1_, a_r, b_i, ALU.mult, rd); TT(t2_, a_i, b_r, ALU.mult, rd)
                if neg_im:
                    TT(t1_, t1_, t2_, ALU.add, rd)
                    TS(o_i, t1_, -1.0, None, ALU.mult, rd=rd, wr=wr)
                else:
                    TT(o_i, t1_, t2_, ALU.add, rd, wr)

            if IMP:
                S.mute = True
            araw = view(GEN, 0, 1024, F32)
            S.dma("sp", s5q, araw[0:16, 0:128], dr["s5_a_re"][l].rearrange("(r gl) n -> r (gl n)", gl=2), writes=[b_s5])
            S.dma("sp", s5q, araw[0:16, 128:256], dr["s5_a_im"][l].rearrange("(r gl) n -> r (gl n)", gl=2), writes=[b_s5])
            ldt = dr["s5_log_dt"][l]
            for gl in range(2):
                S.dma("sp", s5q, s_dt[gl * 64:(gl + 1) * 64, :], bass.AP(ldt.tensor, ldt.offset + gl, [[0, 64], [2, 16]]), writes=[b_s5])
            Bsm = [view(GEN, 1024 + x * 1024, 1024, F32).rearrange("p (r c) -> p r c", r=16) for x in range(2)]
            for x, nm in enumerate(["s5_b_re", "s5_b_im"]):
                bsrc = dr[nm][l]
                for gl in range(2):
                    S.dma("sp", s5q, Bsm[x][gl * 64:(gl + 1) * 64, :, :],
                          bass.AP(bsrc.tensor, bsrc.offset + gl * 1024, [[16, 64], [2048, 16], [1, 16]]), writes=[b_s5])
            Craw = [view(GEN, 3072 + x * 1024, 1024, F32).rearrange("p (q n) -> p q n", q=4) for x in range(2)]
            for x, nm in enumerate(["s5_c_re", "s5_c_im"]):
                csrc = dr[nm][l]
                S.dma("sp", s5q, Craw[x], bass.AP(csrc.tensor, csrc.offset, [[64, 128], [8192, 4], [1, 64]]), writes=[b_s5])
            S.seal(s5q, [b_s5])
            bank, bb = next_bank()
            S.op("pe", lambda e: e.transpose(bank[:, 0:16], araw[0:16, 0:128], ident_f[0:16, 0:16]), reads=[b_s5, b_cst], writes=[bb])
            S.op("pe", lambda e: e.transpose(bank[:, 16:32], araw[0:16, 128:256], ident_f[0:16, 0:16]), reads=[b_s5, b_cst], writes=[bb])
            S.op("dve", lambda e: e.tensor_copy(out=s_are, in_=bank[:, 0:16]), reads=[bb], writes=[b_s5])
            S.op("dve", lambda e: e.tensor_copy(out=s_aim, in_=bank[:, 16:32]), reads=[bb], writes=[b_s5])
            PI = float(np.pi)
            AC(s_dt, s_dt, AF.Exp)
            TT(s_t, s_are, s_dt, ALU.mult)
            AC(s_mag, s_t, AF.Exp)
            AC(s_magL, s_t, AF.Exp, scale=float(TCH))
            AC(s_m128, s_t, AF.Exp, scale=float(TCH * NC_))
            TT(s_th, s_aim, s_dt, ALU.mult)
            for thr in (1.0, 3.0, 5.0, 7.0):
                TS(s_t, s_th, thr * PI, -2.0 * PI, ALU.is_gt, ALU.mult)
                if thr == 1.0:
                    TT(s_t2, s_th, s_t, ALU.add)
                else:
                    TT(s_t2, s_t2, s_t, ALU.add)
            AC(s_sin, s_t2, AF.Sin)
            TS(s_t3, s_t2, 0.5 * PI, None, ALU.add)
            TS(s_t, s_t3, PI, -2.0 * PI, ALU.is_gt, ALU.mult)
            TT(s_t3, s_t3, s_t, ALU.add)
            AC(s_cos, s_t3, AF.Sin)
            TT(s_abr, s_mag, s_cos, ALU.mult); TT(s_abi, s_mag, s_sin, ALU.mult)
            TT(s_t, s_are, s_are, ALU.mult); TT(s_t2, s_aim, s_aim, ALU.mult); TT(s_den, s_t, s_t2, ALU.add)
            S.op("dve", lambda e: e.reciprocal(out=s_den, in_=s_den), reads=[b_s5], writes=[b_s5])
            TS(s_zr, s_abr, -1.0, None, ALU.add)
            TT(s_t, s_zr, s_are, ALU.mult); TT(s_t2, s_abi, s_aim, ALU.mult); TT(s_t, s_t, s_t2, ALU.add); TT(s_sre, s_t, s_den, ALU.mult)
            TT(s_t, s_abi, s_are, ALU.mult); TT(s_t2, s_zr, s_aim, ALU.mult); TT(s_t, s_t, s_t2, ALU.subtract); TT(s_sim, s_t, s_den, ALU.mult)
            S.op("dve", lambda e: e.memset(PW[:, 0, 0, :], 1.0), reads=[b_s5], writes=[b_s5])
            S.op("dve", lambda e: e.memset(PW[:, 0, 1, :], 0.0), reads=[b_s5], writes=[b_s5])
            S.op("dve", lambda e: e.tensor_copy(out=PW[:, 1, 0, :], in_=s_abr), reads=[b_s5], writes=[b_s5])
            S.op("dve", lambda e: e.tensor_copy(out=PW[:, 1, 1, :], in_=s_abi), reads=[b_s5], writes=[b_s5])
            pt1 = view(GEN, 5120, 1024, F32).rearrange("p (j r) -> p j r", r=16)
            pt2 = view(GEN, 6144, 1024, F32).rearrange("p (j r) -> p j r", r=16)
            kk_ = 1
            while kk_ < 16:
                def bj(a):
                    return bass.AP(a.tensor, a.offset, [list(a.ap[0]), [0, kk_], list(a.ap[1])])
                cmul(PW[:, kk_ + 1:2 * kk_ + 1, 0, :], PW[:, kk_ + 1:2 * kk_ + 1, 1, :], PW[:, 1:kk_ + 1, 0, :], PW[:, 1:kk_ + 1, 1, :],
                     bj(PW[:, kk_, 0, :]), bj(PW[:, kk_, 1, :]), pt1[:, 0:kk_, :], pt2[:, 0:kk_, :])
                kk_ *= 2
            S.op("dve", lambda e: e.reciprocal(out=s_t, in_=s_magL), reads=[b_s5], writes=[b_s5])
            TT(EC[:, :, 0], PW[:, 16, 0, :], s_t, ALU.mult, wr=[b_s5, b_tab]); TT(ES[:, :, 0], PW[:, 16, 1, :], s_t, ALU.mult, wr=[b_s5, b_tab])
            et1 = view(GEN, 7168, 4096, F32).rearrange("p (r c) -> p r c", r=16)
            et2 = view(GEN, 11264, 4096, F32).rearrange("p (r c) -> p r c", r=16)
            kk_ = 1
            while kk_ < NC_:
                cmul(EC[:, :, kk_:2 * kk_], ES[:, :, kk_:2 * kk_], EC[:, :, 0:kk_], ES[:, :, 0:kk_],
                     bc_last(EC[:, :, kk_ - 1], kk_), bc_last(ES[:, :, kk_ - 1], kk_), et1[:, :, 0:kk_], et2[:, :, 0:kk_], rd=[b_tab], wr=[b_s5, b_tab])
                kk_ *= 2
            TT(s_l128r, EC[:, :, NC_ - 1], s_m128, ALU.mult, rd=[b_tab]); TT(s_l128i, ES[:, :, NC_ - 1], s_m128, ALU.mult, rd=[b_tab])
            Cin_ = [view(GEN, 5120 + x * 2048, 2048, F32).rearrange("p (q n) -> p q n", q=4) for x in range(2)]
            Cp = [view(GEN, 9216 + x * 2048, 2048, F32).rearrange("p (r n) -> p r n", r=16) for x in range(2)]
            ct1 = view(GEN, 13312, 2048, F32).rearrange("p (r n) -> p r n", r=16)
            ct2 = view(A0, 0, 2048, F32).rearrange("p (r n) -> p r n", r=16)
            for x in range(2):
                TS(Cin_[x][:, :, 0:64], Craw[x], par01[:, 0:1], None, ALU.mult, rd=[b_cst])
                TS(Cin_[x][:, :, 64:128], Craw[x], par01[:, 1:2], None, ALU.mult, rd=[b_cst])
                bank, bb = next_bank()
                for q in range(4):
                    S.op("pe", lambda e: e.transpose(bank[:, q * 128:(q + 1) * 128], Cin_[x][:, q, :], ident_f), inc=(q == 3), reads=[b_s5, b_cst], writes=[bb])
                S.op("dve", lambda e: e.tensor_copy(out=Cp[x].rearrange("p r n -> p (r n)"), in_=bank[:, :]), reads=[bb], writes=[b_s5])
            for j in range(17):
                pr = bc_last(PW[:, j, 0, :], 32); pi_ = bc_last(PW[:, j, 1, :], 32)
                TT(ct1, Cp[0], pr, ALU.mult); TT(ct2, Cp[1], pi_, ALU.mult, rd=b_wz, wr=[b_s5] + b_wz)
                TT(CP[:, j, :, 0, :], ct1, ct2, ALU.subtract, wr=[b_s5, b_cp])
                TT(ct1, Cp[0], pi_, ALU.mult); TT(ct2, Cp[1], pr, ALU.mult, rd=b_wz, wr=[b_s5] + b_wz)
                S.op("dve", lambda e: e.scalar_tensor_tensor(out=CP[:, j, :, 1, :], in0=ct1, scalar=-1.0, in1=ct2, op0=ALU.mult, op1=ALU.subtract),
                     reads=[b_s5], writes=[b_s5, b_cp])

            BB = [view(GEN, 5120 + x * 2048, 2048, F32).rearrange("p (r n) -> p r n", r=16) for x in range(2)]
            BBb = [view(GEN, 9216 + x * 1024, 1024, BF16).rearrange("p (r n) -> p r n", r=16) for x in range(2)]
            bt1 = view(GEN, 11264, 1024, F32).rearrange("p (r c) -> p r c", r=16)
            bt2 = view(GEN, 12288, 1024, F32).rearrange("p (r c) -> p r c", r=16)
            for x in range(2):
                S.op("dve", lambda e: e.memset(BB[x], 0.0), reads=[b_s5], writes=[b_s5])
            sre_b = bc_last(s_sre, 16); sim_b = bc_last(s_sim, 16)
            TT(bt1, Bsm[0], sre_b, ALU.mult); TT(bt2, Bsm[1], sim_b, ALU.mult)
            for gl in range(2):
                ps_ = slice(gl * 64, (gl + 1) * 64)
                TT(BB[0][ps_, :, gl * 16:(gl + 1) * 16], bt1[ps_], bt2[ps_], ALU.subtract)
            TT(bt1, Bsm[1], sre_b, ALU.mult); TT(bt2, Bsm[0], sim_b, ALU.mult)
            for gl in range(2):
                ps_ = slice(gl * 64, (gl + 1) * 64)
                TT(BB[1][ps_, :, gl * 16:(gl + 1) * 16], bt1[ps_], bt2[ps_], ALU.add)
            for x in range(2):
                S.op("dve", lambda e: e.tensor_copy(out=BBb[x], in_=BB[x]), reads=[b_s5], writes=[b_s5])
            bdt = view(GEN, 13312, 512, F32)
            for j in range(16):
                bank, bb = next_bank()
                for q in range(4):
                    for x in range(2):
                        S.op("pe", lambda e: e.matmul(bank[:, q * 128:(q + 1) * 128], lhsT=BBb[x][:, 4 * q:4 * q + 4, :].rearrange("p r n -> p (r n)"),
                                                      rhs=CP[:, j, 4 * q:4 * q + 4, x, :], start=(x == 0), stop=(x == 1)), inc=(q == 3 and x == 1),
                             reads=[b_s5, b_cp], writes=[bb])
                if j == 0:
                    for q in range(4):
                        S.op("dve", lambda e: e.tensor_tensor(out=bdt, in0=bank[:, q * 128:(q + 1) * 128], in1=bdm, op=ALU.mult), reads=[bb, b_cst, b_s5], writes=[b_s5])
                        S.op("dve", lambda e: e.scalar_tensor_tensor(out=BD[:, q, 0, :], in0=ident_f, scalar=dcol[:, q:q + 1], in1=bdt, op0=ALU.mult, op1=ALU.add),
                             reads=[b_s5, b_cst, b_prm], writes=[b_bd[q]])
                else:
                    bdm_b = bass.AP(bdm.tensor, bdm.offset, [list(bdm.ap[0]), [0, 4], list(bdm.ap[1])])
                    S.op("dve", lambda e: e.tensor_tensor(out=BD[:, :, j, :], in0=bank[:, :].rearrange("p (q n) -> p q n", q=4), in1=bdm_b, op=ALU.mult),
                         reads=[bb, b_cst], writes=b_bd)
            mt1 = view(GEN, 13824, 2048, F32).rearrange("p (r n) -> p r n", r=16)
            mt2 = view(GEN, 1024, 2048, F32).rearrange("p (r n) -> p r n", r=16)
            MB = [view(GEN, 3072 + x * 1024, 1024, BF16).rearrange("p (r n) -> p r n", r=16) for x in range(2)]
            for i in range(16):
                j = 15 - i
                pr = bc_last(PW[:, j, 0, :], 32); pi_ = bc_last(PW[:, j, 1, :], 32)
                TT(mt1, BB[0], pr, ALU.mult); TT(mt2, BB[1], pi_, ALU.mult); TT(MB[0], mt1, mt2, ALU.subtract)
                TT(mt1, BB[0], pi_, ALU.mult); TT(mt2, BB[1], pr, ALU.mult); TT(MB[1], mt1, mt2, ALU.add)
                bank, bb = next_bank()
                pb = bank[:, :].bitcast(BF16).rearrange("p (q x n) -> p q x n", q=4, x=2)
                for q in range(4):
                    for x in range(2):
                        S.op("pe", lambda e: e.transpose(pb[:, q, x, :], MB[x][:, 4 * q:4 * q + 4, :].rearrange("p r n -> p (r n)"), ident_b),
                             inc=(q == 3 and x == 1), reads=[b_s5, b_cst], writes=[bb])
                S.op("act", lambda e: e.activation(out=WZ[:, :, i, :, :], in_=pb, func=AF.Copy), reads=[bb], writes=b_wz)
            if stop == "s5gen":
                quiesce(); return
            handoff(b_zl, b_zl + [b_s5])
            for q in range(4):
                for rr in range(4):
                    r = 4 * q + rr
                    bank, bb = next_bank()
                    for x in range(2):
                        col = x * 128
                        for i in range(16):
                            S.op("pe", lambda e: e.matmul(bank[:, col:col + 128], lhsT=WZ[32 * rr:32 * rr + 32, q, i, x, :], rhs=US[32 * rr:32 * rr + 32, q, i, :],
                                                          start=(i == 0), stop=(i == 15), tile_position=(32 * rr, 0)), inc=(i == 15 and x == 1),
                                 reads=[b_wz[q], b_us[q]], writes=[bb])
                    S.op("act", lambda e: e.activation(out=ZL[:, r, :, :].rearrange("p x c -> p (x c)"), in_=bank[:, 0:256], func=AF.Copy),
                         reads=[bb], writes=[b_zl[r]])
            if cfg.get("debug"):
                S.dma("sp", dbq, dr["dbg_zl"], view(A2, 51200, 16384, F32), reads=b_zl, writes=[b_dbg])
                S.dma("sp", dbq, dr["dbg_sc"], view(SP_, 0, 4864, F32), reads=[b_s5], writes=[b_dbg])
                S.dma("sp", dbq, dr["dbg_cp"], view(A2, 0, 34816, BF16), reads=[b_cp], writes=[b_dbg])
                S.dma("sp", dbq, dr["dbg_bd"], view(A0, 32 * KB, 16 * KB, BF16), reads=b_bd, writes=[b_dbg])
                S.dma("sp", dbq, dr["dbg_wz"], view(A0, 0, 32 * KB, BF16), reads=b_wz, writes=[b_dbg])
            handoff(b_ww, b_wz + b_ww)
            dt1 = view(GT, 0, 8192, F32).rearrange("p (r c) -> p r c", r=16)
            dt2 = view(GT, 8192, 8192, F32).rearrange("p (r c) -> p r c", r=16)
            b_dt = S.buf("dt"); handoff([b_dt], b_wz)
            magL_b = bc_last(s_magL, NC_)

            def scan_and_mod(init_ap, b_init, final):
                for r in range(16):
                    for x in range(2):
                        ini = 0.0 if init_ap is None else init_ap[:, r, x:x + 1]
                        S.op("dve", lambda e: e.tensor_tensor_scan(out=ZL[:, r, x, :], data0=magL_b[:, r, :], data1=WW[:, r, x, :], initial=ini,
                                                                   op0=ALU.mult, op1=ALU.add), reads=[b_ww[r], b_s5] + ([b_init] if b_init else []), writes=[b_zl[r]])
                if not final:
                    cmul(zend[:, :, 0], zend[:, :, 1], ZL[:, :, 0, NC_ - 1], ZL[:, :, 1, NC_ - 1], EC[:, :, NC_ - 1], ES[:, :, NC_ - 1], s_t, s_t2,
                         rd=b_zl + [b_tab])
                else:
                    S.op("dve", lambda e: e.tensor_tensor(out=dt1, in0=EC, in1=ZL[:, :, 0, :], op=ALU.mult), reads=[b_tab] + b_zl, writes=[b_dt])
                    S.op("dve", lambda e: e.tensor_tensor(out=dt2, in0=ES, in1=ZL[:, :, 1, :], op=ALU.mult), reads=[b_tab] + b_zl, writes=[b_dt])
                    S.op("dve", lambda e: e.tensor_tensor(out=ZS[:, :, 0, 1:NC_ + 1], in0=dt1, in1=dt2, op=ALU.subtract), reads=[b_dt], writes=b_zs)
                    S.op("dve", lambda e: e.tensor_tensor(out=dt1, in0=EC, in1=ZL[:, :, 1, :], op=ALU.mult), reads=[b_tab] + b_zl, writes=[b_dt])
                    S.op("dve", lambda e: e.tensor_tensor(out=dt2, in0=ES, in1=ZL[:, :, 0, :], op=ALU.mult), reads=[b_tab] + b_zl, writes=[b_dt])
                    S.op("dve", lambda e: e.tensor_tensor(out=ZS[:, :, 1, 1:NC_ + 1], in0=dt1, in1=dt2, op=ALU.add), reads=[b_dt], writes=b_zs)
                    S.op("dve", lambda e: e.tensor_copy(out=ZS[:, :, :, 0], in_=init_ap), reads=[b_init], writes=b_zs)

            S.op("dve", lambda e: e.tensor_tensor(out=dt1, in0=EC, in1=ZL[:, :, 0, :], op=ALU.mult), reads=[b_tab] + b_zl, writes=[b_dt])
            S.op("dve", lambda e: e.tensor_tensor(out=dt2, in0=ES, in1=ZL[:, :, 1, :], op=ALU.mult), reads=[b_tab] + b_zl, writes=[b_dt])
            S.op("dve", lambda e: e.tensor_tensor(out=WW[:, :, 0, :], in0=dt1, in1=dt2, op=ALU.add), reads=[b_dt], writes=b_ww)
            S.op("dve", lambda e: e.tensor_tensor(out=dt1, in0=EC, in1=ZL[:, :, 1, :], op=ALU.mult), reads=[b_tab] + b_zl, writes=[b_dt])
            S.op("dve", lambda e: e.tensor_tensor(out=dt2, in0=ES, in1=ZL[:, :, 0, :], op=ALU.mult), reads=[b_tab] + b_zl, writes=[b_dt])
            S.op("dve", lambda e: e.tensor_tensor(out=WW[:, :, 1, :], in0=dt1, in1=dt2, op=ALU.subtract), reads=[b_dt], writes=b_ww)
            scan_and_mod(None, None, False)
            xfer("e_ww", view(A0, 0, 16384, F32), b_ww)
            xfer("e_cp", view(A2, 0, 34816, F32), [b_cp])
            xfer("e_bd", view(A0, 32 * KB, 16 * KB, F32), b_bd)
            xfer("e_tab", view(A2, 34816, 16384, F32), [b_tab])
            xfer("e_sp", view(SP_, 0, 4864, F32), [b_s5])
            if IMP:
                S.mute = False
            if cfg.get("debug"):
                S.dma("sp", dbq, dr["dbg_zend"], zend.rearrange("p r x -> p (r x)"), reads=[b_s5], writes=[b_dbg])
            if mode == "A":
                S.dma("sp", oq, dr["summ"][:, 0:32], zend.rearrange("p r x -> p (r x)"), reads=[b_s5], writes=[b_out])
                S.wait_all("sp", [b_out])
            if mode != "A":
                sa = dr["summ_all"]
                zall = view(GT, 0, 1024, F32).rearrange("p (j r x) -> p j r x", j=8, x=2)
                zq = S.dma_sem(f"zq{l}")
                S.dma("sp", zq, zall.rearrange("p j r x -> p j (r x)"), sa[:, :, 0:32].rearrange("j p c -> p j c"), reads=[b_dt], writes=[b_dt])
                S.op("dve", lambda e: e.memset(sst, 0.0), reads=[b_s5], writes=[b_s5])
                ctr = sc(24)[:, 0:16]; cti = sc(25)[:, 0:16]
                for j in range(8):
                    cmul(ctr, cti, sst[:, :, 0], sst[:, :, 1], s_l128r, s_l128i, s_t, s_t2)
                    TT(ctr, ctr, zall[:, j, :, 0], ALU.add, rd=[b_dt]); TT(cti, cti, zall[:, j, :, 1], ALU.add, rd=[b_dt])
                    TT(ctr, ctr, sst[:, :, 0], ALU.subtract); TT(cti, cti, sst[:, :, 1], ALU.subtract)
                    S.op("dve", lambda e: e.scalar_tensor_tensor(out=sst[:, :, 0], in0=ctr, scalar=pred[:, j:j + 1], in1=sst[:, :, 0], op0=ALU.mult, op1=ALU.add),
                         reads=[b_s5, b_cst], writes=[b_s5])
                    S.op("dve", lambda e: e.scalar_tensor_tensor(out=sst[:, :, 1], in0=cti, scalar=pred[:, j:j + 1], in1=sst[:, :, 1], op0=ALU.mult, op1=ALU.add),
                         reads=[b_s5, b_cst], writes=[b_s5])
                scan_and_mod(sst, b_s5, True)
                if cfg.get("debug"):
                    S.dma("sp", dbq, dr["dbg_zs"], view(A2, 67584, 8256, BF16), reads=b_zs, writes=[b_dbg])
                handoff(b_zg, b_zl + b_zg)
                for q in range(4):
                    for ib in range(4):
                        bank, bb = next_bank()
                        for i4 in range(4):
                            ip = ib * 4 + i4
                            col = i4 * 128
                            for i in range(ip + 1):
                                S.op("pe", lambda e: e.matmul(bank[:, col:col + 128], lhsT=BD[:, q, ip - i, :], rhs=US[:, q, i, :], start=(i == 0), stop=False),
                                     inc=False, reads=[b_bd[q], b_us[q]], writes=[bb])
                            for rr in range(4):
                                r = 4 * q + rr
                                for x in range(2):
                                    lastw = (rr == 3 and x == 1)
                                    S.op("pe", lambda e: e.matmul(bank[32 * rr:32 * rr + 32, col:col + 128], lhsT=CP[:, ip + 1, r, x, :], rhs=ZS[:, r, x, 0:NC_],
                                                                  start=False, stop=lastw, tile_position=(0, 32 * rr)), inc=(lastw and i4 == 3),
                                         reads=[b_cp, b_zs[r]], writes=[bb])
                        S.op("act", lambda e: e.activation(out=ZG[:, q, ib * 4:ib * 4 + 4, :].rearrange("p i c -> p (i c)"), in_=bank[:, :], func=AF.Gelu_apprx_tanh),
                             reads=[bb], writes=[b_zg[q]])
                if cfg.get("debug"):
                    S.dma("sp", dbq, dr["dbg_zg"], view(A2, 51200, 16384, BF16), reads=b_zg, writes=[b_dbg])
                wg = dr["s5_w_glu"][l].rearrange("(k p) n -> p k n", p=128)
                wgl, bwg = wload(wg, 4, 512)
                gate = view(GT, 0, 2048, F32)
                ZZ = [view(GT, 2048 + ft * 2048, 2048, F32) for ft in range(4)]
                sqb = [view(GT, 10240 + i * 1024, 1024, BF16) for i in range(2)]
                rst = view(GT, 12288, 2048, F32)
                b_gate = S.buf("gate"); b_zz = S.bufs(4, "zz"); b_sqb = S.bufs(2, "sqb"); b_rst = S.buf("rst")
                handoff([b_gate, b_rst] + b_zz + b_sqb, [b_dt] + b_ww)
                YCv = YC[:, 0:4, :].rearrange("p f (c i) -> p f i c", i=16)
                for cb in range(4):
                    bankq, bbq = next_bank()
                    for ft in range(4):
                        bank, bb = next_bank()
                        for kk in range(4):
                            S.op("pe", lambda e: e.matmul(bank[:, :], lhsT=wgl[:, kk, ft * 128:(ft + 1) * 128], rhs=ZG[:, kk, cb * 4:cb * 4 + 4, :],
                                                          start=(kk == 0), stop=(kk == 3)), inc=(kk == 3), reads=[bwg] + b_zg, writes=[bb])
                        S.op("act", lambda e: e.activation(out=gate, in_=bank[:, :], func=AF.Sigmoid, bias=bglu[:, ft:ft + 1]), reads=[bb, b_prm], writes=[b_gate])
                        S.op("dve", lambda e: e.tensor_tensor(out=ZZ[ft], in0=ZG[:, ft, cb * 4:cb * 4 + 4, :].rearrange("p i c -> p (i c)"), in1=gate, op=ALU.mult),
                             reads=[b_zg[ft], b_gate], writes=[b_zz[ft]])
                        S.op("act", lambda e: e.activation(out=sqb[ft % 2], in_=ZZ[ft], func=AF.Square), reads=[b_zz[ft]], writes=[b_sqb[ft % 2]])
                        S.op("pe", lambda e: e.matmul(bankq[:, :], lhsT=ones_b, rhs=sqb[ft % 2], start=(ft == 0), stop=(ft == 3)), inc=True,
                             reads=[b_sqb[ft % 2], b_cst], writes=[bbq])
                    S.op("dve", lambda e: e.tensor_scalar(out=rst, in0=bankq[:, :], scalar1=1.0 / 512, scalar2=EPS, op0=ALU.mult, op1=ALU.add), reads=[bbq], writes=[b_rst])
                    S.op("act", lambda e: e.activation(out=rst, in_=rst, func=AF.Sqrt), reads=[b_rst], writes=[b_rst])
                    S.op("dve", lambda e: e.reciprocal(out=rst, in_=rst), reads=[b_rst], writes=[b_rst])
                    for ft in range(4):
                        S.op("dve", lambda e: e.scalar_tensor_tensor(out=YCv[:, ft, cb * 4:cb * 4 + 4, :], in0=ZZ[ft].rearrange("p (i c) -> p i c", i=4),
                                                                     scalar=outg[:, ft:ft + 1], in1=rst.rearrange("p (i c) -> p i c", i=4), op0=ALU.mult, op1=ALU.mult),
                             reads=[b_zz[ft], b_rst, b_prm], writes=b_yc)
                if cfg.get("debug"):
                    S.dma("sp", dbq, dr["dbg_yc"], view(A1, 0, 32 * KB, BF16), reads=b_yc, writes=[b_dbg])

            if mode != "A":
                if stop == "s5":
                    quiesce(); return
                S.handoff(X_b, b_qk + b_og + b_us + b_wz + b_bd + b_ww + [b_dt, b_gate, b_rst] + b_zz + b_sqb)
                for tt in range(NTT):
                    S.dma("sp", xq[tt], X[:, tt, :], xin_ap[tt * 128:(tt + 1) * 128, :], writes=[X_b[tt]])
                wo = dr["w_out"][l].rearrange("(k p) n -> p k n", p=128)
                ws = WStream([(wo[:, :, h * 512:(h + 1) * 512], 8, 512) for h in range(2)])
                for h in range(2):
                    wc, bw = ws.get(h)
                    for tt in range(NTT):
                        bank, bb = next_bank()
                        for kk in range(8):
                            S.op("pe", lambda e: e.matmul(bank[:, :], lhsT=YC[:, kk, tt * 128:(tt + 1) * 128], rhs=wc[:, kk, :],
                                                          start=(kk == 0), stop=(kk == 7)), inc=(kk == 7), reads=[bw, b_yc[tt]], writes=[bb])
                        S.op("dve", lambda e: e.tensor_tensor(out=X[:, tt, h * 512:(h + 1) * 512], in0=X[:, tt, h * 512:(h + 1) * 512], in1=bank[:, :], op=ALU.add),
                             reads=[bb, X_b[tt]], writes=[X_b[tt]])

                if cfg.get("dbg_x1"):
                    b_o1 = S.buf("o1")
                    for tt in range(NTT):
                        S.dma("sp", oq, xout_ap[tt * 128:(tt + 1) * 128, :], X[:, tt, :], reads=[X_b[tt]], writes=[b_o1])
                    S.wait_all("sp", [b_o1])
                    return
                if stop == "wout":
                    quiesce(); return
                S.handoff(HT_b, b_yc)
                a2_users = [b_g, b_sm] + b_kt + b_cl + [b_ct, b_tc] + b_vw + b_pt + b_hb + b_hn + [b_cp, b_tab, b_s5] + b_zl + b_zs + b_zg + b_va + [b_gr]
                handoff([b_junk, b_xh] + b_xnb, a2_users)
                norm_to_hT(g2, False, False)
                if cfg.get("dbg_ht2"):
                    b_o1 = S.buf("o1")
                    S.dma("sp", oq, dr["dbg_ht"], view(A1, 0, 32832, BF16)[:, 0:8 * 2051], reads=HT_b, writes=[b_o1])
                w1 = dr["w_ff1"][l].rearrange("(k p) n -> p k n", p=128)
                w2 = dr["w_ff2"][l].rearrange("(k p) n -> p k n", p=128)
                c1 = [(w1[:, :, hc * 512:(hc + 1) * 512], 8, 512) for hc in range(8)]
                c2 = [(w2[:, hc * 4:(hc + 1) * 4, :], 4, 1024) for hc in range(8)]
                chunks = [c1[0]]
                for hc in range(8):
                    if hc + 1 < 8:
                        chunks.append(c1[hc + 1])
                    chunks.append(c2[hc])
                ws = WStream(chunks)
                k.wci = 0

                def wnext():
                    r = ws.get(k.wci)
                    k.wci += 1
                    return r
                hid = [view(SCR, i * 16 * KB, 16 * KB, BF16).rearrange("p (f t) -> p f t", f=4) for i in range(2)]
                b_hid = [S.bufs(4, f"hid{i}") for i in range(2)]
                sq = [view(SCR, 32 * KB + i * 2048, 2048, F32) for i in range(2)]
                b_sq = S.bufs(2, "sq")
                k.sqi = 0
                handoff(b_hid[0] + b_hid[1] + b_sq, a2_users + [b_junk, b_xh] + b_xnb)

                def ffn1(hc):
                    wc, bw = wnext()
                    hb = hid[hc % 2]
                    for ft in range(4):
                        for nb in range(4):
                            bank, bb = next_bank()
                            for kk in range(8):
                                S.op("pe", lambda e: e.matmul(bank[:, :], lhsT=wc[:, kk, ft * 128:(ft + 1) * 128], rhs=HT[:, kk, 3 + nb * 512:3 + (nb + 1) * 512],
                                                              start=(kk == 0), stop=(kk == 7)), inc=(kk == 7), reads=[bw] + HT_b[nb * 4:nb * 4 + 4], writes=[bb])
                            si = k.sqi % 2; k.sqi += 1
                            S.op("act", lambda e: e.activation(out=sq[si], in_=bank[:, :], func=AF.Square), reads=[bb], writes=[b_sq[si]])
                            S.op("dve", lambda e: e.scalar_tensor_tensor(out=hb[:, ft, nb * 512:(nb + 1) * 512], in0=bank[:, :], scalar=0.0, in1=sq[si],
                                                                         op0=ALU.is_gt, op1=ALU.mult), reads=[bb, b_sq[si]], writes=[b_hid[hc % 2][nb]])

                def ffn2(hc):
                    wc, bw = wnext()
                    hb = hid[hc % 2]
                    for tt in range(NTT):
                        for h in range(2):
                            bank, bb = next_bank()
                            for kk in range(4):
                                S.op("pe", lambda e: e.matmul(bank[:, :], lhsT=hb[:, kk, tt * 128:(tt + 1) * 128], rhs=wc[:, kk, h * 512:(h + 1) * 512],
                                                              start=(kk == 0), stop=(kk == 3)), inc=(kk == 3), reads=[bw, b_hid[hc % 2][tt // 4]], writes=[bb])
                            S.op("dve", lambda e: e.tensor_tensor(out=X[:, tt, h * 512:(h + 1) * 512], in0=X[:, tt, h * 512:(h + 1) * 512], in1=bank[:, :], op=ALU.add),
                                 reads=[bb, X_b[tt]], writes=[X_b[tt]])

                ffn1(0)
                for hc in range(8):
                    if hc + 1 < 8:
                        ffn1(hc + 1)
                    ffn2(hc)

                if last:
                    gfin = view(SCR, 40 * KB, 4096, F32)
                    b_gf = S.buf("gfin")
                    fg = dr["final_norm_g"]
                    S.dma("sp", gq, gfin, bass.AP(fg.tensor, fg.offset, [[0, 128], [1, D]]), writes=[b_gf])
                    ot = [view(SCR, 44 * KB + i * 4096, 4096, F32) for i in range(2)]
                    b_ot = S.bufs(2, "ot")
                    for tt in range(NTT):
                        rms_stats(X[:, tt, :], 128, tt, X_b[tt])
                        S.op("dve", lambda e: e.scalar_tensor_tensor(out=ot[tt % 2], in0=X[:, tt, :], scalar=rstd[:, tt:tt + 1], in1=gfin,
                                                                     op0=ALU.mult, op1=ALU.mult), reads=[X_b[tt], b_st[tt], b_gf], writes=[b_ot[tt % 2]])
                        S.dma("sp", oq, xout_ap[tt * 128:(tt + 1) * 128, :], ot[tt % 2], reads=[b_ot[tt % 2]], writes=[b_out])
                else:
                    for tt in range(NTT):
                        S.dma("sp", oq, xout_ap[tt * 128:(tt + 1) * 128, :], X[:, tt, :], reads=[X_b[tt]], writes=[b_out])
                S.wait_all("sp", [b_out])

        layers = cfg["layers"]
        for li, l in enumerate(layers):
            layer(l, dr["xin"], dr.get("xhalo"), dr.get("xout"), last=cfg.get("final", False) and li == len(layers) - 1)
    return nc


_NC_CACHE = {}
N_CORES = 8
LAYER_KEYS = ["norm_mix_g", "w_in", "w_out", "norm_ffn_g", "w_ff1", "w_ff2", "ml_conv_w", "ml_conv_b", "ml_b_i", "ml_b_f",
              "ml_norm_g", "s5_a_re", "s5_a_im", "s5_log_dt", "s5_b_re", "s5_b_im", "s5_c_re", "s5_c_im", "s5_d", "s5_w_glu",
              "s5_b_glu", "s5_out_g"]
A_KEYS = ["norm_mix_g", "w_in", "ml_conv_w", "ml_conv_b", "ml_b_i", "ml_b_f", "s5_a_re", "s5_a_im", "s5_log_dt", "s5_b_re", "s5_b_im",
          "s5_c_re", "s5_c_im", "s5_d", "ml_norm_g", "s5_b_glu", "s5_out_g"]
B_KEYS = ["w_out", "norm_ffn_g", "w_ff1", "w_ff2", "s5_w_glu", "ml_norm_g", "s5_b_glu", "s5_out_g"]
XF_NAMES = ["e_a0", "e_va", "e_gates", "e_cl", "e_ww", "e_cp", "e_bd", "e_tab", "e_sp"]
A_CONST = ["ident", "causal", "ones", "par01", "bdmask"]
B_CONST = ["ident", "causal", "ones"]


def _get_nc(mode, final):
    key = (mode, final)
    if key not in _NC_CACHE:
        _NC_CACHE[key] = build(dict(layers=[0], nlayers=1, mode=mode, final=final, debug=False))
    return _NC_CACHE[key]


def _consts():
    par = np.zeros((128, 2), np.float32)
    par[:, 1] = (np.arange(128) // 16) % 2
    par[:, 0] = 1 - par[:, 1]
    return {"ident": np.eye(128, dtype=np.float32), "causal": np.triu(np.ones((128, 128), np.float32)),
            "ones": np.ones((128, 128), np.float32), "par01": par,
            "bdmask": np.kron(np.eye(8), np.ones((16, 16))).astype(np.float32)}


def kernel(**inputs):
    x = np.ascontiguousarray(inputs["x"], dtype=np.float32)
    nb, ls, d = x.shape
    per = ls // 4
    consts = _consts()
    cur = [np.ascontiguousarray(x[c // 4, (c % 4) * per:(c % 4 + 1) * per]) for c in range(N_CORES)]
    preds = []
    for c in range(N_CORES):
        p = np.zeros((128, 8), np.float32)
        for j in range(N_CORES):
            if j // 4 == c // 4 and j < c:
                p[:, j] = 1.0
        preds.append(p)
    depth = inputs["w_in"].shape[0]
    for l in range(depth):
        halos = [np.zeros((3, d), np.float32) if c % 4 == 0 else np.ascontiguousarray(cur[c - 1][-3:]) for c in range(N_CORES)]
        lw = {k: np.ascontiguousarray(np.asarray(inputs[k], dtype=np.float32)[l:l + 1]) for k in LAYER_KEYS}
        final = (l == depth - 1)
        ncA = _get_nc("A", False)
        mapsA = []
        for c in range(N_CORES):
            m = {"xin": cur[c], "xhalo": halos[c], "pred": preds[c]}
            m.update({k: consts[k] for k in A_CONST})
            m.update({k: lw[k] for k in A_KEYS})
            mapsA.append(m)
        resA = run_bass_kernel_spmd(ncA, mapsA, core_ids=list(range(N_CORES)))
        summ_all = np.ascontiguousarray(np.stack([np.asarray(resA.results[c]["summ"]) for c in range(N_CORES)]))
        ncB = _get_nc("B", final)
        mapsB = []
        for c in range(N_CORES):
            m = {"xin": cur[c], "pred": preds[c], "summ_all": summ_all,
                 "final_norm_g": np.ascontiguousarray(inputs["final_norm_g"], dtype=np.float32)}
            m.update({k: consts[k] for k in B_CONST})
            m.update({k: lw[k] for k in B_KEYS})
            m.update({k: np.asarray(resA.results[c][k]) for k in XF_NAMES})
            mapsB.append(m)
        resB = run_bass_kernel_spmd(ncB, mapsB, core_ids=list(range(N_CORES)))
        cur = [np.asarray(resB.results[c]["xout"]) for c in range(N_CORES)]
    out = np.empty_like(x)
    for c in range(N_CORES):
        out[c // 4, (c % 4) * per:(c % 4 + 1) * per] = cur[c]
    return out
```

```python
import numpy as np
import concourse.bass as bass
import concourse.mybir as mybir
from concourse.bass_utils import run_bass_kernel_spmd

F32 = mybir.dt.float32
BF16 = mybir.dt.bfloat16
AF = mybir.ActivationFunctionType
ALU = mybir.AluOpType
AX = mybir.AxisListType


class Buf:
    __slots__ = ("name", "w", "r")

    def __init__(self, name):
        self.name = name
        self.w = {}
        self.r = {}


class Sched:
    def __init__(self, nc, ctx):
        self.nc = nc
        self.ctx = ctx
        self.eng = {"pe": nc.tensor, "act": nc.scalar, "dve": nc.vector, "pool": nc.gpsimd, "sp": nc.sync}
        self.sem = {}
        self.cnt = {}
        for k in self.eng:
            self.sem[k] = ctx.enter_context(nc.semaphore("s_" + k))
            self.cnt[k] = 0
        self.waited = {k: {} for k in self.eng}
        self.ndma = 0
        self.nbuf = 0
        self.mute = False

    def buf(self, name=None):
        self.nbuf += 1
        return Buf(name or f"b{self.nbuf}")

    def bufs(self, n, name="b"):
        return [self.buf(f"{name}{i}") for i in range(n)]

    def dma_sem(self, name=None):
        self.ndma += 1
        key = name or f"dma{self.ndma}"
        self.sem[key] = self.ctx.enter_context(self.nc.semaphore("s_" + key))
        self.cnt[key] = 0
        return key

    def _deps(self, e, reads, writes):
        deps = {}
        for b in reads:
            for k, c in b.w.items():
                if deps.get(k, 0) < c:
                    deps[k] = c
        for b in writes:
            for k, c in b.w.items():
                if deps.get(k, 0) < c:
                    deps[k] = c
            for k, c in b.r.items():
                if deps.get(k, 0) < c:
                    deps[k] = c
        eng = self.eng[e]
        for k, c in deps.items():
            if k == e and e == "pe":
                continue
            if self.waited[e].get(k, 0) < c:
                eng.wait_ge(self.sem[k], c)
                self.waited[e][k] = c

    def _record(self, key, c, reads, writes):
        for b in writes:
            b.w = {key: c}
            b.r = {}
        for b in reads:
            if b.r.get(key, 0) < c:
                b.r[key] = c

    def op(self, e, fn, reads=(), writes=(), inc=True):
        if self.mute:
            return None
        self._deps(e, reads, writes)
        ins = fn(self.eng[e])
        if inc:
            self.cnt[e] += 1
            ins.then_inc(self.sem[e], 1)
            self._record(e, self.cnt[e], reads, writes)
        else:
            self._record(e, self.cnt[e] + 1, reads, writes)
        return ins

    def seal(self, key, bufs):
        if self.mute:
            return
        c = self.cnt[key]
        for b in bufs:
            if key in b.w:
                b.w[key] = c

    def handoff(self, news, olds):
        w = {}
        r = {}
        for ob in olds:
            for k2, c2 in ob.w.items():
                if w.get(k2, 0) < c2:
                    w[k2] = c2
            for k2, c2 in ob.r.items():
                if r.get(k2, 0) < c2:
                    r[k2] = c2
        for nb in news:
            nb.w = dict(w)
            nb.r = dict(r)

    def dma(self, q, dsem, out, in_, reads=(), writes=(), **kw):
        if self.mute:
            return None
        self._deps(q, reads, writes)
        ins = self.eng[q].dma_start(out=out, in_=in_, **kw)
        self.cnt[dsem] += 16
        ins.then_inc(self.sem[dsem], 16)
        self._record(dsem, self.cnt[dsem], reads, writes)
        return ins

    def wait_all(self, e, bufs):
        if self.mute:
            return
        self._deps(e, bufs, ())


import numpy as np
from contextlib import ExitStack

NT = 2048
NTT = 16
D = 1024
DIN = 2568
DFF = 4096
EPS = 1e-6
KB = 1024


class KB_:
    pass


def build(cfg):
    nc = bass.Bass("TRN2", target_bir_lowering=False)
    k = KB_()
    k.nc = nc
    k.cfg = cfg
    L = cfg.get("nlayers", 1)
    dr = {}

    def din(name, shape, dt=F32):
        dr[name] = nc.dram_tensor(name, list(shape), dt, kind="ExternalInput").ap()
        return dr[name]

    def dout(name, shape, dt=F32):
        dr[name] = nc.dram_tensor(name, list(shape), dt, kind="ExternalOutput").ap()
        return dr[name]

    mode = cfg.get("mode", "B")
    IMP = (mode == "B")
    EXP = (mode == "A")
    XF = [("e_a0", 16384), ("e_va", 4256), ("e_gates", 1152), ("e_cl", 8256), ("e_ww", 4096), ("e_cp", 8704), ("e_bd", 4096),
          ("e_tab", 4096), ("e_sp", 1216)]
    din("xin", [NT, D])
    if IMP:
        _din_real = din

        def din(name, shape, dt=F32, _real=_din_real):
            dr[name] = nc.dram_tensor(name, list(shape), dt).ap()
            return dr[name]
    if True:
        din("xhalo", [3, D])
        din("norm_mix_g", [L, D]); din("w_in", [L, D, DIN])
        din("ml_conv_w", [L, 4, 1024]); din("ml_conv_b", [L, 1024])
        din("ml_b_i", [L, 4]); din("ml_b_f", [L, 4])
        din("s5_a_re", [L, 32, 64]); din("s5_a_im", [L, 32, 64]); din("s5_log_dt", [L, 32])
        din("s5_b_re", [L, 32, 64, 16]); din("s5_b_im", [L, 32, 64, 16]); din("s5_c_re", [L, 32, 16, 64]); din("s5_c_im", [L, 32, 16, 64])
        din("s5_d", [L, 32, 16])
        din("par01", [128, 2]); din("bdmask", [128, 128])
    if IMP:
        din = _din_real
    if mode != "A":
        din("w_out", [L, D, D])
        din("norm_ffn_g", [L, D]); din("w_ff1", [L, D, DFF]); din("w_ff2", [L, DFF, D])
        din("final_norm_g", [D])
        din("s5_w_glu", [L, 512, 512])
        din("summ_all", [8, 128, 556])
    din("ident", [128, 128]); din("causal", [128, 128]); din("ones", [128, 128])
    din("ml_norm_g", [L, 512]); din("s5_b_glu", [L, 512]); din("s5_out_g", [L, 512])
    din("pred", [128, 8])
    if EXP:
        dout("summ", [128, 556])
        for nm_, w_ in XF:
            dout(nm_, [128, w_])
    if IMP:
        for nm_, w_ in XF:
            din(nm_, [128, w_])
    if mode != "A":
        dout("xout", [NT, D])
    if cfg.get("dbg_ht2"):
        dout("dbg_ht", [128, 8 * 2051], BF16)
    if cfg.get("debug"):
        dout("dbg_u", [128, 4 * 2048], BF16)
        dout("dbg_qk", [128, 8 * 2048], BF16)
        dout("dbg_v", [128, 16 * 4 * 129], BF16)
        dout("dbg_o", [128, 16 * 512], BF16)
        dout("dbg_if", [128, 128])
        dout("dbg_yc", [128, 8 * 2048], BF16)
        dout("dbg_g", [128, 16 * 64])
        dout("dbg_zl", [128, 4096]); dout("dbg_zs", [128, 16 * 2 * 129], BF16); dout("dbg_zg", [128, 8192], BF16)
        dout("dbg_sc", [128, 1216]); dout("dbg_cp", [128, 17408], BF16); dout("dbg_bd", [128, 8192], BF16); dout("dbg_wz", [128, 16384], BF16)
        dout("dbg_zend", [128, 32])
    k.dr = dr

    with ExitStack() as ctx:
        S = Sched(nc, ctx)
        k.S = S
        ctx.enter_context(nc.allow_non_contiguous_dma(reason="small param loads"))
        ctx.enter_context(nc.allow_low_precision(reason="bf16 matmul operands"))
        A0_B, A1_B, A2_B = 64 * KB, 33 * KB + 256, 79 * KB
        arena = ctx.enter_context(nc.sbuf_tensor("arena", [128, (A0_B + A1_B + A2_B) // 4], F32))
        ring = ctx.enter_context(nc.sbuf_tensor("ring", [128, 3 * 4096], BF16))
        cst = ctx.enter_context(nc.sbuf_tensor("cst", [128, 1024], F32))
        banks = [ctx.enter_context(nc.psum_tensor(f"ps{i}", [128, 512], F32)) for i in range(8)]
        bank_bufs = S.bufs(8, "bank")
        k.bank_i = 0
        block = ctx.enter_context(nc.Block())

        def view(base, off, nbytes, dt):
            assert off % 4 == 0 and nbytes % 4 == 0
            a = arena[:, (base + off) // 4:(base + off + nbytes) // 4]
            return a if dt == F32 else a.bitcast(dt)
        A0, A1, A2 = 0, A0_B, A0_B + A1_B

        def next_bank():
            i = k.bank_i
            k.bank_i = (i + 1) % 8
            return banks[i], bank_bufs[i]

        ident_f = cst[:, 0:128]
        ident_b = cst[:, 128:192].bitcast(BF16)
        b_cst = S.buf("cst")
        dq = S.dma_sem("dq_misc")
        S.dma("sp", dq, ident_f, dr["ident"], writes=[b_cst])
        tri_f = cst[:, 384:512]
        ones_f = cst[:, 512:640]
        tri_b = cst[:, 192:256].bitcast(BF16)
        S.dma("sp", dq, tri_f, dr["causal"], writes=[b_cst])
        S.dma("sp", dq, ones_f, dr["ones"], writes=[b_cst])
        par01 = cst[:, 752:754]
        bdm = cst[:, 768:896]
        ones_b = cst[:, 896:960].bitcast(BF16)
        if not IMP:
            S.dma("sp", dq, par01, dr["par01"], writes=[b_cst])
        pred = cst[:, 972:980]
        pmask = cst[:, 980:988]
        S.dma("sp", dq, pred, dr["pred"], writes=[b_cst])
        if not IMP:
            S.dma("sp", dq, bdm, dr["bdmask"], writes=[b_cst])
        S.seal(dq, [b_cst])
        S.op("dve", lambda e: e.tensor_copy(out=ones_b, in_=ones_f), reads=[b_cst], writes=[b_cst])
        S.op("dve", lambda e: e.tensor_scalar(out=pmask, in0=pred, scalar1=1e6, scalar2=-1e6, op0=ALU.mult, op1=ALU.add), reads=[b_cst], writes=[b_cst])
        S.op("dve", lambda e: e.tensor_copy(out=ident_b, in_=ident_f), reads=[b_cst], writes=[b_cst])
        S.op("dve", lambda e: e.tensor_copy(out=tri_b, in_=tri_f), reads=[b_cst], writes=[b_cst])

        X = view(A0, 0, 64 * KB, F32).rearrange("p (t d) -> p t d", t=NTT)
        X_b = S.bufs(NTT, "X")
        HT = view(A1, 0, 8 * 2051 * 2 + 0, BF16) if False else view(A1, 0, 32832, BF16)[:, 0:8 * 2051].rearrange("p (k t) -> p k t", k=8)
        HT_b = S.bufs(NTT + 1, "HT")
        YC = view(A1, 0, 32 * KB, BF16).rearrange("p (k t) -> p k t", k=8)
        QK = view(A0, 0, 32 * KB, BF16).rearrange("p (f t) -> p f t", f=8)
        OG = view(A0, 32 * KB, 16 * KB, BF16).rearrange("p (t d) -> p t d", t=NTT)
        US = view(A0, 48 * KB, 16 * KB, BF16).rearrange("p (q i c) -> p q i c", q=4, i=16)
        VA = view(A2, 0, 16512, BF16).rearrange("p (t h v) -> p t h v", t=NTT, h=4)
        GR = view(A2, 16512, 512, F32)
        SCR = A2 + 17024

        xq = [S.dma_sem(f"xq{i}") for i in range(16)]
        hq = S.dma_sem("hq"); gq = S.dma_sem("gq")
        oq = S.dma_sem("oq")
        wq = [S.dma_sem(f"wq{i}") for i in range(3)]
        ring_b = S.bufs(3, "ring")
        k.wi = 0

        def wload(src_ap, nk, ncols):
            i = k.wi % 3
            k.wi += 1
            v = ring[:, i * 4096: i * 4096 + nk * ncols].rearrange("p (k n) -> p k n", k=nk)
            S.dma("pool", wq[i], v, src_ap, writes=[ring_b[i]])
            return v, ring_b[i]

        class WStream:
            def __init__(self, chunks):
                self.chunks = chunks
                self.loaded = []

            def get(self, i, ahead=2):
                while len(self.loaded) < min(len(self.chunks), i + 1 + ahead):
                    self.loaded.append(wload(*self.chunks[len(self.loaded)]))
                return self.loaded[i]

        k.pq = None

        def load_pvec(dst, src_1d, b, q="sp"):
            S.dma(q, k.pq, dst, src_1d.rearrange("(k p) -> p k", p=128), writes=[b])

        tmp_b = S.bufs(4, "tmp")

        def quiesce():
            for e_ in ("pe", "act", "dve", "pool"):
                if S.cnt[e_] > 0:
                    nc.sync.wait_ge(S.sem[e_], S.cnt[e_])
            for key_, c_ in S.cnt.items():
                if key_ not in S.eng and c_ > 0:
                    nc.sync.wait_ge(S.sem[key_], c_)

        def layer(l, xin_ap, xh_ap, xout_ap, last):
            stop = cfg.get("stop")
            prm = cst[:, 256:256 + 64]
            b_prm = S.buf("prm")
            pq_l = S.dma_sem(f"pq{l}"); k.pq = pq_l
            g1 = cst[:, 640:648]; g2 = cst[:, 648:656]
            cw = cst[:, 656:688].rearrange("p (j f) -> p j f", j=4)
            cb = cst[:, 688:696]
            b_out = S.buf("out")
            if not IMP:
                load_pvec(g1, dr["norm_mix_g"][l], b_prm)
                for j in range(4):
                    load_pvec(cw[:, j, :], dr["ml_conv_w"][l, j], b_prm)
                load_pvec(cb, dr["ml_conv_b"][l], b_prm)
            if mode != "A":
                load_pvec(g2, dr["norm_ffn_g"][l], b_prm)
            mlg = cst[:, 740:744]
            load_pvec(mlg, dr["ml_norm_g"][l], b_prm)
            bif = cst[:, 744:752]
            dcol = cst[:, 960:964]; bglu = cst[:, 964:968]; outg = cst[:, 968:972]
            if not IMP:
                bi_ = dr["ml_b_i"][l]; bf_ = dr["ml_b_f"][l]
                S.dma("sp", pq_l, bif[:, 0:4], bass.AP(bi_.tensor, bi_.offset, [[0, 128], [1, 4]]), writes=[b_prm])
                S.dma("sp", pq_l, bif[:, 4:8], bass.AP(bf_.tensor, bf_.offset, [[0, 128], [1, 4]]), writes=[b_prm])
                load_pvec(dcol, dr["s5_d"][l].rearrange("g p -> (g p)"), b_prm)
            load_pvec(bglu, dr["s5_b_glu"][l], b_prm)
            load_pvec(outg, dr["s5_out_g"][l], b_prm)
            S.seal(pq_l, [b_prm])

            def xfer(name, ap, bufs):
                was = S.mute; S.mute = False
                if EXP:
                    S.dma("sp", oq, dr[name], ap, reads=bufs, writes=[b_out])
                elif IMP:
                    q_ = S.dma_sem(f"{name}_{l}")
                    S.dma("sp", q_, ap, dr[name], writes=bufs)
                S.mute = was
            if IMP:
                S.mute = True
            ssq = cst[:, 700:717]
            rstd = cst[:, 720:737]
            b_st = S.bufs(17, "st")
            junk = view(SCR, 0, 2048, BF16)
            b_junk = S.buf("junk")
            xnb = [view(SCR, 2048 + i * 2048, 2048, BF16) for i in range(2)]
            b_xnb = S.bufs(2, "xnb")
            xh_t = view(SCR, 6144, 4096, F32)
            b_xh = S.buf("xh")

            def rms_stats(src, np_, col, bsrc):
                S.op("act", lambda e: e.activation(out=junk[:np_], in_=src, func=AF.Square, accum_out=ssq[:np_, col:col + 1]),
                     reads=[bsrc], writes=[b_junk, b_st[col]])
                S.op("dve", lambda e: e.tensor_scalar(out=rstd[:np_, col:col + 1], in0=ssq[:np_, col:col + 1], scalar1=1.0 / D, scalar2=EPS,
                                                      op0=ALU.mult, op1=ALU.add), reads=[b_st[col]], writes=[b_st[col]])
                S.op("act", lambda e: e.activation(out=rstd[:np_, col:col + 1], in_=rstd[:np_, col:col + 1], func=AF.Sqrt),
                     reads=[b_st[col]], writes=[b_st[col]])
                S.op("dve", lambda e: e.reciprocal(out=rstd[:np_, col:col + 1], in_=rstd[:np_, col:col + 1]), reads=[b_st[col]], writes=[b_st[col]])

            b_sg = S.bufs(5, "stg")

            def stats_A(g4):
                tiles = [NTT] if g4 == 4 else range(4 * g4, 4 * g4 + 4)
                for tt in tiles:
                    halo = tt == NTT
                    np_ = 3 if halo else 128
                    src, bsrc = (xh_t[:3, :], b_xh) if halo else (X[:, tt, :], X_b[tt])
                    S.op("act", lambda e: e.activation(out=junk[:np_], in_=src, func=AF.Square, accum_out=ssq[:np_, tt:tt + 1]),
                         reads=[bsrc], writes=[b_junk, b_sg[g4]])

            def stats_B(g4):
                c0, c1 = (NTT, NTT + 1) if g4 == 4 else (4 * g4, 4 * g4 + 4)
                np_ = 3 if g4 == 4 else 128
                S.op("dve", lambda e: e.tensor_scalar(out=rstd[:np_, c0:c1], in0=ssq[:np_, c0:c1], scalar1=1.0 / D, scalar2=EPS,
                                                      op0=ALU.mult, op1=ALU.add), reads=[b_sg[g4]], writes=[b_sg[g4]])
                S.op("act", lambda e: e.activation(out=rstd[:np_, c0:c1], in_=rstd[:np_, c0:c1], func=AF.Sqrt), reads=[b_sg[g4]], writes=[b_sg[g4]])
                S.op("dve", lambda e: e.reciprocal(out=rstd[:np_, c0:c1], in_=rstd[:np_, c0:c1]), reads=[b_sg[g4]], writes=[b_sg[g4]])

            def norm_to_hT(gvec, with_halo, from_dram):
                gb = bass.AP(gvec.tensor, gvec.offset, [list(gvec.ap[0]), list(gvec.ap[1]), [0, 128]])
                for tt in range(NTT):
                    if from_dram:
                        S.dma("sp", xq[tt], X[:, tt, :], xin_ap[tt * 128:(tt + 1) * 128, :], writes=[X_b[tt]])
                if with_halo:
                    S.dma("sp", hq, xh_t[:3, :], xh_ap, writes=[b_xh])

                def stage_C(g4):
                    tiles = [NTT] if g4 == 4 else range(4 * g4, 4 * g4 + 4)
                    for tt in tiles:
                        halo = tt == NTT
                        np_ = 3 if halo else 128
                        src, bsrc = (xh_t[:3, :], b_xh) if halo else (X[:, tt, :], X_b[tt])
                        xb = xnb[tt % 2]; bx = b_xnb[tt % 2]
                        S.op("act", lambda e: e.activation(out=xb[:np_], in_=src, func=AF.Copy, scale=rstd[:np_, tt:tt + 1]),
                             reads=[bsrc, b_sg[g4]], writes=[bx])
                        bank, bb = next_bank()
                        pb = bank[:, 0:512].bitcast(BF16).rearrange("p (k t) -> p k t", k=8)
                        for kk in range(8):
                            S.op("pe", lambda e: e.transpose(pb[:, kk, 0:np_], xb[:np_, kk * 128:(kk + 1) * 128], ident_b[:np_, :np_]),
                                 inc=(kk == 7), reads=[bx, b_cst], writes=[bb])
                        c0 = 0 if halo else 3 + tt * 128
                        S.op("dve", lambda e: e.tensor_tensor(out=HT[:, :, c0:c0 + np_], in0=pb[:, :, 0:np_], in1=gb[:, :, 0:np_], op=ALU.mult),
                             reads=[bb, b_prm], writes=[HT_b[tt]])
                ng = 5 if with_halo else 4
                stats_A(0)
                for g4 in range(ng):
                    if g4 + 1 < ng:
                        stats_A(g4 + 1)
                    stats_B(g4)
                    stage_C(g4)

            norm_to_hT(g1, True, True)

            if stop == "norm1":
                quiesce(); return
            win = dr["w_in"][l].rearrange("(k p) n -> p k n", p=128)
            chunks = [(win[:, :, c * 512:(c + 1) * 512], 8, 512) for c in range(5)] + [(win[:, :, 2560:2568], 8, 8)]
            ws = WStream(chunks)
            stage = [view(SCR, 10240 + i * 8448, 8448, F32) for i in range(2)]
            b_stage = S.bufs(2, "stage")
            acc = view(SCR, 10240 + 2 * 8448, 8192, F32)
            b_acc = S.buf("acc")
            b_us = S.bufs(4, "us")
            b_qk = S.bufs(8, "qk")
            b_va = S.bufs(NTT, "va")
            b_og = S.bufs(NTT, "og")
            b_gr = S.buf("gr")
            allHT = HT_b
            S.op("pool", lambda e: e.memset(VA[:, :, :, 128:129], 1.0), writes=b_va)
            for ci in range(3):
                wc, bw = ws.get(ci)
                for ft in range(4):
                    f = (ci - 1) * 4 + ft
                    if ci > 0:
                        st = stage[f % 2]; bs = b_stage[f % 2]
                        bank, bb = next_bank()
                        for kk in range(8):
                            S.op("pe", lambda e: e.matmul(bank[:, 0:3], lhsT=wc[:, kk, ft * 128:(ft + 1) * 128], rhs=HT[:, kk, 0:3],
                                                          start=(kk == 0), stop=(kk == 7)), inc=(kk == 7), reads=[bw, HT_b[NTT]], writes=[bb])
                        S.op("act", lambda e: e.activation(out=st[:, 0:3], in_=bank[:, 0:3], func=AF.Copy), reads=[bb], writes=[bs])
                    for nb in range(4):
                        bank, bb = next_bank()
                        for kk in range(8):
                            S.op("pe", lambda e: e.matmul(bank[:, :], lhsT=wc[:, kk, ft * 128:(ft + 1) * 128],
                                                          rhs=HT[:, kk, 3 + nb * 512:3 + (nb + 1) * 512], start=(kk == 0), stop=(kk == 7)),
                                 inc=(kk == 7), reads=[bw] + allHT[nb * 4:nb * 4 + 4], writes=[bb])
                        if ci == 0:
                            dst = US[:, ft, :, nb * 32:(nb + 1) * 32]
                            src = bank[:, :].rearrange("p (c i) -> p i c", i=16)
                            S.op("act", lambda e: e.activation(out=dst, in_=src, func=AF.Copy), reads=[bb], writes=[b_us[ft]])
                        else:
                            S.op("act", lambda e: e.activation(out=st[:, 3 + nb * 512:3 + (nb + 1) * 512], in_=bank[:, :], func=AF.Copy),
                                 reads=[bb], writes=[bs])
                    if ci > 0:
                        S.op("dve", lambda e: e.tensor_scalar(out=acc, in0=st[:, 0:2048], scalar1=cw[:, 0, f:f + 1], scalar2=None, op0=ALU.mult),
                             reads=[bs, b_prm], writes=[b_acc])
                        for j in range(1, 4):
                            S.op("dve", lambda e: e.scalar_tensor_tensor(out=acc, in0=st[:, j:j + 2048], scalar=cw[:, j, f:f + 1], in1=acc,
                                                                         op0=ALU.mult, op1=ALU.add), reads=[bs, b_prm, b_acc], writes=[b_acc])
                        S.op("act", lambda e: e.activation(out=QK[:, f, :], in_=acc, func=AF.Silu, bias=cb[:, f:f + 1]),
                             reads=[b_acc, b_prm], writes=[b_qk[f]])
            for ci in (3, 4):
                wc, bw = ws.get(ci)
                for tt in range(NTT):
                    bank, bb = next_bank()
                    for kk in range(8):
                        S.op("pe", lambda e: e.matmul(bank[:, :], lhsT=HT[:, kk, 3 + tt * 128:3 + (tt + 1) * 128], rhs=wc[:, kk, :],
                                                      start=(kk == 0), stop=(kk == 7)), inc=(kk == 7), reads=[bw, HT_b[tt]], writes=[bb])
                    if ci == 3:
                        S.op("act", lambda e: e.activation(out=VA[:, tt, :, 0:128], in_=bank[:, :].rearrange("p (h v) -> p h v", h=4), func=AF.Copy),
                             reads=[bb], writes=[b_va[tt]])
                    else:
                        S.op("act", lambda e: e.activation(out=OG[:, tt, :], in_=bank[:, :], func=AF.Sigmoid), reads=[bb], writes=[b_og[tt]])
            wc, bw = ws.get(5)
            bank, bb = next_bank()
            for tt in range(NTT):
                for kk in range(8):
                    S.op("pe", lambda e: e.matmul(bank[:, tt * 8:(tt + 1) * 8], lhsT=HT[:, kk, 3 + tt * 128:3 + (tt + 1) * 128], rhs=wc[:, kk, :],
                                                  start=(kk == 0), stop=(kk == 7)), inc=(kk == 7), reads=[bw, HT_b[tt]], writes=[bb])
            S.op("dve", lambda e: e.tensor_copy(out=GR, in_=bank[:, 0:128]), reads=[bb], writes=[b_gr])

            if cfg.get("debug"):
                dbq = S.dma_sem("dbq")
                b_dbg = S.buf("dbg")
                S.dma("sp", dbq, dr["dbg_u"], view(A0, 48 * KB, 16 * KB, BF16), reads=b_us, writes=[b_dbg])
                S.dma("sp", dbq, dr["dbg_qk"], view(A0, 0, 32 * KB, BF16), reads=b_qk, writes=[b_dbg])
                S.dma("sp", dbq, dr["dbg_v"], view(A2, 0, 16512, BF16), reads=b_va, writes=[b_dbg])
                S.dma("sp", dbq, dr["dbg_o"], view(A0, 32 * KB, 16 * KB, BF16), reads=b_og, writes=[b_dbg])
                S.dma("sp", dbq, dr["dbg_if"], GR, reads=[b_gr], writes=[b_dbg])
                pass

            if stop == "win":
                quiesce(); return
            xfer("e_a0", view(A0, 0, 64 * KB, F32), b_qk + b_og + b_us)
            xfer("e_va", view(A2, 0, 17024, F32), b_va + [b_gr])
            b_yc = S.bufs(NTT, "yc")
            S.handoff(b_yc, HT_b)

            MS = SCR
            def garr(i):
                return view(MS, i * 256, 256, F32).rearrange("p (m h) -> p m h", h=4)
            nlf, ig, nF, g, Gm, R0, Rr, w0, wv, clamp, dl, ep, incl, t1 = [garr(i) for i in range(14)]
            sm = view(MS, 14 * 256, 256, F32)
            gcol = sm[:, 0:1]; dm = sm[:, 4:8]; rec = sm[:, 8:12]; ss = sm[:, 12:16]; rs4 = sm[:, 16:20]
            Gb = view(MS, 15 * 256, 512, F32)
            b_g = S.buf("gates")
            b_sm = S.buf("sm")
            KT = view(MS, 4608, 16512, BF16)
            kTok = KT[:, 0:16 * 512].rearrange("p (m d) -> p m d", m=16)
            Cin = KT[:, 0:64 * 129].rearrange("p (c v) -> p c v", c=64)
            b_kt = S.bufs(16, "kt")
            CL = view(MS, 4608 + 16512, 64 * 129 * 4, F32).rearrange("p (c v) -> p c v", c=64)
            b_cl = S.bufs(16, "cl")
            T0 = MS + 4608 + 16512 + 64 * 129 * 4
            Ct = view(T0, 0, 2064, F32).rearrange("p (h v) -> p h v", h=4)
            tmpC = view(T0, 2064, 2064, F32).rearrange("p (h v) -> p h v", h=4)
            b_ct = S.buf("ct"); b_tc = S.buf("tmpc")
            Vw = [view(T0, 4128 + i * 1032, 1032, BF16).rearrange("p (h v) -> p h v", h=4) for i in range(2)]
            b_vw = S.bufs(2, "vw")
            PT = [view(T0, 6192 + i * 256, 256, BF16) for i in range(2)]
            b_pt = S.bufs(2, "pt")
            hbuf = [view(T0, 6704 + i * 512, 512, F32) for i in range(4)]
            b_hb = S.bufs(4, "hb")
            hn = [view(T0, 8752 + i * 256, 256, BF16) for i in range(2)]
            b_hn = S.bufs(2, "hn")
            junk2 = view(T0, 9264, 256, BF16)
            assert T0 + 9520 <= A2 + A2_B, (T0 + 9520 - A2 - A2_B)

            handoff = S.handoff
            handoff([b_g, b_sm] + b_kt + b_cl + [b_ct, b_tc] + b_vw + b_pt + b_hb + b_hn, b_stage + [b_acc, b_junk, b_xh] + b_xnb)
            GRv = GR.rearrange("p (m c) -> p m c", c=8)

            def bc_m(ap4):
                return bass.AP(ap4.tensor, ap4.offset, [list(ap4.ap[0]), [0, 16], list(ap4.ap[1])])

            def bc_last(ap, n):
                return bass.AP(ap.tensor, ap.offset, [list(x) for x in ap.ap] + [[0, n]])

            def flat(a):
                return a.rearrange("p m h -> p (m h)")
            LN_S = -0.5 * float(np.log(128.0))
            S.op("dve", lambda e: e.tensor_tensor(out=ig, in0=GRv[:, :, 0:4], in1=bc_m(bif[:, 0:4]), op=ALU.add), reads=[b_gr, b_prm], writes=[b_g])
            S.op("dve", lambda e: e.tensor_tensor(out=t1, in0=GRv[:, :, 4:8], in1=bc_m(bif[:, 4:8]), op=ALU.add), reads=[b_gr, b_prm], writes=[b_g])
            S.op("act", lambda e: e.activation(out=t1, in_=t1, func=AF.Exp, scale=-1.0), reads=[b_g], writes=[b_g])
            S.op("act", lambda e: e.activation(out=nlf, in_=t1, func=AF.Ln, bias=1.0), reads=[b_g], writes=[b_g])
            bankA, bbA = next_bank()
            S.op("pe", lambda e: e.matmul(bankA[:, 0:64], lhsT=tri_f, rhs=flat(nlf), start=True, stop=True), reads=[b_g, b_cst], writes=[bbA])
            bankB, bbB = next_bank()
            S.op("pe", lambda e: e.matmul(bankB[:, 0:64], lhsT=ones_f, rhs=flat(nlf), start=True, stop=True), reads=[b_g, b_cst], writes=[bbB])
            bA = bankA[:, 0:64].rearrange("p (m h) -> p m h", h=4)
            bB = bankB[:, 0:64].rearrange("p (m h) -> p m h", h=4)
            for h in range(4):
                S.op("dve", lambda e: e.tensor_tensor_scan(out=incl[:, :, h], data0=ones_f[:, 0:16], data1=bB[:, :, h], initial=0.0,
                                                           op0=ALU.mult, op1=ALU.add), reads=[bbB, b_cst, b_g], writes=[b_g])
            S.op("dve", lambda e: e.tensor_tensor(out=t1, in0=incl, in1=bB, op=ALU.subtract), reads=[b_g, bbB], writes=[b_g])
            S.op("dve", lambda e: e.tensor_tensor(out=nF, in0=t1, in1=bA, op=ALU.add), reads=[b_g, bbA], writes=[b_g])
            S.op("dve", lambda e: e.tensor_tensor(out=g, in0=ig, in1=nF, op=ALU.add), reads=[b_g], writes=[b_g])
            bankT, bbT = next_bank()
            S.op("pe", lambda e: e.transpose(bankT[0:64, 0:128], flat(g), ident_f), reads=[b_g, b_cst], writes=[bbT])
            S.op("dve", lambda e: e.tensor_reduce(out=gcol[0:64, :], in_=bankT[0:64, 0:128], axis=AX.X, op=ALU.max), reads=[bbT], writes=[b_sm])
            S.op("dve", lambda e: e.tensor_copy(out=Gb[0:64, :], in_=bc_last(gcol[0:64, 0:1], 128)[:, 0, :]), reads=[b_sm], writes=[b_sm])
            bankG, bbG = next_bank()
            S.op("pe", lambda e: e.matmul(bankG[:, 0:64], lhsT=Gb[0:64, :], rhs=ident_f[0:64, 0:64], start=True, stop=True), reads=[b_sm, b_cst], writes=[bbG])
            S.op("dve", lambda e: e.tensor_copy(out=flat(Gm), in_=bankG[:, 0:64]), reads=[bbG], writes=[b_g])
            for h in range(4):
                S.op("dve", lambda e: e.tensor_tensor_scan(out=R0[:, :, h], data0=Gm[:, :, h], data1=Gm[:, :, h], initial=-1e30,
                                                           op0=ALU.max, op1=ALU.max), reads=[b_g], writes=[b_g])
            S.op("dve", lambda e: e.tensor_tensor(out=t1, in0=g, in1=R0, op=ALU.subtract), reads=[b_g], writes=[b_g])
            S.op("act", lambda e: e.activation(out=w0, in_=t1, func=AF.Exp, bias=LN_S), reads=[b_g], writes=[b_g])
            S.op("dve", lambda e: e.tensor_tensor(out=t1[:, 1:16, :], in0=R0[:, 0:15, :], in1=R0[:, 1:16, :], op=ALU.subtract), reads=[b_g], writes=[b_g])
            S.op("act", lambda e: e.activation(out=dl[:, 1:16, :], in_=t1[:, 1:16, :], func=AF.Exp), reads=[b_g], writes=[b_g])
            for m in range(16):
                cs = slice(m * 128, (m + 1) * 128)
                bank, bb = next_bank()
                pb = bank[:, 0:256].bitcast(BF16)
                for h in range(4):
                    S.op("pe", lambda e: e.transpose(pb[:, h * 128:(h + 1) * 128], QK[:, 4 + h, cs], ident_b), inc=(h == 3),
                         reads=[b_qk[4 + h], b_cst], writes=[bb])
                S.op("act", lambda e: e.activation(out=kTok[:, m, :], in_=pb, func=AF.Copy), reads=[bb], writes=[b_kt[m]])
                vw = Vw[m % 2]; bv = b_vw[m % 2]
                S.op("dve", lambda e: e.tensor_tensor(out=vw, in0=VA[:, m, :, :], in1=bc_last(w0[:, m, :], 129), op=ALU.mult),
                     reads=[b_va[m], b_g], writes=[bv])
                for h in range(4):
                    bank, bb = next_bank()
                    S.op("pe", lambda e: e.matmul(bank[:, 0:129], lhsT=kTok[:, m, h * 128:(h + 1) * 128], rhs=vw[:, h, :], start=True, stop=True),
                         reads=[b_kt[m], bv], writes=[bb])
                    if h % 2 == 0:
                        S.op("act", lambda e: e.activation(out=CL[:, m * 4 + h, :], in_=bank[:, 0:129], func=AF.Copy), reads=[bb], writes=[b_cl[m]])
                    else:
                        S.op("dve", lambda e: e.tensor_copy(out=CL[:, m * 4 + h, :], in_=bank[:, 0:129]), reads=[bb], writes=[b_cl[m]])
                if m == 0:
                    S.op("dve", lambda e: e.tensor_copy(out=Ct, in_=CL[:, 0:4, :]), reads=[b_cl[0]], writes=[b_ct])
                else:
                    S.op("dve", lambda e: e.tensor_tensor(out=Ct, in0=Ct, in1=bc_last(dl[:, m, :], 129), op=ALU.mult), reads=[b_ct, b_g], writes=[b_ct])
                    S.op("dve", lambda e: e.tensor_tensor(out=Ct, in0=Ct, in1=CL[:, m * 4:m * 4 + 4, :], op=ALU.add), reads=[b_ct, b_cl[m]], writes=[b_ct])
            xfer("e_gates", view(MS, 0, 4608, F32), [b_g, b_sm])
            xfer("e_cl", view(MS, 4608 + 16512, 64 * 129 * 4, F32), b_cl)
            if IMP:
                S.mute = False
            mst = sm[:, 20:24]
            exq = S.dma_sem(f"exq{l}")
            if mode == "A":
                S.dma("sp", oq, dr["summ"][:, 32:548], Ct.rearrange("p h v -> p (h v)"), reads=[b_ct], writes=[b_out])
                S.dma("sp", oq, dr["summ"][:, 548:552], R0[:, 15, :], reads=[b_g], writes=[b_out])
                S.dma("sp", oq, dr["summ"][:, 552:556], incl[:, 15, :], reads=[b_g], writes=[b_out])
            else:
                sa = dr["summ_all"]
                small = Gb.rearrange("p (j c) -> p j c", j=8)[:, :, 0:8]
                S.dma("sp", exq, small, sa[:, :, 548:556].rearrange("j p c -> p j c"), writes=[b_sm])
                S.seal(exq, [b_sm])
                cm = sm[:, 20:24]; mx = sm[:, 24:28]; ta = sm[:, 28:32]; tb = sm[:, 32:36]; r0j = sm[:, 36:40]; nfj = sm[:, 40:44]; tq = sm[:, 44:48]
                S.op("dve", lambda e: e.memset(tmpC, 0.0), reads=[b_tc], writes=[b_tc])
                S.op("dve", lambda e: e.memset(cm, 0.0), reads=[b_sm], writes=[b_sm])
                cq = [S.dma_sem(f"cq{l}_{i}") for i in range(2)]
                Cj = [Ct, view(T0, 4128, 2064, F32).rearrange("p (h v) -> p h v", h=4)]
                b_cj = [b_ct, S.buf("cj1")]
                S.handoff([b_cj[1]], b_vw)
                for j in range(8):
                    cj = Cj[j % 2]; bcj = b_cj[j % 2]
                    S.dma("sp", cq[j % 2], cj.rearrange("p h v -> p (h v)"), sa[j, :, 32:548], writes=[bcj])
                    S.op("dve", lambda e: e.tensor_scalar(out=r0j, in0=small[:, j, 0:4], scalar1=pred[:, j:j + 1], scalar2=pmask[:, j:j + 1],
                                                          op0=ALU.mult, op1=ALU.add), reads=[b_sm, b_cst], writes=[b_sm])
                    S.op("dve", lambda e: e.tensor_scalar(out=nfj, in0=small[:, j, 4:8], scalar1=pred[:, j:j + 1], scalar2=None, op0=ALU.mult),
                         reads=[b_sm, b_cst], writes=[b_sm])
                    S.op("dve", lambda e: e.tensor_tensor(out=mx, in0=cm, in1=r0j, op=ALU.max), reads=[b_sm], writes=[b_sm])
                    S.op("dve", lambda e: e.tensor_tensor(out=tq, in0=cm, in1=mx, op=ALU.subtract), reads=[b_sm], writes=[b_sm])
                    S.op("act", lambda e: e.activation(out=ta, in_=tq, func=AF.Exp), reads=[b_sm], writes=[b_sm])
                    S.op("dve", lambda e: e.tensor_tensor(out=tq, in0=r0j, in1=mx, op=ALU.subtract), reads=[b_sm], writes=[b_sm])
                    S.op("act", lambda e: e.activation(out=tb, in_=tq, func=AF.Exp), reads=[b_sm], writes=[b_sm])
                    S.op("dve", lambda e: e.tensor_tensor(out=tmpC, in0=tmpC, in1=bc_last(ta, 129), op=ALU.mult), reads=[b_tc, b_sm], writes=[b_tc])
                    S.op("dve", lambda e: e.tensor_tensor(out=cj, in0=cj, in1=bc_last(tb, 129), op=ALU.mult), reads=[bcj, b_sm], writes=[bcj])
                    S.op("dve", lambda e: e.tensor_tensor(out=tmpC, in0=tmpC, in1=cj, op=ALU.add), reads=[b_tc, bcj], writes=[b_tc])
                    S.op("dve", lambda e: e.tensor_tensor(out=cm, in0=mx, in1=nfj, op=ALU.subtract), reads=[b_sm], writes=[b_sm])
            if mode != "A":
                S.op("dve", lambda e: e.tensor_tensor(out=Rr, in0=R0, in1=bc_m(mst), op=ALU.max), reads=[b_g, b_sm], writes=[b_g])
                S.op("dve", lambda e: e.tensor_tensor(out=t1, in0=g, in1=Rr, op=ALU.subtract), reads=[b_g], writes=[b_g])
                S.op("act", lambda e: e.activation(out=wv, in_=t1, func=AF.Exp, bias=LN_S), reads=[b_g], writes=[b_g])
                S.op("dve", lambda e: e.tensor_tensor(out=t1, in0=nF, in1=Rr, op=ALU.subtract), reads=[b_g], writes=[b_g])
                S.op("act", lambda e: e.activation(out=clamp, in_=t1, func=AF.Exp), reads=[b_g], writes=[b_g])
                S.op("dve", lambda e: e.tensor_tensor(out=t1, in0=R0, in1=Rr, op=ALU.subtract), reads=[b_g], writes=[b_g])
                S.op("act", lambda e: e.activation(out=ep, in_=t1, func=AF.Exp), reads=[b_g], writes=[b_g])
                S.op("dve", lambda e: e.tensor_tensor(out=t1[:, 1:16, :], in0=Rr[:, 0:15, :], in1=Rr[:, 1:16, :], op=ALU.subtract), reads=[b_g], writes=[b_g])
                S.op("dve", lambda e: e.tensor_tensor(out=t1[:, 0, :], in0=mst, in1=Rr[:, 0, :], op=ALU.subtract), reads=[b_g, b_sm], writes=[b_g])
                S.op("act", lambda e: e.activation(out=dl, in_=t1, func=AF.Exp), reads=[b_g], writes=[b_g])
                S.op("dve", lambda e: e.tensor_copy(out=Ct, in_=tmpC), reads=[b_tc], writes=[b_ct])
                b_cin = b_kt
                for m in range(16):
                    S.op("dve", lambda e: e.tensor_tensor(out=tmpC, in0=Ct, in1=bc_last(dl[:, m, :], 129), op=ALU.mult), reads=[b_ct, b_g], writes=[b_tc])
                    S.op("act", lambda e: e.activation(out=Cin[:, m * 4:m * 4 + 4, :], in_=tmpC, func=AF.Copy), reads=[b_tc], writes=b_cin)
                    S.op("dve", lambda e: e.tensor_tensor(out=Ct, in0=CL[:, m * 4:m * 4 + 4, :], in1=bc_last(ep[:, m, :], 129), op=ALU.mult),
                         reads=[b_cl[m], b_g], writes=[b_ct])
                    S.op("dve", lambda e: e.tensor_tensor(out=Ct, in0=Ct, in1=tmpC, op=ALU.add), reads=[b_ct, b_tc], writes=[b_ct])
                k.pti = 0
                for m in range(16):
                    cs = slice(m * 128, (m + 1) * 128)
                    for h in range(4):
                        bankS, bbS = next_bank()
                        S.op("pe", lambda e: e.matmul(bankS[:, 0:128], lhsT=QK[:, 4 + h, cs], rhs=QK[:, h, cs], start=True, stop=True),
                             reads=[b_qk[4 + h], b_qk[h]], writes=[bbS])
                        pi = k.pti % 2; k.pti += 1
                        S.op("dve", lambda e: e.scalar_tensor_tensor(out=PT[pi], in0=bankS[:, 0:128], scalar=wv[:, m, h:h + 1], in1=tri_f,
                                                                     op0=ALU.mult, op1=ALU.mult), reads=[bbS, b_g, b_cst], writes=[b_pt[pi]])
                        bankN, bbN = next_bank()
                        S.op("pe", lambda e: e.matmul(bankN[:, 0:129], lhsT=PT[pi], rhs=VA[:, m, h, :], start=True, stop=False), inc=False,
                             reads=[b_pt[pi], b_va[m]], writes=[bbN])
                        S.op("pe", lambda e: e.matmul(bankN[:, 0:129], lhsT=QK[:, h, cs], rhs=Cin[:, m * 4 + h, :], start=False, stop=True),
                             reads=[b_qk[h]] + b_cin, writes=[bbN])
                        S.op("act", lambda e: e.activation(out=dm[:, h:h + 1], in_=bankN[:, 128:129], func=AF.Abs), reads=[bbN], writes=[b_sm])
                        S.op("dve", lambda e: e.tensor_scalar(out=dm[:, h:h + 1], in0=dm[:, h:h + 1], scalar1=clamp[:, m, h:h + 1], scalar2=None,
                                                              op0=ALU.max), reads=[b_sm, b_g], writes=[b_sm])
                        S.op("dve", lambda e: e.reciprocal(out=rec[:, h:h + 1], in_=dm[:, h:h + 1]), reads=[b_sm], writes=[b_sm])
                        S.op("dve", lambda e: e.scalar_tensor_tensor(out=hbuf[h], in0=bankN[:, 0:128], scalar=rec[:, h:h + 1], in1=OG[:, m, h * 128:(h + 1) * 128],
                                                                     op0=ALU.mult, op1=ALU.mult), reads=[bbN, b_sm, b_og[m]], writes=[b_hb[h]])
                        S.op("act", lambda e: e.activation(out=junk2, in_=hbuf[h], func=AF.Square, accum_out=ss[:, h:h + 1]),
                             reads=[b_hb[h]], writes=[b_hn[0], b_sm])
                    S.op("dve", lambda e: e.tensor_scalar(out=rs4, in0=ss, scalar1=1.0 / 128, scalar2=EPS, op0=ALU.mult, op1=ALU.add), reads=[b_sm], writes=[b_sm])
                    S.op("act", lambda e: e.activation(out=rs4, in_=rs4, func=AF.Sqrt), reads=[b_sm], writes=[b_sm])
                    S.op("dve", lambda e: e.reciprocal(out=rs4, in_=rs4), reads=[b_sm], writes=[b_sm])
                    bankO, bbO = next_bank()
                    po = bankO[:, 0:256].bitcast(BF16).rearrange("p (h t) -> p h t", h=4)
                    for h in range(4):
                        hi = h % 2
                        S.op("dve", lambda e: e.tensor_scalar(out=hn[hi], in0=hbuf[h], scalar1=rs4[:, h:h + 1], scalar2=None, op0=ALU.mult),
                             reads=[b_hb[h], b_sm], writes=[b_hn[hi]])
                        S.op("pe", lambda e: e.transpose(po[:, h, :], hn[hi], ident_b), reads=[b_hn[hi], b_cst], writes=[bbO])
                    S.op("dve", lambda e: e.tensor_tensor(out=YC[:, 4:8, cs], in0=po, in1=bc_last(mlg, 128), op=ALU.mult), reads=[bbO, b_prm], writes=[b_yc[m]])

            if cfg.get("debug"):
                S.dma("sp", dbq, dr["dbg_g"].rearrange("p (a c) -> p a c", a=16), view(MS, 0, 4096, F32).rearrange("p (a c) -> p a c", a=16), reads=[b_g, b_sm], writes=[b_dbg])

            if stop == "ml":
                quiesce(); return
            TCH = 16; NC_ = 128
            WZ = view(A0, 0, 32 * KB, BF16).rearrange("p (q i x n) -> p q i x n", q=4, i=16, x=2)
            BD = view(A0, 32 * KB, 16 * KB, BF16).rearrange("p (q j n) -> p q j n", q=4, j=16)
            CP = view(A2, 0, 34816, BF16).rearrange("p (j r x n) -> p j r x n", j=17, r=16, x=2)
            EC = view(A2, 34816, 8192, F32).rearrange("p (r c) -> p r c", r=16)
            ES = view(A2, 34816 + 8192, 8192, F32).rearrange("p (r c) -> p r c", r=16)
            ZL = view(A2, 51200, 16384, F32).rearrange("p (r x c) -> p r x c", r=16, x=2)
            ZS = view(A2, 67584, 8256, BF16).rearrange("p (r x c) -> p r x c", r=16, x=2)
            SP_ = A2 + 76032
            PW = view(SP_, 0, 2176, F32).rearrange("p (j x r) -> p j x r", j=17, x=2)
            def sc(i):
                return view(SP_, 2176 + i * 64, 64, F32)
            assert SP_ + 2176 + 30 * 64 <= A2 + A2_B
            WW = view(A0, 0, 16384, F32).rearrange("p (r x c) -> p r x c", r=16, x=2)
            ZG = view(A2, 51200, 16384, BF16).rearrange("p (q i c) -> p q i c", q=4, i=16)
            GT = A0 + 16 * KB
            GEN = A2 + 51200
            b_s5 = S.buf("s5gen")
            b_wz = S.bufs(4, "wz"); b_bd = S.bufs(4, "bd"); b_cp = S.buf("cp"); b_tab = S.buf("tab")
            b_zl = S.bufs(16, "zl"); b_ww = S.bufs(16, "ww"); b_zs = S.bufs(16, "zs"); b_zg = S.bufs(4, "zg")
            olds = [b_g, b_sm] + b_kt + b_cl + [b_ct, b_tc] + b_vw + b_pt + b_hb + b_hn + b_va + [b_gr] + b_qk + b_og
            handoff([b_s5, b_cp, b_tab] + b_wz + b_bd + b_zl + b_ww + b_zs + b_zg, olds)
            s5q = S.dma_sem(f"s5q{l}")
            (s_are, s_aim, s_dt, s_mag, s_th, s_t, s_sin, s_cos, s_abr, s_abi, s_den, s_zr, s_sre, s_sim, s_t2, s_magL,
             s_l128r, s_l128i, s_t3, s_m128) = [sc(i) for i in range(20)]
            zend = sc(20)[:, 0:16]
            zend = view(SP_, 2176 + 20 * 64, 128, F32).rearrange("p (r x) -> p r x", x=2)
            sst = view(SP_, 2176 + 22 * 64, 128, F32).rearrange("p (r x) -> p r x", x=2)

            def TT(out, in0, in1, op, rd=(), wr=None, eng="dve"):
                S.op(eng, lambda e: e.tensor_tensor(out=out, in0=in0, in1=in1, op=op), reads=[b_s5] + list(rd), writes=[b_s5] if wr is None else wr)

            def TS(out, in0, s1, s2, op0, op1=None, rd=(), wr=None):
                if op1 is None:
                    S.op("dve", lambda e: e.tensor_scalar(out=out, in0=in0, scalar1=s1, scalar2=None, op0=op0), reads=[b_s5] + list(rd), writes=[b_s5] if wr is None else wr)
                else:
                    S.op("dve", lambda e: e.tensor_scalar(out=out, in0=in0, scalar1=s1, scalar2=s2, op0=op0, op1=op1), reads=[b_s5] + list(rd), writes=[b_s5] if wr is None else wr)

            def AC(out, in_, func, rd=(), wr=None, **kw):
                S.op("act", lambda e: e.activation(out=out, in_=in_, func=func, **kw), reads=[b_s5] + list(rd), writes=[b_s5] if wr is None else wr)

            def cmul(o_r, o_i, a_r, a_i, b_r, b_i, t1_, t2_, rd=(), wr=None, neg_im=False):
                TT(t1_, a_r, b_r, ALU.mult, rd); TT(t2_, a_i, b_i, ALU.mult, rd)
                TT(o_r, t1_, t2_, ALU.subtract, rd, wr)
                TT(t1_, a_r, b_i, ALU.mult, rd); TT(t2_, a_i, b_r, ALU.mult, rd)
                if neg_im:
                    TT(t1_, t1_, t2_, ALU.add, rd)
                    TS(o_i, t1_, -1.0, None, ALU.mult, rd=rd, wr=wr)
                else:
                    TT(o_i, t1_, t2_, ALU.add, rd, wr)

            if IMP:
                S.mute = True
            araw = view(GEN, 0, 1024, F32)
            S.dma("sp", s5q, araw[0:16, 0:128], dr["s5_a_re"][l].rearrange("(r gl) n -> r (gl n)", gl=2), writes=[b_s5])
            S.dma("sp", s5q, araw[0:16, 128:256], dr["s5_a_im"][l].rearrange("(r gl) n -> r (gl n)", gl=2), writes=[b_s5])
            ldt = dr["s5_log_dt"][l]
            for gl in range(2):
                S.dma("sp", s5q, s_dt[gl * 64:(gl + 1) * 64, :], bass.AP(ldt.tensor, ldt.offset + gl, [[0, 64], [2, 16]]), writes=[b_s5])
            Bsm = [view(GEN, 1024 + x * 1024, 1024, F32).rearrange("p (r c) -> p r c", r=16) for x in range(2)]
            for x, nm in enumerate(["s5_b_re", "s5_b_im"]):
                bsrc = dr[nm][l]
                for gl in range(2):
                    S.dma("sp", s5q, Bsm[x][gl * 64:(gl + 1) * 64, :, :],
                          bass.AP(bsrc.tensor, bsrc.offset + gl * 1024, [[16, 64], [2048, 16], [1, 16]]), writes=[b_s5])
            Craw = [view(GEN, 3072 + x * 1024, 1024, F32).rearrange("p (q n) -> p q n", q=4) for x in range(2)]
            for x, nm in enumerate(["s5_c_re", "s5_c_im"]):
                csrc = dr[nm][l]
                S.dma("sp", s5q, Craw[x], bass.AP(csrc.tensor, csrc.offset, [[64, 128], [8192, 4], [1, 64]]), writes=[b_s5])
            S.seal(s5q, [b_s5])
            bank, bb = next_bank()
            S.op("pe", lambda e: e.transpose(bank[:, 0:16], araw[0:16, 0:128], ident_f[0:16, 0:16]), reads=[b_s5, b_cst], writes=[bb])
            S.op("pe", lambda e: e.transpose(bank[:, 16:32], araw[0:16, 128:256], ident_f[0:16, 0:16]), reads=[b_s5, b_cst], writes=[bb])
            S.op("dve", lambda e: e.tensor_copy(out=s_are, in_=bank[:, 0:16]), reads=[bb], writes=[b_s5])
            S.op("dve", lambda e: e.tensor_copy(out=s_aim, in_=bank[:, 16:32]), reads=[bb], writes=[b_s5])
            PI = float(np.pi)
            AC(s_dt, s_dt, AF.Exp)
            TT(s_t, s_are, s_dt, ALU.mult)
            AC(s_mag, s_t, AF.Exp)
            AC(s_magL, s_t, AF.Exp, scale=float(TCH))
            AC(s_m128, s_t, AF.Exp, scale=float(TCH * NC_))
            TT(s_th, s_aim, s_dt, ALU.mult)
            for thr in (1.0, 3.0, 5.0, 7.0):
                TS(s_t, s_th, thr * PI, -2.0 * PI, ALU.is_gt, ALU.mult)
                if thr == 1.0:
                    TT(s_t2, s_th, s_t, ALU.add)
                else:
                    TT(s_t2, s_t2, s_t, ALU.add)
            AC(s_sin, s_t2, AF.Sin)
            TS(s_t3, s_t2, 0.5 * PI, None, ALU.add)
            TS(s_t, s_t3, PI, -2.0 * PI, ALU.is_gt, ALU.mult)
            TT(s_t3, s_t3, s_t, ALU.add)
            AC(s_cos, s_t3, AF.Sin)
            TT(s_abr, s_mag, s_cos, ALU.mult); TT(s_abi, s_mag, s_sin, ALU.mult)
            TT(s_t, s_are, s_are, ALU.mult); TT(s_t2, s_aim, s_aim, ALU.mult); TT(s_den, s_t, s_t2, ALU.add)
            S.op("dve", lambda e: e.reciprocal(out=s_den, in_=s_den), reads=[b_s5], writes=[b_s5])
            TS(s_zr, s_abr, -1.0, None, ALU.add)
            TT(s_t, s_zr, s_are, ALU.mult); TT(s_t2, s_abi, s_aim, ALU.mult); TT(s_t, s_t, s_t2, ALU.add); TT(s_sre, s_t, s_den, ALU.mult)
            TT(s_t, s_abi, s_are, ALU.mult); TT(s_t2, s_zr, s_aim, ALU.mult); TT(s_t, s_t, s_t2, ALU.subtract); TT(s_sim, s_t, s_den, ALU.mult)
            S.op("dve", lambda e: e.memset(PW[:, 0, 0, :], 1.0), reads=[b_s5], writes=[b_s5])
            S.op("dve", lambda e: e.memset(PW[:, 0, 1, :], 0.0), reads=[b_s5], writes=[b_s5])
            S.op("dve", lambda e: e.tensor_copy(out=PW[:, 1, 0, :], in_=s_abr), reads=[b_s5], writes=[b_s5])
            S.op("dve", lambda e: e.tensor_copy(out=PW[:, 1, 1, :], in_=s_abi), reads=[b_s5], writes=[b_s5])
            pt1 = view(GEN, 5120, 1024, F32).rearrange("p (j r) -> p j r", r=16)
            pt2 = view(GEN, 6144, 1024, F32).rearrange("p (j r) -> p j r", r=16)
            kk_ = 1
            while kk_ < 16:
                def bj(a):
                    return bass.AP(a.tensor, a.offset, [list(a.ap[0]), [0, kk_], list(a.ap[1])])
                cmul(PW[:, kk_ + 1:2 * kk_ + 1, 0, :], PW[:, kk_ + 1:2 * kk_ + 1, 1, :], PW[:, 1:kk_ + 1, 0, :], PW[:, 1:kk_ + 1, 1, :],
                     bj(PW[:, kk_, 0, :]), bj(PW[:, kk_, 1, :]), pt1[:, 0:kk_, :], pt2[:, 0:kk_, :])
                kk_ *= 2
            S.op("dve", lambda e: e.reciprocal(out=s_t, in_=s_magL), reads=[b_s5], writes=[b_s5])
            TT(EC[:, :, 0], PW[:, 16, 0, :], s_t, ALU.mult, wr=[b_s5, b_tab]); TT(ES[:, :, 0], PW[:, 16, 1, :], s_t, ALU.mult, wr=[b_s5, b_tab])
            et1 = view(GEN, 7168, 4096, F32).rearrange("p (r c) -> p r c", r=16)
            et2 = view(GEN, 11264, 4096, F32).rearrange("p (r c) -> p r c", r=16)
            kk_ = 1
            while kk_ < NC_:
                cmul(EC[:, :, kk_:2 * kk_], ES[:, :, kk_:2 * kk_], EC[:, :, 0:kk_], ES[:, :, 0:kk_],
                     bc_last(EC[:, :, kk_ - 1], kk_), bc_last(ES[:, :, kk_ - 1], kk_), et1[:, :, 0:kk_], et2[:, :, 0:kk_], rd=[b_tab], wr=[b_s5, b_tab])
                kk_ *= 2
            TT(s_l128r, EC[:, :, NC_ - 1], s_m128, ALU.mult, rd=[b_tab]); TT(s_l128i, ES[:, :, NC_ - 1], s_m128, ALU.mult, rd=[b_tab])
            Cin_ = [view(GEN, 5120 + x * 2048, 2048, F32).rearrange("p (q n) -> p q n", q=4) for x in range(2)]
            Cp = [view(GEN, 9216 + x * 2048, 2048, F32).rearrange("p (r n) -> p r n", r=16) for x in range(2)]
            ct1 = view(GEN, 13312, 2048, F32).rearrange("p (r n) -> p r n", r=16)
            ct2 = view(A0, 0, 2048, F32).rearrange("p (r n) -> p r n", r=16)
            for x in range(2):
                TS(Cin_[x][:, :, 0:64], Craw[x], par01[:, 0:1], None, ALU.mult, rd=[b_cst])
                TS(Cin_[x][:, :, 64:128], Craw[x], par01[:, 1:2], None, ALU.mult, rd=[b_cst])
                bank, bb = next_bank()
                for q in range(4):
                    S.op("pe", lambda e: e.transpose(bank[:, q * 128:(q + 1) * 128], Cin_[x][:, q, :], ident_f), inc=(q == 3), reads=[b_s5, b_cst], writes=[bb])
                S.op("dve", lambda e: e.tensor_copy(out=Cp[x].rearrange("p r n -> p (r n)"), in_=bank[:, :]), reads=[bb], writes=[b_s5])
            for j in range(17):
                pr = bc_last(PW[:, j, 0, :], 32); pi_ = bc_last(PW[:, j, 1, :], 32)
                TT(ct1, Cp[0], pr, ALU.mult); TT(ct2, Cp[1], pi_, ALU.mult, rd=b_wz, wr=[b_s5] + b_wz)
                TT(CP[:, j, :, 0, :], ct1, ct2, ALU.subtract, wr=[b_s5, b_cp])
                TT(ct1, Cp[0], pi_, ALU.mult); TT(ct2, Cp[1], pr, ALU.mult, rd=b_wz, wr=[b_s5] + b_wz)
                S.op("dve", lambda e: e.scalar_tensor_tensor(out=CP[:, j, :, 1, :], in0=ct1, scalar=-1.0, in1=ct2, op0=ALU.mult, op1=ALU.subtract),
                     reads=[b_s5], writes=[b_s5, b_cp])

            BB = [view(GEN, 5120 + x * 2048, 2048, F32).rearrange("p (r n) -> p r n", r=16) for x in range(2)]
            BBb = [view(GEN, 9216 + x * 1024, 1024, BF16).rearrange("p (r n) -> p r n", r=16) for x in range(2)]
            bt1 = view(GEN, 11264, 1024, F32).rearrange("p (r c) -> p r c", r=16)
            bt2 = view(GEN, 12288, 1024, F32).rearrange("p (r c) -> p r c", r=16)
            for x in range(2):
                S.op("dve", lambda e: e.memset(BB[x], 0.0), reads=[b_s5], writes=[b_s5])
            sre_b = bc_last(s_sre, 16); sim_b = bc_last(s_sim, 16)
            TT(bt1, Bsm[0], sre_b, ALU.mult); TT(bt2, Bsm[1], sim_b, ALU.mult)
            for gl in range(2):
                ps_ = slice(gl * 64, (gl + 1) * 64)
                TT(BB[0][ps_, :, gl * 16:(gl + 1) * 16], bt1[ps_], bt2[ps_], ALU.subtract)
            TT(bt1, Bsm[1], sre_b, ALU.mult); TT(bt2, Bsm[0], sim_b, ALU.mult)
            for gl in range(2):
                ps_ = slice(gl * 64, (gl + 1) * 64)
                TT(BB[1][ps_, :, gl * 16:(gl + 1) * 16], bt1[ps_], bt2[ps_], ALU.add)
            for x in range(2):
                S.op("dve", lambda e: e.tensor_copy(out=BBb[x], in_=BB[x]), reads=[b_s5], writes=[b_s5])
            bdt = view(GEN, 13312, 512, F32)
            for j in range(16):
                bank, bb = next_bank()
                for q in range(4):
                    for x in range(2):
                        S.op("pe", lambda e: e.matmul(bank[:, q * 128:(q + 1) * 128], lhsT=BBb[x][:, 4 * q:4 * q + 4, :].rearrange("p r n -> p (r n)"),
                                                      rhs=CP[:, j, 4 * q:4 * q + 4, x, :], start=(x == 0), stop=(x == 1)), inc=(q == 3 and x == 1),
                             reads=[b_s5, b_cp], writes=[bb])
                if j == 0:
                    for q in range(4):
                        S.op("dve", lambda e: e.tensor_tensor(out=bdt, in0=bank[:, q * 128:(q + 1) * 128], in1=bdm, op=ALU.mult), reads=[bb, b_cst, b_s5], writes=[b_s5])
                        S.op("dve", lambda e: e.scalar_tensor_tensor(out=BD[:, q, 0, :], in0=ident_f, scalar=dcol[:, q:q + 1], in1=bdt, op0=ALU.mult, op1=ALU.add),
                             reads=[b_s5, b_cst, b_prm], writes=[b_bd[q]])
                else:
                    bdm_b = bass.AP(bdm.tensor, bdm.offset, [list(bdm.ap[0]), [0, 4], list(bdm.ap[1])])
                    S.op("dve", lambda e: e.tensor_tensor(out=BD[:, :, j, :], in0=bank[:, :].rearrange("p (q n) -> p q n", q=4), in1=bdm_b, op=ALU.mult),
                         reads=[bb, b_cst], writes=b_bd)
            mt1 = view(GEN, 13824, 2048, F32).rearrange("p (r n) -> p r n", r=16)
            mt2 = view(GEN, 1024, 2048, F32).rearrange("p (r n) -> p r n", r=16)
            MB = [view(GEN, 3072 + x * 1024, 1024, BF16).rearrange("p (r n) -> p r n", r=16) for x in range(2)]
            for i in range(16):
                j = 15 - i
                pr = bc_last(PW[:, j, 0, :], 32); pi_ = bc_last(PW[:, j, 1, :], 32)
                TT(mt1, BB[0], pr, ALU.mult); TT(mt2, BB[1], pi_, ALU.mult); TT(MB[0], mt1, mt2, ALU.subtract)
                TT(mt1, BB[0], pi_, ALU.mult); TT(mt2, BB[1], pr, ALU.mult); TT(MB[1], mt1, mt2, ALU.add)
                bank, bb = next_bank()
                pb = bank[:, :].bitcast(BF16).rearrange("p (q x n) -> p q x n", q=4, x=2)
                for q in range(4):
                    for x in range(2):
                        S.op("pe", lambda e: e.transpose(pb[:, q, x, :], MB[x][:, 4 * q:4 * q + 4, :].rearrange("p r n -> p (r n)"), ident_b),
                             inc=(q == 3 and x == 1), reads=[b_s5, b_cst], writes=[bb])
                S.op("act", lambda e: e.activation(out=WZ[:, :, i, :, :], in_=pb, func=AF.Copy), reads=[bb], writes=b_wz)
            if stop == "s5gen":
                quiesce(); return
            handoff(b_zl, b_zl + [b_s5])
            for q in range(4):
                for rr in range(4):
                    r = 4 * q + rr
                    bank, bb = next_bank()
                    for x in range(2):
                        col = x * 128
                        for i in range(16):
                            S.op("pe", lambda e: e.matmul(bank[:, col:col + 128], lhsT=WZ[32 * rr:32 * rr + 32, q, i, x, :], rhs=US[32 * rr:32 * rr + 32, q, i, :],
                                                          start=(i == 0), stop=(i == 15), tile_position=(32 * rr, 0)), inc=(i == 15 and x == 1),
                                 reads=[b_wz[q], b_us[q]], writes=[bb])
                    S.op("act", lambda e: e.activation(out=ZL[:, r, :, :].rearrange("p x c -> p (x c)"), in_=bank[:, 0:256], func=AF.Copy),
                         reads=[bb], writes=[b_zl[r]])
            if cfg.get("debug"):
                S.dma("sp", dbq, dr["dbg_zl"], view(A2, 51200, 16384, F32), reads=b_zl, writes=[b_dbg])
                S.dma("sp", dbq, dr["dbg_sc"], view(SP_, 0, 4864, F32), reads=[b_s5], writes=[b_dbg])
                S.dma("sp", dbq, dr["dbg_cp"], view(A2, 0, 34816, BF16), reads=[b_cp], writes=[b_dbg])
                S.dma("sp", dbq, dr["dbg_bd"], view(A0, 32 * KB, 16 * KB, BF16), reads=b_bd, writes=[b_dbg])
                S.dma("sp", dbq, dr["dbg_wz"], view(A0, 0, 32 * KB, BF16), reads=b_wz, writes=[b_dbg])
            handoff(b_ww, b_wz + b_ww)
            dt1 = view(GT, 0, 8192, F32).rearrange("p (r c) -> p r c", r=16)
            dt2 = view(GT, 8192, 8192, F32).rearrange("p (r c) -> p r c", r=16)
            b_dt = S.buf("dt"); handoff([b_dt], b_wz)
            magL_b = bc_last(s_magL, NC_)

            def scan_and_mod(init_ap, b_init, final):
                for r in range(16):
                    for x in range(2):
                        ini = 0.0 if init_ap is None else init_ap[:, r, x:x + 1]
                        S.op("dve", lambda e: e.tensor_tensor_scan(out=ZL[:, r, x, :], data0=magL_b[:, r, :], data1=WW[:, r, x, :], initial=ini,
                                                                   op0=ALU.mult, op1=ALU.add), reads=[b_ww[r], b_s5] + ([b_init] if b_init else []), writes=[b_zl[r]])
                if not final:
                    cmul(zend[:, :, 0], zend[:, :, 1], ZL[:, :, 0, NC_ - 1], ZL[:, :, 1, NC_ - 1], EC[:, :, NC_ - 1], ES[:, :, NC_ - 1], s_t, s_t2,
                         rd=b_zl + [b_tab])
                else:
                    S.op("dve", lambda e: e.tensor_tensor(out=dt1, in0=EC, in1=ZL[:, :, 0, :], op=ALU.mult), reads=[b_tab] + b_zl, writes=[b_dt])
                    S.op("dve", lambda e: e.tensor_tensor(out=dt2, in0=ES, in1=ZL[:, :, 1, :], op=ALU.mult), reads=[b_tab] + b_zl, writes=[b_dt])
                    S.op("dve", lambda e: e.tensor_tensor(out=ZS[:, :, 0, 1:NC_ + 1], in0=dt1, in1=dt2, op=ALU.subtract), reads=[b_dt], writes=b_zs)
                    S.op("dve", lambda e: e.tensor_tensor(out=dt1, in0=EC, in1=ZL[:, :, 1, :], op=ALU.mult), reads=[b_tab] + b_zl, writes=[b_dt])
                    S.op("dve", lambda e: e.tensor_tensor(out=dt2, in0=ES, in1=ZL[:, :, 0, :], op=ALU.mult), reads=[b_tab] + b_zl, writes=[b_dt])
                    S.op("dve", lambda e: e.tensor_tensor(out=ZS[:, :, 1, 1:NC_ + 1], in0=dt1, in1=dt2, op=ALU.add), reads=[b_dt], writes=b_zs)
                    S.op("dve", lambda e: e.tensor_copy(out=ZS[:, :, :, 0], in_=init_ap), reads=[b_init], writes=b_zs)

            S.op("dve", lambda e: e.tensor_tensor(out=dt1, in0=EC, in1=ZL[:, :, 0, :], op=ALU.mult), reads=[b_tab] + b_zl, writes=[b_dt])
            S.op("dve", lambda e: e.tensor_tensor(out=dt2, in0=ES, in1=ZL[:, :, 1, :], op=ALU.mult), reads=[b_tab] + b_zl, writes=[b_dt])
            S.op("dve", lambda e: e.tensor_tensor(out=WW[:, :, 0, :], in0=dt1, in1=dt2, op=ALU.add), reads=[b_dt], writes=b_ww)
            S.op("dve", lambda e: e.tensor_tensor(out=dt1, in0=EC, in1=ZL[:, :, 1, :], op=ALU.mult), reads=[b_tab] + b_zl, writes=[b_dt])
            S.op("dve", lambda e: e.tensor_tensor(out=dt2, in0=ES, in1=ZL[:, :, 0, :], op=ALU.mult), reads=[b_tab] + b_zl, writes=[b_dt])
            S.op("dve", lambda e: e.tensor_tensor(out=WW[:, :, 1, :], in0=dt1, in1=dt2, op=ALU.subtract), reads=[b_dt], writes=b_ww)
            scan_and_mod(None, None, False)
            xfer("e_ww", view(A0, 0, 16384, F32), b_ww)
            xfer("e_cp", view(A2, 0, 34816, F32), [b_cp])
            xfer("e_bd", view(A0, 32 * KB, 16 * KB, F32), b_bd)
            xfer("e_tab", view(A2, 34816, 16384, F32), [b_tab])
            xfer("e_sp", view(SP_, 0, 4864, F32), [b_s5])
            if IMP:
                S.mute = False
            if cfg.get("debug"):
                S.dma("sp", dbq, dr["dbg_zend"], zend.rearrange("p r x -> p (r x)"), reads=[b_s5], writes=[b_dbg])
            if mode == "A":
                S.dma("sp", oq, dr["summ"][:, 0:32], zend.rearrange("p r x -> p (r x)"), reads=[b_s5], writes=[b_out])
                S.wait_all("sp", [b_out])
            if mode != "A":
                sa = dr["summ_all"]
                zall = view(GT, 0, 1024, F32).rearrange("p (j r x) -> p j r x", j=8, x=2)
                zq = S.dma_sem(f"zq{l}")
                S.dma("sp", zq, zall.rearrange("p j r x -> p j (r x)"), sa[:, :, 0:32].rearrange("j p c -> p j c"), reads=[b_dt], writes=[b_dt])
                S.op("dve", lambda e: e.memset(sst, 0.0), reads=[b_s5], writes=[b_s5])
                ctr = sc(24)[:, 0:16]; cti = sc(25)[:, 0:16]
                for j in range(8):
                    cmul(ctr, cti, sst[:, :, 0], sst[:, :, 1], s_l128r, s_l128i, s_t, s_t2)
                    TT(ctr, ctr, zall[:, j, :, 0], ALU.add, rd=[b_dt]); TT(cti, cti, zall[:, j, :, 1], ALU.add, rd=[b_dt])
                    TT(ctr, ctr, sst[:, :, 0], ALU.subtract); TT(cti, cti, sst[:, :, 1], ALU.subtract)
                    S.op("dve", lambda e: e.scalar_tensor_tensor(out=sst[:, :, 0], in0=ctr, scalar=pred[:, j:j + 1], in1=sst[:, :, 0], op0=ALU.mult, op1=ALU.add),
                         reads=[b_s5, b_cst], writes=[b_s5])
                    S.op("dve", lambda e: e.scalar_tensor_tensor(out=sst[:, :, 1], in0=cti, scalar=pred[:, j:j + 1], in1=sst[:, :, 1], op0=ALU.mult, op1=ALU.add),
                         reads=[b_s5, b_cst], writes=[b_s5])
                scan_and_mod(sst, b_s5, True)
                if cfg.get("debug"):
                    S.dma("sp", dbq, dr["dbg_zs"], view(A2, 67584, 8256, BF16), reads=b_zs, writes=[b_dbg])
                handoff(b_zg, b_zl + b_zg)
                for q in range(4):
                    for ib in range(4):
                        bank, bb = next_bank()
                        for i4 in range(4):
                            ip = ib * 4 + i4
                            col = i4 * 128
                            for i in range(ip + 1):
                                S.op("pe", lambda e: e.matmul(bank[:, col:col + 128], lhsT=BD[:, q, ip - i, :], rhs=US[:, q, i, :], start=(i == 0), stop=False),
                                     inc=False, reads=[b_bd[q], b_us[q]], writes=[bb])
                            for rr in range(4):
                                r = 4 * q + rr
                                for x in range(2):
                                    lastw = (rr == 3 and x == 1)
                                    S.op("pe", lambda e: e.matmul(bank[32 * rr:32 * rr + 32, col:col + 128], lhsT=CP[:, ip + 1, r, x, :], rhs=ZS[:, r, x, 0:NC_],
                                                                  start=False, stop=lastw, tile_position=(0, 32 * rr)), inc=(lastw and i4 == 3),
                                         reads=[b_cp, b_zs[r]], writes=[bb])
                        S.op("act", lambda e: e.activation(out=ZG[:, q, ib * 4:ib * 4 + 4, :].rearrange("p i c -> p (i c)"), in_=bank[:, :], func=AF.Gelu_apprx_tanh),
                             reads=[bb], writes=[b_zg[q]])
                if cfg.get("debug"):
                    S.dma("sp", dbq, dr["dbg_zg"], view(A2, 51200, 16384, BF16), reads=b_zg, writes=[b_dbg])
                wg = dr["s5_w_glu"][l].rearrange("(k p) n -> p k n", p=128)
                wgl, bwg = wload(wg, 4, 512)
                gate = view(GT, 0, 2048, F32)
                ZZ = [view(GT, 2048 + ft * 2048, 2048, F32) for ft in range(4)]
                sqb = [view(GT, 10240 + i * 1024, 1024, BF16) for i in range(2)]
                rst = view(GT, 12288, 2048, F32)
                b_gate = S.buf("gate"); b_zz = S.bufs(4, "zz"); b_sqb = S.bufs(2, "sqb"); b_rst = S.buf("rst")
                handoff([b_gate, b_rst] + b_zz + b_sqb, [b_dt] + b_ww)
                YCv = YC[:, 0:4, :].rearrange("p f (c i) -> p f i c", i=16)
                for cb in range(4):
                    bankq, bbq = next_bank()
                    for ft in range(4):
                        bank, bb = next_bank()
                        for kk in range(4):
                            S.op("pe", lambda e: e.matmul(bank[:, :], lhsT=wgl[:, kk, ft * 128:(ft + 1) * 128], rhs=ZG[:, kk, cb * 4:cb * 4 + 4, :],
                                                          start=(kk == 0), stop=(kk == 3)), inc=(kk == 3), reads=[bwg] + b_zg, writes=[bb])
                        S.op("act", lambda e: e.activation(out=gate, in_=bank[:, :], func=AF.Sigmoid, bias=bglu[:, ft:ft + 1]), reads=[bb, b_prm], writes=[b_gate])
                        S.op("dve", lambda e: e.tensor_tensor(out=ZZ[ft], in0=ZG[:, ft, cb * 4:cb * 4 + 4, :].rearrange("p i c -> p (i c)"), in1=gate, op=ALU.mult),
                             reads=[b_zg[ft], b_gate], writes=[b_zz[ft]])
                        S.op("act", lambda e: e.activation(out=sqb[ft % 2], in_=ZZ[ft], func=AF.Square), reads=[b_zz[ft]], writes=[b_sqb[ft % 2]])
                        S.op("pe", lambda e: e.matmul(bankq[:, :], lhsT=ones_b, rhs=sqb[ft % 2], start=(ft == 0), stop=(ft == 3)), inc=True,
                             reads=[b_sqb[ft % 2], b_cst], writes=[bbq])
                    S.op("dve", lambda e: e.tensor_scalar(out=rst, in0=bankq[:, :], scalar1=1.0 / 512, scalar2=EPS, op0=ALU.mult, op1=ALU.add), reads=[bbq], writes=[b_rst])
                    S.op("act", lambda e: e.activation(out=rst, in_=rst, func=AF.Sqrt), reads=[b_rst], writes=[b_rst])
                    S.op("dve", lambda e: e.reciprocal(out=rst, in_=rst), reads=[b_rst], writes=[b_rst])
                    for ft in range(4):
                        S.op("dve", lambda e: e.scalar_tensor_tensor(out=YCv[:, ft, cb * 4:cb * 4 + 4, :], in0=ZZ[ft].rearrange("p (i c) -> p i c", i=4),
                                                                     scalar=outg[:, ft:ft + 1], in1=rst.rearrange("p (i c) -> p i c", i=4), op0=ALU.mult, op1=ALU.mult),
                             reads=[b_zz[ft], b_rst, b_prm], writes=b_yc)
                if cfg.get("debug"):
                    S.dma("sp", dbq, dr["dbg_yc"], view(A1, 0, 32 * KB, BF16), reads=b_yc, writes=[b_dbg])

            if mode != "A":
                if stop == "s5":
                    quiesce(); return
                S.handoff(X_b, b_qk + b_og + b_us + b_wz + b_bd + b_ww + [b_dt, b_gate, b_rst] + b_zz + b_sqb)
                for tt in range(NTT):
                    S.dma("sp", xq[tt], X[:, tt, :], xin_ap[tt * 128:(tt + 1) * 128, :], writes=[X_b[tt]])
                wo = dr["w_out"][l].rearrange("(k p) n -> p k n", p=128)
                ws = WStream([(wo[:, :, h * 512:(h + 1) * 512], 8, 512) for h in range(2)])
                for h in range(2):
                    wc, bw = ws.get(h)
                    for tt in range(NTT):
                        bank, bb = next_bank()
                        for kk in range(8):
                            S.op("pe", lambda e: e.matmul(bank[:, :], lhsT=YC[:, kk, tt * 128:(tt + 1) * 128], rhs=wc[:, kk, :],
                                                          start=(kk == 0), stop=(kk == 7)), inc=(kk == 7), reads=[bw, b_yc[tt]], writes=[bb])
                        S.op("dve", lambda e: e.tensor_tensor(out=X[:, tt, h * 512:(h + 1) * 512], in0=X[:, tt, h * 512:(h + 1) * 512], in1=bank[:, :], op=ALU.add),
                             reads=[bb, X_b[tt]], writes=[X_b[tt]])

                if cfg.get("dbg_x1"):
                    b_o1 = S.buf("o1")
                    for tt in range(NTT):
                        S.dma("sp", oq, xout_ap[tt * 128:(tt + 1) * 128, :], X[:, tt, :], reads=[X_b[tt]], writes=[b_o1])
                    S.wait_all("sp", [b_o1])
                    return
                if stop == "wout":
                    quiesce(); return
                S.handoff(HT_b, b_yc)
                a2_users = [b_g, b_sm] + b_kt + b_cl + [b_ct, b_tc] + b_vw + b_pt + b_hb + b_hn + [b_cp, b_tab, b_s5] + b_zl + b_zs + b_zg + b_va + [b_gr]
                handoff([b_junk, b_xh] + b_xnb, a2_users)
                norm_to_hT(g2, False, False)
                if cfg.get("dbg_ht2"):
                    b_o1 = S.buf("o1")
                    S.dma("sp", oq, dr["dbg_ht"], view(A1, 0, 32832, BF16)[:, 0:8 * 2051], reads=HT_b, writes=[b_o1])
                w1 = dr["w_ff1"][l].rearrange("(k p) n -> p k n", p=128)
                w2 = dr["w_ff2"][l].rearrange("(k p) n -> p k n", p=128)
                c1 = [(w1[:, :, hc * 512:(hc + 1) * 512], 8, 512) for hc in range(8)]
                c2 = [(w2[:, hc * 4:(hc + 1) * 4, :], 4, 1024) for hc in range(8)]
                chunks = [c1[0]]
                for hc in range(8):
                    if hc + 1 < 8:
                        chunks.append(c1[hc + 1])
                    chunks.append(c2[hc])
                ws = WStream(chunks)
                k.wci = 0

                def wnext():
                    r = ws.get(k.wci)
                    k.wci += 1
                    return r
                hid = [view(SCR, i * 16 * KB, 16 * KB, BF16).rearrange("p (f t) -> p f t", f=4) for i in range(2)]
                b_hid = [S.bufs(4, f"hid{i}") for i in range(2)]
                sq = [view(SCR, 32 * KB + i * 2048, 2048, F32) for i in range(2)]
                b_sq = S.bufs(2, "sq")
                k.sqi = 0
                handoff(b_hid[0] + b_hid[1] + b_sq, a2_users + [b_junk, b_xh] + b_xnb)

                def ffn1(hc):
                    wc, bw = wnext()
                    hb = hid[hc % 2]
                    for ft in range(4):
                        for nb in range(4):
                            bank, bb = next_bank()
                            for kk in range(8):
                                S.op("pe", lambda e: e.matmul(bank[:, :], lhsT=wc[:, kk, ft * 128:(ft + 1) * 128], rhs=HT[:, kk, 3 + nb * 512:3 + (nb + 1) * 512],
                                                              start=(kk == 0), stop=(kk == 7)), inc=(kk == 7), reads=[bw] + HT_b[nb * 4:nb * 4 + 4], writes=[bb])
                            si = k.sqi % 2; k.sqi += 1
                            S.op("act", lambda e: e.activation(out=sq[si], in_=bank[:, :], func=AF.Square), reads=[bb], writes=[b_sq[si]])
                            S.op("dve", lambda e: e.scalar_tensor_tensor(out=hb[:, ft, nb * 512:(nb + 1) * 512], in0=bank[:, :], scalar=0.0, in1=sq[si],
                                                                         op0=ALU.is_gt, op1=ALU.mult), reads=[bb, b_sq[si]], writes=[b_hid[hc % 2][nb]])

                def ffn2(hc):
                    wc, bw = wnext()
                    hb = hid[hc % 2]
                    for tt in range(NTT):
                        for h in range(2):
                            bank, bb = next_bank()
                            for kk in range(4):
                                S.op("pe", lambda e: e.matmul(bank[:, :], lhsT=hb[:, kk, tt * 128:(tt + 1) * 128], rhs=wc[:, kk, h * 512:(h + 1) * 512],
                                                              start=(kk == 0), stop=(kk == 3)), inc=(kk == 3), reads=[bw, b_hid[hc % 2][tt // 4]], writes=[bb])
                            S.op("dve", lambda e: e.tensor_tensor(out=X[:, tt, h * 512:(h + 1) * 512], in0=X[:, tt, h * 512:(h + 1) * 512], in1=bank[:, :], op=ALU.add),
                                 reads=[bb, X_b[tt]], writes=[X_b[tt]])

                ffn1(0)
                for hc in range(8):
                    if hc + 1 < 8:
                        ffn1(hc + 1)
                    ffn2(hc)

                if last:
                    gfin = view(SCR, 40 * KB, 4096, F32)
                    b_gf = S.buf("gfin")
                    fg = dr["final_norm_g"]
                    S.dma("sp", gq, gfin, bass.AP(fg.tensor, fg.offset, [[0, 128], [1, D]]), writes=[b_gf])
                    ot = [view(SCR, 44 * KB + i * 4096, 4096, F32) for i in range(2)]
                    b_ot = S.bufs(2, "ot")
                    handoff([b_junk], [b_junk] + b_hid[0] + b_hid[1])
                    stats_A(0)
                    for g4 in range(4):
                        if g4 + 1 < 4:
                            stats_A(g4 + 1)
                        stats_B(g4)
                        for tt in range(4 * g4, 4 * g4 + 4):
                            S.op("dve", lambda e: e.scalar_tensor_tensor(out=ot[tt % 2], in0=X[:, tt, :], scalar=rstd[:, tt:tt + 1], in1=gfin,
                                                                         op0=ALU.mult, op1=ALU.mult), reads=[X_b[tt], b_sg[g4], b_gf], writes=[b_ot[tt % 2]])
                            S.dma("sp", oq, xout_ap[tt * 128:(tt + 1) * 128, :], ot[tt % 2], reads=[b_ot[tt % 2]], writes=[b_out])
                else:
                    for tt in range(NTT):
                        S.dma("sp", oq, xout_ap[tt * 128:(tt + 1) * 128, :], X[:, tt, :], reads=[X_b[tt]], writes=[b_out])
                S.wait_all("sp", [b_out])

        layers = cfg["layers"]
        for li, l in enumerate(layers):
            layer(l, dr["xin"], dr.get("xhalo"), dr.get("xout"), last=cfg.get("final", False) and li == len(layers) - 1)
    return nc


_NC_CACHE = {}
N_CORES = 8
LAYER_KEYS = ["norm_mix_g", "w_in", "w_out", "norm_ffn_g", "w_ff1", "w_ff2", "ml_conv_w", "ml_conv_b", "ml_b_i", "ml_b_f",
              "ml_norm_g", "s5_a_re", "s5_a_im", "s5_log_dt", "s5_b_re", "s5_b_im", "s5_c_re", "s5_c_im", "s5_d", "s5_w_glu",
              "s5_b_glu", "s5_out_g"]
A_KEYS = ["norm_mix_g", "w_in", "ml_conv_w", "ml_conv_b", "ml_b_i", "ml_b_f", "s5_a_re", "s5_a_im", "s5_log_dt", "s5_b_re", "s5_b_im",
          "s5_c_re", "s5_c_im", "s5_d", "ml_norm_g", "s5_b_glu", "s5_out_g"]
B_KEYS = ["w_out", "norm_ffn_g", "w_ff1", "w_ff2", "s5_w_glu", "ml_norm_g", "s5_b_glu", "s5_out_g"]
XF_NAMES = ["e_a0", "e_va", "e_gates", "e_cl", "e_ww", "e_cp", "e_bd", "e_tab", "e_sp"]
A_CONST = ["ident", "causal", "ones", "par01", "bdmask"]
B_CONST = ["ident", "causal", "ones"]


def _get_nc(mode, final):
    key = (mode, final)
    if key not in _NC_CACHE:
        _NC_CACHE[key] = build(dict(layers=[0], nlayers=1, mode=mode, final=final, debug=False))
    return _NC_CACHE[key]


def _consts():
    par = np.zeros((128, 2), np.float32)
    par[:, 1] = (np.arange(128) // 16) % 2
    par[:, 0] = 1 - par[:, 1]
    return {"ident": np.eye(128, dtype=np.float32), "causal": np.triu(np.ones((128, 128), np.float32)),
            "ones": np.ones((128, 128), np.float32), "par01": par,
            "bdmask": np.kron(np.eye(8), np.ones((16, 16))).astype(np.float32)}


def kernel(**inputs):
    x = np.ascontiguousarray(inputs["x"], dtype=np.float32)
    nb, ls, d = x.shape
    per = ls // 4
    consts = _consts()
    cur = [np.ascontiguousarray(x[c // 4, (c % 4) * per:(c % 4 + 1) * per]) for c in range(N_CORES)]
    preds = []
    for c in range(N_CORES):
        p = np.zeros((128, 8), np.float32)
        for j in range(N_CORES):
            if j // 4 == c // 4 and j < c:
                p[:, j] = 1.0
        preds.append(p)
    depth = inputs["w_in"].shape[0]
    for l in range(depth):
        halos = [np.zeros((3, d), np.float32) if c % 4 == 0 else np.ascontiguousarray(cur[c - 1][-3:]) for c in range(N_CORES)]
        lw = {k: np.ascontiguousarray(np.asarray(inputs[k], dtype=np.float32)[l:l + 1]) for k in LAYER_KEYS}
        final = (l == depth - 1)
        ncA = _get_nc("A", False)
        mapsA = []
        for c in range(N_CORES):
            m = {"xin": cur[c], "xhalo": halos[c], "pred": preds[c]}
            m.update({k: consts[k] for k in A_CONST})
            m.update({k: lw[k] for k in A_KEYS})
            mapsA.append(m)
        resA = run_bass_kernel_spmd(ncA, mapsA, core_ids=list(range(N_CORES)))
        summ_all = np.ascontiguousarray(np.stack([np.asarray(resA.results[c]["summ"]) for c in range(N_CORES)]))
        ncB = _get_nc("B", final)
        mapsB = []
        for c in range(N_CORES):
            m = {"xin": cur[c], "pred": preds[c], "summ_all": summ_all,
                 "final_norm_g": np.ascontiguousarray(inputs["final_norm_g"], dtype=np.float32)}
            m.update({k: consts[k] for k in B_CONST})
            m.update({k: lw[k] for k in B_KEYS})
            m.update({k: np.asarray(resA.results[c][k]) for k in XF_NAMES})
            mapsB.append(m)
        resB = run_bass_kernel_spmd(ncB, mapsB, core_ids=list(range(N_CORES)))
        cur = [np.asarray(resB.results[c]["xout"]) for c in range(N_CORES)]
    out = np.empty_like(x)
    for c in range(N_CORES):
        out[c // 4, (c % 4) * per:(c % 4 + 1) * per] = cur[c]
    return out
```

```python
import numpy as np
import concourse.bass as bass
import concourse.mybir as mybir
from concourse.bass_utils import run_bass_kernel_spmd

F32 = mybir.dt.float32
BF16 = mybir.dt.bfloat16
AF = mybir.ActivationFunctionType
ALU = mybir.AluOpType
AX = mybir.AxisListType


class Buf:
    __slots__ = ("name", "w", "r")

    def __init__(self, name):
        self.name = name
        self.w = {}
        self.r = {}


class Sched:
    def __init__(self, nc, ctx):
        self.nc = nc
        self.ctx = ctx
        self.eng = {"pe": nc.tensor, "act": nc.scalar, "dve": nc.vector, "pool": nc.gpsimd, "sp": nc.sync}
        self.sem = {}
        self.cnt = {}
        for k in self.eng:
            self.sem[k] = ctx.enter_context(nc.semaphore("s_" + k))
            self.cnt[k] = 0
        self.waited = {k: {} for k in self.eng}
        self.ndma = 0
        self.nbuf = 0
        self.mute = False

    def buf(self, name=None):
        self.nbuf += 1
        return Buf(name or f"b{self.nbuf}")

    def bufs(self, n, name="b"):
        return [self.buf(f"{name}{i}") for i in range(n)]

    def dma_sem(self, name=None):
        self.ndma += 1
        key = name or f"dma{self.ndma}"
        self.sem[key] = self.ctx.enter_context(self.nc.semaphore("s_" + key))
        self.cnt[key] = 0
        return key

    def _deps(self, e, reads, writes):
        deps = {}
        for b in reads:
            for k, c in b.w.items():
                if deps.get(k, 0) < c:
                    deps[k] = c
        for b in writes:
            for k, c in b.w.items():
                if deps.get(k, 0) < c:
                    deps[k] = c
            for k, c in b.r.items():
                if deps.get(k, 0) < c:
                    deps[k] = c
        eng = self.eng[e]
        for k, c in deps.items():
            if k == e and e == "pe":
                continue
            if self.waited[e].get(k, 0) < c:
                eng.wait_ge(self.sem[k], c)
                self.waited[e][k] = c

    def _record(self, key, c, reads, writes):
        for b in writes:
            b.w = {key: c}
            b.r = {}
        for b in reads:
            if b.r.get(key, 0) < c:
                b.r[key] = c

    def op(self, e, fn, reads=(), writes=(), inc=True):
        if self.mute:
            return None
        self._deps(e, reads, writes)
        ins = fn(self.eng[e])
        if inc:
            self.cnt[e] += 1
            ins.then_inc(self.sem[e], 1)
            self._record(e, self.cnt[e], reads, writes)
        else:
            self._record(e, self.cnt[e] + 1, reads, writes)
        return ins

    def seal(self, key, bufs):
        if self.mute:
            return
        c = self.cnt[key]
        for b in bufs:
            if key in b.w:
                b.w[key] = c

    def handoff(self, news, olds):
        w = {}
        r = {}
        for ob in olds:
            for k2, c2 in ob.w.items():
                if w.get(k2, 0) < c2:
                    w[k2] = c2
            for k2, c2 in ob.r.items():
                if r.get(k2, 0) < c2:
                    r[k2] = c2
        for nb in news:
            nb.w = dict(w)
            nb.r = dict(r)

    def dma(self, q, dsem, out, in_, reads=(), writes=(), **kw):
        if self.mute:
            return None
        self._deps(q, reads, writes)
        ins = self.eng[q].dma_start(out=out, in_=in_, **kw)
        self.cnt[dsem] += 16
        ins.then_inc(self.sem[dsem], 16)
        self._record(dsem, self.cnt[dsem], reads, writes)
        return ins

    def wait_all(self, e, bufs):
        if self.mute:
            return
        self._deps(e, bufs, ())


import numpy as np
from contextlib import ExitStack

NT = 2048
NTT = 16
D = 1024
DIN = 2568
DFF = 4096
EPS = 1e-6
KB = 1024


class KB_:
    pass


def build(cfg):
    nc = bass.Bass("TRN2", target_bir_lowering=False)
    k = KB_()
    k.nc = nc
    k.cfg = cfg
    L = cfg.get("nlayers", 1)
    dr = {}

    def din(name, shape, dt=F32):
        dr[name] = nc.dram_tensor(name, list(shape), dt, kind="ExternalInput").ap()
        return dr[name]

    def dout(name, shape, dt=F32):
        dr[name] = nc.dram_tensor(name, list(shape), dt, kind="ExternalOutput").ap()
        return dr[name]

    mode = cfg.get("mode", "B")
    IMP = (mode == "B")
    EXP = (mode == "A")
    XF = [("e_a0", 16384), ("e_va", 4256), ("e_gates", 1152), ("e_cl", 8256), ("e_ww", 4096), ("e_cp", 8704), ("e_bd", 4096),
          ("e_tab", 4096), ("e_sp", 1216)]
    din("xin", [NT, D])
    if IMP:
        _din_real = din

        def din(name, shape, dt=F32, _real=_din_real):
            dr[name] = nc.dram_tensor(name, list(shape), dt).ap()
            return dr[name]
    if True:
        din("xhalo", [3, D])
        din("norm_mix_g", [L, D]); din("w_in", [L, D, DIN])
        din("ml_conv_w", [L, 4, 1024]); din("ml_conv_b", [L, 1024])
        din("ml_b_i", [L, 4]); din("ml_b_f", [L, 4])
        din("s5_a_re", [L, 32, 64]); din("s5_a_im", [L, 32, 64]); din("s5_log_dt", [L, 32])
        din("s5_b_re", [L, 32, 64, 16]); din("s5_b_im", [L, 32, 64, 16]); din("s5_c_re", [L, 32, 16, 64]); din("s5_c_im", [L, 32, 16, 64])
        din("s5_d", [L, 32, 16])
        din("par01", [128, 2]); din("bdmask", [128, 128])
    if IMP:
        din = _din_real
    if mode != "A":
        din("w_out", [L, D, D])
        din("norm_ffn_g", [L, D]); din("w_ff1", [L, D, DFF]); din("w_ff2", [L, DFF, D])
        din("final_norm_g", [D])
        din("s5_w_glu", [L, 512, 512])
        din("summ_all", [8, 128, 556])
    din("ident", [128, 128]); din("causal", [128, 128]); din("ones", [128, 128])
    din("ml_norm_g", [L, 512]); din("s5_b_glu", [L, 512]); din("s5_out_g", [L, 512])
    din("pred", [128, 8])
    if EXP:
        dout("summ", [128, 556])
        for nm_, w_ in XF:
            dout(nm_, [128, w_])
    if IMP:
        for nm_, w_ in XF:
            din(nm_, [128, w_])
    if mode != "A":
        dout("xout", [NT, D])
    if cfg.get("dbg_ht2"):
        dout("dbg_ht", [128, 8 * 2051], BF16)
    if cfg.get("debug"):
        dout("dbg_u", [128, 4 * 2048], BF16)
        dout("dbg_qk", [128, 8 * 2048], BF16)
        dout("dbg_v", [128, 16 * 4 * 129], BF16)
        dout("dbg_o", [128, 16 * 512], BF16)
        dout("dbg_if", [128, 128])
        dout("dbg_yc", [128, 8 * 2048], BF16)
        dout("dbg_g", [128, 16 * 64])
        dout("dbg_zl", [128, 4096]); dout("dbg_zs", [128, 16 * 2 * 129], BF16); dout("dbg_zg", [128, 8192], BF16)
        dout("dbg_sc", [128, 1216]); dout("dbg_cp", [128, 17408], BF16); dout("dbg_bd", [128, 8192], BF16); dout("dbg_wz", [128, 16384], BF16)
        dout("dbg_zend", [128, 32])
    k.dr = dr

    with ExitStack() as ctx:
        S = Sched(nc, ctx)
        k.S = S
        ctx.enter_context(nc.allow_non_contiguous_dma(reason="small param loads"))
        ctx.enter_context(nc.allow_low_precision(reason="bf16 matmul operands"))
        A0_B, A1_B, A2_B = 64 * KB, 33 * KB + 256, 79 * KB
        arena = ctx.enter_context(nc.sbuf_tensor("arena", [128, (A0_B + A1_B + A2_B) // 4], F32))
        ring = ctx.enter_context(nc.sbuf_tensor("ring", [128, 3 * 4096], BF16))
        cst = ctx.enter_context(nc.sbuf_tensor("cst", [128, 1024], F32))
        banks = [ctx.enter_context(nc.psum_tensor(f"ps{i}", [128, 512], F32)) for i in range(8)]
        bank_bufs = S.bufs(8, "bank")
        k.bank_i = 0
        block = ctx.enter_context(nc.Block())

        def view(base, off, nbytes, dt):
            assert off % 4 == 0 and nbytes % 4 == 0
            a = arena[:, (base + off) // 4:(base + off + nbytes) // 4]
            return a if dt == F32 else a.bitcast(dt)
        A0, A1, A2 = 0, A0_B, A0_B + A1_B

        def next_bank():
            i = k.bank_i
            k.bank_i = (i + 1) % 8
            return banks[i], bank_bufs[i]

        ident_f = cst[:, 0:128]
        ident_b = cst[:, 128:192].bitcast(BF16)
        b_cst = S.buf("cst")
        dq = S.dma_sem("dq_misc")
        S.dma("sp", dq, ident_f, dr["ident"], writes=[b_cst])
        tri_f = cst[:, 384:512]
        ones_f = cst[:, 512:640]
        tri_b = cst[:, 192:256].bitcast(BF16)
        S.dma("sp", dq, tri_f, dr["causal"], writes=[b_cst])
        S.dma("sp", dq, ones_f, dr["ones"], writes=[b_cst])
        par01 = cst[:, 752:754]
        bdm = cst[:, 768:896]
        ones_b = cst[:, 896:960].bitcast(BF16)
        if not IMP:
            S.dma("sp", dq, par01, dr["par01"], writes=[b_cst])
        pred = cst[:, 972:980]
        pmask = cst[:, 980:988]
        S.dma("sp", dq, pred, dr["pred"], writes=[b_cst])
        if not IMP:
            S.dma("sp", dq, bdm, dr["bdmask"], writes=[b_cst])
        S.seal(dq, [b_cst])
        S.op("dve", lambda e: e.tensor_copy(out=ones_b, in_=ones_f), reads=[b_cst], writes=[b_cst])
        S.op("dve", lambda e: e.tensor_scalar(out=pmask, in0=pred, scalar1=1e6, scalar2=-1e6, op0=ALU.mult, op1=ALU.add), reads=[b_cst], writes=[b_cst])
        S.op("dve", lambda e: e.tensor_copy(out=ident_b, in_=ident_f), reads=[b_cst], writes=[b_cst])
        S.op("dve", lambda e: e.tensor_copy(out=tri_b, in_=tri_f), reads=[b_cst], writes=[b_cst])

        X = view(A0, 0, 64 * KB, F32).rearrange("p (t d) -> p t d", t=NTT)
        X_b = S.bufs(NTT, "X")
        HT = view(A1, 0, 8 * 2051 * 2 + 0, BF16) if False else view(A1, 0, 32832, BF16)[:, 0:8 * 2051].rearrange("p (k t) -> p k t", k=8)
        HT_b = S.bufs(NTT + 1, "HT")
        YC = view(A1, 0, 32 * KB, BF16).rearrange("p (k t) -> p k t", k=8)
        QK = view(A0, 0, 32 * KB, BF16).rearrange("p (f t) -> p f t", f=8)
        OG = view(A0, 32 * KB, 16 * KB, BF16).rearrange("p (t d) -> p t d", t=NTT)
        US = view(A0, 48 * KB, 16 * KB, BF16).rearrange("p (q i c) -> p q i c", q=4, i=16)
        VA = view(A2, 0, 16512, BF16).rearrange("p (t h v) -> p t h v", t=NTT, h=4)
        GR = view(A2, 16512, 512, F32)
        SCR = A2 + 17024

        xq = [S.dma_sem(f"xq{i}") for i in range(16)]
        hq = S.dma_sem("hq"); gq = S.dma_sem("gq")
        oq = S.dma_sem("oq")
        wq = [S.dma_sem(f"wq{i}") for i in range(3)]
        ring_b = S.bufs(3, "ring")
        k.wi = 0

        def wload(src_ap, nk, ncols):
            i = k.wi % 3
            k.wi += 1
            v = ring[:, i * 4096: i * 4096 + nk * ncols].rearrange("p (k n) -> p k n", k=nk)
            S.dma("pool", wq[i], v, src_ap, writes=[ring_b[i]])
            return v, ring_b[i]

        class WStream:
            def __init__(self, chunks):
                self.chunks = chunks
                self.loaded = []

            def get(self, i, ahead=2):
                while len(self.loaded) < min(len(self.chunks), i + 1 + ahead):
                    self.loaded.append(wload(*self.chunks[len(self.loaded)]))
                return self.loaded[i]

        k.pq = None

        def load_pvec(dst, src_1d, b, q="sp"):
            S.dma(q, k.pq, dst, src_1d.rearrange("(k p) -> p k", p=128), writes=[b])

        tmp_b = S.bufs(4, "tmp")

        def quiesce():
            for e_ in ("pe", "act", "dve", "pool"):
                if S.cnt[e_] > 0:
                    nc.sync.wait_ge(S.sem[e_], S.cnt[e_])
            for key_, c_ in S.cnt.items():
                if key_ not in S.eng and c_ > 0:
                    nc.sync.wait_ge(S.sem[key_], c_)

        def layer(l, xin_ap, xh_ap, xout_ap, last):
            stop = cfg.get("stop")
            prm = cst[:, 256:256 + 64]
            b_prm = S.buf("prm")
            pq_l = S.dma_sem(f"pq{l}"); k.pq = pq_l
            g1 = cst[:, 640:648]; g2 = cst[:, 648:656]
            cw = cst[:, 656:688].rearrange("p (j f) -> p j f", j=4)
            cb = cst[:, 688:696]
            b_out = S.buf("out")
            if not IMP:
                load_pvec(g1, dr["norm_mix_g"][l], b_prm)
                for j in range(4):
                    load_pvec(cw[:, j, :], dr["ml_conv_w"][l, j], b_prm)
                load_pvec(cb, dr["ml_conv_b"][l], b_prm)
            if mode != "A":
                load_pvec(g2, dr["norm_ffn_g"][l], b_prm)
            mlg = cst[:, 740:744]
            load_pvec(mlg, dr["ml_norm_g"][l], b_prm)
            bif = cst[:, 744:752]
            dcol = cst[:, 960:964]; bglu = cst[:, 964:968]; outg = cst[:, 968:972]
            if not IMP:
                bi_ = dr["ml_b_i"][l]; bf_ = dr["ml_b_f"][l]
                S.dma("sp", pq_l, bif[:, 0:4], bass.AP(bi_.tensor, bi_.offset, [[0, 128], [1, 4]]), writes=[b_prm])
                S.dma("sp", pq_l, bif[:, 4:8], bass.AP(bf_.tensor, bf_.offset, [[0, 128], [1, 4]]), writes=[b_prm])
                load_pvec(dcol, dr["s5_d"][l].rearrange("g p -> (g p)"), b_prm)
            load_pvec(bglu, dr["s5_b_glu"][l], b_prm)
            load_pvec(outg, dr["s5_out_g"][l], b_prm)
            S.seal(pq_l, [b_prm])

            def xfer(name, ap, bufs):
                was = S.mute; S.mute = False
                if EXP:
                    S.dma("sp", oq, dr[name], ap, reads=bufs, writes=[b_out])
                elif IMP:
                    q_ = S.dma_sem(f"{name}_{l}")
                    S.dma("sp", q_, ap, dr[name], writes=bufs)
                S.mute = was
            if IMP:
                S.mute = True
            ssq = cst[:, 700:717]
            rstd = cst[:, 720:737]
            b_st = S.bufs(17, "st")
            junk = view(SCR, 0, 2048, BF16)
            b_junk = S.buf("junk")
            xnb = [view(SCR, 2048 + i * 2048, 2048, BF16) for i in range(2)]
            b_xnb = S.bufs(2, "xnb")
            xh_t = view(SCR, 6144, 4096, F32)
            b_xh = S.buf("xh")

            def rms_stats(src, np_, col, bsrc):
                S.op("act", lambda e: e.activation(out=junk[:np_], in_=src, func=AF.Square, accum_out=ssq[:np_, col:col + 1]),
                     reads=[bsrc], writes=[b_junk, b_st[col]])
                S.op("dve", lambda e: e.tensor_scalar(out=rstd[:np_, col:col + 1], in0=ssq[:np_, col:col + 1], scalar1=1.0 / D, scalar2=EPS,
                                                      op0=ALU.mult, op1=ALU.add), reads=[b_st[col]], writes=[b_st[col]])
                S.op("act", lambda e: e.activation(out=rstd[:np_, col:col + 1], in_=rstd[:np_, col:col + 1], func=AF.Sqrt),
                     reads=[b_st[col]], writes=[b_st[col]])
                S.op("dve", lambda e: e.reciprocal(out=rstd[:np_, col:col + 1], in_=rstd[:np_, col:col + 1]), reads=[b_st[col]], writes=[b_st[col]])

            b_sg = S.bufs(5, "stg")

            def stats_A(g4):
                tiles = [NTT] if g4 == 4 else range(4 * g4, 4 * g4 + 4)
                for tt in tiles:
                    halo = tt == NTT
                    np_ = 3 if halo else 128
                    src, bsrc = (xh_t[:3, :], b_xh) if halo else (X[:, tt, :], X_b[tt])
                    S.op("act", lambda e: e.activation(out=junk[:np_], in_=src, func=AF.Square, accum_out=ssq[:np_, tt:tt + 1]),
                         reads=[bsrc], writes=[b_junk, b_sg[g4]])

            def stats_B(g4):
                c0, c1 = (NTT, NTT + 1) if g4 == 4 else (4 * g4, 4 * g4 + 4)
                np_ = 3 if g4 == 4 else 128
                S.op("dve", lambda e: e.tensor_scalar(out=rstd[:np_, c0:c1], in0=ssq[:np_, c0:c1], scalar1=1.0 / D, scalar2=EPS,
                                                      op0=ALU.mult, op1=ALU.add), reads=[b_sg[g4]], writes=[b_sg[g4]])
                S.op("act", lambda e: e.activation(out=rstd[:np_, c0:c1], in_=rstd[:np_, c0:c1], func=AF.Sqrt), reads=[b_sg[g4]], writes=[b_sg[g4]])
                S.op("dve", lambda e: e.reciprocal(out=rstd[:np_, c0:c1], in_=rstd[:np_, c0:c1]), reads=[b_sg[g4]], writes=[b_sg[g4]])

            def norm_to_hT(gvec, with_halo, from_dram):
                gb = bass.AP(gvec.tensor, gvec.offset, [list(gvec.ap[0]), list(gvec.ap[1]), [0, 128]])
                for tt in range(NTT):
                    if from_dram:
                        S.dma("sp", xq[tt], X[:, tt, :], xin_ap[tt * 128:(tt + 1) * 128, :], writes=[X_b[tt]])
                if with_halo:
                    S.dma("sp", hq, xh_t[:3, :], xh_ap, writes=[b_xh])

                def stage_C(g4):
                    tiles = [NTT] if g4 == 4 else range(4 * g4, 4 * g4 + 4)
                    for tt in tiles:
                        halo = tt == NTT
                        np_ = 3 if halo else 128
                        src, bsrc = (xh_t[:3, :], b_xh) if halo else (X[:, tt, :], X_b[tt])
                        xb = xnb[tt % 2]; bx = b_xnb[tt % 2]
                        S.op("act", lambda e: e.activation(out=xb[:np_], in_=src, func=AF.Copy, scale=rstd[:np_, tt:tt + 1]),
                             reads=[bsrc, b_sg[g4]], writes=[bx])
                        bank, bb = next_bank()
                        pb = bank[:, 0:512].bitcast(BF16).rearrange("p (k t) -> p k t", k=8)
                        for kk in range(8):
                            S.op("pe", lambda e: e.transpose(pb[:, kk, 0:np_], xb[:np_, kk * 128:(kk + 1) * 128], ident_b[:np_, :np_]),
                                 inc=(kk == 7), reads=[bx, b_cst], writes=[bb])
                        c0 = 0 if halo else 3 + tt * 128
                        S.op("dve", lambda e: e.tensor_tensor(out=HT[:, :, c0:c0 + np_], in0=pb[:, :, 0:np_], in1=gb[:, :, 0:np_], op=ALU.mult),
                             reads=[bb, b_prm], writes=[HT_b[tt]])
                ng = 5 if with_halo else 4
                stats_A(0)
                for g4 in range(ng):
                    if g4 + 1 < ng:
                        stats_A(g4 + 1)
                    stats_B(g4)
                    stage_C(g4)

            norm_to_hT(g1, True, True)

            if stop == "norm1":
                quiesce(); return
            win = dr["w_in"][l].rearrange("(k p) n -> p k n", p=128)
            chunks = [(win[:, :, c * 512:(c + 1) * 512], 8, 512) for c in range(5)] + [(win[:, :, 2560:2568], 8, 8)]
            ws = WStream(chunks)
            stage = [view(SCR, 10240 + i * 8448, 8448, F32) for i in range(2)]
            b_stage = S.bufs(2, "stage")
            acc = view(SCR, 10240 + 2 * 8448, 8192, F32)
            b_acc = S.buf("acc")
            b_us = S.bufs(4, "us")
            b_qk = S.bufs(8, "qk")
            b_va = S.bufs(NTT, "va")
            b_og = S.bufs(NTT, "og")
            b_gr = S.buf("gr")
            allHT = HT_b
            S.op("pool", lambda e: e.memset(VA[:, :, :, 128:129], 1.0), writes=b_va)
            for ci in range(3):
                wc, bw = ws.get(ci)
                for ft in range(4):
                    f = (ci - 1) * 4 + ft
                    if ci > 0:
                        st = stage[f % 2]; bs = b_stage[f % 2]
                        bank, bb = next_bank()
                        for kk in range(8):
                            S.op("pe", lambda e: e.matmul(bank[:, 0:3], lhsT=wc[:, kk, ft * 128:(ft + 1) * 128], rhs=HT[:, kk, 0:3],
                                                          start=(kk == 0), stop=(kk == 7)), inc=(kk == 7), reads=[bw, HT_b[NTT]], writes=[bb])
                        S.op("act", lambda e: e.activation(out=st[:, 0:3], in_=bank[:, 0:3], func=AF.Copy), reads=[bb], writes=[bs])
                    for nb in range(4):
                        bank, bb = next_bank()
                        for kk in range(8):
                            S.op("pe", lambda e: e.matmul(bank[:, :], lhsT=wc[:, kk, ft * 128:(ft + 1) * 128],
                                                          rhs=HT[:, kk, 3 + nb * 512:3 + (nb + 1) * 512], start=(kk == 0), stop=(kk == 7)),
                                 inc=(kk == 7), reads=[bw] + allHT[nb * 4:nb * 4 + 4], writes=[bb])
                        if ci == 0:
                            dst = US[:, ft, :, nb * 32:(nb + 1) * 32]
                            src = bank[:, :].rearrange("p (c i) -> p i c", i=16)
                            S.op("act", lambda e: e.activation(out=dst, in_=src, func=AF.Copy), reads=[bb], writes=[b_us[ft]])
                        else:
                            S.op("act", lambda e: e.activation(out=st[:, 3 + nb * 512:3 + (nb + 1) * 512], in_=bank[:, :], func=AF.Copy),
                                 reads=[bb], writes=[bs])
                    if ci > 0:
                        S.op("dve", lambda e: e.tensor_scalar(out=acc, in0=st[:, 0:2048], scalar1=cw[:, 0, f:f + 1], scalar2=None, op0=ALU.mult),
                             reads=[bs, b_prm], writes=[b_acc])
                        for j in range(1, 4):
                            S.op("dve", lambda e: e.scalar_tensor_tensor(out=acc, in0=st[:, j:j + 2048], scalar=cw[:, j, f:f + 1], in1=acc,
                                                                         op0=ALU.mult, op1=ALU.add), reads=[bs, b_prm, b_acc], writes=[b_acc])
                        S.op("act", lambda e: e.activation(out=QK[:, f, :], in_=acc, func=AF.Silu, bias=cb[:, f:f + 1]),
                             reads=[b_acc, b_prm], writes=[b_qk[f]])
            for ci in (3, 4):
                wc, bw = ws.get(ci)
                for tt in range(NTT):
                    bank, bb = next_bank()
                    for kk in range(8):
                        S.op("pe", lambda e: e.matmul(bank[:, :], lhsT=HT[:, kk, 3 + tt * 128:3 + (tt + 1) * 128], rhs=wc[:, kk, :],
                                                      start=(kk == 0), stop=(kk == 7)), inc=(kk == 7), reads=[bw, HT_b[tt]], writes=[bb])
                    if ci == 3:
                        S.op("act", lambda e: e.activation(out=VA[:, tt, :, 0:128], in_=bank[:, :].rearrange("p (h v) -> p h v", h=4), func=AF.Copy),
                             reads=[bb], writes=[b_va[tt]])
                    else:
                        S.op("act", lambda e: e.activation(out=OG[:, tt, :], in_=bank[:, :], func=AF.Sigmoid), reads=[bb], writes=[b_og[tt]])
            wc, bw = ws.get(5)
            bank, bb = next_bank()
            for tt in range(NTT):
                for kk in range(8):
                    S.op("pe", lambda e: e.matmul(bank[:, tt * 8:(tt + 1) * 8], lhsT=HT[:, kk, 3 + tt * 128:3 + (tt + 1) * 128], rhs=wc[:, kk, :],
                                                  start=(kk == 0), stop=(kk == 7)), inc=(kk == 7), reads=[bw, HT_b[tt]], writes=[bb])
            S.op("dve", lambda e: e.tensor_copy(out=GR, in_=bank[:, 0:128]), reads=[bb], writes=[b_gr])

            if cfg.get("debug"):
                dbq = S.dma_sem("dbq")
                b_dbg = S.buf("dbg")
                S.dma("sp", dbq, dr["dbg_u"], view(A0, 48 * KB, 16 * KB, BF16), reads=b_us, writes=[b_dbg])
                S.dma("sp", dbq, dr["dbg_qk"], view(A0, 0, 32 * KB, BF16), reads=b_qk, writes=[b_dbg])
                S.dma("sp", dbq, dr["dbg_v"], view(A2, 0, 16512, BF16), reads=b_va, writes=[b_dbg])
                S.dma("sp", dbq, dr["dbg_o"], view(A0, 32 * KB, 16 * KB, BF16), reads=b_og, writes=[b_dbg])
                S.dma("sp", dbq, dr["dbg_if"], GR, reads=[b_gr], writes=[b_dbg])
                pass

            if stop == "win":
                quiesce(); return
            xfer("e_a0", view(A0, 0, 64 * KB, F32), b_qk + b_og + b_us)
            xfer("e_va", view(A2, 0, 17024, F32), b_va + [b_gr])
            b_yc = S.bufs(NTT, "yc")
            S.handoff(b_yc, HT_b)

            MS = SCR
            def garr(i):
                return view(MS, i * 256, 256, F32).rearrange("p (m h) -> p m h", h=4)
            nlf, ig, nF, g, Gm, R0, Rr, w0, wv, clamp, dl, ep, incl, t1 = [garr(i) for i in range(14)]
            sm = view(MS, 14 * 256, 256, F32)
            gcol = sm[:, 0:1]; dm = sm[:, 4:8]; rec = sm[:, 8:12]; ss = sm[:, 12:16]; rs4 = sm[:, 16:20]
            Gb = view(MS, 15 * 256, 512, F32)
            b_g = S.buf("gates")
            b_sm = S.buf("sm")
            KT = view(MS, 4608, 16512, BF16)
            kTok = KT[:, 0:16 * 512].rearrange("p (m d) -> p m d", m=16)
            Cin = KT[:, 0:64 * 129].rearrange("p (c v) -> p c v", c=64)
            b_kt = S.bufs(16, "kt")
            CL = view(MS, 4608 + 16512, 64 * 129 * 4, F32).rearrange("p (c v) -> p c v", c=64)
            b_cl = S.bufs(16, "cl")
            T0 = MS + 4608 + 16512 + 64 * 129 * 4
            Ct = view(T0, 0, 2064, F32).rearrange("p (h v) -> p h v", h=4)
            tmpC = view(T0, 2064, 2064, F32).rearrange("p (h v) -> p h v", h=4)
            b_ct = S.buf("ct"); b_tc = S.buf("tmpc")
            Vw = [view(T0, 4128 + i * 1032, 1032, BF16).rearrange("p (h v) -> p h v", h=4) for i in range(2)]
            b_vw = S.bufs(2, "vw")
            PT = [view(T0, 6192 + i * 256, 256, BF16) for i in range(2)]
            b_pt = S.bufs(2, "pt")
            hbuf = [view(T0, 6704 + i * 512, 512, F32) for i in range(4)]
            b_hb = S.bufs(4, "hb")
            hn = [view(T0, 8752 + i * 256, 256, BF16) for i in range(2)]
            b_hn = S.bufs(2, "hn")
            junk2 = view(T0, 9264, 256, BF16)
            assert T0 + 9520 <= A2 + A2_B, (T0 + 9520 - A2 - A2_B)

            handoff = S.handoff
            handoff([b_g, b_sm] + b_kt + b_cl + [b_ct, b_tc] + b_vw + b_pt + b_hb + b_hn, b_stage + [b_acc, b_junk, b_xh] + b_xnb)
            GRv = GR.rearrange("p (m c) -> p m c", c=8)

            def bc_m(ap4):
                return bass.AP(ap4.tensor, ap4.offset, [list(ap4.ap[0]), [0, 16], list(ap4.ap[1])])

            def bc_last(ap, n):
                return bass.AP(ap.tensor, ap.offset, [list(x) for x in ap.ap] + [[0, n]])

            def flat(a):
                return a.rearrange("p m h -> p (m h)")
            LN_S = -0.5 * float(np.log(128.0))
            S.op("pool", lambda e: e.memset(view(MS, 0, 4608, F32), 0.0), writes=[b_g, b_sm])
            S.op("dve", lambda e: e.tensor_tensor(out=ig, in0=GRv[:, :, 0:4], in1=bc_m(bif[:, 0:4]), op=ALU.add), reads=[b_gr, b_prm], writes=[b_g])
            S.op("dve", lambda e: e.tensor_tensor(out=t1, in0=GRv[:, :, 4:8], in1=bc_m(bif[:, 4:8]), op=ALU.add), reads=[b_gr, b_prm], writes=[b_g])
            S.op("act", lambda e: e.activation(out=t1, in_=t1, func=AF.Exp, scale=-1.0), reads=[b_g], writes=[b_g])
            S.op("act", lambda e: e.activation(out=nlf, in_=t1, func=AF.Ln, bias=1.0), reads=[b_g], writes=[b_g])
            bankA, bbA = next_bank()
            S.op("pe", lambda e: e.matmul(bankA[:, 0:64], lhsT=tri_f, rhs=flat(nlf), start=True, stop=True), reads=[b_g, b_cst], writes=[bbA])
            bankB, bbB = next_bank()
            S.op("pe", lambda e: e.matmul(bankB[:, 0:64], lhsT=ones_f, rhs=flat(nlf), start=True, stop=True), reads=[b_g, b_cst], writes=[bbB])
            bA = bankA[:, 0:64].rearrange("p (m h) -> p m h", h=4)
            bB = bankB[:, 0:64].rearrange("p (m h) -> p m h", h=4)
            for h in range(4):
                S.op("dve", lambda e: e.tensor_tensor_scan(out=incl[:, :, h], data0=ones_f[:, 0:16], data1=bB[:, :, h], initial=0.0,
                                                           op0=ALU.mult, op1=ALU.add), reads=[bbB, b_cst, b_g], writes=[b_g])
            S.op("dve", lambda e: e.tensor_tensor(out=t1, in0=incl, in1=bB, op=ALU.subtract), reads=[b_g, bbB], writes=[b_g])
            S.op("dve", lambda e: e.tensor_tensor(out=nF, in0=t1, in1=bA, op=ALU.add), reads=[b_g, bbA], writes=[b_g])
            S.op("dve", lambda e: e.tensor_tensor(out=g, in0=ig, in1=nF, op=ALU.add), reads=[b_g], writes=[b_g])
            bankT, bbT = next_bank()
            S.op("pe", lambda e: e.transpose(bankT[0:64, 0:128], flat(g), ident_f), reads=[b_g, b_cst], writes=[bbT])
            S.op("dve", lambda e: e.tensor_reduce(out=gcol[0:64, :], in_=bankT[0:64, 0:128], axis=AX.X, op=ALU.max), reads=[bbT], writes=[b_sm])
            S.op("dve", lambda e: e.tensor_copy(out=Gb[0:64, :], in_=bc_last(gcol[0:64, 0:1], 128)[:, 0, :]), reads=[b_sm], writes=[b_sm])
            bankG, bbG = next_bank()
            S.op("pe", lambda e: e.matmul(bankG[:, 0:64], lhsT=Gb[0:64, :], rhs=ident_f[0:64, 0:64], start=True, stop=True), reads=[b_sm, b_cst], writes=[bbG])
            S.op("dve", lambda e: e.tensor_copy(out=flat(Gm), in_=bankG[:, 0:64]), reads=[bbG], writes=[b_g])
            for h in range(4):
                S.op("dve", lambda e: e.tensor_tensor_scan(out=R0[:, :, h], data0=Gm[:, :, h], data1=Gm[:, :, h], initial=-1e30,
                                                           op0=ALU.max, op1=ALU.max), reads=[b_g], writes=[b_g])
            S.op("dve", lambda e: e.tensor_tensor(out=t1, in0=g, in1=R0, op=ALU.subtract), reads=[b_g], writes=[b_g])
            S.op("act", lambda e: e.activation(out=w0, in_=t1, func=AF.Exp, bias=LN_S), reads=[b_g], writes=[b_g])
            S.op("dve", lambda e: e.tensor_tensor(out=t1[:, 1:16, :], in0=R0[:, 0:15, :], in1=R0[:, 1:16, :], op=ALU.subtract), reads=[b_g], writes=[b_g])
            S.op("act", lambda e: e.activation(out=dl[:, 1:16, :], in_=t1[:, 1:16, :], func=AF.Exp), reads=[b_g], writes=[b_g])
            for m in range(16):
                cs = slice(m * 128, (m + 1) * 128)
                bank, bb = next_bank()
                pb = bank[:, 0:256].bitcast(BF16)
                for h in range(4):
                    S.op("pe", lambda e: e.transpose(pb[:, h * 128:(h + 1) * 128], QK[:, 4 + h, cs], ident_b), inc=(h == 3),
                         reads=[b_qk[4 + h], b_cst], writes=[bb])
                S.op("act", lambda e: e.activation(out=kTok[:, m, :], in_=pb, func=AF.Copy), reads=[bb], writes=[b_kt[m]])
                vw = Vw[m % 2]; bv = b_vw[m % 2]
                S.op("dve", lambda e: e.tensor_tensor(out=vw, in0=VA[:, m, :, :], in1=bc_last(w0[:, m, :], 129), op=ALU.mult),
                     reads=[b_va[m], b_g], writes=[bv])
                for h in range(4):
                    bank, bb = next_bank()
                    S.op("pe", lambda e: e.matmul(bank[:, 0:129], lhsT=kTok[:, m, h * 128:(h + 1) * 128], rhs=vw[:, h, :], start=True, stop=True),
                         reads=[b_kt[m], bv], writes=[bb])
                    if h % 2 == 0:
                        S.op("act", lambda e: e.activation(out=CL[:, m * 4 + h, :], in_=bank[:, 0:129], func=AF.Copy), reads=[bb], writes=[b_cl[m]])
                    else:
                        S.op("dve", lambda e: e.tensor_copy(out=CL[:, m * 4 + h, :], in_=bank[:, 0:129]), reads=[bb], writes=[b_cl[m]])
                if m == 0:
                    S.op("dve", lambda e: e.tensor_copy(out=Ct, in_=CL[:, 0:4, :]), reads=[b_cl[0]], writes=[b_ct])
                else:
                    S.op("dve", lambda e: e.tensor_tensor(out=Ct, in0=Ct, in1=bc_last(dl[:, m, :], 129), op=ALU.mult), reads=[b_ct, b_g], writes=[b_ct])
                    S.op("dve", lambda e: e.tensor_tensor(out=Ct, in0=Ct, in1=CL[:, m * 4:m * 4 + 4, :], op=ALU.add), reads=[b_ct, b_cl[m]], writes=[b_ct])
            ml_extra = []
            xfer("e_gates", view(MS, 0, 4608, F32), [b_g, b_sm])
            xfer("e_cl", view(MS, 4608 + 16512, 64 * 129 * 4, F32), b_cl)
            if IMP:
                S.mute = False
            mst = sm[:, 20:24]
            exq = S.dma_sem(f"exq{l}")
            if mode == "A":
                S.dma("sp", oq, dr["summ"][:, 32:548], Ct.rearrange("p h v -> p (h v)"), reads=[b_ct], writes=[b_out])
                S.dma("sp", oq, dr["summ"][:, 548:552], R0[:, 15, :], reads=[b_g], writes=[b_out])
                S.dma("sp", oq, dr["summ"][:, 552:556], incl[:, 15, :], reads=[b_g], writes=[b_out])
            else:
                sa = dr["summ_all"]
                small = Gb.rearrange("p (j c) -> p j c", j=8)[:, :, 0:8]
                S.dma("sp", exq, small, sa[:, :, 548:556].rearrange("j p c -> p j c"), writes=[b_sm])
                S.seal(exq, [b_sm])
                cm = sm[:, 20:24]; mx = sm[:, 24:28]; ta = sm[:, 28:32]; tb = sm[:, 32:36]; r0j = sm[:, 36:40]; nfj = sm[:, 40:44]; tq = sm[:, 44:48]
                S.op("dve", lambda e: e.memset(tmpC, 0.0), reads=[b_tc], writes=[b_tc])
                S.op("dve", lambda e: e.memset(cm, 0.0), reads=[b_sm], writes=[b_sm])
                cq = [S.dma_sem(f"cq{l}_{i}") for i in range(2)]
                Cj = [Ct, view(T0, 4128, 2064, F32).rearrange("p (h v) -> p h v", h=4)]
                b_cj = [b_ct, S.buf("cj1")]
                S.handoff([b_cj[1]], b_vw)
                for j in range(8):
                    cj = Cj[j % 2]; bcj = b_cj[j % 2]
                    S.dma("sp", cq[j % 2], cj.rearrange("p h v -> p (h v)"), sa[j, :, 32:548], writes=[bcj])
                    S.op("dve", lambda e: e.tensor_scalar(out=r0j, in0=small[:, j, 0:4], scalar1=pred[:, j:j + 1], scalar2=pmask[:, j:j + 1],
                                                          op0=ALU.mult, op1=ALU.add), reads=[b_sm, b_cst], writes=[b_sm])
                    S.op("dve", lambda e: e.tensor_scalar(out=nfj, in0=small[:, j, 4:8], scalar1=pred[:, j:j + 1], scalar2=None, op0=ALU.mult),
                         reads=[b_sm, b_cst], writes=[b_sm])
                    S.op("dve", lambda e: e.tensor_tensor(out=mx, in0=cm, in1=r0j, op=ALU.max), reads=[b_sm], writes=[b_sm])
                    S.op("dve", lambda e: e.tensor_tensor(out=tq, in0=cm, in1=mx, op=ALU.subtract), reads=[b_sm], writes=[b_sm])
                    S.op("act", lambda e: e.activation(out=ta, in_=tq, func=AF.Exp), reads=[b_sm], writes=[b_sm])
                    S.op("dve", lambda e: e.tensor_tensor(out=tq, in0=r0j, in1=mx, op=ALU.subtract), reads=[b_sm], writes=[b_sm])
                    S.op("act", lambda e: e.activation(out=tb, in_=tq, func=AF.Exp), reads=[b_sm], writes=[b_sm])
                    S.op("dve", lambda e: e.tensor_tensor(out=tmpC, in0=tmpC, in1=bc_last(ta, 129), op=ALU.mult), reads=[b_tc, b_sm], writes=[b_tc])
                    S.op("dve", lambda e: e.tensor_tensor(out=cj, in0=cj, in1=bc_last(tb, 129), op=ALU.mult), reads=[bcj, b_sm], writes=[bcj])
                    S.op("dve", lambda e: e.tensor_tensor(out=tmpC, in0=tmpC, in1=cj, op=ALU.add), reads=[b_tc, bcj], writes=[b_tc])
                    S.op("dve", lambda e: e.tensor_tensor(out=cm, in0=mx, in1=nfj, op=ALU.subtract), reads=[b_sm], writes=[b_sm])
            if mode != "A":
                S.op("dve", lambda e: e.tensor_tensor(out=Rr, in0=R0, in1=bc_m(mst), op=ALU.max), reads=[b_g, b_sm], writes=[b_g])
                S.op("dve", lambda e: e.tensor_tensor(out=t1, in0=g, in1=Rr, op=ALU.subtract), reads=[b_g], writes=[b_g])
                S.op("act", lambda e: e.activation(out=wv, in_=t1, func=AF.Exp, bias=LN_S), reads=[b_g], writes=[b_g])
                S.op("dve", lambda e: e.tensor_tensor(out=t1, in0=nF, in1=Rr, op=ALU.subtract), reads=[b_g], writes=[b_g])
                S.op("act", lambda e: e.activation(out=clamp, in_=t1, func=AF.Exp), reads=[b_g], writes=[b_g])
                S.op("dve", lambda e: e.tensor_tensor(out=t1, in0=R0, in1=Rr, op=ALU.subtract), reads=[b_g], writes=[b_g])
                S.op("act", lambda e: e.activation(out=ep, in_=t1, func=AF.Exp), reads=[b_g], writes=[b_g])
                S.op("dve", lambda e: e.tensor_tensor(out=t1[:, 1:16, :], in0=Rr[:, 0:15, :], in1=Rr[:, 1:16, :], op=ALU.subtract), reads=[b_g], writes=[b_g])
                S.op("dve", lambda e: e.tensor_tensor(out=t1[:, 0, :], in0=mst, in1=Rr[:, 0, :], op=ALU.subtract), reads=[b_g, b_sm], writes=[b_g])
                S.op("act", lambda e: e.activation(out=dl, in_=t1, func=AF.Exp), reads=[b_g], writes=[b_g])
                S.op("dve", lambda e: e.tensor_copy(out=Ct, in_=tmpC), reads=[b_tc], writes=[b_ct])
                b_cin = b_kt
                for m in range(16):
                    S.op("dve", lambda e: e.tensor_tensor(out=tmpC, in0=Ct, in1=bc_last(dl[:, m, :], 129), op=ALU.mult), reads=[b_ct, b_g], writes=[b_tc])
                    S.op("act", lambda e: e.activation(out=Cin[:, m * 4:m * 4 + 4, :], in_=tmpC, func=AF.Copy), reads=[b_tc], writes=b_cin)
                    S.op("dve", lambda e: e.tensor_tensor(out=Ct, in0=CL[:, m * 4:m * 4 + 4, :], in1=bc_last(ep[:, m, :], 129), op=ALU.mult),
                         reads=[b_cl[m], b_g], writes=[b_ct])
                    S.op("dve", lambda e: e.tensor_tensor(out=Ct, in0=Ct, in1=tmpC, op=ALU.add), reads=[b_ct, b_tc], writes=[b_ct])
                PT4 = [view(T0, i * 256, 256, BF16) for i in range(4)]
                hn4 = [view(T0, 1024 + i * 1024, 1024, BF16).rearrange("p (h t) -> p h t", h=4) for i in range(2)]
                hbA = view(T0, 6704, 2048, F32).rearrange("p (h t) -> p h t", h=4)
                hbB = view(T0, 3072, 2048, F32).rearrange("p (h t) -> p h t", h=4)
                hb4 = [hbA, hbB]
                b_pt4 = S.bufs(4, "pt4"); b_hn4 = S.bufs(2, "hn4"); b_hb4 = [S.bufs(4, "hbA"), S.bufs(4, "hbB")]
                b_j2 = S.buf("j2")
                b_ep = [S.buf("ep0"), S.buf("ep1")]
                S.handoff(b_pt4 + b_hn4 + b_hb4[0] + b_hb4[1] + [b_j2] + b_ep, [b_ct, b_tc, b_sm] + b_vw + b_pt + b_hb + b_hn + ([b_cj[1]] if mode != "A" else []))
                ml_extra += b_pt4 + b_hn4 + b_hb4[0] + b_hb4[1] + [b_j2] + b_ep
                smx = [sm[:, 4:20], sm[:, 48:64]]
                for m in range(16):
                    cs = slice(m * 128, (m + 1) * 128)
                    par = m % 2
                    dm_, rec_, ss_, rs_ = smx[par][:, 0:4], smx[par][:, 4:8], smx[par][:, 8:12], smx[par][:, 12:16]
                    be = b_ep[par]
                    bankS, bbS = next_bank()
                    for h in range(4):
                        S.op("pe", lambda e: e.matmul(bankS[:, h * 128:(h + 1) * 128], lhsT=QK[:, 4 + h, cs], rhs=QK[:, h, cs], start=True, stop=True),
                             inc=(h == 3), reads=[b_qk[4 + h], b_qk[h]], writes=[bbS])
                    for h in range(4):
                        S.op("dve", lambda e: e.scalar_tensor_tensor(out=PT4[h], in0=bankS[:, h * 128:(h + 1) * 128], scalar=wv[:, m, h:h + 1], in1=tri_f,
                                                                     op0=ALU.mult, op1=ALU.mult), reads=[bbS, b_g, b_cst], writes=[b_pt4[h]])
                    bN = []
                    for j in range(2):
                        bankN, bbN = next_bank()
                        bN.append((bankN, bbN))
                        for hh_ in range(2):
                            h = 2 * j + hh_
                            co = hh_ * 129
                            S.op("pe", lambda e: e.matmul(bankN[:, co:co + 129], lhsT=PT4[h], rhs=VA[:, m, h, :], start=True, stop=False), inc=False,
                                 reads=[b_pt4[h], b_va[m]], writes=[bbN])
                            S.op("pe", lambda e: e.matmul(bankN[:, co:co + 129], lhsT=QK[:, h, cs], rhs=Cin[:, m * 4 + h, :], start=False, stop=True),
                                 inc=(hh_ == 1), reads=[b_qk[h]] + b_cin, writes=[bbN])
                        den = bankN[:, 0:258].rearrange("p (h v) -> p h v", v=129)[:, :, 128]
                        S.op("act", lambda e: e.activation(out=dm_[:, 2 * j:2 * j + 2], in_=den, func=AF.Abs), reads=[bbN], writes=[be])
                        S.op("dve", lambda e: e.tensor_tensor(out=dm_[:, 2 * j:2 * j + 2], in0=dm_[:, 2 * j:2 * j + 2], in1=clamp[:, m, 2 * j:2 * j + 2], op=ALU.max),
                             reads=[be, b_g], writes=[be])
                    S.op("dve", lambda e: e.reciprocal(out=rec_, in_=dm_), reads=[be], writes=[be])
                    for h in range(4):
                        bankN, bbN = bN[h // 2]
                        co = (h % 2) * 129
                        S.op("dve", lambda e: e.scalar_tensor_tensor(out=hb4[par][:, h, :], in0=bankN[:, co:co + 128], scalar=rec_[:, h:h + 1],
                                                                     in1=OG[:, m, h * 128:(h + 1) * 128], op0=ALU.mult, op1=ALU.mult),
                             reads=[bbN, be, b_og[m]], writes=[b_hb4[par][h]])
                        S.op("act", lambda e: e.activation(out=junk2, in_=hb4[par][:, h, :], func=AF.Square, accum_out=ss_[:, h:h + 1]),
                             reads=[b_hb4[par][h]], writes=[b_j2, be])
                    S.op("dve", lambda e: e.tensor_scalar(out=rs_, in0=ss_, scalar1=1.0 / 128, scalar2=EPS, op0=ALU.mult, op1=ALU.add), reads=[be], writes=[be])
                    S.op("act", lambda e: e.activation(out=rs_, in_=rs_, func=AF.Sqrt), reads=[be], writes=[be])
                    S.op("dve", lambda e: e.reciprocal(out=rs_, in_=rs_), reads=[be], writes=[be])
                    S.op("dve", lambda e: e.tensor_tensor(out=hn4[par], in0=hb4[par], in1=bc_last(rs_, 128), op=ALU.mult),
                         reads=b_hb4[par] + [be], writes=[b_hn4[par]])
                    bankO, bbO = next_bank()
                    po = bankO[:, 0:256].bitcast(BF16).rearrange("p (h t) -> p h t", h=4)
                    for h in range(4):
                        S.op("pe", lambda e: e.transpose(po[:, h, :], hn4[par][:, h, :], ident_b), inc=(h == 3), reads=[b_hn4[par], b_cst], writes=[bbO])
                    S.op("dve", lambda e: e.tensor_tensor(out=YC[:, 4:8, cs], in0=po, in1=bc_last(mlg, 128), op=ALU.mult), reads=[bbO, b_prm], writes=[b_yc[m]])

            if cfg.get("debug"):
                S.dma("sp", dbq, dr["dbg_g"].rearrange("p (a c) -> p a c", a=16), view(MS, 0, 4096, F32).rearrange("p (a c) -> p a c", a=16), reads=[b_g, b_sm], writes=[b_dbg])

            if stop == "ml":
                quiesce(); return
            TCH = 16; NC_ = 128
            WZ = view(A0, 0, 32 * KB, BF16).rearrange("p (q i x n) -> p q i x n", q=4, i=16, x=2)
            BD = view(A0, 32 * KB, 16 * KB, BF16).rearrange("p (q j n) -> p q j n", q=4, j=16)
            CP = view(A2, 0, 34816, BF16).rearrange("p (j r x n) -> p j r x n", j=17, r=16, x=2)
            EC = view(A2, 34816, 8192, F32).rearrange("p (r c) -> p r c", r=16)
            ES = view(A2, 34816 + 8192, 8192, F32).rearrange("p (r c) -> p r c", r=16)
            ZL = view(A2, 51200, 16384, F32).rearrange("p (r x c) -> p r x c", r=16, x=2)
            ZS = view(A2, 67584, 8256, BF16).rearrange("p (r x c) -> p r x c", r=16, x=2)
            SP_ = A2 + 76032
            PW = view(SP_, 0, 2176, F32).rearrange("p (j x r) -> p j x r", j=17, x=2)
            def sc(i):
                return view(SP_, 2176 + i * 64, 64, F32)
            assert SP_ + 2176 + 30 * 64 <= A2 + A2_B
            WW = view(A0, 0, 16384, F32).rearrange("p (r x c) -> p r x c", r=16, x=2)
            ZG = view(A2, 51200, 16384, BF16).rearrange("p (q i c) -> p q i c", q=4, i=16)
            GT = A0 + 16 * KB
            GEN = A2 + 51200
            b_s5 = S.buf("s5gen")
            b_wz = S.bufs(4, "wz"); b_bd = S.bufs(4, "bd"); b_cp = S.buf("cp"); b_tab = S.buf("tab")
            b_zl = S.bufs(16, "zl"); b_ww = S.bufs(16, "ww"); b_zs = S.bufs(16, "zs"); b_zg = S.bufs(4, "zg")
            olds = [b_g, b_sm] + b_kt + b_cl + [b_ct, b_tc] + b_vw + b_pt + b_hb + b_hn + b_va + [b_gr] + b_qk + b_og + ml_extra
            handoff([b_s5, b_cp, b_tab] + b_wz + b_bd + b_zl + b_ww + b_zs + b_zg, olds)
            s5q = S.dma_sem(f"s5q{l}")
            (s_are, s_aim, s_dt, s_mag, s_th, s_t, s_sin, s_cos, s_abr, s_abi, s_den, s_zr, s_sre, s_sim, s_t2, s_magL,
             s_l128r, s_l128i, s_t3, s_m128) = [sc(i) for i in range(20)]
            zend = sc(20)[:, 0:16]
            zend = view(SP_, 2176 + 20 * 64, 128, F32).rearrange("p (r x) -> p r x", x=2)
            sst = view(SP_, 2176 + 22 * 64, 128, F32).rearrange("p (r x) -> p r x", x=2)

            def TT(out, in0, in1, op, rd=(), wr=None, eng="dve"):
                S.op(eng, lambda e: e.tensor_tensor(out=out, in0=in0, in1=in1, op=op), reads=[b_s5] + list(rd), writes=[b_s5] if wr is None else wr)

            def TS(out, in0, s1, s2, op0, op1=None, rd=(), wr=None):
                if op1 is None:
                    S.op("dve", lambda e: e.tensor_scalar(out=out, in0=in0, scalar1=s1, scalar2=None, op0=op0), reads=[b_s5] + list(rd), writes=[b_s5] if wr is None else wr)
                else:
                    S.op("dve", lambda e: e.tensor_scalar(out=out, in0=in0, scalar1=s1, scalar2=s2, op0=op0, op1=op1), reads=[b_s5] + list(rd), writes=[b_s5] if wr is None else wr)

            def AC(out, in_, func, rd=(), wr=None, **kw):
                S.op("act", lambda e: e.activation(out=out, in_=in_, func=func, **kw), reads=[b_s5] + list(rd), writes=[b_s5] if wr is None else wr)

            def cmul(o_r, o_i, a_r, a_i, b_r, b_i, t1_, t2_, rd=(), wr=None, neg_im=False):
                TT(t1_, a_r, b_r, ALU.mult, rd); TT(t2_, a_i, b_i, ALU.mult, rd)
                TT(o_r, t1_, t2_, ALU.subtract, rd, wr)
                TT(t1_, a_r, b_i, ALU.mult, rd); TT(t2_, a_i, b_r, ALU.mult, rd)
                if neg_im:
                    TT(t1_, t1_, t2_, ALU.add, rd)
                    TS(o_i, t1_, -1.0, None, ALU.mult, rd=rd, wr=wr)
                else:
                    TT(o_i, t1_, t2_, ALU.add, rd, wr)

            if IMP:
                S.mute = True
            S.op("pool", lambda e: e.memset(view(SP_, 0, 4864, F32), 0.0), writes=[b_s5])
            araw = view(GEN, 0, 1024, F32)
            S.dma("sp", s5q, araw[0:16, 0:128], dr["s5_a_re"][l].rearrange("(r gl) n -> r (gl n)", gl=2), writes=[b_s5])
            S.dma("sp", s5q, araw[0:16, 128:256], dr["s5_a_im"][l].rearrange("(r gl) n -> r (gl n)", gl=2), writes=[b_s5])
            ldt = dr["s5_log_dt"][l]
            for gl in range(2):
                S.dma("sp", s5q, s_dt[gl * 64:(gl + 1) * 64, :], bass.AP(ldt.tensor, ldt.offset + gl, [[0, 64], [2, 16]]), writes=[b_s5])
            Bsm = [view(GEN, 1024 + x * 1024, 1024, F32).rearrange("p (r c) -> p r c", r=16) for x in range(2)]
            for x, nm in enumerate(["s5_b_re", "s5_b_im"]):
                bsrc = dr[nm][l]
                for gl in range(2):
                    S.dma("sp", s5q, Bsm[x][gl * 64:(gl + 1) * 64, :, :],
                          bass.AP(bsrc.tensor, bsrc.offset + gl * 1024, [[16, 64], [2048, 16], [1, 16]]), writes=[b_s5])
            Craw = [view(GEN, 3072 + x * 1024, 1024, F32).rearrange("p (q n) -> p q n", q=4) for x in range(2)]
            for x, nm in enumerate(["s5_c_re", "s5_c_im"]):
                csrc = dr[nm][l]
                S.dma("sp", s5q, Craw[x], bass.AP(csrc.tensor, csrc.offset, [[64, 128], [8192, 4], [1, 64]]), writes=[b_s5])
            S.seal(s5q, [b_s5])
            bank, bb = next_bank()
            S.op("pe", lambda e: e.transpose(bank[:, 0:16], araw[0:16, 0:128], ident_f[0:16, 0:16]), reads=[b_s5, b_cst], writes=[bb])
            S.op("pe", lambda e: e.transpose(bank[:, 16:32], araw[0:16, 128:256], ident_f[0:16, 0:16]), reads=[b_s5, b_cst], writes=[bb])
            S.op("dve", lambda e: e.tensor_copy(out=s_are, in_=bank[:, 0:16]), reads=[bb], writes=[b_s5])
            S.op("dve", lambda e: e.tensor_copy(out=s_aim, in_=bank[:, 16:32]), reads=[bb], writes=[b_s5])
            PI = float(np.pi)
            AC(s_dt, s_dt, AF.Exp)
            TT(s_t, s_are, s_dt, ALU.mult)
            AC(s_mag, s_t, AF.Exp)
            AC(s_magL, s_t, AF.Exp, scale=float(TCH))
            AC(s_m128, s_t, AF.Exp, scale=float(TCH * NC_))
            TT(s_th, s_aim, s_dt, ALU.mult)
            for thr in (1.0, 3.0, 5.0, 7.0):
                TS(s_t, s_th, thr * PI, -2.0 * PI, ALU.is_gt, ALU.mult)
                if thr == 1.0:
                    TT(s_t2, s_th, s_t, ALU.add)
                else:
                    TT(s_t2, s_t2, s_t, ALU.add)
            AC(s_sin, s_t2, AF.Sin)
            TS(s_t3, s_t2, 0.5 * PI, None, ALU.add)
            TS(s_t, s_t3, PI, -2.0 * PI, ALU.is_gt, ALU.mult)
            TT(s_t3, s_t3, s_t, ALU.add)
            AC(s_cos, s_t3, AF.Sin)
            TT(s_abr, s_mag, s_cos, ALU.mult); TT(s_abi, s_mag, s_sin, ALU.mult)
            TT(s_t, s_are, s_are, ALU.mult); TT(s_t2, s_aim, s_aim, ALU.mult); TT(s_den, s_t, s_t2, ALU.add)
            S.op("dve", lambda e: e.reciprocal(out=s_den, in_=s_den), reads=[b_s5], writes=[b_s5])
            TS(s_zr, s_abr, -1.0, None, ALU.add)
            TT(s_t, s_zr, s_are, ALU.mult); TT(s_t2, s_abi, s_aim, ALU.mult); TT(s_t, s_t, s_t2, ALU.add); TT(s_sre, s_t, s_den, ALU.mult)
            TT(s_t, s_abi, s_are, ALU.mult); TT(s_t2, s_zr, s_aim, ALU.mult); TT(s_t, s_t, s_t2, ALU.subtract); TT(s_sim, s_t, s_den, ALU.mult)
            S.op("dve", lambda e: e.memset(PW[:, 0, 0, :], 1.0), reads=[b_s5], writes=[b_s5])
            S.op("dve", lambda e: e.memset(PW[:, 0, 1, :], 0.0), reads=[b_s5], writes=[b_s5])
            S.op("dve", lambda e: e.tensor_copy(out=PW[:, 1, 0, :], in_=s_abr), reads=[b_s5], writes=[b_s5])
            S.op("dve", lambda e: e.tensor_copy(out=PW[:, 1, 1, :], in_=s_abi), reads=[b_s5], writes=[b_s5])
            pt1 = view(GEN, 5120, 1024, F32).rearrange("p (j r) -> p j r", r=16)
            pt2 = view(GEN, 6144, 1024, F32).rearrange("p (j r) -> p j r", r=16)
            kk_ = 1
            while kk_ < 16:
                def bj(a):
                    return bass.AP(a.tensor, a.offset, [list(a.ap[0]), [0, kk_], list(a.ap[1])])
                cmul(PW[:, kk_ + 1:2 * kk_ + 1, 0, :], PW[:, kk_ + 1:2 * kk_ + 1, 1, :], PW[:, 1:kk_ + 1, 0, :], PW[:, 1:kk_ + 1, 1, :],
                     bj(PW[:, kk_, 0, :]), bj(PW[:, kk_, 1, :]), pt1[:, 0:kk_, :], pt2[:, 0:kk_, :])
                kk_ *= 2
            S.op("dve", lambda e: e.reciprocal(out=s_t, in_=s_magL), reads=[b_s5], writes=[b_s5])
            TT(EC[:, :, 0], PW[:, 16, 0, :], s_t, ALU.mult, wr=[b_s5, b_tab]); TT(ES[:, :, 0], PW[:, 16, 1, :], s_t, ALU.mult, wr=[b_s5, b_tab])
            et1 = view(GEN, 7168, 4096, F32).rearrange("p (r c) -> p r c", r=16)
            et2 = view(GEN, 11264, 4096, F32).rearrange("p (r c) -> p r c", r=16)
            kk_ = 1
            while kk_ < NC_:
                cmul(EC[:, :, kk_:2 * kk_], ES[:, :, kk_:2 * kk_], EC[:, :, 0:kk_], ES[:, :, 0:kk_],
                     bc_last(EC[:, :, kk_ - 1], kk_), bc_last(ES[:, :, kk_ - 1], kk_), et1[:, :, 0:kk_], et2[:, :, 0:kk_], rd=[b_tab], wr=[b_s5, b_tab])
                kk_ *= 2
            TT(s_l128r, EC[:, :, NC_ - 1], s_m128, ALU.mult, rd=[b_tab]); TT(s_l128i, ES[:, :, NC_ - 1], s_m128, ALU.mult, rd=[b_tab])
            Cin_ = [view(GEN, 5120 + x * 2048, 2048, F32).rearrange("p (q n) -> p q n", q=4) for x in range(2)]
            Cp = [view(GEN, 9216 + x * 2048, 2048, F32).rearrange("p (r n) -> p r n", r=16) for x in range(2)]
            ct1 = view(GEN, 13312, 2048, F32).rearrange("p (r n) -> p r n", r=16)
            ct2 = view(A0, 0, 2048, F32).rearrange("p (r n) -> p r n", r=16)
            for x in range(2):
                TS(Cin_[x][:, :, 0:64], Craw[x], par01[:, 0:1], None, ALU.mult, rd=[b_cst])
                TS(Cin_[x][:, :, 64:128], Craw[x], par01[:, 1:2], None, ALU.mult, rd=[b_cst])
                bank, bb = next_bank()
                for q in range(4):
                    S.op("pe", lambda e: e.transpose(bank[:, q * 128:(q + 1) * 128], Cin_[x][:, q, :], ident_f), inc=(q == 3), reads=[b_s5, b_cst], writes=[bb])
                S.op("dve", lambda e: e.tensor_copy(out=Cp[x].rearrange("p r n -> p (r n)"), in_=bank[:, :]), reads=[bb], writes=[b_s5])
            for j in range(17):
                pr = bc_last(PW[:, j, 0, :], 32); pi_ = bc_last(PW[:, j, 1, :], 32)
                TT(ct1, Cp[0], pr, ALU.mult); TT(ct2, Cp[1], pi_, ALU.mult, rd=b_wz, wr=[b_s5] + b_wz)
                TT(CP[:, j, :, 0, :], ct1, ct2, ALU.subtract, wr=[b_s5, b_cp])
                TT(ct1, Cp[0], pi_, ALU.mult); TT(ct2, Cp[1], pr, ALU.mult, rd=b_wz, wr=[b_s5] + b_wz)
                S.op("dve", lambda e: e.scalar_tensor_tensor(out=CP[:, j, :, 1, :], in0=ct1, scalar=-1.0, in1=ct2, op0=ALU.mult, op1=ALU.subtract),
                     reads=[b_s5], writes=[b_s5, b_cp])

            BB = [view(GEN, 5120 + x * 2048, 2048, F32).rearrange("p (r n) -> p r n", r=16) for x in range(2)]
            BBb = [view(GEN, 9216 + x * 1024, 1024, BF16).rearrange("p (r n) -> p r n", r=16) for x in range(2)]
            bt1 = view(GEN, 11264, 1024, F32).rearrange("p (r c) -> p r c", r=16)
            bt2 = view(GEN, 12288, 1024, F32).rearrange("p (r c) -> p r c", r=16)
            for x in range(2):
                S.op("dve", lambda e: e.memset(BB[x], 0.0), reads=[b_s5], writes=[b_s5])
            sre_b = bc_last(s_sre, 16); sim_b = bc_last(s_sim, 16)
            TT(bt1, Bsm[0], sre_b, ALU.mult); TT(bt2, Bsm[1], sim_b, ALU.mult)
            for gl in range(2):
                ps_ = slice(gl * 64, (gl + 1) * 64)
                TT(BB[0][ps_, :, gl * 16:(gl + 1) * 16], bt1[ps_], bt2[ps_], ALU.subtract)
            TT(bt1, Bsm[1], sre_b, ALU.mult); TT(bt2, Bsm[0], sim_b, ALU.mult)
            for gl in range(2):
                ps_ = slice(gl * 64, (gl + 1) * 64)
                TT(BB[1][ps_, :, gl * 16:(gl + 1) * 16], bt1[ps_], bt2[ps_], ALU.add)
            for x in range(2):
                S.op("dve", lambda e: e.tensor_copy(out=BBb[x], in_=BB[x]), reads=[b_s5], writes=[b_s5])
            bdt = view(GEN, 13312, 512, F32)
            for j in range(16):
                bank, bb = next_bank()
                for q in range(4):
                    for x in range(2):
                        S.op("pe", lambda e: e.matmul(bank[:, q * 128:(q + 1) * 128], lhsT=BBb[x][:, 4 * q:4 * q + 4, :].rearrange("p r n -> p (r n)"),
                                                      rhs=CP[:, j, 4 * q:4 * q + 4, x, :], start=(x == 0), stop=(x == 1)), inc=(q == 3 and x == 1),
                             reads=[b_s5, b_cp], writes=[bb])
                if j == 0:
                    for q in range(4):
                        S.op("dve", lambda e: e.tensor_tensor(out=bdt, in0=bank[:, q * 128:(q + 1) * 128], in1=bdm, op=ALU.mult), reads=[bb, b_cst, b_s5], writes=[b_s5])
                        S.op("dve", lambda e: e.scalar_tensor_tensor(out=BD[:, q, 0, :], in0=ident_f, scalar=dcol[:, q:q + 1], in1=bdt, op0=ALU.mult, op1=ALU.add),
                             reads=[b_s5, b_cst, b_prm], writes=[b_bd[q]])
                else:
                    bdm_b = bass.AP(bdm.tensor, bdm.offset, [list(bdm.ap[0]), [0, 4], list(bdm.ap[1])])
                    S.op("dve", lambda e: e.tensor_tensor(out=BD[:, :, j, :], in0=bank[:, :].rearrange("p (q n) -> p q n", q=4), in1=bdm_b, op=ALU.mult),
                         reads=[bb, b_cst], writes=b_bd)
            mt1 = view(GEN, 13824, 2048, F32).rearrange("p (r n) -> p r n", r=16)
            mt2 = view(GEN, 1024, 2048, F32).rearrange("p (r n) -> p r n", r=16)
            MB = [view(GEN, 3072 + x * 1024, 1024, BF16).rearrange("p (r n) -> p r n", r=16) for x in range(2)]
            for i in range(16):
                j = 15 - i
                pr = bc_last(PW[:, j, 0, :], 32); pi_ = bc_last(PW[:, j, 1, :], 32)
                TT(mt1, BB[0], pr, ALU.mult); TT(mt2, BB[1], pi_, ALU.mult); TT(MB[0], mt1, mt2, ALU.subtract)
                TT(mt1, BB[0], pi_, ALU.mult); TT(mt2, BB[1], pr, ALU.mult); TT(MB[1], mt1, mt2, ALU.add)
                bank, bb = next_bank()
                pb = bank[:, :].bitcast(BF16).rearrange("p (q x n) -> p q x n", q=4, x=2)
                for q in range(4):
                    for x in range(2):
                        S.op("pe", lambda e: e.transpose(pb[:, q, x, :], MB[x][:, 4 * q:4 * q + 4, :].rearrange("p r n -> p (r n)"), ident_b),
                             inc=(q == 3 and x == 1), reads=[b_s5, b_cst], writes=[bb])
                S.op("act", lambda e: e.activation(out=WZ[:, :, i, :, :], in_=pb, func=AF.Copy), reads=[bb], writes=b_wz)
            if stop == "s5gen":
                quiesce(); return
            handoff(b_zl, b_zl + [b_s5])
            for q in range(4):
                for rr in range(4):
                    r = 4 * q + rr
                    bank, bb = next_bank()
                    for x in range(2):
                        col = x * 128
                        for i in range(16):
                            S.op("pe", lambda e: e.matmul(bank[:, col:col + 128], lhsT=WZ[32 * rr:32 * rr + 32, q, i, x, :], rhs=US[32 * rr:32 * rr + 32, q, i, :],
                                                          start=(i == 0), stop=(i == 15), tile_position=(32 * rr, 0)), inc=(i == 15 and x == 1),
                                 reads=[b_wz[q], b_us[q]], writes=[bb])
                    S.op("act", lambda e: e.activation(out=ZL[:, r, :, :].rearrange("p x c -> p (x c)"), in_=bank[:, 0:256], func=AF.Copy),
                         reads=[bb], writes=[b_zl[r]])
            if cfg.get("debug"):
                S.dma("sp", dbq, dr["dbg_zl"], view(A2, 51200, 16384, F32), reads=b_zl, writes=[b_dbg])
                S.dma("sp", dbq, dr["dbg_sc"], view(SP_, 0, 4864, F32), reads=[b_s5], writes=[b_dbg])
                S.dma("sp", dbq, dr["dbg_cp"], view(A2, 0, 34816, BF16), reads=[b_cp], writes=[b_dbg])
                S.dma("sp", dbq, dr["dbg_bd"], view(A0, 32 * KB, 16 * KB, BF16), reads=b_bd, writes=[b_dbg])
                S.dma("sp", dbq, dr["dbg_wz"], view(A0, 0, 32 * KB, BF16), reads=b_wz, writes=[b_dbg])
            handoff(b_ww, b_wz + b_ww)
            dt1 = view(GT, 0, 8192, F32).rearrange("p (r c) -> p r c", r=16)
            dt2 = view(GT, 8192, 8192, F32).rearrange("p (r c) -> p r c", r=16)
            b_dt = S.buf("dt"); handoff([b_dt], b_wz)
            magL_b = bc_last(s_magL, NC_)

            def scan_and_mod(init_ap, b_init, final):
                for r in range(16):
                    for x in range(2):
                        ini = 0.0 if init_ap is None else init_ap[:, r, x:x + 1]
                        S.op("dve", lambda e: e.tensor_tensor_scan(out=ZL[:, r, x, :], data0=magL_b[:, r, :], data1=WW[:, r, x, :], initial=ini,
                                                                   op0=ALU.mult, op1=ALU.add), reads=[b_ww[r], b_s5] + ([b_init] if b_init else []), writes=[b_zl[r]])
                if not final:
                    cmul(zend[:, :, 0], zend[:, :, 1], ZL[:, :, 0, NC_ - 1], ZL[:, :, 1, NC_ - 1], EC[:, :, NC_ - 1], ES[:, :, NC_ - 1], s_t, s_t2,
                         rd=b_zl + [b_tab])
                else:
                    S.op("dve", lambda e: e.tensor_tensor(out=dt1, in0=EC, in1=ZL[:, :, 0, :], op=ALU.mult), reads=[b_tab] + b_zl, writes=[b_dt])
                    S.op("dve", lambda e: e.tensor_tensor(out=dt2, in0=ES, in1=ZL[:, :, 1, :], op=ALU.mult), reads=[b_tab] + b_zl, writes=[b_dt])
                    S.op("dve", lambda e: e.tensor_tensor(out=ZS[:, :, 0, 1:NC_ + 1], in0=dt1, in1=dt2, op=ALU.subtract), reads=[b_dt], writes=b_zs)
                    S.op("dve", lambda e: e.tensor_tensor(out=dt1, in0=EC, in1=ZL[:, :, 1, :], op=ALU.mult), reads=[b_tab] + b_zl, writes=[b_dt])
                    S.op("dve", lambda e: e.tensor_tensor(out=dt2, in0=ES, in1=ZL[:, :, 0, :], op=ALU.mult), reads=[b_tab] + b_zl, writes=[b_dt])
                    S.op("dve", lambda e: e.tensor_tensor(out=ZS[:, :, 1, 1:NC_ + 1], in0=dt1, in1=dt2, op=ALU.add), reads=[b_dt], writes=b_zs)
                    S.op("dve", lambda e: e.tensor_copy(out=ZS[:, :, :, 0], in_=init_ap), reads=[b_init], writes=b_zs)

            S.op("dve", lambda e: e.tensor_tensor(out=dt1, in0=EC, in1=ZL[:, :, 0, :], op=ALU.mult), reads=[b_tab] + b_zl, writes=[b_dt])
            S.op("dve", lambda e: e.tensor_tensor(out=dt2, in0=ES, in1=ZL[:, :, 1, :], op=ALU.mult), reads=[b_tab] + b_zl, writes=[b_dt])
            S.op("dve", lambda e: e.tensor_tensor(out=WW[:, :, 0, :], in0=dt1, in1=dt2, op=ALU.add), reads=[b_dt], writes=b_ww)
            S.op("dve", lambda e: e.tensor_tensor(out=dt1, in0=EC, in1=ZL[:, :, 1, :], op=ALU.mult), reads=[b_tab] + b_zl, writes=[b_dt])
            S.op("dve", lambda e: e.tensor_tensor(out=dt2, in0=ES, in1=ZL[:, :, 0, :], op=ALU.mult), reads=[b_tab] + b_zl, writes=[b_dt])
            S.op("dve", lambda e: e.tensor_tensor(out=WW[:, :, 1, :], in0=dt1, in1=dt2, op=ALU.subtract), reads=[b_dt], writes=b_ww)
            scan_and_mod(None, None, False)
            xfer("e_ww", view(A0, 0, 16384, F32), b_ww)
            xfer("e_cp", view(A2, 0, 34816, F32), [b_cp])
            xfer("e_bd", view(A0, 32 * KB, 16 * KB, F32), b_bd)
            xfer("e_tab", view(A2, 34816, 16384, F32), [b_tab])
            xfer("e_sp", view(SP_, 0, 4864, F32), [b_s5])
            if IMP:
                S.mute = False
            if cfg.get("debug"):
                S.dma("sp", dbq, dr["dbg_zend"], zend.rearrange("p r x -> p (r x)"), reads=[b_s5], writes=[b_dbg])
            if mode == "A":
                S.dma("sp", oq, dr["summ"][:, 0:32], zend.rearrange("p r x -> p (r x)"), reads=[b_s5], writes=[b_out])
                S.wait_all("sp", [b_out])
            if mode != "A":
                sa = dr["summ_all"]
                zall = view(GT, 0, 1024, F32).rearrange("p (j r x) -> p j r x", j=8, x=2)
                zq = S.dma_sem(f"zq{l}")
                S.dma("sp", zq, zall.rearrange("p j r x -> p j (r x)"), sa[:, :, 0:32].rearrange("j p c -> p j c"), reads=[b_dt], writes=[b_dt])
                S.op("dve", lambda e: e.memset(sst, 0.0), reads=[b_s5], writes=[b_s5])
                ctr = sc(24)[:, 0:16]; cti = sc(25)[:, 0:16]
                for j in range(8):
                    cmul(ctr, cti, sst[:, :, 0], sst[:, :, 1], s_l128r, s_l128i, s_t, s_t2)
                    TT(ctr, ctr, zall[:, j, :, 0], ALU.add, rd=[b_dt]); TT(cti, cti, zall[:, j, :, 1], ALU.add, rd=[b_dt])
                    TT(ctr, ctr, sst[:, :, 0], ALU.subtract); TT(cti, cti, sst[:, :, 1], ALU.subtract)
                    S.op("dve", lambda e: e.scalar_tensor_tensor(out=sst[:, :, 0], in0=ctr, scalar=pred[:, j:j + 1], in1=sst[:, :, 0], op0=ALU.mult, op1=ALU.add),
                         reads=[b_s5, b_cst], writes=[b_s5])
                    S.op("dve", lambda e: e.scalar_tensor_tensor(out=sst[:, :, 1], in0=cti, scalar=pred[:, j:j + 1], in1=sst[:, :, 1], op0=ALU.mult, op1=ALU.add),
                         reads=[b_s5, b_cst], writes=[b_s5])
                scan_and_mod(sst, b_s5, True)
                if cfg.get("debug"):
                    S.dma("sp", dbq, dr["dbg_zs"], view(A2, 67584, 8256, BF16), reads=b_zs, writes=[b_dbg])
                handoff(b_zg, b_zl + b_zg)
                for q in range(4):
                    for ib in range(4):
                        bank, bb = next_bank()
                        for i4 in range(4):
                            ip = ib * 4 + i4
                            col = i4 * 128
                            for i in range(ip + 1):
                                S.op("pe", lambda e: e.matmul(bank[:, col:col + 128], lhsT=BD[:, q, ip - i, :], rhs=US[:, q, i, :], start=(i == 0), stop=False),
                                     inc=False, reads=[b_bd[q], b_us[q]], writes=[bb])
                            for rr in range(4):
                                r = 4 * q + rr
                                for x in range(2):
                                    lastw = (rr == 3 and x == 1)
                                    S.op("pe", lambda e: e.matmul(bank[32 * rr:32 * rr + 32, col:col + 128], lhsT=CP[:, ip + 1, r, x, :], rhs=ZS[:, r, x, 0:NC_],
                                                                  start=False, stop=lastw, tile_position=(0, 32 * rr)), inc=(lastw and i4 == 3),
                                         reads=[b_cp, b_zs[r]], writes=[bb])
                        S.op("act", lambda e: e.activation(out=ZG[:, q, ib * 4:ib * 4 + 4, :].rearrange("p i c -> p (i c)"), in_=bank[:, :], func=AF.Gelu_apprx_tanh),
                             reads=[bb], writes=[b_zg[q]])
                if cfg.get("debug"):
                    S.dma("sp", dbq, dr["dbg_zg"], view(A2, 51200, 16384, BF16), reads=b_zg, writes=[b_dbg])
                wg = dr["s5_w_glu"][l].rearrange("(k p) n -> p k n", p=128)
                wgl, bwg = wload(wg, 4, 512)
                gate = view(GT, 0, 2048, F32)
                ZZ = [view(GT, 2048 + ft * 2048, 2048, F32) for ft in range(4)]
                sqb = [view(GT, 10240 + i * 1024, 1024, BF16) for i in range(2)]
                rst = view(GT, 12288, 2048, F32)
                b_gate = S.buf("gate"); b_zz = S.bufs(4, "zz"); b_sqb = S.bufs(2, "sqb"); b_rst = S.buf("rst")
                handoff([b_gate, b_rst] + b_zz + b_sqb, [b_dt] + b_ww)
                YCv = YC[:, 0:4, :].rearrange("p f (c i) -> p f i c", i=16)
                for cb in range(4):
                    bankq, bbq = next_bank()
                    for ft in range(4):
                        bank, bb = next_bank()
                        for kk in range(4):
                            S.op("pe", lambda e: e.matmul(bank[:, :], lhsT=wgl[:, kk, ft * 128:(ft + 1) * 128], rhs=ZG[:, kk, cb * 4:cb * 4 + 4, :],
                                                          start=(kk == 0), stop=(kk == 3)), inc=(kk == 3), reads=[bwg] + b_zg, writes=[bb])
                        S.op("act", lambda e: e.activation(out=gate, in_=bank[:, :], func=AF.Sigmoid, bias=bglu[:, ft:ft + 1]), reads=[bb, b_prm], writes=[b_gate])
                        S.op("dve", lambda e: e.tensor_tensor(out=ZZ[ft], in0=ZG[:, ft, cb * 4:cb * 4 + 4, :].rearrange("p i c -> p (i c)"), in1=gate, op=ALU.mult),
                             reads=[b_zg[ft], b_gate], writes=[b_zz[ft]])
                        S.op("act", lambda e: e.activation(out=sqb[ft % 2], in_=ZZ[ft], func=AF.Square), reads=[b_zz[ft]], writes=[b_sqb[ft % 2]])
                        S.op("pe", lambda e: e.matmul(bankq[:, :], lhsT=ones_b, rhs=sqb[ft % 2], start=(ft == 0), stop=(ft == 3)), inc=True,
                             reads=[b_sqb[ft % 2], b_cst], writes=[bbq])
                    S.op("dve", lambda e: e.tensor_scalar(out=rst, in0=bankq[:, :], scalar1=1.0 / 512, scalar2=EPS, op0=ALU.mult, op1=ALU.add), reads=[bbq], writes=[b_rst])
                    S.op("act", lambda e: e.activation(out=rst, in_=rst, func=AF.Sqrt), reads=[b_rst], writes=[b_rst])
                    S.op("dve", lambda e: e.reciprocal(out=rst, in_=rst), reads=[b_rst], writes=[b_rst])
                    for ft in range(4):
                        S.op("dve", lambda e: e.scalar_tensor_tensor(out=YCv[:, ft, cb * 4:cb * 4 + 4, :], in0=ZZ[ft].rearrange("p (i c) -> p i c", i=4),
                                                                     scalar=outg[:, ft:ft + 1], in1=rst.rearrange("p (i c) -> p i c", i=4), op0=ALU.mult, op1=ALU.mult),
                             reads=[b_zz[ft], b_rst, b_prm], writes=b_yc)
                if cfg.get("debug"):
                    S.dma("sp", dbq, dr["dbg_yc"], view(A1, 0, 32 * KB, BF16), reads=b_yc, writes=[b_dbg])

            if mode != "A":
                if stop == "s5":
                    quiesce(); return
                S.handoff(X_b, b_qk + b_og + b_us + b_wz + b_bd + b_ww + [b_dt, b_gate, b_rst] + b_zz + b_sqb)
                for tt in range(NTT):
                    S.dma("sp", xq[tt], X[:, tt, :], xin_ap[tt * 128:(tt + 1) * 128, :], writes=[X_b[tt]])
                wo = dr["w_out"][l].rearrange("(k p) n -> p k n", p=128)
                ws = WStream([(wo[:, :, h * 512:(h + 1) * 512], 8, 512) for h in range(2)])
                for h in range(2):
                    wc, bw = ws.get(h)
                    for tt in range(NTT):
                        bank, bb = next_bank()
                        for kk in range(8):
                            S.op("pe", lambda e: e.matmul(bank[:, :], lhsT=YC[:, kk, tt * 128:(tt + 1) * 128], rhs=wc[:, kk, :],
                                                          start=(kk == 0), stop=(kk == 7)), inc=(kk == 7), reads=[bw, b_yc[tt]], writes=[bb])
                        S.op("dve", lambda e: e.tensor_tensor(out=X[:, tt, h * 512:(h + 1) * 512], in0=X[:, tt, h * 512:(h + 1) * 512], in1=bank[:, :], op=ALU.add),
                             reads=[bb, X_b[tt]], writes=[X_b[tt]])

                if cfg.get("dbg_x1"):
                    b_o1 = S.buf("o1")
                    for tt in range(NTT):
                        S.dma("sp", oq, xout_ap[tt * 128:(tt + 1) * 128, :], X[:, tt, :], reads=[X_b[tt]], writes=[b_o1])
                    S.wait_all("sp", [b_o1])
                    return
                if stop == "wout":
                    quiesce(); return
                S.handoff(HT_b, b_yc)
                a2_users = [b_g, b_sm] + b_kt + b_cl + [b_ct, b_tc] + b_vw + b_pt + b_hb + b_hn + [b_cp, b_tab, b_s5] + b_zl + b_zs + b_zg + b_va + [b_gr] + ml_extra
                handoff([b_junk, b_xh] + b_xnb, a2_users)
                norm_to_hT(g2, False, False)
                if cfg.get("dbg_ht2"):
                    b_o1 = S.buf("o1")
                    S.dma("sp", oq, dr["dbg_ht"], view(A1, 0, 32832, BF16)[:, 0:8 * 2051], reads=HT_b, writes=[b_o1])
                w1 = dr["w_ff1"][l].rearrange("(k p) n -> p k n", p=128)
                w2 = dr["w_ff2"][l].rearrange("(k p) n -> p k n", p=128)
                c1 = [(w1[:, :, hc * 512:(hc + 1) * 512], 8, 512) for hc in range(8)]
                c2 = [(w2[:, hc * 4:(hc + 1) * 4, :], 4, 1024) for hc in range(8)]
                chunks = [c1[0]]
                for hc in range(8):
                    if hc + 1 < 8:
                        chunks.append(c1[hc + 1])
                    chunks.append(c2[hc])
                ws = WStream(chunks)
                k.wci = 0

                def wnext():
                    r = ws.get(k.wci)
                    k.wci += 1
                    return r
                hid = [view(SCR, i * 16 * KB, 16 * KB, BF16).rearrange("p (f t) -> p f t", f=4) for i in range(2)]
                b_hid = [S.bufs(4, f"hid{i}") for i in range(2)]
                sq = [view(SCR, 32 * KB + i * 2048, 2048, F32) for i in range(2)]
                b_sq = S.bufs(2, "sq")
                k.sqi = 0
                handoff(b_hid[0] + b_hid[1] + b_sq, a2_users + [b_junk, b_xh] + b_xnb)

                def ffn1(hc):
                    wc, bw = wnext()
                    hb = hid[hc % 2]
                    for ft in range(4):
                        for nb in range(4):
                            bank, bb = next_bank()
                            for kk in range(8):
                                S.op("pe", lambda e: e.matmul(bank[:, :], lhsT=wc[:, kk, ft * 128:(ft + 1) * 128], rhs=HT[:, kk, 3 + nb * 512:3 + (nb + 1) * 512],
                                                              start=(kk == 0), stop=(kk == 7)), inc=(kk == 7), reads=[bw] + HT_b[nb * 4:nb * 4 + 4], writes=[bb])
                            si = k.sqi % 2; k.sqi += 1
                            S.op("act", lambda e: e.activation(out=sq[si], in_=bank[:, :], func=AF.Square), reads=[bb], writes=[b_sq[si]])
                            S.op("dve", lambda e: e.scalar_tensor_tensor(out=hb[:, ft, nb * 512:(nb + 1) * 512], in0=bank[:, :], scalar=0.0, in1=sq[si],
                                                                         op0=ALU.is_gt, op1=ALU.mult), reads=[bb, b_sq[si]], writes=[b_hid[hc % 2][nb]])

                def ffn2(hc):
                    wc, bw = wnext()
                    hb = hid[hc % 2]
                    for tt in range(NTT):
                        for h in range(2):
                            bank, bb = next_bank()
                            for kk in range(4):
                                S.op("pe", lambda e: e.matmul(bank[:, :], lhsT=hb[:, kk, tt * 128:(tt + 1) * 128], rhs=wc[:, kk, h * 512:(h + 1) * 512],
                                                              start=(kk == 0), stop=(kk == 3)), inc=(kk == 3), reads=[bw, b_hid[hc % 2][tt // 4]], writes=[bb])
                            S.op("dve", lambda e: e.tensor_tensor(out=X[:, tt, h * 512:(h + 1) * 512], in0=X[:, tt, h * 512:(h + 1) * 512], in1=bank[:, :], op=ALU.add),
                                 reads=[bb, X_b[tt]], writes=[X_b[tt]])

                ffn1(0)
                for hc in range(8):
                    if hc + 1 < 8:
                        ffn1(hc + 1)
                    ffn2(hc)

                if last:
                    gfin = view(SCR, 40 * KB, 4096, F32)
                    b_gf = S.buf("gfin")
                    fg = dr["final_norm_g"]
                    S.dma("sp", gq, gfin, bass.AP(fg.tensor, fg.offset, [[0, 128], [1, D]]), writes=[b_gf])
                    ot = [view(SCR, 44 * KB + i * 4096, 4096, F32) for i in range(2)]
                    b_ot = S.bufs(2, "ot")
                    handoff([b_junk], [b_junk] + b_hid[0] + b_hid[1])
                    stats_A(0)
                    for g4 in range(4):
                        if g4 + 1 < 4:
                            stats_A(g4 + 1)
                        stats_B(g4)
                        for tt in range(4 * g4, 4 * g4 + 4):
                            S.op("dve", lambda e: e.scalar_tensor_tensor(out=ot[tt % 2], in0=X[:, tt, :], scalar=rstd[:, tt:tt + 1], in1=gfin,
                                                                         op0=ALU.mult, op1=ALU.mult), reads=[X_b[tt], b_sg[g4], b_gf], writes=[b_ot[tt % 2]])
                            S.dma("sp", oq, xout_ap[tt * 128:(tt + 1) * 128, :], ot[tt % 2], reads=[b_ot[tt % 2]], writes=[b_out])
                else:
                    for tt in range(NTT):
                        S.dma("sp", oq, xout_ap[tt * 128:(tt + 1) * 128, :], X[:, tt, :], reads=[X_b[tt]], writes=[b_out])
                S.wait_all("sp", [b_out])

        layers = cfg["layers"]
        for li, l in enumerate(layers):
            layer(l, dr["xin"], dr.get("xhalo"), dr.get("xout"), last=cfg.get("final", False) and li == len(layers) - 1)
    return nc


_NC_CACHE = {}
N_CORES = 8
LAYER_KEYS = ["norm_mix_g", "w_in", "w_out", "norm_ffn_g", "w_ff1", "w_ff2", "ml_conv_w", "ml_conv_b", "ml_b_i", "ml_b_f",
              "ml_norm_g", "s5_a_re", "s5_a_im", "s5_log_dt", "s5_b_re", "s5_b_im", "s5_c_re", "s5_c_im", "s5_d", "s5_w_glu",
              "s5_b_glu", "s5_out_g"]
A_KEYS = ["norm_mix_g", "w_in", "ml_conv_w", "ml_conv_b", "ml_b_i", "ml_b_f", "s5_a_re", "s5_a_im", "s5_log_dt", "s5_b_re", "s5_b_im",
          "s5_c_re", "s5_c_im", "s5_d", "ml_norm_g", "s5_b_glu", "s5_out_g"]
B_KEYS = ["w_out", "norm_ffn_g", "w_ff1", "w_ff2", "s5_w_glu", "ml_norm_g", "s5_b_glu", "s5_out_g"]
XF_NAMES = ["e_a0", "e_va", "e_gates", "e_cl", "e_ww", "e_cp", "e_bd", "e_tab", "e_sp"]
A_CONST = ["ident", "causal", "ones", "par01", "bdmask"]
B_CONST = ["ident", "causal", "ones"]


def _get_nc(mode, final):
    key = (mode, final)
    if key not in _NC_CACHE:
        _NC_CACHE[key] = build(dict(layers=[0], nlayers=1, mode=mode, final=final, debug=False))
    return _NC_CACHE[key]


def _consts():
    par = np.zeros((128, 2), np.float32)
    par[:, 1] = (np.arange(128) // 16) % 2
    par[:, 0] = 1 - par[:, 1]
    return {"ident": np.eye(128, dtype=np.float32), "causal": np.triu(np.ones((128, 128), np.float32)),
            "ones": np.ones((128, 128), np.float32), "par01": par,
            "bdmask": np.kron(np.eye(8), np.ones((16, 16))).astype(np.float32)}


def kernel(**inputs):
    x = np.ascontiguousarray(inputs["x"], dtype=np.float32)
    nb, ls, d = x.shape
    per = ls // 4
    consts = _consts()
    cur = [np.ascontiguousarray(x[c // 4, (c % 4) * per:(c % 4 + 1) * per]) for c in range(N_CORES)]
    preds = []
    for c in range(N_CORES):
        p = np.zeros((128, 8), np.float32)
        for j in range(N_CORES):
            if j // 4 == c // 4 and j < c:
                p[:, j] = 1.0
        preds.append(p)
    depth = inputs["w_in"].shape[0]
    for l in range(depth):
        halos = [np.zeros((3, d), np.float32) if c % 4 == 0 else np.ascontiguousarray(cur[c - 1][-3:]) for c in range(N_CORES)]
        lw = {k: np.ascontiguousarray(np.asarray(inputs[k], dtype=np.float32)[l:l + 1]) for k in LAYER_KEYS}
        final = (l == depth - 1)
        ncA = _get_nc("A", False)
        mapsA = []
        for c in range(N_CORES):
            m = {"xin": cur[c], "xhalo": halos[c], "pred": preds[c]}
            m.update({k: consts[k] for k in A_CONST})
            m.update({k: lw[k] for k in A_KEYS})
            mapsA.append(m)
        resA = run_bass_kernel_spmd(ncA, mapsA, core_ids=list(range(N_CORES)))
        summ_all = np.ascontiguousarray(np.stack([np.asarray(resA.results[c]["summ"]) for c in range(N_CORES)]))
        ncB = _get_nc("B", final)
        mapsB = []
        for c in range(N_CORES):
            m = {"xin": cur[c], "pred": preds[c], "summ_all": summ_all,
                 "final_norm_g": np.ascontiguousarray(inputs["final_norm_g"], dtype=np.float32)}
            m.update({k: consts[k] for k in B_CONST})
            m.update({k: lw[k] for k in B_KEYS})
            m.update({k: np.asarray(resA.results[c][k]) for k in XF_NAMES})
            mapsB.append(m)
        resB = run_bass_kernel_spmd(ncB, mapsB, core_ids=list(range(N_CORES)))
        cur = [np.asarray(resB.results[c]["xout"]) for c in range(N_CORES)]
    out = np.empty_like(x)
    for c in range(N_CORES):
        out[c // 4, (c % 4) * per:(c % 4 + 1) * per] = cur[c]
    return out
```

```python
import numpy as np
import concourse.bass as bass
import concourse.mybir as mybir
from concourse.bass_utils import run_bass_kernel_spmd

F32 = mybir.dt.float32
BF16 = mybir.dt.bfloat16
AF = mybir.ActivationFunctionType
ALU = mybir.AluOpType
AX = mybir.AxisListType


class Buf:
    __slots__ = ("name", "w", "r")

    def __init__(self, name):
        self.name = name
        self.w = {}
        self.r = {}


class Sched:
    def __init__(self, nc, ctx):
        self.nc = nc
        self.ctx = ctx
        self.eng = {"pe": nc.tensor, "act": nc.scalar, "dve": nc.vector, "pool": nc.gpsimd, "sp": nc.sync}
        self.sem = {}
        self.cnt = {}
        for k in self.eng:
            self.sem[k] = ctx.enter_context(nc.semaphore("s_" + k))
            self.cnt[k] = 0
        self.waited = {k: {} for k in self.eng}
        self.ndma = 0
        self.nbuf = 0
        self.mute = False

    def buf(self, name=None):
        self.nbuf += 1
        return Buf(name or f"b{self.nbuf}")

    def bufs(self, n, name="b"):
        return [self.buf(f"{name}{i}") for i in range(n)]

    def dma_sem(self, name=None):
        self.ndma += 1
        key = name or f"dma{self.ndma}"
        self.sem[key] = self.ctx.enter_context(self.nc.semaphore("s_" + key))
        self.cnt[key] = 0
        return key

    def _deps(self, e, reads, writes):
        deps = {}
        for b in reads:
            for k, c in b.w.items():
                if deps.get(k, 0) < c:
                    deps[k] = c
        for b in writes:
            for k, c in b.w.items():
                if deps.get(k, 0) < c:
                    deps[k] = c
            for k, c in b.r.items():
                if deps.get(k, 0) < c:
                    deps[k] = c
        eng = self.eng[e]
        for k, c in deps.items():
            if k == e and e == "pe":
                continue
            if self.waited[e].get(k, 0) < c:
                eng.wait_ge(self.sem[k], c)
                self.waited[e][k] = c

    def _record(self, key, c, reads, writes):
        for b in writes:
            b.w = {key: c}
            b.r = {}
        for b in reads:
            if b.r.get(key, 0) < c:
                b.r[key] = c

    def op(self, e, fn, reads=(), writes=(), inc=True):
        if self.mute:
            return None
        self._deps(e, reads, writes)
        ins = fn(self.eng[e])
        if inc:
            self.cnt[e] += 1
            ins.then_inc(self.sem[e], 1)
            self._record(e, self.cnt[e], reads, writes)
        else:
            self._record(e, self.cnt[e] + 1, reads, writes)
        return ins

    def seal(self, key, bufs):
        if self.mute:
            return
        c = self.cnt[key]
        for b in bufs:
            if key in b.w:
                b.w[key] = c

    def handoff(self, news, olds):
        w = {}
        r = {}
        for ob in olds:
            for k2, c2 in ob.w.items():
                if w.get(k2, 0) < c2:
                    w[k2] = c2
            for k2, c2 in ob.r.items():
                if r.get(k2, 0) < c2:
                    r[k2] = c2
        for nb in news:
            nb.w = dict(w)
            nb.r = dict(r)

    def dma(self, q, dsem, out, in_, reads=(), writes=(), **kw):
        if self.mute:
            return None
        self._deps(q, reads, writes)
        ins = self.eng[q].dma_start(out=out, in_=in_, **kw)
        self.cnt[dsem] += 16
        ins.then_inc(self.sem[dsem], 16)
        self._record(dsem, self.cnt[dsem], reads, writes)
        return ins

    def wait_all(self, e, bufs):
        if self.mute:
            return
        self._deps(e, bufs, ())


import numpy as np
from contextlib import ExitStack

NT = 2048
NTT = 16
D = 1024
DIN = 2568
DFF = 4096
EPS = 1e-6
KB = 1024


class KB_:
    pass


def build(cfg):
    nc = bass.Bass("TRN2", target_bir_lowering=False)
    k = KB_()
    k.nc = nc
    k.cfg = cfg
    L = cfg.get("nlayers", 1)
    dr = {}

    def din(name, shape, dt=F32):
        dr[name] = nc.dram_tensor(name, list(shape), dt, kind="ExternalInput").ap()
        return dr[name]

    def dout(name, shape, dt=F32):
        dr[name] = nc.dram_tensor(name, list(shape), dt, kind="ExternalOutput").ap()
        return dr[name]

    mode = cfg.get("mode", "B")
    IMP = (mode == "B")
    EXP = (mode == "A")
    XF = [("e_a0", 16384), ("e_va", 4256), ("e_gates", 1152), ("e_cl", 8256), ("e_ww", 4096), ("e_cp", 8704), ("e_bd", 4096),
          ("e_tab", 4096), ("e_sp", 1216)]
    din("xin", [NT, D])
    if IMP:
        _din_real = din

        def din(name, shape, dt=F32, _real=_din_real):
            dr[name] = nc.dram_tensor(name, list(shape), dt).ap()
            return dr[name]
    if True:
        din("xhalo", [3, D])
        din("norm_mix_g", [L, D]); din("w_in", [L, D, DIN])
        din("ml_conv_w", [L, 4, 1024]); din("ml_conv_b", [L, 1024])
        din("ml_b_i", [L, 4]); din("ml_b_f", [L, 4])
        din("s5_a_re", [L, 32, 64]); din("s5_a_im", [L, 32, 64]); din("s5_log_dt", [L, 32])
        din("s5_b_re", [L, 32, 64, 16]); din("s5_b_im", [L, 32, 64, 16]); din("s5_c_re", [L, 32, 16, 64]); din("s5_c_im", [L, 32, 16, 64])
        din("s5_d", [L, 32, 16])
        din("par01", [128, 2]); din("bdmask", [128, 128])
    if IMP:
        din = _din_real
    if mode != "A":
        din("w_out", [L, D, D])
        din("norm_ffn_g", [L, D]); din("w_ff1", [L, D, DFF]); din("w_ff2", [L, DFF, D])
        din("final_norm_g", [D])
        din("s5_w_glu", [L, 512, 512])
        din("summ_all", [8, 128, 556])
    din("ident", [128, 128]); din("causal", [128, 128]); din("ones", [128, 128])
    din("ml_norm_g", [L, 512]); din("s5_b_glu", [L, 512]); din("s5_out_g", [L, 512])
    din("pred", [128, 8])
    if EXP:
        dout("summ", [128, 556])
        for nm_, w_ in XF:
            dout(nm_, [128, w_])
    if IMP:
        for nm_, w_ in XF:
            din(nm_, [128, w_])
    if mode != "A":
        dout("xout", [NT, D])
    if cfg.get("dbg_ht2"):
        dout("dbg_ht", [128, 8 * 2051], BF16)
    if cfg.get("debug"):
        dout("dbg_u", [128, 4 * 2048], BF16)
        dout("dbg_qk", [128, 8 * 2048], BF16)
        dout("dbg_v", [128, 16 * 4 * 129], BF16)
        dout("dbg_o", [128, 16 * 512], BF16)
        dout("dbg_if", [128, 128])
        dout("dbg_yc", [128, 8 * 2048], BF16)
        dout("dbg_g", [128, 16 * 64])
        dout("dbg_zl", [128, 4096]); dout("dbg_zs", [128, 16 * 2 * 129], BF16); dout("dbg_zg", [128, 8192], BF16)
        dout("dbg_sc", [128, 1216]); dout("dbg_cp", [128, 17408], BF16); dout("dbg_bd", [128, 8192], BF16); dout("dbg_wz", [128, 16384], BF16)
        dout("dbg_zend", [128, 32])
    k.dr = dr

    with ExitStack() as ctx:
        S = Sched(nc, ctx)
        k.S = S
        ctx.enter_context(nc.allow_non_contiguous_dma(reason="small param loads"))
        ctx.enter_context(nc.allow_low_precision(reason="bf16 matmul operands"))
        A0_B, A1_B, A2_B = 64 * KB, 33 * KB + 256, 79 * KB
        arena = ctx.enter_context(nc.sbuf_tensor("arena", [128, (A0_B + A1_B + A2_B) // 4], F32))
        ring = ctx.enter_context(nc.sbuf_tensor("ring", [128, 3 * 4096], BF16))
        cst = ctx.enter_context(nc.sbuf_tensor("cst", [128, 1024], F32))
        banks = [ctx.enter_context(nc.psum_tensor(f"ps{i}", [128, 512], F32)) for i in range(8)]
        bank_bufs = S.bufs(8, "bank")
        k.bank_i = 0
        block = ctx.enter_context(nc.Block())

        def view(base, off, nbytes, dt):
            assert off % 4 == 0 and nbytes % 4 == 0
            a = arena[:, (base + off) // 4:(base + off + nbytes) // 4]
            return a if dt == F32 else a.bitcast(dt)
        A0, A1, A2 = 0, A0_B, A0_B + A1_B

        def next_bank():
            i = k.bank_i
            k.bank_i = (i + 1) % 8
            return banks[i], bank_bufs[i]

        ident_f = cst[:, 0:128]
        ident_b = cst[:, 128:192].bitcast(BF16)
        b_cst = S.buf("cst")
        dq = S.dma_sem("dq_misc")
        S.dma("sp", dq, ident_f, dr["ident"], writes=[b_cst])
        tri_f = cst[:, 384:512]
        ones_f = cst[:, 512:640]
        tri_b = cst[:, 192:256].bitcast(BF16)
        S.dma("sp", dq, tri_f, dr["causal"], writes=[b_cst])
        S.dma("sp", dq, ones_f, dr["ones"], writes=[b_cst])
        par01 = cst[:, 752:754]
        bdm = cst[:, 768:896]
        ones_b = cst[:, 896:960].bitcast(BF16)
        if not IMP:
            S.dma("sp", dq, par01, dr["par01"], writes=[b_cst])
        pred = cst[:, 972:980]
        pmask = cst[:, 980:988]
        S.dma("sp", dq, pred, dr["pred"], writes=[b_cst])
        if not IMP:
            S.dma("sp", dq, bdm, dr["bdmask"], writes=[b_cst])
        S.seal(dq, [b_cst])
        S.op("dve", lambda e: e.tensor_copy(out=ones_b, in_=ones_f), reads=[b_cst], writes=[b_cst])
        S.op("dve", lambda e: e.tensor_scalar(out=pmask, in0=pred, scalar1=1e6, scalar2=-1e6, op0=ALU.mult, op1=ALU.add), reads=[b_cst], writes=[b_cst])
        S.op("dve", lambda e: e.tensor_copy(out=ident_b, in_=ident_f), reads=[b_cst], writes=[b_cst])
        S.op("dve", lambda e: e.tensor_copy(out=tri_b, in_=tri_f), reads=[b_cst], writes=[b_cst])

        X = view(A0, 0, 64 * KB, F32).rearrange("p (t d) -> p t d", t=NTT)
        X_b = S.bufs(NTT, "X")
        HT = view(A1, 0, 8 * 2051 * 2 + 0, BF16) if False else view(A1, 0, 32832, BF16)[:, 0:8 * 2051].rearrange("p (k t) -> p k t", k=8)
        HT_b = S.bufs(NTT + 1, "HT")
        YC = view(A1, 0, 32 * KB, BF16).rearrange("p (k t) -> p k t", k=8)
        QK = view(A0, 0, 32 * KB, BF16).rearrange("p (f t) -> p f t", f=8)
        OG = view(A0, 32 * KB, 16 * KB, BF16).rearrange("p (t d) -> p t d", t=NTT)
        US = view(A0, 48 * KB, 16 * KB, BF16).rearrange("p (q i c) -> p q i c", q=4, i=16)
        VA = view(A2, 0, 16512, BF16).rearrange("p (t h v) -> p t h v", t=NTT, h=4)
        GR = view(A2, 16512, 512, F32)
        SCR = A2 + 17024

        xq = [S.dma_sem(f"xq{i}") for i in range(16)]
        hq = S.dma_sem("hq"); gq = S.dma_sem("gq")
        oq = S.dma_sem("oq")
        wq = [S.dma_sem(f"wq{i}") for i in range(3)]
        ring_b = S.bufs(3, "ring")
        k.wi = 0

        def wload(src_ap, nk, ncols):
            i = k.wi % 3
            k.wi += 1
            v = ring[:, i * 4096: i * 4096 + nk * ncols].rearrange("p (k n) -> p k n", k=nk)
            S.dma("pool", wq[i], v, src_ap, writes=[ring_b[i]])
            return v, ring_b[i]

        class WStream:
            def __init__(self, chunks):
                self.chunks = chunks
                self.loaded = []

            def get(self, i, ahead=2):
                while len(self.loaded) < min(len(self.chunks), i + 1 + ahead):
                    self.loaded.append(wload(*self.chunks[len(self.loaded)]))
                return self.loaded[i]

        k.pq = None

        def load_pvec(dst, src_1d, b, q="sp"):
            S.dma(q, k.pq, dst, src_1d.rearrange("(k p) -> p k", p=128), writes=[b])

        tmp_b = S.bufs(4, "tmp")

        def quiesce():
            for e_ in ("pe", "act", "dve", "pool"):
                if S.cnt[e_] > 0:
                    nc.sync.wait_ge(S.sem[e_], S.cnt[e_])
            for key_, c_ in S.cnt.items():
                if key_ not in S.eng and c_ > 0:
                    nc.sync.wait_ge(S.sem[key_], c_)

        def layer(l, xin_ap, xh_ap, xout_ap, last):
            stop = cfg.get("stop")
            prm = cst[:, 256:256 + 64]
            b_prm = S.buf("prm")
            pq_l = S.dma_sem(f"pq{l}"); k.pq = pq_l
            g1 = cst[:, 640:648]; g2 = cst[:, 648:656]
            cw = cst[:, 656:688].rearrange("p (j f) -> p j f", j=4)
            cb = cst[:, 688:696]
            b_out = S.buf("out")
            if not IMP:
                load_pvec(g1, dr["norm_mix_g"][l], b_prm)
                for j in range(4):
                    load_pvec(cw[:, j, :], dr["ml_conv_w"][l, j], b_prm)
                load_pvec(cb, dr["ml_conv_b"][l], b_prm)
            if mode != "A":
                load_pvec(g2, dr["norm_ffn_g"][l], b_prm)
            mlg = cst[:, 740:744]
            load_pvec(mlg, dr["ml_norm_g"][l], b_prm)
            bif = cst[:, 744:752]
            dcol = cst[:, 960:964]; bglu = cst[:, 964:968]; outg = cst[:, 968:972]
            if not IMP:
                bi_ = dr["ml_b_i"][l]; bf_ = dr["ml_b_f"][l]
                S.dma("sp", pq_l, bif[:, 0:4], bass.AP(bi_.tensor, bi_.offset, [[0, 128], [1, 4]]), writes=[b_prm])
                S.dma("sp", pq_l, bif[:, 4:8], bass.AP(bf_.tensor, bf_.offset, [[0, 128], [1, 4]]), writes=[b_prm])
                load_pvec(dcol, dr["s5_d"][l].rearrange("g p -> (g p)"), b_prm)
            load_pvec(bglu, dr["s5_b_glu"][l], b_prm)
            load_pvec(outg, dr["s5_out_g"][l], b_prm)
            S.seal(pq_l, [b_prm])

            def xfer(name, ap, bufs):
                was = S.mute; S.mute = False
                if EXP:
                    S.dma("sp", oq, dr[name], ap, reads=bufs, writes=[b_out])
                elif IMP:
                    q_ = S.dma_sem(f"{name}_{l}")
                    S.dma("sp", q_, ap, dr[name], writes=bufs)
                S.mute = was
            if IMP:
                S.mute = True
            ssq = cst[:, 700:717]
            rstd = cst[:, 720:737]
            b_st = S.bufs(17, "st")
            junk = view(SCR, 0, 2048, BF16)
            b_junk = S.buf("junk")
            xnb = [view(SCR, 2048 + i * 2048, 2048, BF16) for i in range(2)]
            b_xnb = S.bufs(2, "xnb")
            xh_t = view(SCR, 6144, 4096, F32)
            b_xh = S.buf("xh")

            def rms_stats(src, np_, col, bsrc):
                S.op("act", lambda e: e.activation(out=junk[:np_], in_=src, func=AF.Square, accum_out=ssq[:np_, col:col + 1]),
                     reads=[bsrc], writes=[b_junk, b_st[col]])
                S.op("dve", lambda e: e.tensor_scalar(out=rstd[:np_, col:col + 1], in0=ssq[:np_, col:col + 1], scalar1=1.0 / D, scalar2=EPS,
                                                      op0=ALU.mult, op1=ALU.add), reads=[b_st[col]], writes=[b_st[col]])
                S.op("act", lambda e: e.activation(out=rstd[:np_, col:col + 1], in_=rstd[:np_, col:col + 1], func=AF.Sqrt),
                     reads=[b_st[col]], writes=[b_st[col]])
                S.op("dve", lambda e: e.reciprocal(out=rstd[:np_, col:col + 1], in_=rstd[:np_, col:col + 1]), reads=[b_st[col]], writes=[b_st[col]])

            b_sg = S.bufs(5, "stg")

            def stats_A(g4):
                tiles = [NTT] if g4 == 4 else range(4 * g4, 4 * g4 + 4)
                for tt in tiles:
                    halo = tt == NTT
                    np_ = 3 if halo else 128
                    src, bsrc = (xh_t[:3, :], b_xh) if halo else (X[:, tt, :], X_b[tt])
                    S.op("act", lambda e: e.activation(out=junk[:np_], in_=src, func=AF.Square, accum_out=ssq[:np_, tt:tt + 1]),
                         reads=[bsrc], writes=[b_junk, b_sg[g4]])

            def stats_B(g4):
                c0, c1 = (NTT, NTT + 1) if g4 == 4 else (4 * g4, 4 * g4 + 4)
                np_ = 3 if g4 == 4 else 128
                S.op("dve", lambda e: e.tensor_scalar(out=rstd[:np_, c0:c1], in0=ssq[:np_, c0:c1], scalar1=1.0 / D, scalar2=EPS,
                                                      op0=ALU.mult, op1=ALU.add), reads=[b_sg[g4]], writes=[b_sg[g4]])
                S.op("act", lambda e: e.activation(out=rstd[:np_, c0:c1], in_=rstd[:np_, c0:c1], func=AF.Sqrt), reads=[b_sg[g4]], writes=[b_sg[g4]])
                S.op("dve", lambda e: e.reciprocal(out=rstd[:np_, c0:c1], in_=rstd[:np_, c0:c1]), reads=[b_sg[g4]], writes=[b_sg[g4]])

            def norm_to_hT(gvec, with_halo, from_dram):
                gb = bass.AP(gvec.tensor, gvec.offset, [list(gvec.ap[0]), list(gvec.ap[1]), [0, 128]])
                for tt in range(NTT):
                    if from_dram:
                        S.dma("sp", xq[tt], X[:, tt, :], xin_ap[tt * 128:(tt + 1) * 128, :], writes=[X_b[tt]])
                if with_halo:
                    S.dma("sp", hq, xh_t[:3, :], xh_ap, writes=[b_xh])

                def stage_C(g4):
                    tiles = [NTT] if g4 == 4 else range(4 * g4, 4 * g4 + 4)
                    for tt in tiles:
                        halo = tt == NTT
                        np_ = 3 if halo else 128
                        src, bsrc = (xh_t[:3, :], b_xh) if halo else (X[:, tt, :], X_b[tt])
                        xb = xnb[tt % 2]; bx = b_xnb[tt % 2]
                        S.op("act", lambda e: e.activation(out=xb[:np_], in_=src, func=AF.Copy, scale=rstd[:np_, tt:tt + 1]),
                             reads=[bsrc, b_sg[g4]], writes=[bx])
                        bank, bb = next_bank()
                        pb = bank[:, 0:512].bitcast(BF16).rearrange("p (k t) -> p k t", k=8)
                        for kk in range(8):
                            S.op("pe", lambda e: e.transpose(pb[:, kk, 0:np_], xb[:np_, kk * 128:(kk + 1) * 128], ident_b[:np_, :np_]),
                                 inc=(kk == 7), reads=[bx, b_cst], writes=[bb])
                        c0 = 0 if halo else 3 + tt * 128
                        S.op("dve", lambda e: e.tensor_tensor(out=HT[:, :, c0:c0 + np_], in0=pb[:, :, 0:np_], in1=gb[:, :, 0:np_], op=ALU.mult),
                             reads=[bb, b_prm], writes=[HT_b[tt]])
                ng = 5 if with_halo else 4
                stats_A(0)
                for g4 in range(ng):
                    if g4 + 1 < ng:
                        stats_A(g4 + 1)
                    stats_B(g4)
                    stage_C(g4)

            norm_to_hT(g1, True, True)

            if stop == "norm1":
                quiesce(); return
            win = dr["w_in"][l].rearrange("(k p) n -> p k n", p=128)
            chunks = [(win[:, :, c * 512:(c + 1) * 512], 8, 512) for c in range(5)] + [(win[:, :, 2560:2568], 8, 8)]
            ws = WStream(chunks)
            stage = [view(SCR, 10240 + i * 8448, 8448, F32) for i in range(2)]
            b_stage = S.bufs(2, "stage")
            acc = view(SCR, 10240 + 2 * 8448, 8192, F32)
            b_acc = S.buf("acc")
            b_us = S.bufs(4, "us")
            b_qk = S.bufs(8, "qk")
            b_va = S.bufs(NTT, "va")
            b_og = S.bufs(NTT, "og")
            b_gr = S.buf("gr")
            allHT = HT_b
            S.handoff(b_us + b_qk + b_og, X_b)
            S.op("pool", lambda e: e.memset(VA[:, :, :, 128:129], 1.0), writes=b_va)
            for ci in range(3):
                wc, bw = ws.get(ci)
                for ft in range(4):
                    f = (ci - 1) * 4 + ft
                    if ci > 0:
                        st = stage[f % 2]; bs = b_stage[f % 2]
                        bank, bb = next_bank()
                        for kk in range(8):
                            S.op("pe", lambda e: e.matmul(bank[:, 0:3], lhsT=wc[:, kk, ft * 128:(ft + 1) * 128], rhs=HT[:, kk, 0:3],
                                                          start=(kk == 0), stop=(kk == 7)), inc=(kk == 7), reads=[bw, HT_b[NTT]], writes=[bb])
                        S.op("act", lambda e: e.activation(out=st[:, 0:3], in_=bank[:, 0:3], func=AF.Copy), reads=[bb], writes=[bs])
                    for nb in range(4):
                        bank, bb = next_bank()
                        for kk in range(8):
                            S.op("pe", lambda e: e.matmul(bank[:, :], lhsT=wc[:, kk, ft * 128:(ft + 1) * 128],
                                                          rhs=HT[:, kk, 3 + nb * 512:3 + (nb + 1) * 512], start=(kk == 0), stop=(kk == 7)),
                                 inc=(kk == 7), reads=[bw] + allHT[nb * 4:nb * 4 + 4], writes=[bb])
                        if ci == 0:
                            dst = US[:, ft, :, nb * 32:(nb + 1) * 32]
                            src = bank[:, :].rearrange("p (c i) -> p i c", i=16)
                            S.op("act", lambda e: e.activation(out=dst, in_=src, func=AF.Copy), reads=[bb], writes=[b_us[ft]])
                        else:
                            S.op("act", lambda e: e.activation(out=st[:, 3 + nb * 512:3 + (nb + 1) * 512], in_=bank[:, :], func=AF.Copy),
                                 reads=[bb], writes=[bs])
                    if ci > 0:
                        S.op("dve", lambda e: e.tensor_scalar(out=acc, in0=st[:, 0:2048], scalar1=cw[:, 0, f:f + 1], scalar2=None, op0=ALU.mult),
                             reads=[bs, b_prm], writes=[b_acc])
                        for j in range(1, 4):
                            S.op("dve", lambda e: e.scalar_tensor_tensor(out=acc, in0=st[:, j:j + 2048], scalar=cw[:, j, f:f + 1], in1=acc,
                                                                         op0=ALU.mult, op1=ALU.add), reads=[bs, b_prm, b_acc], writes=[b_acc])
                        S.op("act", lambda e: e.activation(out=QK[:, f, :], in_=acc, func=AF.Silu, bias=cb[:, f:f + 1]),
                             reads=[b_acc, b_prm], writes=[b_qk[f]])
            for ci in (3, 4):
                wc, bw = ws.get(ci)
                for tt in range(NTT):
                    bank, bb = next_bank()
                    for kk in range(8):
                        S.op("pe", lambda e: e.matmul(bank[:, :], lhsT=HT[:, kk, 3 + tt * 128:3 + (tt + 1) * 128], rhs=wc[:, kk, :],
                                                      start=(kk == 0), stop=(kk == 7)), inc=(kk == 7), reads=[bw, HT_b[tt]], writes=[bb])
                    if ci == 3:
                        S.op("act", lambda e: e.activation(out=VA[:, tt, :, 0:128], in_=bank[:, :].rearrange("p (h v) -> p h v", h=4), func=AF.Copy),
                             reads=[bb], writes=[b_va[tt]])
                    else:
                        S.op("act", lambda e: e.activation(out=OG[:, tt, :], in_=bank[:, :], func=AF.Sigmoid), reads=[bb], writes=[b_og[tt]])
            wc, bw = ws.get(5)
            bank, bb = next_bank()
            for tt in range(NTT):
                for kk in range(8):
                    S.op("pe", lambda e: e.matmul(bank[:, tt * 8:(tt + 1) * 8], lhsT=HT[:, kk, 3 + tt * 128:3 + (tt + 1) * 128], rhs=wc[:, kk, :],
                                                  start=(kk == 0), stop=(kk == 7)), inc=(kk == 7), reads=[bw, HT_b[tt]], writes=[bb])
            S.op("dve", lambda e: e.tensor_copy(out=GR, in_=bank[:, 0:128]), reads=[bb], writes=[b_gr])

            if cfg.get("debug"):
                dbq = S.dma_sem("dbq")
                b_dbg = S.buf("dbg")
                S.dma("sp", dbq, dr["dbg_u"], view(A0, 48 * KB, 16 * KB, BF16), reads=b_us, writes=[b_dbg])
                S.dma("sp", dbq, dr["dbg_qk"], view(A0, 0, 32 * KB, BF16), reads=b_qk, writes=[b_dbg])
                S.dma("sp", dbq, dr["dbg_v"], view(A2, 0, 16512, BF16), reads=b_va, writes=[b_dbg])
                S.dma("sp", dbq, dr["dbg_o"], view(A0, 32 * KB, 16 * KB, BF16), reads=b_og, writes=[b_dbg])
                S.dma("sp", dbq, dr["dbg_if"], GR, reads=[b_gr], writes=[b_dbg])
                pass

            if stop == "win":
                quiesce(); return
            xfer("e_a0", view(A0, 0, 64 * KB, F32), b_qk + b_og + b_us)
            xfer("e_va", view(A2, 0, 17024, F32), b_va + [b_gr])
            b_yc = S.bufs(NTT, "yc")
            S.handoff(b_yc, HT_b)

            MS = SCR
            def garr(i):
                return view(MS, i * 256, 256, F32).rearrange("p (m h) -> p m h", h=4)
            nlf, ig, nF, g, Gm, R0, Rr, w0, wv, clamp, dl, ep, incl, t1 = [garr(i) for i in range(14)]
            sm = view(MS, 14 * 256, 256, F32)
            gcol = sm[:, 0:1]; dm = sm[:, 4:8]; rec = sm[:, 8:12]; ss = sm[:, 12:16]; rs4 = sm[:, 16:20]
            Gb = view(MS, 15 * 256, 512, F32)
            b_g = S.buf("gates")
            b_sm = S.buf("sm")
            KT = view(MS, 4608, 16512, BF16)
            kTok = KT[:, 0:16 * 512].rearrange("p (m d) -> p m d", m=16)
            Cin = KT[:, 0:64 * 129].rearrange("p (c v) -> p c v", c=64)
            b_kt = S.bufs(16, "kt")
            CL = view(MS, 4608 + 16512, 64 * 129 * 4, F32).rearrange("p (c v) -> p c v", c=64)
            b_cl = S.bufs(16, "cl")
            T0 = MS + 4608 + 16512 + 64 * 129 * 4
            Ct = view(T0, 0, 2064, F32).rearrange("p (h v) -> p h v", h=4)
            tmpC = view(T0, 2064, 2064, F32).rearrange("p (h v) -> p h v", h=4)
            b_ct = S.buf("ct"); b_tc = S.buf("tmpc")
            Vw = [view(T0, 4128 + i * 1032, 1032, BF16).rearrange("p (h v) -> p h v", h=4) for i in range(2)]
            b_vw = S.bufs(2, "vw")
            PT = [view(T0, 6192 + i * 256, 256, BF16) for i in range(2)]
            b_pt = S.bufs(2, "pt")
            hbuf = [view(T0, 6704 + i * 512, 512, F32) for i in range(4)]
            b_hb = S.bufs(4, "hb")
            hn = [view(T0, 8752 + i * 256, 256, BF16) for i in range(2)]
            b_hn = S.bufs(2, "hn")
            junk2 = view(T0, 9264, 256, BF16)
            assert T0 + 9520 <= A2 + A2_B, (T0 + 9520 - A2 - A2_B)

            handoff = S.handoff
            handoff([b_g, b_sm] + b_kt + b_cl + [b_ct, b_tc] + b_vw + b_pt + b_hb + b_hn, b_stage + [b_acc, b_junk, b_xh] + b_xnb)
            GRv = GR.rearrange("p (m c) -> p m c", c=8)

            def bc_m(ap4):
                return bass.AP(ap4.tensor, ap4.offset, [list(ap4.ap[0]), [0, 16], list(ap4.ap[1])])

            def bc_last(ap, n):
                return bass.AP(ap.tensor, ap.offset, [list(x) for x in ap.ap] + [[0, n]])

            def flat(a):
                return a.rearrange("p m h -> p (m h)")
            LN_S = -0.5 * float(np.log(128.0))
            S.op("pool", lambda e: e.memset(view(MS, 0, 4608, F32), 0.0), writes=[b_g, b_sm])
            S.op("dve", lambda e: e.tensor_tensor(out=ig, in0=GRv[:, :, 0:4], in1=bc_m(bif[:, 0:4]), op=ALU.add), reads=[b_gr, b_prm], writes=[b_g])
            S.op("dve", lambda e: e.tensor_tensor(out=t1, in0=GRv[:, :, 4:8], in1=bc_m(bif[:, 4:8]), op=ALU.add), reads=[b_gr, b_prm], writes=[b_g])
            S.op("act", lambda e: e.activation(out=t1, in_=t1, func=AF.Exp, scale=-1.0), reads=[b_g], writes=[b_g])
            S.op("act", lambda e: e.activation(out=nlf, in_=t1, func=AF.Ln, bias=1.0), reads=[b_g], writes=[b_g])
            bankA, bbA = next_bank()
            S.op("pe", lambda e: e.matmul(bankA[:, 0:64], lhsT=tri_f, rhs=flat(nlf), start=True, stop=True), reads=[b_g, b_cst], writes=[bbA])
            bankB, bbB = next_bank()
            S.op("pe", lambda e: e.matmul(bankB[:, 0:64], lhsT=ones_f, rhs=flat(nlf), start=True, stop=True), reads=[b_g, b_cst], writes=[bbB])
            bA = bankA[:, 0:64].rearrange("p (m h) -> p m h", h=4)
            bB = bankB[:, 0:64].rearrange("p (m h) -> p m h", h=4)
            for h in range(4):
                S.op("dve", lambda e: e.tensor_tensor_scan(out=incl[:, :, h], data0=ones_f[:, 0:16], data1=bB[:, :, h], initial=0.0,
                                                           op0=ALU.mult, op1=ALU.add), reads=[bbB, b_cst, b_g], writes=[b_g])
            S.op("dve", lambda e: e.tensor_tensor(out=t1, in0=incl, in1=bB, op=ALU.subtract), reads=[b_g, bbB], writes=[b_g])
            S.op("dve", lambda e: e.tensor_tensor(out=nF, in0=t1, in1=bA, op=ALU.add), reads=[b_g, bbA], writes=[b_g])
            S.op("dve", lambda e: e.tensor_tensor(out=g, in0=ig, in1=nF, op=ALU.add), reads=[b_g], writes=[b_g])
            bankT, bbT = next_bank()
            S.op("pe", lambda e: e.transpose(bankT[0:64, 0:128], flat(g), ident_f), reads=[b_g, b_cst], writes=[bbT])
            S.op("dve", lambda e: e.tensor_reduce(out=gcol[0:64, :], in_=bankT[0:64, 0:128], axis=AX.X, op=ALU.max), reads=[bbT], writes=[b_sm])
            S.op("dve", lambda e: e.tensor_copy(out=Gb[0:64, :], in_=bc_last(gcol[0:64, 0:1], 128)[:, 0, :]), reads=[b_sm], writes=[b_sm])
            bankG, bbG = next_bank()
            S.op("pe", lambda e: e.matmul(bankG[:, 0:64], lhsT=Gb[0:64, :], rhs=ident_f[0:64, 0:64], start=True, stop=True), reads=[b_sm, b_cst], writes=[bbG])
            S.op("dve", lambda e: e.tensor_copy(out=flat(Gm), in_=bankG[:, 0:64]), reads=[bbG], writes=[b_g])
            for h in range(4):
                S.op("dve", lambda e: e.tensor_tensor_scan(out=R0[:, :, h], data0=Gm[:, :, h], data1=Gm[:, :, h], initial=-1e30,
                                                           op0=ALU.max, op1=ALU.max), reads=[b_g], writes=[b_g])
            S.op("dve", lambda e: e.tensor_tensor(out=t1, in0=g, in1=R0, op=ALU.subtract), reads=[b_g], writes=[b_g])
            S.op("act", lambda e: e.activation(out=w0, in_=t1, func=AF.Exp, bias=LN_S), reads=[b_g], writes=[b_g])
            S.op("dve", lambda e: e.tensor_tensor(out=t1[:, 1:16, :], in0=R0[:, 0:15, :], in1=R0[:, 1:16, :], op=ALU.subtract), reads=[b_g], writes=[b_g])
            S.op("act", lambda e: e.activation(out=dl[:, 1:16, :], in_=t1[:, 1:16, :], func=AF.Exp), reads=[b_g], writes=[b_g])
            for m in range(16):
                cs = slice(m * 128, (m + 1) * 128)
                bank, bb = next_bank()
                pb = bank[:, 0:256].bitcast(BF16)
                for h in range(4):
                    S.op("pe", lambda e: e.transpose(pb[:, h * 128:(h + 1) * 128], QK[:, 4 + h, cs], ident_b), inc=(h == 3),
                         reads=[b_qk[4 + h], b_cst], writes=[bb])
                S.op("act", lambda e: e.activation(out=kTok[:, m, :], in_=pb, func=AF.Copy), reads=[bb], writes=[b_kt[m]])
                vw = Vw[m % 2]; bv = b_vw[m % 2]
                S.op("dve", lambda e: e.tensor_tensor(out=vw, in0=VA[:, m, :, :], in1=bc_last(w0[:, m, :], 129), op=ALU.mult),
                     reads=[b_va[m], b_g], writes=[bv])
                for h in range(4):
                    bank, bb = next_bank()
                    S.op("pe", lambda e: e.matmul(bank[:, 0:129], lhsT=kTok[:, m, h * 128:(h + 1) * 128], rhs=vw[:, h, :], start=True, stop=True),
                         reads=[b_kt[m], bv], writes=[bb])
                    if h % 2 == 0:
                        S.op("act", lambda e: e.activation(out=CL[:, m * 4 + h, :], in_=bank[:, 0:129], func=AF.Copy), reads=[bb], writes=[b_cl[m]])
                    else:
                        S.op("dve", lambda e: e.tensor_copy(out=CL[:, m * 4 + h, :], in_=bank[:, 0:129]), reads=[bb], writes=[b_cl[m]])
                if m == 0:
                    S.op("dve", lambda e: e.tensor_copy(out=Ct, in_=CL[:, 0:4, :]), reads=[b_cl[0]], writes=[b_ct])
                else:
                    S.op("dve", lambda e: e.tensor_tensor(out=Ct, in0=Ct, in1=bc_last(dl[:, m, :], 129), op=ALU.mult), reads=[b_ct, b_g], writes=[b_ct])
                    S.op("dve", lambda e: e.tensor_tensor(out=Ct, in0=Ct, in1=CL[:, m * 4:m * 4 + 4, :], op=ALU.add), reads=[b_ct, b_cl[m]], writes=[b_ct])
            ml_extra = []
            xfer("e_gates", view(MS, 0, 4608, F32), [b_g, b_sm])
            xfer("e_cl", view(MS, 4608 + 16512, 64 * 129 * 4, F32), b_cl)
            if IMP:
                S.mute = False
            mst = sm[:, 20:24]
            exq = S.dma_sem(f"exq{l}")
            if mode == "A":
                S.dma("sp", oq, dr["summ"][:, 32:548], Ct.rearrange("p h v -> p (h v)"), reads=[b_ct], writes=[b_out])
                S.dma("sp", oq, dr["summ"][:, 548:552], R0[:, 15, :], reads=[b_g], writes=[b_out])
                S.dma("sp", oq, dr["summ"][:, 552:556], incl[:, 15, :], reads=[b_g], writes=[b_out])
            else:
                sa = dr["summ_all"]
                small = Gb.rearrange("p (j c) -> p j c", j=8)[:, :, 0:8]
                S.dma("sp", exq, small, sa[:, :, 548:556].rearrange("j p c -> p j c"), writes=[b_sm])
                S.seal(exq, [b_sm])
                cm = sm[:, 20:24]; mx = sm[:, 24:28]; ta = sm[:, 28:32]; tb = sm[:, 32:36]; r0j = sm[:, 36:40]; nfj = sm[:, 40:44]; tq = sm[:, 44:48]
                S.op("dve", lambda e: e.memset(tmpC, 0.0), reads=[b_tc], writes=[b_tc])
                S.op("dve", lambda e: e.memset(cm, 0.0), reads=[b_sm], writes=[b_sm])
                cq = [S.dma_sem(f"cq{l}_{i}") for i in range(2)]
                Cj = [Ct, view(T0, 4128, 2064, F32).rearrange("p (h v) -> p h v", h=4)]
                b_cj = [b_ct, S.buf("cj1")]
                S.handoff([b_cj[1]], b_vw)
                for j in range(8):
                    cj = Cj[j % 2]; bcj = b_cj[j % 2]
                    S.dma("sp", cq[j % 2], cj.rearrange("p h v -> p (h v)"), sa[j, :, 32:548], writes=[bcj])
                    S.op("dve", lambda e: e.tensor_scalar(out=r0j, in0=small[:, j, 0:4], scalar1=pred[:, j:j + 1], scalar2=pmask[:, j:j + 1],
                                                          op0=ALU.mult, op1=ALU.add), reads=[b_sm, b_cst], writes=[b_sm])
                    S.op("dve", lambda e: e.tensor_scalar(out=nfj, in0=small[:, j, 4:8], scalar1=pred[:, j:j + 1], scalar2=None, op0=ALU.mult),
                         reads=[b_sm, b_cst], writes=[b_sm])
                    S.op("dve", lambda e: e.tensor_tensor(out=mx, in0=cm, in1=r0j, op=ALU.max), reads=[b_sm], writes=[b_sm])
                    S.op("dve", lambda e: e.tensor_tensor(out=tq, in0=cm, in1=mx, op=ALU.subtract), reads=[b_sm], writes=[b_sm])
                    S.op("act", lambda e: e.activation(out=ta, in_=tq, func=AF.Exp), reads=[b_sm], writes=[b_sm])
                    S.op("dve", lambda e: e.tensor_tensor(out=tq, in0=r0j, in1=mx, op=ALU.subtract), reads=[b_sm], writes=[b_sm])
                    S.op("act", lambda e: e.activation(out=tb, in_=tq, func=AF.Exp), reads=[b_sm], writes=[b_sm])
                    S.op("dve", lambda e: e.tensor_tensor(out=tmpC, in0=tmpC, in1=bc_last(ta, 129), op=ALU.mult), reads=[b_tc, b_sm], writes=[b_tc])
                    S.op("dve", lambda e: e.tensor_tensor(out=cj, in0=cj, in1=bc_last(tb, 129), op=ALU.mult), reads=[bcj, b_sm], writes=[bcj])
                    S.op("dve", lambda e: e.tensor_tensor(out=tmpC, in0=tmpC, in1=cj, op=ALU.add), reads=[b_tc, bcj], writes=[b_tc])
                    S.op("dve", lambda e: e.tensor_tensor(out=cm, in0=mx, in1=nfj, op=ALU.subtract), reads=[b_sm], writes=[b_sm])
            if mode != "A":
                S.op("dve", lambda e: e.tensor_tensor(out=Rr, in0=R0, in1=bc_m(mst), op=ALU.max), reads=[b_g, b_sm], writes=[b_g])
                S.op("dve", lambda e: e.tensor_tensor(out=t1, in0=g, in1=Rr, op=ALU.subtract), reads=[b_g], writes=[b_g])
                S.op("act", lambda e: e.activation(out=wv, in_=t1, func=AF.Exp, bias=LN_S), reads=[b_g], writes=[b_g])
                S.op("dve", lambda e: e.tensor_tensor(out=t1, in0=nF, in1=Rr, op=ALU.subtract), reads=[b_g], writes=[b_g])
                S.op("act", lambda e: e.activation(out=clamp, in_=t1, func=AF.Exp), reads=[b_g], writes=[b_g])
                S.op("dve", lambda e: e.tensor_tensor(out=t1, in0=R0, in1=Rr, op=ALU.subtract), reads=[b_g], writes=[b_g])
                S.op("act", lambda e: e.activation(out=ep, in_=t1, func=AF.Exp), reads=[b_g], writes=[b_g])
                S.op("dve", lambda e: e.tensor_tensor(out=t1[:, 1:16, :], in0=Rr[:, 0:15, :], in1=Rr[:, 1:16, :], op=ALU.subtract), reads=[b_g], writes=[b_g])
                S.op("dve", lambda e: e.tensor_tensor(out=t1[:, 0, :], in0=mst, in1=Rr[:, 0, :], op=ALU.subtract), reads=[b_g, b_sm], writes=[b_g])
                S.op("act", lambda e: e.activation(out=dl, in_=t1, func=AF.Exp), reads=[b_g], writes=[b_g])
                S.op("dve", lambda e: e.tensor_copy(out=Ct, in_=tmpC), reads=[b_tc], writes=[b_ct])
                b_cin = b_kt
                for m in range(16):
                    S.op("dve", lambda e: e.tensor_tensor(out=tmpC, in0=Ct, in1=bc_last(dl[:, m, :], 129), op=ALU.mult), reads=[b_ct, b_g], writes=[b_tc])
                    S.op("act", lambda e: e.activation(out=Cin[:, m * 4:m * 4 + 4, :], in_=tmpC, func=AF.Copy), reads=[b_tc], writes=b_cin)
                    S.op("dve", lambda e: e.tensor_tensor(out=Ct, in0=CL[:, m * 4:m * 4 + 4, :], in1=bc_last(ep[:, m, :], 129), op=ALU.mult),
                         reads=[b_cl[m], b_g], writes=[b_ct])
                    S.op("dve", lambda e: e.tensor_tensor(out=Ct, in0=Ct, in1=tmpC, op=ALU.add), reads=[b_ct, b_tc], writes=[b_ct])
                PT4 = [view(T0, i * 256, 256, BF16) for i in range(4)]
                hn4 = [view(T0, 1024 + i * 1024, 1024, BF16).rearrange("p (h t) -> p h t", h=4) for i in range(2)]
                hbA = view(T0, 6704, 2048, F32).rearrange("p (h t) -> p h t", h=4)
                hbB = view(T0, 3072, 2048, F32).rearrange("p (h t) -> p h t", h=4)
                hb4 = [hbA, hbB]
                b_pt4 = S.bufs(4, "pt4"); b_hn4 = S.bufs(2, "hn4"); b_hb4 = [S.bufs(4, "hbA"), S.bufs(4, "hbB")]
                b_j2 = S.buf("j2")
                b_ep = [S.buf("ep0"), S.buf("ep1")]
                S.handoff(b_pt4 + b_hn4 + b_hb4[0] + b_hb4[1] + [b_j2] + b_ep, [b_ct, b_tc, b_sm] + b_vw + b_pt + b_hb + b_hn + ([b_cj[1]] if mode != "A" else []))
                ml_extra += b_pt4 + b_hn4 + b_hb4[0] + b_hb4[1] + [b_j2] + b_ep
                smx = [sm[:, 4:20], sm[:, 48:64]]
                for m in range(16):
                    cs = slice(m * 128, (m + 1) * 128)
                    par = m % 2
                    dm_, rec_, ss_, rs_ = smx[par][:, 0:4], smx[par][:, 4:8], smx[par][:, 8:12], smx[par][:, 12:16]
                    be = b_ep[par]
                    bankS, bbS = next_bank()
                    for h in range(4):
                        S.op("pe", lambda e: e.matmul(bankS[:, h * 128:(h + 1) * 128], lhsT=QK[:, 4 + h, cs], rhs=QK[:, h, cs], start=True, stop=True),
                             inc=(h == 3), reads=[b_qk[4 + h], b_qk[h]], writes=[bbS])
                    for h in range(4):
                        S.op("dve", lambda e: e.scalar_tensor_tensor(out=PT4[h], in0=bankS[:, h * 128:(h + 1) * 128], scalar=wv[:, m, h:h + 1], in1=tri_f,
                                                                     op0=ALU.mult, op1=ALU.mult), reads=[bbS, b_g, b_cst], writes=[b_pt4[h]])
                    bN = []
                    for j in range(2):
                        bankN, bbN = next_bank()
                        bN.append((bankN, bbN))
                        for hh_ in range(2):
                            h = 2 * j + hh_
                            co = hh_ * 129
                            S.op("pe", lambda e: e.matmul(bankN[:, co:co + 129], lhsT=PT4[h], rhs=VA[:, m, h, :], start=True, stop=False), inc=False,
                                 reads=[b_pt4[h], b_va[m]], writes=[bbN])
                            S.op("pe", lambda e: e.matmul(bankN[:, co:co + 129], lhsT=QK[:, h, cs], rhs=Cin[:, m * 4 + h, :], start=False, stop=True),
                                 inc=(hh_ == 1), reads=[b_qk[h]] + b_cin, writes=[bbN])
                        den = bankN[:, 0:258].rearrange("p (h v) -> p h v", v=129)[:, :, 128]
                        S.op("act", lambda e: e.activation(out=dm_[:, 2 * j:2 * j + 2], in_=den, func=AF.Abs), reads=[bbN], writes=[be])
                        S.op("dve", lambda e: e.tensor_tensor(out=dm_[:, 2 * j:2 * j + 2], in0=dm_[:, 2 * j:2 * j + 2], in1=clamp[:, m, 2 * j:2 * j + 2], op=ALU.max),
                             reads=[be, b_g], writes=[be])
                    S.op("dve", lambda e: e.reciprocal(out=rec_, in_=dm_), reads=[be], writes=[be])
                    for h in range(4):
                        bankN, bbN = bN[h // 2]
                        co = (h % 2) * 129
                        S.op("dve", lambda e: e.scalar_tensor_tensor(out=hb4[par][:, h, :], in0=bankN[:, co:co + 128], scalar=rec_[:, h:h + 1],
                                                                     in1=OG[:, m, h * 128:(h + 1) * 128], op0=ALU.mult, op1=ALU.mult),
                             reads=[bbN, be, b_og[m]], writes=[b_hb4[par][h]])
                        S.op("act", lambda e: e.activation(out=junk2, in_=hb4[par][:, h, :], func=AF.Square, accum_out=ss_[:, h:h + 1]),
                             reads=[b_hb4[par][h]], writes=[b_j2, be])
                    S.op("dve", lambda e: e.tensor_scalar(out=rs_, in0=ss_, scalar1=1.0 / 128, scalar2=EPS, op0=ALU.mult, op1=ALU.add), reads=[be], writes=[be])
                    S.op("act", lambda e: e.activation(out=rs_, in_=rs_, func=AF.Sqrt), reads=[be], writes=[be])
                    S.op("dve", lambda e: e.reciprocal(out=rs_, in_=rs_), reads=[be], writes=[be])
                    S.op("dve", lambda e: e.tensor_tensor(out=hn4[par], in0=hb4[par], in1=bc_last(rs_, 128), op=ALU.mult),
                         reads=b_hb4[par] + [be], writes=[b_hn4[par]])
                    bankO, bbO = next_bank()
                    po = bankO[:, 0:256].bitcast(BF16).rearrange("p (h t) -> p h t", h=4)
                    for h in range(4):
                        S.op("pe", lambda e: e.transpose(po[:, h, :], hn4[par][:, h, :], ident_b), inc=(h == 3), reads=[b_hn4[par], b_cst], writes=[bbO])
                    S.op("dve", lambda e: e.tensor_tensor(out=YC[:, 4:8, cs], in0=po, in1=bc_last(mlg, 128), op=ALU.mult), reads=[bbO, b_prm], writes=[b_yc[m]])

            if cfg.get("debug"):
                S.dma("sp", dbq, dr["dbg_g"].rearrange("p (a c) -> p a c", a=16), view(MS, 0, 4096, F32).rearrange("p (a c) -> p a c", a=16), reads=[b_g, b_sm], writes=[b_dbg])

            if stop == "ml":
                quiesce(); return
            TCH = 16; NC_ = 128
            WZ = view(A0, 0, 32 * KB, BF16).rearrange("p (q i x n) -> p q i x n", q=4, i=16, x=2)
            BD = view(A0, 32 * KB, 16 * KB, BF16).rearrange("p (q j n) -> p q j n", q=4, j=16)
            CP = view(A2, 0, 34816, BF16).rearrange("p (j r x n) -> p j r x n", j=17, r=16, x=2)
            EC = view(A2, 34816, 8192, F32).rearrange("p (r c) -> p r c", r=16)
            ES = view(A2, 34816 + 8192, 8192, F32).rearrange("p (r c) -> p r c", r=16)
            ZL = view(A2, 51200, 16384, F32).rearrange("p (r x c) -> p r x c", r=16, x=2)
            ZS = view(A2, 67584, 8256, BF16).rearrange("p (r x c) -> p r x c", r=16, x=2)
            SP_ = A2 + 76032
            PW = view(SP_, 0, 2176, F32).rearrange("p (j x r) -> p j x r", j=17, x=2)
            def sc(i):
                return view(SP_, 2176 + i * 64, 64, F32)
            assert SP_ + 2176 + 30 * 64 <= A2 + A2_B
            WW = view(A0, 0, 16384, F32).rearrange("p (r x c) -> p r x c", r=16, x=2)
            ZG = view(A2, 51200, 16384, BF16).rearrange("p (q i c) -> p q i c", q=4, i=16)
            GT = A0 + 16 * KB
            GEN = A2 + 51200
            b_s5 = S.buf("s5gen")
            b_wz = S.bufs(4, "wz"); b_bd = S.bufs(4, "bd"); b_cp = S.buf("cp"); b_tab = S.buf("tab")
            b_zl = S.bufs(16, "zl"); b_ww = S.bufs(16, "ww"); b_zs = S.bufs(16, "zs"); b_zg = S.bufs(4, "zg")
            olds = [b_g, b_sm] + b_kt + b_cl + [b_ct, b_tc] + b_vw + b_pt + b_hb + b_hn + b_va + [b_gr] + b_qk + b_og + ml_extra
            handoff([b_s5, b_cp, b_tab] + b_wz + b_bd + b_zl + b_ww + b_zs + b_zg, olds)
            s5q = S.dma_sem(f"s5q{l}")
            (s_are, s_aim, s_dt, s_mag, s_th, s_t, s_sin, s_cos, s_abr, s_abi, s_den, s_zr, s_sre, s_sim, s_t2, s_magL,
             s_l128r, s_l128i, s_t3, s_m128) = [sc(i) for i in range(20)]
            zend = sc(20)[:, 0:16]
            zend = view(SP_, 2176 + 20 * 64, 128, F32).rearrange("p (r x) -> p r x", x=2)
            sst = view(SP_, 2176 + 22 * 64, 128, F32).rearrange("p (r x) -> p r x", x=2)

            def TT(out, in0, in1, op, rd=(), wr=None, eng="dve"):
                S.op(eng, lambda e: e.tensor_tensor(out=out, in0=in0, in1=in1, op=op), reads=[b_s5] + list(rd), writes=[b_s5] if wr is None else wr)

            def TS(out, in0, s1, s2, op0, op1=None, rd=(), wr=None):
                if op1 is None:
                    S.op("dve", lambda e: e.tensor_scalar(out=out, in0=in0, scalar1=s1, scalar2=None, op0=op0), reads=[b_s5] + list(rd), writes=[b_s5] if wr is None else wr)
                else:
                    S.op("dve", lambda e: e.tensor_scalar(out=out, in0=in0, scalar1=s1, scalar2=s2, op0=op0, op1=op1), reads=[b_s5] + list(rd), writes=[b_s5] if wr is None else wr)

            def AC(out, in_, func, rd=(), wr=None, **kw):
                S.op("act", lambda e: e.activation(out=out, in_=in_, func=func, **kw), reads=[b_s5] + list(rd), writes=[b_s5] if wr is None else wr)

            def cmul(o_r, o_i, a_r, a_i, b_r, b_i, t1_, t2_, rd=(), wr=None, neg_im=False):
                TT(t1_, a_r, b_r, ALU.mult, rd); TT(t2_, a_i, b_i, ALU.mult, rd)
                TT(o_r, t1_, t2_, ALU.subtract, rd, wr)
                TT(t1_, a_r, b_i, ALU.mult, rd); TT(t2_, a_i, b_r, ALU.mult, rd)
                if neg_im:
                    TT(t1_, t1_, t2_, ALU.add, rd)
                    TS(o_i, t1_, -1.0, None, ALU.mult, rd=rd, wr=wr)
                else:
                    TT(o_i, t1_, t2_, ALU.add, rd, wr)

            if IMP:
                S.mute = True
            S.op("pool", lambda e: e.memset(view(SP_, 0, 4864, F32), 0.0), writes=[b_s5])
            araw = view(GEN, 0, 1024, F32)
            S.dma("sp", s5q, araw[0:16, 0:128], dr["s5_a_re"][l].rearrange("(r gl) n -> r (gl n)", gl=2), writes=[b_s5])
            S.dma("sp", s5q, araw[0:16, 128:256], dr["s5_a_im"][l].rearrange("(r gl) n -> r (gl n)", gl=2), writes=[b_s5])
            ldt = dr["s5_log_dt"][l]
            for gl in range(2):
                S.dma("sp", s5q, s_dt[gl * 64:(gl + 1) * 64, :], bass.AP(ldt.tensor, ldt.offset + gl, [[0, 64], [2, 16]]), writes=[b_s5])
            Bsm = [view(GEN, 1024 + x * 1024, 1024, F32).rearrange("p (r c) -> p r c", r=16) for x in range(2)]
            for x, nm in enumerate(["s5_b_re", "s5_b_im"]):
                bsrc = dr[nm][l]
                for gl in range(2):
                    S.dma("sp", s5q, Bsm[x][gl * 64:(gl + 1) * 64, :, :],
                          bass.AP(bsrc.tensor, bsrc.offset + gl * 1024, [[16, 64], [2048, 16], [1, 16]]), writes=[b_s5])
            Craw = [view(GEN, 3072 + x * 1024, 1024, F32).rearrange("p (q n) -> p q n", q=4) for x in range(2)]
            for x, nm in enumerate(["s5_c_re", "s5_c_im"]):
                csrc = dr[nm][l]
                S.dma("sp", s5q, Craw[x], bass.AP(csrc.tensor, csrc.offset, [[64, 128], [8192, 4], [1, 64]]), writes=[b_s5])
            S.seal(s5q, [b_s5])
            bank, bb = next_bank()
            S.op("pe", lambda e: e.transpose(bank[:, 0:16], araw[0:16, 0:128], ident_f[0:16, 0:16]), reads=[b_s5, b_cst], writes=[bb])
            S.op("pe", lambda e: e.transpose(bank[:, 16:32], araw[0:16, 128:256], ident_f[0:16, 0:16]), reads=[b_s5, b_cst], writes=[bb])
            S.op("dve", lambda e: e.tensor_copy(out=s_are, in_=bank[:, 0:16]), reads=[bb], writes=[b_s5])
            S.op("dve", lambda e: e.tensor_copy(out=s_aim, in_=bank[:, 16:32]), reads=[bb], writes=[b_s5])
            PI = float(np.pi)
            AC(s_dt, s_dt, AF.Exp)
            TT(s_t, s_are, s_dt, ALU.mult)
            AC(s_mag, s_t, AF.Exp)
            AC(s_magL, s_t, AF.Exp, scale=float(TCH))
            AC(s_m128, s_t, AF.Exp, scale=float(TCH * NC_))
            TT(s_th, s_aim, s_dt, ALU.mult)
            for thr in (1.0, 3.0, 5.0, 7.0):
                TS(s_t, s_th, thr * PI, -2.0 * PI, ALU.is_gt, ALU.mult)
                if thr == 1.0:
                    TT(s_t2, s_th, s_t, ALU.add)
                else:
                    TT(s_t2, s_t2, s_t, ALU.add)
            AC(s_sin, s_t2, AF.Sin)
            TS(s_t3, s_t2, 0.5 * PI, None, ALU.add)
            TS(s_t, s_t3, PI, -2.0 * PI, ALU.is_gt, ALU.mult)
            TT(s_t3, s_t3, s_t, ALU.add)
            AC(s_cos, s_t3, AF.Sin)
            TT(s_abr, s_mag, s_cos, ALU.mult); TT(s_abi, s_mag, s_sin, ALU.mult)
            TT(s_t, s_are, s_are, ALU.mult); TT(s_t2, s_aim, s_aim, ALU.mult); TT(s_den, s_t, s_t2, ALU.add)
            S.op("dve", lambda e: e.reciprocal(out=s_den, in_=s_den), reads=[b_s5], writes=[b_s5])
            TS(s_zr, s_abr, -1.0, None, ALU.add)
            TT(s_t, s_zr, s_are, ALU.mult); TT(s_t2, s_abi, s_aim, ALU.mult); TT(s_t, s_t, s_t2, ALU.add); TT(s_sre, s_t, s_den, ALU.mult)
            TT(s_t, s_abi, s_are, ALU.mult); TT(s_t2, s_zr, s_aim, ALU.mult); TT(s_t, s_t, s_t2, ALU.subtract); TT(s_sim, s_t, s_den, ALU.mult)
            S.op("dve", lambda e: e.memset(PW[:, 0, 0, :], 1.0), reads=[b_s5], writes=[b_s5])
            S.op("dve", lambda e: e.memset(PW[:, 0, 1, :], 0.0), reads=[b_s5], writes=[b_s5])
            S.op("dve", lambda e: e.tensor_copy(out=PW[:, 1, 0, :], in_=s_abr), reads=[b_s5], writes=[b_s5])
            S.op("dve", lambda e: e.tensor_copy(out=PW[:, 1, 1, :], in_=s_abi), reads=[b_s5], writes=[b_s5])
            pt1 = view(GEN, 5120, 1024, F32).rearrange("p (j r) -> p j r", r=16)
            pt2 = view(GEN, 6144, 1024, F32).rearrange("p (j r) -> p j r", r=16)
            kk_ = 1
            while kk_ < 16:
                def bj(a):
                    return bass.AP(a.tensor, a.offset, [list(a.ap[0]), [0, kk_], list(a.ap[1])])
                cmul(PW[:, kk_ + 1:2 * kk_ + 1, 0, :], PW[:, kk_ + 1:2 * kk_ + 1, 1, :], PW[:, 1:kk_ + 1, 0, :], PW[:, 1:kk_ + 1, 1, :],
                     bj(PW[:, kk_, 0, :]), bj(PW[:, kk_, 1, :]), pt1[:, 0:kk_, :], pt2[:, 0:kk_, :])
                kk_ *= 2
            S.op("dve", lambda e: e.reciprocal(out=s_t, in_=s_magL), reads=[b_s5], writes=[b_s5])
            TT(EC[:, :, 0], PW[:, 16, 0, :], s_t, ALU.mult, wr=[b_s5, b_tab]); TT(ES[:, :, 0], PW[:, 16, 1, :], s_t, ALU.mult, wr=[b_s5, b_tab])
            et1 = view(GEN, 7168, 4096, F32).rearrange("p (r c) -> p r c", r=16)
            et2 = view(GEN, 11264, 4096, F32).rearrange("p (r c) -> p r c", r=16)
            kk_ = 1
            while kk_ < NC_:
                cmul(EC[:, :, kk_:2 * kk_], ES[:, :, kk_:2 * kk_], EC[:, :, 0:kk_], ES[:, :, 0:kk_],
                     bc_last(EC[:, :, kk_ - 1], kk_), bc_last(ES[:, :, kk_ - 1], kk_), et1[:, :, 0:kk_], et2[:, :, 0:kk_], rd=[b_tab], wr=[b_s5, b_tab])
                kk_ *= 2
            TT(s_l128r, EC[:, :, NC_ - 1], s_m128, ALU.mult, rd=[b_tab]); TT(s_l128i, ES[:, :, NC_ - 1], s_m128, ALU.mult, rd=[b_tab])
            Cin_ = [view(GEN, 5120 + x * 2048, 2048, F32).rearrange("p (q n) -> p q n", q=4) for x in range(2)]
            Cp = [view(GEN, 9216 + x * 2048, 2048, F32).rearrange("p (r n) -> p r n", r=16) for x in range(2)]
            ct1 = view(GEN, 13312, 2048, F32).rearrange("p (r n) -> p r n", r=16)
            ct2 = view(A0, 0, 2048, F32).rearrange("p (r n) -> p r n", r=16)
            for x in range(2):
                TS(Cin_[x][:, :, 0:64], Craw[x], par01[:, 0:1], None, ALU.mult, rd=[b_cst])
                TS(Cin_[x][:, :, 64:128], Craw[x], par01[:, 1:2], None, ALU.mult, rd=[b_cst])
                bank, bb = next_bank()
                for q in range(4):
                    S.op("pe", lambda e: e.transpose(bank[:, q * 128:(q + 1) * 128], Cin_[x][:, q, :], ident_f), inc=(q == 3), reads=[b_s5, b_cst], writes=[bb])
                S.op("dve", lambda e: e.tensor_copy(out=Cp[x].rearrange("p r n -> p (r n)"), in_=bank[:, :]), reads=[bb], writes=[b_s5])
            for j in range(17):
                pr = bc_last(PW[:, j, 0, :], 32); pi_ = bc_last(PW[:, j, 1, :], 32)
                TT(ct1, Cp[0], pr, ALU.mult); TT(ct2, Cp[1], pi_, ALU.mult, rd=b_wz, wr=[b_s5] + b_wz)
                TT(CP[:, j, :, 0, :], ct1, ct2, ALU.subtract, wr=[b_s5, b_cp])
                TT(ct1, Cp[0], pi_, ALU.mult); TT(ct2, Cp[1], pr, ALU.mult, rd=b_wz, wr=[b_s5] + b_wz)
                S.op("dve", lambda e: e.scalar_tensor_tensor(out=CP[:, j, :, 1, :], in0=ct1, scalar=-1.0, in1=ct2, op0=ALU.mult, op1=ALU.subtract),
                     reads=[b_s5], writes=[b_s5, b_cp])

            BB = [view(GEN, 5120 + x * 2048, 2048, F32).rearrange("p (r n) -> p r n", r=16) for x in range(2)]
            BBb = [view(GEN, 9216 + x * 1024, 1024, BF16).rearrange("p (r n) -> p r n", r=16) for x in range(2)]
            bt1 = view(GEN, 11264, 1024, F32).rearrange("p (r c) -> p r c", r=16)
            bt2 = view(GEN, 12288, 1024, F32).rearrange("p (r c) -> p r c", r=16)
            for x in range(2):
                S.op("dve", lambda e: e.memset(BB[x], 0.0), reads=[b_s5], writes=[b_s5])
            sre_b = bc_last(s_sre, 16); sim_b = bc_last(s_sim, 16)
            TT(bt1, Bsm[0], sre_b, ALU.mult); TT(bt2, Bsm[1], sim_b, ALU.mult)
            for gl in range(2):
                ps_ = slice(gl * 64, (gl + 1) * 64)
                TT(BB[0][ps_, :, gl * 16:(gl + 1) * 16], bt1[ps_], bt2[ps_], ALU.subtract)
            TT(bt1, Bsm[1], sre_b, ALU.mult); TT(bt2, Bsm[0], sim_b, ALU.mult)
            for gl in range(2):
                ps_ = slice(gl * 64, (gl + 1) * 64)
                TT(BB[1][ps_, :, gl * 16:(gl + 1) * 16], bt1[ps_], bt2[ps_], ALU.add)
            for x in range(2):
                S.op("dve", lambda e: e.tensor_copy(out=BBb[x], in_=BB[x]), reads=[b_s5], writes=[b_s5])
            bdt = view(GEN, 13312, 512, F32)
            for j in range(16):
                bank, bb = next_bank()
                for q in range(4):
                    for x in range(2):
                        S.op("pe", lambda e: e.matmul(bank[:, q * 128:(q + 1) * 128], lhsT=BBb[x][:, 4 * q:4 * q + 4, :].rearrange("p r n -> p (r n)"),
                                                      rhs=CP[:, j, 4 * q:4 * q + 4, x, :], start=(x == 0), stop=(x == 1)), inc=(q == 3 and x == 1),
                             reads=[b_s5, b_cp], writes=[bb])
                if j == 0:
                    for q in range(4):
                        S.op("dve", lambda e: e.tensor_tensor(out=bdt, in0=bank[:, q * 128:(q + 1) * 128], in1=bdm, op=ALU.mult), reads=[bb, b_cst, b_s5], writes=[b_s5])
                        S.op("dve", lambda e: e.scalar_tensor_tensor(out=BD[:, q, 0, :], in0=ident_f, scalar=dcol[:, q:q + 1], in1=bdt, op0=ALU.mult, op1=ALU.add),
                             reads=[b_s5, b_cst, b_prm], writes=[b_bd[q]])
                else:
                    bdm_b = bass.AP(bdm.tensor, bdm.offset, [list(bdm.ap[0]), [0, 4], list(bdm.ap[1])])
                    S.op("dve", lambda e: e.tensor_tensor(out=BD[:, :, j, :], in0=bank[:, :].rearrange("p (q n) -> p q n", q=4), in1=bdm_b, op=ALU.mult),
                         reads=[bb, b_cst], writes=b_bd)
            mt1 = view(GEN, 13824, 2048, F32).rearrange("p (r n) -> p r n", r=16)
            mt2 = view(GEN, 1024, 2048, F32).rearrange("p (r n) -> p r n", r=16)
            MB = [view(GEN, 3072 + x * 1024, 1024, BF16).rearrange("p (r n) -> p r n", r=16) for x in range(2)]
            for i in range(16):
                j = 15 - i
                pr = bc_last(PW[:, j, 0, :], 32); pi_ = bc_last(PW[:, j, 1, :], 32)
                TT(mt1, BB[0], pr, ALU.mult); TT(mt2, BB[1], pi_, ALU.mult); TT(MB[0], mt1, mt2, ALU.subtract)
                TT(mt1, BB[0], pi_, ALU.mult); TT(mt2, BB[1], pr, ALU.mult); TT(MB[1], mt1, mt2, ALU.add)
                bank, bb = next_bank()
                pb = bank[:, :].bitcast(BF16).rearrange("p (q x n) -> p q x n", q=4, x=2)
                for q in range(4):
                    for x in range(2):
                        S.op("pe", lambda e: e.transpose(pb[:, q, x, :], MB[x][:, 4 * q:4 * q + 4, :].rearrange("p r n -> p (r n)"), ident_b),
                             inc=(q == 3 and x == 1), reads=[b_s5, b_cst], writes=[bb])
                S.op("act", lambda e: e.activation(out=WZ[:, :, i, :, :], in_=pb, func=AF.Copy), reads=[bb], writes=b_wz)
            if stop == "s5gen":
                quiesce(); return
            if EXP:
                xfer("e_cp", view(A2, 0, 34816, F32), [b_cp])
                xfer("e_bd", view(A0, 32 * KB, 16 * KB, F32), b_bd)
                xfer("e_tab", view(A2, 34816, 16384, F32), [b_tab])
            handoff(b_zl, b_zl + [b_s5])
            for q in range(4):
                for rr in range(4):
                    r = 4 * q + rr
                    bank, bb = next_bank()
                    for x in range(2):
                        col = x * 128
                        for i in range(16):
                            S.op("pe", lambda e: e.matmul(bank[:, col:col + 128], lhsT=WZ[32 * rr:32 * rr + 32, q, i, x, :], rhs=US[32 * rr:32 * rr + 32, q, i, :],
                                                          start=(i == 0), stop=(i == 15), tile_position=(32 * rr, 0)), inc=(i == 15 and x == 1),
                                 reads=[b_wz[q], b_us[q]], writes=[bb])
                    S.op("act", lambda e: e.activation(out=ZL[:, r, :, :].rearrange("p x c -> p (x c)"), in_=bank[:, 0:256], func=AF.Copy),
                         reads=[bb], writes=[b_zl[r]])
            if cfg.get("debug"):
                S.dma("sp", dbq, dr["dbg_zl"], view(A2, 51200, 16384, F32), reads=b_zl, writes=[b_dbg])
                S.dma("sp", dbq, dr["dbg_sc"], view(SP_, 0, 4864, F32), reads=[b_s5], writes=[b_dbg])
                S.dma("sp", dbq, dr["dbg_cp"], view(A2, 0, 34816, BF16), reads=[b_cp], writes=[b_dbg])
                S.dma("sp", dbq, dr["dbg_bd"], view(A0, 32 * KB, 16 * KB, BF16), reads=b_bd, writes=[b_dbg])
                S.dma("sp", dbq, dr["dbg_wz"], view(A0, 0, 32 * KB, BF16), reads=b_wz, writes=[b_dbg])
            handoff(b_ww, b_wz + b_ww)
            dt1 = view(GT, 0, 8192, F32).rearrange("p (r c) -> p r c", r=16)
            dt2 = view(GT, 8192, 8192, F32).rearrange("p (r c) -> p r c", r=16)
            b_dt = S.buf("dt"); handoff([b_dt], b_wz)
            magL_b = bc_last(s_magL, NC_)

            def scan_and_mod(init_ap, b_init, final):
                for r in range(16):
                    for x in range(2):
                        ini = 0.0 if init_ap is None else init_ap[:, r, x:x + 1]
                        S.op("dve", lambda e: e.tensor_tensor_scan(out=ZL[:, r, x, :], data0=magL_b[:, r, :], data1=WW[:, r, x, :], initial=ini,
                                                                   op0=ALU.mult, op1=ALU.add), reads=[b_ww[r], b_s5] + ([b_init] if b_init else []), writes=[b_zl[r]])
                if not final:
                    cmul(zend[:, :, 0], zend[:, :, 1], ZL[:, :, 0, NC_ - 1], ZL[:, :, 1, NC_ - 1], EC[:, :, NC_ - 1], ES[:, :, NC_ - 1], s_t, s_t2,
                         rd=b_zl + [b_tab])
                else:
                    S.op("dve", lambda e: e.tensor_tensor(out=dt1, in0=EC, in1=ZL[:, :, 0, :], op=ALU.mult), reads=[b_tab] + b_zl, writes=[b_dt])
                    S.op("dve", lambda e: e.tensor_tensor(out=dt2, in0=ES, in1=ZL[:, :, 1, :], op=ALU.mult), reads=[b_tab] + b_zl, writes=[b_dt])
                    S.op("dve", lambda e: e.tensor_tensor(out=ZS[:, :, 0, 1:NC_ + 1], in0=dt1, in1=dt2, op=ALU.subtract), reads=[b_dt], writes=b_zs)
                    S.op("dve", lambda e: e.tensor_tensor(out=dt1, in0=EC, in1=ZL[:, :, 1, :], op=ALU.mult), reads=[b_tab] + b_zl, writes=[b_dt])
                    S.op("dve", lambda e: e.tensor_tensor(out=dt2, in0=ES, in1=ZL[:, :, 0, :], op=ALU.mult), reads=[b_tab] + b_zl, writes=[b_dt])
                    S.op("dve", lambda e: e.tensor_tensor(out=ZS[:, :, 1, 1:NC_ + 1], in0=dt1, in1=dt2, op=ALU.add), reads=[b_dt], writes=b_zs)
                    S.op("dve", lambda e: e.tensor_copy(out=ZS[:, :, :, 0], in_=init_ap), reads=[b_init], writes=b_zs)

            S.op("dve", lambda e: e.tensor_tensor(out=dt1, in0=EC, in1=ZL[:, :, 0, :], op=ALU.mult), reads=[b_tab] + b_zl, writes=[b_dt])
            S.op("dve", lambda e: e.tensor_tensor(out=dt2, in0=ES, in1=ZL[:, :, 1, :], op=ALU.mult), reads=[b_tab] + b_zl, writes=[b_dt])
            S.op("dve", lambda e: e.tensor_tensor(out=WW[:, :, 0, :], in0=dt1, in1=dt2, op=ALU.add), reads=[b_dt], writes=b_ww)
            S.op("dve", lambda e: e.tensor_tensor(out=dt1, in0=EC, in1=ZL[:, :, 1, :], op=ALU.mult), reads=[b_tab] + b_zl, writes=[b_dt])
            S.op("dve", lambda e: e.tensor_tensor(out=dt2, in0=ES, in1=ZL[:, :, 0, :], op=ALU.mult), reads=[b_tab] + b_zl, writes=[b_dt])
            S.op("dve", lambda e: e.tensor_tensor(out=WW[:, :, 1, :], in0=dt1, in1=dt2, op=ALU.subtract), reads=[b_dt], writes=b_ww)
            scan_and_mod(None, None, False)
            xfer("e_ww", view(A0, 0, 16384, F32), b_ww)
            if IMP:
                xfer("e_cp", view(A2, 0, 34816, F32), [b_cp])
                xfer("e_bd", view(A0, 32 * KB, 16 * KB, F32), b_bd)
                xfer("e_tab", view(A2, 34816, 16384, F32), [b_tab])
            xfer("e_sp", view(SP_, 0, 4864, F32), [b_s5])
            if IMP:
                S.mute = False
            if cfg.get("debug"):
                S.dma("sp", dbq, dr["dbg_zend"], zend.rearrange("p r x -> p (r x)"), reads=[b_s5], writes=[b_dbg])
            if mode == "A":
                S.dma("sp", oq, dr["summ"][:, 0:32], zend.rearrange("p r x -> p (r x)"), reads=[b_s5], writes=[b_out])
                S.wait_all("sp", [b_out])
            if mode != "A":
                sa = dr["summ_all"]
                zall = view(GT, 0, 1024, F32).rearrange("p (j r x) -> p j r x", j=8, x=2)
                zq = S.dma_sem(f"zq{l}")
                S.dma("sp", zq, zall.rearrange("p j r x -> p j (r x)"), sa[:, :, 0:32].rearrange("j p c -> p j c"), reads=[b_dt], writes=[b_dt])
                S.op("dve", lambda e: e.memset(sst, 0.0), reads=[b_s5], writes=[b_s5])
                ctr = sc(24)[:, 0:16]; cti = sc(25)[:, 0:16]
                for j in range(8):
                    cmul(ctr, cti, sst[:, :, 0], sst[:, :, 1], s_l128r, s_l128i, s_t, s_t2)
                    TT(ctr, ctr, zall[:, j, :, 0], ALU.add, rd=[b_dt]); TT(cti, cti, zall[:, j, :, 1], ALU.add, rd=[b_dt])
                    TT(ctr, ctr, sst[:, :, 0], ALU.subtract); TT(cti, cti, sst[:, :, 1], ALU.subtract)
                    S.op("dve", lambda e: e.scalar_tensor_tensor(out=sst[:, :, 0], in0=ctr, scalar=pred[:, j:j + 1], in1=sst[:, :, 0], op0=ALU.mult, op1=ALU.add),
                         reads=[b_s5, b_cst], writes=[b_s5])
                    S.op("dve", lambda e: e.scalar_tensor_tensor(out=sst[:, :, 1], in0=cti, scalar=pred[:, j:j + 1], in1=sst[:, :, 1], op0=ALU.mult, op1=ALU.add),
                         reads=[b_s5, b_cst], writes=[b_s5])
                scan_and_mod(sst, b_s5, True)
                if cfg.get("debug"):
                    S.dma("sp", dbq, dr["dbg_zs"], view(A2, 67584, 8256, BF16), reads=b_zs, writes=[b_dbg])
                handoff(b_zg, b_zl + b_zg)
                for q in range(4):
                    for ib in range(4):
                        bank, bb = next_bank()
                        for i4 in range(4):
                            ip = ib * 4 + i4
                            col = i4 * 128
                            for i in range(ip + 1):
                                S.op("pe", lambda e: e.matmul(bank[:, col:col + 128], lhsT=BD[:, q, ip - i, :], rhs=US[:, q, i, :], start=(i == 0), stop=False),
                                     inc=False, reads=[b_bd[q], b_us[q]], writes=[bb])
                            for rr in range(4):
                                r = 4 * q + rr
                                for x in range(2):
                                    lastw = (rr == 3 and x == 1)
                                    S.op("pe", lambda e: e.matmul(bank[32 * rr:32 * rr + 32, col:col + 128], lhsT=CP[:, ip + 1, r, x, :], rhs=ZS[:, r, x, 0:NC_],
                                                                  start=False, stop=lastw, tile_position=(0, 32 * rr)), inc=(lastw and i4 == 3),
                                         reads=[b_cp, b_zs[r]], writes=[bb])
                        S.op("act", lambda e: e.activation(out=ZG[:, q, ib * 4:ib * 4 + 4, :].rearrange("p i c -> p (i c)"), in_=bank[:, :], func=AF.Gelu_apprx_tanh),
                             reads=[bb], writes=[b_zg[q]])
                if cfg.get("debug"):
                    S.dma("sp", dbq, dr["dbg_zg"], view(A2, 51200, 16384, BF16), reads=b_zg, writes=[b_dbg])
                wg = dr["s5_w_glu"][l].rearrange("(k p) n -> p k n", p=128)
                wgl, bwg = wload(wg, 4, 512)
                gate = view(GT, 0, 2048, F32)
                ZZ = [view(GT, 2048 + ft * 2048, 2048, F32) for ft in range(4)]
                sqb = [view(GT, 10240 + i * 1024, 1024, BF16) for i in range(2)]
                rst = view(GT, 12288, 2048, F32)
                b_gate = S.buf("gate"); b_zz = S.bufs(4, "zz"); b_sqb = S.bufs(2, "sqb"); b_rst = S.buf("rst")
                handoff([b_gate, b_rst] + b_zz + b_sqb, [b_dt] + b_ww)
                YCv = YC[:, 0:4, :].rearrange("p f (c i) -> p f i c", i=16)
                for cb in range(4):
                    bankq, bbq = next_bank()
                    for ft in range(4):
                        bank, bb = next_bank()
                        for kk in range(4):
                            S.op("pe", lambda e: e.matmul(bank[:, :], lhsT=wgl[:, kk, ft * 128:(ft + 1) * 128], rhs=ZG[:, kk, cb * 4:cb * 4 + 4, :],
                                                          start=(kk == 0), stop=(kk == 3)), inc=(kk == 3), reads=[bwg] + b_zg, writes=[bb])
                        S.op("act", lambda e: e.activation(out=gate, in_=bank[:, :], func=AF.Sigmoid, bias=bglu[:, ft:ft + 1]), reads=[bb, b_prm], writes=[b_gate])
                        S.op("dve", lambda e: e.tensor_tensor(out=ZZ[ft], in0=ZG[:, ft, cb * 4:cb * 4 + 4, :].rearrange("p i c -> p (i c)"), in1=gate, op=ALU.mult),
                             reads=[b_zg[ft], b_gate], writes=[b_zz[ft]])
                        S.op("act", lambda e: e.activation(out=sqb[ft % 2], in_=ZZ[ft], func=AF.Square), reads=[b_zz[ft]], writes=[b_sqb[ft % 2]])
                        S.op("pe", lambda e: e.matmul(bankq[:, :], lhsT=ones_b, rhs=sqb[ft % 2], start=(ft == 0), stop=(ft == 3)), inc=True,
                             reads=[b_sqb[ft % 2], b_cst], writes=[bbq])
                    S.op("dve", lambda e: e.tensor_scalar(out=rst, in0=bankq[:, :], scalar1=1.0 / 512, scalar2=EPS, op0=ALU.mult, op1=ALU.add), reads=[bbq], writes=[b_rst])
                    S.op("act", lambda e: e.activation(out=rst, in_=rst, func=AF.Sqrt), reads=[b_rst], writes=[b_rst])
                    S.op("dve", lambda e: e.reciprocal(out=rst, in_=rst), reads=[b_rst], writes=[b_rst])
                    for ft in range(4):
                        S.op("dve", lambda e: e.scalar_tensor_tensor(out=YCv[:, ft, cb * 4:cb * 4 + 4, :], in0=ZZ[ft].rearrange("p (i c) -> p i c", i=4),
                                                                     scalar=outg[:, ft:ft + 1], in1=rst.rearrange("p (i c) -> p i c", i=4), op0=ALU.mult, op1=ALU.mult),
                             reads=[b_zz[ft], b_rst, b_prm], writes=b_yc)
                if cfg.get("debug"):
                    S.dma("sp", dbq, dr["dbg_yc"], view(A1, 0, 32 * KB, BF16), reads=b_yc, writes=[b_dbg])

            if mode != "A":
                if stop == "s5":
                    quiesce(); return
                S.handoff(X_b, b_qk + b_og + b_us + b_wz + b_bd + b_ww + [b_dt, b_gate, b_rst] + b_zz + b_sqb)
                for tt in range(NTT):
                    S.dma("sp", xq[tt], X[:, tt, :], xin_ap[tt * 128:(tt + 1) * 128, :], writes=[X_b[tt]])
                wo = dr["w_out"][l].rearrange("(k p) n -> p k n", p=128)
                ws = WStream([(wo[:, :, h * 512:(h + 1) * 512], 8, 512) for h in range(2)])
                for h in range(2):
                    wc, bw = ws.get(h)
                    for tt in range(NTT):
                        bank, bb = next_bank()
                        for kk in range(8):
                            S.op("pe", lambda e: e.matmul(bank[:, :], lhsT=YC[:, kk, tt * 128:(tt + 1) * 128], rhs=wc[:, kk, :],
                                                          start=(kk == 0), stop=(kk == 7)), inc=(kk == 7), reads=[bw, b_yc[tt]], writes=[bb])
                        S.op("dve", lambda e: e.tensor_tensor(out=X[:, tt, h * 512:(h + 1) * 512], in0=X[:, tt, h * 512:(h + 1) * 512], in1=bank[:, :], op=ALU.add),
                             reads=[bb, X_b[tt]], writes=[X_b[tt]])

                if cfg.get("dbg_x1"):
                    b_o1 = S.buf("o1")
                    for tt in range(NTT):
                        S.dma("sp", oq, xout_ap[tt * 128:(tt + 1) * 128, :], X[:, tt, :], reads=[X_b[tt]], writes=[b_o1])
                    S.wait_all("sp", [b_o1])
                    return
                if stop == "wout":
                    quiesce(); return
                S.handoff(HT_b, b_yc)
                a2_users = [b_g, b_sm] + b_kt + b_cl + [b_ct, b_tc] + b_vw + b_pt + b_hb + b_hn + [b_cp, b_tab, b_s5] + b_zl + b_zs + b_zg + b_va + [b_gr] + ml_extra
                handoff([b_junk, b_xh] + b_xnb, a2_users)
                norm_to_hT(g2, False, False)
                if cfg.get("dbg_ht2"):
                    b_o1 = S.buf("o1")
                    S.dma("sp", oq, dr["dbg_ht"], view(A1, 0, 32832, BF16)[:, 0:8 * 2051], reads=HT_b, writes=[b_o1])
                w1 = dr["w_ff1"][l].rearrange("(k p) n -> p k n", p=128)
                w2 = dr["w_ff2"][l].rearrange("(k p) n -> p k n", p=128)
                c1 = [(w1[:, :, hc * 512:(hc + 1) * 512], 8, 512) for hc in range(8)]
                c2 = [(w2[:, hc * 4:(hc + 1) * 4, :], 4, 1024) for hc in range(8)]
                chunks = [c1[0]]
                for hc in range(8):
                    if hc + 1 < 8:
                        chunks.append(c1[hc + 1])
                    chunks.append(c2[hc])
                ws = WStream(chunks)
                k.wci = 0

                def wnext():
                    r = ws.get(k.wci)
                    k.wci += 1
                    return r
                hid = [view(SCR, i * 16 * KB, 16 * KB, BF16).rearrange("p (f t) -> p f t", f=4) for i in range(2)]
                b_hid = [S.bufs(4, f"hid{i}") for i in range(2)]
                sq = [view(SCR, 32 * KB + i * 2048, 2048, F32) for i in range(2)]
                b_sq = S.bufs(2, "sq")
                k.sqi = 0
                handoff(b_hid[0] + b_hid[1] + b_sq, a2_users + [b_junk, b_xh] + b_xnb)

                def ffn1(hc):
                    wc, bw = wnext()
                    hb = hid[hc % 2]
                    for ft in range(4):
                        for nb in range(4):
                            bank, bb = next_bank()
                            for kk in range(8):
                                S.op("pe", lambda e: e.matmul(bank[:, :], lhsT=wc[:, kk, ft * 128:(ft + 1) * 128], rhs=HT[:, kk, 3 + nb * 512:3 + (nb + 1) * 512],
                                                              start=(kk == 0), stop=(kk == 7)), inc=(kk == 7), reads=[bw] + HT_b[nb * 4:nb * 4 + 4], writes=[bb])
                            si = k.sqi % 2; k.sqi += 1
                            S.op("act", lambda e: e.activation(out=sq[si], in_=bank[:, :], func=AF.Square), reads=[bb], writes=[b_sq[si]])
                            S.op("dve", lambda e: e.scalar_tensor_tensor(out=hb[:, ft, nb * 512:(nb + 1) * 512], in0=bank[:, :], scalar=0.0, in1=sq[si],
                                                                         op0=ALU.is_gt, op1=ALU.mult), reads=[bb, b_sq[si]], writes=[b_hid[hc % 2][nb]])

                def ffn2(hc):
                    wc, bw = wnext()
                    hb = hid[hc % 2]
                    for tt in range(NTT):
                        for h in range(2):
                            bank, bb = next_bank()
                            for kk in range(4):
                                S.op("pe", lambda e: e.matmul(bank[:, :], lhsT=hb[:, kk, tt * 128:(tt + 1) * 128], rhs=wc[:, kk, h * 512:(h + 1) * 512],
                                                              start=(kk == 0), stop=(kk == 3)), inc=(kk == 3), reads=[bw, b_hid[hc % 2][tt // 4]], writes=[bb])
                            S.op("dve", lambda e: e.tensor_tensor(out=X[:, tt, h * 512:(h + 1) * 512], in0=X[:, tt, h * 512:(h + 1) * 512], in1=bank[:, :], op=ALU.add),
                                 reads=[bb, X_b[tt]], writes=[X_b[tt]])

                ffn1(0)
                for hc in range(8):
                    if hc + 1 < 8:
                        ffn1(hc + 1)
                    ffn2(hc)

                if last:
                    gfin = view(SCR, 40 * KB, 4096, F32)
                    b_gf = S.buf("gfin")
                    fg = dr["final_norm_g"]
                    S.dma("sp", gq, gfin, bass.AP(fg.tensor, fg.offset, [[0, 128], [1, D]]), writes=[b_gf])
                    ot = [view(SCR, 44 * KB + i * 4096, 4096, F32) for i in range(2)]
                    b_ot = S.bufs(2, "ot")
                    handoff([b_junk], [b_junk] + b_hid[0] + b_hid[1])
                    stats_A(0)
                    for g4 in range(4):
                        if g4 + 1 < 4:
                            stats_A(g4 + 1)
                        stats_B(g4)
                        for tt in range(4 * g4, 4 * g4 + 4):
                            S.op("dve", lambda e: e.scalar_tensor_tensor(out=ot[tt % 2], in0=X[:, tt, :], scalar=rstd[:, tt:tt + 1], in1=gfin,
                                                                         op0=ALU.mult, op1=ALU.mult), reads=[X_b[tt], b_sg[g4], b_gf], writes=[b_ot[tt % 2]])
                            S.dma("sp", oq, xout_ap[tt * 128:(tt + 1) * 128, :], ot[tt % 2], reads=[b_ot[tt % 2]], writes=[b_out])
                else:
                    for tt in range(NTT):
                        S.dma("sp", oq, xout_ap[tt * 128:(tt + 1) * 128, :], X[:, tt, :], reads=[X_b[tt]], writes=[b_out])
                S.wait_all("sp", [b_out])

        layers = cfg["layers"]
        for li, l in enumerate(layers):
            layer(l, dr["xin"], dr.get("xhalo"), dr.get("xout"), last=cfg.get("final", False) and li == len(layers) - 1)
    return nc


_NC_CACHE = {}
N_CORES = 8
LAYER_KEYS = ["norm_mix_g", "w_in", "w_out", "norm_ffn_g", "w_ff1", "w_ff2", "ml_conv_w", "ml_conv_b", "ml_b_i", "ml_b_f",
              "ml_norm_g", "s5_a_re", "s5_a_im", "s5_log_dt", "s5_b_re", "s5_b_im", "s5_c_re", "s5_c_im", "s5_d", "s5_w_glu",
              "s5_b_glu", "s5_out_g"]
A_KEYS = ["norm_mix_g", "w_in", "ml_conv_w", "ml_conv_b", "ml_b_i", "ml_b_f", "s5_a_re", "s5_a_im", "s5_log_dt", "s5_b_re", "s5_b_im",
          "s5_c_re", "s5_c_im", "s5_d", "ml_norm_g", "s5_b_glu", "s5_out_g"]
B_KEYS = ["w_out", "norm_ffn_g", "w_ff1", "w_ff2", "s5_w_glu", "ml_norm_g", "s5_b_glu", "s5_out_g"]
XF_NAMES = ["e_a0", "e_va", "e_gates", "e_cl", "e_ww", "e_cp", "e_bd", "e_tab", "e_sp"]
A_CONST = ["ident", "causal", "ones", "par01", "bdmask"]
B_CONST = ["ident", "causal", "ones"]


def _get_nc(mode, final):
    key = (mode, final)
    if key not in _NC_CACHE:
        _NC_CACHE[key] = build(dict(layers=[0], nlayers=1, mode=mode, final=final, debug=False))
    return _NC_CACHE[key]


def _consts():
    par = np.zeros((128, 2), np.float32)
    par[:, 1] = (np.arange(128) // 16) % 2
    par[:, 0] = 1 - par[:, 1]
    return {"ident": np.eye(128, dtype=np.float32), "causal": np.triu(np.ones((128, 128), np.float32)),
            "ones": np.ones((128, 128), np.float32), "par01": par,
            "bdmask": np.kron(np.eye(8), np.ones((16, 16))).astype(np.float32)}


def kernel(**inputs):
    x = np.ascontiguousarray(inputs["x"], dtype=np.float32)
    nb, ls, d = x.shape
    per = ls // 4
    consts = _consts()
    cur = [np.ascontiguousarray(x[c // 4, (c % 4) * per:(c % 4 + 1) * per]) for c in range(N_CORES)]
    preds = []
    for c in range(N_CORES):
        p = np.zeros((128, 8), np.float32)
        for j in range(N_CORES):
            if j // 4 == c // 4 and j < c:
                p[:, j] = 1.0
        preds.append(p)
    depth = inputs["w_in"].shape[0]
    for l in range(depth):
        halos = [np.zeros((3, d), np.float32) if c % 4 == 0 else np.ascontiguousarray(cur[c - 1][-3:]) for c in range(N_CORES)]
        lw = {k: np.ascontiguousarray(np.asarray(inputs[k], dtype=np.float32)[l:l + 1]) for k in LAYER_KEYS}
        final = (l == depth - 1)
        ncA = _get_nc("A", False)
        mapsA = []
        for c in range(N_CORES):
            m = {"xin": cur[c], "xhalo": halos[c], "pred": preds[c]}
            m.update({k: consts[k] for k in A_CONST})
            m.update({k: lw[k] for k in A_KEYS})
            mapsA.append(m)
        resA = run_bass_kernel_spmd(ncA, mapsA, core_ids=list(range(N_CORES)))
        summ_all = np.ascontiguousarray(np.stack([np.asarray(resA.results[c]["summ"]) for c in range(N_CORES)]))
        ncB = _get_nc("B", final)
        mapsB = []
        for c in range(N_CORES):
            m = {"xin": cur[c], "pred": preds[c], "summ_all": summ_all,
                 "final_norm_g": np.ascontiguousarray(inputs["final_norm_g"], dtype=np.float32)}
            m.update({k: consts[k] for k in B_CONST})
            m.update({k: lw[k] for k in B_KEYS})
            m.update({k: np.asarray(resA.results[c][k]) for k in XF_NAMES})
            mapsB.append(m)
        resB = run_bass_kernel_spmd(ncB, mapsB, core_ids=list(range(N_CORES)))
        cur = [np.asarray(resB.results[c]["xout"]) for c in range(N_CORES)]
    out = np.empty_like(x)
    for c in range(N_CORES):
        out[c // 4, (c % 4) * per:(c % 4 + 1) * per] = cur[c]
    return out
```

```python
import numpy as np
import concourse.bass as bass
import concourse.mybir as mybir
from concourse.bass_utils import run_bass_kernel_spmd

F32 = mybir.dt.float32
BF16 = mybir.dt.bfloat16
AF = mybir.ActivationFunctionType
ALU = mybir.AluOpType
AX = mybir.AxisListType


class Buf:
    __slots__ = ("name", "w", "r")

    def __init__(self, name):
        self.name = name
        self.w = {}
        self.r = {}


class Sched:
    def __init__(self, nc, ctx):
        self.nc = nc
        self.ctx = ctx
        self.eng = {"pe": nc.tensor, "act": nc.scalar, "dve": nc.vector, "pool": nc.gpsimd, "sp": nc.sync}
        self.sem = {}
        self.cnt = {}
        for k in self.eng:
            self.sem[k] = ctx.enter_context(nc.semaphore("s_" + k))
            self.cnt[k] = 0
        self.waited = {k: {} for k in self.eng}
        self.ndma = 0
        self.nbuf = 0
        self.mute = False

    def buf(self, name=None):
        self.nbuf += 1
        return Buf(name or f"b{self.nbuf}")

    def bufs(self, n, name="b"):
        return [self.buf(f"{name}{i}") for i in range(n)]

    def dma_sem(self, name=None):
        self.ndma += 1
        key = name or f"dma{self.ndma}"
        self.sem[key] = self.ctx.enter_context(self.nc.semaphore("s_" + key))
        self.cnt[key] = 0
        return key

    def _deps(self, e, reads, writes):
        deps = {}
        for b in reads:
            for k, c in b.w.items():
                if deps.get(k, 0) < c:
                    deps[k] = c
        for b in writes:
            for k, c in b.w.items():
                if deps.get(k, 0) < c:
                    deps[k] = c
            for k, c in b.r.items():
                if deps.get(k, 0) < c:
                    deps[k] = c
        eng = self.eng[e]
        for k, c in deps.items():
            if k == e and e == "pe":
                continue
            if self.waited[e].get(k, 0) < c:
                eng.wait_ge(self.sem[k], c)
                self.waited[e][k] = c

    def _record(self, key, c, reads, writes):
        for b in writes:
            b.w = {key: c}
            b.r = {}
        for b in reads:
            if b.r.get(key, 0) < c:
                b.r[key] = c

    def op(self, e, fn, reads=(), writes=(), inc=True):
        if self.mute:
            return None
        self._deps(e, reads, writes)
        ins = fn(self.eng[e])
        if inc:
            self.cnt[e] += 1
            ins.then_inc(self.sem[e], 1)
            self._record(e, self.cnt[e], reads, writes)
        else:
            self._record(e, self.cnt[e] + 1, reads, writes)
        return ins

    def seal(self, key, bufs):
        if self.mute:
            return
        c = self.cnt[key]
        for b in bufs:
            if key in b.w:
                b.w[key] = c

    def handoff(self, news, olds):
        w = {}
        r = {}
        for ob in olds:
            for k2, c2 in ob.w.items():
                if w.get(k2, 0) < c2:
                    w[k2] = c2
            for k2, c2 in ob.r.items():
                if r.get(k2, 0) < c2:
                    r[k2] = c2
        for nb in news:
            nb.w = dict(w)
            nb.r = dict(r)

    def dma(self, q, dsem, out, in_, reads=(), writes=(), **kw):
        if self.mute:
            return None
        self._deps(q, reads, writes)
        ins = self.eng[q].dma_start(out=out, in_=in_, **kw)
        self.cnt[dsem] += 16
        ins.then_inc(self.sem[dsem], 16)
        self._record(dsem, self.cnt[dsem], reads, writes)
        return ins

    def wait_all(self, e, bufs):
        if self.mute:
            return
        self._deps(e, bufs, ())


import numpy as np
from contextlib import ExitStack

NT = 2048
NTT = 16
D = 1024
DIN = 2568
DFF = 4096
EPS = 1e-6
KB = 1024


class KB_:
    pass


def build(cfg):
    nc = bass.Bass("TRN2", target_bir_lowering=False)
    k = KB_()
    k.nc = nc
    k.cfg = cfg
    L = cfg.get("nlayers", 1)
    dr = {}

    def din(name, shape, dt=F32):
        dr[name] = nc.dram_tensor(name, list(shape), dt, kind="ExternalInput").ap()
        return dr[name]

    def dout(name, shape, dt=F32):
        dr[name] = nc.dram_tensor(name, list(shape), dt, kind="ExternalOutput").ap()
        return dr[name]

    mode = cfg.get("mode", "B")
    IMP = (mode == "B")
    EXP = (mode == "A")
    XF = [("e_a0", 16384), ("e_va", 4256), ("e_gates", 1152), ("e_cl", 8256), ("e_ww", 4096), ("e_cp", 8704), ("e_bd", 4096),
          ("e_tab", 4096), ("e_sp", 1216)]
    din("xin", [NT, D])
    if IMP:
        _din_real = din

        def din(name, shape, dt=F32, _real=_din_real):
            dr[name] = nc.dram_tensor(name, list(shape), dt).ap()
            return dr[name]
    if True:
        din("xhalo", [3, D])
        din("norm_mix_g", [L, D]); din("w_in", [L, D, DIN])
        din("ml_conv_w", [L, 4, 1024]); din("ml_conv_b", [L, 1024])
        din("ml_b_i", [L, 4]); din("ml_b_f", [L, 4])
        din("s5_a_re", [L, 32, 64]); din("s5_a_im", [L, 32, 64]); din("s5_log_dt", [L, 32])
        din("s5_b_re", [L, 32, 64, 16]); din("s5_b_im", [L, 32, 64, 16]); din("s5_c_re", [L, 32, 16, 64]); din("s5_c_im", [L, 32, 16, 64])
        din("s5_d", [L, 32, 16])
        din("par01", [128, 2]); din("bdmask", [128, 128])
    if IMP:
        din = _din_real
    if mode != "A":
        din("w_out", [L, D, D])
        din("norm_ffn_g", [L, D]); din("w_ff1", [L, D, DFF]); din("w_ff2", [L, DFF, D])
        din("final_norm_g", [D])
        din("s5_w_glu", [L, 512, 512])
        din("summ_all", [8, 128, 556])
    din("ident", [128, 128]); din("causal", [128, 128]); din("ones", [128, 128])
    din("ml_norm_g", [L, 512]); din("s5_b_glu", [L, 512]); din("s5_out_g", [L, 512])
    din("pred", [128, 8])
    if EXP:
        dout("summ", [128, 556])
        for nm_, w_ in XF:
            dout(nm_, [128, w_])
    if IMP:
        for nm_, w_ in XF:
            din(nm_, [128, w_])
    if mode != "A":
        dout("xout", [NT, D])
    if cfg.get("dbg_ht2"):
        dout("dbg_ht", [128, 8 * 2051], BF16)
    if cfg.get("debug"):
        dout("dbg_u", [128, 4 * 2048], BF16)
        dout("dbg_qk", [128, 8 * 2048], BF16)
        dout("dbg_v", [128, 16 * 4 * 129], BF16)
        dout("dbg_o", [128, 16 * 512], BF16)
        dout("dbg_if", [128, 128])
        dout("dbg_yc", [128, 8 * 2048], BF16)
        dout("dbg_g", [128, 16 * 64])
        dout("dbg_zl", [128, 4096]); dout("dbg_zs", [128, 16 * 2 * 129], BF16); dout("dbg_zg", [128, 8192], BF16)
        dout("dbg_sc", [128, 1216]); dout("dbg_cp", [128, 17408], BF16); dout("dbg_bd", [128, 8192], BF16); dout("dbg_wz", [128, 16384], BF16)
        dout("dbg_zend", [128, 32])
    k.dr = dr

    with ExitStack() as ctx:
        S = Sched(nc, ctx)
        k.S = S
        ctx.enter_context(nc.allow_non_contiguous_dma(reason="small param loads"))
        ctx.enter_context(nc.allow_low_precision(reason="bf16 matmul operands"))
        A0_B, A1_B, A2_B = 64 * KB, 33 * KB + 256, 79 * KB
        arena = ctx.enter_context(nc.sbuf_tensor("arena", [128, (A0_B + A1_B + A2_B) // 4], F32))
        ring = ctx.enter_context(nc.sbuf_tensor("ring", [128, 3 * 4096], BF16))
        cst = ctx.enter_context(nc.sbuf_tensor("cst", [128, 1024], F32))
        banks = [ctx.enter_context(nc.psum_tensor(f"ps{i}", [128, 512], F32)) for i in range(8)]
        bank_bufs = S.bufs(8, "bank")
        k.bank_i = 0
        block = ctx.enter_context(nc.Block())

        def view(base, off, nbytes, dt):
            assert off % 4 == 0 and nbytes % 4 == 0
            a = arena[:, (base + off) // 4:(base + off + nbytes) // 4]
            return a if dt == F32 else a.bitcast(dt)
        A0, A1, A2 = 0, A0_B, A0_B + A1_B

        def next_bank():
            i = k.bank_i
            k.bank_i = (i + 1) % 8
            return banks[i], bank_bufs[i]

        ident_f = cst[:, 0:128]
        ident_b = cst[:, 128:192].bitcast(BF16)
        b_cst = S.buf("cst")
        dq = S.dma_sem("dq_misc")
        S.dma("sp", dq, ident_f, dr["ident"], writes=[b_cst])
        tri_f = cst[:, 384:512]
        ones_f = cst[:, 512:640]
        tri_b = cst[:, 192:256].bitcast(BF16)
        S.dma("sp", dq, tri_f, dr["causal"], writes=[b_cst])
        S.dma("sp", dq, ones_f, dr["ones"], writes=[b_cst])
        par01 = cst[:, 752:754]
        bdm = cst[:, 768:896]
        ones_b = cst[:, 896:960].bitcast(BF16)
        if not IMP:
            S.dma("sp", dq, par01, dr["par01"], writes=[b_cst])
        pred = cst[:, 972:980]
        pmask = cst[:, 980:988]
        S.dma("sp", dq, pred, dr["pred"], writes=[b_cst])
        if not IMP:
            S.dma("sp", dq, bdm, dr["bdmask"], writes=[b_cst])
        S.seal(dq, [b_cst])
        S.op("dve", lambda e: e.tensor_copy(out=ones_b, in_=ones_f), reads=[b_cst], writes=[b_cst])
        S.op("dve", lambda e: e.tensor_scalar(out=pmask, in0=pred, scalar1=1e6, scalar2=-1e6, op0=ALU.mult, op1=ALU.add), reads=[b_cst], writes=[b_cst])
        S.op("dve", lambda e: e.tensor_copy(out=ident_b, in_=ident_f), reads=[b_cst], writes=[b_cst])
        S.op("dve", lambda e: e.tensor_copy(out=tri_b, in_=tri_f), reads=[b_cst], writes=[b_cst])

        X = view(A0, 0, 64 * KB, F32).rearrange("p (t d) -> p t d", t=NTT)
        X_b = S.bufs(NTT, "X")
        HT = view(A1, 0, 8 * 2051 * 2 + 0, BF16) if False else view(A1, 0, 32832, BF16)[:, 0:8 * 2051].rearrange("p (k t) -> p k t", k=8)
        HT_b = S.bufs(NTT + 1, "HT")
        YC = view(A1, 0, 32 * KB, BF16).rearrange("p (k t) -> p k t", k=8)
        QK = view(A0, 0, 32 * KB, BF16).rearrange("p (f t) -> p f t", f=8)
        OG = view(A0, 32 * KB, 16 * KB, BF16).rearrange("p (t d) -> p t d", t=NTT)
        US = view(A0, 48 * KB, 16 * KB, BF16).rearrange("p (q i c) -> p q i c", q=4, i=16)
        VA = view(A2, 0, 16512, BF16).rearrange("p (t h v) -> p t h v", t=NTT, h=4)
        GR = view(A2, 16512, 512, F32)
        SCR = A2 + 17024

        xq = [S.dma_sem(f"xq{i}") for i in range(16)]
        hq = S.dma_sem("hq"); gq = S.dma_sem("gq")
        oq = S.dma_sem("oq")
        wq = [S.dma_sem(f"wq{i}") for i in range(3)]
        ring_b = S.bufs(3, "ring")
        k.wi = 0

        def wload(src_ap, nk, ncols):
            i = k.wi % 3
            k.wi += 1
            v = ring[:, i * 4096: i * 4096 + nk * ncols].rearrange("p (k n) -> p k n", k=nk)
            S.dma("pool", wq[i], v, src_ap, writes=[ring_b[i]])
            return v, ring_b[i]

        class WStream:
            def __init__(self, chunks):
                self.chunks = chunks
                self.loaded = []

            def get(self, i, ahead=2):
                while len(self.loaded) < min(len(self.chunks), i + 1 + ahead):
                    self.loaded.append(wload(*self.chunks[len(self.loaded)]))
                return self.loaded[i]

        k.pq = None

        def load_pvec(dst, src_1d, b, q="sp"):
            S.dma(q, k.pq, dst, src_1d.rearrange("(k p) -> p k", p=128), writes=[b])

        tmp_b = S.bufs(4, "tmp")

        def quiesce():
            for e_ in ("pe", "act", "dve", "pool"):
                if S.cnt[e_] > 0:
                    nc.sync.wait_ge(S.sem[e_], S.cnt[e_])
            for key_, c_ in S.cnt.items():
                if key_ not in S.eng and c_ > 0:
                    nc.sync.wait_ge(S.sem[key_], c_)

        def layer(l, xin_ap, xh_ap, xout_ap, last):
            stop = cfg.get("stop")
            prm = cst[:, 256:256 + 64]
            b_prm = S.buf("prm")
            pq_l = S.dma_sem(f"pq{l}"); k.pq = pq_l
            g1 = cst[:, 640:648]; g2 = cst[:, 648:656]
            cw = cst[:, 656:688].rearrange("p (j f) -> p j f", j=4)
            cb = cst[:, 688:696]
            b_out = S.buf("out")
            if not IMP:
                load_pvec(g1, dr["norm_mix_g"][l], b_prm)
                for j in range(4):
                    load_pvec(cw[:, j, :], dr["ml_conv_w"][l, j], b_prm)
                load_pvec(cb, dr["ml_conv_b"][l], b_prm)
            if mode != "A":
                load_pvec(g2, dr["norm_ffn_g"][l], b_prm)
            mlg = cst[:, 740:744]
            load_pvec(mlg, dr["ml_norm_g"][l], b_prm)
            bif = cst[:, 744:752]
            dcol = cst[:, 960:964]; bglu = cst[:, 964:968]; outg = cst[:, 968:972]
            if not IMP:
                bi_ = dr["ml_b_i"][l]; bf_ = dr["ml_b_f"][l]
                S.dma("sp", pq_l, bif[:, 0:4], bass.AP(bi_.tensor, bi_.offset, [[0, 128], [1, 4]]), writes=[b_prm])
                S.dma("sp", pq_l, bif[:, 4:8], bass.AP(bf_.tensor, bf_.offset, [[0, 128], [1, 4]]), writes=[b_prm])
                load_pvec(dcol, dr["s5_d"][l].rearrange("g p -> (g p)"), b_prm)
            load_pvec(bglu, dr["s5_b_glu"][l], b_prm)
            load_pvec(outg, dr["s5_out_g"][l], b_prm)
            S.seal(pq_l, [b_prm])

            def xfer(name, ap, bufs):
                was = S.mute; S.mute = False
                if EXP:
                    S.dma("sp", oq, dr[name], ap, reads=bufs, writes=[b_out])
                elif IMP:
                    q_ = S.dma_sem(f"{name}_{l}")
                    S.dma("sp", q_, ap, dr[name], writes=bufs)
                S.mute = was
            if IMP:
                S.mute = True
            ssq = cst[:, 700:717]
            rstd = cst[:, 720:737]
            b_st = S.bufs(17, "st")
            junk = view(SCR, 0, 2048, BF16)
            b_junk = S.buf("junk")
            xnb = [view(SCR, 2048 + i * 2048, 2048, BF16) for i in range(2)]
            b_xnb = S.bufs(2, "xnb")
            xh_t = view(SCR, 6144, 4096, F32)
            b_xh = S.buf("xh")

            def rms_stats(src, np_, col, bsrc):
                S.op("act", lambda e: e.activation(out=junk[:np_], in_=src, func=AF.Square, accum_out=ssq[:np_, col:col + 1]),
                     reads=[bsrc], writes=[b_junk, b_st[col]])
                S.op("dve", lambda e: e.tensor_scalar(out=rstd[:np_, col:col + 1], in0=ssq[:np_, col:col + 1], scalar1=1.0 / D, scalar2=EPS,
                                                      op0=ALU.mult, op1=ALU.add), reads=[b_st[col]], writes=[b_st[col]])
                S.op("act", lambda e: e.activation(out=rstd[:np_, col:col + 1], in_=rstd[:np_, col:col + 1], func=AF.Sqrt),
                     reads=[b_st[col]], writes=[b_st[col]])
                S.op("dve", lambda e: e.reciprocal(out=rstd[:np_, col:col + 1], in_=rstd[:np_, col:col + 1]), reads=[b_st[col]], writes=[b_st[col]])

            b_sg = S.bufs(5, "stg")

            def stats_A(g4):
                tiles = [NTT] if g4 == 4 else range(4 * g4, 4 * g4 + 4)
                for tt in tiles:
                    halo = tt == NTT
                    np_ = 3 if halo else 128
                    src, bsrc = (xh_t[:3, :], b_xh) if halo else (X[:, tt, :], X_b[tt])
                    S.op("act", lambda e: e.activation(out=junk[:np_], in_=src, func=AF.Square, accum_out=ssq[:np_, tt:tt + 1]),
                         reads=[bsrc], writes=[b_junk, b_sg[g4]])

            def stats_B(g4):
                c0, c1 = (NTT, NTT + 1) if g4 == 4 else (4 * g4, 4 * g4 + 4)
                np_ = 3 if g4 == 4 else 128
                S.op("dve", lambda e: e.tensor_scalar(out=rstd[:np_, c0:c1], in0=ssq[:np_, c0:c1], scalar1=1.0 / D, scalar2=EPS,
                                                      op0=ALU.mult, op1=ALU.add), reads=[b_sg[g4]], writes=[b_sg[g4]])
                S.op("act", lambda e: e.activation(out=rstd[:np_, c0:c1], in_=rstd[:np_, c0:c1], func=AF.Sqrt), reads=[b_sg[g4]], writes=[b_sg[g4]])
                S.op("dve", lambda e: e.reciprocal(out=rstd[:np_, c0:c1], in_=rstd[:np_, c0:c1]), reads=[b_sg[g4]], writes=[b_sg[g4]])

            def norm_to_hT(gvec, with_halo, from_dram):
                gb = bass.AP(gvec.tensor, gvec.offset, [list(gvec.ap[0]), list(gvec.ap[1]), [0, 128]])
                for tt in range(NTT):
                    if from_dram:
                        S.dma("sp", xq[tt], X[:, tt, :], xin_ap[tt * 128:(tt + 1) * 128, :], writes=[X_b[tt]])
                if with_halo:
                    S.dma("sp", hq, xh_t[:3, :], xh_ap, writes=[b_xh])

                def stage_C(g4):
                    tiles = [NTT] if g4 == 4 else range(4 * g4, 4 * g4 + 4)
                    for tt in tiles:
                        halo = tt == NTT
                        np_ = 3 if halo else 128
                        src, bsrc = (xh_t[:3, :], b_xh) if halo else (X[:, tt, :], X_b[tt])
                        xb = xnb[tt % 2]; bx = b_xnb[tt % 2]
                        S.op("act", lambda e: e.activation(out=xb[:np_], in_=src, func=AF.Copy, scale=rstd[:np_, tt:tt + 1]),
                             reads=[bsrc, b_sg[g4]], writes=[bx])
                        bank, bb = next_bank()
                        pb = bank[:, 0:512].bitcast(BF16).rearrange("p (k t) -> p k t", k=8)
                        for kk in range(8):
                            S.op("pe", lambda e: e.transpose(pb[:, kk, 0:np_], xb[:np_, kk * 128:(kk + 1) * 128], ident_b[:np_, :np_]),
                                 inc=(kk == 7), reads=[bx, b_cst], writes=[bb])
                        c0 = 0 if halo else 3 + tt * 128
                        S.op("dve", lambda e: e.tensor_tensor(out=HT[:, :, c0:c0 + np_], in0=pb[:, :, 0:np_], in1=gb[:, :, 0:np_], op=ALU.mult),
                             reads=[bb, b_prm], writes=[HT_b[tt]])
                ng = 5 if with_halo else 4
                stats_A(0)
                for g4 in range(ng):
                    if g4 + 1 < ng:
                        stats_A(g4 + 1)
                    stats_B(g4)
                    stage_C(g4)

            norm_to_hT(g1, True, True)

            if stop == "norm1":
                quiesce(); return
            win = dr["w_in"][l].rearrange("(k p) n -> p k n", p=128)
            chunks = [(win[:, :, c * 512:(c + 1) * 512], 8, 512) for c in range(5)] + [(win[:, :, 2560:2568], 8, 8)]
            ws = WStream(chunks)
            stage = [view(SCR, 10240 + i * 8448, 8448, F32) for i in range(2)]
            b_stage = S.bufs(2, "stage")
            acc = view(SCR, 10240 + 2 * 8448, 8192, F32)
            b_acc = S.buf("acc")
            b_us = S.bufs(4, "us")
            b_qk = S.bufs(8, "qk")
            b_va = S.bufs(NTT, "va")
            b_og = S.bufs(NTT, "og")
            b_gr = S.buf("gr")
            allHT = HT_b
            S.handoff(b_us + b_qk + b_og, X_b)
            S.op("pool", lambda e: e.memset(VA[:, :, :, 128:129], 1.0), writes=b_va)
            for ci in range(3):
                wc, bw = ws.get(ci)
                for ft in range(4):
                    f = (ci - 1) * 4 + ft
                    if ci > 0:
                        st = stage[f % 2]; bs = b_stage[f % 2]
                        bank, bb = next_bank()
                        for kk in range(8):
                            S.op("pe", lambda e: e.matmul(bank[:, 0:3], lhsT=wc[:, kk, ft * 128:(ft + 1) * 128], rhs=HT[:, kk, 0:3],
                                                          start=(kk == 0), stop=(kk == 7)), inc=(kk == 7), reads=[bw, HT_b[NTT]], writes=[bb])
                        S.op("act", lambda e: e.activation(out=st[:, 0:3], in_=bank[:, 0:3], func=AF.Copy), reads=[bb], writes=[bs])
                    for nb in range(4):
                        bank, bb = next_bank()
                        for kk in range(8):
                            S.op("pe", lambda e: e.matmul(bank[:, :], lhsT=wc[:, kk, ft * 128:(ft + 1) * 128],
                                                          rhs=HT[:, kk, 3 + nb * 512:3 + (nb + 1) * 512], start=(kk == 0), stop=(kk == 7)),
                                 inc=(kk == 7), reads=[bw] + allHT[nb * 4:nb * 4 + 4], writes=[bb])
                        if ci == 0:
                            dst = US[:, ft, :, nb * 32:(nb + 1) * 32]
                            src = bank[:, :].rearrange("p (c i) -> p i c", i=16)
                            S.op("act", lambda e: e.activation(out=dst, in_=src, func=AF.Copy), reads=[bb], writes=[b_us[ft]])
                        else:
                            S.op("act", lambda e: e.activation(out=st[:, 3 + nb * 512:3 + (nb + 1) * 512], in_=bank[:, :], func=AF.Copy),
                                 reads=[bb], writes=[bs])
                    if ci > 0:
                        S.op("dve", lambda e: e.tensor_scalar(out=acc, in0=st[:, 0:2048], scalar1=cw[:, 0, f:f + 1], scalar2=None, op0=ALU.mult),
                             reads=[bs, b_prm], writes=[b_acc])
                        for j in range(1, 4):
                            S.op("dve", lambda e: e.scalar_tensor_tensor(out=acc, in0=st[:, j:j + 2048], scalar=cw[:, j, f:f + 1], in1=acc,
                                                                         op0=ALU.mult, op1=ALU.add), reads=[bs, b_prm, b_acc], writes=[b_acc])
                        S.op("act", lambda e: e.activation(out=QK[:, f, :], in_=acc, func=AF.Silu, bias=cb[:, f:f + 1]),
                             reads=[b_acc, b_prm], writes=[b_qk[f]])
            for ci in (3, 4):
                wc, bw = ws.get(ci)
                for tt in range(NTT):
                    bank, bb = next_bank()
                    for kk in range(8):
                        S.op("pe", lambda e: e.matmul(bank[:, :], lhsT=HT[:, kk, 3 + tt * 128:3 + (tt + 1) * 128], rhs=wc[:, kk, :],
                                                      start=(kk == 0), stop=(kk == 7)), inc=(kk == 7), reads=[bw, HT_b[tt]], writes=[bb])
                    if ci == 3:
                        S.op("act", lambda e: e.activation(out=VA[:, tt, :, 0:128], in_=bank[:, :].rearrange("p (h v) -> p h v", h=4), func=AF.Copy),
                             reads=[bb], writes=[b_va[tt]])
                    else:
                        S.op("act", lambda e: e.activation(out=OG[:, tt, :], in_=bank[:, :], func=AF.Sigmoid), reads=[bb], writes=[b_og[tt]])
            wc, bw = ws.get(5)
            bank, bb = next_bank()
            for tt in range(NTT):
                for kk in range(8):
                    S.op("pe", lambda e: e.matmul(bank[:, tt * 8:(tt + 1) * 8], lhsT=HT[:, kk, 3 + tt * 128:3 + (tt + 1) * 128], rhs=wc[:, kk, :],
                                                  start=(kk == 0), stop=(kk == 7)), inc=(kk == 7), reads=[bw, HT_b[tt]], writes=[bb])
            S.op("dve", lambda e: e.tensor_copy(out=GR, in_=bank[:, 0:128]), reads=[bb], writes=[b_gr])

            if cfg.get("debug"):
                dbq = S.dma_sem("dbq")
                b_dbg = S.buf("dbg")
                S.dma("sp", dbq, dr["dbg_u"], view(A0, 48 * KB, 16 * KB, BF16), reads=b_us, writes=[b_dbg])
                S.dma("sp", dbq, dr["dbg_qk"], view(A0, 0, 32 * KB, BF16), reads=b_qk, writes=[b_dbg])
                S.dma("sp", dbq, dr["dbg_v"], view(A2, 0, 16512, BF16), reads=b_va, writes=[b_dbg])
                S.dma("sp", dbq, dr["dbg_o"], view(A0, 32 * KB, 16 * KB, BF16), reads=b_og, writes=[b_dbg])
                S.dma("sp", dbq, dr["dbg_if"], GR, reads=[b_gr], writes=[b_dbg])
                pass

            if stop == "win":
                quiesce(); return
            xfer("e_a0", view(A0, 0, 64 * KB, F32), b_qk + b_og + b_us)
            xfer("e_va", view(A2, 0, 17024, F32), b_va + [b_gr])
            b_yc = S.bufs(NTT, "yc")
            S.handoff(b_yc, HT_b)

            MS = SCR
            def garr(i):
                return view(MS, i * 256, 256, F32).rearrange("p (m h) -> p m h", h=4)
            nlf, ig, nF, g, Gm, R0, Rr, w0, wv, clamp, dl, ep, incl, t1 = [garr(i) for i in range(14)]
            sm = view(MS, 14 * 256, 256, F32)
            gcol = sm[:, 0:1]; dm = sm[:, 4:8]; rec = sm[:, 8:12]; ss = sm[:, 12:16]; rs4 = sm[:, 16:20]
            Gb = view(MS, 15 * 256, 512, F32)
            b_g = S.buf("gates")
            b_sm = S.buf("sm")
            KT = view(MS, 4608, 16512, BF16)
            kTok = KT[:, 0:16 * 512].rearrange("p (m d) -> p m d", m=16)
            Cin = KT[:, 0:64 * 129].rearrange("p (c v) -> p c v", c=64)
            b_kt = S.bufs(16, "kt")
            CL = view(MS, 4608 + 16512, 64 * 129 * 4, F32).rearrange("p (c v) -> p c v", c=64)
            b_cl = S.bufs(16, "cl")
            T0 = MS + 4608 + 16512 + 64 * 129 * 4
            Ct = view(T0, 0, 2064, F32).rearrange("p (h v) -> p h v", h=4)
            tmpC = view(T0, 2064, 2064, F32).rearrange("p (h v) -> p h v", h=4)
            b_ct = S.buf("ct"); b_tc = S.buf("tmpc")
            Vw = [view(T0, 4128 + i * 1032, 1032, BF16).rearrange("p (h v) -> p h v", h=4) for i in range(2)]
            b_vw = S.bufs(2, "vw")
            PT = [view(T0, 6192 + i * 256, 256, BF16) for i in range(2)]
            b_pt = S.bufs(2, "pt")
            hbuf = [view(T0, 6704 + i * 512, 512, F32) for i in range(4)]
            b_hb = S.bufs(4, "hb")
            hn = [view(T0, 8752 + i * 256, 256, BF16) for i in range(2)]
            b_hn = S.bufs(2, "hn")
            junk2 = view(T0, 9264, 256, BF16)
            assert T0 + 9520 <= A2 + A2_B, (T0 + 9520 - A2 - A2_B)

            handoff = S.handoff
            handoff([b_g, b_sm] + b_kt + b_cl + [b_ct, b_tc] + b_vw + b_pt + b_hb + b_hn, b_stage + [b_acc, b_junk, b_xh] + b_xnb)
            GRv = GR.rearrange("p (m c) -> p m c", c=8)

            def bc_m(ap4):
                return bass.AP(ap4.tensor, ap4.offset, [list(ap4.ap[0]), [0, 16], list(ap4.ap[1])])

            def bc_last(ap, n):
                return bass.AP(ap.tensor, ap.offset, [list(x) for x in ap.ap] + [[0, n]])

            def flat(a):
                return a.rearrange("p m h -> p (m h)")
            LN_S = -0.5 * float(np.log(128.0))
            S.op("pool", lambda e: e.memset(view(MS, 0, 4608, F32), 0.0), writes=[b_g, b_sm])
            S.op("dve", lambda e: e.tensor_tensor(out=ig, in0=GRv[:, :, 0:4], in1=bc_m(bif[:, 0:4]), op=ALU.add), reads=[b_gr, b_prm], writes=[b_g])
            S.op("dve", lambda e: e.tensor_tensor(out=t1, in0=GRv[:, :, 4:8], in1=bc_m(bif[:, 4:8]), op=ALU.add), reads=[b_gr, b_prm], writes=[b_g])
            S.op("act", lambda e: e.activation(out=t1, in_=t1, func=AF.Exp, scale=-1.0), reads=[b_g], writes=[b_g])
            S.op("act", lambda e: e.activation(out=nlf, in_=t1, func=AF.Ln, bias=1.0), reads=[b_g], writes=[b_g])
            bankA, bbA = next_bank()
            S.op("pe", lambda e: e.matmul(bankA[:, 0:64], lhsT=tri_f, rhs=flat(nlf), start=True, stop=True), reads=[b_g, b_cst], writes=[bbA])
            bankB, bbB = next_bank()
            S.op("pe", lambda e: e.matmul(bankB[:, 0:64], lhsT=ones_f, rhs=flat(nlf), start=True, stop=True), reads=[b_g, b_cst], writes=[bbB])
            bA = bankA[:, 0:64].rearrange("p (m h) -> p m h", h=4)
            bB = bankB[:, 0:64].rearrange("p (m h) -> p m h", h=4)
            for h in range(4):
                S.op("dve", lambda e: e.tensor_tensor_scan(out=incl[:, :, h], data0=ones_f[:, 0:16], data1=bB[:, :, h], initial=0.0,
                                                           op0=ALU.mult, op1=ALU.add), reads=[bbB, b_cst, b_g], writes=[b_g])
            S.op("dve", lambda e: e.tensor_tensor(out=t1, in0=incl, in1=bB, op=ALU.subtract), reads=[b_g, bbB], writes=[b_g])
            S.op("dve", lambda e: e.tensor_tensor(out=nF, in0=t1, in1=bA, op=ALU.add), reads=[b_g, bbA], writes=[b_g])
            S.op("dve", lambda e: e.tensor_tensor(out=g, in0=ig, in1=nF, op=ALU.add), reads=[b_g], writes=[b_g])
            bankT, bbT = next_bank()
            S.op("pe", lambda e: e.transpose(bankT[0:64, 0:128], flat(g), ident_f), reads=[b_g, b_cst], writes=[bbT])
            S.op("dve", lambda e: e.tensor_reduce(out=gcol[0:64, :], in_=bankT[0:64, 0:128], axis=AX.X, op=ALU.max), reads=[bbT], writes=[b_sm])
            S.op("dve", lambda e: e.tensor_copy(out=Gb[0:64, :], in_=bc_last(gcol[0:64, 0:1], 128)[:, 0, :]), reads=[b_sm], writes=[b_sm])
            bankG, bbG = next_bank()
            S.op("pe", lambda e: e.matmul(bankG[:, 0:64], lhsT=Gb[0:64, :], rhs=ident_f[0:64, 0:64], start=True, stop=True), reads=[b_sm, b_cst], writes=[bbG])
            S.op("dve", lambda e: e.tensor_copy(out=flat(Gm), in_=bankG[:, 0:64]), reads=[bbG], writes=[b_g])
            for h in range(4):
                S.op("dve", lambda e: e.tensor_tensor_scan(out=R0[:, :, h], data0=Gm[:, :, h], data1=Gm[:, :, h], initial=-1e30,
                                                           op0=ALU.max, op1=ALU.max), reads=[b_g], writes=[b_g])
            S.op("dve", lambda e: e.tensor_tensor(out=t1, in0=g, in1=R0, op=ALU.subtract), reads=[b_g], writes=[b_g])
            S.op("act", lambda e: e.activation(out=w0, in_=t1, func=AF.Exp, bias=LN_S), reads=[b_g], writes=[b_g])
            S.op("dve", lambda e: e.tensor_tensor(out=t1[:, 1:16, :], in0=R0[:, 0:15, :], in1=R0[:, 1:16, :], op=ALU.subtract), reads=[b_g], writes=[b_g])
            S.op("act", lambda e: e.activation(out=dl[:, 1:16, :], in_=t1[:, 1:16, :], func=AF.Exp), reads=[b_g], writes=[b_g])
            for m in range(16):
                cs = slice(m * 128, (m + 1) * 128)
                bank, bb = next_bank()
                pb = bank[:, 0:256].bitcast(BF16)
                for h in range(4):
                    S.op("pe", lambda e: e.transpose(pb[:, h * 128:(h + 1) * 128], QK[:, 4 + h, cs], ident_b), inc=(h == 3),
                         reads=[b_qk[4 + h], b_cst], writes=[bb])
                S.op("act", lambda e: e.activation(out=kTok[:, m, :], in_=pb, func=AF.Copy), reads=[bb], writes=[b_kt[m]])
                vw = Vw[m % 2]; bv = b_vw[m % 2]
                S.op("dve", lambda e: e.tensor_tensor(out=vw, in0=VA[:, m, :, :], in1=bc_last(w0[:, m, :], 129), op=ALU.mult),
                     reads=[b_va[m], b_g], writes=[bv])
                for h in range(4):
                    bank, bb = next_bank()
                    S.op("pe", lambda e: e.matmul(bank[:, 0:129], lhsT=kTok[:, m, h * 128:(h + 1) * 128], rhs=vw[:, h, :], start=True, stop=True),
                         reads=[b_kt[m], bv], writes=[bb])
                    if h % 2 == 0:
                        S.op("act", lambda e: e.activation(out=CL[:, m * 4 + h, :], in_=bank[:, 0:129], func=AF.Copy), reads=[bb], writes=[b_cl[m]])
                    else:
                        S.op("dve", lambda e: e.tensor_copy(out=CL[:, m * 4 + h, :], in_=bank[:, 0:129]), reads=[bb], writes=[b_cl[m]])
                if m == 0:
                    S.op("dve", lambda e: e.tensor_copy(out=Ct, in_=CL[:, 0:4, :]), reads=[b_cl[0]], writes=[b_ct])
                else:
                    S.op("dve", lambda e: e.tensor_tensor(out=Ct, in0=Ct, in1=bc_last(dl[:, m, :], 129), op=ALU.mult), reads=[b_ct, b_g], writes=[b_ct])
                    S.op("dve", lambda e: e.tensor_tensor(out=Ct, in0=Ct, in1=CL[:, m * 4:m * 4 + 4, :], op=ALU.add), reads=[b_ct, b_cl[m]], writes=[b_ct])
            ml_extra = []
            xfer("e_gates", view(MS, 0, 4608, F32), [b_g, b_sm])
            xfer("e_cl", view(MS, 4608 + 16512, 64 * 129 * 4, F32), b_cl)
            if IMP:
                S.mute = False
            mst = sm[:, 20:24]
            exq = S.dma_sem(f"exq{l}")
            if mode == "A":
                S.dma("sp", oq, dr["summ"][:, 32:548], Ct.rearrange("p h v -> p (h v)"), reads=[b_ct], writes=[b_out])
                S.dma("sp", oq, dr["summ"][:, 548:552], R0[:, 15, :], reads=[b_g], writes=[b_out])
                S.dma("sp", oq, dr["summ"][:, 552:556], incl[:, 15, :], reads=[b_g], writes=[b_out])
            else:
                sa = dr["summ_all"]
                small = Gb.rearrange("p (j c) -> p j c", j=8)[:, :, 0:8]
                S.dma("sp", exq, small, sa[:, :, 548:556].rearrange("j p c -> p j c"), writes=[b_sm])
                S.seal(exq, [b_sm])
                cm = sm[:, 20:24]; mx = sm[:, 24:28]; ta = sm[:, 28:32]; tb = sm[:, 32:36]; r0j = sm[:, 36:40]; nfj = sm[:, 40:44]; tq = sm[:, 44:48]
                S.op("dve", lambda e: e.memset(tmpC, 0.0), reads=[b_tc], writes=[b_tc])
                S.op("dve", lambda e: e.memset(cm, 0.0), reads=[b_sm], writes=[b_sm])
                cq = [S.dma_sem(f"cq{l}_{i}") for i in range(2)]
                Cj = [Ct, view(T0, 4128, 2064, F32).rearrange("p (h v) -> p h v", h=4)]
                b_cj = [b_ct, S.buf("cj1")]
                S.handoff([b_cj[1]], b_vw)
                for j in range(8):
                    cj = Cj[j % 2]; bcj = b_cj[j % 2]
                    S.dma("sp", cq[j % 2], cj.rearrange("p h v -> p (h v)"), sa[j, :, 32:548], writes=[bcj])
                    S.op("dve", lambda e: e.tensor_scalar(out=r0j, in0=small[:, j, 0:4], scalar1=pred[:, j:j + 1], scalar2=pmask[:, j:j + 1],
                                                          op0=ALU.mult, op1=ALU.add), reads=[b_sm, b_cst], writes=[b_sm])
                    S.op("dve", lambda e: e.tensor_scalar(out=nfj, in0=small[:, j, 4:8], scalar1=pred[:, j:j + 1], scalar2=None, op0=ALU.mult),
                         reads=[b_sm, b_cst], writes=[b_sm])
                    S.op("dve", lambda e: e.tensor_tensor(out=mx, in0=cm, in1=r0j, op=ALU.max), reads=[b_sm], writes=[b_sm])
                    S.op("dve", lambda e: e.tensor_tensor(out=tq, in0=cm, in1=mx, op=ALU.subtract), reads=[b_sm], writes=[b_sm])
                    S.op("act", lambda e: e.activation(out=ta, in_=tq, func=AF.Exp), reads=[b_sm], writes=[b_sm])
                    S.op("dve", lambda e: e.tensor_tensor(out=tq, in0=r0j, in1=mx, op=ALU.subtract), reads=[b_sm], writes=[b_sm])
                    S.op("act", lambda e: e.activation(out=tb, in_=tq, func=AF.Exp), reads=[b_sm], writes=[b_sm])
                    S.op("dve", lambda e: e.tensor_tensor(out=tmpC, in0=tmpC, in1=bc_last(ta, 129), op=ALU.mult), reads=[b_tc, b_sm], writes=[b_tc])
                    S.op("dve", lambda e: e.tensor_tensor(out=cj, in0=cj, in1=bc_last(tb, 129), op=ALU.mult), reads=[bcj, b_sm], writes=[bcj])
                    S.op("dve", lambda e: e.tensor_tensor(out=tmpC, in0=tmpC, in1=cj, op=ALU.add), reads=[b_tc, bcj], writes=[b_tc])
                    S.op("dve", lambda e: e.tensor_tensor(out=cm, in0=mx, in1=nfj, op=ALU.subtract), reads=[b_sm], writes=[b_sm])
            if mode != "A":
                S.op("dve", lambda e: e.tensor_tensor(out=Rr, in0=R0, in1=bc_m(mst), op=ALU.max), reads=[b_g, b_sm], writes=[b_g])
                S.op("dve", lambda e: e.tensor_tensor(out=t1, in0=g, in1=Rr, op=ALU.subtract), reads=[b_g], writes=[b_g])
                S.op("act", lambda e: e.activation(out=wv, in_=t1, func=AF.Exp, bias=LN_S), reads=[b_g], writes=[b_g])
                S.op("dve", lambda e: e.tensor_tensor(out=t1, in0=nF, in1=Rr, op=ALU.subtract), reads=[b_g], writes=[b_g])
                S.op("act", lambda e: e.activation(out=clamp, in_=t1, func=AF.Exp), reads=[b_g], writes=[b_g])
                S.op("dve", lambda e: e.tensor_tensor(out=t1, in0=R0, in1=Rr, op=ALU.subtract), reads=[b_g], writes=[b_g])
                S.op("act", lambda e: e.activation(out=ep, in_=t1, func=AF.Exp), reads=[b_g], writes=[b_g])
                S.op("dve", lambda e: e.tensor_tensor(out=t1[:, 1:16, :], in0=Rr[:, 0:15, :], in1=Rr[:, 1:16, :], op=ALU.subtract), reads=[b_g], writes=[b_g])
                S.op("dve", lambda e: e.tensor_tensor(out=t1[:, 0, :], in0=mst, in1=Rr[:, 0, :], op=ALU.subtract), reads=[b_g, b_sm], writes=[b_g])
                S.op("act", lambda e: e.activation(out=dl, in_=t1, func=AF.Exp), reads=[b_g], writes=[b_g])
                S.op("dve", lambda e: e.tensor_copy(out=Ct, in_=tmpC), reads=[b_tc], writes=[b_ct])
                b_cin = b_kt
                for m in range(16):
                    S.op("dve", lambda e: e.tensor_tensor(out=tmpC, in0=Ct, in1=bc_last(dl[:, m, :], 129), op=ALU.mult), reads=[b_ct, b_g], writes=[b_tc])
                    S.op("act", lambda e: e.activation(out=Cin[:, m * 4:m * 4 + 4, :], in_=tmpC, func=AF.Copy), reads=[b_tc], writes=b_cin)
                    S.op("dve", lambda e: e.tensor_tensor(out=Ct, in0=CL[:, m * 4:m * 4 + 4, :], in1=bc_last(ep[:, m, :], 129), op=ALU.mult),
                         reads=[b_cl[m], b_g], writes=[b_ct])
                    S.op("dve", lambda e: e.tensor_tensor(out=Ct, in0=Ct, in1=tmpC, op=ALU.add), reads=[b_ct, b_tc], writes=[b_ct])
                PT4 = [view(T0, i * 256, 256, BF16) for i in range(4)]
                hn4 = [view(T0, 1024 + i * 1024, 1024, BF16).rearrange("p (h t) -> p h t", h=4) for i in range(2)]
                hbA = view(T0, 6704, 2048, F32).rearrange("p (h t) -> p h t", h=4)
                hbB = view(T0, 3072, 2048, F32).rearrange("p (h t) -> p h t", h=4)
                hb4 = [hbA, hbB]
                b_pt4 = S.bufs(4, "pt4"); b_hn4 = S.bufs(2, "hn4"); b_hb4 = [S.bufs(4, "hbA"), S.bufs(4, "hbB")]
                b_j2 = S.buf("j2")
                b_ep = [S.buf("ep0"), S.buf("ep1")]
                S.handoff(b_pt4 + b_hn4 + b_hb4[0] + b_hb4[1] + [b_j2] + b_ep, [b_ct, b_tc, b_sm] + b_vw + b_pt + b_hb + b_hn + ([b_cj[1]] if mode != "A" else []))
                ml_extra += b_pt4 + b_hn4 + b_hb4[0] + b_hb4[1] + [b_j2] + b_ep
                smx = [sm[:, 4:20], sm[:, 48:64]]
                for m in range(16):
                    cs = slice(m * 128, (m + 1) * 128)
                    par = m % 2
                    dm_, rec_, ss_, rs_ = smx[par][:, 0:4], smx[par][:, 4:8], smx[par][:, 8:12], smx[par][:, 12:16]
                    be = b_ep[par]
                    bankS, bbS = next_bank()
                    for h in range(4):
                        S.op("pe", lambda e: e.matmul(bankS[:, h * 128:(h + 1) * 128], lhsT=QK[:, 4 + h, cs], rhs=QK[:, h, cs], start=True, stop=True),
                             inc=(h == 3), reads=[b_qk[4 + h], b_qk[h]], writes=[bbS])
                    for h in range(4):
                        S.op("dve", lambda e: e.scalar_tensor_tensor(out=PT4[h], in0=bankS[:, h * 128:(h + 1) * 128], scalar=wv[:, m, h:h + 1], in1=tri_f,
                                                                     op0=ALU.mult, op1=ALU.mult), reads=[bbS, b_g, b_cst], writes=[b_pt4[h]])
                    bN = []
                    for j in range(2):
                        bankN, bbN = next_bank()
                        bN.append((bankN, bbN))
                        for hh_ in range(2):
                            h = 2 * j + hh_
                            co = hh_ * 129
                            S.op("pe", lambda e: e.matmul(bankN[:, co:co + 129], lhsT=PT4[h], rhs=VA[:, m, h, :], start=True, stop=False), inc=False,
                                 reads=[b_pt4[h], b_va[m]], writes=[bbN])
                            S.op("pe", lambda e: e.matmul(bankN[:, co:co + 129], lhsT=QK[:, h, cs], rhs=Cin[:, m * 4 + h, :], start=False, stop=True),
                                 inc=(hh_ == 1), reads=[b_qk[h]] + b_cin, writes=[bbN])
                        den = bankN[:, 0:258].rearrange("p (h v) -> p h v", v=129)[:, :, 128]
                        S.op("act", lambda e: e.activation(out=dm_[:, 2 * j:2 * j + 2], in_=den, func=AF.Abs), reads=[bbN], writes=[be])
                        S.op("dve", lambda e: e.tensor_tensor(out=dm_[:, 2 * j:2 * j + 2], in0=dm_[:, 2 * j:2 * j + 2], in1=clamp[:, m, 2 * j:2 * j + 2], op=ALU.max),
                             reads=[be, b_g], writes=[be])
                    S.op("dve", lambda e: e.reciprocal(out=rec_, in_=dm_), reads=[be], writes=[be])
                    for h in range(4):
                        bankN, bbN = bN[h // 2]
                        co = (h % 2) * 129
                        S.op("dve", lambda e: e.scalar_tensor_tensor(out=hb4[par][:, h, :], in0=bankN[:, co:co + 128], scalar=rec_[:, h:h + 1],
                                                                     in1=OG[:, m, h * 128:(h + 1) * 128], op0=ALU.mult, op1=ALU.mult),
                             reads=[bbN, be, b_og[m]], writes=[b_hb4[par][h]])
                        S.op("act", lambda e: e.activation(out=junk2, in_=hb4[par][:, h, :], func=AF.Square, accum_out=ss_[:, h:h + 1]),
                             reads=[b_hb4[par][h]], writes=[b_j2, be])
                    S.op("dve", lambda e: e.tensor_scalar(out=rs_, in0=ss_, scalar1=1.0 / 128, scalar2=EPS, op0=ALU.mult, op1=ALU.add), reads=[be], writes=[be])
                    S.op("act", lambda e: e.activation(out=rs_, in_=rs_, func=AF.Sqrt), reads=[be], writes=[be])
                    S.op("dve", lambda e: e.reciprocal(out=rs_, in_=rs_), reads=[be], writes=[be])
                    S.op("dve", lambda e: e.tensor_tensor(out=hn4[par], in0=hb4[par], in1=bc_last(rs_, 128), op=ALU.mult),
                         reads=b_hb4[par] + [be], writes=[b_hn4[par]])
                    bankO, bbO = next_bank()
                    po = bankO[:, 0:256].bitcast(BF16).rearrange("p (h t) -> p h t", h=4)
                    for h in range(4):
                        S.op("pe", lambda e: e.transpose(po[:, h, :], hn4[par][:, h, :], ident_b), inc=(h == 3), reads=[b_hn4[par], b_cst], writes=[bbO])
                    S.op("dve", lambda e: e.tensor_tensor(out=YC[:, 4:8, cs], in0=po, in1=bc_last(mlg, 128), op=ALU.mult), reads=[bbO, b_prm], writes=[b_yc[m]])

            if cfg.get("debug"):
                S.dma("sp", dbq, dr["dbg_g"].rearrange("p (a c) -> p a c", a=16), view(MS, 0, 4096, F32).rearrange("p (a c) -> p a c", a=16), reads=[b_g, b_sm], writes=[b_dbg])

            if stop == "ml":
                quiesce(); return
            TCH = 16; NC_ = 128
            WZ = view(A0, 0, 32 * KB, BF16).rearrange("p (q i x n) -> p q i x n", q=4, i=16, x=2)
            BD = view(A0, 32 * KB, 16 * KB, BF16).rearrange("p (q j n) -> p q j n", q=4, j=16)
            CP = view(A2, 0, 34816, BF16).rearrange("p (j r x n) -> p j r x n", j=17, r=16, x=2)
            EC = view(A2, 34816, 8192, F32).rearrange("p (r c) -> p r c", r=16)
            ES = view(A2, 34816 + 8192, 8192, F32).rearrange("p (r c) -> p r c", r=16)
            ZL = view(A2, 51200, 16384, F32).rearrange("p (r x c) -> p r x c", r=16, x=2)
            ZS = view(A2, 67584, 8256, BF16).rearrange("p (r x c) -> p r x c", r=16, x=2)
            SP_ = A2 + 76032
            PW = view(SP_, 0, 2176, F32).rearrange("p (j x r) -> p j x r", j=17, x=2)
            def sc(i):
                return view(SP_, 2176 + i * 64, 64, F32)
            assert SP_ + 2176 + 30 * 64 <= A2 + A2_B
            WW = view(A0, 0, 16384, F32).rearrange("p (r x c) -> p r x c", r=16, x=2)
            ZG = view(A2, 51200, 16384, BF16).rearrange("p (q i c) -> p q i c", q=4, i=16)
            GT = A0 + 16 * KB
            GEN = A2 + 51200
            b_s5 = S.buf("s5gen")
            b_wz = S.bufs(4, "wz"); b_bd = S.bufs(4, "bd"); b_cp = S.buf("cp"); b_tab = S.buf("tab")
            b_zl = S.bufs(16, "zl"); b_ww = S.bufs(16, "ww"); b_zs = S.bufs(16, "zs"); b_zg = S.bufs(4, "zg")
            olds = [b_g, b_sm] + b_kt + b_cl + [b_ct, b_tc] + b_vw + b_pt + b_hb + b_hn + b_va + [b_gr] + b_qk + b_og + ml_extra
            handoff([b_s5, b_cp, b_tab] + b_wz + b_bd + b_zl + b_ww + b_zs + b_zg, olds)
            s5q = S.dma_sem(f"s5q{l}")
            (s_are, s_aim, s_dt, s_mag, s_th, s_t, s_sin, s_cos, s_abr, s_abi, s_den, s_zr, s_sre, s_sim, s_t2, s_magL,
             s_l128r, s_l128i, s_t3, s_m128) = [sc(i) for i in range(20)]
            zend = sc(20)[:, 0:16]
            zend = view(SP_, 2176 + 20 * 64, 128, F32).rearrange("p (r x) -> p r x", x=2)
            sst = view(SP_, 2176 + 22 * 64, 128, F32).rearrange("p (r x) -> p r x", x=2)

            def TT(out, in0, in1, op, rd=(), wr=None, eng="dve"):
                S.op(eng, lambda e: e.tensor_tensor(out=out, in0=in0, in1=in1, op=op), reads=[b_s5] + list(rd), writes=[b_s5] if wr is None else wr)

            def TS(out, in0, s1, s2, op0, op1=None, rd=(), wr=None):
                if op1 is None:
                    S.op("dve", lambda e: e.tensor_scalar(out=out, in0=in0, scalar1=s1, scalar2=None, op0=op0), reads=[b_s5] + list(rd), writes=[b_s5] if wr is None else wr)
                else:
                    S.op("dve", lambda e: e.tensor_scalar(out=out, in0=in0, scalar1=s1, scalar2=s2, op0=op0, op1=op1), reads=[b_s5] + list(rd), writes=[b_s5] if wr is None else wr)

            def AC(out, in_, func, rd=(), wr=None, **kw):
                S.op("act", lambda e: e.activation(out=out, in_=in_, func=func, **kw), reads=[b_s5] + list(rd), writes=[b_s5] if wr is None else wr)

            def cmul(o_r, o_i, a_r, a_i, b_r, b_i, t1_, t2_, rd=(), wr=None, neg_im=False):
                TT(t1_, a_r, b_r, ALU.mult, rd); TT(t2_, a_i, b_i, ALU.mult, rd)
                TT(o_r, t1_, t2_, ALU.subtract, rd, wr)
                TT(t1_, a_r, b_i, ALU.mult, rd); TT(t2_, a_i, b_r, ALU.mult, rd)
                if neg_im:
                    TT(t1_, t1_, t2_, ALU.add, rd)
                    TS(o_i, t1_, -1.0, None, ALU.mult, rd=rd, wr=wr)
                else:
                    TT(o_i, t1_, t2_, ALU.add, rd, wr)

            if IMP:
                S.mute = True
            S.op("pool", lambda e: e.memset(view(SP_, 0, 4864, F32), 0.0), writes=[b_s5])
            araw = view(GEN, 0, 1024, F32)
            S.dma("sp", s5q, araw[0:16, 0:128], dr["s5_a_re"][l].rearrange("(r gl) n -> r (gl n)", gl=2), writes=[b_s5])
            S.dma("sp", s5q, araw[0:16, 128:256], dr["s5_a_im"][l].rearrange("(r gl) n -> r (gl n)", gl=2), writes=[b_s5])
            ldt = dr["s5_log_dt"][l]
            for gl in range(2):
                S.dma("sp", s5q, s_dt[gl * 64:(gl + 1) * 64, :], bass.AP(ldt.tensor, ldt.offset + gl, [[0, 64], [2, 16]]), writes=[b_s5])
            Bsm = [view(GEN, 1024 + x * 1024, 1024, F32).rearrange("p (r c) -> p r c", r=16) for x in range(2)]
            for x, nm in enumerate(["s5_b_re", "s5_b_im"]):
                bsrc = dr[nm][l]
                for gl in range(2):
                    S.dma("sp", s5q, Bsm[x][gl * 64:(gl + 1) * 64, :, :],
                          bass.AP(bsrc.tensor, bsrc.offset + gl * 1024, [[16, 64], [2048, 16], [1, 16]]), writes=[b_s5])
            Craw = [view(GEN, 3072 + x * 1024, 1024, F32).rearrange("p (q n) -> p q n", q=4) for x in range(2)]
            for x, nm in enumerate(["s5_c_re", "s5_c_im"]):
                csrc = dr[nm][l]
                S.dma("sp", s5q, Craw[x], bass.AP(csrc.tensor, csrc.offset, [[64, 128], [8192, 4], [1, 64]]), writes=[b_s5])
            S.seal(s5q, [b_s5])
            bank, bb = next_bank()
            S.op("pe", lambda e: e.transpose(bank[:, 0:16], araw[0:16, 0:128], ident_f[0:16, 0:16]), reads=[b_s5, b_cst], writes=[bb])
            S.op("pe", lambda e: e.transpose(bank[:, 16:32], araw[0:16, 128:256], ident_f[0:16, 0:16]), reads=[b_s5, b_cst], writes=[bb])
            S.op("dve", lambda e: e.tensor_copy(out=s_are, in_=bank[:, 0:16]), reads=[bb], writes=[b_s5])
            S.op("dve", lambda e: e.tensor_copy(out=s_aim, in_=bank[:, 16:32]), reads=[bb], writes=[b_s5])
            PI = float(np.pi)
            AC(s_dt, s_dt, AF.Exp)
            TT(s_t, s_are, s_dt, ALU.mult)
            AC(s_mag, s_t, AF.Exp)
            AC(s_magL, s_t, AF.Exp, scale=float(TCH))
            AC(s_m128, s_t, AF.Exp, scale=float(TCH * NC_))
            TT(s_th, s_aim, s_dt, ALU.mult)
            for thr in (1.0, 3.0, 5.0, 7.0):
                TS(s_t, s_th, thr * PI, -2.0 * PI, ALU.is_gt, ALU.mult)
                if thr == 1.0:
                    TT(s_t2, s_th, s_t, ALU.add)
                else:
                    TT(s_t2, s_t2, s_t, ALU.add)
            AC(s_sin, s_t2, AF.Sin)
            TS(s_t3, s_t2, 0.5 * PI, None, ALU.add)
            TS(s_t, s_t3, PI, -2.0 * PI, ALU.is_gt, ALU.mult)
            TT(s_t3, s_t3, s_t, ALU.add)
            AC(s_cos, s_t3, AF.Sin)
            TT(s_abr, s_mag, s_cos, ALU.mult); TT(s_abi, s_mag, s_sin, ALU.mult)
            TT(s_t, s_are, s_are, ALU.mult); TT(s_t2, s_aim, s_aim, ALU.mult); TT(s_den, s_t, s_t2, ALU.add)
            S.op("dve", lambda e: e.reciprocal(out=s_den, in_=s_den), reads=[b_s5], writes=[b_s5])
            TS(s_zr, s_abr, -1.0, None, ALU.add)
            TT(s_t, s_zr, s_are, ALU.mult); TT(s_t2, s_abi, s_aim, ALU.mult); TT(s_t, s_t, s_t2, ALU.add); TT(s_sre, s_t, s_den, ALU.mult)
            TT(s_t, s_abi, s_are, ALU.mult); TT(s_t2, s_zr, s_aim, ALU.mult); TT(s_t, s_t, s_t2, ALU.subtract); TT(s_sim, s_t, s_den, ALU.mult)
            S.op("dve", lambda e: e.memset(PW[:, 0, 0, :], 1.0), reads=[b_s5], writes=[b_s5])
            S.op("dve", lambda e: e.memset(PW[:, 0, 1, :], 0.0), reads=[b_s5], writes=[b_s5])
            S.op("dve", lambda e: e.tensor_copy(out=PW[:, 1, 0, :], in_=s_abr), reads=[b_s5], writes=[b_s5])
            S.op("dve", lambda e: e.tensor_copy(out=PW[:, 1, 1, :], in_=s_abi), reads=[b_s5], writes=[b_s5])
            pt1 = view(GEN, 5120, 1024, F32).rearrange("p (j r) -> p j r", r=16)
            pt2 = view(GEN, 6144, 1024, F32).rearrange("p (j r) -> p j r", r=16)
            kk_ = 1
            while kk_ < 16:
                def bj(a):
                    return bass.AP(a.tensor, a.offset, [list(a.ap[0]), [0, kk_], list(a.ap[1])])
                cmul(PW[:, kk_ + 1:2 * kk_ + 1, 0, :], PW[:, kk_ + 1:2 * kk_ + 1, 1, :], PW[:, 1:kk_ + 1, 0, :], PW[:, 1:kk_ + 1, 1, :],
                     bj(PW[:, kk_, 0, :]), bj(PW[:, kk_, 1, :]), pt1[:, 0:kk_, :], pt2[:, 0:kk_, :])
                kk_ *= 2
            S.op("dve", lambda e: e.reciprocal(out=s_t, in_=s_magL), reads=[b_s5], writes=[b_s5])
            TT(EC[:, :, 0], PW[:, 16, 0, :], s_t, ALU.mult, wr=[b_s5, b_tab]); TT(ES[:, :, 0], PW[:, 16, 1, :], s_t, ALU.mult, wr=[b_s5, b_tab])
            et1 = view(GEN, 7168, 4096, F32).rearrange("p (r c) -> p r c", r=16)
            et2 = view(GEN, 11264, 4096, F32).rearrange("p (r c) -> p r c", r=16)
            kk_ = 1
            while kk_ < NC_:
                cmul(EC[:, :, kk_:2 * kk_], ES[:, :, kk_:2 * kk_], EC[:, :, 0:kk_], ES[:, :, 0:kk_],
                     bc_last(EC[:, :, kk_ - 1], kk_), bc_last(ES[:, :, kk_ - 1], kk_), et1[:, :, 0:kk_], et2[:, :, 0:kk_], rd=[b_tab], wr=[b_s5, b_tab])
                kk_ *= 2
            TT(s_l128r, EC[:, :, NC_ - 1], s_m128, ALU.mult, rd=[b_tab]); TT(s_l128i, ES[:, :, NC_ - 1], s_m128, ALU.mult, rd=[b_tab])
            Cin_ = [view(GEN, 5120 + x * 2048, 2048, F32).rearrange("p (q n) -> p q n", q=4) for x in range(2)]
            Cp = [view(GEN, 9216 + x * 2048, 2048, F32).rearrange("p (r n) -> p r n", r=16) for x in range(2)]
            ct1 = view(GEN, 13312, 2048, F32).rearrange("p (r n) -> p r n", r=16)
            ct2 = view(A0, 0, 2048, F32).rearrange("p (r n) -> p r n", r=16)
            for x in range(2):
                TS(Cin_[x][:, :, 0:64], Craw[x], par01[:, 0:1], None, ALU.mult, rd=[b_cst])
                TS(Cin_[x][:, :, 64:128], Craw[x], par01[:, 1:2], None, ALU.mult, rd=[b_cst])
                bank, bb = next_bank()
                for q in range(4):
                    S.op("pe", lambda e: e.transpose(bank[:, q * 128:(q + 1) * 128], Cin_[x][:, q, :], ident_f), inc=(q == 3), reads=[b_s5, b_cst], writes=[bb])
                S.op("dve", lambda e: e.tensor_copy(out=Cp[x].rearrange("p r n -> p (r n)"), in_=bank[:, :]), reads=[bb], writes=[b_s5])
            for j in range(17):
                pr = bc_last(PW[:, j, 0, :], 32); pi_ = bc_last(PW[:, j, 1, :], 32)
                TT(ct1, Cp[0], pr, ALU.mult); TT(ct2, Cp[1], pi_, ALU.mult, rd=b_wz, wr=[b_s5] + b_wz)
                TT(CP[:, j, :, 0, :], ct1, ct2, ALU.subtract, wr=[b_s5, b_cp])
                TT(ct1, Cp[0], pi_, ALU.mult); TT(ct2, Cp[1], pr, ALU.mult, rd=b_wz, wr=[b_s5] + b_wz)
                S.op("dve", lambda e: e.scalar_tensor_tensor(out=CP[:, j, :, 1, :], in0=ct1, scalar=-1.0, in1=ct2, op0=ALU.mult, op1=ALU.subtract),
                     reads=[b_s5], writes=[b_s5, b_cp])

            BB = [view(GEN, 5120 + x * 2048, 2048, F32).rearrange("p (r n) -> p r n", r=16) for x in range(2)]
            BBb = [view(GEN, 9216 + x * 1024, 1024, BF16).rearrange("p (r n) -> p r n", r=16) for x in range(2)]
            bt1 = view(GEN, 11264, 1024, F32).rearrange("p (r c) -> p r c", r=16)
            bt2 = view(GEN, 12288, 1024, F32).rearrange("p (r c) -> p r c", r=16)
            for x in range(2):
                S.op("dve", lambda e: e.memset(BB[x], 0.0), reads=[b_s5], writes=[b_s5])
            sre_b = bc_last(s_sre, 16); sim_b = bc_last(s_sim, 16)
            TT(bt1, Bsm[0], sre_b, ALU.mult); TT(bt2, Bsm[1], sim_b, ALU.mult)
            for gl in range(2):
                ps_ = slice(gl * 64, (gl + 1) * 64)
                TT(BB[0][ps_, :, gl * 16:(gl + 1) * 16], bt1[ps_], bt2[ps_], ALU.subtract)
            TT(bt1, Bsm[1], sre_b, ALU.mult); TT(bt2, Bsm[0], sim_b, ALU.mult)
            for gl in range(2):
                ps_ = slice(gl * 64, (gl + 1) * 64)
                TT(BB[1][ps_, :, gl * 16:(gl + 1) * 16], bt1[ps_], bt2[ps_], ALU.add)
            for x in range(2):
                S.op("dve", lambda e: e.tensor_copy(out=BBb[x], in_=BB[x]), reads=[b_s5], writes=[b_s5])
            bdt = view(GEN, 13312, 512, F32)
            for j in range(16):
                bank, bb = next_bank()
                for q in range(4):
                    for x in range(2):
                        S.op("pe", lambda e: e.matmul(bank[:, q * 128:(q + 1) * 128], lhsT=BBb[x][:, 4 * q:4 * q + 4, :].rearrange("p r n -> p (r n)"),
                                                      rhs=CP[:, j, 4 * q:4 * q + 4, x, :], start=(x == 0), stop=(x == 1)), inc=(q == 3 and x == 1),
                             reads=[b_s5, b_cp], writes=[bb])
                if j == 0:
                    for q in range(4):
                        S.op("dve", lambda e: e.tensor_tensor(out=bdt, in0=bank[:, q * 128:(q + 1) * 128], in1=bdm, op=ALU.mult), reads=[bb, b_cst, b_s5], writes=[b_s5])
                        S.op("dve", lambda e: e.scalar_tensor_tensor(out=BD[:, q, 0, :], in0=ident_f, scalar=dcol[:, q:q + 1], in1=bdt, op0=ALU.mult, op1=ALU.add),
                             reads=[b_s5, b_cst, b_prm], writes=[b_bd[q]])
                else:
                    bdm_b = bass.AP(bdm.tensor, bdm.offset, [list(bdm.ap[0]), [0, 4], list(bdm.ap[1])])
                    S.op("dve", lambda e: e.tensor_tensor(out=BD[:, :, j, :], in0=bank[:, :].rearrange("p (q n) -> p q n", q=4), in1=bdm_b, op=ALU.mult),
                         reads=[bb, b_cst], writes=b_bd)
            mt1 = view(GEN, 13824, 2048, F32).rearrange("p (r n) -> p r n", r=16)
            mt2 = view(GEN, 1024, 2048, F32).rearrange("p (r n) -> p r n", r=16)
            MB = [view(GEN, 3072 + x * 1024, 1024, BF16).rearrange("p (r n) -> p r n", r=16) for x in range(2)]
            for i in range(16):
                j = 15 - i
                pr = bc_last(PW[:, j, 0, :], 32); pi_ = bc_last(PW[:, j, 1, :], 32)
                TT(mt1, BB[0], pr, ALU.mult); TT(mt2, BB[1], pi_, ALU.mult); TT(MB[0], mt1, mt2, ALU.subtract)
                TT(mt1, BB[0], pi_, ALU.mult); TT(mt2, BB[1], pr, ALU.mult); TT(MB[1], mt1, mt2, ALU.add)
                bank, bb = next_bank()
                pb = bank[:, :].bitcast(BF16).rearrange("p (q x n) -> p q x n", q=4, x=2)
                for q in range(4):
                    for x in range(2):
                        S.op("pe", lambda e: e.transpose(pb[:, q, x, :], MB[x][:, 4 * q:4 * q + 4, :].rearrange("p r n -> p (r n)"), ident_b),
                             inc=(q == 3 and x == 1), reads=[b_s5, b_cst], writes=[bb])
                S.op("act", lambda e: e.activation(out=WZ[:, :, i, :, :], in_=pb, func=AF.Copy), reads=[bb], writes=b_wz)
            if stop == "s5gen":
                quiesce(); return
            if EXP:
                xfer("e_cp", view(A2, 0, 34816, F32), [b_cp])
                xfer("e_bd", view(A0, 32 * KB, 16 * KB, F32), b_bd)
                xfer("e_tab", view(A2, 34816, 16384, F32), [b_tab])
            handoff(b_zl, b_zl + [b_s5])
            for q in range(4):
                for rr in range(4):
                    r = 4 * q + rr
                    bank, bb = next_bank()
                    for x in range(2):
                        col = x * 128
                        for i in range(16):
                            S.op("pe", lambda e: e.matmul(bank[:, col:col + 128], lhsT=WZ[32 * rr:32 * rr + 32, q, i, x, :], rhs=US[32 * rr:32 * rr + 32, q, i, :],
                                                          start=(i == 0), stop=(i == 15), tile_position=(32 * rr, 0)), inc=(i == 15 and x == 1),
                                 reads=[b_wz[q], b_us[q]], writes=[bb])
                    S.op("act", lambda e: e.activation(out=ZL[:, r, :, :].rearrange("p x c -> p (x c)"), in_=bank[:, 0:256], func=AF.Copy),
                         reads=[bb], writes=[b_zl[r]])
            if cfg.get("debug"):
                S.dma("sp", dbq, dr["dbg_zl"], view(A2, 51200, 16384, F32), reads=b_zl, writes=[b_dbg])
                S.dma("sp", dbq, dr["dbg_sc"], view(SP_, 0, 4864, F32), reads=[b_s5], writes=[b_dbg])
                S.dma("sp", dbq, dr["dbg_cp"], view(A2, 0, 34816, BF16), reads=[b_cp], writes=[b_dbg])
                S.dma("sp", dbq, dr["dbg_bd"], view(A0, 32 * KB, 16 * KB, BF16), reads=b_bd, writes=[b_dbg])
                S.dma("sp", dbq, dr["dbg_wz"], view(A0, 0, 32 * KB, BF16), reads=b_wz, writes=[b_dbg])
            handoff(b_ww, b_wz + b_ww)
            dt1 = view(GT, 0, 8192, F32).rearrange("p (r c) -> p r c", r=16)
            dt2 = view(GT, 8192, 8192, F32).rearrange("p (r c) -> p r c", r=16)
            b_dt = S.buf("dt"); handoff([b_dt], b_wz)
            magL_b = bc_last(s_magL, NC_)

            def scan_and_mod(init_ap, b_init, final):
                for r in range(16):
                    for x in range(2):
                        ini = 0.0 if init_ap is None else init_ap[:, r, x:x + 1]
                        S.op("dve", lambda e: e.tensor_tensor_scan(out=ZL[:, r, x, :], data0=magL_b[:, r, :], data1=WW[:, r, x, :], initial=ini,
                                                                   op0=ALU.mult, op1=ALU.add), reads=[b_ww[r], b_s5] + ([b_init] if b_init else []), writes=[b_zl[r]])
                if not final:
                    cmul(zend[:, :, 0], zend[:, :, 1], ZL[:, :, 0, NC_ - 1], ZL[:, :, 1, NC_ - 1], EC[:, :, NC_ - 1], ES[:, :, NC_ - 1], s_t, s_t2,
                         rd=b_zl + [b_tab])
                else:
                    S.op("dve", lambda e: e.tensor_tensor(out=dt1, in0=EC, in1=ZL[:, :, 0, :], op=ALU.mult), reads=[b_tab] + b_zl, writes=[b_dt])
                    S.op("dve", lambda e: e.tensor_tensor(out=dt2, in0=ES, in1=ZL[:, :, 1, :], op=ALU.mult), reads=[b_tab] + b_zl, writes=[b_dt])
                    S.op("dve", lambda e: e.tensor_tensor(out=ZS[:, :, 0, 1:NC_ + 1], in0=dt1, in1=dt2, op=ALU.subtract), reads=[b_dt], writes=b_zs)
                    S.op("dve", lambda e: e.tensor_tensor(out=dt1, in0=EC, in1=ZL[:, :, 1, :], op=ALU.mult), reads=[b_tab] + b_zl, writes=[b_dt])
                    S.op("dve", lambda e: e.tensor_tensor(out=dt2, in0=ES, in1=ZL[:, :, 0, :], op=ALU.mult), reads=[b_tab] + b_zl, writes=[b_dt])
                    S.op("dve", lambda e: e.tensor_tensor(out=ZS[:, :, 1, 1:NC_ + 1], in0=dt1, in1=dt2, op=ALU.add), reads=[b_dt], writes=b_zs)
                    S.op("dve", lambda e: e.tensor_copy(out=ZS[:, :, :, 0], in_=init_ap), reads=[b_init], writes=b_zs)

            S.op("dve", lambda e: e.tensor_tensor(out=dt1, in0=EC, in1=ZL[:, :, 0, :], op=ALU.mult), reads=[b_tab] + b_zl, writes=[b_dt])
            S.op("dve", lambda e: e.tensor_tensor(out=dt2, in0=ES, in1=ZL[:, :, 1, :], op=ALU.mult), reads=[b_tab] + b_zl, writes=[b_dt])
            S.op("dve", lambda e: e.tensor_tensor(out=WW[:, :, 0, :], in0=dt1, in1=dt2, op=ALU.add), reads=[b_dt], writes=b_ww)
            S.op("dve", lambda e: e.tensor_tensor(out=dt1, in0=EC, in1=ZL[:, :, 1, :], op=ALU.mult), reads=[b_tab] + b_zl, writes=[b_dt])
            S.op("dve", lambda e: e.tensor_tensor(out=dt2, in0=ES, in1=ZL[:, :, 0, :], op=ALU.mult), reads=[b_tab] + b_zl, writes=[b_dt])
            S.op("dve", lambda e: e.tensor_tensor(out=WW[:, :, 1, :], in0=dt1, in1=dt2, op=ALU.subtract), reads=[b_dt], writes=b_ww)
            scan_and_mod(None, None, False)
            xfer("e_ww", view(A0, 0, 16384, F32), b_ww)
            if IMP:
                xfer("e_cp", view(A2, 0, 34816, F32), [b_cp])
                xfer("e_bd", view(A0, 32 * KB, 16 * KB, F32), b_bd)
                xfer("e_tab", view(A2, 34816, 16384, F32), [b_tab])
            xfer("e_sp", view(SP_, 0, 4864, F32), [b_s5])
            if IMP:
                S.mute = False
            if cfg.get("debug"):
                S.dma("sp", dbq, dr["dbg_zend"], zend.rearrange("p r x -> p (r x)"), reads=[b_s5], writes=[b_dbg])
            if mode == "A":
                S.dma("sp", oq, dr["summ"][:, 0:32], zend.rearrange("p r x -> p (r x)"), reads=[b_s5], writes=[b_out])
                S.wait_all("sp", [b_out])
            if mode != "A":
                sa = dr["summ_all"]
                zall = view(GT, 0, 1024, F32).rearrange("p (j r x) -> p j r x", j=8, x=2)
                zq = S.dma_sem(f"zq{l}")
                S.dma("sp", zq, zall.rearrange("p j r x -> p j (r x)"), sa[:, :, 0:32].rearrange("j p c -> p j c"), reads=[b_dt], writes=[b_dt])
                S.op("dve", lambda e: e.memset(sst, 0.0), reads=[b_s5], writes=[b_s5])
                ctr = sc(24)[:, 0:16]; cti = sc(25)[:, 0:16]
                for j in range(8):
                    cmul(ctr, cti, sst[:, :, 0], sst[:, :, 1], s_l128r, s_l128i, s_t, s_t2)
                    TT(ctr, ctr, zall[:, j, :, 0], ALU.add, rd=[b_dt]); TT(cti, cti, zall[:, j, :, 1], ALU.add, rd=[b_dt])
                    TT(ctr, ctr, sst[:, :, 0], ALU.subtract); TT(cti, cti, sst[:, :, 1], ALU.subtract)
                    S.op("dve", lambda e: e.scalar_tensor_tensor(out=sst[:, :, 0], in0=ctr, scalar=pred[:, j:j + 1], in1=sst[:, :, 0], op0=ALU.mult, op1=ALU.add),
                         reads=[b_s5, b_cst], writes=[b_s5])
                    S.op("dve", lambda e: e.scalar_tensor_tensor(out=sst[:, :, 1], in0=cti, scalar=pred[:, j:j + 1], in1=sst[:, :, 1], op0=ALU.mult, op1=ALU.add),
                         reads=[b_s5, b_cst], writes=[b_s5])
                scan_and_mod(sst, b_s5, True)
                if cfg.get("debug"):
                    S.dma("sp", dbq, dr["dbg_zs"], view(A2, 67584, 8256, BF16), reads=b_zs, writes=[b_dbg])
                handoff(b_zg, b_zl + b_zg)
                for q in range(4):
                    for ib in range(4):
                        bank, bb = next_bank()
                        for i4 in range(4):
                            ip = ib * 4 + i4
                            col = i4 * 128
                            for i in range(ip + 1):
                                S.op("pe", lambda e: e.matmul(bank[:, col:col + 128], lhsT=BD[:, q, ip - i, :], rhs=US[:, q, i, :], start=(i == 0), stop=False),
                                     inc=False, reads=[b_bd[q], b_us[q]], writes=[bb])
                            for rr in range(4):
                                r = 4 * q + rr
                                for x in range(2):
                                    lastw = (rr == 3 and x == 1)
                                    S.op("pe", lambda e: e.matmul(bank[32 * rr:32 * rr + 32, col:col + 128], lhsT=CP[:, ip + 1, r, x, :], rhs=ZS[:, r, x, 0:NC_],
                                                                  start=False, stop=(x == 1), tile_position=(0, 32 * rr)), inc=(lastw and i4 == 3),
                                         reads=[b_cp, b_zs[r]], writes=[bb])
                        S.op("act", lambda e: e.activation(out=ZG[:, q, ib * 4:ib * 4 + 4, :].rearrange("p i c -> p (i c)"), in_=bank[:, :], func=AF.Gelu_apprx_tanh),
                             reads=[bb], writes=[b_zg[q]])
                if cfg.get("debug"):
                    S.dma("sp", dbq, dr["dbg_zg"], view(A2, 51200, 16384, BF16), reads=b_zg, writes=[b_dbg])
                wg = dr["s5_w_glu"][l].rearrange("(k p) n -> p k n", p=128)
                wgl, bwg = wload(wg, 4, 512)
                gate = view(GT, 0, 2048, F32)
                ZZ = [view(GT, 2048 + ft * 2048, 2048, F32) for ft in range(4)]
                sqb = [view(GT, 10240 + i * 1024, 1024, BF16) for i in range(2)]
                rst = view(GT, 12288, 2048, F32)
                b_gate = S.buf("gate"); b_zz = S.bufs(4, "zz"); b_sqb = S.bufs(2, "sqb"); b_rst = S.buf("rst")
                handoff([b_gate, b_rst] + b_zz + b_sqb, [b_dt] + b_ww)
                YCv = YC[:, 0:4, :].rearrange("p f (c i) -> p f i c", i=16)
                for cb in range(4):
                    bankq, bbq = next_bank()
                    for ft in range(4):
                        bank, bb = next_bank()
                        for kk in range(4):
                            S.op("pe", lambda e: e.matmul(bank[:, :], lhsT=wgl[:, kk, ft * 128:(ft + 1) * 128], rhs=ZG[:, kk, cb * 4:cb * 4 + 4, :],
                                                          start=(kk == 0), stop=(kk == 3)), inc=(kk == 3), reads=[bwg] + b_zg, writes=[bb])
                        S.op("act", lambda e: e.activation(out=gate, in_=bank[:, :], func=AF.Sigmoid, bias=bglu[:, ft:ft + 1]), reads=[bb, b_prm], writes=[b_gate])
                        S.op("dve", lambda e: e.tensor_tensor(out=ZZ[ft], in0=ZG[:, ft, cb * 4:cb * 4 + 4, :].rearrange("p i c -> p (i c)"), in1=gate, op=ALU.mult),
                             reads=[b_zg[ft], b_gate], writes=[b_zz[ft]])
                        S.op("act", lambda e: e.activation(out=sqb[ft % 2], in_=ZZ[ft], func=AF.Square), reads=[b_zz[ft]], writes=[b_sqb[ft % 2]])
                        S.op("pe", lambda e: e.matmul(bankq[:, :], lhsT=ones_b, rhs=sqb[ft % 2], start=(ft == 0), stop=(ft == 3)), inc=True,
                             reads=[b_sqb[ft % 2], b_cst], writes=[bbq])
                    S.op("dve", lambda e: e.tensor_scalar(out=rst, in0=bankq[:, :], scalar1=1.0 / 512, scalar2=EPS, op0=ALU.mult, op1=ALU.add), reads=[bbq], writes=[b_rst])
                    S.op("act", lambda e: e.activation(out=rst, in_=rst, func=AF.Sqrt), reads=[b_rst], writes=[b_rst])
                    S.op("dve", lambda e: e.reciprocal(out=rst, in_=rst), reads=[b_rst], writes=[b_rst])
                    for ft in range(4):
                        S.op("dve", lambda e: e.scalar_tensor_tensor(out=YCv[:, ft, cb * 4:cb * 4 + 4, :], in0=ZZ[ft].rearrange("p (i c) -> p i c", i=4),
                                                                     scalar=outg[:, ft:ft + 1], in1=rst.rearrange("p (i c) -> p i c", i=4), op0=ALU.mult, op1=ALU.mult),
                             reads=[b_zz[ft], b_rst, b_prm], writes=b_yc)
                if cfg.get("debug"):
                    S.dma("sp", dbq, dr["dbg_yc"], view(A1, 0, 32 * KB, BF16), reads=b_yc, writes=[b_dbg])

            if mode != "A":
                if stop == "s5":
                    quiesce(); return
                S.handoff(X_b, b_qk + b_og + b_us + b_wz + b_bd + b_ww + [b_dt, b_gate, b_rst] + b_zz + b_sqb)
                for tt in range(NTT):
                    S.dma("sp", xq[tt], X[:, tt, :], xin_ap[tt * 128:(tt + 1) * 128, :], writes=[X_b[tt]])
                wo = dr["w_out"][l].rearrange("(k p) n -> p k n", p=128)
                ws = WStream([(wo[:, :, h * 512:(h + 1) * 512], 8, 512) for h in range(2)])
                for h in range(2):
                    wc, bw = ws.get(h)
                    for tt in range(NTT):
                        bank, bb = next_bank()
                        for kk in range(8):
                            S.op("pe", lambda e: e.matmul(bank[:, :], lhsT=YC[:, kk, tt * 128:(tt + 1) * 128], rhs=wc[:, kk, :],
                                                          start=(kk == 0), stop=(kk == 7)), inc=(kk == 7), reads=[bw, b_yc[tt]], writes=[bb])
                        S.op("dve", lambda e: e.tensor_tensor(out=X[:, tt, h * 512:(h + 1) * 512], in0=X[:, tt, h * 512:(h + 1) * 512], in1=bank[:, :], op=ALU.add),
                             reads=[bb, X_b[tt]], writes=[X_b[tt]])

                if cfg.get("dbg_x1"):
                    b_o1 = S.buf("o1")
                    for tt in range(NTT):
                        S.dma("sp", oq, xout_ap[tt * 128:(tt + 1) * 128, :], X[:, tt, :], reads=[X_b[tt]], writes=[b_o1])
                    S.wait_all("sp", [b_o1])
                    return
                if stop == "wout":
                    quiesce(); return
                S.handoff(HT_b, b_yc)
                a2_users = [b_g, b_sm] + b_kt + b_cl + [b_ct, b_tc] + b_vw + b_pt + b_hb + b_hn + [b_cp, b_tab, b_s5] + b_zl + b_zs + b_zg + b_va + [b_gr] + ml_extra
                handoff([b_junk, b_xh] + b_xnb, a2_users)
                norm_to_hT(g2, False, False)
                if cfg.get("dbg_ht2"):
                    b_o1 = S.buf("o1")
                    S.dma("sp", oq, dr["dbg_ht"], view(A1, 0, 32832, BF16)[:, 0:8 * 2051], reads=HT_b, writes=[b_o1])
                w1 = dr["w_ff1"][l].rearrange("(k p) n -> p k n", p=128)
                w2 = dr["w_ff2"][l].rearrange("(k p) n -> p k n", p=128)
                c1 = [(w1[:, :, hc * 512:(hc + 1) * 512], 8, 512) for hc in range(8)]
                c2 = [(w2[:, hc * 4:(hc + 1) * 4, :], 4, 1024) for hc in range(8)]
                chunks = [c1[0]]
                for hc in range(8):
                    if hc + 1 < 8:
                        chunks.append(c1[hc + 1])
                    chunks.append(c2[hc])
                ws = WStream(chunks)
                k.wci = 0

                def wnext():
                    r = ws.get(k.wci)
                    k.wci += 1
                    return r
                hid = [view(SCR, i * 16 * KB, 16 * KB, BF16).rearrange("p (f t) -> p f t", f=4) for i in range(2)]
                b_hid = [S.bufs(4, f"hid{i}") for i in range(2)]
                sq = [view(SCR, 32 * KB + i * 2048, 2048, F32) for i in range(2)]
                b_sq = S.bufs(2, "sq")
                k.sqi = 0
                handoff(b_hid[0] + b_hid[1] + b_sq, a2_users + [b_junk, b_xh] + b_xnb)

                def ffn1(hc):
                    wc, bw = wnext()
                    hb = hid[hc % 2]
                    for ft in range(4):
                        for nb in range(4):
                            bank, bb = next_bank()
                            for kk in range(8):
                                S.op("pe", lambda e: e.matmul(bank[:, :], lhsT=wc[:, kk, ft * 128:(ft + 1) * 128], rhs=HT[:, kk, 3 + nb * 512:3 + (nb + 1) * 512],
                                                              start=(kk == 0), stop=(kk == 7)), inc=(kk == 7), reads=[bw] + HT_b[nb * 4:nb * 4 + 4], writes=[bb])
                            si = k.sqi % 2; k.sqi += 1
                            S.op("act", lambda e: e.activation(out=sq[si], in_=bank[:, :], func=AF.Square), reads=[bb], writes=[b_sq[si]])
                            S.op("dve", lambda e: e.scalar_tensor_tensor(out=hb[:, ft, nb * 512:(nb + 1) * 512], in0=bank[:, :], scalar=0.0, in1=sq[si],
                                                                         op0=ALU.is_gt, op1=ALU.mult), reads=[bb, b_sq[si]], writes=[b_hid[hc % 2][nb]])

                def ffn2(hc):
                    wc, bw = wnext()
                    hb = hid[hc % 2]
                    for tt in range(NTT):
                        for h in range(2):
                            bank, bb = next_bank()
                            for kk in range(4):
                                S.op("pe", lambda e: e.matmul(bank[:, :], lhsT=hb[:, kk, tt * 128:(tt + 1) * 128], rhs=wc[:, kk, h * 512:(h + 1) * 512],
                                                              start=(kk == 0), stop=(kk == 3)), inc=(kk == 3), reads=[bw, b_hid[hc % 2][tt // 4]], writes=[bb])
                            S.op("dve", lambda e: e.tensor_tensor(out=X[:, tt, h * 512:(h + 1) * 512], in0=X[:, tt, h * 512:(h + 1) * 512], in1=bank[:, :], op=ALU.add),
                                 reads=[bb, X_b[tt]], writes=[X_b[tt]])

                ffn1(0)
                for hc in range(8):
                    if hc + 1 < 8:
                        ffn1(hc + 1)
                    ffn2(hc)

                if last:
                    gfin = view(SCR, 40 * KB, 4096, F32)
                    b_gf = S.buf("gfin")
                    fg = dr["final_norm_g"]
                    S.dma("sp", gq, gfin, bass.AP(fg.tensor, fg.offset, [[0, 128], [1, D]]), writes=[b_gf])
                    ot = [view(SCR, 44 * KB + i * 4096, 4096, F32) for i in range(2)]
                    b_ot = S.bufs(2, "ot")
                    handoff([b_junk], [b_junk] + b_hid[0] + b_hid[1])
                    stats_A(0)
                    for g4 in range(4):
                        if g4 + 1 < 4:
                            stats_A(g4 + 1)
                        stats_B(g4)
                        for tt in range(4 * g4, 4 * g4 + 4):
                            S.op("dve", lambda e: e.scalar_tensor_tensor(out=ot[tt % 2], in0=X[:, tt, :], scalar=rstd[:, tt:tt + 1], in1=gfin,
                                                                         op0=ALU.mult, op1=ALU.mult), reads=[X_b[tt], b_sg[g4], b_gf], writes=[b_ot[tt % 2]])
                            S.dma("sp", oq, xout_ap[tt * 128:(tt + 1) * 128, :], ot[tt % 2], reads=[b_ot[tt % 2]], writes=[b_out])
                else:
                    for tt in range(NTT):
                        S.dma("sp", oq, xout_ap[tt * 128:(tt + 1) * 128, :], X[:, tt, :], reads=[X_b[tt]], writes=[b_out])
                S.wait_all("sp", [b_out])

        layers = cfg["layers"]
        for li, l in enumerate(layers):
            layer(l, dr["xin"], dr.get("xhalo"), dr.get("xout"), last=cfg.get("final", False) and li == len(layers) - 1)
    return nc


_NC_CACHE = {}
N_CORES = 8
LAYER_KEYS = ["norm_mix_g", "w_in", "w_out", "norm_ffn_g", "w_ff1", "w_ff2", "ml_conv_w", "ml_conv_b", "ml_b_i", "ml_b_f",
              "ml_norm_g", "s5_a_re", "s5_a_im", "s5_log_dt", "s5_b_re", "s5_b_im", "s5_c_re", "s5_c_im", "s5_d", "s5_w_glu",
              "s5_b_glu", "s5_out_g"]
A_KEYS = ["norm_mix_g", "w_in", "ml_conv_w", "ml_conv_b", "ml_b_i", "ml_b_f", "s5_a_re", "s5_a_im", "s5_log_dt", "s5_b_re", "s5_b_im",
          "s5_c_re", "s5_c_im", "s5_d", "ml_norm_g", "s5_b_glu", "s5_out_g"]
B_KEYS = ["w_out", "norm_ffn_g", "w_ff1", "w_ff2", "s5_w_glu", "ml_norm_g", "s5_b_glu", "s5_out_g"]
XF_NAMES = ["e_a0", "e_va", "e_gates", "e_cl", "e_ww", "e_cp", "e_bd", "e_tab", "e_sp"]
A_CONST = ["ident", "causal", "ones", "par01", "bdmask"]
B_CONST = ["ident", "causal", "ones"]


def _get_nc(mode, final):
    key = (mode, final)
    if key not in _NC_CACHE:
        _NC_CACHE[key] = build(dict(layers=[0], nlayers=1, mode=mode, final=final, debug=False))
    return _NC_CACHE[key]


def _consts():
    par = np.zeros((128, 2), np.float32)
    par[:, 1] = (np.arange(128) // 16) % 2
    par[:, 0] = 1 - par[:, 1]
    return {"ident": np.eye(128, dtype=np.float32), "causal": np.triu(np.ones((128, 128), np.float32)),
            "ones": np.ones((128, 128), np.float32), "par01": par,
            "bdmask": np.kron(np.eye(8), np.ones((16, 16))).astype(np.float32)}


def kernel(**inputs):
    x = np.ascontiguousarray(inputs["x"], dtype=np.float32)
    nb, ls, d = x.shape
    per = ls // 4
    consts = _consts()
    cur = [np.ascontiguousarray(x[c // 4, (c % 4) * per:(c % 4 + 1) * per]) for c in range(N_CORES)]
    preds = []
    for c in range(N_CORES):
        p = np.zeros((128, 8), np.float32)
        for j in range(N_CORES):
            if j // 4 == c // 4 and j < c:
                p[:, j] = 1.0
        preds.append(p)
    depth = inputs["w_in"].shape[0]
    for l in range(depth):
        halos = [np.zeros((3, d), np.float32) if c % 4 == 0 else np.ascontiguousarray(cur[c - 1][-3:]) for c in range(N_CORES)]
        lw = {k: np.ascontiguousarray(np.asarray(inputs[k], dtype=np.float32)[l:l + 1]) for k in LAYER_KEYS}
        final = (l == depth - 1)
        ncA = _get_nc("A", False)
        mapsA = []
        for c in range(N_CORES):
            m = {"xin": cur[c], "xhalo": halos[c], "pred": preds[c]}
            m.update({k: consts[k] for k in A_CONST})
            m.update({k: lw[k] for k in A_KEYS})
            mapsA.append(m)
        resA = run_bass_kernel_spmd(ncA, mapsA, core_ids=list(range(N_CORES)))
        summ_all = np.ascontiguousarray(np.stack([np.asarray(resA.results[c]["summ"]) for c in range(N_CORES)]))
        ncB = _get_nc("B", final)
        mapsB = []
        for c in range(N_CORES):
            m = {"xin": cur[c], "pred": preds[c], "summ_all": summ_all,
                 "final_norm_g": np.ascontiguousarray(inputs["final_norm_g"], dtype=np.float32)}
            m.update({k: consts[k] for k in B_CONST})
            m.update({k: lw[k] for k in B_KEYS})
            m.update({k: np.asarray(resA.results[c][k]) for k in XF_NAMES})
            mapsB.append(m)
        resB = run_bass_kernel_spmd(ncB, mapsB, core_ids=list(range(N_CORES)))
        cur = [np.asarray(resB.results[c]["xout"]) for c in range(N_CORES)]
    out = np.empty_like(x)
    for c in range(N_CORES):
        out[c // 4, (c % 4) * per:(c % 4 + 1) * per] = cur[c]
    return out
```

```python
import numpy as np
import concourse.bass as bass
import concourse.mybir as mybir
from concourse.bass_utils import run_bass_kernel_spmd

F32 = mybir.dt.float32
BF16 = mybir.dt.bfloat16
AF = mybir.ActivationFunctionType
ALU = mybir.AluOpType
AX = mybir.AxisListType


class Buf:
    __slots__ = ("name", "w", "r")

    def __init__(self, name):
        self.name = name
        self.w = {}
        self.r = {}


class Sched:
    def __init__(self, nc, ctx):
        self.nc = nc
        self.ctx = ctx
        self.eng = {"pe": nc.tensor, "act": nc.scalar, "dve": nc.vector, "pool": nc.gpsimd, "sp": nc.sync}
        self.sem = {}
        self.cnt = {}
        for k in self.eng:
            self.sem[k] = ctx.enter_context(nc.semaphore("s_" + k))
            self.cnt[k] = 0
        self.waited = {k: {} for k in self.eng}
        self.ndma = 0
        self.nbuf = 0
        self.mute = False

    def buf(self, name=None):
        self.nbuf += 1
        return Buf(name or f"b{self.nbuf}")

    def bufs(self, n, name="b"):
        return [self.buf(f"{name}{i}") for i in range(n)]

    def dma_sem(self, name=None):
        self.ndma += 1
        key = name or f"dma{self.ndma}"
        self.sem[key] = self.ctx.enter_context(self.nc.semaphore("s_" + key))
        self.cnt[key] = 0
        return key

    def _deps(self, e, reads, writes):
        deps = {}
        for b in reads:
            for k, c in b.w.items():
                if deps.get(k, 0) < c:
                    deps[k] = c
        for b in writes:
            for k, c in b.w.items():
                if deps.get(k, 0) < c:
                    deps[k] = c
            for k, c in b.r.items():
                if deps.get(k, 0) < c:
                    deps[k] = c
        eng = self.eng[e]
        for k, c in deps.items():
            if k == e and e == "pe":
                continue
            if self.waited[e].get(k, 0) < c:
                eng.wait_ge(self.sem[k], c)
                self.waited[e][k] = c

    def _record(self, key, c, reads, writes):
        for b in writes:
            b.w = {key: c}
            b.r = {}
        for b in reads:
            if b.r.get(key, 0) < c:
                b.r[key] = c

    def op(self, e, fn, reads=(), writes=(), inc=True):
        if self.mute:
            return None
        self._deps(e, reads, writes)
        ins = fn(self.eng[e])
        if inc:
            self.cnt[e] += 1
            ins.then_inc(self.sem[e], 1)
            self._record(e, self.cnt[e], reads, writes)
        else:
            self._record(e, self.cnt[e] + 1, reads, writes)
        return ins

    def seal(self, key, bufs):
        if self.mute:
            return
        c = self.cnt[key]
        for b in bufs:
            if key in b.w:
                b.w[key] = c

    def handoff(self, news, olds):
        w = {}
        r = {}
        for ob in olds:
            for k2, c2 in ob.w.items():
                if w.get(k2, 0) < c2:
                    w[k2] = c2
            for k2, c2 in ob.r.items():
                if r.get(k2, 0) < c2:
                    r[k2] = c2
        for nb in news:
            nb.w = dict(w)
            nb.r = dict(r)

    def dma(self, q, dsem, out, in_, reads=(), writes=(), **kw):
        if self.mute:
            return None
        self._deps(q, reads, writes)
        ins = self.eng[q].dma_start(out=out, in_=in_, **kw)
        self.cnt[dsem] += 16
        ins.then_inc(self.sem[dsem], 16)
        self._record(dsem, self.cnt[dsem], reads, writes)
        return ins

    def wait_all(self, e, bufs):
        if self.mute:
            return
        self._deps(e, bufs, ())


import numpy as np
from contextlib import ExitStack

NT = 2048
NTT = 16
D = 1024
DIN = 2568
DFF = 4096
EPS = 1e-6
KB = 1024


class KB_:
    pass


def build(cfg):
    nc = bass.Bass("TRN2", target_bir_lowering=False)
    k = KB_()
    k.nc = nc
    k.cfg = cfg
    L = cfg.get("nlayers", 1)
    dr = {}

    def din(name, shape, dt=F32):
        dr[name] = nc.dram_tensor(name, list(shape), dt, kind="ExternalInput").ap()
        return dr[name]

    def dout(name, shape, dt=F32):
        dr[name] = nc.dram_tensor(name, list(shape), dt, kind="ExternalOutput").ap()
        return dr[name]

    mode = cfg.get("mode", "B")
    IMP = (mode == "B")
    EXP = (mode == "A")
    XF = [("e_a0", 16384), ("e_va", 4256), ("e_gates", 1152), ("e_cl", 8256), ("e_ww", 4096), ("e_cp", 8704), ("e_bd", 4096),
          ("e_tab", 4096), ("e_sp", 1216)]
    din("xin", [NT, D])
    if IMP:
        _din_real = din

        def din(name, shape, dt=F32, _real=_din_real):
            dr[name] = nc.dram_tensor(name, list(shape), dt).ap()
            return dr[name]
    if True:
        din("xhalo", [3, D])
        din("norm_mix_g", [L, D]); din("w_in", [L, D, DIN])
        din("ml_conv_w", [L, 4, 1024]); din("ml_conv_b", [L, 1024])
        din("ml_b_i", [L, 4]); din("ml_b_f", [L, 4])
        din("s5_a_re", [L, 32, 64]); din("s5_a_im", [L, 32, 64]); din("s5_log_dt", [L, 32])
        din("s5_b_re", [L, 32, 64, 16]); din("s5_b_im", [L, 32, 64, 16]); din("s5_c_re", [L, 32, 16, 64]); din("s5_c_im", [L, 32, 16, 64])
        din("s5_d", [L, 32, 16])
        din("par01", [128, 2]); din("bdmask", [128, 128])
    if IMP:
        din = _din_real
    if mode != "A":
        din("w_out", [L, D, D])
        din("norm_ffn_g", [L, D]); din("w_ff1", [L, D, DFF]); din("w_ff2", [L, DFF, D])
        din("final_norm_g", [D])
        din("s5_w_glu", [L, 512, 512])
        din("summ_all", [4, 128, 556])
    din("ident", [128, 128]); din("causal", [128, 128]); din("ones", [128, 128])
    din("ml_norm_g", [L, 512]); din("s5_b_glu", [L, 512]); din("s5_out_g", [L, 512])
    din("pred", [128, 4])
    if EXP:
        dout("summ", [128, 556])
        for nm_, w_ in XF:
            dout(nm_, [128, w_])
    if IMP:
        for nm_, w_ in XF:
            din(nm_, [128, w_])
    if mode != "A":
        dout("xout", [NT, D])
    if cfg.get("dbg_ht2"):
        dout("dbg_ht", [128, 8 * 2051], BF16)
    if cfg.get("debug"):
        dout("dbg_u", [128, 4 * 2048], BF16)
        dout("dbg_qk", [128, 8 * 2048], BF16)
        dout("dbg_v", [128, 16 * 4 * 129], BF16)
        dout("dbg_o", [128, 16 * 512], BF16)
        dout("dbg_if", [128, 128])
        dout("dbg_yc", [128, 8 * 2048], BF16)
        dout("dbg_g", [128, 16 * 64])
        dout("dbg_zl", [128, 4096]); dout("dbg_zs", [128, 16 * 2 * 129], BF16); dout("dbg_zg", [128, 8192], BF16)
        dout("dbg_sc", [128, 1216]); dout("dbg_cp", [128, 17408], BF16); dout("dbg_bd", [128, 8192], BF16); dout("dbg_wz", [128, 16384], BF16)
        dout("dbg_zend", [128, 32])
    k.dr = dr

    with ExitStack() as ctx:
        S = Sched(nc, ctx)
        k.S = S
        ctx.enter_context(nc.allow_non_contiguous_dma(reason="small param loads"))
        ctx.enter_context(nc.allow_low_precision(reason="bf16 matmul operands"))
        A0_B, A1_B, A2_B = 64 * KB, 33 * KB + 256, 79 * KB
        arena = ctx.enter_context(nc.sbuf_tensor("arena", [128, (A0_B + A1_B + A2_B) // 4], F32))
        ring = ctx.enter_context(nc.sbuf_tensor("ring", [128, 3 * 4096], BF16))
        cst = ctx.enter_context(nc.sbuf_tensor("cst", [128, 1024], F32))
        banks = [ctx.enter_context(nc.psum_tensor(f"ps{i}", [128, 512], F32)) for i in range(8)]
        bank_bufs = S.bufs(8, "bank")
        k.bank_i = 0
        block = ctx.enter_context(nc.Block())

        def view(base, off, nbytes, dt):
            assert off % 4 == 0 and nbytes % 4 == 0
            a = arena[:, (base + off) // 4:(base + off + nbytes) // 4]
            return a if dt == F32 else a.bitcast(dt)
        A0, A1, A2 = 0, A0_B, A0_B + A1_B

        def next_bank():
            i = k.bank_i
            k.bank_i = (i + 1) % 8
            return banks[i], bank_bufs[i]

        ident_f = cst[:, 0:128]
        ident_b = cst[:, 128:192].bitcast(BF16)
        b_cst = S.buf("cst")
        dq = S.dma_sem("dq_misc")
        S.dma("sp", dq, ident_f, dr["ident"], writes=[b_cst])
        tri_f = cst[:, 384:512]
        ones_f = cst[:, 512:640]
        tri_b = cst[:, 192:256].bitcast(BF16)
        S.dma("sp", dq, tri_f, dr["causal"], writes=[b_cst])
        S.dma("sp", dq, ones_f, dr["ones"], writes=[b_cst])
        par01 = cst[:, 752:754]
        bdm = cst[:, 768:896]
        ones_b = cst[:, 896:960].bitcast(BF16)
        if not IMP:
            S.dma("sp", dq, par01, dr["par01"], writes=[b_cst])
        pred = cst[:, 972:976]
        pmask = cst[:, 980:984]
        S.dma("sp", dq, pred, dr["pred"], writes=[b_cst])
        if not IMP:
            S.dma("sp", dq, bdm, dr["bdmask"], writes=[b_cst])
        S.seal(dq, [b_cst])
        S.op("dve", lambda e: e.tensor_copy(out=ones_b, in_=ones_f), reads=[b_cst], writes=[b_cst])
        S.op("dve", lambda e: e.tensor_scalar(out=pmask, in0=pred, scalar1=1e6, scalar2=-1e6, op0=ALU.mult, op1=ALU.add), reads=[b_cst], writes=[b_cst])
        S.op("dve", lambda e: e.tensor_copy(out=ident_b, in_=ident_f), reads=[b_cst], writes=[b_cst])
        S.op("dve", lambda e: e.tensor_copy(out=tri_b, in_=tri_f), reads=[b_cst], writes=[b_cst])

        X = view(A0, 0, 64 * KB, F32).rearrange("p (t d) -> p t d", t=NTT)
        X_b = S.bufs(NTT, "X")
        HT = view(A1, 0, 8 * 2051 * 2 + 0, BF16) if False else view(A1, 0, 32832, BF16)[:, 0:8 * 2051].rearrange("p (k t) -> p k t", k=8)
        HT_b = S.bufs(NTT + 1, "HT")
        YC = view(A1, 0, 32 * KB, BF16).rearrange("p (k t) -> p k t", k=8)
        QK = view(A0, 0, 32 * KB, BF16).rearrange("p (f t) -> p f t", f=8)
        OG = view(A0, 32 * KB, 16 * KB, BF16).rearrange("p (t d) -> p t d", t=NTT)
        US = view(A0, 48 * KB, 16 * KB, BF16).rearrange("p (q i c) -> p q i c", q=4, i=16)
        VA = view(A2, 0, 16512, BF16).rearrange("p (t h v) -> p t h v", t=NTT, h=4)
        GR = view(A2, 16512, 512, F32)
        SCR = A2 + 17024

        xq = [S.dma_sem(f"xq{i}") for i in range(16)]
        hq = S.dma_sem("hq"); gq = S.dma_sem("gq")
        oq = S.dma_sem("oq")
        wq = [S.dma_sem(f"wq{i}") for i in range(3)]
        ring_b = S.bufs(3, "ring")
        k.wi = 0

        def wload(src_ap, nk, ncols):
            i = k.wi % 3
            k.wi += 1
            v = ring[:, i * 4096: i * 4096 + nk * ncols].rearrange("p (k n) -> p k n", k=nk)
            S.dma("pool", wq[i], v, src_ap, writes=[ring_b[i]])
            return v, ring_b[i]

        class WStream:
            def __init__(self, chunks):
                self.chunks = chunks
                self.loaded = []

            def get(self, i, ahead=2):
                while len(self.loaded) < min(len(self.chunks), i + 1 + ahead):
                    self.loaded.append(wload(*self.chunks[len(self.loaded)]))
                return self.loaded[i]

        k.pq = None

        def load_pvec(dst, src_1d, b, q="sp"):
            S.dma(q, k.pq, dst, src_1d.rearrange("(k p) -> p k", p=128), writes=[b])

        tmp_b = S.bufs(4, "tmp")

        def quiesce():
            for e_ in ("pe", "act", "dve", "pool"):
                if S.cnt[e_] > 0:
                    nc.sync.wait_ge(S.sem[e_], S.cnt[e_])
            for key_, c_ in S.cnt.items():
                if key_ not in S.eng and c_ > 0:
                    nc.sync.wait_ge(S.sem[key_], c_)

        def layer(l, xin_ap, xh_ap, xout_ap, last):
            stop = cfg.get("stop")
            prm = cst[:, 256:256 + 64]
            b_prm = S.buf("prm")
            pq_l = S.dma_sem(f"pq{l}"); k.pq = pq_l
            g1 = cst[:, 640:648]; g2 = cst[:, 648:656]
            cw = cst[:, 656:688].rearrange("p (j f) -> p j f", j=4)
            cb = cst[:, 688:696]
            b_out = S.buf("out")
            if not IMP:
                load_pvec(g1, dr["norm_mix_g"][l], b_prm)
                for j in range(4):
                    load_pvec(cw[:, j, :], dr["ml_conv_w"][l, j], b_prm)
                load_pvec(cb, dr["ml_conv_b"][l], b_prm)
            if mode != "A":
                load_pvec(g2, dr["norm_ffn_g"][l], b_prm)
            mlg = cst[:, 740:744]
            load_pvec(mlg, dr["ml_norm_g"][l], b_prm)
            bif = cst[:, 744:752]
            dcol = cst[:, 960:964]; bglu = cst[:, 964:968]; outg = cst[:, 968:972]
            if not IMP:
                bi_ = dr["ml_b_i"][l]; bf_ = dr["ml_b_f"][l]
                S.dma("sp", pq_l, bif[:, 0:4], bass.AP(bi_.tensor, bi_.offset, [[0, 128], [1, 4]]), writes=[b_prm])
                S.dma("sp", pq_l, bif[:, 4:8], bass.AP(bf_.tensor, bf_.offset, [[0, 128], [1, 4]]), writes=[b_prm])
                load_pvec(dcol, dr["s5_d"][l].rearrange("g p -> (g p)"), b_prm)
            load_pvec(bglu, dr["s5_b_glu"][l], b_prm)
            load_pvec(outg, dr["s5_out_g"][l], b_prm)
            S.seal(pq_l, [b_prm])

            def xfer(name, ap, bufs):
                was = S.mute; S.mute = False
                if EXP:
                    S.dma("sp", oq, dr[name], ap, reads=bufs, writes=[b_out])
                elif IMP:
                    q_ = S.dma_sem(f"{name}_{l}")
                    S.dma("sp", q_, ap, dr[name], writes=bufs)
                S.mute = was
            if IMP:
                S.mute = True
            ssq = cst[:, 700:717]
            rstd = cst[:, 720:737]
            b_st = S.bufs(17, "st")
            junk = view(SCR, 0, 2048, BF16)
            b_junk = S.buf("junk")
            xnb = [view(SCR, 2048 + i * 2048, 2048, BF16) for i in range(2)]
            b_xnb = S.bufs(2, "xnb")
            xh_t = view(SCR, 6144, 4096, F32)
            b_xh = S.buf("xh")

            def rms_stats(src, np_, col, bsrc):
                S.op("act", lambda e: e.activation(out=junk[:np_], in_=src, func=AF.Square, accum_out=ssq[:np_, col:col + 1]),
                     reads=[bsrc], writes=[b_junk, b_st[col]])
                S.op("dve", lambda e: e.tensor_scalar(out=rstd[:np_, col:col + 1], in0=ssq[:np_, col:col + 1], scalar1=1.0 / D, scalar2=EPS,
                                                      op0=ALU.mult, op1=ALU.add), reads=[b_st[col]], writes=[b_st[col]])
                S.op("act", lambda e: e.activation(out=rstd[:np_, col:col + 1], in_=rstd[:np_, col:col + 1], func=AF.Sqrt),
                     reads=[b_st[col]], writes=[b_st[col]])
                S.op("dve", lambda e: e.reciprocal(out=rstd[:np_, col:col + 1], in_=rstd[:np_, col:col + 1]), reads=[b_st[col]], writes=[b_st[col]])

            b_sg = S.bufs(5, "stg")

            def stats_A(g4):
                tiles = [NTT] if g4 == 4 else range(4 * g4, 4 * g4 + 4)
                for tt in tiles:
                    halo = tt == NTT
                    np_ = 3 if halo else 128
                    src, bsrc = (xh_t[:3, :], b_xh) if halo else (X[:, tt, :], X_b[tt])
                    S.op("act", lambda e: e.activation(out=junk[:np_], in_=src, func=AF.Square, accum_out=ssq[:np_, tt:tt + 1]),
                         reads=[bsrc], writes=[b_junk, b_sg[g4]])

            def stats_B(g4):
                c0, c1 = (NTT, NTT + 1) if g4 == 4 else (4 * g4, 4 * g4 + 4)
                np_ = 3 if g4 == 4 else 128
                S.op("dve", lambda e: e.tensor_scalar(out=rstd[:np_, c0:c1], in0=ssq[:np_, c0:c1], scalar1=1.0 / D, scalar2=EPS,
                                                      op0=ALU.mult, op1=ALU.add), reads=[b_sg[g4]], writes=[b_sg[g4]])
                S.op("act", lambda e: e.activation(out=rstd[:np_, c0:c1], in_=rstd[:np_, c0:c1], func=AF.Sqrt), reads=[b_sg[g4]], writes=[b_sg[g4]])
                S.op("dve", lambda e: e.reciprocal(out=rstd[:np_, c0:c1], in_=rstd[:np_, c0:c1]), reads=[b_sg[g4]], writes=[b_sg[g4]])

            def norm_to_hT(gvec, with_halo, from_dram):
                gb = bass.AP(gvec.tensor, gvec.offset, [list(gvec.ap[0]), list(gvec.ap[1]), [0, 128]])
                for tt in range(NTT):
                    if from_dram:
                        S.dma("sp", xq[tt], X[:, tt, :], xin_ap[tt * 128:(tt + 1) * 128, :], writes=[X_b[tt]])
                if with_halo:
                    S.dma("sp", hq, xh_t[:3, :], xh_ap, writes=[b_xh])

                def stage_C(g4):
                    tiles = [NTT] if g4 == 4 else range(4 * g4, 4 * g4 + 4)
                    for tt in tiles:
                        halo = tt == NTT
                        np_ = 3 if halo else 128
                        src, bsrc = (xh_t[:3, :], b_xh) if halo else (X[:, tt, :], X_b[tt])
                        xb = xnb[tt % 2]; bx = b_xnb[tt % 2]
                        S.op("act", lambda e: e.activation(out=xb[:np_], in_=src, func=AF.Copy, scale=rstd[:np_, tt:tt + 1]),
                             reads=[bsrc, b_sg[g4]], writes=[bx])
                        bank, bb = next_bank()
                        pb = bank[:, 0:512].bitcast(BF16).rearrange("p (k t) -> p k t", k=8)
                        for kk in range(8):
                            S.op("pe", lambda e: e.transpose(pb[:, kk, 0:np_], xb[:np_, kk * 128:(kk + 1) * 128], ident_b[:np_, :np_]),
                                 inc=(kk == 7), reads=[bx, b_cst], writes=[bb])
                        c0 = 0 if halo else 3 + tt * 128
                        S.op("dve", lambda e: e.tensor_tensor(out=HT[:, :, c0:c0 + np_], in0=pb[:, :, 0:np_], in1=gb[:, :, 0:np_], op=ALU.mult),
                             reads=[bb, b_prm], writes=[HT_b[tt]])
                ng = 5 if with_halo else 4
                stats_A(0)
                for g4 in range(ng):
                    if g4 + 1 < ng:
                        stats_A(g4 + 1)
                    stats_B(g4)
                    stage_C(g4)

            norm_to_hT(g1, True, True)

            if stop == "norm1":
                quiesce(); return
            win = dr["w_in"][l].rearrange("(k p) n -> p k n", p=128)
            chunks = [(win[:, :, c * 512:(c + 1) * 512], 8, 512) for c in range(5)] + [(win[:, :, 2560:2568], 8, 8)]
            ws = WStream(chunks)
            stage = [view(SCR, 10240 + i * 8448, 8448, F32) for i in range(2)]
            b_stage = S.bufs(2, "stage")
            acc = view(SCR, 10240 + 2 * 8448, 8192, F32)
            b_acc = S.buf("acc")
            b_us = S.bufs(4, "us")
            b_qk = S.bufs(8, "qk")
            b_va = S.bufs(NTT, "va")
            b_og = S.bufs(NTT, "og")
            b_gr = S.buf("gr")
            allHT = HT_b
            S.handoff(b_us + b_qk + b_og, X_b)
            S.op("pool", lambda e: e.memset(VA[:, :, :, 128:129], 1.0), writes=b_va)
            for ci in range(3):
                wc, bw = ws.get(ci)
                for ft in range(4):
                    f = (ci - 1) * 4 + ft
                    if ci > 0:
                        st = stage[f % 2]; bs = b_stage[f % 2]
                        bank, bb = next_bank()
                        for kk in range(8):
                            S.op("pe", lambda e: e.matmul(bank[:, 0:3], lhsT=wc[:, kk, ft * 128:(ft + 1) * 128], rhs=HT[:, kk, 0:3],
                                                          start=(kk == 0), stop=(kk == 7)), inc=(kk == 7), reads=[bw, HT_b[NTT]], writes=[bb])
                        S.op("act", lambda e: e.activation(out=st[:, 0:3], in_=bank[:, 0:3], func=AF.Copy), reads=[bb], writes=[bs])
                    for nb in range(4):
                        bank, bb = next_bank()
                        for kk in range(8):
                            S.op("pe", lambda e: e.matmul(bank[:, :], lhsT=wc[:, kk, ft * 128:(ft + 1) * 128],
                                                          rhs=HT[:, kk, 3 + nb * 512:3 + (nb + 1) * 512], start=(kk == 0), stop=(kk == 7)),
                                 inc=(kk == 7), reads=[bw] + allHT[nb * 4:nb * 4 + 4], writes=[bb])
                        if ci == 0:
                            dst = US[:, ft, :, nb * 32:(nb + 1) * 32]
                            src = bank[:, :].rearrange("p (c i) -> p i c", i=16)
                            S.op("act", lambda e: e.activation(out=dst, in_=src, func=AF.Copy), reads=[bb], writes=[b_us[ft]])
                        else:
                            S.op("act", lambda e: e.activation(out=st[:, 3 + nb * 512:3 + (nb + 1) * 512], in_=bank[:, :], func=AF.Copy),
                                 reads=[bb], writes=[bs])
                    if ci > 0:
                        S.op("dve", lambda e: e.tensor_scalar(out=acc, in0=st[:, 0:2048], scalar1=cw[:, 0, f:f + 1], scalar2=None, op0=ALU.mult),
                             reads=[bs, b_prm], writes=[b_acc])
                        for j in range(1, 4):
                            S.op("dve", lambda e: e.scalar_tensor_tensor(out=acc, in0=st[:, j:j + 2048], scalar=cw[:, j, f:f + 1], in1=acc,
                                                                         op0=ALU.mult, op1=ALU.add), reads=[bs, b_prm, b_acc], writes=[b_acc])
                        S.op("act", lambda e: e.activation(out=QK[:, f, :], in_=acc, func=AF.Silu, bias=cb[:, f:f + 1]),
                             reads=[b_acc, b_prm], writes=[b_qk[f]])
            for ci in (3, 4):
                wc, bw = ws.get(ci)
                for tt in range(NTT):
                    bank, bb = next_bank()
                    for kk in range(8):
                        S.op("pe", lambda e: e.matmul(bank[:, :], lhsT=HT[:, kk, 3 + tt * 128:3 + (tt + 1) * 128], rhs=wc[:, kk, :],
                                                      start=(kk == 0), stop=(kk == 7)), inc=(kk == 7), reads=[bw, HT_b[tt]], writes=[bb])
                    if ci == 3:
                        S.op("act", lambda e: e.activation(out=VA[:, tt, :, 0:128], in_=bank[:, :].rearrange("p (h v) -> p h v", h=4), func=AF.Copy),
                             reads=[bb], writes=[b_va[tt]])
                    else:
                        S.op("act", lambda e: e.activation(out=OG[:, tt, :], in_=bank[:, :], func=AF.Sigmoid), reads=[bb], writes=[b_og[tt]])
            wc, bw = ws.get(5)
            bank, bb = next_bank()
            for tt in range(NTT):
                for kk in range(8):
                    S.op("pe", lambda e: e.matmul(bank[:, tt * 8:(tt + 1) * 8], lhsT=HT[:, kk, 3 + tt * 128:3 + (tt + 1) * 128], rhs=wc[:, kk, :],
                                                  start=(kk == 0), stop=(kk == 7)), inc=(kk == 7), reads=[bw, HT_b[tt]], writes=[bb])
            S.op("dve", lambda e: e.tensor_copy(out=GR, in_=bank[:, 0:128]), reads=[bb], writes=[b_gr])

            if cfg.get("debug"):
                dbq = S.dma_sem("dbq")
                b_dbg = S.buf("dbg")
                S.dma("sp", dbq, dr["dbg_u"], view(A0, 48 * KB, 16 * KB, BF16), reads=b_us, writes=[b_dbg])
                S.dma("sp", dbq, dr["dbg_qk"], view(A0, 0, 32 * KB, BF16), reads=b_qk, writes=[b_dbg])
                S.dma("sp", dbq, dr["dbg_v"], view(A2, 0, 16512, BF16), reads=b_va, writes=[b_dbg])
                S.dma("sp", dbq, dr["dbg_o"], view(A0, 32 * KB, 16 * KB, BF16), reads=b_og, writes=[b_dbg])
                S.dma("sp", dbq, dr["dbg_if"], GR, reads=[b_gr], writes=[b_dbg])
                pass

            if stop == "win":
                quiesce(); return
            xfer("e_a0", view(A0, 0, 64 * KB, F32), b_qk + b_og + b_us)
            xfer("e_va", view(A2, 0, 17024, F32), b_va + [b_gr])
            b_yc = S.bufs(NTT, "yc")
            S.handoff(b_yc, HT_b)

            MS = SCR
            def garr(i):
                return view(MS, i * 256, 256, F32).rearrange("p (m h) -> p m h", h=4)
            nlf, ig, nF, g, Gm, R0, Rr, w0, wv, clamp, dl, ep, incl, t1 = [garr(i) for i in range(14)]
            sm = view(MS, 14 * 256, 256, F32)
            gcol = sm[:, 0:1]; dm = sm[:, 4:8]; rec = sm[:, 8:12]; ss = sm[:, 12:16]; rs4 = sm[:, 16:20]
            Gb = view(MS, 15 * 256, 512, F32)
            b_g = S.buf("gates")
            b_sm = S.buf("sm")
            KT = view(MS, 4608, 16512, BF16)
            kTok = KT[:, 0:16 * 512].rearrange("p (m d) -> p m d", m=16)
            Cin = KT[:, 0:64 * 129].rearrange("p (c v) -> p c v", c=64)
            b_kt = S.bufs(16, "kt")
            CL = view(MS, 4608 + 16512, 64 * 129 * 4, F32).rearrange("p (c v) -> p c v", c=64)
            b_cl = S.bufs(16, "cl")
            T0 = MS + 4608 + 16512 + 64 * 129 * 4
            Ct = view(T0, 0, 2064, F32).rearrange("p (h v) -> p h v", h=4)
            tmpC = view(T0, 2064, 2064, F32).rearrange("p (h v) -> p h v", h=4)
            b_ct = S.buf("ct"); b_tc = S.buf("tmpc")
            Vw = [view(T0, 4128 + i * 1032, 1032, BF16).rearrange("p (h v) -> p h v", h=4) for i in range(2)]
            b_vw = S.bufs(2, "vw")
            PT = [view(T0, 6192 + i * 256, 256, BF16) for i in range(2)]
            b_pt = S.bufs(2, "pt")
            hbuf = [view(T0, 6704 + i * 512, 512, F32) for i in range(4)]
            b_hb = S.bufs(4, "hb")
            hn = [view(T0, 8752 + i * 256, 256, BF16) for i in range(2)]
            b_hn = S.bufs(2, "hn")
            junk2 = view(T0, 9264, 256, BF16)
            assert T0 + 9520 <= A2 + A2_B, (T0 + 9520 - A2 - A2_B)

            handoff = S.handoff
            handoff([b_g, b_sm] + b_kt + b_cl + [b_ct, b_tc] + b_vw + b_pt + b_hb + b_hn, b_stage + [b_acc, b_junk, b_xh] + b_xnb)
            GRv = GR.rearrange("p (m c) -> p m c", c=8)

            def bc_m(ap4):
                return bass.AP(ap4.tensor, ap4.offset, [list(ap4.ap[0]), [0, 16], list(ap4.ap[1])])

            def bc_last(ap, n):
                return bass.AP(ap.tensor, ap.offset, [list(x) for x in ap.ap] + [[0, n]])

            def flat(a):
                return a.rearrange("p m h -> p (m h)")
            LN_S = -0.5 * float(np.log(128.0))
            S.op("pool", lambda e: e.memset(view(MS, 0, 4608, F32), 0.0), writes=[b_g, b_sm])
            S.op("dve", lambda e: e.tensor_tensor(out=ig, in0=GRv[:, :, 0:4], in1=bc_m(bif[:, 0:4]), op=ALU.add), reads=[b_gr, b_prm], writes=[b_g])
            S.op("dve", lambda e: e.tensor_tensor(out=t1, in0=GRv[:, :, 4:8], in1=bc_m(bif[:, 4:8]), op=ALU.add), reads=[b_gr, b_prm], writes=[b_g])
            S.op("act", lambda e: e.activation(out=t1, in_=t1, func=AF.Exp, scale=-1.0), reads=[b_g], writes=[b_g])
            S.op("act", lambda e: e.activation(out=nlf, in_=t1, func=AF.Ln, bias=1.0), reads=[b_g], writes=[b_g])
            bankA, bbA = next_bank()
            S.op("pe", lambda e: e.matmul(bankA[:, 0:64], lhsT=tri_f, rhs=flat(nlf), start=True, stop=True), reads=[b_g, b_cst], writes=[bbA])
            bankB, bbB = next_bank()
            S.op("pe", lambda e: e.matmul(bankB[:, 0:64], lhsT=ones_f, rhs=flat(nlf), start=True, stop=True), reads=[b_g, b_cst], writes=[bbB])
            bA = bankA[:, 0:64].rearrange("p (m h) -> p m h", h=4)
            bB = bankB[:, 0:64].rearrange("p (m h) -> p m h", h=4)
            for h in range(4):
                S.op("dve", lambda e: e.tensor_tensor_scan(out=incl[:, :, h], data0=ones_f[:, 0:16], data1=bB[:, :, h], initial=0.0,
                                                           op0=ALU.mult, op1=ALU.add), reads=[bbB, b_cst, b_g], writes=[b_g])
            S.op("dve", lambda e: e.tensor_tensor(out=t1, in0=incl, in1=bB, op=ALU.subtract), reads=[b_g, bbB], writes=[b_g])
            S.op("dve", lambda e: e.tensor_tensor(out=nF, in0=t1, in1=bA, op=ALU.add), reads=[b_g, bbA], writes=[b_g])
            S.op("dve", lambda e: e.tensor_tensor(out=g, in0=ig, in1=nF, op=ALU.add), reads=[b_g], writes=[b_g])
            bankT, bbT = next_bank()
            S.op("pe", lambda e: e.transpose(bankT[0:64, 0:128], flat(g), ident_f), reads=[b_g, b_cst], writes=[bbT])
            S.op("dve", lambda e: e.tensor_reduce(out=gcol[0:64, :], in_=bankT[0:64, 0:128], axis=AX.X, op=ALU.max), reads=[bbT], writes=[b_sm])
            S.op("dve", lambda e: e.tensor_copy(out=Gb[0:64, :], in_=bc_last(gcol[0:64, 0:1], 128)[:, 0, :]), reads=[b_sm], writes=[b_sm])
            bankG, bbG = next_bank()
            S.op("pe", lambda e: e.matmul(bankG[:, 0:64], lhsT=Gb[0:64, :], rhs=ident_f[0:64, 0:64], start=True, stop=True), reads=[b_sm, b_cst], writes=[bbG])
            S.op("dve", lambda e: e.tensor_copy(out=flat(Gm), in_=bankG[:, 0:64]), reads=[bbG], writes=[b_g])
            for h in range(4):
                S.op("dve", lambda e: e.tensor_tensor_scan(out=R0[:, :, h], data0=Gm[:, :, h], data1=Gm[:, :, h], initial=-1e30,
                                                           op0=ALU.max, op1=ALU.max), reads=[b_g], writes=[b_g])
            S.op("dve", lambda e: e.tensor_tensor(out=t1, in0=g, in1=R0, op=ALU.subtract), reads=[b_g], writes=[b_g])
            S.op("act", lambda e: e.activation(out=w0, in_=t1, func=AF.Exp, bias=LN_S), reads=[b_g], writes=[b_g])
            S.op("dve", lambda e: e.tensor_tensor(out=t1[:, 1:16, :], in0=R0[:, 0:15, :], in1=R0[:, 1:16, :], op=ALU.subtract), reads=[b_g], writes=[b_g])
            S.op("act", lambda e: e.activation(out=dl[:, 1:16, :], in_=t1[:, 1:16, :], func=AF.Exp), reads=[b_g], writes=[b_g])
            for m in range(16):
                cs = slice(m * 128, (m + 1) * 128)
                bank, bb = next_bank()
                pb = bank[:, 0:256].bitcast(BF16)
                for h in range(4):
                    S.op("pe", lambda e: e.transpose(pb[:, h * 128:(h + 1) * 128], QK[:, 4 + h, cs], ident_b), inc=(h == 3),
                         reads=[b_qk[4 + h], b_cst], writes=[bb])
                S.op("act", lambda e: e.activation(out=kTok[:, m, :], in_=pb, func=AF.Copy), reads=[bb], writes=[b_kt[m]])
                vw = Vw[m % 2]; bv = b_vw[m % 2]
                S.op("dve", lambda e: e.tensor_tensor(out=vw, in0=VA[:, m, :, :], in1=bc_last(w0[:, m, :], 129), op=ALU.mult),
                     reads=[b_va[m], b_g], writes=[bv])
                for h in range(4):
                    bank, bb = next_bank()
                    S.op("pe", lambda e: e.matmul(bank[:, 0:129], lhsT=kTok[:, m, h * 128:(h + 1) * 128], rhs=vw[:, h, :], start=True, stop=True),
                         reads=[b_kt[m], bv], writes=[bb])
                    if h % 2 == 0:
                        S.op("act", lambda e: e.activation(out=CL[:, m * 4 + h, :], in_=bank[:, 0:129], func=AF.Copy), reads=[bb], writes=[b_cl[m]])
                    else:
                        S.op("dve", lambda e: e.tensor_copy(out=CL[:, m * 4 + h, :], in_=bank[:, 0:129]), reads=[bb], writes=[b_cl[m]])
                if m == 0:
                    S.op("dve", lambda e: e.tensor_copy(out=Ct, in_=CL[:, 0:4, :]), reads=[b_cl[0]], writes=[b_ct])
                else:
                    S.op("dve", lambda e: e.tensor_tensor(out=Ct, in0=Ct, in1=bc_last(dl[:, m, :], 129), op=ALU.mult), reads=[b_ct, b_g], writes=[b_ct])
                    S.op("dve", lambda e: e.tensor_tensor(out=Ct, in0=Ct, in1=CL[:, m * 4:m * 4 + 4, :], op=ALU.add), reads=[b_ct, b_cl[m]], writes=[b_ct])
            ml_extra = []
            xfer("e_gates", view(MS, 0, 4608, F32), [b_g, b_sm])
            xfer("e_cl", view(MS, 4608 + 16512, 64 * 129 * 4, F32), b_cl)
            if IMP:
                S.mute = False
            mst = sm[:, 20:24]
            exq = S.dma_sem(f"exq{l}")
            if mode == "A":
                S.dma("sp", oq, dr["summ"][:, 32:548], Ct.rearrange("p h v -> p (h v)"), reads=[b_ct], writes=[b_out])
                S.dma("sp", oq, dr["summ"][:, 548:552], R0[:, 15, :], reads=[b_g], writes=[b_out])
                S.dma("sp", oq, dr["summ"][:, 552:556], incl[:, 15, :], reads=[b_g], writes=[b_out])
            else:
                sa = dr["summ_all"]
                small = Gb.rearrange("p (j c) -> p j c", j=4)[:, :, 0:8]
                S.dma("sp", exq, small, sa[:, :, 548:556].rearrange("j p c -> p j c"), writes=[b_sm])
                S.seal(exq, [b_sm])
                cm = sm[:, 20:24]; mx = sm[:, 24:28]; ta = sm[:, 28:32]; tb = sm[:, 32:36]; r0j = sm[:, 36:40]; nfj = sm[:, 40:44]; tq = sm[:, 44:48]
                S.op("dve", lambda e: e.memset(tmpC, 0.0), reads=[b_tc], writes=[b_tc])
                S.op("dve", lambda e: e.memset(cm, 0.0), reads=[b_sm], writes=[b_sm])
                cq = [S.dma_sem(f"cq{l}_{i}") for i in range(2)]
                Cj = [Ct, view(T0, 4128, 2064, F32).rearrange("p (h v) -> p h v", h=4)]
                b_cj = [b_ct, S.buf("cj1")]
                S.handoff([b_cj[1]], b_vw)
                for j in range(4):
                    cj = Cj[j % 2]; bcj = b_cj[j % 2]
                    S.dma("sp", cq[j % 2], cj.rearrange("p h v -> p (h v)"), sa[j, :, 32:548], writes=[bcj])
                    S.op("dve", lambda e: e.tensor_scalar(out=r0j, in0=small[:, j, 0:4], scalar1=pred[:, j:j + 1], scalar2=pmask[:, j:j + 1],
                                                          op0=ALU.mult, op1=ALU.add), reads=[b_sm, b_cst], writes=[b_sm])
                    S.op("dve", lambda e: e.tensor_scalar(out=nfj, in0=small[:, j, 4:8], scalar1=pred[:, j:j + 1], scalar2=None, op0=ALU.mult),
                         reads=[b_sm, b_cst], writes=[b_sm])
                    S.op("dve", lambda e: e.tensor_tensor(out=mx, in0=cm, in1=r0j, op=ALU.max), reads=[b_sm], writes=[b_sm])
                    S.op("dve", lambda e: e.tensor_tensor(out=tq, in0=cm, in1=mx, op=ALU.subtract), reads=[b_sm], writes=[b_sm])
                    S.op("act", lambda e: e.activation(out=ta, in_=tq, func=AF.Exp), reads=[b_sm], writes=[b_sm])
                    S.op("dve", lambda e: e.tensor_tensor(out=tq, in0=r0j, in1=mx, op=ALU.subtract), reads=[b_sm], writes=[b_sm])
                    S.op("act", lambda e: e.activation(out=tb, in_=tq, func=AF.Exp), reads=[b_sm], writes=[b_sm])
                    S.op("dve", lambda e: e.tensor_tensor(out=tmpC, in0=tmpC, in1=bc_last(ta, 129), op=ALU.mult), reads=[b_tc, b_sm], writes=[b_tc])
                    S.op("dve", lambda e: e.tensor_tensor(out=cj, in0=cj, in1=bc_last(tb, 129), op=ALU.mult), reads=[bcj, b_sm], writes=[bcj])
                    S.op("dve", lambda e: e.tensor_tensor(out=tmpC, in0=tmpC, in1=cj, op=ALU.add), reads=[b_tc, bcj], writes=[b_tc])
                    S.op("dve", lambda e: e.tensor_tensor(out=cm, in0=mx, in1=nfj, op=ALU.subtract), reads=[b_sm], writes=[b_sm])
            if mode != "A":
                S.op("dve", lambda e: e.tensor_tensor(out=Rr, in0=R0, in1=bc_m(mst), op=ALU.max), reads=[b_g, b_sm], writes=[b_g])
                S.op("dve", lambda e: e.tensor_tensor(out=t1, in0=g, in1=Rr, op=ALU.subtract), reads=[b_g], writes=[b_g])
                S.op("act", lambda e: e.activation(out=wv, in_=t1, func=AF.Exp, bias=LN_S), reads=[b_g], writes=[b_g])
                S.op("dve", lambda e: e.tensor_tensor(out=t1, in0=nF, in1=Rr, op=ALU.subtract), reads=[b_g], writes=[b_g])
                S.op("act", lambda e: e.activation(out=clamp, in_=t1, func=AF.Exp), reads=[b_g], writes=[b_g])
                S.op("dve", lambda e: e.tensor_tensor(out=t1, in0=R0, in1=Rr, op=ALU.subtract), reads=[b_g], writes=[b_g])
                S.op("act", lambda e: e.activation(out=ep, in_=t1, func=AF.Exp), reads=[b_g], writes=[b_g])
                S.op("dve", lambda e: e.tensor_tensor(out=t1[:, 1:16, :], in0=Rr[:, 0:15, :], in1=Rr[:, 1:16, :], op=ALU.subtract), reads=[b_g], writes=[b_g])
                S.op("dve", lambda e: e.tensor_tensor(out=t1[:, 0, :], in0=mst, in1=Rr[:, 0, :], op=ALU.subtract), reads=[b_g, b_sm], writes=[b_g])
                S.op("act", lambda e: e.activation(out=dl, in_=t1, func=AF.Exp), reads=[b_g], writes=[b_g])
                S.op("dve", lambda e: e.tensor_copy(out=Ct, in_=tmpC), reads=[b_tc], writes=[b_ct])
                b_cin = b_kt
                for m in range(16):
                    S.op("dve", lambda e: e.tensor_tensor(out=tmpC, in0=Ct, in1=bc_last(dl[:, m, :], 129), op=ALU.mult), reads=[b_ct, b_g], writes=[b_tc])
                    S.op("act", lambda e: e.activation(out=Cin[:, m * 4:m * 4 + 4, :], in_=tmpC, func=AF.Copy), reads=[b_tc], writes=b_cin)
                    S.op("dve", lambda e: e.tensor_tensor(out=Ct, in0=CL[:, m * 4:m * 4 + 4, :], in1=bc_last(ep[:, m, :], 129), op=ALU.mult),
                         reads=[b_cl[m], b_g], writes=[b_ct])
                    S.op("dve", lambda e: e.tensor_tensor(out=Ct, in0=Ct, in1=tmpC, op=ALU.add), reads=[b_ct, b_tc], writes=[b_ct])
                PT4 = [view(T0, i * 256, 256, BF16) for i in range(4)]
                hn4 = [view(T0, 1024 + i * 1024, 1024, BF16).rearrange("p (h t) -> p h t", h=4) for i in range(2)]
                hbA = view(T0, 6704, 2048, F32).rearrange("p (h t) -> p h t", h=4)
                hbB = view(T0, 3072, 2048, F32).rearrange("p (h t) -> p h t", h=4)
                hb4 = [hbA, hbB]
                b_pt4 = S.bufs(4, "pt4"); b_hn4 = S.bufs(2, "hn4"); b_hb4 = [S.bufs(4, "hbA"), S.bufs(4, "hbB")]
                b_j2 = S.buf("j2")
                b_ep = [S.buf("ep0"), S.buf("ep1")]
                S.handoff(b_pt4 + b_hn4 + b_hb4[0] + b_hb4[1] + [b_j2] + b_ep, [b_ct, b_tc, b_sm] + b_vw + b_pt + b_hb + b_hn + ([b_cj[1]] if mode != "A" else []))
                ml_extra += b_pt4 + b_hn4 + b_hb4[0] + b_hb4[1] + [b_j2] + b_ep
                smx = [sm[:, 4:20], sm[:, 48:64]]
                for m in range(16):
                    cs = slice(m * 128, (m + 1) * 128)
                    par = m % 2
                    dm_, rec_, ss_, rs_ = smx[par][:, 0:4], smx[par][:, 4:8], smx[par][:, 8:12], smx[par][:, 12:16]
                    be = b_ep[par]
                    bankS, bbS = next_bank()
                    for h in range(4):
                        S.op("pe", lambda e: e.matmul(bankS[:, h * 128:(h + 1) * 128], lhsT=QK[:, 4 + h, cs], rhs=QK[:, h, cs], start=True, stop=True),
                             inc=(h == 3), reads=[b_qk[4 + h], b_qk[h]], writes=[bbS])
                    for h in range(4):
                        S.op("dve", lambda e: e.scalar_tensor_tensor(out=PT4[h], in0=bankS[:, h * 128:(h + 1) * 128], scalar=wv[:, m, h:h + 1], in1=tri_f,
                                                                     op0=ALU.mult, op1=ALU.mult), reads=[bbS, b_g, b_cst], writes=[b_pt4[h]])
                    bN = []
                    for j in range(2):
                        bankN, bbN = next_bank()
                        bN.append((bankN, bbN))
                        for hh_ in range(2):
                            h = 2 * j + hh_
                            co = hh_ * 129
                            S.op("pe", lambda e: e.matmul(bankN[:, co:co + 129], lhsT=PT4[h], rhs=VA[:, m, h, :], start=True, stop=False), inc=False,
                                 reads=[b_pt4[h], b_va[m]], writes=[bbN])
                            S.op("pe", lambda e: e.matmul(bankN[:, co:co + 129], lhsT=QK[:, h, cs], rhs=Cin[:, m * 4 + h, :], start=False, stop=True),
                                 inc=(hh_ == 1), reads=[b_qk[h]] + b_cin, writes=[bbN])
                        den = bankN[:, 0:258].rearrange("p (h v) -> p h v", v=129)[:, :, 128]
                        S.op("act", lambda e: e.activation(out=dm_[:, 2 * j:2 * j + 2], in_=den, func=AF.Abs), reads=[bbN], writes=[be])
                        S.op("dve", lambda e: e.tensor_tensor(out=dm_[:, 2 * j:2 * j + 2], in0=dm_[:, 2 * j:2 * j + 2], in1=clamp[:, m, 2 * j:2 * j + 2], op=ALU.max),
                             reads=[be, b_g], writes=[be])
                    S.op("dve", lambda e: e.reciprocal(out=rec_, in_=dm_), reads=[be], writes=[be])
                    for h in range(4):
                        bankN, bbN = bN[h // 2]
                        co = (h % 2) * 129
                        S.op("dve", lambda e: e.scalar_tensor_tensor(out=hb4[par][:, h, :], in0=bankN[:, co:co + 128], scalar=rec_[:, h:h + 1],
                                                                     in1=OG[:, m, h * 128:(h + 1) * 128], op0=ALU.mult, op1=ALU.mult),
                             reads=[bbN, be, b_og[m]], writes=[b_hb4[par][h]])
                        S.op("act", lambda e: e.activation(out=junk2, in_=hb4[par][:, h, :], func=AF.Square, accum_out=ss_[:, h:h + 1]),
                             reads=[b_hb4[par][h]], writes=[b_j2, be])
                    S.op("dve", lambda e: e.tensor_scalar(out=rs_, in0=ss_, scalar1=1.0 / 128, scalar2=EPS, op0=ALU.mult, op1=ALU.add), reads=[be], writes=[be])
                    S.op("act", lambda e: e.activation(out=rs_, in_=rs_, func=AF.Sqrt), reads=[be], writes=[be])
                    S.op("dve", lambda e: e.reciprocal(out=rs_, in_=rs_), reads=[be], writes=[be])
                    S.op("dve", lambda e: e.tensor_tensor(out=hn4[par], in0=hb4[par], in1=bc_last(rs_, 128), op=ALU.mult),
                         reads=b_hb4[par] + [be], writes=[b_hn4[par]])
                    bankO, bbO = next_bank()
                    po = bankO[:, 0:256].bitcast(BF16).rearrange("p (h t) -> p h t", h=4)
                    for h in range(4):
                        S.op("pe", lambda e: e.transpose(po[:, h, :], hn4[par][:, h, :], ident_b), inc=(h == 3), reads=[b_hn4[par], b_cst], writes=[bbO])
                    S.op("dve", lambda e: e.tensor_tensor(out=YC[:, 4:8, cs], in0=po, in1=bc_last(mlg, 128), op=ALU.mult), reads=[bbO, b_prm], writes=[b_yc[m]])

            if cfg.get("debug"):
                S.dma("sp", dbq, dr["dbg_g"].rearrange("p (a c) -> p a c", a=16), view(MS, 0, 4096, F32).rearrange("p (a c) -> p a c", a=16), reads=[b_g, b_sm], writes=[b_dbg])

            if stop == "ml":
                quiesce(); return
            TCH = 16; NC_ = 128
            WZ = view(A0, 0, 32 * KB, BF16).rearrange("p (q i x n) -> p q i x n", q=4, i=16, x=2)
            BD = view(A0, 32 * KB, 16 * KB, BF16).rearrange("p (q j n) -> p q j n", q=4, j=16)
            CP = view(A2, 0, 34816, BF16).rearrange("p (j r x n) -> p j r x n", j=17, r=16, x=2)
            EC = view(A2, 34816, 8192, F32).rearrange("p (r c) -> p r c", r=16)
            ES = view(A2, 34816 + 8192, 8192, F32).rearrange("p (r c) -> p r c", r=16)
            ZL = view(A2, 51200, 16384, F32).rearrange("p (r x c) -> p r x c", r=16, x=2)
            ZS = view(A2, 67584, 8256, BF16).rearrange("p (r x c) -> p r x c", r=16, x=2)
            SP_ = A2 + 76032
            PW = view(SP_, 0, 2176, F32).rearrange("p (j x r) -> p j x r", j=17, x=2)
            def sc(i):
                return view(SP_, 2176 + i * 64, 64, F32)
            assert SP_ + 2176 + 30 * 64 <= A2 + A2_B
            WW = view(A0, 0, 16384, F32).rearrange("p (r x c) -> p r x c", r=16, x=2)
            ZG = view(A2, 51200, 16384, BF16).rearrange("p (q i c) -> p q i c", q=4, i=16)
            GT = A0 + 16 * KB
            GEN = A2 + 51200
            b_s5 = S.buf("s5gen")
            b_wz = S.bufs(4, "wz"); b_bd = S.bufs(4, "bd"); b_cp = S.buf("cp"); b_tab = S.buf("tab")
            b_zl = S.bufs(16, "zl"); b_ww = S.bufs(16, "ww"); b_zs = S.bufs(16, "zs"); b_zg = S.bufs(4, "zg")
            olds = [b_g, b_sm] + b_kt + b_cl + [b_ct, b_tc] + b_vw + b_pt + b_hb + b_hn + b_va + [b_gr] + b_qk + b_og + ml_extra
            handoff([b_s5, b_cp, b_tab] + b_wz + b_bd + b_zl + b_ww + b_zs + b_zg, olds)
            s5q = S.dma_sem(f"s5q{l}")
            (s_are, s_aim, s_dt, s_mag, s_th, s_t, s_sin, s_cos, s_abr, s_abi, s_den, s_zr, s_sre, s_sim, s_t2, s_magL,
             s_l128r, s_l128i, s_t3, s_m128) = [sc(i) for i in range(20)]
            zend = sc(20)[:, 0:16]
            zend = view(SP_, 2176 + 20 * 64, 128, F32).rearrange("p (r x) -> p r x", x=2)
            sst = view(SP_, 2176 + 22 * 64, 128, F32).rearrange("p (r x) -> p r x", x=2)

            def TT(out, in0, in1, op, rd=(), wr=None, eng="dve"):
                S.op(eng, lambda e: e.tensor_tensor(out=out, in0=in0, in1=in1, op=op), reads=[b_s5] + list(rd), writes=[b_s5] if wr is None else wr)

            def TS(out, in0, s1, s2, op0, op1=None, rd=(), wr=None):
                if op1 is None:
                    S.op("dve", lambda e: e.tensor_scalar(out=out, in0=in0, scalar1=s1, scalar2=None, op0=op0), reads=[b_s5] + list(rd), writes=[b_s5] if wr is None else wr)
                else:
                    S.op("dve", lambda e: e.tensor_scalar(out=out, in0=in0, scalar1=s1, scalar2=s2, op0=op0, op1=op1), reads=[b_s5] + list(rd), writes=[b_s5] if wr is None else wr)

            def AC(out, in_, func, rd=(), wr=None, **kw):
                S.op("act", lambda e: e.activation(out=out, in_=in_, func=func, **kw), reads=[b_s5] + list(rd), writes=[b_s5] if wr is None else wr)

            def cmul(o_r, o_i, a_r, a_i, b_r, b_i, t1_, t2_, rd=(), wr=None, neg_im=False):
                TT(t1_, a_r, b_r, ALU.mult, rd); TT(t2_, a_i, b_i, ALU.mult, rd)
                TT(o_r, t1_, t2_, ALU.subtract, rd, wr)
                TT(t1_, a_r, b_i, ALU.mult, rd); TT(t2_, a_i, b_r, ALU.mult, rd)
                if neg_im:
                    TT(t1_, t1_, t2_, ALU.add, rd)
                    TS(o_i, t1_, -1.0, None, ALU.mult, rd=rd, wr=wr)
                else:
                    TT(o_i, t1_, t2_, ALU.add, rd, wr)

            if IMP:
                S.mute = True
            S.op("pool", lambda e: e.memset(view(SP_, 0, 4864, F32), 0.0), writes=[b_s5])
            araw = view(GEN, 0, 1024, F32)
            S.dma("sp", s5q, araw[0:16, 0:128], dr["s5_a_re"][l].rearrange("(r gl) n -> r (gl n)", gl=2), writes=[b_s5])
            S.dma("sp", s5q, araw[0:16, 128:256], dr["s5_a_im"][l].rearrange("(r gl) n -> r (gl n)", gl=2), writes=[b_s5])
            ldt = dr["s5_log_dt"][l]
            for gl in range(2):
                S.dma("sp", s5q, s_dt[gl * 64:(gl + 1) * 64, :], bass.AP(ldt.tensor, ldt.offset + gl, [[0, 64], [2, 16]]), writes=[b_s5])
            Bsm = [view(GEN, 1024 + x * 1024, 1024, F32).rearrange("p (r c) -> p r c", r=16) for x in range(2)]
            for x, nm in enumerate(["s5_b_re", "s5_b_im"]):
                bsrc = dr[nm][l]
                for gl in range(2):
                    S.dma("sp", s5q, Bsm[x][gl * 64:(gl + 1) * 64, :, :],
                          bass.AP(bsrc.tensor, bsrc.offset + gl * 1024, [[16, 64], [2048, 16], [1, 16]]), writes=[b_s5])
            Craw = [view(GEN, 3072 + x * 1024, 1024, F32).rearrange("p (q n) -> p q n", q=4) for x in range(2)]
            for x, nm in enumerate(["s5_c_re", "s5_c_im"]):
                csrc = dr[nm][l]
                S.dma("sp", s5q, Craw[x], bass.AP(csrc.tensor, csrc.offset, [[64, 128], [8192, 4], [1, 64]]), writes=[b_s5])
            S.seal(s5q, [b_s5])
            bank, bb = next_bank()
            S.op("pe", lambda e: e.transpose(bank[:, 0:16], araw[0:16, 0:128], ident_f[0:16, 0:16]), reads=[b_s5, b_cst], writes=[bb])
            S.op("pe", lambda e: e.transpose(bank[:, 16:32], araw[0:16, 128:256], ident_f[0:16, 0:16]), reads=[b_s5, b_cst], writes=[bb])
            S.op("dve", lambda e: e.tensor_copy(out=s_are, in_=bank[:, 0:16]), reads=[bb], writes=[b_s5])
            S.op("dve", lambda e: e.tensor_copy(out=s_aim, in_=bank[:, 16:32]), reads=[bb], writes=[b_s5])
            PI = float(np.pi)
            AC(s_dt, s_dt, AF.Exp)
            TT(s_t, s_are, s_dt, ALU.mult)
            AC(s_mag, s_t, AF.Exp)
            AC(s_magL, s_t, AF.Exp, scale=float(TCH))
            AC(s_m128, s_t, AF.Exp, scale=float(TCH * NC_))
            TT(s_th, s_aim, s_dt, ALU.mult)
            for thr in (1.0, 3.0, 5.0, 7.0):
                TS(s_t, s_th, thr * PI, -2.0 * PI, ALU.is_gt, ALU.mult)
                if thr == 1.0:
                    TT(s_t2, s_th, s_t, ALU.add)
                else:
                    TT(s_t2, s_t2, s_t, ALU.add)
            AC(s_sin, s_t2, AF.Sin)
            TS(s_t3, s_t2, 0.5 * PI, None, ALU.add)
            TS(s_t, s_t3, PI, -2.0 * PI, ALU.is_gt, ALU.mult)
            TT(s_t3, s_t3, s_t, ALU.add)
            AC(s_cos, s_t3, AF.Sin)
            TT(s_abr, s_mag, s_cos, ALU.mult); TT(s_abi, s_mag, s_sin, ALU.mult)
            TT(s_t, s_are, s_are, ALU.mult); TT(s_t2, s_aim, s_aim, ALU.mult); TT(s_den, s_t, s_t2, ALU.add)
            S.op("dve", lambda e: e.reciprocal(out=s_den, in_=s_den), reads=[b_s5], writes=[b_s5])
            TS(s_zr, s_abr, -1.0, None, ALU.add)
            TT(s_t, s_zr, s_are, ALU.mult); TT(s_t2, s_abi, s_aim, ALU.mult); TT(s_t, s_t, s_t2, ALU.add); TT(s_sre, s_t, s_den, ALU.mult)
            TT(s_t, s_abi, s_are, ALU.mult); TT(s_t2, s_zr, s_aim, ALU.mult); TT(s_t, s_t, s_t2, ALU.subtract); TT(s_sim, s_t, s_den, ALU.mult)
            S.op("dve", lambda e: e.memset(PW[:, 0, 0, :], 1.0), reads=[b_s5], writes=[b_s5])
            S.op("dve", lambda e: e.memset(PW[:, 0, 1, :], 0.0), reads=[b_s5], writes=[b_s5])
            S.op("dve", lambda e: e.tensor_copy(out=PW[:, 1, 0, :], in_=s_abr), reads=[b_s5], writes=[b_s5])
            S.op("dve", lambda e: e.tensor_copy(out=PW[:, 1, 1, :], in_=s_abi), reads=[b_s5], writes=[b_s5])
            pt1 = view(GEN, 5120, 1024, F32).rearrange("p (j r) -> p j r", r=16)
            pt2 = view(GEN, 6144, 1024, F32).rearrange("p (j r) -> p j r", r=16)
            kk_ = 1
            while kk_ < 16:
                def bj(a):
                    return bass.AP(a.tensor, a.offset, [list(a.ap[0]), [0, kk_], list(a.ap[1])])
                cmul(PW[:, kk_ + 1:2 * kk_ + 1, 0, :], PW[:, kk_ + 1:2 * kk_ + 1, 1, :], PW[:, 1:kk_ + 1, 0, :], PW[:, 1:kk_ + 1, 1, :],
                     bj(PW[:, kk_, 0, :]), bj(PW[:, kk_, 1, :]), pt1[:, 0:kk_, :], pt2[:, 0:kk_, :])
                kk_ *= 2
            S.op("dve", lambda e: e.reciprocal(out=s_t, in_=s_magL), reads=[b_s5], writes=[b_s5])
            TT(EC[:, :, 0], PW[:, 16, 0, :], s_t, ALU.mult, wr=[b_s5, b_tab]); TT(ES[:, :, 0], PW[:, 16, 1, :], s_t, ALU.mult, wr=[b_s5, b_tab])
            et1 = view(GEN, 7168, 4096, F32).rearrange("p (r c) -> p r c", r=16)
            et2 = view(GEN, 11264, 4096, F32).rearrange("p (r c) -> p r c", r=16)
            kk_ = 1
            while kk_ < NC_:
                cmul(EC[:, :, kk_:2 * kk_], ES[:, :, kk_:2 * kk_], EC[:, :, 0:kk_], ES[:, :, 0:kk_],
                     bc_last(EC[:, :, kk_ - 1], kk_), bc_last(ES[:, :, kk_ - 1], kk_), et1[:, :, 0:kk_], et2[:, :, 0:kk_], rd=[b_tab], wr=[b_s5, b_tab])
                kk_ *= 2
            TT(s_l128r, EC[:, :, NC_ - 1], s_m128, ALU.mult, rd=[b_tab]); TT(s_l128i, ES[:, :, NC_ - 1], s_m128, ALU.mult, rd=[b_tab])
            Cin_ = [view(GEN, 5120 + x * 2048, 2048, F32).rearrange("p (q n) -> p q n", q=4) for x in range(2)]
            Cp = [view(GEN, 9216 + x * 2048, 2048, F32).rearrange("p (r n) -> p r n", r=16) for x in range(2)]
            ct1 = view(GEN, 13312, 2048, F32).rearrange("p (r n) -> p r n", r=16)
            ct2 = view(A0, 0, 2048, F32).rearrange("p (r n) -> p r n", r=16)
            for x in range(2):
                TS(Cin_[x][:, :, 0:64], Craw[x], par01[:, 0:1], None, ALU.mult, rd=[b_cst])
                TS(Cin_[x][:, :, 64:128], Craw[x], par01[:, 1:2], None, ALU.mult, rd=[b_cst])
                bank, bb = next_bank()
                for q in range(4):
                    S.op("pe", lambda e: e.transpose(bank[:, q * 128:(q + 1) * 128], Cin_[x][:, q, :], ident_f), inc=(q == 3), reads=[b_s5, b_cst], writes=[bb])
                S.op("dve", lambda e: e.tensor_copy(out=Cp[x].rearrange("p r n -> p (r n)"), in_=bank[:, :]), reads=[bb], writes=[b_s5])
            for j in range(17):
                pr = bc_last(PW[:, j, 0, :], 32); pi_ = bc_last(PW[:, j, 1, :], 32)
                TT(ct1, Cp[0], pr, ALU.mult); TT(ct2, Cp[1], pi_, ALU.mult, rd=b_wz, wr=[b_s5] + b_wz)
                TT(CP[:, j, :, 0, :], ct1, ct2, ALU.subtract, wr=[b_s5, b_cp])
                TT(ct1, Cp[0], pi_, ALU.mult); TT(ct2, Cp[1], pr, ALU.mult, rd=b_wz, wr=[b_s5] + b_wz)
                S.op("dve", lambda e: e.scalar_tensor_tensor(out=CP[:, j, :, 1, :], in0=ct1, scalar=-1.0, in1=ct2, op0=ALU.mult, op1=ALU.subtract),
                     reads=[b_s5], writes=[b_s5, b_cp])

            BB = [view(GEN, 5120 + x * 2048, 2048, F32).rearrange("p (r n) -> p r n", r=16) for x in range(2)]
            BBb = [view(GEN, 9216 + x * 1024, 1024, BF16).rearrange("p (r n) -> p r n", r=16) for x in range(2)]
            bt1 = view(GEN, 11264, 1024, F32).rearrange("p (r c) -> p r c", r=16)
            bt2 = view(GEN, 12288, 1024, F32).rearrange("p (r c) -> p r c", r=16)
            for x in range(2):
                S.op("dve", lambda e: e.memset(BB[x], 0.0), reads=[b_s5], writes=[b_s5])
            sre_b = bc_last(s_sre, 16); sim_b = bc_last(s_sim, 16)
            TT(bt1, Bsm[0], sre_b, ALU.mult); TT(bt2, Bsm[1], sim_b, ALU.mult)
            for gl in range(2):
                ps_ = slice(gl * 64, (gl + 1) * 64)
                TT(BB[0][ps_, :, gl * 16:(gl + 1) * 16], bt1[ps_], bt2[ps_], ALU.subtract)
            TT(bt1, Bsm[1], sre_b, ALU.mult); TT(bt2, Bsm[0], sim_b, ALU.mult)
            for gl in range(2):
                ps_ = slice(gl * 64, (gl + 1) * 64)
                TT(BB[1][ps_, :, gl * 16:(gl + 1) * 16], bt1[ps_], bt2[ps_], ALU.add)
            for x in range(2):
                S.op("dve", lambda e: e.tensor_copy(out=BBb[x], in_=BB[x]), reads=[b_s5], writes=[b_s5])
            bdt = view(GEN, 13312, 512, F32)
            for j in range(16):
                bank, bb = next_bank()
                for q in range(4):
                    for x in range(2):
                        S.op("pe", lambda e: e.matmul(bank[:, q * 128:(q + 1) * 128], lhsT=BBb[x][:, 4 * q:4 * q + 4, :].rearrange("p r n -> p (r n)"),
                                                      rhs=CP[:, j, 4 * q:4 * q + 4, x, :], start=(x == 0), stop=(x == 1)), inc=(q == 3 and x == 1),
                             reads=[b_s5, b_cp], writes=[bb])
                if j == 0:
                    for q in range(4):
                        S.op("dve", lambda e: e.tensor_tensor(out=bdt, in0=bank[:, q * 128:(q + 1) * 128], in1=bdm, op=ALU.mult), reads=[bb, b_cst, b_s5], writes=[b_s5])
                        S.op("dve", lambda e: e.scalar_tensor_tensor(out=BD[:, q, 0, :], in0=ident_f, scalar=dcol[:, q:q + 1], in1=bdt, op0=ALU.mult, op1=ALU.add),
                             reads=[b_s5, b_cst, b_prm], writes=[b_bd[q]])
                else:
                    bdm_b = bass.AP(bdm.tensor, bdm.offset, [list(bdm.ap[0]), [0, 4], list(bdm.ap[1])])
                    S.op("dve", lambda e: e.tensor_tensor(out=BD[:, :, j, :], in0=bank[:, :].rearrange("p (q n) -> p q n", q=4), in1=bdm_b, op=ALU.mult),
                         reads=[bb, b_cst], writes=b_bd)
            mt1 = view(GEN, 13824, 2048, F32).rearrange("p (r n) -> p r n", r=16)
            mt2 = view(GEN, 1024, 2048, F32).rearrange("p (r n) -> p r n", r=16)
            MB = [view(GEN, 3072 + x * 1024, 1024, BF16).rearrange("p (r n) -> p r n", r=16) for x in range(2)]
            for i in range(16):
                j = 15 - i
                pr = bc_last(PW[:, j, 0, :], 32); pi_ = bc_last(PW[:, j, 1, :], 32)
                TT(mt1, BB[0], pr, ALU.mult); TT(mt2, BB[1], pi_, ALU.mult); TT(MB[0], mt1, mt2, ALU.subtract)
                TT(mt1, BB[0], pi_, ALU.mult); TT(mt2, BB[1], pr, ALU.mult); TT(MB[1], mt1, mt2, ALU.add)
                bank, bb = next_bank()
                pb = bank[:, :].bitcast(BF16).rearrange("p (q x n) -> p q x n", q=4, x=2)
                for q in range(4):
                    for x in range(2):
                        S.op("pe", lambda e: e.transpose(pb[:, q, x, :], MB[x][:, 4 * q:4 * q + 4, :].rearrange("p r n -> p (r n)"), ident_b),
                             inc=(q == 3 and x == 1), reads=[b_s5, b_cst], writes=[bb])
                S.op("act", lambda e: e.activation(out=WZ[:, :, i, :, :], in_=pb, func=AF.Copy), reads=[bb], writes=b_wz)
            if stop == "s5gen":
                quiesce(); return
            if EXP:
                xfer("e_cp", view(A2, 0, 34816, F32), [b_cp])
                xfer("e_bd", view(A0, 32 * KB, 16 * KB, F32), b_bd)
                xfer("e_tab", view(A2, 34816, 16384, F32), [b_tab])
            handoff(b_zl, b_zl + [b_s5])
            for q in range(4):
                for rr in range(4):
                    r = 4 * q + rr
                    bank, bb = next_bank()
                    for x in range(2):
                        col = x * 128
                        for i in range(16):
                            S.op("pe", lambda e: e.matmul(bank[:, col:col + 128], lhsT=WZ[32 * rr:32 * rr + 32, q, i, x, :], rhs=US[32 * rr:32 * rr + 32, q, i, :],
                                                          start=(i == 0), stop=(i == 15), tile_position=(32 * rr, 0)), inc=(i == 15 and x == 1),
                                 reads=[b_wz[q], b_us[q]], writes=[bb])
                    S.op("act", lambda e: e.activation(out=ZL[:, r, :, :].rearrange("p x c -> p (x c)"), in_=bank[:, 0:256], func=AF.Copy),
                         reads=[bb], writes=[b_zl[r]])
            if cfg.get("debug"):
                S.dma("sp", dbq, dr["dbg_zl"], view(A2, 51200, 16384, F32), reads=b_zl, writes=[b_dbg])
                S.dma("sp", dbq, dr["dbg_sc"], view(SP_, 0, 4864, F32), reads=[b_s5], writes=[b_dbg])
                S.dma("sp", dbq, dr["dbg_cp"], view(A2, 0, 34816, BF16), reads=[b_cp], writes=[b_dbg])
                S.dma("sp", dbq, dr["dbg_bd"], view(A0, 32 * KB, 16 * KB, BF16), reads=b_bd, writes=[b_dbg])
                S.dma("sp", dbq, dr["dbg_wz"], view(A0, 0, 32 * KB, BF16), reads=b_wz, writes=[b_dbg])
            handoff(b_ww, b_wz + b_ww)
            dt1 = view(GT, 0, 8192, F32).rearrange("p (r c) -> p r c", r=16)
            dt2 = view(GT, 8192, 8192, F32).rearrange("p (r c) -> p r c", r=16)
            b_dt = S.buf("dt"); handoff([b_dt], b_wz)
            magL_b = bc_last(s_magL, NC_)

            def scan_and_mod(init_ap, b_init, final):
                for r in range(16):
                    for x in range(2):
                        ini = 0.0 if init_ap is None else init_ap[:, r, x:x + 1]
                        S.op("dve", lambda e: e.tensor_tensor_scan(out=ZL[:, r, x, :], data0=magL_b[:, r, :], data1=WW[:, r, x, :], initial=ini,
                                                                   op0=ALU.mult, op1=ALU.add), reads=[b_ww[r], b_s5] + ([b_init] if b_init else []), writes=[b_zl[r]])
                if not final:
                    cmul(zend[:, :, 0], zend[:, :, 1], ZL[:, :, 0, NC_ - 1], ZL[:, :, 1, NC_ - 1], EC[:, :, NC_ - 1], ES[:, :, NC_ - 1], s_t, s_t2,
                         rd=b_zl + [b_tab])
                else:
                    S.op("dve", lambda e: e.tensor_tensor(out=dt1, in0=EC, in1=ZL[:, :, 0, :], op=ALU.mult), reads=[b_tab] + b_zl, writes=[b_dt])
                    S.op("dve", lambda e: e.tensor_tensor(out=dt2, in0=ES, in1=ZL[:, :, 1, :], op=ALU.mult), reads=[b_tab] + b_zl, writes=[b_dt])
                    S.op("dve", lambda e: e.tensor_tensor(out=ZS[:, :, 0, 1:NC_ + 1], in0=dt1, in1=dt2, op=ALU.subtract), reads=[b_dt], writes=b_zs)
                    S.op("dve", lambda e: e.tensor_tensor(out=dt1, in0=EC, in1=ZL[:, :, 1, :], op=ALU.mult), reads=[b_tab] + b_zl, writes=[b_dt])
                    S.op("dve", lambda e: e.tensor_tensor(out=dt2, in0=ES, in1=ZL[:, :, 0, :], op=ALU.mult), reads=[b_tab] + b_zl, writes=[b_dt])
                    S.op("dve", lambda e: e.tensor_tensor(out=ZS[:, :, 1, 1:NC_ + 1], in0=dt1, in1=dt2, op=ALU.add), reads=[b_dt], writes=b_zs)
                    S.op("dve", lambda e: e.tensor_copy(out=ZS[:, :, :, 0], in_=init_ap), reads=[b_init], writes=b_zs)

            S.op("dve", lambda e: e.tensor_tensor(out=dt1, in0=EC, in1=ZL[:, :, 0, :], op=ALU.mult), reads=[b_tab] + b_zl, writes=[b_dt])
            S.op("dve", lambda e: e.tensor_tensor(out=dt2, in0=ES, in1=ZL[:, :, 1, :], op=ALU.mult), reads=[b_tab] + b_zl, writes=[b_dt])
            S.op("dve", lambda e: e.tensor_tensor(out=WW[:, :, 0, :], in0=dt1, in1=dt2, op=ALU.add), reads=[b_dt], writes=b_ww)
            S.op("dve", lambda e: e.tensor_tensor(out=dt1, in0=EC, in1=ZL[:, :, 1, :], op=ALU.mult), reads=[b_tab] + b_zl, writes=[b_dt])
            S.op("dve", lambda e: e.tensor_tensor(out=dt2, in0=ES, in1=ZL[:, :, 0, :], op=ALU.mult), reads=[b_tab] + b_zl, writes=[b_dt])
            S.op("dve", lambda e: e.tensor_tensor(out=WW[:, :, 1, :], in0=dt1, in1=dt2, op=ALU.subtract), reads=[b_dt], writes=b_ww)
            scan_and_mod(None, None, False)
            xfer("e_ww", view(A0, 0, 16384, F32), b_ww)
            if IMP:
                xfer("e_cp", view(A2, 0, 34816, F32), [b_cp])
                xfer("e_bd", view(A0, 32 * KB, 16 * KB, F32), b_bd)
                xfer("e_tab", view(A2, 34816, 16384, F32), [b_tab])
            xfer("e_sp", view(SP_, 0, 4864, F32), [b_s5])
            if IMP:
                S.mute = False
            if cfg.get("debug"):
                S.dma("sp", dbq, dr["dbg_zend"], zend.rearrange("p r x -> p (r x)"), reads=[b_s5], writes=[b_dbg])
            if mode == "A":
                S.dma("sp", oq, dr["summ"][:, 0:32], zend.rearrange("p r x -> p (r x)"), reads=[b_s5], writes=[b_out])
                S.wait_all("sp", [b_out])
            if mode != "A":
                sa = dr["summ_all"]
                zall = view(GT, 0, 512, F32).rearrange("p (j r x) -> p j r x", j=4, x=2)
                zq = S.dma_sem(f"zq{l}")
                S.dma("sp", zq, zall.rearrange("p j r x -> p j (r x)"), sa[:, :, 0:32].rearrange("j p c -> p j c"), reads=[b_dt], writes=[b_dt])
                S.op("dve", lambda e: e.memset(sst, 0.0), reads=[b_s5], writes=[b_s5])
                ctr = sc(24)[:, 0:16]; cti = sc(25)[:, 0:16]
                for j in range(4):
                    cmul(ctr, cti, sst[:, :, 0], sst[:, :, 1], s_l128r, s_l128i, s_t, s_t2)
                    TT(ctr, ctr, zall[:, j, :, 0], ALU.add, rd=[b_dt]); TT(cti, cti, zall[:, j, :, 1], ALU.add, rd=[b_dt])
                    TT(ctr, ctr, sst[:, :, 0], ALU.subtract); TT(cti, cti, sst[:, :, 1], ALU.subtract)
                    S.op("dve", lambda e: e.scalar_tensor_tensor(out=sst[:, :, 0], in0=ctr, scalar=pred[:, j:j + 1], in1=sst[:, :, 0], op0=ALU.mult, op1=ALU.add),
                         reads=[b_s5, b_cst], writes=[b_s5])
                    S.op("dve", lambda e: e.scalar_tensor_tensor(out=sst[:, :, 1], in0=cti, scalar=pred[:, j:j + 1], in1=sst[:, :, 1], op0=ALU.mult, op1=ALU.add),
                         reads=[b_s5, b_cst], writes=[b_s5])
                scan_and_mod(sst, b_s5, True)
                if cfg.get("debug"):
                    S.dma("sp", dbq, dr["dbg_zs"], view(A2, 67584, 8256, BF16), reads=b_zs, writes=[b_dbg])
                handoff(b_zg, b_zl + b_zg)
                for q in range(4):
                    for ib in range(4):
                        bank, bb = next_bank()
                        for i4 in range(4):
                            ip = ib * 4 + i4
                            col = i4 * 128
                            for i in range(ip + 1):
                                S.op("pe", lambda e: e.matmul(bank[:, col:col + 128], lhsT=BD[:, q, ip - i, :], rhs=US[:, q, i, :], start=(i == 0), stop=False),
                                     inc=False, reads=[b_bd[q], b_us[q]], writes=[bb])
                            for rr in range(4):
                                r = 4 * q + rr
                                for x in range(2):
                                    lastw = (rr == 3 and x == 1)
                                    S.op("pe", lambda e: e.matmul(bank[32 * rr:32 * rr + 32, col:col + 128], lhsT=CP[:, ip + 1, r, x, :], rhs=ZS[:, r, x, 0:NC_],
                                                                  start=False, stop=(x == 1), tile_position=(0, 32 * rr)), inc=(lastw and i4 == 3),
                                         reads=[b_cp, b_zs[r]], writes=[bb])
                        S.op("act", lambda e: e.activation(out=ZG[:, q, ib * 4:ib * 4 + 4, :].rearrange("p i c -> p (i c)"), in_=bank[:, :], func=AF.Gelu_apprx_tanh),
                             reads=[bb], writes=[b_zg[q]])
                if cfg.get("debug"):
                    S.dma("sp", dbq, dr["dbg_zg"], view(A2, 51200, 16384, BF16), reads=b_zg, writes=[b_dbg])
                wg = dr["s5_w_glu"][l].rearrange("(k p) n -> p k n", p=128)
                wgl, bwg = wload(wg, 4, 512)
                gate = view(GT, 0, 2048, F32)
                ZZ = [view(GT, 2048 + ft * 2048, 2048, F32) for ft in range(4)]
                sqb = [view(GT, 10240 + i * 1024, 1024, BF16) for i in range(2)]
                rst = view(GT, 12288, 2048, F32)
                b_gate = S.buf("gate"); b_zz = S.bufs(4, "zz"); b_sqb = S.bufs(2, "sqb"); b_rst = S.buf("rst")
                handoff([b_gate, b_rst] + b_zz + b_sqb, [b_dt] + b_ww)
                YCv = YC[:, 0:4, :].rearrange("p f (c i) -> p f i c", i=16)
                for cb in range(4):
                    bankq, bbq = next_bank()
                    for ft in range(4):
                        bank, bb = next_bank()
                        for kk in range(4):
                            S.op("pe", lambda e: e.matmul(bank[:, :], lhsT=wgl[:, kk, ft * 128:(ft + 1) * 128], rhs=ZG[:, kk, cb * 4:cb * 4 + 4, :],
                                                          start=(kk == 0), stop=(kk == 3)), inc=(kk == 3), reads=[bwg] + b_zg, writes=[bb])
                        S.op("act", lambda e: e.activation(out=gate, in_=bank[:, :], func=AF.Sigmoid, bias=bglu[:, ft:ft + 1]), reads=[bb, b_prm], writes=[b_gate])
                        S.op("dve", lambda e: e.tensor_tensor(out=ZZ[ft], in0=ZG[:, ft, cb * 4:cb * 4 + 4, :].rearrange("p i c -> p (i c)"), in1=gate, op=ALU.mult),
                             reads=[b_zg[ft], b_gate], writes=[b_zz[ft]])
                        S.op("act", lambda e: e.activation(out=sqb[ft % 2], in_=ZZ[ft], func=AF.Square), reads=[b_zz[ft]], writes=[b_sqb[ft % 2]])
                        S.op("pe", lambda e: e.matmul(bankq[:, :], lhsT=ones_b, rhs=sqb[ft % 2], start=(ft == 0), stop=(ft == 3)), inc=True,
                             reads=[b_sqb[ft % 2], b_cst], writes=[bbq])
                    S.op("dve", lambda e: e.tensor_scalar(out=rst, in0=bankq[:, :], scalar1=1.0 / 512, scalar2=EPS, op0=ALU.mult, op1=ALU.add), reads=[bbq], writes=[b_rst])
                    S.op("act", lambda e: e.activation(out=rst, in_=rst, func=AF.Sqrt), reads=[b_rst], writes=[b_rst])
                    S.op("dve", lambda e: e.reciprocal(out=rst, in_=rst), reads=[b_rst], writes=[b_rst])
                    for ft in range(4):
                        S.op("dve", lambda e: e.scalar_tensor_tensor(out=YCv[:, ft, cb * 4:cb * 4 + 4, :], in0=ZZ[ft].rearrange("p (i c) -> p i c", i=4),
                                                                     scalar=outg[:, ft:ft + 1], in1=rst.rearrange("p (i c) -> p i c", i=4), op0=ALU.mult, op1=ALU.mult),
                             reads=[b_zz[ft], b_rst, b_prm], writes=b_yc)
                if cfg.get("debug"):
                    S.dma("sp", dbq, dr["dbg_yc"], view(A1, 0, 32 * KB, BF16), reads=b_yc, writes=[b_dbg])

            if mode != "A":
                if stop == "s5":
                    quiesce(); return
                S.handoff(X_b, b_qk + b_og + b_us + b_wz + b_bd + b_ww + [b_dt, b_gate, b_rst] + b_zz + b_sqb)
                for tt in range(NTT):
                    S.dma("sp", xq[tt], X[:, tt, :], xin_ap[tt * 128:(tt + 1) * 128, :], writes=[X_b[tt]])
                wo = dr["w_out"][l].rearrange("(k p) n -> p k n", p=128)
                ws = WStream([(wo[:, :, h * 512:(h + 1) * 512], 8, 512) for h in range(2)])
                for h in range(2):
                    wc, bw = ws.get(h)
                    for tt in range(NTT):
                        bank, bb = next_bank()
                        for kk in range(8):
                            S.op("pe", lambda e: e.matmul(bank[:, :], lhsT=YC[:, kk, tt * 128:(tt + 1) * 128], rhs=wc[:, kk, :],
                                                          start=(kk == 0), stop=(kk == 7)), inc=(kk == 7), reads=[bw, b_yc[tt]], writes=[bb])
                        S.op("dve", lambda e: e.tensor_tensor(out=X[:, tt, h * 512:(h + 1) * 512], in0=X[:, tt, h * 512:(h + 1) * 512], in1=bank[:, :], op=ALU.add),
                             reads=[bb, X_b[tt]], writes=[X_b[tt]])

                if cfg.get("dbg_x1"):
                    b_o1 = S.buf("o1")
                    for tt in range(NTT):
                        S.dma("sp", oq, xout_ap[tt * 128:(tt + 1) * 128, :], X[:, tt, :], reads=[X_b[tt]], writes=[b_o1])
                    S.wait_all("sp", [b_o1])
                    return
                if stop == "wout":
                    quiesce(); return
                S.handoff(HT_b, b_yc)
                a2_users = [b_g, b_sm] + b_kt + b_cl + [b_ct, b_tc] + b_vw + b_pt + b_hb + b_hn + [b_cp, b_tab, b_s5] + b_zl + b_zs + b_zg + b_va + [b_gr] + ml_extra
                handoff([b_junk, b_xh] + b_xnb, a2_users)
                norm_to_hT(g2, False, False)
                if cfg.get("dbg_ht2"):
                    b_o1 = S.buf("o1")
                    S.dma("sp", oq, dr["dbg_ht"], view(A1, 0, 32832, BF16)[:, 0:8 * 2051], reads=HT_b, writes=[b_o1])
                w1 = dr["w_ff1"][l].rearrange("(k p) n -> p k n", p=128)
                w2 = dr["w_ff2"][l].rearrange("(k p) n -> p k n", p=128)
                c1 = [(w1[:, :, hc * 512:(hc + 1) * 512], 8, 512) for hc in range(8)]
                c2 = [(w2[:, hc * 4:(hc + 1) * 4, :], 4, 1024) for hc in range(8)]
                chunks = [c1[0]]
                for hc in range(8):
                    if hc + 1 < 8:
                        chunks.append(c1[hc + 1])
                    chunks.append(c2[hc])
                ws = WStream(chunks)
                k.wci = 0

                def wnext():
                    r = ws.get(k.wci)
                    k.wci += 1
                    return r
                hid = [view(SCR, i * 16 * KB, 16 * KB, BF16).rearrange("p (f t) -> p f t", f=4) for i in range(2)]
                b_hid = [S.bufs(4, f"hid{i}") for i in range(2)]
                sq = [view(SCR, 32 * KB + i * 2048, 2048, F32) for i in range(2)]
                b_sq = S.bufs(2, "sq")
                k.sqi = 0
                handoff(b_hid[0] + b_hid[1] + b_sq, a2_users + [b_junk, b_xh] + b_xnb)

                def ffn1(hc):
                    wc, bw = wnext()
                    hb = hid[hc % 2]
                    for ft in range(4):
                        for nb in range(4):
                            bank, bb = next_bank()
                            for kk in range(8):
                                S.op("pe", lambda e: e.matmul(bank[:, :], lhsT=wc[:, kk, ft * 128:(ft + 1) * 128], rhs=HT[:, kk, 3 + nb * 512:3 + (nb + 1) * 512],
                                                              start=(kk == 0), stop=(kk == 7)), inc=(kk == 7), reads=[bw] + HT_b[nb * 4:nb * 4 + 4], writes=[bb])
                            si = k.sqi % 2; k.sqi += 1
                            S.op("act", lambda e: e.activation(out=sq[si], in_=bank[:, :], func=AF.Square), reads=[bb], writes=[b_sq[si]])
                            S.op("dve", lambda e: e.scalar_tensor_tensor(out=hb[:, ft, nb * 512:(nb + 1) * 512], in0=bank[:, :], scalar=0.0, in1=sq[si],
                                                                         op0=ALU.is_gt, op1=ALU.mult), reads=[bb, b_sq[si]], writes=[b_hid[hc % 2][nb]])

                def ffn2(hc):
                    wc, bw = wnext()
                    hb = hid[hc % 2]
                    for tt in range(NTT):
                        for h in range(2):
                            bank, bb = next_bank()
                            for kk in range(4):
                                S.op("pe", lambda e: e.matmul(bank[:, :], lhsT=hb[:, kk, tt * 128:(tt + 1) * 128], rhs=wc[:, kk, h * 512:(h + 1) * 512],
                                                              start=(kk == 0), stop=(kk == 3)), inc=(kk == 3), reads=[bw, b_hid[hc % 2][tt // 4]], writes=[bb])
                            S.op("dve", lambda e: e.tensor_tensor(out=X[:, tt, h * 512:(h + 1) * 512], in0=X[:, tt, h * 512:(h + 1) * 512], in1=bank[:, :], op=ALU.add),
                                 reads=[bb, X_b[tt]], writes=[X_b[tt]])

                ffn1(0)
                for hc in range(8):
                    if hc + 1 < 8:
                        ffn1(hc + 1)
                    ffn2(hc)

                if last:
                    gfin = view(SCR, 40 * KB, 4096, F32)
                    b_gf = S.buf("gfin")
                    fg = dr["final_norm_g"]
                    S.dma("sp", gq, gfin, bass.AP(fg.tensor, fg.offset, [[0, 128], [1, D]]), writes=[b_gf])
                    ot = [view(SCR, 44 * KB + i * 4096, 4096, F32) for i in range(2)]
                    b_ot = S.bufs(2, "ot")
                    handoff([b_junk], [b_junk] + b_hid[0] + b_hid[1])
                    stats_A(0)
                    for g4 in range(4):
                        if g4 + 1 < 4:
                            stats_A(g4 + 1)
                        stats_B(g4)
                        for tt in range(4 * g4, 4 * g4 + 4):
                            S.op("dve", lambda e: e.scalar_tensor_tensor(out=ot[tt % 2], in0=X[:, tt, :], scalar=rstd[:, tt:tt + 1], in1=gfin,
                                                                         op0=ALU.mult, op1=ALU.mult), reads=[X_b[tt], b_sg[g4], b_gf], writes=[b_ot[tt % 2]])
                            S.dma("sp", oq, xout_ap[tt * 128:(tt + 1) * 128, :], ot[tt % 2], reads=[b_ot[tt % 2]], writes=[b_out])
                else:
                    for tt in range(NTT):
                        S.dma("sp", oq, xout_ap[tt * 128:(tt + 1) * 128, :], X[:, tt, :], reads=[X_b[tt]], writes=[b_out])
                S.wait_all("sp", [b_out])

        layers = cfg["layers"]
        for li, l in enumerate(layers):
            layer(l, dr["xin"], dr.get("xhalo"), dr.get("xout"), last=cfg.get("final", False) and li == len(layers) - 1)
    return nc


_NC_CACHE = {}
N_CORES = 8
LAYER_KEYS = ["norm_mix_g", "w_in", "w_out", "norm_ffn_g", "w_ff1", "w_ff2", "ml_conv_w", "ml_conv_b", "ml_b_i", "ml_b_f",
              "ml_norm_g", "s5_a_re", "s5_a_im", "s5_log_dt", "s5_b_re", "s5_b_im", "s5_c_re", "s5_c_im", "s5_d", "s5_w_glu",
              "s5_b_glu", "s5_out_g"]
A_KEYS = ["norm_mix_g", "w_in", "ml_conv_w", "ml_conv_b", "ml_b_i", "ml_b_f", "s5_a_re", "s5_a_im", "s5_log_dt", "s5_b_re", "s5_b_im",
          "s5_c_re", "s5_c_im", "s5_d", "ml_norm_g", "s5_b_glu", "s5_out_g"]
B_KEYS = ["w_out", "norm_ffn_g", "w_ff1", "w_ff2", "s5_w_glu", "ml_norm_g", "s5_b_glu", "s5_out_g"]
XF_NAMES = ["e_a0", "e_va", "e_gates", "e_cl", "e_ww", "e_cp", "e_bd", "e_tab", "e_sp"]
A_CONST = ["ident", "causal", "ones", "par01", "bdmask"]
B_CONST = ["ident", "causal", "ones"]


def _get_nc(mode, final):
    key = (mode, final)
    if key not in _NC_CACHE:
        _NC_CACHE[key] = build(dict(layers=[0], nlayers=1, mode=mode, final=final, debug=False))
    return _NC_CACHE[key]


def _consts():
    par = np.zeros((128, 2), np.float32)
    par[:, 1] = (np.arange(128) // 16) % 2
    par[:, 0] = 1 - par[:, 1]
    return {"ident": np.eye(128, dtype=np.float32), "causal": np.triu(np.ones((128, 128), np.float32)),
            "ones": np.ones((128, 128), np.float32), "par01": par,
            "bdmask": np.kron(np.eye(8), np.ones((16, 16))).astype(np.float32)}


def kernel(**inputs):
    x = np.ascontiguousarray(inputs["x"], dtype=np.float32)
    nb, ls, d = x.shape
    per = ls // 4
    consts = _consts()
    cur = [np.ascontiguousarray(x[c // 4, (c % 4) * per:(c % 4 + 1) * per]) for c in range(N_CORES)]
    preds = []
    for c in range(N_CORES):
        p = np.zeros((128, 4), np.float32)
        for j in range(4):
            if j < c % 4:
                p[:, j] = 1.0
        preds.append(p)
    depth = inputs["w_in"].shape[0]
    for l in range(depth):
        halos = [np.zeros((3, d), np.float32) if c % 4 == 0 else np.ascontiguousarray(cur[c - 1][-3:]) for c in range(N_CORES)]
        lw = {k: np.ascontiguousarray(np.asarray(inputs[k], dtype=np.float32)[l:l + 1]) for k in LAYER_KEYS}
        final = (l == depth - 1)
        ncA = _get_nc("A", False)
        mapsA = []
        for c in range(N_CORES):
            m = {"xin": cur[c], "xhalo": halos[c], "pred": preds[c]}
            m.update({k: consts[k] for k in A_CONST})
            m.update({k: lw[k] for k in A_KEYS})
            mapsA.append(m)
        resA = run_bass_kernel_spmd(ncA, mapsA, core_ids=list(range(N_CORES)))
        summ = [np.asarray(resA.results[c]["summ"]) for c in range(N_CORES)]
        summ_grp = [np.ascontiguousarray(np.stack(summ[4 * g:4 * g + 4])) for g in range(N_CORES // 4)]
        ncB = _get_nc("B", final)
        mapsB = []
        for c in range(N_CORES):
            m = {"xin": cur[c], "pred": preds[c], "summ_all": summ_grp[c // 4],
                 "final_norm_g": np.ascontiguousarray(inputs["final_norm_g"], dtype=np.float32)}
            m.update({k: consts[k] for k in B_CONST})
            m.update({k: lw[k] for k in B_KEYS})
            m.update({k: np.asarray(resA.results[c][k]) for k in XF_NAMES})
            mapsB.append(m)
        resB = run_bass_kernel_spmd(ncB, mapsB, core_ids=list(range(N_CORES)))
        cur = [np.asarray(resB.results[c]["xout"]) for c in range(N_CORES)]
    out = np.empty_like(x)
    for c in range(N_CORES):
        out[c // 4, (c % 4) * per:(c % 4 + 1) * per] = cur[c]
    return out
```

```python
import numpy as np
import concourse.bass as bass
import concourse.mybir as mybir
from concourse.bass_utils import run_bass_kernel_spmd

F32 = mybir.dt.float32
BF16 = mybir.dt.bfloat16
AF = mybir.ActivationFunctionType
ALU = mybir.AluOpType
AX = mybir.AxisListType


class Buf:
    __slots__ = ("name", "w", "r")

    def __init__(self, name):
        self.name = name
        self.w = {}
        self.r = {}


class Sched:
    def __init__(self, nc, ctx):
        self.nc = nc
        self.ctx = ctx
        self.eng = {"pe": nc.tensor, "act": nc.scalar, "dve": nc.vector, "pool": nc.gpsimd, "sp": nc.sync}
        self.sem = {}
        self.cnt = {}
        for k in self.eng:
            self.sem[k] = ctx.enter_context(nc.semaphore("s_" + k))
            self.cnt[k] = 0
        self.waited = {k: {} for k in self.eng}
        self.ndma = 0
        self.nbuf = 0
        self.mute = False

    def buf(self, name=None):
        self.nbuf += 1
        return Buf(name or f"b{self.nbuf}")

    def bufs(self, n, name="b"):
        return [self.buf(f"{name}{i}") for i in range(n)]

    def dma_sem(self, name=None):
        self.ndma += 1
        key = name or f"dma{self.ndma}"
        self.sem[key] = self.ctx.enter_context(self.nc.semaphore("s_" + key))
        self.cnt[key] = 0
        return key

    def _deps(self, e, reads, writes):
        deps = {}
        for b in reads:
            for k, c in b.w.items():
                if deps.get(k, 0) < c:
                    deps[k] = c
        for b in writes:
            for k, c in b.w.items():
                if deps.get(k, 0) < c:
                    deps[k] = c
            for k, c in b.r.items():
                if deps.get(k, 0) < c:
                    deps[k] = c
        eng = self.eng[e]
        for k, c in deps.items():
            if k == e and e == "pe":
                continue
            if self.waited[e].get(k, 0) < c:
                eng.wait_ge(self.sem[k], c)
                self.waited[e][k] = c

    def _record(self, key, c, reads, writes):
        for b in writes:
            b.w = {key: c}
            b.r = {}
        for b in reads:
            if b.r.get(key, 0) < c:
                b.r[key] = c

    def op(self, e, fn, reads=(), writes=(), inc=True):
        if self.mute:
            return None
        self._deps(e, reads, writes)
        ins = fn(self.eng[e])
        if inc:
            self.cnt[e] += 1
            ins.then_inc(self.sem[e], 1)
            self._record(e, self.cnt[e], reads, writes)
        else:
            self._record(e, self.cnt[e] + 1, reads, writes)
        return ins

    def seal(self, key, bufs):
        if self.mute:
            return
        c = self.cnt[key]
        for b in bufs:
            if key in b.w:
                b.w[key] = c

    def handoff(self, news, olds):
        w = {}
        r = {}
        for ob in olds:
            for k2, c2 in ob.w.items():
                if w.get(k2, 0) < c2:
                    w[k2] = c2
            for k2, c2 in ob.r.items():
                if r.get(k2, 0) < c2:
                    r[k2] = c2
        for nb in news:
            nb.w = dict(w)
            nb.r = dict(r)

    def dma(self, q, dsem, out, in_, reads=(), writes=(), **kw):
        if self.mute:
            return None
        self._deps(q, reads, writes)
        ins = self.eng[q].dma_start(out=out, in_=in_, **kw)
        self.cnt[dsem] += 16
        ins.then_inc(self.sem[dsem], 16)
        self._record(dsem, self.cnt[dsem], reads, writes)
        return ins

    def wait_all(self, e, bufs):
        if self.mute:
            return
        self._deps(e, bufs, ())


import numpy as np
from contextlib import ExitStack

NT = 2048
NTT = 16
D = 1024
DIN = 2568
DFF = 4096
EPS = 1e-6
KB = 1024


class KB_:
    pass


def build(cfg):
    nc = bass.Bass("TRN2", target_bir_lowering=False)
    k = KB_()
    k.nc = nc
    k.cfg = cfg
    L = cfg.get("nlayers", 1)
    dr = {}

    def din(name, shape, dt=F32):
        dr[name] = nc.dram_tensor(name, list(shape), dt, kind="ExternalInput").ap()
        return dr[name]

    def dout(name, shape, dt=F32):
        dr[name] = nc.dram_tensor(name, list(shape), dt, kind="ExternalOutput").ap()
        return dr[name]

    mode = cfg.get("mode", "B")
    IMP = (mode == "B")
    EXP = (mode == "A")
    PREP = (mode == "P")
    S5IMP = mode in ("A", "B")
    XF = {"e_a0": (16384, "A", "B"), "e_va": (4256, "A", "B"), "e_gates": (1152, "A", "B"), "e_cl": (8256, "A", "B"),
          "e_ww": (4096, "A", "B"), "e_wz": (8192, "P", "A"), "e_cp": (8704, "P", "B"), "e_bd": (4096, "P", "B"),
          "e_tab": (4096, "P", "AB"), "e_sp": (1216, "P", "AB")}
    S5RAW = ["s5_a_re", "s5_a_im", "s5_log_dt", "s5_b_re", "s5_b_im", "s5_c_re", "s5_c_im", "s5_d", "par01", "bdmask"]
    PASS1 = ["xhalo", "norm_mix_g", "w_in", "ml_conv_w", "ml_conv_b", "ml_b_i", "ml_b_f"]
    PASS2 = ["w_out", "norm_ffn_g", "w_ff1", "w_ff2", "final_norm_g", "s5_w_glu", "summ_all"]
    COMMON = ["ident", "causal", "ones", "pred"]
    SMALLP = ["ml_norm_g", "s5_b_glu", "s5_out_g"]
    REAL = {"B0": ["xin"] + PASS1 + S5RAW + PASS2 + COMMON + SMALLP,
            "A": ["xin"] + PASS1 + COMMON + SMALLP,
            "B": ["xin"] + PASS2 + COMMON + SMALLP,
            "P": S5RAW + COMMON}[mode]
    _din_real = din

    def din(name, shape, dt=F32):
        if name in REAL:
            return _din_real(name, shape, dt)
        dr[name] = nc.dram_tensor(name, list(shape), dt).ap()
        return dr[name]
    din("xin", [NT, D])
    din("xhalo", [3, D])
    din("norm_mix_g", [L, D]); din("w_in", [L, D, DIN])
    din("ml_conv_w", [L, 4, 1024]); din("ml_conv_b", [L, 1024])
    din("ml_b_i", [L, 4]); din("ml_b_f", [L, 4])
    din("s5_a_re", [L, 32, 64]); din("s5_a_im", [L, 32, 64]); din("s5_log_dt", [L, 32])
    din("s5_b_re", [L, 32, 64, 16]); din("s5_b_im", [L, 32, 64, 16]); din("s5_c_re", [L, 32, 16, 64]); din("s5_c_im", [L, 32, 16, 64])
    din("s5_d", [L, 32, 16])
    din("par01", [128, 2]); din("bdmask", [128, 128])
    din("w_out", [L, D, D])
    din("norm_ffn_g", [L, D]); din("w_ff1", [L, D, DFF]); din("w_ff2", [L, DFF, D])
    din("final_norm_g", [D])
    din("s5_w_glu", [L, 512, 512])
    din("summ_all", [4, 128, 556])
    din("ident", [128, 128]); din("causal", [128, 128]); din("ones", [128, 128])
    din("ml_norm_g", [L, 512]); din("s5_b_glu", [L, 512]); din("s5_out_g", [L, 512])
    din("pred", [128, 4])
    if EXP:
        dout("summ", [128, 556])
    for nm_, (w_, prod_, cons_) in XF.items():
        if mode == prod_:
            dout(nm_, [128, w_])
        elif mode in cons_:
            _din_real(nm_, [128, w_])
    if mode in ("B", "B0"):
        dout("xout", [NT, D])
    if cfg.get("dbg_ht2"):
        dout("dbg_ht", [128, 8 * 2051], BF16)
    if cfg.get("debug"):
        dout("dbg_u", [128, 4 * 2048], BF16)
        dout("dbg_qk", [128, 8 * 2048], BF16)
        dout("dbg_v", [128, 16 * 4 * 129], BF16)
        dout("dbg_o", [128, 16 * 512], BF16)
        dout("dbg_if", [128, 128])
        dout("dbg_yc", [128, 8 * 2048], BF16)
        dout("dbg_g", [128, 16 * 64])
        dout("dbg_zl", [128, 4096]); dout("dbg_zs", [128, 16 * 2 * 129], BF16); dout("dbg_zg", [128, 8192], BF16)
        dout("dbg_sc", [128, 1216]); dout("dbg_cp", [128, 17408], BF16); dout("dbg_bd", [128, 8192], BF16); dout("dbg_wz", [128, 16384], BF16)
        dout("dbg_zend", [128, 32])
    k.dr = dr

    with ExitStack() as ctx:
        S = Sched(nc, ctx)
        k.S = S
        ctx.enter_context(nc.allow_non_contiguous_dma(reason="small param loads"))
        ctx.enter_context(nc.allow_low_precision(reason="bf16 matmul operands"))
        A0_B, A1_B, A2_B = 64 * KB, 33 * KB + 256, 79 * KB
        arena = ctx.enter_context(nc.sbuf_tensor("arena", [128, (A0_B + A1_B + A2_B) // 4], F32))
        ring = ctx.enter_context(nc.sbuf_tensor("ring", [128, 3 * 4096], BF16))
        cst = ctx.enter_context(nc.sbuf_tensor("cst", [128, 1024], F32))
        banks = [ctx.enter_context(nc.psum_tensor(f"ps{i}", [128, 512], F32)) for i in range(8)]
        bank_bufs = S.bufs(8, "bank")
        k.bank_i = 0
        block = ctx.enter_context(nc.Block())

        def view(base, off, nbytes, dt):
            assert off % 4 == 0 and nbytes % 4 == 0
            a = arena[:, (base + off) // 4:(base + off + nbytes) // 4]
            return a if dt == F32 else a.bitcast(dt)
        A0, A1, A2 = 0, A0_B, A0_B + A1_B

        def next_bank():
            i = k.bank_i
            k.bank_i = (i + 1) % 8
            return banks[i], bank_bufs[i]

        ident_f = cst[:, 0:128]
        ident_b = cst[:, 128:192].bitcast(BF16)
        b_cst = S.buf("cst")
        dq = S.dma_sem("dq_misc")
        S.dma("sp", dq, ident_f, dr["ident"], writes=[b_cst])
        tri_f = cst[:, 384:512]
        ones_f = cst[:, 512:640]
        tri_b = cst[:, 192:256].bitcast(BF16)
        S.dma("sp", dq, tri_f, dr["causal"], writes=[b_cst])
        S.dma("sp", dq, ones_f, dr["ones"], writes=[b_cst])
        par01 = cst[:, 752:754]
        bdm = cst[:, 768:896]
        ones_b = cst[:, 896:960].bitcast(BF16)
        if "par01" in REAL:
            S.dma("sp", dq, par01, dr["par01"], writes=[b_cst])
        pred = cst[:, 972:976]
        pmask = cst[:, 980:984]
        S.dma("sp", dq, pred, dr["pred"], writes=[b_cst])
        if "bdmask" in REAL:
            S.dma("sp", dq, bdm, dr["bdmask"], writes=[b_cst])
        S.seal(dq, [b_cst])
        S.op("dve", lambda e: e.tensor_copy(out=ones_b, in_=ones_f), reads=[b_cst], writes=[b_cst])
        S.op("dve", lambda e: e.tensor_scalar(out=pmask, in0=pred, scalar1=1e6, scalar2=-1e6, op0=ALU.mult, op1=ALU.add), reads=[b_cst], writes=[b_cst])
        S.op("dve", lambda e: e.tensor_copy(out=ident_b, in_=ident_f), reads=[b_cst], writes=[b_cst])
        S.op("dve", lambda e: e.tensor_copy(out=tri_b, in_=tri_f), reads=[b_cst], writes=[b_cst])

        X = view(A0, 0, 64 * KB, F32).rearrange("p (t d) -> p t d", t=NTT)
        X_b = S.bufs(NTT, "X")
        HT = view(A1, 0, 8 * 2051 * 2 + 0, BF16) if False else view(A1, 0, 32832, BF16)[:, 0:8 * 2051].rearrange("p (k t) -> p k t", k=8)
        HT_b = S.bufs(NTT + 1, "HT")
        YC = view(A1, 0, 32 * KB, BF16).rearrange("p (k t) -> p k t", k=8)
        QK = view(A0, 0, 32 * KB, BF16).rearrange("p (f t) -> p f t", f=8)
        OG = view(A0, 32 * KB, 16 * KB, BF16).rearrange("p (t d) -> p t d", t=NTT)
        US = view(A0, 48 * KB, 16 * KB, BF16).rearrange("p (q i c) -> p q i c", q=4, i=16)
        VA = view(A2, 0, 16512, BF16).rearrange("p (t h v) -> p t h v", t=NTT, h=4)
        GR = view(A2, 16512, 512, F32)
        SCR = A2 + 17024

        xq = [S.dma_sem(f"xq{i}") for i in range(16)]
        hq = S.dma_sem("hq"); gq = S.dma_sem("gq")
        oq = S.dma_sem("oq")
        wq = [S.dma_sem(f"wq{i}") for i in range(3)]
        ring_b = S.bufs(3, "ring")
        k.wi = 0

        def wload(src_ap, nk, ncols):
            i = k.wi % 3
            k.wi += 1
            v = ring[:, i * 4096: i * 4096 + nk * ncols].rearrange("p (k n) -> p k n", k=nk)
            S.dma("pool", wq[i], v, src_ap, writes=[ring_b[i]])
            return v, ring_b[i]

        class WStream:
            def __init__(self, chunks):
                self.chunks = chunks
                self.loaded = []

            def get(self, i, ahead=2):
                while len(self.loaded) < min(len(self.chunks), i + 1 + ahead):
                    self.loaded.append(wload(*self.chunks[len(self.loaded)]))
                return self.loaded[i]

        k.pq = None

        def load_pvec(dst, src_1d, b, q="sp"):
            S.dma(q, k.pq, dst, src_1d.rearrange("(k p) -> p k", p=128), writes=[b])

        tmp_b = S.bufs(4, "tmp")

        def quiesce():
            for e_ in ("pe", "act", "dve", "pool"):
                if S.cnt[e_] > 0:
                    nc.sync.wait_ge(S.sem[e_], S.cnt[e_])
            for key_, c_ in S.cnt.items():
                if key_ not in S.eng and c_ > 0:
                    nc.sync.wait_ge(S.sem[key_], c_)

        def layer(l, xin_ap, xh_ap, xout_ap, last):
            stop = cfg.get("stop")
            prm = cst[:, 256:256 + 64]
            b_prm = S.buf("prm")
            pq_l = S.dma_sem(f"pq{l}"); k.pq = pq_l
            g1 = cst[:, 640:648]; g2 = cst[:, 648:656]
            cw = cst[:, 656:688].rearrange("p (j f) -> p j f", j=4)
            cb = cst[:, 688:696]
            b_out = S.buf("out")
            if "norm_mix_g" in REAL:
                load_pvec(g1, dr["norm_mix_g"][l], b_prm)
                for j in range(4):
                    load_pvec(cw[:, j, :], dr["ml_conv_w"][l, j], b_prm)
                load_pvec(cb, dr["ml_conv_b"][l], b_prm)
            if "norm_ffn_g" in REAL:
                load_pvec(g2, dr["norm_ffn_g"][l], b_prm)
            mlg = cst[:, 740:744]
            bif = cst[:, 744:752]
            dcol = cst[:, 960:964]; bglu = cst[:, 964:968]; outg = cst[:, 968:972]
            if "ml_b_i" in REAL:
                bi_ = dr["ml_b_i"][l]; bf_ = dr["ml_b_f"][l]
                S.dma("sp", pq_l, bif[:, 0:4], bass.AP(bi_.tensor, bi_.offset, [[0, 128], [1, 4]]), writes=[b_prm])
                S.dma("sp", pq_l, bif[:, 4:8], bass.AP(bf_.tensor, bf_.offset, [[0, 128], [1, 4]]), writes=[b_prm])
            if "s5_d" in REAL:
                load_pvec(dcol, dr["s5_d"][l].rearrange("g p -> (g p)"), b_prm)
            if "ml_norm_g" in REAL:
                load_pvec(mlg, dr["ml_norm_g"][l], b_prm)
                load_pvec(bglu, dr["s5_b_glu"][l], b_prm)
                load_pvec(outg, dr["s5_out_g"][l], b_prm)
            if S.cnt[pq_l] == 0:
                S.op("dve", lambda e: e.memset(cst[:, 740:744], 0.0), writes=[b_prm])
            S.seal(pq_l, [b_prm])

            def xfer(name, ap, bufs):
                w_, prod_, cons_ = XF[name]
                was = S.mute; S.mute = False
                if mode == prod_:
                    S.dma("sp", oq, dr[name], ap, reads=bufs, writes=[b_out])
                elif mode in cons_:
                    q_ = S.dma_sem(f"{name}_{l}")
                    S.dma("sp", q_, ap, dr[name], writes=bufs)
                S.mute = was
            if IMP or PREP:
                S.mute = True
            ssq = cst[:, 700:717]
            rstd = cst[:, 720:737]
            b_st = S.bufs(17, "st")
            junk = view(SCR, 0, 2048, BF16)
            b_junk = S.buf("junk")
            xnb = [view(SCR, 2048 + i * 2048, 2048, BF16) for i in range(2)]
            b_xnb = S.bufs(2, "xnb")
            xh_t = view(SCR, 6144, 4096, F32)
            b_xh = S.buf("xh")

            def rms_stats(src, np_, col, bsrc):
                S.op("act", lambda e: e.activation(out=junk[:np_], in_=src, func=AF.Square, accum_out=ssq[:np_, col:col + 1]),
                     reads=[bsrc], writes=[b_junk, b_st[col]])
                S.op("dve", lambda e: e.tensor_scalar(out=rstd[:np_, col:col + 1], in0=ssq[:np_, col:col + 1], scalar1=1.0 / D, scalar2=EPS,
                                                      op0=ALU.mult, op1=ALU.add), reads=[b_st[col]], writes=[b_st[col]])
                S.op("act", lambda e: e.activation(out=rstd[:np_, col:col + 1], in_=rstd[:np_, col:col + 1], func=AF.Sqrt),
                     reads=[b_st[col]], writes=[b_st[col]])
                S.op("dve", lambda e: e.reciprocal(out=rstd[:np_, col:col + 1], in_=rstd[:np_, col:col + 1]), reads=[b_st[col]], writes=[b_st[col]])

            b_sg = S.bufs(5, "stg")

            def stats_A(g4):
                tiles = [NTT] if g4 == 4 else range(4 * g4, 4 * g4 + 4)
                for tt in tiles:
                    halo = tt == NTT
                    np_ = 3 if halo else 128
                    src, bsrc = (xh_t[:3, :], b_xh) if halo else (X[:, tt, :], X_b[tt])
                    S.op("act", lambda e: e.activation(out=junk[:np_], in_=src, func=AF.Square, accum_out=ssq[:np_, tt:tt + 1]),
                         reads=[bsrc], writes=[b_junk, b_sg[g4]])

            def stats_B(g4):
                c0, c1 = (NTT, NTT + 1) if g4 == 4 else (4 * g4, 4 * g4 + 4)
                np_ = 3 if g4 == 4 else 128
                S.op("dve", lambda e: e.tensor_scalar(out=rstd[:np_, c0:c1], in0=ssq[:np_, c0:c1], scalar1=1.0 / D, scalar2=EPS,
                                                      op0=ALU.mult, op1=ALU.add), reads=[b_sg[g4]], writes=[b_sg[g4]])
                S.op("act", lambda e: e.activation(out=rstd[:np_, c0:c1], in_=rstd[:np_, c0:c1], func=AF.Sqrt), reads=[b_sg[g4]], writes=[b_sg[g4]])
                S.op("dve", lambda e: e.reciprocal(out=rstd[:np_, c0:c1], in_=rstd[:np_, c0:c1]), reads=[b_sg[g4]], writes=[b_sg[g4]])

            def norm_to_hT(gvec, with_halo, from_dram):
                gb = bass.AP(gvec.tensor, gvec.offset, [list(gvec.ap[0]), list(gvec.ap[1]), [0, 128]])
                for tt in range(NTT):
                    if from_dram:
                        S.dma("sp", xq[tt], X[:, tt, :], xin_ap[tt * 128:(tt + 1) * 128, :], writes=[X_b[tt]])
                if with_halo:
                    S.dma("sp", hq, xh_t[:3, :], xh_ap, writes=[b_xh])

                def stage_C(g4):
                    tiles = [NTT] if g4 == 4 else range(4 * g4, 4 * g4 + 4)
                    for tt in tiles:
                        halo = tt == NTT
                        np_ = 3 if halo else 128
                        src, bsrc = (xh_t[:3, :], b_xh) if halo else (X[:, tt, :], X_b[tt])
                        xb = xnb[tt % 2]; bx = b_xnb[tt % 2]
                        S.op("act", lambda e: e.activation(out=xb[:np_], in_=src, func=AF.Copy, scale=rstd[:np_, tt:tt + 1]),
                             reads=[bsrc, b_sg[g4]], writes=[bx])
                        bank, bb = next_bank()
                        pb = bank[:, 0:512].bitcast(BF16).rearrange("p (k t) -> p k t", k=8)
                        for kk in range(8):
                            S.op("pe", lambda e: e.transpose(pb[:, kk, 0:np_], xb[:np_, kk * 128:(kk + 1) * 128], ident_b[:np_, :np_]),
                                 inc=(kk == 7), reads=[bx, b_cst], writes=[bb])
                        c0 = 0 if halo else 3 + tt * 128
                        S.op("dve", lambda e: e.tensor_tensor(out=HT[:, :, c0:c0 + np_], in0=pb[:, :, 0:np_], in1=gb[:, :, 0:np_], op=ALU.mult),
                             reads=[bb, b_prm], writes=[HT_b[tt]])
                ng = 5 if with_halo else 4
                stats_A(0)
                for g4 in range(ng):
                    if g4 + 1 < ng:
                        stats_A(g4 + 1)
                    stats_B(g4)
                    stage_C(g4)

            norm_to_hT(g1, True, True)

            if stop == "norm1":
                quiesce(); return
            win = dr["w_in"][l].rearrange("(k p) n -> p k n", p=128)
            chunks = [(win[:, :, c * 512:(c + 1) * 512], 8, 512) for c in range(5)] + [(win[:, :, 2560:2568], 8, 8)]
            ws = WStream(chunks)
            stage = [view(SCR, 10240 + i * 8448, 8448, F32) for i in range(2)]
            b_stage = S.bufs(2, "stage")
            acc = view(SCR, 10240 + 2 * 8448, 8192, F32)
            b_acc = S.buf("acc")
            b_us = S.bufs(4, "us")
            b_qk = S.bufs(8, "qk")
            b_va = S.bufs(NTT, "va")
            b_og = S.bufs(NTT, "og")
            b_gr = S.buf("gr")
            allHT = HT_b
            S.handoff(b_us + b_qk + b_og, X_b)
            S.op("pool", lambda e: e.memset(VA[:, :, :, 128:129], 1.0), writes=b_va)
            for ci in range(3):
                wc, bw = ws.get(ci)
                for ft in range(4):
                    f = (ci - 1) * 4 + ft
                    if ci > 0:
                        st = stage[f % 2]; bs = b_stage[f % 2]
                        bank, bb = next_bank()
                        for kk in range(8):
                            S.op("pe", lambda e: e.matmul(bank[:, 0:3], lhsT=wc[:, kk, ft * 128:(ft + 1) * 128], rhs=HT[:, kk, 0:3],
                                                          start=(kk == 0), stop=(kk == 7)), inc=(kk == 7), reads=[bw, HT_b[NTT]], writes=[bb])
                        S.op("act", lambda e: e.activation(out=st[:, 0:3], in_=bank[:, 0:3], func=AF.Copy), reads=[bb], writes=[bs])
                    for nb in range(4):
                        bank, bb = next_bank()
                        for kk in range(8):
                            S.op("pe", lambda e: e.matmul(bank[:, :], lhsT=wc[:, kk, ft * 128:(ft + 1) * 128],
                                                          rhs=HT[:, kk, 3 + nb * 512:3 + (nb + 1) * 512], start=(kk == 0), stop=(kk == 7)),
                                 inc=(kk == 7), reads=[bw] + allHT[nb * 4:nb * 4 + 4], writes=[bb])
                        if ci == 0:
                            dst = US[:, ft, :, nb * 32:(nb + 1) * 32]
                            src = bank[:, :].rearrange("p (c i) -> p i c", i=16)
                            S.op("act", lambda e: e.activation(out=dst, in_=src, func=AF.Copy), reads=[bb], writes=[b_us[ft]])
                        else:
                            S.op("act", lambda e: e.activation(out=st[:, 3 + nb * 512:3 + (nb + 1) * 512], in_=bank[:, :], func=AF.Copy),
                                 reads=[bb], writes=[bs])
                    if ci > 0:
                        S.op("dve", lambda e: e.tensor_scalar(out=acc, in0=st[:, 0:2048], scalar1=cw[:, 0, f:f + 1], scalar2=None, op0=ALU.mult),
                             reads=[bs, b_prm], writes=[b_acc])
                        for j in range(1, 4):
                            S.op("dve", lambda e: e.scalar_tensor_tensor(out=acc, in0=st[:, j:j + 2048], scalar=cw[:, j, f:f + 1], in1=acc,
                                                                         op0=ALU.mult, op1=ALU.add), reads=[bs, b_prm, b_acc], writes=[b_acc])
                        S.op("act", lambda e: e.activation(out=QK[:, f, :], in_=acc, func=AF.Silu, bias=cb[:, f:f + 1]),
                             reads=[b_acc, b_prm], writes=[b_qk[f]])
            for ci in (3, 4):
                wc, bw = ws.get(ci)
                for tt in range(NTT):
                    bank, bb = next_bank()
                    for kk in range(8):
                        S.op("pe", lambda e: e.matmul(bank[:, :], lhsT=HT[:, kk, 3 + tt * 128:3 + (tt + 1) * 128], rhs=wc[:, kk, :],
                                                      start=(kk == 0), stop=(kk == 7)), inc=(kk == 7), reads=[bw, HT_b[tt]], writes=[bb])
                    if ci == 3:
                        S.op("act", lambda e: e.activation(out=VA[:, tt, :, 0:128], in_=bank[:, :].rearrange("p (h v) -> p h v", h=4), func=AF.Copy),
                             reads=[bb], writes=[b_va[tt]])
                    else:
                        S.op("act", lambda e: e.activation(out=OG[:, tt, :], in_=bank[:, :], func=AF.Sigmoid), reads=[bb], writes=[b_og[tt]])
            wc, bw = ws.get(5)
            bank, bb = next_bank()
            for tt in range(NTT):
                for kk in range(8):
                    S.op("pe", lambda e: e.matmul(bank[:, tt * 8:(tt + 1) * 8], lhsT=HT[:, kk, 3 + tt * 128:3 + (tt + 1) * 128], rhs=wc[:, kk, :],
                                                  start=(kk == 0), stop=(kk == 7)), inc=(kk == 7), reads=[bw, HT_b[tt]], writes=[bb])
            S.op("dve", lambda e: e.tensor_copy(out=GR, in_=bank[:, 0:128]), reads=[bb], writes=[b_gr])

            if cfg.get("debug"):
                dbq = S.dma_sem("dbq")
                b_dbg = S.buf("dbg")
                S.dma("sp", dbq, dr["dbg_u"], view(A0, 48 * KB, 16 * KB, BF16), reads=b_us, writes=[b_dbg])
                S.dma("sp", dbq, dr["dbg_qk"], view(A0, 0, 32 * KB, BF16), reads=b_qk, writes=[b_dbg])
                S.dma("sp", dbq, dr["dbg_v"], view(A2, 0, 16512, BF16), reads=b_va, writes=[b_dbg])
                S.dma("sp", dbq, dr["dbg_o"], view(A0, 32 * KB, 16 * KB, BF16), reads=b_og, writes=[b_dbg])
                S.dma("sp", dbq, dr["dbg_if"], GR, reads=[b_gr], writes=[b_dbg])
                pass

            if stop == "win":
                quiesce(); return
            xfer("e_a0", view(A0, 0, 64 * KB, F32), b_qk + b_og + b_us)
            xfer("e_va", view(A2, 0, 17024, F32), b_va + [b_gr])
            b_yc = S.bufs(NTT, "yc")
            S.handoff(b_yc, HT_b)

            MS = SCR
            def garr(i):
                return view(MS, i * 256, 256, F32).rearrange("p (m h) -> p m h", h=4)
            nlf, ig, nF, g, Gm, R0, Rr, w0, wv, clamp, dl, ep, incl, t1 = [garr(i) for i in range(14)]
            sm = view(MS, 14 * 256, 256, F32)
            gcol = sm[:, 0:1]; dm = sm[:, 4:8]; rec = sm[:, 8:12]; ss = sm[:, 12:16]; rs4 = sm[:, 16:20]
            Gb = view(MS, 15 * 256, 512, F32)
            b_g = S.buf("gates")
            b_sm = S.buf("sm")
            KT = view(MS, 4608, 16512, BF16)
            kTok = KT[:, 0:16 * 512].rearrange("p (m d) -> p m d", m=16)
            Cin = KT[:, 0:64 * 129].rearrange("p (c v) -> p c v", c=64)
            b_kt = S.bufs(16, "kt")
            CL = view(MS, 4608 + 16512, 64 * 129 * 4, F32).rearrange("p (c v) -> p c v", c=64)
            b_cl = S.bufs(16, "cl")
            T0 = MS + 4608 + 16512 + 64 * 129 * 4
            Ct = view(T0, 0, 2064, F32).rearrange("p (h v) -> p h v", h=4)
            tmpC = view(T0, 2064, 2064, F32).rearrange("p (h v) -> p h v", h=4)
            b_ct = S.buf("ct"); b_tc = S.buf("tmpc")
            Vw = [view(T0, 4128 + i * 1032, 1032, BF16).rearrange("p (h v) -> p h v", h=4) for i in range(2)]
            b_vw = S.bufs(2, "vw")
            PT = [view(T0, 6192 + i * 256, 256, BF16) for i in range(2)]
            b_pt = S.bufs(2, "pt")
            hbuf = [view(T0, 6704 + i * 512, 512, F32) for i in range(4)]
            b_hb = S.bufs(4, "hb")
            hn = [view(T0, 8752 + i * 256, 256, BF16) for i in range(2)]
            b_hn = S.bufs(2, "hn")
            junk2 = view(T0, 9264, 256, BF16)
            assert T0 + 9520 <= A2 + A2_B, (T0 + 9520 - A2 - A2_B)

            handoff = S.handoff
            handoff([b_g, b_sm] + b_kt + b_cl + [b_ct, b_tc] + b_vw + b_pt + b_hb + b_hn, b_stage + [b_acc, b_junk, b_xh] + b_xnb)
            GRv = GR.rearrange("p (m c) -> p m c", c=8)

            def bc_m(ap4):
                return bass.AP(ap4.tensor, ap4.offset, [list(ap4.ap[0]), [0, 16], list(ap4.ap[1])])

            def bc_last(ap, n):
                return bass.AP(ap.tensor, ap.offset, [list(x) for x in ap.ap] + [[0, n]])

            def flat(a):
                return a.rearrange("p m h -> p (m h)")
            LN_S = -0.5 * float(np.log(128.0))
            S.op("pool", lambda e: e.memset(view(MS, 0, 4608, F32), 0.0), writes=[b_g, b_sm])
            S.op("dve", lambda e: e.tensor_tensor(out=ig, in0=GRv[:, :, 0:4], in1=bc_m(bif[:, 0:4]), op=ALU.add), reads=[b_gr, b_prm], writes=[b_g])
            S.op("dve", lambda e: e.tensor_tensor(out=t1, in0=GRv[:, :, 4:8], in1=bc_m(bif[:, 4:8]), op=ALU.add), reads=[b_gr, b_prm], writes=[b_g])
            S.op("act", lambda e: e.activation(out=t1, in_=t1, func=AF.Exp, scale=-1.0), reads=[b_g], writes=[b_g])
            S.op("act", lambda e: e.activation(out=nlf, in_=t1, func=AF.Ln, bias=1.0), reads=[b_g], writes=[b_g])
            bankA, bbA = next_bank()
            S.op("pe", lambda e: e.matmul(bankA[:, 0:64], lhsT=tri_f, rhs=flat(nlf), start=True, stop=True), reads=[b_g, b_cst], writes=[bbA])
            bankB, bbB = next_bank()
            S.op("pe", lambda e: e.matmul(bankB[:, 0:64], lhsT=ones_f, rhs=flat(nlf), start=True, stop=True), reads=[b_g, b_cst], writes=[bbB])
            bA = bankA[:, 0:64].rearrange("p (m h) -> p m h", h=4)
            bB = bankB[:, 0:64].rearrange("p (m h) -> p m h", h=4)
            for h in range(4):
                S.op("dve", lambda e: e.tensor_tensor_scan(out=incl[:, :, h], data0=ones_f[:, 0:16], data1=bB[:, :, h], initial=0.0,
                                                           op0=ALU.mult, op1=ALU.add), reads=[bbB, b_cst, b_g], writes=[b_g])
            S.op("dve", lambda e: e.tensor_tensor(out=t1, in0=incl, in1=bB, op=ALU.subtract), reads=[b_g, bbB], writes=[b_g])
            S.op("dve", lambda e: e.tensor_tensor(out=nF, in0=t1, in1=bA, op=ALU.add), reads=[b_g, bbA], writes=[b_g])
            S.op("dve", lambda e: e.tensor_tensor(out=g, in0=ig, in1=nF, op=ALU.add), reads=[b_g], writes=[b_g])
            bankT, bbT = next_bank()
            S.op("pe", lambda e: e.transpose(bankT[0:64, 0:128], flat(g), ident_f), reads=[b_g, b_cst], writes=[bbT])
            S.op("dve", lambda e: e.tensor_reduce(out=gcol[0:64, :], in_=bankT[0:64, 0:128], axis=AX.X, op=ALU.max), reads=[bbT], writes=[b_sm])
            S.op("dve", lambda e: e.tensor_copy(out=Gb[0:64, :], in_=bc_last(gcol[0:64, 0:1], 128)[:, 0, :]), reads=[b_sm], writes=[b_sm])
            bankG, bbG = next_bank()
            S.op("pe", lambda e: e.matmul(bankG[:, 0:64], lhsT=Gb[0:64, :], rhs=ident_f[0:64, 0:64], start=True, stop=True), reads=[b_sm, b_cst], writes=[bbG])
            S.op("dve", lambda e: e.tensor_copy(out=flat(Gm), in_=bankG[:, 0:64]), reads=[bbG], writes=[b_g])
            for h in range(4):
                S.op("dve", lambda e: e.tensor_tensor_scan(out=R0[:, :, h], data0=Gm[:, :, h], data1=Gm[:, :, h], initial=-1e30,
                                                           op0=ALU.max, op1=ALU.max), reads=[b_g], writes=[b_g])
            S.op("dve", lambda e: e.tensor_tensor(out=t1, in0=g, in1=R0, op=ALU.subtract), reads=[b_g], writes=[b_g])
            S.op("act", lambda e: e.activation(out=w0, in_=t1, func=AF.Exp, bias=LN_S), reads=[b_g], writes=[b_g])
            S.op("dve", lambda e: e.tensor_tensor(out=t1[:, 1:16, :], in0=R0[:, 0:15, :], in1=R0[:, 1:16, :], op=ALU.subtract), reads=[b_g], writes=[b_g])
            S.op("act", lambda e: e.activation(out=dl[:, 1:16, :], in_=t1[:, 1:16, :], func=AF.Exp), reads=[b_g], writes=[b_g])
            for m in range(16):
                cs = slice(m * 128, (m + 1) * 128)
                bank, bb = next_bank()
                pb = bank[:, 0:256].bitcast(BF16)
                for h in range(4):
                    S.op("pe", lambda e: e.transpose(pb[:, h * 128:(h + 1) * 128], QK[:, 4 + h, cs], ident_b), inc=(h == 3),
                         reads=[b_qk[4 + h], b_cst], writes=[bb])
                S.op("act", lambda e: e.activation(out=kTok[:, m, :], in_=pb, func=AF.Copy), reads=[bb], writes=[b_kt[m]])
                vw = Vw[m % 2]; bv = b_vw[m % 2]
                S.op("dve", lambda e: e.tensor_tensor(out=vw, in0=VA[:, m, :, :], in1=bc_last(w0[:, m, :], 129), op=ALU.mult),
                     reads=[b_va[m], b_g], writes=[bv])
                for h in range(4):
                    bank, bb = next_bank()
                    S.op("pe", lambda e: e.matmul(bank[:, 0:129], lhsT=kTok[:, m, h * 128:(h + 1) * 128], rhs=vw[:, h, :], start=True, stop=True),
                         reads=[b_kt[m], bv], writes=[bb])
                    if h % 2 == 0:
                        S.op("act", lambda e: e.activation(out=CL[:, m * 4 + h, :], in_=bank[:, 0:129], func=AF.Copy), reads=[bb], writes=[b_cl[m]])
                    else:
                        S.op("dve", lambda e: e.tensor_copy(out=CL[:, m * 4 + h, :], in_=bank[:, 0:129]), reads=[bb], writes=[b_cl[m]])
                if m == 0:
                    S.op("dve", lambda e: e.tensor_copy(out=Ct, in_=CL[:, 0:4, :]), reads=[b_cl[0]], writes=[b_ct])
                else:
                    S.op("dve", lambda e: e.tensor_tensor(out=Ct, in0=Ct, in1=bc_last(dl[:, m, :], 129), op=ALU.mult), reads=[b_ct, b_g], writes=[b_ct])
                    S.op("dve", lambda e: e.tensor_tensor(out=Ct, in0=Ct, in1=CL[:, m * 4:m * 4 + 4, :], op=ALU.add), reads=[b_ct, b_cl[m]], writes=[b_ct])
            ml_extra = []
            xfer("e_gates", view(MS, 0, 4608, F32), [b_g, b_sm])
            xfer("e_cl", view(MS, 4608 + 16512, 64 * 129 * 4, F32), b_cl)
            if IMP:
                S.mute = False
            mst = sm[:, 20:24]
            exq = S.dma_sem(f"exq{l}")
            if mode == "A":
                S.dma("sp", oq, dr["summ"][:, 32:548], Ct.rearrange("p h v -> p (h v)"), reads=[b_ct], writes=[b_out])
                S.dma("sp", oq, dr["summ"][:, 548:552], R0[:, 15, :], reads=[b_g], writes=[b_out])
                S.dma("sp", oq, dr["summ"][:, 552:556], incl[:, 15, :], reads=[b_g], writes=[b_out])
            else:
                sa = dr["summ_all"]
                small = Gb.rearrange("p (j c) -> p j c", j=4)[:, :, 0:8]
                S.dma("sp", exq, small, sa[:, :, 548:556].rearrange("j p c -> p j c"), writes=[b_sm])
                S.seal(exq, [b_sm])
                cm = sm[:, 20:24]; mx = sm[:, 24:28]; ta = sm[:, 28:32]; tb = sm[:, 32:36]; r0j = sm[:, 36:40]; nfj = sm[:, 40:44]; tq = sm[:, 44:48]
                S.op("dve", lambda e: e.memset(tmpC, 0.0), reads=[b_tc], writes=[b_tc])
                S.op("dve", lambda e: e.memset(cm, 0.0), reads=[b_sm], writes=[b_sm])
                cq = [S.dma_sem(f"cq{l}_{i}") for i in range(2)]
                Cj = [Ct, view(T0, 4128, 2064, F32).rearrange("p (h v) -> p h v", h=4)]
                b_cj = [b_ct, S.buf("cj1")]
                S.handoff([b_cj[1]], b_vw)
                for j in range(4):
                    cj = Cj[j % 2]; bcj = b_cj[j % 2]
                    S.dma("sp", cq[j % 2], cj.rearrange("p h v -> p (h v)"), sa[j, :, 32:548], writes=[bcj])
                    S.op("dve", lambda e: e.tensor_scalar(out=r0j, in0=small[:, j, 0:4], scalar1=pred[:, j:j + 1], scalar2=pmask[:, j:j + 1],
                                                          op0=ALU.mult, op1=ALU.add), reads=[b_sm, b_cst], writes=[b_sm])
                    S.op("dve", lambda e: e.tensor_scalar(out=nfj, in0=small[:, j, 4:8], scalar1=pred[:, j:j + 1], scalar2=None, op0=ALU.mult),
                         reads=[b_sm, b_cst], writes=[b_sm])
                    S.op("dve", lambda e: e.tensor_tensor(out=mx, in0=cm, in1=r0j, op=ALU.max), reads=[b_sm], writes=[b_sm])
                    S.op("dve", lambda e: e.tensor_tensor(out=tq, in0=cm, in1=mx, op=ALU.subtract), reads=[b_sm], writes=[b_sm])
                    S.op("act", lambda e: e.activation(out=ta, in_=tq, func=AF.Exp), reads=[b_sm], writes=[b_sm])
                    S.op("dve", lambda e: e.tensor_tensor(out=tq, in0=r0j, in1=mx, op=ALU.subtract), reads=[b_sm], writes=[b_sm])
                    S.op("act", lambda e: e.activation(out=tb, in_=tq, func=AF.Exp), reads=[b_sm], writes=[b_sm])
                    S.op("dve", lambda e: e.tensor_tensor(out=tmpC, in0=tmpC, in1=bc_last(ta, 129), op=ALU.mult), reads=[b_tc, b_sm], writes=[b_tc])
                    S.op("dve", lambda e: e.tensor_tensor(out=cj, in0=cj, in1=bc_last(tb, 129), op=ALU.mult), reads=[bcj, b_sm], writes=[bcj])
                    S.op("dve", lambda e: e.tensor_tensor(out=tmpC, in0=tmpC, in1=cj, op=ALU.add), reads=[b_tc, bcj], writes=[b_tc])
                    S.op("dve", lambda e: e.tensor_tensor(out=cm, in0=mx, in1=nfj, op=ALU.subtract), reads=[b_sm], writes=[b_sm])
            if mode != "A":
                S.op("dve", lambda e: e.tensor_tensor(out=Rr, in0=R0, in1=bc_m(mst), op=ALU.max), reads=[b_g, b_sm], writes=[b_g])
                S.op("dve", lambda e: e.tensor_tensor(out=t1, in0=g, in1=Rr, op=ALU.subtract), reads=[b_g], writes=[b_g])
                S.op("act", lambda e: e.activation(out=wv, in_=t1, func=AF.Exp, bias=LN_S), reads=[b_g], writes=[b_g])
                S.op("dve", lambda e: e.tensor_tensor(out=t1, in0=nF, in1=Rr, op=ALU.subtract), reads=[b_g], writes=[b_g])
                S.op("act", lambda e: e.activation(out=clamp, in_=t1, func=AF.Exp), reads=[b_g], writes=[b_g])
                S.op("dve", lambda e: e.tensor_tensor(out=t1, in0=R0, in1=Rr, op=ALU.subtract), reads=[b_g], writes=[b_g])
                S.op("act", lambda e: e.activation(out=ep, in_=t1, func=AF.Exp), reads=[b_g], writes=[b_g])
                S.op("dve", lambda e: e.tensor_tensor(out=t1[:, 1:16, :], in0=Rr[:, 0:15, :], in1=Rr[:, 1:16, :], op=ALU.subtract), reads=[b_g], writes=[b_g])
                S.op("dve", lambda e: e.tensor_tensor(out=t1[:, 0, :], in0=mst, in1=Rr[:, 0, :], op=ALU.subtract), reads=[b_g, b_sm], writes=[b_g])
                S.op("act", lambda e: e.activation(out=dl, in_=t1, func=AF.Exp), reads=[b_g], writes=[b_g])
                S.op("dve", lambda e: e.tensor_copy(out=Ct, in_=tmpC), reads=[b_tc], writes=[b_ct])
                b_cin = b_kt
                for m in range(16):
                    S.op("dve", lambda e: e.tensor_tensor(out=tmpC, in0=Ct, in1=bc_last(dl[:, m, :], 129), op=ALU.mult), reads=[b_ct, b_g], writes=[b_tc])
                    S.op("act", lambda e: e.activation(out=Cin[:, m * 4:m * 4 + 4, :], in_=tmpC, func=AF.Copy), reads=[b_tc], writes=b_cin)
                    S.op("dve", lambda e: e.tensor_tensor(out=Ct, in0=CL[:, m * 4:m * 4 + 4, :], in1=bc_last(ep[:, m, :], 129), op=ALU.mult),
                         reads=[b_cl[m], b_g], writes=[b_ct])
                    S.op("dve", lambda e: e.tensor_tensor(out=Ct, in0=Ct, in1=tmpC, op=ALU.add), reads=[b_ct, b_tc], writes=[b_ct])
                PT4 = [view(T0, i * 256, 256, BF16) for i in range(4)]
                hn4 = [view(T0, 1024 + i * 1024, 1024, BF16).rearrange("p (h t) -> p h t", h=4) for i in range(2)]
                hbA = view(T0, 6704, 2048, F32).rearrange("p (h t) -> p h t", h=4)
                hbB = view(T0, 3072, 2048, F32).rearrange("p (h t) -> p h t", h=4)
                hb4 = [hbA, hbB]
                b_pt4 = S.bufs(4, "pt4"); b_hn4 = S.bufs(2, "hn4"); b_hb4 = [S.bufs(4, "hbA"), S.bufs(4, "hbB")]
                b_j2 = S.buf("j2")
                b_ep = [S.buf("ep0"), S.buf("ep1")]
                S.handoff(b_pt4 + b_hn4 + b_hb4[0] + b_hb4[1] + [b_j2] + b_ep, [b_ct, b_tc, b_sm] + b_vw + b_pt + b_hb + b_hn + ([b_cj[1]] if mode != "A" else []))
                ml_extra += b_pt4 + b_hn4 + b_hb4[0] + b_hb4[1] + [b_j2] + b_ep
                smx = [sm[:, 4:20], sm[:, 48:64]]
                for m in range(16):
                    cs = slice(m * 128, (m + 1) * 128)
                    par = m % 2
                    dm_, rec_, ss_, rs_ = smx[par][:, 0:4], smx[par][:, 4:8], smx[par][:, 8:12], smx[par][:, 12:16]
                    be = b_ep[par]
                    bankS, bbS = next_bank()
                    for h in range(4):
                        S.op("pe", lambda e: e.matmul(bankS[:, h * 128:(h + 1) * 128], lhsT=QK[:, 4 + h, cs], rhs=QK[:, h, cs], start=True, stop=True),
                             inc=(h == 3), reads=[b_qk[4 + h], b_qk[h]], writes=[bbS])
                    for h in range(4):
                        S.op("dve", lambda e: e.scalar_tensor_tensor(out=PT4[h], in0=bankS[:, h * 128:(h + 1) * 128], scalar=wv[:, m, h:h + 1], in1=tri_f,
                                                                     op0=ALU.mult, op1=ALU.mult), reads=[bbS, b_g, b_cst], writes=[b_pt4[h]])
                    bN = []
                    for j in range(2):
                        bankN, bbN = next_bank()
                        bN.append((bankN, bbN))
                        for hh_ in range(2):
                            h = 2 * j + hh_
                            co = hh_ * 129
                            S.op("pe", lambda e: e.matmul(bankN[:, co:co + 129], lhsT=PT4[h], rhs=VA[:, m, h, :], start=True, stop=False), inc=False,
                                 reads=[b_pt4[h], b_va[m]], writes=[bbN])
                            S.op("pe", lambda e: e.matmul(bankN[:, co:co + 129], lhsT=QK[:, h, cs], rhs=Cin[:, m * 4 + h, :], start=False, stop=True),
                                 inc=(hh_ == 1), reads=[b_qk[h]] + b_cin, writes=[bbN])
                        den = bankN[:, 0:258].rearrange("p (h v) -> p h v", v=129)[:, :, 128]
                        S.op("act", lambda e: e.activation(out=dm_[:, 2 * j:2 * j + 2], in_=den, func=AF.Abs), reads=[bbN], writes=[be])
                        S.op("dve", lambda e: e.tensor_tensor(out=dm_[:, 2 * j:2 * j + 2], in0=dm_[:, 2 * j:2 * j + 2], in1=clamp[:, m, 2 * j:2 * j + 2], op=ALU.max),
                             reads=[be, b_g], writes=[be])
                    S.op("dve", lambda e: e.reciprocal(out=rec_, in_=dm_), reads=[be], writes=[be])
                    for h in range(4):
                        bankN, bbN = bN[h // 2]
                        co = (h % 2) * 129
                        S.op("dve", lambda e: e.scalar_tensor_tensor(out=hb4[par][:, h, :], in0=bankN[:, co:co + 128], scalar=rec_[:, h:h + 1],
                                                                     in1=OG[:, m, h * 128:(h + 1) * 128], op0=ALU.mult, op1=ALU.mult),
                             reads=[bbN, be, b_og[m]], writes=[b_hb4[par][h]])
                        S.op("act", lambda e: e.activation(out=junk2, in_=hb4[par][:, h, :], func=AF.Square, accum_out=ss_[:, h:h + 1]),
                             reads=[b_hb4[par][h]], writes=[b_j2, be])
                    S.op("dve", lambda e: e.tensor_scalar(out=rs_, in0=ss_, scalar1=1.0 / 128, scalar2=EPS, op0=ALU.mult, op1=ALU.add), reads=[be], writes=[be])
                    S.op("act", lambda e: e.activation(out=rs_, in_=rs_, func=AF.Sqrt), reads=[be], writes=[be])
                    S.op("dve", lambda e: e.reciprocal(out=rs_, in_=rs_), reads=[be], writes=[be])
                    S.op("dve", lambda e: e.tensor_tensor(out=hn4[par], in0=hb4[par], in1=bc_last(rs_, 128), op=ALU.mult),
                         reads=b_hb4[par] + [be], writes=[b_hn4[par]])
                    bankO, bbO = next_bank()
                    po = bankO[:, 0:256].bitcast(BF16).rearrange("p (h t) -> p h t", h=4)
                    for h in range(4):
                        S.op("pe", lambda e: e.transpose(po[:, h, :], hn4[par][:, h, :], ident_b), inc=(h == 3), reads=[b_hn4[par], b_cst], writes=[bbO])
                    S.op("dve", lambda e: e.tensor_tensor(out=YC[:, 4:8, cs], in0=po, in1=bc_last(mlg, 128), op=ALU.mult), reads=[bbO, b_prm], writes=[b_yc[m]])

            if cfg.get("debug"):
                S.dma("sp", dbq, dr["dbg_g"].rearrange("p (a c) -> p a c", a=16), view(MS, 0, 4096, F32).rearrange("p (a c) -> p a c", a=16), reads=[b_g, b_sm], writes=[b_dbg])

            if stop == "ml":
                quiesce(); return
            TCH = 16; NC_ = 128
            WZ = view(A0, 0, 32 * KB, BF16).rearrange("p (q i x n) -> p q i x n", q=4, i=16, x=2)
            BD = view(A0, 32 * KB, 16 * KB, BF16).rearrange("p (q j n) -> p q j n", q=4, j=16)
            CP = view(A2, 0, 34816, BF16).rearrange("p (j r x n) -> p j r x n", j=17, r=16, x=2)
            EC = view(A2, 34816, 8192, F32).rearrange("p (r c) -> p r c", r=16)
            ES = view(A2, 34816 + 8192, 8192, F32).rearrange("p (r c) -> p r c", r=16)
            ZL = view(A2, 51200, 16384, F32).rearrange("p (r x c) -> p r x c", r=16, x=2)
            ZS = view(A2, 67584, 8256, BF16).rearrange("p (r x c) -> p r x c", r=16, x=2)
            SP_ = A2 + 76032
            PW = view(SP_, 0, 2176, F32).rearrange("p (j x r) -> p j x r", j=17, x=2)
            def sc(i):
                return view(SP_, 2176 + i * 64, 64, F32)
            assert SP_ + 2176 + 30 * 64 <= A2 + A2_B
            WW = view(A0, 0, 16384, F32).rearrange("p (r x c) -> p r x c", r=16, x=2)
            ZG = view(A2, 51200, 16384, BF16).rearrange("p (q i c) -> p q i c", q=4, i=16)
            GT = A0 + 16 * KB
            GEN = A2 + 51200
            b_s5 = S.buf("s5gen")
            b_wz = S.bufs(4, "wz"); b_bd = S.bufs(4, "bd"); b_cp = S.buf("cp"); b_tab = S.buf("tab")
            b_zl = S.bufs(16, "zl"); b_ww = S.bufs(16, "ww"); b_zs = S.bufs(16, "zs"); b_zg = S.bufs(4, "zg")
            olds = [b_g, b_sm] + b_kt + b_cl + [b_ct, b_tc] + b_vw + b_pt + b_hb + b_hn + b_va + [b_gr] + b_qk + b_og + ml_extra
            handoff([b_s5, b_cp, b_tab] + b_wz + b_bd + b_zl + b_ww + b_zs + b_zg, olds)
            s5q = S.dma_sem(f"s5q{l}")
            (s_are, s_aim, s_dt, s_mag, s_th, s_t, s_sin, s_cos, s_abr, s_abi, s_den, s_zr, s_sre, s_sim, s_t2, s_magL,
             s_l128r, s_l128i, s_t3, s_m128) = [sc(i) for i in range(20)]
            zend = sc(20)[:, 0:16]
            zend = view(SP_, 2176 + 20 * 64, 128, F32).rearrange("p (r x) -> p r x", x=2)
            sst = view(SP_, 2176 + 22 * 64, 128, F32).rearrange("p (r x) -> p r x", x=2)

            def TT(out, in0, in1, op, rd=(), wr=None, eng="dve"):
                S.op(eng, lambda e: e.tensor_tensor(out=out, in0=in0, in1=in1, op=op), reads=[b_s5] + list(rd), writes=[b_s5] if wr is None else wr)

            def TS(out, in0, s1, s2, op0, op1=None, rd=(), wr=None):
                if op1 is None:
                    S.op("dve", lambda e: e.tensor_scalar(out=out, in0=in0, scalar1=s1, scalar2=None, op0=op0), reads=[b_s5] + list(rd), writes=[b_s5] if wr is None else wr)
                else:
                    S.op("dve", lambda e: e.tensor_scalar(out=out, in0=in0, scalar1=s1, scalar2=s2, op0=op0, op1=op1), reads=[b_s5] + list(rd), writes=[b_s5] if wr is None else wr)

            def AC(out, in_, func, rd=(), wr=None, **kw):
                S.op("act", lambda e: e.activation(out=out, in_=in_, func=func, **kw), reads=[b_s5] + list(rd), writes=[b_s5] if wr is None else wr)

            def cmul(o_r, o_i, a_r, a_i, b_r, b_i, t1_, t2_, rd=(), wr=None, neg_im=False):
                TT(t1_, a_r, b_r, ALU.mult, rd); TT(t2_, a_i, b_i, ALU.mult, rd)
                TT(o_r, t1_, t2_, ALU.subtract, rd, wr)
                TT(t1_, a_r, b_i, ALU.mult, rd); TT(t2_, a_i, b_r, ALU.mult, rd)
                if neg_im:
                    TT(t1_, t1_, t2_, ALU.add, rd)
                    TS(o_i, t1_, -1.0, None, ALU.mult, rd=rd, wr=wr)
                else:
                    TT(o_i, t1_, t2_, ALU.add, rd, wr)

            if S5IMP:
                S.mute = True
            if PREP:
                S.mute = False
            S.op("pool", lambda e: e.memset(view(SP_, 0, 4864, F32), 0.0), writes=[b_s5])
            araw = view(GEN, 0, 1024, F32)
            S.dma("sp", s5q, araw[0:16, 0:128], dr["s5_a_re"][l].rearrange("(r gl) n -> r (gl n)", gl=2), writes=[b_s5])
            S.dma("sp", s5q, araw[0:16, 128:256], dr["s5_a_im"][l].rearrange("(r gl) n -> r (gl n)", gl=2), writes=[b_s5])
            ldt = dr["s5_log_dt"][l]
            for gl in range(2):
                S.dma("sp", s5q, s_dt[gl * 64:(gl + 1) * 64, :], bass.AP(ldt.tensor, ldt.offset + gl, [[0, 64], [2, 16]]), writes=[b_s5])
            Bsm = [view(GEN, 1024 + x * 1024, 1024, F32).rearrange("p (r c) -> p r c", r=16) for x in range(2)]
            for x, nm in enumerate(["s5_b_re", "s5_b_im"]):
                bsrc = dr[nm][l]
                for gl in range(2):
                    S.dma("sp", s5q, Bsm[x][gl * 64:(gl + 1) * 64, :, :],
                          bass.AP(bsrc.tensor, bsrc.offset + gl * 1024, [[16, 64], [2048, 16], [1, 16]]), writes=[b_s5])
            Craw = [view(GEN, 3072 + x * 1024, 1024, F32).rearrange("p (q n) -> p q n", q=4) for x in range(2)]
            for x, nm in enumerate(["s5_c_re", "s5_c_im"]):
                csrc = dr[nm][l]
                S.dma("sp", s5q, Craw[x], bass.AP(csrc.tensor, csrc.offset, [[64, 128], [8192, 4], [1, 64]]), writes=[b_s5])
            S.seal(s5q, [b_s5])
            bank, bb = next_bank()
            S.op("pe", lambda e: e.transpose(bank[:, 0:16], araw[0:16, 0:128], ident_f[0:16, 0:16]), reads=[b_s5, b_cst], writes=[bb])
            S.op("pe", lambda e: e.transpose(bank[:, 16:32], araw[0:16, 128:256], ident_f[0:16, 0:16]), reads=[b_s5, b_cst], writes=[bb])
            S.op("dve", lambda e: e.tensor_copy(out=s_are, in_=bank[:, 0:16]), reads=[bb], writes=[b_s5])
            S.op("dve", lambda e: e.tensor_copy(out=s_aim, in_=bank[:, 16:32]), reads=[bb], writes=[b_s5])
            PI = float(np.pi)
            AC(s_dt, s_dt, AF.Exp)
            TT(s_t, s_are, s_dt, ALU.mult)
            AC(s_mag, s_t, AF.Exp)
            AC(s_magL, s_t, AF.Exp, scale=float(TCH))
            AC(s_m128, s_t, AF.Exp, scale=float(TCH * NC_))
            TT(s_th, s_aim, s_dt, ALU.mult)
            for thr in (1.0, 3.0, 5.0, 7.0):
                TS(s_t, s_th, thr * PI, -2.0 * PI, ALU.is_gt, ALU.mult)
                if thr == 1.0:
                    TT(s_t2, s_th, s_t, ALU.add)
                else:
                    TT(s_t2, s_t2, s_t, ALU.add)
            AC(s_sin, s_t2, AF.Sin)
            TS(s_t3, s_t2, 0.5 * PI, None, ALU.add)
            TS(s_t, s_t3, PI, -2.0 * PI, ALU.is_gt, ALU.mult)
            TT(s_t3, s_t3, s_t, ALU.add)
            AC(s_cos, s_t3, AF.Sin)
            TT(s_abr, s_mag, s_cos, ALU.mult); TT(s_abi, s_mag, s_sin, ALU.mult)
            TT(s_t, s_are, s_are, ALU.mult); TT(s_t2, s_aim, s_aim, ALU.mult); TT(s_den, s_t, s_t2, ALU.add)
            S.op("dve", lambda e: e.reciprocal(out=s_den, in_=s_den), reads=[b_s5], writes=[b_s5])
            TS(s_zr, s_abr, -1.0, None, ALU.add)
            TT(s_t, s_zr, s_are, ALU.mult); TT(s_t2, s_abi, s_aim, ALU.mult); TT(s_t, s_t, s_t2, ALU.add); TT(s_sre, s_t, s_den, ALU.mult)
            TT(s_t, s_abi, s_are, ALU.mult); TT(s_t2, s_zr, s_aim, ALU.mult); TT(s_t, s_t, s_t2, ALU.subtract); TT(s_sim, s_t, s_den, ALU.mult)
            S.op("dve", lambda e: e.memset(PW[:, 0, 0, :], 1.0), reads=[b_s5], writes=[b_s5])
            S.op("dve", lambda e: e.memset(PW[:, 0, 1, :], 0.0), reads=[b_s5], writes=[b_s5])
            S.op("dve", lambda e: e.tensor_copy(out=PW[:, 1, 0, :], in_=s_abr), reads=[b_s5], writes=[b_s5])
            S.op("dve", lambda e: e.tensor_copy(out=PW[:, 1, 1, :], in_=s_abi), reads=[b_s5], writes=[b_s5])
            pt1 = view(GEN, 5120, 1024, F32).rearrange("p (j r) -> p j r", r=16)
            pt2 = view(GEN, 6144, 1024, F32).rearrange("p (j r) -> p j r", r=16)
            kk_ = 1
            while kk_ < 16:
                def bj(a):
                    return bass.AP(a.tensor, a.offset, [list(a.ap[0]), [0, kk_], list(a.ap[1])])
                cmul(PW[:, kk_ + 1:2 * kk_ + 1, 0, :], PW[:, kk_ + 1:2 * kk_ + 1, 1, :], PW[:, 1:kk_ + 1, 0, :], PW[:, 1:kk_ + 1, 1, :],
                     bj(PW[:, kk_, 0, :]), bj(PW[:, kk_, 1, :]), pt1[:, 0:kk_, :], pt2[:, 0:kk_, :])
                kk_ *= 2
            S.op("dve", lambda e: e.reciprocal(out=s_t, in_=s_magL), reads=[b_s5], writes=[b_s5])
            TT(EC[:, :, 0], PW[:, 16, 0, :], s_t, ALU.mult, wr=[b_s5, b_tab]); TT(ES[:, :, 0], PW[:, 16, 1, :], s_t, ALU.mult, wr=[b_s5, b_tab])
            et1 = view(GEN, 7168, 4096, F32).rearrange("p (r c) -> p r c", r=16)
            et2 = view(GEN, 11264, 4096, F32).rearrange("p (r c) -> p r c", r=16)
            kk_ = 1
            while kk_ < NC_:
                cmul(EC[:, :, kk_:2 * kk_], ES[:, :, kk_:2 * kk_], EC[:, :, 0:kk_], ES[:, :, 0:kk_],
                     bc_last(EC[:, :, kk_ - 1], kk_), bc_last(ES[:, :, kk_ - 1], kk_), et1[:, :, 0:kk_], et2[:, :, 0:kk_], rd=[b_tab], wr=[b_s5, b_tab])
                kk_ *= 2
            TT(s_l128r, EC[:, :, NC_ - 1], s_m128, ALU.mult, rd=[b_tab]); TT(s_l128i, ES[:, :, NC_ - 1], s_m128, ALU.mult, rd=[b_tab])
            Cin_ = [view(GEN, 5120 + x * 2048, 2048, F32).rearrange("p (q n) -> p q n", q=4) for x in range(2)]
            Cp = [view(GEN, 9216 + x * 2048, 2048, F32).rearrange("p (r n) -> p r n", r=16) for x in range(2)]
            ct1 = view(GEN, 13312, 2048, F32).rearrange("p (r n) -> p r n", r=16)
            ct2 = view(A0, 0, 2048, F32).rearrange("p (r n) -> p r n", r=16)
            for x in range(2):
                TS(Cin_[x][:, :, 0:64], Craw[x], par01[:, 0:1], None, ALU.mult, rd=[b_cst])
                TS(Cin_[x][:, :, 64:128], Craw[x], par01[:, 1:2], None, ALU.mult, rd=[b_cst])
                bank, bb = next_bank()
                for q in range(4):
                    S.op("pe", lambda e: e.transpose(bank[:, q * 128:(q + 1) * 128], Cin_[x][:, q, :], ident_f), inc=(q == 3), reads=[b_s5, b_cst], writes=[bb])
                S.op("dve", lambda e: e.tensor_copy(out=Cp[x].rearrange("p r n -> p (r n)"), in_=bank[:, :]), reads=[bb], writes=[b_s5])
            for j in range(17):
                pr = bc_last(PW[:, j, 0, :], 32); pi_ = bc_last(PW[:, j, 1, :], 32)
                TT(ct1, Cp[0], pr, ALU.mult); TT(ct2, Cp[1], pi_, ALU.mult, rd=b_wz, wr=[b_s5] + b_wz)
                TT(CP[:, j, :, 0, :], ct1, ct2, ALU.subtract, wr=[b_s5, b_cp])
                TT(ct1, Cp[0], pi_, ALU.mult); TT(ct2, Cp[1], pr, ALU.mult, rd=b_wz, wr=[b_s5] + b_wz)
                S.op("dve", lambda e: e.scalar_tensor_tensor(out=CP[:, j, :, 1, :], in0=ct1, scalar=-1.0, in1=ct2, op0=ALU.mult, op1=ALU.subtract),
                     reads=[b_s5], writes=[b_s5, b_cp])

            BB = [view(GEN, 5120 + x * 2048, 2048, F32).rearrange("p (r n) -> p r n", r=16) for x in range(2)]
            BBb = [view(GEN, 9216 + x * 1024, 1024, BF16).rearrange("p (r n) -> p r n", r=16) for x in range(2)]
            bt1 = view(GEN, 11264, 1024, F32).rearrange("p (r c) -> p r c", r=16)
            bt2 = view(GEN, 12288, 1024, F32).rearrange("p (r c) -> p r c", r=16)
            for x in range(2):
                S.op("dve", lambda e: e.memset(BB[x], 0.0), reads=[b_s5], writes=[b_s5])
            sre_b = bc_last(s_sre, 16); sim_b = bc_last(s_sim, 16)
            TT(bt1, Bsm[0], sre_b, ALU.mult); TT(bt2, Bsm[1], sim_b, ALU.mult)
            for gl in range(2):
                ps_ = slice(gl * 64, (gl + 1) * 64)
                TT(BB[0][ps_, :, gl * 16:(gl + 1) * 16], bt1[ps_], bt2[ps_], ALU.subtract)
            TT(bt1, Bsm[1], sre_b, ALU.mult); TT(bt2, Bsm[0], sim_b, ALU.mult)
            for gl in range(2):
                ps_ = slice(gl * 64, (gl + 1) * 64)
                TT(BB[1][ps_, :, gl * 16:(gl + 1) * 16], bt1[ps_], bt2[ps_], ALU.add)
            for x in range(2):
                S.op("dve", lambda e: e.tensor_copy(out=BBb[x], in_=BB[x]), reads=[b_s5], writes=[b_s5])
            bdt = view(GEN, 13312, 512, F32)
            for j in range(16):
                bank, bb = next_bank()
                for q in range(4):
                    for x in range(2):
                        S.op("pe", lambda e: e.matmul(bank[:, q * 128:(q + 1) * 128], lhsT=BBb[x][:, 4 * q:4 * q + 4, :].rearrange("p r n -> p (r n)"),
                                                      rhs=CP[:, j, 4 * q:4 * q + 4, x, :], start=(x == 0), stop=(x == 1)), inc=(q == 3 and x == 1),
                             reads=[b_s5, b_cp], writes=[bb])
                if j == 0:
                    for q in range(4):
                        S.op("dve", lambda e: e.tensor_tensor(out=bdt, in0=bank[:, q * 128:(q + 1) * 128], in1=bdm, op=ALU.mult), reads=[bb, b_cst, b_s5], writes=[b_s5])
                        S.op("dve", lambda e: e.scalar_tensor_tensor(out=BD[:, q, 0, :], in0=ident_f, scalar=dcol[:, q:q + 1], in1=bdt, op0=ALU.mult, op1=ALU.add),
                             reads=[b_s5, b_cst, b_prm], writes=[b_bd[q]])
                else:
                    bdm_b = bass.AP(bdm.tensor, bdm.offset, [list(bdm.ap[0]), [0, 4], list(bdm.ap[1])])
                    S.op("dve", lambda e: e.tensor_tensor(out=BD[:, :, j, :], in0=bank[:, :].rearrange("p (q n) -> p q n", q=4), in1=bdm_b, op=ALU.mult),
                         reads=[bb, b_cst], writes=b_bd)
            mt1 = view(GEN, 13824, 2048, F32).rearrange("p (r n) -> p r n", r=16)
            mt2 = view(GEN, 1024, 2048, F32).rearrange("p (r n) -> p r n", r=16)
            MB = [view(GEN, 3072 + x * 1024, 1024, BF16).rearrange("p (r n) -> p r n", r=16) for x in range(2)]
            for i in range(16):
                j = 15 - i
                pr = bc_last(PW[:, j, 0, :], 32); pi_ = bc_last(PW[:, j, 1, :], 32)
                TT(mt1, BB[0], pr, ALU.mult); TT(mt2, BB[1], pi_, ALU.mult); TT(MB[0], mt1, mt2, ALU.subtract)
                TT(mt1, BB[0], pi_, ALU.mult); TT(mt2, BB[1], pr, ALU.mult); TT(MB[1], mt1, mt2, ALU.add)
                bank, bb = next_bank()
                pb = bank[:, :].bitcast(BF16).rearrange("p (q x n) -> p q x n", q=4, x=2)
                for q in range(4):
                    for x in range(2):
                        S.op("pe", lambda e: e.transpose(pb[:, q, x, :], MB[x][:, 4 * q:4 * q + 4, :].rearrange("p r n -> p (r n)"), ident_b),
                             inc=(q == 3 and x == 1), reads=[b_s5, b_cst], writes=[bb])
                S.op("act", lambda e: e.activation(out=WZ[:, :, i, :, :], in_=pb, func=AF.Copy), reads=[bb], writes=b_wz)
            if stop == "s5gen":
                quiesce(); return
            if PREP:
                xfer("e_wz", view(A0, 0, 32 * KB, F32), b_wz)
                xfer("e_cp", view(A2, 0, 34816, F32), [b_cp])
                xfer("e_bd", view(A0, 32 * KB, 16 * KB, F32), b_bd)
                xfer("e_tab", view(A2, 34816, 16384, F32), [b_tab])
                xfer("e_sp", view(SP_, 0, 4864, F32), [b_s5])
                S.wait_all("sp", [b_out])
                return
            if EXP:
                S.mute = False
                xfer("e_wz", view(A0, 0, 32 * KB, F32), b_wz)
                xfer("e_tab", view(A2, 34816, 16384, F32), [b_tab])
                xfer("e_sp", view(SP_, 0, 4864, F32), [b_s5])
            handoff(b_zl, b_zl + [b_s5])
            for q in range(4):
                for rr in range(4):
                    r = 4 * q + rr
                    bank, bb = next_bank()
                    for x in range(2):
                        col = x * 128
                        for i in range(16):
                            S.op("pe", lambda e: e.matmul(bank[:, col:col + 128], lhsT=WZ[32 * rr:32 * rr + 32, q, i, x, :], rhs=US[32 * rr:32 * rr + 32, q, i, :],
                                                          start=(i == 0), stop=(i == 15), tile_position=(32 * rr, 0)), inc=(i == 15 and x == 1),
                                 reads=[b_wz[q], b_us[q]], writes=[bb])
                    S.op("act", lambda e: e.activation(out=ZL[:, r, :, :].rearrange("p x c -> p (x c)"), in_=bank[:, 0:256], func=AF.Copy),
                         reads=[bb], writes=[b_zl[r]])
            if cfg.get("debug"):
                S.dma("sp", dbq, dr["dbg_zl"], view(A2, 51200, 16384, F32), reads=b_zl, writes=[b_dbg])
                S.dma("sp", dbq, dr["dbg_sc"], view(SP_, 0, 4864, F32), reads=[b_s5], writes=[b_dbg])
                S.dma("sp", dbq, dr["dbg_cp"], view(A2, 0, 34816, BF16), reads=[b_cp], writes=[b_dbg])
                S.dma("sp", dbq, dr["dbg_bd"], view(A0, 32 * KB, 16 * KB, BF16), reads=b_bd, writes=[b_dbg])
                S.dma("sp", dbq, dr["dbg_wz"], view(A0, 0, 32 * KB, BF16), reads=b_wz, writes=[b_dbg])
            handoff(b_ww, b_wz + b_ww)
            dt1 = view(GT, 0, 8192, F32).rearrange("p (r c) -> p r c", r=16)
            dt2 = view(GT, 8192, 8192, F32).rearrange("p (r c) -> p r c", r=16)
            b_dt = S.buf("dt"); handoff([b_dt], b_wz)
            magL_b = bc_last(s_magL, NC_)

            def scan_and_mod(init_ap, b_init, final):
                for r in range(16):
                    for x in range(2):
                        ini = 0.0 if init_ap is None else init_ap[:, r, x:x + 1]
                        S.op("dve", lambda e: e.tensor_tensor_scan(out=ZL[:, r, x, :], data0=magL_b[:, r, :], data1=WW[:, r, x, :], initial=ini,
                                                                   op0=ALU.mult, op1=ALU.add), reads=[b_ww[r], b_s5] + ([b_init] if b_init else []), writes=[b_zl[r]])
                if not final:
                    cmul(zend[:, :, 0], zend[:, :, 1], ZL[:, :, 0, NC_ - 1], ZL[:, :, 1, NC_ - 1], EC[:, :, NC_ - 1], ES[:, :, NC_ - 1], s_t, s_t2,
                         rd=b_zl + [b_tab])
                else:
                    S.op("dve", lambda e: e.tensor_tensor(out=dt1, in0=EC, in1=ZL[:, :, 0, :], op=ALU.mult), reads=[b_tab] + b_zl, writes=[b_dt])
                    S.op("dve", lambda e: e.tensor_tensor(out=dt2, in0=ES, in1=ZL[:, :, 1, :], op=ALU.mult), reads=[b_tab] + b_zl, writes=[b_dt])
                    S.op("dve", lambda e: e.tensor_tensor(out=ZS[:, :, 0, 1:NC_ + 1], in0=dt1, in1=dt2, op=ALU.subtract), reads=[b_dt], writes=b_zs)
                    S.op("dve", lambda e: e.tensor_tensor(out=dt1, in0=EC, in1=ZL[:, :, 1, :], op=ALU.mult), reads=[b_tab] + b_zl, writes=[b_dt])
                    S.op("dve", lambda e: e.tensor_tensor(out=dt2, in0=ES, in1=ZL[:, :, 0, :], op=ALU.mult), reads=[b_tab] + b_zl, writes=[b_dt])
                    S.op("dve", lambda e: e.tensor_tensor(out=ZS[:, :, 1, 1:NC_ + 1], in0=dt1, in1=dt2, op=ALU.add), reads=[b_dt], writes=b_zs)
                    S.op("dve", lambda e: e.tensor_copy(out=ZS[:, :, :, 0], in_=init_ap), reads=[b_init], writes=b_zs)

            S.op("dve", lambda e: e.tensor_tensor(out=dt1, in0=EC, in1=ZL[:, :, 0, :], op=ALU.mult), reads=[b_tab] + b_zl, writes=[b_dt])
            S.op("dve", lambda e: e.tensor_tensor(out=dt2, in0=ES, in1=ZL[:, :, 1, :], op=ALU.mult), reads=[b_tab] + b_zl, writes=[b_dt])
            S.op("dve", lambda e: e.tensor_tensor(out=WW[:, :, 0, :], in0=dt1, in1=dt2, op=ALU.add), reads=[b_dt], writes=b_ww)
            S.op("dve", lambda e: e.tensor_tensor(out=dt1, in0=EC, in1=ZL[:, :, 1, :], op=ALU.mult), reads=[b_tab] + b_zl, writes=[b_dt])
            S.op("dve", lambda e: e.tensor_tensor(out=dt2, in0=ES, in1=ZL[:, :, 0, :], op=ALU.mult), reads=[b_tab] + b_zl, writes=[b_dt])
            S.op("dve", lambda e: e.tensor_tensor(out=WW[:, :, 1, :], in0=dt1, in1=dt2, op=ALU.subtract), reads=[b_dt], writes=b_ww)
            scan_and_mod(None, None, False)
            xfer("e_ww", view(A0, 0, 16384, F32), b_ww)
            if IMP:
                xfer("e_cp", view(A2, 0, 34816, F32), [b_cp])
                xfer("e_bd", view(A0, 32 * KB, 16 * KB, F32), b_bd)
                xfer("e_tab", view(A2, 34816, 16384, F32), [b_tab])
            if IMP:
                xfer("e_sp", view(SP_, 0, 4864, F32), [b_s5])
                S.mute = False
            if cfg.get("debug"):
                S.dma("sp", dbq, dr["dbg_zend"], zend.rearrange("p r x -> p (r x)"), reads=[b_s5], writes=[b_dbg])
            if mode == "A":
                S.dma("sp", oq, dr["summ"][:, 0:32], zend.rearrange("p r x -> p (r x)"), reads=[b_s5], writes=[b_out])
                S.wait_all("sp", [b_out])
            if mode != "A":
                sa = dr["summ_all"]
                zall = view(GT, 0, 512, F32).rearrange("p (j r x) -> p j r x", j=4, x=2)
                zq = S.dma_sem(f"zq{l}")
                S.dma("sp", zq, zall.rearrange("p j r x -> p j (r x)"), sa[:, :, 0:32].rearrange("j p c -> p j c"), reads=[b_dt], writes=[b_dt])
                S.op("dve", lambda e: e.memset(sst, 0.0), reads=[b_s5], writes=[b_s5])
                ctr = sc(24)[:, 0:16]; cti = sc(25)[:, 0:16]
                for j in range(4):
                    cmul(ctr, cti, sst[:, :, 0], sst[:, :, 1], s_l128r, s_l128i, s_t, s_t2)
                    TT(ctr, ctr, zall[:, j, :, 0], ALU.add, rd=[b_dt]); TT(cti, cti, zall[:, j, :, 1], ALU.add, rd=[b_dt])
                    TT(ctr, ctr, sst[:, :, 0], ALU.subtract); TT(cti, cti, sst[:, :, 1], ALU.subtract)
                    S.op("dve", lambda e: e.scalar_tensor_tensor(out=sst[:, :, 0], in0=ctr, scalar=pred[:, j:j + 1], in1=sst[:, :, 0], op0=ALU.mult, op1=ALU.add),
                         reads=[b_s5, b_cst], writes=[b_s5])
                    S.op("dve", lambda e: e.scalar_tensor_tensor(out=sst[:, :, 1], in0=cti, scalar=pred[:, j:j + 1], in1=sst[:, :, 1], op0=ALU.mult, op1=ALU.add),
                         reads=[b_s5, b_cst], writes=[b_s5])
                scan_and_mod(sst, b_s5, True)
                if cfg.get("debug"):
                    S.dma("sp", dbq, dr["dbg_zs"], view(A2, 67584, 8256, BF16), reads=b_zs, writes=[b_dbg])
                handoff(b_zg, b_zl + b_zg)
                for q in range(4):
                    for ib in range(4):
                        bank, bb = next_bank()
                        for i4 in range(4):
                            ip = ib * 4 + i4
                            col = i4 * 128
                            for i in range(ip + 1):
                                S.op("pe", lambda e: e.matmul(bank[:, col:col + 128], lhsT=BD[:, q, ip - i, :], rhs=US[:, q, i, :], start=(i == 0), stop=False),
                                     inc=False, reads=[b_bd[q], b_us[q]], writes=[bb])
                            for rr in range(4):
                                r = 4 * q + rr
                                for x in range(2):
                                    lastw = (rr == 3 and x == 1)
                                    S.op("pe", lambda e: e.matmul(bank[32 * rr:32 * rr + 32, col:col + 128], lhsT=CP[:, ip + 1, r, x, :], rhs=ZS[:, r, x, 0:NC_],
                                                                  start=False, stop=(x == 1), tile_position=(0, 32 * rr)), inc=(lastw and i4 == 3),
                                         reads=[b_cp, b_zs[r]], writes=[bb])
                        S.op("act", lambda e: e.activation(out=ZG[:, q, ib * 4:ib * 4 + 4, :].rearrange("p i c -> p (i c)"), in_=bank[:, :], func=AF.Gelu_apprx_tanh),
                             reads=[bb], writes=[b_zg[q]])
                if cfg.get("debug"):
                    S.dma("sp", dbq, dr["dbg_zg"], view(A2, 51200, 16384, BF16), reads=b_zg, writes=[b_dbg])
                wg = dr["s5_w_glu"][l].rearrange("(k p) n -> p k n", p=128)
                wgl, bwg = wload(wg, 4, 512)
                gate = view(GT, 0, 2048, F32)
                ZZ = [view(GT, 2048 + ft * 2048, 2048, F32) for ft in range(4)]
                sqb = [view(GT, 10240 + i * 1024, 1024, BF16) for i in range(2)]
                rst = view(GT, 12288, 2048, F32)
                b_gate = S.buf("gate"); b_zz = S.bufs(4, "zz"); b_sqb = S.bufs(2, "sqb"); b_rst = S.buf("rst")
                handoff([b_gate, b_rst] + b_zz + b_sqb, [b_dt] + b_ww)
                YCv = YC[:, 0:4, :].rearrange("p f (c i) -> p f i c", i=16)
                for cb in range(4):
                    bankq, bbq = next_bank()
                    for ft in range(4):
                        bank, bb = next_bank()
                        for kk in range(4):
                            S.op("pe", lambda e: e.matmul(bank[:, :], lhsT=wgl[:, kk, ft * 128:(ft + 1) * 128], rhs=ZG[:, kk, cb * 4:cb * 4 + 4, :],
                                                          start=(kk == 0), stop=(kk == 3)), inc=(kk == 3), reads=[bwg] + b_zg, writes=[bb])
                        S.op("act", lambda e: e.activation(out=gate, in_=bank[:, :], func=AF.Sigmoid, bias=bglu[:, ft:ft + 1]), reads=[bb, b_prm], writes=[b_gate])
                        S.op("dve", lambda e: e.tensor_tensor(out=ZZ[ft], in0=ZG[:, ft, cb * 4:cb * 4 + 4, :].rearrange("p i c -> p (i c)"), in1=gate, op=ALU.mult),
                             reads=[b_zg[ft], b_gate], writes=[b_zz[ft]])
                        S.op("act", lambda e: e.activation(out=sqb[ft % 2], in_=ZZ[ft], func=AF.Square), reads=[b_zz[ft]], writes=[b_sqb[ft % 2]])
                        S.op("pe", lambda e: e.matmul(bankq[:, :], lhsT=ones_b, rhs=sqb[ft % 2], start=(ft == 0), stop=(ft == 3)), inc=True,
                             reads=[b_sqb[ft % 2], b_cst], writes=[bbq])
                    S.op("dve", lambda e: e.tensor_scalar(out=rst, in0=bankq[:, :], scalar1=1.0 / 512, scalar2=EPS, op0=ALU.mult, op1=ALU.add), reads=[bbq], writes=[b_rst])
                    S.op("act", lambda e: e.activation(out=rst, in_=rst, func=AF.Sqrt), reads=[b_rst], writes=[b_rst])
                    S.op("dve", lambda e: e.reciprocal(out=rst, in_=rst), reads=[b_rst], writes=[b_rst])
                    for ft in range(4):
                        S.op("dve", lambda e: e.scalar_tensor_tensor(out=YCv[:, ft, cb * 4:cb * 4 + 4, :], in0=ZZ[ft].rearrange("p (i c) -> p i c", i=4),
                                                                     scalar=outg[:, ft:ft + 1], in1=rst.rearrange("p (i c) -> p i c", i=4), op0=ALU.mult, op1=ALU.mult),
                             reads=[b_zz[ft], b_rst, b_prm], writes=b_yc)
                if cfg.get("debug"):
                    S.dma("sp", dbq, dr["dbg_yc"], view(A1, 0, 32 * KB, BF16), reads=b_yc, writes=[b_dbg])

            if mode != "A":
                if stop == "s5":
                    quiesce(); return
                S.handoff(X_b, b_qk + b_og + b_us + b_wz + b_bd + b_ww + [b_dt, b_gate, b_rst] + b_zz + b_sqb)
                for tt in range(NTT):
                    S.dma("sp", xq[tt], X[:, tt, :], xin_ap[tt * 128:(tt + 1) * 128, :], writes=[X_b[tt]])
                wo = dr["w_out"][l].rearrange("(k p) n -> p k n", p=128)
                ws = WStream([(wo[:, :, h * 512:(h + 1) * 512], 8, 512) for h in range(2)])
                for h in range(2):
                    wc, bw = ws.get(h)
                    for tt in range(NTT):
                        bank, bb = next_bank()
                        for kk in range(8):
                            S.op("pe", lambda e: e.matmul(bank[:, :], lhsT=YC[:, kk, tt * 128:(tt + 1) * 128], rhs=wc[:, kk, :],
                                                          start=(kk == 0), stop=(kk == 7)), inc=(kk == 7), reads=[bw, b_yc[tt]], writes=[bb])
                        S.op("dve", lambda e: e.tensor_tensor(out=X[:, tt, h * 512:(h + 1) * 512], in0=X[:, tt, h * 512:(h + 1) * 512], in1=bank[:, :], op=ALU.add),
                             reads=[bb, X_b[tt]], writes=[X_b[tt]])

                if cfg.get("dbg_x1"):
                    b_o1 = S.buf("o1")
                    for tt in range(NTT):
                        S.dma("sp", oq, xout_ap[tt * 128:(tt + 1) * 128, :], X[:, tt, :], reads=[X_b[tt]], writes=[b_o1])
                    S.wait_all("sp", [b_o1])
                    return
                if stop == "wout":
                    quiesce(); return
                S.handoff(HT_b, b_yc)
                a2_users = [b_g, b_sm] + b_kt + b_cl + [b_ct, b_tc] + b_vw + b_pt + b_hb + b_hn + [b_cp, b_tab, b_s5] + b_zl + b_zs + b_zg + b_va + [b_gr] + ml_extra
                handoff([b_junk, b_xh] + b_xnb, a2_users)
                norm_to_hT(g2, False, False)
                if cfg.get("dbg_ht2"):
                    b_o1 = S.buf("o1")
                    S.dma("sp", oq, dr["dbg_ht"], view(A1, 0, 32832, BF16)[:, 0:8 * 2051], reads=HT_b, writes=[b_o1])
                w1 = dr["w_ff1"][l].rearrange("(k p) n -> p k n", p=128)
                w2 = dr["w_ff2"][l].rearrange("(k p) n -> p k n", p=128)
                c1 = [(w1[:, :, hc * 512:(hc + 1) * 512], 8, 512) for hc in range(8)]
                c2 = [(w2[:, hc * 4:(hc + 1) * 4, :], 4, 1024) for hc in range(8)]
                chunks = [c1[0]]
                for hc in range(8):
                    if hc + 1 < 8:
                        chunks.append(c1[hc + 1])
                    chunks.append(c2[hc])
                ws = WStream(chunks)
                k.wci = 0

                def wnext():
                    r = ws.get(k.wci)
                    k.wci += 1
                    return r
                hid = [view(SCR, i * 16 * KB, 16 * KB, BF16).rearrange("p (f t) -> p f t", f=4) for i in range(2)]
                b_hid = [S.bufs(4, f"hid{i}") for i in range(2)]
                sq = [view(SCR, 32 * KB + i * 2048, 2048, F32) for i in range(2)]
                b_sq = S.bufs(2, "sq")
                k.sqi = 0
                handoff(b_hid[0] + b_hid[1] + b_sq, a2_users + [b_junk, b_xh] + b_xnb)

                def ffn1(hc):
                    wc, bw = wnext()
                    hb = hid[hc % 2]
                    for ft in range(4):
                        for nb in range(4):
                            bank, bb = next_bank()
                            for kk in range(8):
                                S.op("pe", lambda e: e.matmul(bank[:, :], lhsT=wc[:, kk, ft * 128:(ft + 1) * 128], rhs=HT[:, kk, 3 + nb * 512:3 + (nb + 1) * 512],
                                                              start=(kk == 0), stop=(kk == 7)), inc=(kk == 7), reads=[bw] + HT_b[nb * 4:nb * 4 + 4], writes=[bb])
                            si = k.sqi % 2; k.sqi += 1
                            S.op("act", lambda e: e.activation(out=sq[si], in_=bank[:, :], func=AF.Square), reads=[bb], writes=[b_sq[si]])
                            S.op("dve", lambda e: e.scalar_tensor_tensor(out=hb[:, ft, nb * 512:(nb + 1) * 512], in0=bank[:, :], scalar=0.0, in1=sq[si],
                                                                         op0=ALU.is_gt, op1=ALU.mult), reads=[bb, b_sq[si]], writes=[b_hid[hc % 2][nb]])

                def ffn2(hc):
                    wc, bw = wnext()
                    hb = hid[hc % 2]
                    for tt in range(NTT):
                        for h in range(2):
                            bank, bb = next_bank()
                            for kk in range(4):
                                S.op("pe", lambda e: e.matmul(bank[:, :], lhsT=hb[:, kk, tt * 128:(tt + 1) * 128], rhs=wc[:, kk, h * 512:(h + 1) * 512],
                                                              start=(kk == 0), stop=(kk == 3)), inc=(kk == 3), reads=[bw, b_hid[hc % 2][tt // 4]], writes=[bb])
                            S.op("dve", lambda e: e.tensor_tensor(out=X[:, tt, h * 512:(h + 1) * 512], in0=X[:, tt, h * 512:(h + 1) * 512], in1=bank[:, :], op=ALU.add),
                                 reads=[bb, X_b[tt]], writes=[X_b[tt]])

                ffn1(0)
                for hc in range(8):
                    if hc + 1 < 8:
                        ffn1(hc + 1)
                    ffn2(hc)

                if last:
                    gfin = view(SCR, 40 * KB, 4096, F32)
                    b_gf = S.buf("gfin")
                    fg = dr["final_norm_g"]
                    S.dma("sp", gq, gfin, bass.AP(fg.tensor, fg.offset, [[0, 128], [1, D]]), writes=[b_gf])
                    ot = [view(SCR, 44 * KB + i * 4096, 4096, F32) for i in range(2)]
                    b_ot = S.bufs(2, "ot")
                    handoff([b_junk], [b_junk] + b_hid[0] + b_hid[1])
                    stats_A(0)
                    for g4 in range(4):
                        if g4 + 1 < 4:
                            stats_A(g4 + 1)
                        stats_B(g4)
                        for tt in range(4 * g4, 4 * g4 + 4):
                            S.op("dve", lambda e: e.scalar_tensor_tensor(out=ot[tt % 2], in0=X[:, tt, :], scalar=rstd[:, tt:tt + 1], in1=gfin,
                                                                         op0=ALU.mult, op1=ALU.mult), reads=[X_b[tt], b_sg[g4], b_gf], writes=[b_ot[tt % 2]])
                            S.dma("sp", oq, xout_ap[tt * 128:(tt + 1) * 128, :], ot[tt % 2], reads=[b_ot[tt % 2]], writes=[b_out])
                else:
                    for tt in range(NTT):
                        S.dma("sp", oq, xout_ap[tt * 128:(tt + 1) * 128, :], X[:, tt, :], reads=[X_b[tt]], writes=[b_out])
                S.wait_all("sp", [b_out])

        layers = cfg["layers"]
        for li, l in enumerate(layers):
            layer(l, dr["xin"], dr.get("xhalo"), dr.get("xout"), last=cfg.get("final", False) and li == len(layers) - 1)
    return nc


_NC_CACHE = {}
N_CORES = 8
S5RAW = ["s5_a_re", "s5_a_im", "s5_log_dt", "s5_b_re", "s5_b_im", "s5_c_re", "s5_c_im", "s5_d"]
A_KEYS = ["norm_mix_g", "w_in", "ml_conv_w", "ml_conv_b", "ml_b_i", "ml_b_f", "ml_norm_g", "s5_b_glu", "s5_out_g"]
B_KEYS = ["w_out", "norm_ffn_g", "w_ff1", "w_ff2", "s5_w_glu", "ml_norm_g", "s5_b_glu", "s5_out_g"]
XF_AB = ["e_a0", "e_va", "e_gates", "e_cl", "e_ww"]
XF_PA = ["e_wz", "e_tab", "e_sp"]
XF_PB = ["e_cp", "e_bd", "e_tab", "e_sp"]
CONST3 = ["ident", "causal", "ones"]


def _get_nc(mode, final):
    key = (mode, final)
    if key not in _NC_CACHE:
        _NC_CACHE[key] = build(dict(layers=[0], nlayers=1, mode=mode, final=final, debug=False))
    return _NC_CACHE[key]


def _consts():
    par = np.zeros((128, 2), np.float32)
    par[:, 1] = (np.arange(128) // 16) % 2
    par[:, 0] = 1 - par[:, 1]
    return {"ident": np.eye(128, dtype=np.float32), "causal": np.triu(np.ones((128, 128), np.float32)),
            "ones": np.ones((128, 128), np.float32), "par01": par,
            "bdmask": np.kron(np.eye(8), np.ones((16, 16))).astype(np.float32)}


def kernel(**inputs):
    x = np.ascontiguousarray(inputs["x"], dtype=np.float32)
    nb, ls, d = x.shape
    per = ls // 4
    consts = _consts()
    cur = [np.ascontiguousarray(x[c // 4, (c % 4) * per:(c % 4 + 1) * per]) for c in range(N_CORES)]
    preds = []
    for c in range(N_CORES):
        p = np.zeros((128, 4), np.float32)
        for j in range(4):
            if j < c % 4:
                p[:, j] = 1.0
        preds.append(p)
    depth = inputs["w_in"].shape[0]
    f32 = lambda a: np.ascontiguousarray(np.asarray(a, dtype=np.float32))
    ncP = _get_nc("P", False)
    mapsP = []
    for c in range(N_CORES):
        lp = min(c // 4, depth - 1)
        m = {k: f32(inputs[k][lp:lp + 1]) for k in S5RAW}
        m.update({k: consts[k] for k in CONST3 + ["par01", "bdmask"]})
        m["pred"] = preds[c]
        mapsP.append(m)
    resP = run_bass_kernel_spmd(ncP, mapsP, core_ids=list(range(N_CORES)))
    s5w = [{k: np.asarray(resP.results[min(4 * l, N_CORES - 1)][k]) for k in set(XF_PA + XF_PB)} for l in range(depth)]
    for l in range(depth):
        halos = [np.zeros((3, d), np.float32) if c % 4 == 0 else np.ascontiguousarray(cur[c - 1][-3:]) for c in range(N_CORES)]
        lw = {k: f32(inputs[k][l:l + 1]) for k in set(A_KEYS + B_KEYS)}
        final = (l == depth - 1)
        ncA = _get_nc("A", False)
        mapsA = []
        for c in range(N_CORES):
            m = {"xin": cur[c], "xhalo": halos[c], "pred": preds[c]}
            m.update({k: consts[k] for k in CONST3})
            m.update({k: lw[k] for k in A_KEYS})
            m.update({k: s5w[l][k] for k in XF_PA})
            mapsA.append(m)
        resA = run_bass_kernel_spmd(ncA, mapsA, core_ids=list(range(N_CORES)))
        summ = [np.asarray(resA.results[c]["summ"]) for c in range(N_CORES)]
        summ_grp = [np.ascontiguousarray(np.stack(summ[4 * g:4 * g + 4])) for g in range(N_CORES // 4)]
        ncB = _get_nc("B", final)
        mapsB = []
        for c in range(N_CORES):
            m = {"xin": cur[c], "pred": preds[c], "summ_all": summ_grp[c // 4],
                 "final_norm_g": f32(inputs["final_norm_g"])}
            m.update({k: consts[k] for k in CONST3})
            m.update({k: lw[k] for k in B_KEYS})
            m.update({k: np.asarray(resA.results[c][k]) for k in XF_AB})
            m.update({k: s5w[l][k] for k in XF_PB})
            mapsB.append(m)
        resB = run_bass_kernel_spmd(ncB, mapsB, core_ids=list(range(N_CORES)))
        cur = [np.asarray(resB.results[c]["xout"]) for c in range(N_CORES)]
    out = np.empty_like(x)
    for c in range(N_CORES):
        out[c // 4, (c % 4) * per:(c % 4 + 1) * per] = cur[c]
    return out
```

```python
import numpy as np
import concourse.bass as bass
import concourse.mybir as mybir
from concourse.bass_utils import run_bass_kernel_spmd

F32 = mybir.dt.float32
BF16 = mybir.dt.bfloat16
AF = mybir.ActivationFunctionType
ALU = mybir.AluOpType
AX = mybir.AxisListType


class Buf:
    __slots__ = ("name", "w", "r")

    def __init__(self, name):
        self.name = name
        self.w = {}
        self.r = {}


class Sched:
    def __init__(self, nc, ctx):
        self.nc = nc
        self.ctx = ctx
        self.eng = {"pe": nc.tensor, "act": nc.scalar, "dve": nc.vector, "pool": nc.gpsimd, "sp": nc.sync}
        self.sem = {}
        self.cnt = {}
        for k in self.eng:
            self.sem[k] = ctx.enter_context(nc.semaphore("s_" + k))
            self.cnt[k] = 0
        self.waited = {k: {} for k in self.eng}
        self.ndma = 0
        self.nbuf = 0
        self.mute = False

    def buf(self, name=None):
        self.nbuf += 1
        return Buf(name or f"b{self.nbuf}")

    def bufs(self, n, name="b"):
        return [self.buf(f"{name}{i}") for i in range(n)]

    def dma_sem(self, name=None):
        self.ndma += 1
        key = name or f"dma{self.ndma}"
        self.sem[key] = self.ctx.enter_context(self.nc.semaphore("s_" + key))
        self.cnt[key] = 0
        return key

    def _deps(self, e, reads, writes):
        deps = {}
        for b in reads:
            for k, c in b.w.items():
                if deps.get(k, 0) < c:
                    deps[k] = c
        for b in writes:
            for k, c in b.w.items():
                if deps.get(k, 0) < c:
                    deps[k] = c
            for k, c in b.r.items():
                if deps.get(k, 0) < c:
                    deps[k] = c
        eng = self.eng[e]
        for k, c in deps.items():
            if k == e and e == "pe":
                continue
            if self.waited[e].get(k, 0) < c:
                eng.wait_ge(self.sem[k], c)
                self.waited[e][k] = c

    def _record(self, key, c, reads, writes):
        for b in writes:
            b.w = {key: c}
            b.r = {}
        for b in reads:
            if b.r.get(key, 0) < c:
                b.r[key] = c

    def op(self, e, fn, reads=(), writes=(), inc=True):
        if self.mute:
            return None
        self._deps(e, reads, writes)
        ins = fn(self.eng[e])
        if inc:
            self.cnt[e] += 1
            ins.then_inc(self.sem[e], 1)
            self._record(e, self.cnt[e], reads, writes)
        else:
            self._record(e, self.cnt[e] + 1, reads, writes)
        return ins

    def seal(self, key, bufs):
        if self.mute:
            return
        c = self.cnt[key]
        for b in bufs:
            if key in b.w:
                b.w[key] = c

    def handoff(self, news, olds):
        w = {}
        r = {}
        for ob in olds:
            for k2, c2 in ob.w.items():
                if w.get(k2, 0) < c2:
                    w[k2] = c2
            for k2, c2 in ob.r.items():
                if r.get(k2, 0) < c2:
                    r[k2] = c2
        for nb in news:
            nb.w = dict(w)
            nb.r = dict(r)

    def dma(self, q, dsem, out, in_, reads=(), writes=(), **kw):
        if self.mute:
            return None
        self._deps(q, reads, writes)
        ins = self.eng[q].dma_start(out=out, in_=in_, **kw)
        self.cnt[dsem] += 16
        ins.then_inc(self.sem[dsem], 16)
        self._record(dsem, self.cnt[dsem], reads, writes)
        return ins

    def wait_all(self, e, bufs):
        if self.mute:
            return
        self._deps(e, bufs, ())


import numpy as np
from contextlib import ExitStack

NT = 2048
NTT = 16
D = 1024
DIN = 2568
DFF = 4096
EPS = 1e-6
KB = 1024


class KB_:
    pass


def build(cfg):
    nc = bass.Bass("TRN2", target_bir_lowering=False)
    k = KB_()
    k.nc = nc
    k.cfg = cfg
    L = cfg.get("nlayers", 1)
    dr = {}

    def din(name, shape, dt=F32):
        dr[name] = nc.dram_tensor(name, list(shape), dt, kind="ExternalInput").ap()
        return dr[name]

    def dout(name, shape, dt=F32):
        dr[name] = nc.dram_tensor(name, list(shape), dt, kind="ExternalOutput").ap()
        return dr[name]

    mode = cfg.get("mode", "B")
    IMP = (mode == "B")
    EXP = (mode == "A")
    PREP = (mode == "P")
    S5IMP = mode in ("A", "B")
    XF = {"e_a0": (16384, "A", "B"), "e_va": (4256, "A", "B"), "e_gates": (1152, "A", "B"), "e_cl": (8256, "A", "B"),
          "e_ww": (4096, "A", "B"), "e_wz": (8192, "P", "A"), "e_cp": (8704, "P", "B"), "e_bd": (4096, "P", "B"),
          "e_tab": (4096, "P", "AB"), "e_sp": (1216, "P", "AB")}
    S5RAW = ["s5_a_re", "s5_a_im", "s5_log_dt", "s5_b_re", "s5_b_im", "s5_c_re", "s5_c_im", "s5_d", "par01", "bdmask"]
    PASS1 = ["xhalo", "norm_mix_g", "w_in", "ml_conv_w", "ml_conv_b", "ml_b_i", "ml_b_f"]
    PASS2 = ["w_out", "norm_ffn_g", "w_ff1", "w_ff2", "final_norm_g", "s5_w_glu", "summ_all"]
    COMMON = ["ident", "causal", "ones", "pred"]
    SMALLP = ["ml_norm_g", "s5_b_glu", "s5_out_g"]
    REAL = {"B0": ["xin"] + PASS1 + S5RAW + PASS2 + COMMON + SMALLP,
            "A": ["xin"] + PASS1 + COMMON + SMALLP,
            "B": ["xin"] + PASS2 + COMMON + SMALLP,
            "P": S5RAW + COMMON}[mode]
    _din_real = din

    def din(name, shape, dt=F32):
        if name in REAL:
            return _din_real(name, shape, dt)
        dr[name] = nc.dram_tensor(name, list(shape), dt).ap()
        return dr[name]
    din("xin", [NT, D])
    din("xhalo", [3, D])
    din("norm_mix_g", [L, D]); din("w_in", [L, D, DIN])
    din("ml_conv_w", [L, 4, 1024]); din("ml_conv_b", [L, 1024])
    din("ml_b_i", [L, 4]); din("ml_b_f", [L, 4])
    din("s5_a_re", [L, 32, 64]); din("s5_a_im", [L, 32, 64]); din("s5_log_dt", [L, 32])
    din("s5_b_re", [L, 32, 64, 16]); din("s5_b_im", [L, 32, 64, 16]); din("s5_c_re", [L, 32, 16, 64]); din("s5_c_im", [L, 32, 16, 64])
    din("s5_d", [L, 32, 16])
    din("par01", [128, 2]); din("bdmask", [128, 128])
    din("w_out", [L, D, D])
    din("norm_ffn_g", [L, D]); din("w_ff1", [L, D, DFF]); din("w_ff2", [L, DFF, D])
    din("final_norm_g", [D])
    din("s5_w_glu", [L, 512, 512])
    din("summ_all", [4, 128, 556])
    din("ident", [128, 128]); din("causal", [128, 128]); din("ones", [128, 128])
    din("ml_norm_g", [L, 512]); din("s5_b_glu", [L, 512]); din("s5_out_g", [L, 512])
    din("pred", [128, 4])
    if EXP:
        dout("summ", [128, 556])
    for nm_, (w_, prod_, cons_) in XF.items():
        if mode == prod_:
            dout(nm_, [128, w_])
        elif mode in cons_:
            _din_real(nm_, [128, w_])
    if mode in ("B", "B0"):
        dout("xout", [NT, D])
    if cfg.get("dbg_ht2"):
        dout("dbg_ht", [128, 8 * 2051], BF16)
    if cfg.get("debug"):
        dout("dbg_u", [128, 4 * 2048], BF16)
        dout("dbg_qk", [128, 8 * 2048], BF16)
        dout("dbg_v", [128, 16 * 4 * 129], BF16)
        dout("dbg_o", [128, 16 * 512], BF16)
        dout("dbg_if", [128, 128])
        dout("dbg_yc", [128, 8 * 2048], BF16)
        dout("dbg_g", [128, 16 * 64])
        dout("dbg_zl", [128, 4096]); dout("dbg_zs", [128, 16 * 2 * 129], BF16); dout("dbg_zg", [128, 8192], BF16)
        dout("dbg_sc", [128, 1216]); dout("dbg_cp", [128, 17408], BF16); dout("dbg_bd", [128, 8192], BF16); dout("dbg_wz", [128, 16384], BF16)
        dout("dbg_zend", [128, 32])
    k.dr = dr

    with ExitStack() as ctx:
        S = Sched(nc, ctx)
        k.S = S
        ctx.enter_context(nc.allow_non_contiguous_dma(reason="small param loads"))
        ctx.enter_context(nc.allow_low_precision(reason="bf16 matmul operands"))
        A0_B, A1_B, A2_B = 64 * KB, 33 * KB + 256, 79 * KB
        arena = ctx.enter_context(nc.sbuf_tensor("arena", [128, (A0_B + A1_B + A2_B) // 4], F32))
        ring = ctx.enter_context(nc.sbuf_tensor("ring", [128, 3 * 4096], BF16))
        cst = ctx.enter_context(nc.sbuf_tensor("cst", [128, 1024], F32))
        banks = [ctx.enter_context(nc.psum_tensor(f"ps{i}", [128, 512], F32)) for i in range(8)]
        bank_bufs = S.bufs(8, "bank")
        k.bank_i = 0
        block = ctx.enter_context(nc.Block())

        def view(base, off, nbytes, dt):
            assert off % 4 == 0 and nbytes % 4 == 0
            a = arena[:, (base + off) // 4:(base + off + nbytes) // 4]
            return a if dt == F32 else a.bitcast(dt)
        A0, A1, A2 = 0, A0_B, A0_B + A1_B

        def next_bank():
            i = k.bank_i
            k.bank_i = (i + 1) % 8
            return banks[i], bank_bufs[i]

        ident_f = cst[:, 0:128]
        ident_b = cst[:, 128:192].bitcast(BF16)
        b_cst = S.buf("cst")
        dq = S.dma_sem("dq_misc")
        S.dma("sp", dq, ident_f, dr["ident"], writes=[b_cst])
        tri_f = cst[:, 384:512]
        ones_f = cst[:, 512:640]
        tri_b = cst[:, 192:256].bitcast(BF16)
        S.dma("sp", dq, tri_f, dr["causal"], writes=[b_cst])
        S.dma("sp", dq, ones_f, dr["ones"], writes=[b_cst])
        par01 = cst[:, 752:754]
        bdm = cst[:, 768:896]
        ones_b = cst[:, 896:960].bitcast(BF16)
        if "par01" in REAL:
            S.dma("sp", dq, par01, dr["par01"], writes=[b_cst])
        pred = cst[:, 972:976]
        pmask = cst[:, 980:984]
        S.dma("sp", dq, pred, dr["pred"], writes=[b_cst])
        if "bdmask" in REAL:
            S.dma("sp", dq, bdm, dr["bdmask"], writes=[b_cst])
        S.seal(dq, [b_cst])
        S.op("dve", lambda e: e.tensor_copy(out=ones_b, in_=ones_f), reads=[b_cst], writes=[b_cst])
        S.op("dve", lambda e: e.tensor_scalar(out=pmask, in0=pred, scalar1=1e6, scalar2=-1e6, op0=ALU.mult, op1=ALU.add), reads=[b_cst], writes=[b_cst])
        S.op("dve", lambda e: e.tensor_copy(out=ident_b, in_=ident_f), reads=[b_cst], writes=[b_cst])
        S.op("dve", lambda e: e.tensor_copy(out=tri_b, in_=tri_f), reads=[b_cst], writes=[b_cst])

        X = view(A0, 0, 64 * KB, F32).rearrange("p (t d) -> p t d", t=NTT)
        X_b = S.bufs(NTT, "X")
        HT = view(A1, 0, 8 * 2051 * 2 + 0, BF16) if False else view(A1, 0, 32832, BF16)[:, 0:8 * 2051].rearrange("p (k t) -> p k t", k=8)
        HT_b = S.bufs(NTT + 1, "HT")
        YC = view(A1, 0, 32 * KB, BF16).rearrange("p (k t) -> p k t", k=8)
        QK = view(A0, 0, 32 * KB, BF16).rearrange("p (f t) -> p f t", f=8)
        OG = view(A0, 32 * KB, 16 * KB, BF16).rearrange("p (t d) -> p t d", t=NTT)
        US = view(A0, 48 * KB, 16 * KB, BF16).rearrange("p (q i c) -> p q i c", q=4, i=16)
        VA = view(A2, 0, 16512, BF16).rearrange("p (t h v) -> p t h v", t=NTT, h=4)
        GR = view(A2, 16512, 512, F32)
        SCR = A2 + 17024

        xq = [S.dma_sem(f"xq{i}") for i in range(16)]
        hq = S.dma_sem("hq"); gq = S.dma_sem("gq")
        oq = S.dma_sem("oq")
        wq = [S.dma_sem(f"wq{i}") for i in range(3)]
        ring_b = S.bufs(3, "ring")
        k.wi = 0

        def wload(src_ap, nk, ncols):
            i = k.wi % 3
            k.wi += 1
            v = ring[:, i * 4096: i * 4096 + nk * ncols].rearrange("p (k n) -> p k n", k=nk)
            S.dma("pool", wq[i], v, src_ap, writes=[ring_b[i]])
            return v, ring_b[i]

        class WStream:
            def __init__(self, chunks):
                self.chunks = chunks
                self.loaded = []

            def get(self, i, ahead=2):
                while len(self.loaded) < min(len(self.chunks), i + 1 + ahead):
                    self.loaded.append(wload(*self.chunks[len(self.loaded)]))
                return self.loaded[i]

        k.pq = None

        def load_pvec(dst, src_1d, b, q="sp"):
            S.dma(q, k.pq, dst, src_1d.rearrange("(k p) -> p k", p=128), writes=[b])

        tmp_b = S.bufs(4, "tmp")

        def quiesce():
            for e_ in ("pe", "act", "dve", "pool"):
                if S.cnt[e_] > 0:
                    nc.sync.wait_ge(S.sem[e_], S.cnt[e_])
            for key_, c_ in S.cnt.items():
                if key_ not in S.eng and c_ > 0:
                    nc.sync.wait_ge(S.sem[key_], c_)

        def layer(l, xin_ap, xh_ap, xout_ap, last):
            stop = cfg.get("stop")
            prm = cst[:, 256:256 + 64]
            b_prm = S.buf("prm")
            pq_l = S.dma_sem(f"pq{l}"); k.pq = pq_l
            g1 = cst[:, 640:648]; g2 = cst[:, 648:656]
            cw = cst[:, 656:688].rearrange("p (j f) -> p j f", j=4)
            cb = cst[:, 688:696]
            b_out = S.buf("out")
            if "norm_mix_g" in REAL:
                load_pvec(g1, dr["norm_mix_g"][l], b_prm)
                for j in range(4):
                    load_pvec(cw[:, j, :], dr["ml_conv_w"][l, j], b_prm)
                load_pvec(cb, dr["ml_conv_b"][l], b_prm)
            if "norm_ffn_g" in REAL:
                load_pvec(g2, dr["norm_ffn_g"][l], b_prm)
            mlg = cst[:, 740:744]
            bif = cst[:, 744:752]
            dcol = cst[:, 960:964]; bglu = cst[:, 964:968]; outg = cst[:, 968:972]
            if "ml_b_i" in REAL:
                bi_ = dr["ml_b_i"][l]; bf_ = dr["ml_b_f"][l]
                S.dma("sp", pq_l, bif[:, 0:4], bass.AP(bi_.tensor, bi_.offset, [[0, 128], [1, 4]]), writes=[b_prm])
                S.dma("sp", pq_l, bif[:, 4:8], bass.AP(bf_.tensor, bf_.offset, [[0, 128], [1, 4]]), writes=[b_prm])
            if "s5_d" in REAL:
                load_pvec(dcol, dr["s5_d"][l].rearrange("g p -> (g p)"), b_prm)
            if "ml_norm_g" in REAL:
                load_pvec(mlg, dr["ml_norm_g"][l], b_prm)
                load_pvec(bglu, dr["s5_b_glu"][l], b_prm)
                load_pvec(outg, dr["s5_out_g"][l], b_prm)
            if S.cnt[pq_l] == 0:
                S.op("dve", lambda e: e.memset(cst[:, 740:744], 0.0), writes=[b_prm])
            S.seal(pq_l, [b_prm])

            def xfer(name, ap, bufs):
                w_, prod_, cons_ = XF[name]
                was = S.mute; S.mute = False
                if mode == prod_:
                    S.dma("sp", oq, dr[name], ap, reads=bufs, writes=[b_out])
                elif mode in cons_:
                    q_ = S.dma_sem(f"{name}_{l}")
                    S.dma("sp", q_, ap, dr[name], writes=bufs)
                S.mute = was
            if IMP or PREP:
                S.mute = True
            ssq = cst[:, 700:717]
            rstd = cst[:, 720:737]
            b_st = S.bufs(17, "st")
            junk = view(SCR, 0, 2048, BF16)
            b_junk = S.buf("junk")
            xnb = [view(SCR, 2048 + i * 2048, 2048, BF16) for i in range(2)]
            b_xnb = S.bufs(2, "xnb")
            xh_t = view(SCR, 6144, 4096, F32)
            b_xh = S.buf("xh")

            def rms_stats(src, np_, col, bsrc):
                S.op("act", lambda e: e.activation(out=junk[:np_], in_=src, func=AF.Square, accum_out=ssq[:np_, col:col + 1]),
                     reads=[bsrc], writes=[b_junk, b_st[col]])
                S.op("dve", lambda e: e.tensor_scalar(out=rstd[:np_, col:col + 1], in0=ssq[:np_, col:col + 1], scalar1=1.0 / D, scalar2=EPS,
                                                      op0=ALU.mult, op1=ALU.add), reads=[b_st[col]], writes=[b_st[col]])
                S.op("act", lambda e: e.activation(out=rstd[:np_, col:col + 1], in_=rstd[:np_, col:col + 1], func=AF.Sqrt),
                     reads=[b_st[col]], writes=[b_st[col]])
                S.op("dve", lambda e: e.reciprocal(out=rstd[:np_, col:col + 1], in_=rstd[:np_, col:col + 1]), reads=[b_st[col]], writes=[b_st[col]])

            b_sg = S.bufs(5, "stg")

            def stats_A(g4):
                tiles = [NTT] if g4 == 4 else range(4 * g4, 4 * g4 + 4)
                for tt in tiles:
                    halo = tt == NTT
                    np_ = 3 if halo else 128
                    src, bsrc = (xh_t[:3, :], b_xh) if halo else (X[:, tt, :], X_b[tt])
                    S.op("act", lambda e: e.activation(out=junk[:np_], in_=src, func=AF.Square, accum_out=ssq[:np_, tt:tt + 1]),
                         reads=[bsrc], writes=[b_junk, b_sg[g4]])

            def stats_B(g4):
                c0, c1 = (NTT, NTT + 1) if g4 == 4 else (4 * g4, 4 * g4 + 4)
                np_ = 3 if g4 == 4 else 128
                S.op("dve", lambda e: e.tensor_scalar(out=rstd[:np_, c0:c1], in0=ssq[:np_, c0:c1], scalar1=1.0 / D, scalar2=EPS,
                                                      op0=ALU.mult, op1=ALU.add), reads=[b_sg[g4]], writes=[b_sg[g4]])
                S.op("act", lambda e: e.activation(out=rstd[:np_, c0:c1], in_=rstd[:np_, c0:c1], func=AF.Sqrt), reads=[b_sg[g4]], writes=[b_sg[g4]])
                S.op("dve", lambda e: e.reciprocal(out=rstd[:np_, c0:c1], in_=rstd[:np_, c0:c1]), reads=[b_sg[g4]], writes=[b_sg[g4]])

            def norm_to_hT(gvec, with_halo, from_dram):
                gb = bass.AP(gvec.tensor, gvec.offset, [list(gvec.ap[0]), list(gvec.ap[1]), [0, 128]])
                for tt in range(NTT):
                    if from_dram:
                        S.dma("sp", xq[tt], X[:, tt, :], xin_ap[tt * 128:(tt + 1) * 128, :], writes=[X_b[tt]])
                if with_halo:
                    S.dma("sp", hq, xh_t[:3, :], xh_ap, writes=[b_xh])

                def stage_C(g4):
                    tiles = [NTT] if g4 == 4 else range(4 * g4, 4 * g4 + 4)
                    for tt in tiles:
                        halo = tt == NTT
                        np_ = 3 if halo else 128
                        src, bsrc = (xh_t[:3, :], b_xh) if halo else (X[:, tt, :], X_b[tt])
                        xb = xnb[tt % 2]; bx = b_xnb[tt % 2]
                        S.op("act", lambda e: e.activation(out=xb[:np_], in_=src, func=AF.Copy, scale=rstd[:np_, tt:tt + 1]),
                             reads=[bsrc, b_sg[g4]], writes=[bx])
                        bank, bb = next_bank()
                        pb = bank[:, 0:512].bitcast(BF16).rearrange("p (k t) -> p k t", k=8)
                        for kk in range(8):
                            S.op("pe", lambda e: e.transpose(pb[:, kk, 0:np_], xb[:np_, kk * 128:(kk + 1) * 128], ident_b[:np_, :np_]),
                                 inc=(kk == 7), reads=[bx, b_cst], writes=[bb])
                        c0 = 0 if halo else 3 + tt * 128
                        S.op("dve", lambda e: e.tensor_tensor(out=HT[:, :, c0:c0 + np_], in0=pb[:, :, 0:np_], in1=gb[:, :, 0:np_], op=ALU.mult),
                             reads=[bb, b_prm], writes=[HT_b[tt]])
                ng = 5 if with_halo else 4
                stats_A(0)
                for g4 in range(ng):
                    if g4 + 1 < ng:
                        stats_A(g4 + 1)
                    stats_B(g4)
                    stage_C(g4)

            norm_to_hT(g1, True, True)

            if stop == "norm1":
                quiesce(); return
            win = dr["w_in"][l].rearrange("(k p) n -> p k n", p=128)
            chunks = [(win[:, :, c * 512:(c + 1) * 512], 8, 512) for c in range(5)] + [(win[:, :, 2560:2568], 8, 8)]
            ws = WStream(chunks)
            stage = [view(SCR, 10240 + i * 8448, 8448, F32) for i in range(2)]
            b_stage = S.bufs(2, "stage")
            acc = view(SCR, 10240 + 2 * 8448, 8192, F32)
            b_acc = S.buf("acc")
            b_us = S.bufs(4, "us")
            b_qk = S.bufs(8, "qk")
            b_va = S.bufs(NTT, "va")
            b_og = S.bufs(NTT, "og")
            b_gr = S.buf("gr")
            allHT = HT_b
            S.handoff(b_us + b_qk + b_og, X_b)
            S.op("pool", lambda e: e.memset(VA[:, :, :, 128:129], 1.0), writes=b_va)
            for ci in range(3):
                wc, bw = ws.get(ci)
                for ft in range(4):
                    f = (ci - 1) * 4 + ft
                    if ci > 0:
                        st = stage[f % 2]; bs = b_stage[f % 2]
                        bank, bb = next_bank()
                        for kk in range(8):
                            S.op("pe", lambda e: e.matmul(bank[:, 0:3], lhsT=wc[:, kk, ft * 128:(ft + 1) * 128], rhs=HT[:, kk, 0:3],
                                                          start=(kk == 0), stop=(kk == 7)), inc=(kk == 7), reads=[bw, HT_b[NTT]], writes=[bb])
                        S.op("act", lambda e: e.activation(out=st[:, 0:3], in_=bank[:, 0:3], func=AF.Copy), reads=[bb], writes=[bs])
                    for nb in range(4):
                        bank, bb = next_bank()
                        for kk in range(8):
                            S.op("pe", lambda e: e.matmul(bank[:, :], lhsT=wc[:, kk, ft * 128:(ft + 1) * 128],
                                                          rhs=HT[:, kk, 3 + nb * 512:3 + (nb + 1) * 512], start=(kk == 0), stop=(kk == 7)),
                                 inc=(kk == 7), reads=[bw] + allHT[nb * 4:nb * 4 + 4], writes=[bb])
                        if ci == 0:
                            dst = US[:, ft, :, nb * 32:(nb + 1) * 32]
                            src = bank[:, :].rearrange("p (c i) -> p i c", i=16)
                            S.op("act", lambda e: e.activation(out=dst, in_=src, func=AF.Copy), reads=[bb], writes=[b_us[ft]])
                        else:
                            S.op("act", lambda e: e.activation(out=st[:, 3 + nb * 512:3 + (nb + 1) * 512], in_=bank[:, :], func=AF.Copy),
                                 reads=[bb], writes=[bs])
                    if ci > 0:
                        S.op("dve", lambda e: e.tensor_scalar(out=acc, in0=st[:, 0:2048], scalar1=cw[:, 0, f:f + 1], scalar2=None, op0=ALU.mult),
                             reads=[bs, b_prm], writes=[b_acc])
                        for j in range(1, 4):
                            S.op("dve", lambda e: e.scalar_tensor_tensor(out=acc, in0=st[:, j:j + 2048], scalar=cw[:, j, f:f + 1], in1=acc,
                                                                         op0=ALU.mult, op1=ALU.add), reads=[bs, b_prm, b_acc], writes=[b_acc])
                        S.op("act", lambda e: e.activation(out=QK[:, f, :], in_=acc, func=AF.Silu, bias=cb[:, f:f + 1]),
                             reads=[b_acc, b_prm], writes=[b_qk[f]])
            for ci in (3, 4):
                wc, bw = ws.get(ci)
                for tt in range(NTT):
                    bank, bb = next_bank()
                    for kk in range(8):
                        S.op("pe", lambda e: e.matmul(bank[:, :], lhsT=HT[:, kk, 3 + tt * 128:3 + (tt + 1) * 128], rhs=wc[:, kk, :],
                                                      start=(kk == 0), stop=(kk == 7)), inc=(kk == 7), reads=[bw, HT_b[tt]], writes=[bb])
                    if ci == 3:
                        S.op("act", lambda e: e.activation(out=VA[:, tt, :, 0:128], in_=bank[:, :].rearrange("p (h v) -> p h v", h=4), func=AF.Copy),
                             reads=[bb], writes=[b_va[tt]])
                    else:
                        S.op("act", lambda e: e.activation(out=OG[:, tt, :], in_=bank[:, :], func=AF.Sigmoid), reads=[bb], writes=[b_og[tt]])
            wc, bw = ws.get(5)
            bank, bb = next_bank()
            for tt in range(NTT):
                for kk in range(8):
                    S.op("pe", lambda e: e.matmul(bank[:, tt * 8:(tt + 1) * 8], lhsT=HT[:, kk, 3 + tt * 128:3 + (tt + 1) * 128], rhs=wc[:, kk, :],
                                                  start=(kk == 0), stop=(kk == 7)), inc=(kk == 7), reads=[bw, HT_b[tt]], writes=[bb])
            S.op("dve", lambda e: e.tensor_copy(out=GR, in_=bank[:, 0:128]), reads=[bb], writes=[b_gr])

            if cfg.get("debug"):
                dbq = S.dma_sem("dbq")
                b_dbg = S.buf("dbg")
                S.dma("sp", dbq, dr["dbg_u"], view(A0, 48 * KB, 16 * KB, BF16), reads=b_us, writes=[b_dbg])
                S.dma("sp", dbq, dr["dbg_qk"], view(A0, 0, 32 * KB, BF16), reads=b_qk, writes=[b_dbg])
                S.dma("sp", dbq, dr["dbg_v"], view(A2, 0, 16512, BF16), reads=b_va, writes=[b_dbg])
                S.dma("sp", dbq, dr["dbg_o"], view(A0, 32 * KB, 16 * KB, BF16), reads=b_og, writes=[b_dbg])
                S.dma("sp", dbq, dr["dbg_if"], GR, reads=[b_gr], writes=[b_dbg])
                pass

            if stop == "win":
                quiesce(); return
            xfer("e_a0", view(A0, 0, 64 * KB, F32), b_qk + b_og + b_us)
            xfer("e_va", view(A2, 0, 17024, F32), b_va + [b_gr])
            b_yc = S.bufs(NTT, "yc")
            S.handoff(b_yc, HT_b)

            MS = SCR
            def garr(i):
                return view(MS, i * 256, 256, F32).rearrange("p (m h) -> p m h", h=4)
            nlf, ig, nF, g, Gm, R0, Rr, w0, wv, clamp, dl, ep, incl, t1 = [garr(i) for i in range(14)]
            sm = view(MS, 14 * 256, 256, F32)
            gcol = sm[:, 0:1]; dm = sm[:, 4:8]; rec = sm[:, 8:12]; ss = sm[:, 12:16]; rs4 = sm[:, 16:20]
            Gb = view(MS, 15 * 256, 512, F32)
            b_g = S.buf("gates")
            b_sm = S.buf("sm")
            KT = view(MS, 4608, 16512, BF16)
            kTok = KT[:, 0:16 * 512].rearrange("p (m d) -> p m d", m=16)
            Cin = KT[:, 0:64 * 129].rearrange("p (c v) -> p c v", c=64)
            b_kt = S.bufs(16, "kt")
            CL = view(MS, 4608 + 16512, 64 * 129 * 4, F32).rearrange("p (c v) -> p c v", c=64)
            b_cl = S.bufs(16, "cl")
            T0 = MS + 4608 + 16512 + 64 * 129 * 4
            Ct = view(T0, 0, 2064, F32).rearrange("p (h v) -> p h v", h=4)
            tmpC = view(T0, 2064, 2064, F32).rearrange("p (h v) -> p h v", h=4)
            b_ct = S.buf("ct"); b_tc = S.buf("tmpc")
            Vw = [view(T0, 4128 + i * 1032, 1032, BF16).rearrange("p (h v) -> p h v", h=4) for i in range(2)]
            b_vw = S.bufs(2, "vw")
            PT = [view(T0, 6192 + i * 256, 256, BF16) for i in range(2)]
            b_pt = S.bufs(2, "pt")
            hbuf = [view(T0, 6704 + i * 512, 512, F32) for i in range(4)]
            b_hb = S.bufs(4, "hb")
            hn = [view(T0, 8752 + i * 256, 256, BF16) for i in range(2)]
            b_hn = S.bufs(2, "hn")
            junk2 = view(T0, 9264, 256, BF16)
            assert T0 + 9520 <= A2 + A2_B, (T0 + 9520 - A2 - A2_B)

            handoff = S.handoff
            handoff([b_g, b_sm] + b_kt + b_cl + [b_ct, b_tc] + b_vw + b_pt + b_hb + b_hn, b_stage + [b_acc, b_junk, b_xh] + b_xnb)
            GRv = GR.rearrange("p (m c) -> p m c", c=8)

            def bc_m(ap4):
                return bass.AP(ap4.tensor, ap4.offset, [list(ap4.ap[0]), [0, 16], list(ap4.ap[1])])

            def bc_last(ap, n):
                return bass.AP(ap.tensor, ap.offset, [list(x) for x in ap.ap] + [[0, n]])

            def flat(a):
                return a.rearrange("p m h -> p (m h)")
            LN_S = -0.5 * float(np.log(128.0))
            S.op("pool", lambda e: e.memset(view(MS, 0, 4608, F32), 0.0), writes=[b_g, b_sm])
            S.op("dve", lambda e: e.tensor_tensor(out=ig, in0=GRv[:, :, 0:4], in1=bc_m(bif[:, 0:4]), op=ALU.add), reads=[b_gr, b_prm], writes=[b_g])
            S.op("dve", lambda e: e.tensor_tensor(out=t1, in0=GRv[:, :, 4:8], in1=bc_m(bif[:, 4:8]), op=ALU.add), reads=[b_gr, b_prm], writes=[b_g])
            S.op("act", lambda e: e.activation(out=t1, in_=t1, func=AF.Exp, scale=-1.0), reads=[b_g], writes=[b_g])
            S.op("act", lambda e: e.activation(out=nlf, in_=t1, func=AF.Ln, bias=1.0), reads=[b_g], writes=[b_g])
            bankA, bbA = next_bank()
            S.op("pe", lambda e: e.matmul(bankA[:, 0:64], lhsT=tri_f, rhs=flat(nlf), start=True, stop=True), reads=[b_g, b_cst], writes=[bbA])
            bankB, bbB = next_bank()
            S.op("pe", lambda e: e.matmul(bankB[:, 0:64], lhsT=ones_f, rhs=flat(nlf), start=True, stop=True), reads=[b_g, b_cst], writes=[bbB])
            bA = bankA[:, 0:64].rearrange("p (m h) -> p m h", h=4)
            bB = bankB[:, 0:64].rearrange("p (m h) -> p m h", h=4)
            for h in range(4):
                S.op("dve", lambda e: e.tensor_tensor_scan(out=incl[:, :, h], data0=ones_f[:, 0:16], data1=bB[:, :, h], initial=0.0,
                                                           op0=ALU.mult, op1=ALU.add), reads=[bbB, b_cst, b_g], writes=[b_g])
            S.op("dve", lambda e: e.tensor_tensor(out=t1, in0=incl, in1=bB, op=ALU.subtract), reads=[b_g, bbB], writes=[b_g])
            S.op("dve", lambda e: e.tensor_tensor(out=nF, in0=t1, in1=bA, op=ALU.add), reads=[b_g, bbA], writes=[b_g])
            S.op("dve", lambda e: e.tensor_tensor(out=g, in0=ig, in1=nF, op=ALU.add), reads=[b_g], writes=[b_g])
            bankT, bbT = next_bank()
            S.op("pe", lambda e: e.transpose(bankT[0:64, 0:128], flat(g), ident_f), reads=[b_g, b_cst], writes=[bbT])
            S.op("dve", lambda e: e.tensor_reduce(out=gcol[0:64, :], in_=bankT[0:64, 0:128], axis=AX.X, op=ALU.max), reads=[bbT], writes=[b_sm])
            S.op("dve", lambda e: e.tensor_copy(out=Gb[0:64, :], in_=bc_last(gcol[0:64, 0:1], 128)[:, 0, :]), reads=[b_sm], writes=[b_sm])
            bankG, bbG = next_bank()
            S.op("pe", lambda e: e.matmul(bankG[:, 0:64], lhsT=Gb[0:64, :], rhs=ident_f[0:64, 0:64], start=True, stop=True), reads=[b_sm, b_cst], writes=[bbG])
            S.op("dve", lambda e: e.tensor_copy(out=flat(Gm), in_=bankG[:, 0:64]), reads=[bbG], writes=[b_g])
            for h in range(4):
                S.op("dve", lambda e: e.tensor_tensor_scan(out=R0[:, :, h], data0=Gm[:, :, h], data1=Gm[:, :, h], initial=-1e30,
                                                           op0=ALU.max, op1=ALU.max), reads=[b_g], writes=[b_g])
            S.op("dve", lambda e: e.tensor_tensor(out=t1, in0=g, in1=R0, op=ALU.subtract), reads=[b_g], writes=[b_g])
            S.op("act", lambda e: e.activation(out=w0, in_=t1, func=AF.Exp, bias=LN_S), reads=[b_g], writes=[b_g])
            S.op("dve", lambda e: e.tensor_tensor(out=t1[:, 1:16, :], in0=R0[:, 0:15, :], in1=R0[:, 1:16, :], op=ALU.subtract), reads=[b_g], writes=[b_g])
            S.op("act", lambda e: e.activation(out=dl[:, 1:16, :], in_=t1[:, 1:16, :], func=AF.Exp), reads=[b_g], writes=[b_g])
            for m in range(16):
                cs = slice(m * 128, (m + 1) * 128)
                bank, bb = next_bank()
                pb = bank[:, 0:256].bitcast(BF16)
                for h in range(4):
                    S.op("pe", lambda e: e.transpose(pb[:, h * 128:(h + 1) * 128], QK[:, 4 + h, cs], ident_b), inc=(h == 3),
                         reads=[b_qk[4 + h], b_cst], writes=[bb])
                S.op("act", lambda e: e.activation(out=kTok[:, m, :], in_=pb, func=AF.Copy), reads=[bb], writes=[b_kt[m]])
                vw = Vw[m % 2]; bv = b_vw[m % 2]
                S.op("dve", lambda e: e.tensor_tensor(out=vw, in0=VA[:, m, :, :], in1=bc_last(w0[:, m, :], 129), op=ALU.mult),
                     reads=[b_va[m], b_g], writes=[bv])
                for h in range(4):
                    bank, bb = next_bank()
                    S.op("pe", lambda e: e.matmul(bank[:, 0:129], lhsT=kTok[:, m, h * 128:(h + 1) * 128], rhs=vw[:, h, :], start=True, stop=True),
                         reads=[b_kt[m], bv], writes=[bb])
                    if h % 2 == 0:
                        S.op("act", lambda e: e.activation(out=CL[:, m * 4 + h, :], in_=bank[:, 0:129], func=AF.Copy), reads=[bb], writes=[b_cl[m]])
                    else:
                        S.op("dve", lambda e: e.tensor_copy(out=CL[:, m * 4 + h, :], in_=bank[:, 0:129]), reads=[bb], writes=[b_cl[m]])
                if m == 0:
                    S.op("dve", lambda e: e.tensor_copy(out=Ct, in_=CL[:, 0:4, :]), reads=[b_cl[0]], writes=[b_ct])
                else:
                    for h in range(4):
                        S.op("dve", lambda e: e.scalar_tensor_tensor(out=Ct[:, h, :], in0=Ct[:, h, :], scalar=dl[:, m, h:h + 1], in1=CL[:, m * 4 + h, :],
                                                                     op0=ALU.mult, op1=ALU.add), reads=[b_ct, b_g, b_cl[m]], writes=[b_ct])
            ml_extra = []
            xfer("e_gates", view(MS, 0, 4608, F32), [b_g, b_sm])
            xfer("e_cl", view(MS, 4608 + 16512, 64 * 129 * 4, F32), b_cl)
            if IMP:
                S.mute = False
            mst = sm[:, 20:24]
            exq = S.dma_sem(f"exq{l}")
            if mode == "A":
                S.dma("sp", oq, dr["summ"][:, 32:548], Ct.rearrange("p h v -> p (h v)"), reads=[b_ct], writes=[b_out])
                S.dma("sp", oq, dr["summ"][:, 548:552], R0[:, 15, :], reads=[b_g], writes=[b_out])
                S.dma("sp", oq, dr["summ"][:, 552:556], incl[:, 15, :], reads=[b_g], writes=[b_out])
            else:
                sa = dr["summ_all"]
                small = Gb.rearrange("p (j c) -> p j c", j=4)[:, :, 0:8]
                S.dma("sp", exq, small, sa[:, :, 548:556].rearrange("j p c -> p j c"), writes=[b_sm])
                S.seal(exq, [b_sm])
                cm = sm[:, 20:24]; mx = sm[:, 24:28]; ta = sm[:, 28:32]; tb = sm[:, 32:36]; r0j = sm[:, 36:40]; nfj = sm[:, 40:44]; tq = sm[:, 44:48]
                S.op("dve", lambda e: e.memset(tmpC, 0.0), reads=[b_tc], writes=[b_tc])
                S.op("dve", lambda e: e.memset(cm, 0.0), reads=[b_sm], writes=[b_sm])
                cq = [S.dma_sem(f"cq{l}_{i}") for i in range(2)]
                Cj = [Ct, view(T0, 4128, 2064, F32).rearrange("p (h v) -> p h v", h=4)]
                b_cj = [b_ct, S.buf("cj1")]
                S.handoff([b_cj[1]], b_vw)
                for j in range(4):
                    cj = Cj[j % 2]; bcj = b_cj[j % 2]
                    S.dma("sp", cq[j % 2], cj.rearrange("p h v -> p (h v)"), sa[j, :, 32:548], writes=[bcj])
                    S.op("dve", lambda e: e.tensor_scalar(out=r0j, in0=small[:, j, 0:4], scalar1=pred[:, j:j + 1], scalar2=pmask[:, j:j + 1],
                                                          op0=ALU.mult, op1=ALU.add), reads=[b_sm, b_cst], writes=[b_sm])
                    S.op("dve", lambda e: e.tensor_scalar(out=nfj, in0=small[:, j, 4:8], scalar1=pred[:, j:j + 1], scalar2=None, op0=ALU.mult),
                         reads=[b_sm, b_cst], writes=[b_sm])
                    S.op("dve", lambda e: e.tensor_tensor(out=mx, in0=cm, in1=r0j, op=ALU.max), reads=[b_sm], writes=[b_sm])
                    S.op("dve", lambda e: e.tensor_tensor(out=tq, in0=cm, in1=mx, op=ALU.subtract), reads=[b_sm], writes=[b_sm])
                    S.op("act", lambda e: e.activation(out=ta, in_=tq, func=AF.Exp), reads=[b_sm], writes=[b_sm])
                    S.op("dve", lambda e: e.tensor_tensor(out=tq, in0=r0j, in1=mx, op=ALU.subtract), reads=[b_sm], writes=[b_sm])
                    S.op("act", lambda e: e.activation(out=tb, in_=tq, func=AF.Exp), reads=[b_sm], writes=[b_sm])
                    S.op("dve", lambda e: e.tensor_tensor(out=tmpC, in0=tmpC, in1=bc_last(ta, 129), op=ALU.mult), reads=[b_tc, b_sm], writes=[b_tc])
                    S.op("dve", lambda e: e.tensor_tensor(out=cj, in0=cj, in1=bc_last(tb, 129), op=ALU.mult), reads=[bcj, b_sm], writes=[bcj])
                    S.op("dve", lambda e: e.tensor_tensor(out=tmpC, in0=tmpC, in1=cj, op=ALU.add), reads=[b_tc, bcj], writes=[b_tc])
                    S.op("dve", lambda e: e.tensor_tensor(out=cm, in0=mx, in1=nfj, op=ALU.subtract), reads=[b_sm], writes=[b_sm])
            if mode != "A":
                S.op("dve", lambda e: e.tensor_tensor(out=Rr, in0=R0, in1=bc_m(mst), op=ALU.max), reads=[b_g, b_sm], writes=[b_g])
                S.op("dve", lambda e: e.tensor_tensor(out=t1, in0=g, in1=Rr, op=ALU.subtract), reads=[b_g], writes=[b_g])
                S.op("act", lambda e: e.activation(out=wv, in_=t1, func=AF.Exp, bias=LN_S), reads=[b_g], writes=[b_g])
                S.op("dve", lambda e: e.tensor_tensor(out=t1, in0=nF, in1=Rr, op=ALU.subtract), reads=[b_g], writes=[b_g])
                S.op("act", lambda e: e.activation(out=clamp, in_=t1, func=AF.Exp), reads=[b_g], writes=[b_g])
                S.op("dve", lambda e: e.tensor_tensor(out=t1, in0=R0, in1=Rr, op=ALU.subtract), reads=[b_g], writes=[b_g])
                S.op("act", lambda e: e.activation(out=ep, in_=t1, func=AF.Exp), reads=[b_g], writes=[b_g])
                S.op("dve", lambda e: e.tensor_tensor(out=t1[:, 1:16, :], in0=Rr[:, 0:15, :], in1=Rr[:, 1:16, :], op=ALU.subtract), reads=[b_g], writes=[b_g])
                S.op("dve", lambda e: e.tensor_tensor(out=t1[:, 0, :], in0=mst, in1=Rr[:, 0, :], op=ALU.subtract), reads=[b_g, b_sm], writes=[b_g])
                S.op("act", lambda e: e.activation(out=dl, in_=t1, func=AF.Exp), reads=[b_g], writes=[b_g])
                S.op("dve", lambda e: e.tensor_copy(out=Ct, in_=tmpC), reads=[b_tc], writes=[b_ct])
                b_cin = b_kt
                S.op("dve", lambda e: e.tensor_tensor(out=CL, in0=CL, in1=bc_last(ep.rearrange("p m h -> p (m h)"), 129), op=ALU.mult),
                     reads=b_cl + [b_g], writes=b_cl)
                b_cth = S.bufs(4, "cth")
                S.handoff(b_cth, [b_ct])
                for m in range(16):
                    for h in range(4):
                        S.op("act", lambda e: e.activation(out=Cin[:, m * 4 + h, :], in_=Ct[:, h, :], func=AF.Copy, scale=dl[:, m, h:h + 1]),
                             reads=[b_cth[h], b_g], writes=[b_cin[m]])
                        S.op("dve", lambda e: e.scalar_tensor_tensor(out=Ct[:, h, :], in0=Ct[:, h, :], scalar=dl[:, m, h:h + 1], in1=CL[:, m * 4 + h, :],
                                                                     op0=ALU.mult, op1=ALU.add), reads=[b_cth[h], b_g, b_cl[m]], writes=[b_cth[h]])
                S.handoff([b_ct], b_cth + [b_ct])
                PT4 = [view(T0, i * 256, 256, BF16) for i in range(4)]
                hn4 = [view(T0, 1024 + i * 1024, 1024, BF16).rearrange("p (h t) -> p h t", h=4) for i in range(2)]
                hbA = view(T0, 6704, 2048, F32).rearrange("p (h t) -> p h t", h=4)
                hbB = view(T0, 3072, 2048, F32).rearrange("p (h t) -> p h t", h=4)
                hb4 = [hbA, hbB]
                b_pt4 = S.bufs(4, "pt4"); b_hn4 = S.bufs(2, "hn4"); b_hb4 = [S.bufs(4, "hbA"), S.bufs(4, "hbB")]
                b_j2 = S.buf("j2")
                b_ep = [S.buf("ep0"), S.buf("ep1")]
                S.handoff(b_pt4 + b_hn4 + b_hb4[0] + b_hb4[1] + [b_j2] + b_ep, [b_ct, b_tc, b_sm] + b_vw + b_pt + b_hb + b_hn + ([b_cj[1]] if mode != "A" else []))
                ml_extra += b_pt4 + b_hn4 + b_hb4[0] + b_hb4[1] + [b_j2] + b_ep
                smx = [sm[:, 4:20], sm[:, 48:64]]
                for m in range(16):
                    cs = slice(m * 128, (m + 1) * 128)
                    par = m % 2
                    dm_, rec_, ss_, rs_ = smx[par][:, 0:4], smx[par][:, 4:8], smx[par][:, 8:12], smx[par][:, 12:16]
                    be = b_ep[par]
                    bankS, bbS = next_bank()
                    for h in range(4):
                        S.op("pe", lambda e: e.matmul(bankS[:, h * 128:(h + 1) * 128], lhsT=QK[:, 4 + h, cs], rhs=QK[:, h, cs], start=True, stop=True),
                             inc=(h == 3), reads=[b_qk[4 + h], b_qk[h]], writes=[bbS])
                    for h in range(4):
                        S.op("dve", lambda e: e.scalar_tensor_tensor(out=PT4[h], in0=bankS[:, h * 128:(h + 1) * 128], scalar=wv[:, m, h:h + 1], in1=tri_f,
                                                                     op0=ALU.mult, op1=ALU.mult), reads=[bbS, b_g, b_cst], writes=[b_pt4[h]])
                    bN = []
                    for j in range(2):
                        bankN, bbN = next_bank()
                        bN.append((bankN, bbN))
                        for hh_ in range(2):
                            h = 2 * j + hh_
                            co = hh_ * 129
                            S.op("pe", lambda e: e.matmul(bankN[:, co:co + 129], lhsT=PT4[h], rhs=VA[:, m, h, :], start=True, stop=False), inc=False,
                                 reads=[b_pt4[h], b_va[m]], writes=[bbN])
                            S.op("pe", lambda e: e.matmul(bankN[:, co:co + 129], lhsT=QK[:, h, cs], rhs=Cin[:, m * 4 + h, :], start=False, stop=True),
                                 inc=(hh_ == 1), reads=[b_qk[h]] + b_cin, writes=[bbN])
                        den = bankN[:, 0:258].rearrange("p (h v) -> p h v", v=129)[:, :, 128]
                        S.op("act", lambda e: e.activation(out=dm_[:, 2 * j:2 * j + 2], in_=den, func=AF.Abs), reads=[bbN], writes=[be])
                        S.op("dve", lambda e: e.tensor_tensor(out=dm_[:, 2 * j:2 * j + 2], in0=dm_[:, 2 * j:2 * j + 2], in1=clamp[:, m, 2 * j:2 * j + 2], op=ALU.max),
                             reads=[be, b_g], writes=[be])
                    S.op("dve", lambda e: e.reciprocal(out=rec_, in_=dm_), reads=[be], writes=[be])
                    for h in range(4):
                        bankN, bbN = bN[h // 2]
                        co = (h % 2) * 129
                        S.op("dve", lambda e: e.scalar_tensor_tensor(out=hb4[par][:, h, :], in0=bankN[:, co:co + 128], scalar=rec_[:, h:h + 1],
                                                                     in1=OG[:, m, h * 128:(h + 1) * 128], op0=ALU.mult, op1=ALU.mult),
                             reads=[bbN, be, b_og[m]], writes=[b_hb4[par][h]])
                        S.op("act", lambda e: e.activation(out=junk2, in_=hb4[par][:, h, :], func=AF.Square, accum_out=ss_[:, h:h + 1]),
                             reads=[b_hb4[par][h]], writes=[b_j2, be])
                    S.op("dve", lambda e: e.tensor_scalar(out=rs_, in0=ss_, scalar1=1.0 / 128, scalar2=EPS, op0=ALU.mult, op1=ALU.add), reads=[be], writes=[be])
                    S.op("act", lambda e: e.activation(out=rs_, in_=rs_, func=AF.Sqrt), reads=[be], writes=[be])
                    S.op("dve", lambda e: e.reciprocal(out=rs_, in_=rs_), reads=[be], writes=[be])
                    S.op("dve", lambda e: e.tensor_tensor(out=hn4[par], in0=hb4[par], in1=bc_last(rs_, 128), op=ALU.mult),
                         reads=b_hb4[par] + [be], writes=[b_hn4[par]])
                    bankO, bbO = next_bank()
                    po = bankO[:, 0:256].bitcast(BF16).rearrange("p (h t) -> p h t", h=4)
                    for h in range(4):
                        S.op("pe", lambda e: e.transpose(po[:, h, :], hn4[par][:, h, :], ident_b), inc=(h == 3), reads=[b_hn4[par], b_cst], writes=[bbO])
                    S.op("dve", lambda e: e.tensor_tensor(out=YC[:, 4:8, cs], in0=po, in1=bc_last(mlg, 128), op=ALU.mult), reads=[bbO, b_prm], writes=[b_yc[m]])

            if cfg.get("debug"):
                S.dma("sp", dbq, dr["dbg_g"].rearrange("p (a c) -> p a c", a=16), view(MS, 0, 4096, F32).rearrange("p (a c) -> p a c", a=16), reads=[b_g, b_sm], writes=[b_dbg])

            if stop == "ml":
                quiesce(); return
            TCH = 16; NC_ = 128
            WZ = view(A0, 0, 32 * KB, BF16).rearrange("p (q i x n) -> p q i x n", q=4, i=16, x=2)
            BD = view(A0, 32 * KB, 16 * KB, BF16).rearrange("p (q j n) -> p q j n", q=4, j=16)
            CP = view(A2, 0, 34816, BF16).rearrange("p (j r x n) -> p j r x n", j=17, r=16, x=2)
            EC = view(A2, 34816, 8192, F32).rearrange("p (r c) -> p r c", r=16)
            ES = view(A2, 34816 + 8192, 8192, F32).rearrange("p (r c) -> p r c", r=16)
            ZL = view(A2, 51200, 16384, F32).rearrange("p (r x c) -> p r x c", r=16, x=2)
            ZS = view(A2, 67584, 8256, BF16).rearrange("p (r x c) -> p r x c", r=16, x=2)
            SP_ = A2 + 76032
            PW = view(SP_, 0, 2176, F32).rearrange("p (j x r) -> p j x r", j=17, x=2)
            def sc(i):
                return view(SP_, 2176 + i * 64, 64, F32)
            assert SP_ + 2176 + 30 * 64 <= A2 + A2_B
            WW = view(A0, 0, 16384, F32).rearrange("p (r x c) -> p r x c", r=16, x=2)
            ZG = view(A2, 51200, 16384, BF16).rearrange("p (q i c) -> p q i c", q=4, i=16)
            GT = A0 + 16 * KB
            GEN = A2 + 51200
            b_s5 = S.buf("s5gen")
            b_wz = S.bufs(4, "wz"); b_bd = S.bufs(4, "bd"); b_cp = S.buf("cp"); b_tab = S.buf("tab")
            b_zl = S.bufs(16, "zl"); b_ww = S.bufs(16, "ww"); b_zs = S.bufs(16, "zs"); b_zg = S.bufs(4, "zg")
            olds = [b_g, b_sm] + b_kt + b_cl + [b_ct, b_tc] + b_vw + b_pt + b_hb + b_hn + b_va + [b_gr] + b_qk + b_og + ml_extra
            handoff([b_s5, b_cp, b_tab] + b_wz + b_bd + b_zl + b_ww + b_zs + b_zg, olds)
            s5q = S.dma_sem(f"s5q{l}")
            (s_are, s_aim, s_dt, s_mag, s_th, s_t, s_sin, s_cos, s_abr, s_abi, s_den, s_zr, s_sre, s_sim, s_t2, s_magL,
             s_l128r, s_l128i, s_t3, s_m128) = [sc(i) for i in range(20)]
            zend = sc(20)[:, 0:16]
            zend = view(SP_, 2176 + 20 * 64, 128, F32).rearrange("p (r x) -> p r x", x=2)
            sst = view(SP_, 2176 + 22 * 64, 128, F32).rearrange("p (r x) -> p r x", x=2)

            def TT(out, in0, in1, op, rd=(), wr=None, eng="dve"):
                S.op(eng, lambda e: e.tensor_tensor(out=out, in0=in0, in1=in1, op=op), reads=[b_s5] + list(rd), writes=[b_s5] if wr is None else wr)

            def TS(out, in0, s1, s2, op0, op1=None, rd=(), wr=None):
                if op1 is None:
                    S.op("dve", lambda e: e.tensor_scalar(out=out, in0=in0, scalar1=s1, scalar2=None, op0=op0), reads=[b_s5] + list(rd), writes=[b_s5] if wr is None else wr)
                else:
                    S.op("dve", lambda e: e.tensor_scalar(out=out, in0=in0, scalar1=s1, scalar2=s2, op0=op0, op1=op1), reads=[b_s5] + list(rd), writes=[b_s5] if wr is None else wr)

            def AC(out, in_, func, rd=(), wr=None, **kw):
                S.op("act", lambda e: e.activation(out=out, in_=in_, func=func, **kw), reads=[b_s5] + list(rd), writes=[b_s5] if wr is None else wr)

            def cmul(o_r, o_i, a_r, a_i, b_r, b_i, t1_, t2_, rd=(), wr=None, neg_im=False):
                TT(t1_, a_r, b_r, ALU.mult, rd); TT(t2_, a_i, b_i, ALU.mult, rd)
                TT(o_r, t1_, t2_, ALU.subtract, rd, wr)
                TT(t1_, a_r, b_i, ALU.mult, rd); TT(t2_, a_i, b_r, ALU.mult, rd)
                if neg_im:
                    TT(t1_, t1_, t2_, ALU.add, rd)
                    TS(o_i, t1_, -1.0, None, ALU.mult, rd=rd, wr=wr)
                else:
                    TT(o_i, t1_, t2_, ALU.add, rd, wr)

            if S5IMP:
                S.mute = True
            if PREP:
                S.mute = False
            S.op("pool", lambda e: e.memset(view(SP_, 0, 4864, F32), 0.0), writes=[b_s5])
            araw = view(GEN, 0, 1024, F32)
            S.dma("sp", s5q, araw[0:16, 0:128], dr["s5_a_re"][l].rearrange("(r gl) n -> r (gl n)", gl=2), writes=[b_s5])
            S.dma("sp", s5q, araw[0:16, 128:256], dr["s5_a_im"][l].rearrange("(r gl) n -> r (gl n)", gl=2), writes=[b_s5])
            ldt = dr["s5_log_dt"][l]
            for gl in range(2):
                S.dma("sp", s5q, s_dt[gl * 64:(gl + 1) * 64, :], bass.AP(ldt.tensor, ldt.offset + gl, [[0, 64], [2, 16]]), writes=[b_s5])
            Bsm = [view(GEN, 1024 + x * 1024, 1024, F32).rearrange("p (r c) -> p r c", r=16) for x in range(2)]
            for x, nm in enumerate(["s5_b_re", "s5_b_im"]):
                bsrc = dr[nm][l]
                for gl in range(2):
                    S.dma("sp", s5q, Bsm[x][gl * 64:(gl + 1) * 64, :, :],
                          bass.AP(bsrc.tensor, bsrc.offset + gl * 1024, [[16, 64], [2048, 16], [1, 16]]), writes=[b_s5])
            Craw = [view(GEN, 3072 + x * 1024, 1024, F32).rearrange("p (q n) -> p q n", q=4) for x in range(2)]
            for x, nm in enumerate(["s5_c_re", "s5_c_im"]):
                csrc = dr[nm][l]
                S.dma("sp", s5q, Craw[x], bass.AP(csrc.tensor, csrc.offset, [[64, 128], [8192, 4], [1, 64]]), writes=[b_s5])
            S.seal(s5q, [b_s5])
            bank, bb = next_bank()
            S.op("pe", lambda e: e.transpose(bank[:, 0:16], araw[0:16, 0:128], ident_f[0:16, 0:16]), reads=[b_s5, b_cst], writes=[bb])
            S.op("pe", lambda e: e.transpose(bank[:, 16:32], araw[0:16, 128:256], ident_f[0:16, 0:16]), reads=[b_s5, b_cst], writes=[bb])
            S.op("dve", lambda e: e.tensor_copy(out=s_are, in_=bank[:, 0:16]), reads=[bb], writes=[b_s5])
            S.op("dve", lambda e: e.tensor_copy(out=s_aim, in_=bank[:, 16:32]), reads=[bb], writes=[b_s5])
            PI = float(np.pi)
            AC(s_dt, s_dt, AF.Exp)
            TT(s_t, s_are, s_dt, ALU.mult)
            AC(s_mag, s_t, AF.Exp)
            AC(s_magL, s_t, AF.Exp, scale=float(TCH))
            AC(s_m128, s_t, AF.Exp, scale=float(TCH * NC_))
            TT(s_th, s_aim, s_dt, ALU.mult)
            for thr in (1.0, 3.0, 5.0, 7.0):
                TS(s_t, s_th, thr * PI, -2.0 * PI, ALU.is_gt, ALU.mult)
                if thr == 1.0:
                    TT(s_t2, s_th, s_t, ALU.add)
                else:
                    TT(s_t2, s_t2, s_t, ALU.add)
            AC(s_sin, s_t2, AF.Sin)
            TS(s_t3, s_t2, 0.5 * PI, None, ALU.add)
            TS(s_t, s_t3, PI, -2.0 * PI, ALU.is_gt, ALU.mult)
            TT(s_t3, s_t3, s_t, ALU.add)
            AC(s_cos, s_t3, AF.Sin)
            TT(s_abr, s_mag, s_cos, ALU.mult); TT(s_abi, s_mag, s_sin, ALU.mult)
            TT(s_t, s_are, s_are, ALU.mult); TT(s_t2, s_aim, s_aim, ALU.mult); TT(s_den, s_t, s_t2, ALU.add)
            S.op("dve", lambda e: e.reciprocal(out=s_den, in_=s_den), reads=[b_s5], writes=[b_s5])
            TS(s_zr, s_abr, -1.0, None, ALU.add)
            TT(s_t, s_zr, s_are, ALU.mult); TT(s_t2, s_abi, s_aim, ALU.mult); TT(s_t, s_t, s_t2, ALU.add); TT(s_sre, s_t, s_den, ALU.mult)
            TT(s_t, s_abi, s_are, ALU.mult); TT(s_t2, s_zr, s_aim, ALU.mult); TT(s_t, s_t, s_t2, ALU.subtract); TT(s_sim, s_t, s_den, ALU.mult)
            S.op("dve", lambda e: e.memset(PW[:, 0, 0, :], 1.0), reads=[b_s5], writes=[b_s5])
            S.op("dve", lambda e: e.memset(PW[:, 0, 1, :], 0.0), reads=[b_s5], writes=[b_s5])
            S.op("dve", lambda e: e.tensor_copy(out=PW[:, 1, 0, :], in_=s_abr), reads=[b_s5], writes=[b_s5])
            S.op("dve", lambda e: e.tensor_copy(out=PW[:, 1, 1, :], in_=s_abi), reads=[b_s5], writes=[b_s5])
            pt1 = view(GEN, 5120, 1024, F32).rearrange("p (j r) -> p j r", r=16)
            pt2 = view(GEN, 6144, 1024, F32).rearrange("p (j r) -> p j r", r=16)
            kk_ = 1
            while kk_ < 16:
                def bj(a):
                    return bass.AP(a.tensor, a.offset, [list(a.ap[0]), [0, kk_], list(a.ap[1])])
                cmul(PW[:, kk_ + 1:2 * kk_ + 1, 0, :], PW[:, kk_ + 1:2 * kk_ + 1, 1, :], PW[:, 1:kk_ + 1, 0, :], PW[:, 1:kk_ + 1, 1, :],
                     bj(PW[:, kk_, 0, :]), bj(PW[:, kk_, 1, :]), pt1[:, 0:kk_, :], pt2[:, 0:kk_, :])
                kk_ *= 2
            S.op("dve", lambda e: e.reciprocal(out=s_t, in_=s_magL), reads=[b_s5], writes=[b_s5])
            TT(EC[:, :, 0], PW[:, 16, 0, :], s_t, ALU.mult, wr=[b_s5, b_tab]); TT(ES[:, :, 0], PW[:, 16, 1, :], s_t, ALU.mult, wr=[b_s5, b_tab])
            et1 = view(GEN, 7168, 4096, F32).rearrange("p (r c) -> p r c", r=16)
            et2 = view(GEN, 11264, 4096, F32).rearrange("p (r c) -> p r c", r=16)
            kk_ = 1
            while kk_ < NC_:
                cmul(EC[:, :, kk_:2 * kk_], ES[:, :, kk_:2 * kk_], EC[:, :, 0:kk_], ES[:, :, 0:kk_],
                     bc_last(EC[:, :, kk_ - 1], kk_), bc_last(ES[:, :, kk_ - 1], kk_), et1[:, :, 0:kk_], et2[:, :, 0:kk_], rd=[b_tab], wr=[b_s5, b_tab])
                kk_ *= 2
            TT(s_l128r, EC[:, :, NC_ - 1], s_m128, ALU.mult, rd=[b_tab]); TT(s_l128i, ES[:, :, NC_ - 1], s_m128, ALU.mult, rd=[b_tab])
            Cin_ = [view(GEN, 5120 + x * 2048, 2048, F32).rearrange("p (q n) -> p q n", q=4) for x in range(2)]
            Cp = [view(GEN, 9216 + x * 2048, 2048, F32).rearrange("p (r n) -> p r n", r=16) for x in range(2)]
            ct1 = view(GEN, 13312, 2048, F32).rearrange("p (r n) -> p r n", r=16)
            ct2 = view(A0, 0, 2048, F32).rearrange("p (r n) -> p r n", r=16)
            for x in range(2):
                TS(Cin_[x][:, :, 0:64], Craw[x], par01[:, 0:1], None, ALU.mult, rd=[b_cst])
                TS(Cin_[x][:, :, 64:128], Craw[x], par01[:, 1:2], None, ALU.mult, rd=[b_cst])
                bank, bb = next_bank()
                for q in range(4):
                    S.op("pe", lambda e: e.transpose(bank[:, q * 128:(q + 1) * 128], Cin_[x][:, q, :], ident_f), inc=(q == 3), reads=[b_s5, b_cst], writes=[bb])
                S.op("dve", lambda e: e.tensor_copy(out=Cp[x].rearrange("p r n -> p (r n)"), in_=bank[:, :]), reads=[bb], writes=[b_s5])
            for j in range(17):
                pr = bc_last(PW[:, j, 0, :], 32); pi_ = bc_last(PW[:, j, 1, :], 32)
                TT(ct1, Cp[0], pr, ALU.mult); TT(ct2, Cp[1], pi_, ALU.mult, rd=b_wz, wr=[b_s5] + b_wz)
                TT(CP[:, j, :, 0, :], ct1, ct2, ALU.subtract, wr=[b_s5, b_cp])
                TT(ct1, Cp[0], pi_, ALU.mult); TT(ct2, Cp[1], pr, ALU.mult, rd=b_wz, wr=[b_s5] + b_wz)
                S.op("dve", lambda e: e.scalar_tensor_tensor(out=CP[:, j, :, 1, :], in0=ct1, scalar=-1.0, in1=ct2, op0=ALU.mult, op1=ALU.subtract),
                     reads=[b_s5], writes=[b_s5, b_cp])

            BB = [view(GEN, 5120 + x * 2048, 2048, F32).rearrange("p (r n) -> p r n", r=16) for x in range(2)]
            BBb = [view(GEN, 9216 + x * 1024, 1024, BF16).rearrange("p (r n) -> p r n", r=16) for x in range(2)]
            bt1 = view(GEN, 11264, 1024, F32).rearrange("p (r c) -> p r c", r=16)
            bt2 = view(GEN, 12288, 1024, F32).rearrange("p (r c) -> p r c", r=16)
            for x in range(2):
                S.op("dve", lambda e: e.memset(BB[x], 0.0), reads=[b_s5], writes=[b_s5])
            sre_b = bc_last(s_sre, 16); sim_b = bc_last(s_sim, 16)
            TT(bt1, Bsm[0], sre_b, ALU.mult); TT(bt2, Bsm[1], sim_b, ALU.mult)
            for gl in range(2):
                ps_ = slice(gl * 64, (gl + 1) * 64)
                TT(BB[0][ps_, :, gl * 16:(gl + 1) * 16], bt1[ps_], bt2[ps_], ALU.subtract)
            TT(bt1, Bsm[1], sre_b, ALU.mult); TT(bt2, Bsm[0], sim_b, ALU.mult)
            for gl in range(2):
                ps_ = slice(gl * 64, (gl + 1) * 64)
                TT(BB[1][ps_, :, gl * 16:(gl + 1) * 16], bt1[ps_], bt2[ps_], ALU.add)
            for x in range(2):
                S.op("dve", lambda e: e.tensor_copy(out=BBb[x], in_=BB[x]), reads=[b_s5], writes=[b_s5])
            bdt = view(GEN, 13312, 512, F32)
            for j in range(16):
                bank, bb = next_bank()
                for q in range(4):
                    for x in range(2):
                        S.op("pe", lambda e: e.matmul(bank[:, q * 128:(q + 1) * 128], lhsT=BBb[x][:, 4 * q:4 * q + 4, :].rearrange("p r n -> p (r n)"),
                                                      rhs=CP[:, j, 4 * q:4 * q + 4, x, :], start=(x == 0), stop=(x == 1)), inc=(q == 3 and x == 1),
                             reads=[b_s5, b_cp], writes=[bb])
                if j == 0:
                    for q in range(4):
                        S.op("dve", lambda e: e.tensor_tensor(out=bdt, in0=bank[:, q * 128:(q + 1) * 128], in1=bdm, op=ALU.mult), reads=[bb, b_cst, b_s5], writes=[b_s5])
                        S.op("dve", lambda e: e.scalar_tensor_tensor(out=BD[:, q, 0, :], in0=ident_f, scalar=dcol[:, q:q + 1], in1=bdt, op0=ALU.mult, op1=ALU.add),
                             reads=[b_s5, b_cst, b_prm], writes=[b_bd[q]])
                else:
                    bdm_b = bass.AP(bdm.tensor, bdm.offset, [list(bdm.ap[0]), [0, 4], list(bdm.ap[1])])
                    S.op("dve", lambda e: e.tensor_tensor(out=BD[:, :, j, :], in0=bank[:, :].rearrange("p (q n) -> p q n", q=4), in1=bdm_b, op=ALU.mult),
                         reads=[bb, b_cst], writes=b_bd)
            mt1 = view(GEN, 13824, 2048, F32).rearrange("p (r n) -> p r n", r=16)
            mt2 = view(GEN, 1024, 2048, F32).rearrange("p (r n) -> p r n", r=16)
            MB = [view(GEN, 3072 + x * 1024, 1024, BF16).rearrange("p (r n) -> p r n", r=16) for x in range(2)]
            for i in range(16):
                j = 15 - i
                pr = bc_last(PW[:, j, 0, :], 32); pi_ = bc_last(PW[:, j, 1, :], 32)
                TT(mt1, BB[0], pr, ALU.mult); TT(mt2, BB[1], pi_, ALU.mult); TT(MB[0], mt1, mt2, ALU.subtract)
                TT(mt1, BB[0], pi_, ALU.mult); TT(mt2, BB[1], pr, ALU.mult); TT(MB[1], mt1, mt2, ALU.add)
                bank, bb = next_bank()
                pb = bank[:, :].bitcast(BF16).rearrange("p (q x n) -> p q x n", q=4, x=2)
                for q in range(4):
                    for x in range(2):
                        S.op("pe", lambda e: e.transpose(pb[:, q, x, :], MB[x][:, 4 * q:4 * q + 4, :].rearrange("p r n -> p (r n)"), ident_b),
                             inc=(q == 3 and x == 1), reads=[b_s5, b_cst], writes=[bb])
                S.op("act", lambda e: e.activation(out=WZ[:, :, i, :, :], in_=pb, func=AF.Copy), reads=[bb], writes=b_wz)
            if stop == "s5gen":
                quiesce(); return
            if PREP:
                xfer("e_wz", view(A0, 0, 32 * KB, F32), b_wz)
                xfer("e_cp", view(A2, 0, 34816, F32), [b_cp])
                xfer("e_bd", view(A0, 32 * KB, 16 * KB, F32), b_bd)
                xfer("e_tab", view(A2, 34816, 16384, F32), [b_tab])
                xfer("e_sp", view(SP_, 0, 4864, F32), [b_s5])
                S.wait_all("sp", [b_out])
                return
            if EXP:
                S.mute = False
                xfer("e_wz", view(A0, 0, 32 * KB, F32), b_wz)
                xfer("e_tab", view(A2, 34816, 16384, F32), [b_tab])
                xfer("e_sp", view(SP_, 0, 4864, F32), [b_s5])
            handoff(b_zl, b_zl + [b_s5])
            for q in range(4):
                for rr in range(4):
                    r = 4 * q + rr
                    bank, bb = next_bank()
                    for x in range(2):
                        col = x * 128
                        for i in range(16):
                            S.op("pe", lambda e: e.matmul(bank[:, col:col + 128], lhsT=WZ[32 * rr:32 * rr + 32, q, i, x, :], rhs=US[32 * rr:32 * rr + 32, q, i, :],
                                                          start=(i == 0), stop=(i == 15), tile_position=(32 * rr, 0)), inc=(i == 15 and x == 1),
                                 reads=[b_wz[q], b_us[q]], writes=[bb])
                    S.op("act", lambda e: e.activation(out=ZL[:, r, :, :].rearrange("p x c -> p (x c)"), in_=bank[:, 0:256], func=AF.Copy),
                         reads=[bb], writes=[b_zl[r]])
            if cfg.get("debug"):
                S.dma("sp", dbq, dr["dbg_zl"], view(A2, 51200, 16384, F32), reads=b_zl, writes=[b_dbg])
                S.dma("sp", dbq, dr["dbg_sc"], view(SP_, 0, 4864, F32), reads=[b_s5], writes=[b_dbg])
                S.dma("sp", dbq, dr["dbg_cp"], view(A2, 0, 34816, BF16), reads=[b_cp], writes=[b_dbg])
                S.dma("sp", dbq, dr["dbg_bd"], view(A0, 32 * KB, 16 * KB, BF16), reads=b_bd, writes=[b_dbg])
                S.dma("sp", dbq, dr["dbg_wz"], view(A0, 0, 32 * KB, BF16), reads=b_wz, writes=[b_dbg])
            handoff(b_ww, b_wz + b_ww)
            dt1 = view(GT, 0, 8192, F32).rearrange("p (r c) -> p r c", r=16)
            dt2 = view(GT, 8192, 8192, F32).rearrange("p (r c) -> p r c", r=16)
            b_dt = S.buf("dt"); handoff([b_dt], b_wz)
            magL_b = bc_last(s_magL, NC_)

            def scan_and_mod(init_ap, b_init, final):
                for r in range(16):
                    for x in range(2):
                        ini = 0.0 if init_ap is None else init_ap[:, r, x:x + 1]
                        S.op("dve", lambda e: e.tensor_tensor_scan(out=ZL[:, r, x, :], data0=magL_b[:, r, :], data1=WW[:, r, x, :], initial=ini,
                                                                   op0=ALU.mult, op1=ALU.add), reads=[b_ww[r], b_s5] + ([b_init] if b_init else []), writes=[b_zl[r]])
                if not final:
                    cmul(zend[:, :, 0], zend[:, :, 1], ZL[:, :, 0, NC_ - 1], ZL[:, :, 1, NC_ - 1], EC[:, :, NC_ - 1], ES[:, :, NC_ - 1], s_t, s_t2,
                         rd=b_zl + [b_tab])
                else:
                    S.op("dve", lambda e: e.tensor_tensor(out=dt1, in0=EC, in1=ZL[:, :, 0, :], op=ALU.mult), reads=[b_tab] + b_zl, writes=[b_dt])
                    S.op("dve", lambda e: e.tensor_tensor(out=dt2, in0=ES, in1=ZL[:, :, 1, :], op=ALU.mult), reads=[b_tab] + b_zl, writes=[b_dt])
                    S.op("dve", lambda e: e.tensor_tensor(out=ZS[:, :, 0, 1:NC_ + 1], in0=dt1, in1=dt2, op=ALU.subtract), reads=[b_dt], writes=b_zs)
                    S.op("dve", lambda e: e.tensor_tensor(out=dt1, in0=EC, in1=ZL[:, :, 1, :], op=ALU.mult), reads=[b_tab] + b_zl, writes=[b_dt])
                    S.op("dve", lambda e: e.tensor_tensor(out=dt2, in0=ES, in1=ZL[:, :, 0, :], op=ALU.mult), reads=[b_tab] + b_zl, writes=[b_dt])
                    S.op("dve", lambda e: e.tensor_tensor(out=ZS[:, :, 1, 1:NC_ + 1], in0=dt1, in1=dt2, op=ALU.add), reads=[b_dt], writes=b_zs)
                    S.op("dve", lambda e: e.tensor_copy(out=ZS[:, :, :, 0], in_=init_ap), reads=[b_init], writes=b_zs)

            S.op("dve", lambda e: e.tensor_tensor(out=dt1, in0=EC, in1=ZL[:, :, 0, :], op=ALU.mult), reads=[b_tab] + b_zl, writes=[b_dt])
            S.op("dve", lambda e: e.tensor_tensor(out=dt2, in0=ES, in1=ZL[:, :, 1, :], op=ALU.mult), reads=[b_tab] + b_zl, writes=[b_dt])
            S.op("dve", lambda e: e.tensor_tensor(out=WW[:, :, 0, :], in0=dt1, in1=dt2, op=ALU.add), reads=[b_dt], writes=b_ww)
            S.op("dve", lambda e: e.tensor_tensor(out=dt1, in0=EC, in1=ZL[:, :, 1, :], op=ALU.mult), reads=[b_tab] + b_zl, writes=[b_dt])
            S.op("dve", lambda e: e.tensor_tensor(out=dt2, in0=ES, in1=ZL[:, :, 0, :], op=ALU.mult), reads=[b_tab] + b_zl, writes=[b_dt])
            S.op("dve", lambda e: e.tensor_tensor(out=WW[:, :, 1, :], in0=dt1, in1=dt2, op=ALU.subtract), reads=[b_dt], writes=b_ww)
            scan_and_mod(None, None, False)
            xfer("e_ww", view(A0, 0, 16384, F32), b_ww)
            if IMP:
                xfer("e_cp", view(A2, 0, 34816, F32), [b_cp])
                xfer("e_bd", view(A0, 32 * KB, 16 * KB, F32), b_bd)
                xfer("e_tab", view(A2, 34816, 16384, F32), [b_tab])
            if IMP:
                xfer("e_sp", view(SP_, 0, 4864, F32), [b_s5])
                S.mute = False
            if cfg.get("debug"):
                S.dma("sp", dbq, dr["dbg_zend"], zend.rearrange("p r x -> p (r x)"), reads=[b_s5], writes=[b_dbg])
            if mode == "A":
                S.dma("sp", oq, dr["summ"][:, 0:32], zend.rearrange("p r x -> p (r x)"), reads=[b_s5], writes=[b_out])
                S.wait_all("sp", [b_out])
            if mode != "A":
                sa = dr["summ_all"]
                zall = view(GT, 0, 512, F32).rearrange("p (j r x) -> p j r x", j=4, x=2)
                zq = S.dma_sem(f"zq{l}")
                S.dma("sp", zq, zall.rearrange("p j r x -> p j (r x)"), sa[:, :, 0:32].rearrange("j p c -> p j c"), reads=[b_dt], writes=[b_dt])
                S.op("dve", lambda e: e.memset(sst, 0.0), reads=[b_s5], writes=[b_s5])
                ctr = sc(24)[:, 0:16]; cti = sc(25)[:, 0:16]
                for j in range(4):
                    cmul(ctr, cti, sst[:, :, 0], sst[:, :, 1], s_l128r, s_l128i, s_t, s_t2)
                    TT(ctr, ctr, zall[:, j, :, 0], ALU.add, rd=[b_dt]); TT(cti, cti, zall[:, j, :, 1], ALU.add, rd=[b_dt])
                    TT(ctr, ctr, sst[:, :, 0], ALU.subtract); TT(cti, cti, sst[:, :, 1], ALU.subtract)
                    S.op("dve", lambda e: e.scalar_tensor_tensor(out=sst[:, :, 0], in0=ctr, scalar=pred[:, j:j + 1], in1=sst[:, :, 0], op0=ALU.mult, op1=ALU.add),
                         reads=[b_s5, b_cst], writes=[b_s5])
                    S.op("dve", lambda e: e.scalar_tensor_tensor(out=sst[:, :, 1], in0=cti, scalar=pred[:, j:j + 1], in1=sst[:, :, 1], op0=ALU.mult, op1=ALU.add),
                         reads=[b_s5, b_cst], writes=[b_s5])
                scan_and_mod(sst, b_s5, True)
                if cfg.get("debug"):
                    S.dma("sp", dbq, dr["dbg_zs"], view(A2, 67584, 8256, BF16), reads=b_zs, writes=[b_dbg])
                handoff(b_zg, b_zl + b_zg)
                for q in range(4):
                    for ib in range(4):
                        bank, bb = next_bank()
                        for i4 in range(4):
                            ip = ib * 4 + i4
                            col = i4 * 128
                            for i in range(ip + 1):
                                S.op("pe", lambda e: e.matmul(bank[:, col:col + 128], lhsT=BD[:, q, ip - i, :], rhs=US[:, q, i, :], start=(i == 0), stop=False),
                                     inc=False, reads=[b_bd[q], b_us[q]], writes=[bb])
                            for rr in range(4):
                                r = 4 * q + rr
                                for x in range(2):
                                    lastw = (rr == 3 and x == 1)
                                    S.op("pe", lambda e: e.matmul(bank[32 * rr:32 * rr + 32, col:col + 128], lhsT=CP[:, ip + 1, r, x, :], rhs=ZS[:, r, x, 0:NC_],
                                                                  start=False, stop=(x == 1), tile_position=(0, 32 * rr)), inc=(lastw and i4 == 3),
                                         reads=[b_cp, b_zs[r]], writes=[bb])
                        S.op("act", lambda e: e.activation(out=ZG[:, q, ib * 4:ib * 4 + 4, :].rearrange("p i c -> p (i c)"), in_=bank[:, :], func=AF.Gelu_apprx_tanh),
                             reads=[bb], writes=[b_zg[q]])
                if cfg.get("debug"):
                    S.dma("sp", dbq, dr["dbg_zg"], view(A2, 51200, 16384, BF16), reads=b_zg, writes=[b_dbg])
                wg = dr["s5_w_glu"][l].rearrange("(k p) n -> p k n", p=128)
                wgl, bwg = wload(wg, 4, 512)
                gate = view(GT, 0, 2048, F32)
                ZZ = [view(GT, 2048 + ft * 2048, 2048, F32) for ft in range(4)]
                sqb = [view(GT, 10240 + i * 1024, 1024, BF16) for i in range(2)]
                rst = view(GT, 12288, 2048, F32)
                b_gate = S.buf("gate"); b_zz = S.bufs(4, "zz"); b_sqb = S.bufs(2, "sqb"); b_rst = S.buf("rst")
                handoff([b_gate, b_rst] + b_zz + b_sqb, [b_dt] + b_ww)
                YCv = YC[:, 0:4, :].rearrange("p f (c i) -> p f i c", i=16)
                for cb in range(4):
                    bankq, bbq = next_bank()
                    for ft in range(4):
                        bank, bb = next_bank()
                        for kk in range(4):
                            S.op("pe", lambda e: e.matmul(bank[:, :], lhsT=wgl[:, kk, ft * 128:(ft + 1) * 128], rhs=ZG[:, kk, cb * 4:cb * 4 + 4, :],
                                                          start=(kk == 0), stop=(kk == 3)), inc=(kk == 3), reads=[bwg] + b_zg, writes=[bb])
                        S.op("act", lambda e: e.activation(out=gate, in_=bank[:, :], func=AF.Sigmoid, bias=bglu[:, ft:ft + 1]), reads=[bb, b_prm], writes=[b_gate])
                        S.op("dve", lambda e: e.tensor_tensor(out=ZZ[ft], in0=ZG[:, ft, cb * 4:cb * 4 + 4, :].rearrange("p i c -> p (i c)"), in1=gate, op=ALU.mult),
                             reads=[b_zg[ft], b_gate], writes=[b_zz[ft]])
                        S.op("act", lambda e: e.activation(out=sqb[ft % 2], in_=ZZ[ft], func=AF.Square), reads=[b_zz[ft]], writes=[b_sqb[ft % 2]])
                        S.op("pe", lambda e: e.matmul(bankq[:, :], lhsT=ones_b, rhs=sqb[ft % 2], start=(ft == 0), stop=(ft == 3)), inc=True,
                             reads=[b_sqb[ft % 2], b_cst], writes=[bbq])
                    S.op("dve", lambda e: e.tensor_scalar(out=rst, in0=bankq[:, :], scalar1=1.0 / 512, scalar2=EPS, op0=ALU.mult, op1=ALU.add), reads=[bbq], writes=[b_rst])
                    S.op("act", lambda e: e.activation(out=rst, in_=rst, func=AF.Sqrt), reads=[b_rst], writes=[b_rst])
                    S.op("dve", lambda e: e.reciprocal(out=rst, in_=rst), reads=[b_rst], writes=[b_rst])
                    for ft in range(4):
                        S.op("dve", lambda e: e.scalar_tensor_tensor(out=YCv[:, ft, cb * 4:cb * 4 + 4, :], in0=ZZ[ft].rearrange("p (i c) -> p i c", i=4),
                                                                     scalar=outg[:, ft:ft + 1], in1=rst.rearrange("p (i c) -> p i c", i=4), op0=ALU.mult, op1=ALU.mult),
                             reads=[b_zz[ft], b_rst, b_prm], writes=b_yc)
                if cfg.get("debug"):
                    S.dma("sp", dbq, dr["dbg_yc"], view(A1, 0, 32 * KB, BF16), reads=b_yc, writes=[b_dbg])

            if mode != "A":
                if stop == "s5":
                    quiesce(); return
                S.handoff(X_b, b_qk + b_og + b_us + b_wz + b_bd + b_ww + [b_dt, b_gate, b_rst] + b_zz + b_sqb)
                for tt in range(NTT):
                    S.dma("sp", xq[tt], X[:, tt, :], xin_ap[tt * 128:(tt + 1) * 128, :], writes=[X_b[tt]])
                wo = dr["w_out"][l].rearrange("(k p) n -> p k n", p=128)
                ws = WStream([(wo[:, :, h * 512:(h + 1) * 512], 8, 512) for h in range(2)])
                for h in range(2):
                    wc, bw = ws.get(h)
                    for tt in range(NTT):
                        bank, bb = next_bank()
                        for kk in range(8):
                            S.op("pe", lambda e: e.matmul(bank[:, :], lhsT=YC[:, kk, tt * 128:(tt + 1) * 128], rhs=wc[:, kk, :],
                                                          start=(kk == 0), stop=(kk == 7)), inc=(kk == 7), reads=[bw, b_yc[tt]], writes=[bb])
                        S.op("dve", lambda e: e.tensor_tensor(out=X[:, tt, h * 512:(h + 1) * 512], in0=X[:, tt, h * 512:(h + 1) * 512], in1=bank[:, :], op=ALU.add),
                             reads=[bb, X_b[tt]], writes=[X_b[tt]])

                if cfg.get("dbg_x1"):
                    b_o1 = S.buf("o1")
                    for tt in range(NTT):
                        S.dma("sp", oq, xout_ap[tt * 128:(tt + 1) * 128, :], X[:, tt, :], reads=[X_b[tt]], writes=[b_o1])
                    S.wait_all("sp", [b_o1])
                    return
                if stop == "wout":
                    quiesce(); return
                S.handoff(HT_b, b_yc)
                a2_users = [b_g, b_sm] + b_kt + b_cl + [b_ct, b_tc] + b_vw + b_pt + b_hb + b_hn + [b_cp, b_tab, b_s5] + b_zl + b_zs + b_zg + b_va + [b_gr] + ml_extra
                handoff([b_junk, b_xh] + b_xnb, a2_users)
                norm_to_hT(g2, False, False)
                if cfg.get("dbg_ht2"):
                    b_o1 = S.buf("o1")
                    S.dma("sp", oq, dr["dbg_ht"], view(A1, 0, 32832, BF16)[:, 0:8 * 2051], reads=HT_b, writes=[b_o1])
                w1 = dr["w_ff1"][l].rearrange("(k p) n -> p k n", p=128)
                w2 = dr["w_ff2"][l].rearrange("(k p) n -> p k n", p=128)
                c1 = [(w1[:, :, hc * 512:(hc + 1) * 512], 8, 512) for hc in range(8)]
                c2 = [(w2[:, hc * 4:(hc + 1) * 4, :], 4, 1024) for hc in range(8)]
                chunks = [c1[0]]
                for hc in range(8):
                    if hc + 1 < 8:
                        chunks.append(c1[hc + 1])
                    chunks.append(c2[hc])
                ws = WStream(chunks)
                k.wci = 0

                def wnext():
                    r = ws.get(k.wci)
                    k.wci += 1
                    return r
                hid = [view(SCR, i * 16 * KB, 16 * KB, BF16).rearrange("p (f t) -> p f t", f=4) for i in range(2)]
                b_hid = [S.bufs(4, f"hid{i}") for i in range(2)]
                sq = [view(SCR, 32 * KB + i * 2048, 2048, F32) for i in range(2)]
                b_sq = S.bufs(2, "sq")
                k.sqi = 0
                handoff(b_hid[0] + b_hid[1] + b_sq, a2_users + [b_junk, b_xh] + b_xnb)

                def ffn1(hc):
                    wc, bw = wnext()
                    hb = hid[hc % 2]
                    for ft in range(4):
                        for nb in range(4):
                            bank, bb = next_bank()
                            for kk in range(8):
                                S.op("pe", lambda e: e.matmul(bank[:, :], lhsT=wc[:, kk, ft * 128:(ft + 1) * 128], rhs=HT[:, kk, 3 + nb * 512:3 + (nb + 1) * 512],
                                                              start=(kk == 0), stop=(kk == 7)), inc=(kk == 7), reads=[bw] + HT_b[nb * 4:nb * 4 + 4], writes=[bb])
                            si = k.sqi % 2; k.sqi += 1
                            S.op("act", lambda e: e.activation(out=sq[si], in_=bank[:, :], func=AF.Square), reads=[bb], writes=[b_sq[si]])
                            S.op("dve", lambda e: e.scalar_tensor_tensor(out=hb[:, ft, nb * 512:(nb + 1) * 512], in0=bank[:, :], scalar=0.0, in1=sq[si],
                                                                         op0=ALU.is_gt, op1=ALU.mult), reads=[bb, b_sq[si]], writes=[b_hid[hc % 2][nb]])

                def ffn2(hc):
                    wc, bw = wnext()
                    hb = hid[hc % 2]
                    for tt in range(NTT):
                        for h in range(2):
                            bank, bb = next_bank()
                            for kk in range(4):
                                S.op("pe", lambda e: e.matmul(bank[:, :], lhsT=hb[:, kk, tt * 128:(tt + 1) * 128], rhs=wc[:, kk, h * 512:(h + 1) * 512],
                                                              start=(kk == 0), stop=(kk == 3)), inc=(kk == 3), reads=[bw, b_hid[hc % 2][tt // 4]], writes=[bb])
                            S.op("dve", lambda e: e.tensor_tensor(out=X[:, tt, h * 512:(h + 1) * 512], in0=X[:, tt, h * 512:(h + 1) * 512], in1=bank[:, :], op=ALU.add),
                                 reads=[bb, X_b[tt]], writes=[X_b[tt]])

                ffn1(0)
                for hc in range(8):
                    if hc + 1 < 8:
                        ffn1(hc + 1)
                    ffn2(hc)

                if last:
                    gfin = view(SCR, 40 * KB, 4096, F32)
                    b_gf = S.buf("gfin")
                    fg = dr["final_norm_g"]
                    S.dma("sp", gq, gfin, bass.AP(fg.tensor, fg.offset, [[0, 128], [1, D]]), writes=[b_gf])
                    ot = [view(SCR, 44 * KB + i * 4096, 4096, F32) for i in range(2)]
                    b_ot = S.bufs(2, "ot")
                    handoff([b_junk], [b_junk] + b_hid[0] + b_hid[1])
                    stats_A(0)
                    for g4 in range(4):
                        if g4 + 1 < 4:
                            stats_A(g4 + 1)
                        stats_B(g4)
                        for tt in range(4 * g4, 4 * g4 + 4):
                            S.op("dve", lambda e: e.scalar_tensor_tensor(out=ot[tt % 2], in0=X[:, tt, :], scalar=rstd[:, tt:tt + 1], in1=gfin,
                                                                         op0=ALU.mult, op1=ALU.mult), reads=[X_b[tt], b_sg[g4], b_gf], writes=[b_ot[tt % 2]])
                            S.dma("sp", oq, xout_ap[tt * 128:(tt + 1) * 128, :], ot[tt % 2], reads=[b_ot[tt % 2]], writes=[b_out])
                else:
                    for tt in range(NTT):
                        S.dma("sp", oq, xout_ap[tt * 128:(tt + 1) * 128, :], X[:, tt, :], reads=[X_b[tt]], writes=[b_out])
                S.wait_all("sp", [b_out])

        layers = cfg["layers"]
        for li, l in enumerate(layers):
            layer(l, dr["xin"], dr.get("xhalo"), dr.get("xout"), last=cfg.get("final", False) and li == len(layers) - 1)
    return nc


_NC_CACHE = {}
N_CORES = 8
S5RAW = ["s5_a_re", "s5_a_im", "s5_log_dt", "s5_b_re", "s5_b_im", "s5_c_re", "s5_c_im", "s5_d"]
A_KEYS = ["norm_mix_g", "w_in", "ml_conv_w", "ml_conv_b", "ml_b_i", "ml_b_f", "ml_norm_g", "s5_b_glu", "s5_out_g"]
B_KEYS = ["w_out", "norm_ffn_g", "w_ff1", "w_ff2", "s5_w_glu", "ml_norm_g", "s5_b_glu", "s5_out_g"]
XF_AB = ["e_a0", "e_va", "e_gates", "e_cl", "e_ww"]
XF_PA = ["e_wz", "e_tab", "e_sp"]
XF_PB = ["e_cp", "e_bd", "e_tab", "e_sp"]
CONST3 = ["ident", "causal", "ones"]


def _get_nc(mode, final):
    key = (mode, final)
    if key not in _NC_CACHE:
        _NC_CACHE[key] = build(dict(layers=[0], nlayers=1, mode=mode, final=final, debug=False))
    return _NC_CACHE[key]


def _consts():
    par = np.zeros((128, 2), np.float32)
    par[:, 1] = (np.arange(128) // 16) % 2
    par[:, 0] = 1 - par[:, 1]
    return {"ident": np.eye(128, dtype=np.float32), "causal": np.triu(np.ones((128, 128), np.float32)),
            "ones": np.ones((128, 128), np.float32), "par01": par,
            "bdmask": np.kron(np.eye(8), np.ones((16, 16))).astype(np.float32)}


def kernel(**inputs):
    x = np.ascontiguousarray(inputs["x"], dtype=np.float32)
    nb, ls, d = x.shape
    per = ls // 4
    consts = _consts()
    cur = [np.ascontiguousarray(x[c // 4, (c % 4) * per:(c % 4 + 1) * per]) for c in range(N_CORES)]
    preds = []
    for c in range(N_CORES):
        p = np.zeros((128, 4), np.float32)
        for j in range(4):
            if j < c % 4:
                p[:, j] = 1.0
        preds.append(p)
    depth = inputs["w_in"].shape[0]
    f32 = lambda a: np.ascontiguousarray(np.asarray(a, dtype=np.float32))
    ncP = _get_nc("P", False)
    mapsP = []
    for c in range(N_CORES):
        lp = min(c // 4, depth - 1)
        m = {k: f32(inputs[k][lp:lp + 1]) for k in S5RAW}
        m.update({k: consts[k] for k in CONST3 + ["par01", "bdmask"]})
        m["pred"] = preds[c]
        mapsP.append(m)
    resP = run_bass_kernel_spmd(ncP, mapsP, core_ids=list(range(N_CORES)))
    s5w = [{k: np.asarray(resP.results[min(4 * l, N_CORES - 1)][k]) for k in set(XF_PA + XF_PB)} for l in range(depth)]
    for l in range(depth):
        halos = [np.zeros((3, d), np.float32) if c % 4 == 0 else np.ascontiguousarray(cur[c - 1][-3:]) for c in range(N_CORES)]
        lw = {k: f32(inputs[k][l:l + 1]) for k in set(A_KEYS + B_KEYS)}
        final = (l == depth - 1)
        ncA = _get_nc("A", False)
        mapsA = []
        for c in range(N_CORES):
            m = {"xin": cur[c], "xhalo": halos[c], "pred": preds[c]}
            m.update({k: consts[k] for k in CONST3})
            m.update({k: lw[k] for k in A_KEYS})
            m.update({k: s5w[l][k] for k in XF_PA})
            mapsA.append(m)
        resA = run_bass_kernel_spmd(ncA, mapsA, core_ids=list(range(N_CORES)))
        summ = [np.asarray(resA.results[c]["summ"]) for c in range(N_CORES)]
        summ_grp = [np.ascontiguousarray(np.stack(summ[4 * g:4 * g + 4])) for g in range(N_CORES // 4)]
        ncB = _get_nc("B", final)
        mapsB = []
        for c in range(N_CORES):
            m = {"xin": cur[c], "pred": preds[c], "summ_all": summ_grp[c // 4],
                 "final_norm_g": f32(inputs["final_norm_g"])}
            m.update({k: consts[k] for k in CONST3})
            m.update({k: lw[k] for k in B_KEYS})
            m.update({k: np.asarray(resA.results[c][k]) for k in XF_AB})
            m.update({k: s5w[l][k] for k in XF_PB})
            mapsB.append(m)
        resB = run_bass_kernel_spmd(ncB, mapsB, core_ids=list(range(N_CORES)))
        cur = [np.asarray(resB.results[c]["xout"]) for c in range(N_CORES)]
    out = np.empty_like(x)
    for c in range(N_CORES):
        out[c // 4, (c % 4) * per:(c % 4 + 1) * per] = cur[c]
    return out
```
